# Optimizing a Trainium2 kernel written in Bass

```python
import math
import jax, jax.numpy as jnp
from jax import lax
import numpy as np

D_MODEL = 1024
BATCH = 8
SEQ = 2048
DEPTH = 2
DEC_BATCH = 128
DEC_SEQ = 8
PAST_LEN = 16384
PAGE_SIZE = 128

N_MIXERS = 2
N_A = (DEPTH + N_MIXERS - 1) // N_MIXERS
N_B = DEPTH // N_MIXERS
N_META = 16
EPS = 1e-6
GDN_HEADS = 8
GDN_DK = 128
GDN_DV = 128
GDN_KEY = GDN_HEADS * GDN_DK
GDN_VAL = GDN_HEADS * GDN_DV
GDN_CONV_CH = 2 * GDN_KEY + GDN_VAL
GDN_PROJ = GDN_CONV_CH + GDN_VAL + 2 * GDN_HEADS
GDN_CONV_W = 4
GDN_CHUNK = 64
POOL_WINDOWS = (2, 4, 8, 16)
POOL_GROUPS = 4
POOL_GW = D_MODEL // POOL_GROUPS
POOL_BUF = max(POOL_WINDOWS) - 1
D_FF = 2816
FFN_CONV_W = 3

kernel_name = 'gdn_pool_convffn_hybrid_step'


def rmsnorm(x, w):
    xf = x.astype(jnp.float32)
    y = xf * lax.rsqrt(jnp.mean(xf * xf, axis=-1, keepdims=True) + EPS)
    return (y * w.astype(jnp.float32)).astype(x.dtype)


def l2norm(x):
    return x * lax.rsqrt(jnp.sum(x * x, axis=-1, keepdims=True) + EPS)


def causal_dwconv(prev, x, w):
    width = w.shape[0]
    L = x.shape[1]
    xcat = jnp.concatenate([prev.astype(x.dtype), x], axis=1)
    out = w[0].astype(x.dtype) * xcat[:, :L]
    for j in range(1, width):
        out = out + w[j].astype(x.dtype) * xcat[:, j:j + L]
    return out, xcat[:, L:]


def _to_blocks(t, pad, chunk):
    t = jnp.pad(t, [(0, 0), (pad, 0)] + [(0, 0)] * (t.ndim - 2))
    t = t.reshape((t.shape[0], t.shape[1] // chunk, chunk) + t.shape[2:])
    return jnp.moveaxis(t, 3, 1)


def gated_delta_rule(q, k, v, g, beta, S0, chunk):
    B, L, H, _ = q.shape
    DV = v.shape[-1]
    pad = (-L) % chunk
    q, k, v, g, beta = [_to_blocks(t.astype(jnp.float32), pad, chunk) for t in (q, k, v, g, beta)]
    G = jnp.cumsum(g, axis=-1)
    idx = jnp.arange(chunk)
    causal = idx[:, None] >= idx[None, :]
    strict = idx[:, None] > idx[None, :]
    decay = jnp.exp(jnp.where(causal, G[..., :, None] - G[..., None, :], -jnp.inf))
    kk = jnp.einsum('bhnid,bhnjd->bhnij', k, k)
    lower = jnp.where(strict, beta[..., :, None] * kk * decay, 0.0)
    eye = jnp.eye(chunk, dtype=jnp.float32)
    T = lax.linalg.triangular_solve(lower + eye, jnp.broadcast_to(eye, lower.shape),
                                    left_side=True, lower=True, unit_diagonal=True)
    U = jnp.einsum('bhnij,bhnje->bhnie', T, v * beta[..., None])
    W = jnp.einsum('bhnij,bhnjd->bhnid', T, k * (beta * jnp.exp(G))[..., None])
    A = jnp.einsum('bhnid,bhnjd->bhnij', q, k) * decay
    q_dec = q * jnp.exp(G)[..., None]
    k_dec = k * jnp.exp(G[..., -1:] - G)[..., None]
    g_tot = jnp.exp(G[..., -1])
    blocks = tuple(jnp.moveaxis(t, 2, 0) for t in (U, W, A, q_dec, k_dec, g_tot))

    def step(S, blk):
        U_c, W_c, A_c, qd_c, kd_c, gt_c = blk
        v_new = U_c - jnp.einsum('bhid,bhde->bhie', W_c, S)
        o = jnp.einsum('bhid,bhde->bhie', qd_c, S) + jnp.einsum('bhij,bhje->bhie', A_c, v_new)
        S = S * gt_c[..., None, None] + jnp.einsum('bhid,bhie->bhde', kd_c, v_new)
        return S, o

    S, o = lax.scan(step, S0.astype(jnp.float32), blocks)
    o = jnp.transpose(o, (1, 0, 3, 2, 4)).reshape(B, -1, H, DV)[:, pad:]
    return o, S


def gdn_mixer(h, conv_prev, S0, w_in, conv_w, A_log, dt_bias, norm_w, w_out, chunk):
    B, L, _ = h.shape
    proj = jnp.einsum('bld,de->ble', h, w_in)
    qkv = proj[..., :GDN_CONV_CH]
    z = proj[..., GDN_CONV_CH:GDN_CONV_CH + GDN_VAL]
    b_logit = proj[..., GDN_CONV_CH + GDN_VAL:GDN_CONV_CH + GDN_VAL + GDN_HEADS]
    a_logit = proj[..., GDN_CONV_CH + GDN_VAL + GDN_HEADS:]
    qkv_c, conv_new = causal_dwconv(conv_prev, qkv, conv_w)
    qkv_c = jax.nn.silu(qkv_c.astype(jnp.float32))
    q = qkv_c[..., :GDN_KEY].reshape(B, L, GDN_HEADS, GDN_DK)
    k = qkv_c[..., GDN_KEY:2 * GDN_KEY].reshape(B, L, GDN_HEADS, GDN_DK)
    v = qkv_c[..., 2 * GDN_KEY:].reshape(B, L, GDN_HEADS, GDN_DV)
    q = l2norm(q) * (GDN_DK ** -0.5)
    k = l2norm(k)
    beta = jax.nn.sigmoid(b_logit.astype(jnp.float32))
    g = -jnp.exp(A_log.astype(jnp.float32)) * jax.nn.softplus(
        a_logit.astype(jnp.float32) + dt_bias.astype(jnp.float32))
    o, S = gated_delta_rule(q, k, v, g, beta, S0, chunk)
    o = o * lax.rsqrt(jnp.mean(o * o, axis=-1, keepdims=True) + EPS) * norm_w.astype(jnp.float32)
    o = o * jax.nn.silu(z.astype(jnp.float32)).reshape(B, L, GDN_HEADS, GDN_DV)
    out = jnp.einsum('ble,ed->bld', o.reshape(B, L, GDN_VAL).astype(h.dtype), w_out)
    return out, conv_new, S


def pool_mixer(h, prev, start_pos, w_grp, scale):
    B, L, _ = h.shape
    hcat = jnp.concatenate([prev.astype(h.dtype), h], axis=1)
    csum = jnp.cumsum(hcat.astype(jnp.float32), axis=1)
    csum = jnp.concatenate([jnp.zeros((B, 1, D_MODEL), jnp.float32), csum], axis=1)
    off = POOL_BUF + 1
    pos = start_pos + jnp.arange(L)
    hf = h.astype(jnp.float32)
    outs = []
    for gi, win in enumerate(POOL_WINDOWS):
        lo, hi = gi * POOL_GW, (gi + 1) * POOL_GW
        wsum = csum[:, off:off + L, lo:hi] - csum[:, off - win:off - win + L, lo:hi]
        cnt = jnp.minimum(win, pos + 1).astype(jnp.float32)[None, :, None]
        pooled = wsum / cnt - hf[..., lo:hi]
        outs.append(jnp.einsum('blc,ce->ble', pooled, w_grp[gi].astype(jnp.float32)))
    out = jnp.concatenate(outs, axis=-1) * scale.astype(jnp.float32)
    return out.astype(h.dtype), hcat[:, L:]


def conv_ffn(h, prev, w_up, conv_w, conv_b, w_down):
    u = jnp.einsum('bld,df->blf', h, w_up)
    uc, buf = causal_dwconv(prev, u, conv_w)
    uc = uc + conv_b.astype(uc.dtype)
    a, b = uc[..., :D_FF], uc[..., D_FF:]
    return jnp.einsum('blf,fd->bld', jax.nn.silu(a) * b, w_down), buf


def trunk(x, gdn_conv_prev, gdn_S0, pool_prev, ffn_prev, start_pos, chunk,
          norm_mix, norm_ffn, gdn_w_in, gdn_conv_w, gdn_A_log, gdn_dt_bias, gdn_norm_w, gdn_w_out,
          pool_w, pool_scale, ffn_w_up, ffn_conv_w, ffn_conv_b, ffn_w_down, norm_final):
    conv_new, S_new, pool_new, ffn_new = [], [], [], []
    for i in range(DEPTH):
        j = i // N_MIXERS
        h = rmsnorm(x, norm_mix[i])
        if i % N_MIXERS == 0:
            m, c, S = gdn_mixer(h, gdn_conv_prev[j], gdn_S0[j], gdn_w_in[j], gdn_conv_w[j],
                                gdn_A_log[j], gdn_dt_bias[j], gdn_norm_w[j], gdn_w_out[j], chunk)
            conv_new.append(c)
            S_new.append(S)
        else:
            m, pbuf = pool_mixer(h, pool_prev[j], start_pos, pool_w[j], pool_scale[j])
            pool_new.append(pbuf)
        x = x + m.astype(x.dtype)
        h = rmsnorm(x, norm_ffn[i])
        f, fbuf = conv_ffn(h, ffn_prev[i], ffn_w_up[i], ffn_conv_w[i], ffn_conv_b[i], ffn_w_down[i])
        ffn_new.append(fbuf)
        x = x + f.astype(x.dtype)
    y = rmsnorm(x, norm_final)
    return y, jnp.stack(conv_new), jnp.stack(S_new), jnp.stack(pool_new), jnp.stack(ffn_new)


def setup_inputs(seed: int = 0) -> dict:
    key = jax.random.key(seed)
    ks = jax.random.split(key, 22)

    def nrm(k, shape, s):
        return jax.random.normal(k, shape, jnp.float32) * s

    dt = jnp.exp(jax.random.uniform(ks[12], (N_A, GDN_HEADS), jnp.float32,
                                    minval=math.log(1e-3), maxval=math.log(1e-1)))
    return {
        'x_prompt': nrm(ks[0], (BATCH, SEQ, D_MODEL), 1.0),
        'x_sample': nrm(ks[1], (DEC_BATCH, DEC_SEQ, D_MODEL), 1.0),
        'state_gdn_conv': nrm(ks[2], (N_A, DEC_BATCH, GDN_CONV_W - 1, GDN_CONV_CH), 1.0),
        'state_gdn_rec': nrm(ks[3], (N_A, DEC_BATCH, GDN_HEADS, GDN_DK, GDN_DV), GDN_DK ** -0.5),
        'state_pool': nrm(ks[4], (N_B, DEC_BATCH, POOL_BUF, D_MODEL), 1.0),
        'state_ffn_conv': nrm(ks[5], (DEPTH, DEC_BATCH, FFN_CONV_W - 1, 2 * D_FF), 1.0),
        'meta_tokens': nrm(ks[6], (N_META, D_MODEL), 1.0),
        'norm_mix': 1.0 + nrm(ks[7], (DEPTH, D_MODEL), 0.02),
        'norm_ffn': 1.0 + nrm(ks[8], (DEPTH, D_MODEL), 0.02),
        'gdn_w_in': nrm(ks[9], (N_A, D_MODEL, GDN_PROJ), D_MODEL ** -0.5),
        'gdn_conv_w': nrm(ks[10], (N_A, GDN_CONV_W, GDN_CONV_CH), GDN_CONV_W ** -0.5),
        'gdn_A_log': jnp.log(jax.random.uniform(ks[11], (N_A, GDN_HEADS), jnp.float32, minval=1.0, maxval=16.0)),
        'gdn_dt_bias': dt + jnp.log(-jnp.expm1(-dt)),
        'gdn_norm_w': 1.0 + nrm(ks[13], (N_A, GDN_DV), 0.02),
        'gdn_w_out': nrm(ks[14], (N_A, GDN_VAL, D_MODEL), GDN_VAL ** -0.5),
        'pool_w': nrm(ks[15], (N_B, POOL_GROUPS, POOL_GW, POOL_GW), POOL_GW ** -0.5),
        'pool_scale': 1.0 + nrm(ks[16], (N_B, D_MODEL), 0.05),
        'ffn_w_up': nrm(ks[17], (DEPTH, D_MODEL, 2 * D_FF), D_MODEL ** -0.5),
        'ffn_conv_w': nrm(ks[18], (DEPTH, FFN_CONV_W, 2 * D_FF), FFN_CONV_W ** -0.5),
        'ffn_conv_b': nrm(ks[19], (DEPTH, 2 * D_FF), 0.01),
        'ffn_w_down': nrm(ks[20], (DEPTH, D_FF, D_MODEL), D_FF ** -0.5),
        'norm_final': 1.0 + nrm(ks[21], (D_MODEL,), 0.02),
    }


def reference(x_prompt, x_sample, state_gdn_conv, state_gdn_rec, state_pool, state_ffn_conv,
              meta_tokens, norm_mix, norm_ffn, gdn_w_in, gdn_conv_w, gdn_A_log, gdn_dt_bias,
              gdn_norm_w, gdn_w_out, pool_w, pool_scale, ffn_w_up, ffn_conv_w, ffn_conv_b,
              ffn_w_down, norm_final):
    Bp = x_prompt.shape[0]
    dtp = x_prompt.dtype
    meta = jnp.broadcast_to(meta_tokens.astype(dtp)[None], (Bp, N_META, D_MODEL))
    xp = jnp.concatenate([meta, x_prompt], axis=1)
    zc = jnp.zeros((N_A, Bp, GDN_CONV_W - 1, GDN_CONV_CH), dtp)
    zS = jnp.zeros((N_A, Bp, GDN_HEADS, GDN_DK, GDN_DV), jnp.float32)
    zp = jnp.zeros((N_B, Bp, POOL_BUF, D_MODEL), dtp)
    zf = jnp.zeros((DEPTH, Bp, FFN_CONV_W - 1, 2 * D_FF), dtp)
    yp, p_conv, p_rec, p_pool, p_ffn = trunk(
        xp, zc, zS, zp, zf, 0, GDN_CHUNK,
        norm_mix, norm_ffn, gdn_w_in, gdn_conv_w, gdn_A_log, gdn_dt_bias, gdn_norm_w, gdn_w_out,
        pool_w, pool_scale, ffn_w_up, ffn_conv_w, ffn_conv_b, ffn_w_down, norm_final)
    y_sample, s_conv, s_rec, s_pool, s_ffn = trunk(
        x_sample, state_gdn_conv, state_gdn_rec, state_pool, state_ffn_conv, PAST_LEN,
        min(GDN_CHUNK, x_sample.shape[1]),
        norm_mix, norm_ffn, gdn_w_in, gdn_conv_w, gdn_A_log, gdn_dt_bias, gdn_norm_w, gdn_w_out,
        pool_w, pool_scale, ffn_w_up, ffn_conv_w, ffn_conv_b, ffn_w_down, norm_final)
    y_prompt = yp[:, N_META:]
    return (y_prompt, y_sample, p_conv, p_rec, p_pool, p_ffn, s_conv, s_rec, s_pool, s_ffn)
```

```python
import contextlib
import numpy as np
import concourse.bass as bass
import concourse.mybir as mybir
from concourse.bass_utils import run_bass_kernel_spmd

F32 = mybir.dt.float32
BF16 = mybir.dt.bfloat16
ALU = mybir.AluOpType
AF = mybir.ActivationFunctionType
AX = mybir.AxisListType

D = 1024
NH = 8
DFF = 2816
NFC = 44
SEQ = 2048
NMETA = 16
EPS = 1e-6
NEG = -1.0e30
DEBUG_MAP = None
WINS = (2, 4, 8, 16)


class Tok:
    __slots__ = ("lastw", "readers", "excl")

    def __init__(self):
        self.lastw = None
        self.readers = []
        self.excl = False


class Op:
    __slots__ = ("eng", "fn", "deps", "ms", "dma_sem", "dma_val", "is_dma", "where")

    def __init__(self, eng, fn):
        import sys as _s
        f = _s._getframe(3)
        self.where = (f.f_lineno, f.f_back.f_lineno if f.f_back else 0)
        self.eng = eng
        self.fn = fn
        self.deps = []
        self.ms = None
        self.is_dma = False
        self.dma_sem = None
        self.dma_val = 0


class Prog:
    ENGS = ("pe", "act", "dve", "pool", "sp")

    def __init__(self, nc):
        self.nc = nc
        self.ops = {e: [] for e in self.ENGS}
        self.streams = {}
        self.pending = {}

    def barrier(self):
        lasts = [self.ops[e][-1] for e in self.ENGS if self.ops[e]]
        lasts += [st[0] for st in self.streams.values() if st[0] is not None]
        for e in self.ENGS:
            self.pending[e] = list(lasts)

    def op(self, eng, fn, reads=(), writes=(), stream=None):
        o = Op(eng, fn)
        is_dma = stream is not None
        deps = []
        for t in reads:
            if t.lastw is not None:
                deps.append((t.lastw, True))
            if t.excl:
                for r in t.readers:
                    if r.eng != eng:
                        deps.append((r, True))
        for t in writes:
            if t.lastw is not None:
                deps.append((t.lastw, False))
            for r in t.readers:
                deps.append((r, False))
        for d in self.pending.pop(eng, []):
            deps.append((d, True))
        if is_dma:
            o.is_dma = True
            st = self.streams.setdefault(stream, [None, 0])
            if st[0] is not None:
                deps.append((st[0], True))
            st[1] += 1
            o.dma_sem = stream
            o.dma_val = 16 * st[1]
            st[0] = o
        seen = set()
        for d, raw in deps:
            if d is o or id(d) in seen:
                continue
            if (not d.is_dma) and (not is_dma) and d.eng == eng and eng == "pe":
                continue
            seen.add(id(d))
            o.deps.append(d)
        for t in reads:
            t.readers.append(o)
        for t in writes:
            t.lastw = o
            t.readers = []
        self.ops[eng].append(o)
        return o

    def emit(self):
        nc = self.nc
        for e in self.ENGS:
            for o in self.ops[e]:
                for d in o.deps:
                    if not d.is_dma:
                        d.ms = True
        for e in self.ENGS:
            k = 0
            for o in self.ops[e]:
                if o.ms and not o.is_dma:
                    k += 1
                    o.ms = k
        with contextlib.ExitStack() as es:
            esem = {e: es.enter_context(nc.semaphore("s_" + e)) for e in self.ENGS}
            dsem = {k: es.enter_context(nc.semaphore("d_%d" % i)) for i, k in enumerate(self.streams)}
            block = es.enter_context(nc.Block())
            prog = self

            def run(e, engobj):
                seen = {}
                for o in prog.ops[e]:
                    for d in o.deps:
                        if d.is_dma:
                            key, val, sem = ("d", d.dma_sem), d.dma_val, dsem[d.dma_sem]
                        else:
                            key, val, sem = ("e", d.eng), d.ms, esem[d.eng]
                        if seen.get(key, 0) >= val:
                            continue
                        seen[key] = val
                        engobj.wait_ge(sem, val)
                    ins = o.fn(engobj)
                    if DEBUG_MAP is not None:
                        try:
                            DEBUG_MAP[str(ins.ins.name)] = o.where
                        except Exception as ex:
                            DEBUG_MAP["err"] = repr(ex)
                    if o.is_dma:
                        ins.then_inc(dsem[o.dma_sem], 16)
                    elif o.ms:
                        ins.then_inc(esem[e], 1)
                if e == "sp":
                    for k, st in prog.streams.items():
                        engobj.wait_ge(dsem[k], 16 * st[1])

            block.tensor(lambda eng: run("pe", eng))
            block.scalar(lambda eng: run("act", eng))
            block.vector(lambda eng: run("dve", eng))
            block.gpsimd(lambda eng: run("pool", eng))
            block.sync(lambda eng: run("sp", eng))


VEC_COLS = {}


def _vec_layout():
    off = 0
    for name, n in (("nm0", 8), ("nm1", 8), ("nf0", 8), ("nf1", 8), ("nfin", 8),
                    ("gcw", 96), ("fcw0", 132), ("fcw1", 132), ("fcb0", 44), ("fcb1", 44),
                    ("psc", 8), ("gnw", 1), ("alog", 1), ("dtb", 1)):
        VEC_COLS[name] = off
        off += n
    return off


NV = _vec_layout()
C_ID, C_TRI, C_NEGU, C_POSL, C_INVC = 0, 128, 192, 256, 320
C_TRI8, C_NEGU8, C_POSL8, C_SEL, C_RM = 380, 444, 508, 572, 636
NCONST = 636 + 8


def build_consts():
    c = np.zeros((128, NCONST), np.float32)
    c[:, C_ID:C_ID + 128] = np.eye(128, dtype=np.float32)
    p = np.arange(64)[:, None]
    f = np.arange(64)[None, :]
    c[:64, C_TRI:C_TRI + 64] = (f >= p).astype(np.float32)
    c[:64, C_NEGU:C_NEGU + 64] = np.where(f >= p, 0.0, NEG)
    c[:64, C_POSL:C_POSL + 64] = np.where(f < p, 0.0, -NEG)
    for gi, w in enumerate(WINS):
        for t in range(15):
            c[:, C_INVC + gi * 15 + t] = 1.0 / min(w, t + 1)
    same = (p // 8) == (f // 8)
    c[:64, C_TRI8:C_TRI8 + 64] = (same & (f >= p)).astype(np.float32)
    c[:64, C_NEGU8:C_NEGU8 + 64] = np.where(same & (f >= p), 0.0, NEG)
    c[:64, C_POSL8:C_POSL8 + 64] = np.where(same & (f < p), 0.0, -NEG)
    c[:64, C_SEL:C_SEL + 64] = (p == 8 * (f // 8) + 7).astype(np.float32)
    c[:64, C_RM:C_RM + 8] = ((p // 8) == np.arange(8)[None, :]).astype(np.float32)
    return c


def build_vecs(inp):
    v = np.zeros((128, NV), np.float32)

    def put(name, arr):
        a = np.asarray(arr, np.float32).reshape(-1, 128).T
        v[:, VEC_COLS[name]:VEC_COLS[name] + a.shape[1]] = a

    put("nm0", inp["norm_mix"][0]); put("nm1", inp["norm_mix"][1])
    put("nf0", inp["norm_ffn"][0]); put("nf1", inp["norm_ffn"][1])
    put("nfin", inp["norm_final"])
    put("gcw", inp["gdn_conv_w"][0].reshape(-1))
    put("fcw0", inp["ffn_conv_w"][0].reshape(-1)); put("fcw1", inp["ffn_conv_w"][1].reshape(-1))
    put("fcb0", inp["ffn_conv_b"][0]); put("fcb1", inp["ffn_conv_b"][1])
    put("psc", inp["pool_scale"][0])
    put("gnw", inp["gdn_norm_w"][0])
    v[0:8, VEC_COLS["alog"]] = inp["gdn_A_log"][0]
    v[0:8, VEC_COLS["dtb"]] = inp["gdn_dt_bias"][0]
    return v


class Seg:
    def __init__(self, kind, n, off, pos0=0):
        self.kind, self.n, self.off, self.pos0 = kind, n, off, pos0

    def tiles(self):
        if self.kind == "S":
            return [(self, 0, 128)]
        k = (self.n + 511) // 512
        base = (self.n // k + 7) // 8 * 8
        out, t = [], 0
        while t < self.n:
            m = min(base, self.n - t)
            out.append((self, t, m))
            t += m
        return out


class ST:
    def __init__(self, segs, first, last):
        self.segs, self.first, self.last = segs, first, last
        self.NT = sum(s.n for s in segs)

    def tiles(self):
        return [t for s in self.segs for t in s.tiles()]


SUPER = [
    ST([Seg("P", 592, 0, 0), Seg("S", 128, 592)], True, False),
    ST([Seg("P", 704, 0, 592)], False, False),
    ST([Seg("P", 768, 0, 1296)], False, True),
]
NTMAX = 768


class Builder:
    def __init__(self):
        self.nc = nc = bass.Bass("TRN2", target_bir_lowering=False)
        self.P = Prog(nc)
        self.es = contextlib.ExitStack()
        self.toks = {}
        self.rrc = {}
        self.phase_id = 0

        def din(name, shape):
            return nc.dram_tensor(name, list(shape), F32, kind="ExternalInput").ap()

        def dout(name, shape):
            return nc.dram_tensor(name, list(shape), F32, kind="ExternalOutput").ap()

        self.xp = din("xp", [SEQ, D]); self.xs = din("xs", [128, D])
        self.st_conv = din("st_conv", [48, 3072]); self.st_rec = din("st_rec", [16, 8, 128, 128])
        self.st_pool = din("st_pool", [240, D]); self.st_ffn = din("st_ffn", [2, 32, 5632])
        self.meta = din("meta", [NMETA, D])
        self.w_in = din("w_in", [D, 4112]); self.wba = din("wba", [D, 40])
        self.w_out = din("w_out", [D, D]); self.pool_w = din("pool_w", [4, 256, 256])
        self.w_up = din("w_up", [2, D, 5632]); self.w_down = din("w_down", [2, DFF, D])
        self.vecs_d = din("vecs", [128, NV]); self.consts_d = din("consts", [128, NCONST])
        self.yp = dout("yp", [SEQ, D]); self.ys = dout("ys", [128, D])
        self.o_pconv = dout("o_pconv", [3, 3072]); self.o_prec = dout("o_prec", [8, 128, 128])
        self.o_ppool = dout("o_ppool", [15, D]); self.o_pffn = dout("o_pffn", [2, 2, 5632])
        self.o_sconv = dout("o_sconv", [48, 3072]); self.o_srec = dout("o_srec", [16, 8, 128, 128])
        self.o_spool = dout("o_spool", [240, D]); self.o_sffn = dout("o_sffn", [2, 32, 5632])

    def tok(self, *key):
        t = self.toks.get(key)
        if t is None:
            t = self.toks[key] = Tok()
            if key[0] == "bank":
                t.excl = True
        return t

    def sb(self, name, shape, dt):
        return self.es.enter_context(self.nc.sbuf_tensor(name, list(shape), dt))

    def rr(self, name, choices):
        i = self.rrc.get(name, 0)
        self.rrc[name] = i + 1
        return choices[i % len(choices)]

    def bank(self):
        i = self.rrc.get("bank", 0)
        self.rrc["bank"] = i + 1
        i %= 8
        return self.banks[i], self.tok("bank", i)

    def phase(self):
        self.P.barrier()
        self.aoff = 0
        self.phase_id += 1

    def A(self, shape, dt, key=None):
        n = int(np.prod(shape[1:]))
        nb = n * (4 if dt == F32 else 2)
        nb = (nb + 31) // 32 * 32
        ne = nb // 2
        assert self.aoff + ne <= self.arena_n, ("arena overflow", self.aoff, ne, self.arena_n)
        ap = self.arena[0:shape[0], self.aoff:self.aoff + ne]
        self.aoff += ne
        if dt == F32:
            ap = ap.bitcast(F32)
        ap = ap[:, 0:n]
        if len(shape) == 3:
            ap = ap.rearrange("p (a b) -> p a b", b=shape[2])
        elif len(shape) == 4:
            ap = ap.rearrange("p (a b c) -> p a b c", b=shape[2], c=shape[3])
        return ap, self.tok("arena", self.phase_id, self.aoff)

    def mm(self, out, lhsT, rhs, r, w, start=True, stop=True, skip=False):
        if skip:
            self.P.op("pe", lambda e: e.matmul(out, lhsT=lhsT, rhs=rhs, start=start, stop=stop, skip_group_check=True), r, w)
        else:
            self.P.op("pe", lambda e: e.matmul(out, lhsT=lhsT, rhs=rhs, start=start, stop=stop), r, w)

    def tr(self, out, in_, ident, r, w):
        self.P.op("pe", lambda e: e.transpose(out=out, in_=in_, identity=ident), r, w)

    def act(self, out, in_, func, r, w, bias=None, scale=None):
        kw = {}
        if bias is not None:
            kw["bias"] = bias
        if scale is not None:
            kw["scale"] = scale
        self.P.op("act", lambda e: e.activation(out=out, in_=in_, func=func, **kw), r, w)

    def tt(self, eng, out, in0, in1, op, r, w):
        self.P.op(eng, lambda e: e.tensor_tensor(out=out, in0=in0, in1=in1, op=op), r, w)

    def ts(self, eng, out, in0, s1, op0, r, w, s2=None, op1=None):
        if op1 is None:
            self.P.op(eng, lambda e: e.tensor_scalar(out=out, in0=in0, scalar1=s1, scalar2=None, op0=op0), r, w)
        else:
            self.P.op(eng, lambda e: e.tensor_scalar(out=out, in0=in0, scalar1=s1, scalar2=s2, op0=op0, op1=op1), r, w)

    def stt(self, out, in0, scalar, in1, op0, op1, r, w):
        self.P.op("dve", lambda e: e.scalar_tensor_tensor(out=out, in0=in0, scalar=scalar, in1=in1, op0=op0, op1=op1), r, w)

    def cp(self, eng, out, in_, r, w):
        if eng == "act":
            self.act(out, in_, AF.Copy, r, w)
        else:
            self.P.op(eng, lambda e: e.tensor_copy(out=out, in_=in_), r, w)

    def dma(self, q, out, in_, r, w, stream):
        self.P.op(q, lambda e: e.dma_start(out=out, in_=in_), r, w, stream=stream)

    def memset(self, eng, ap, val, w):
        self.P.op(eng, lambda e: e.memset(ap, val), (), w)

    def vcol(self, name, j=0, np_=128):
        c = VEC_COLS[name] + j
        return self.vecs[0:np_, c:c + 1]

    @staticmethod
    def V(seg, ap):
        if seg.kind == "S":
            return ap.rearrange("p (s t) -> p s t", t=8)
        return ap

    @staticmethod
    def ext_dst(seg, buf, H, n):
        if seg.kind == "S":
            return buf[:, 0:16 * (H + 8)].rearrange("p (s w) -> p s w", w=H + 8)[:, :, H:H + 8]
        return buf[:, H:H + n]

    @staticmethod
    def ext_tap(seg, buf, H, j, n):
        if seg.kind == "S":
            return buf[:, 0:16 * (H + 8)].rearrange("p (s w) -> p s w", w=H + 8)[:, :, j:j + 8]
        return buf[:, j:j + n]

    @staticmethod
    def ext_halo(seg, buf, H):
        if seg.kind == "S":
            return buf[:, 0:16 * (H + 8)].rearrange("p (s w) -> p s w", w=H + 8)[:, :, 0:H]
        return buf[:, 0:H]

    @staticmethod
    def ext_tail(seg, buf, H, n):
        if seg.kind == "S":
            return buf[:, 0:16 * (H + 8)].rearrange("p (s w) -> p s w", w=H + 8)[:, :, 8:8 + H]
        return buf[:, n:n + H]

    def build(self):
        nc = self.nc
        with self.es:
            self.xT = self.sb("xT", [128, 8, NTMAX], F32)
            self.S32 = self.sb("S32", [128, 8, 128], F32)
            self.S16 = self.sb("S16", [128, 8, 128], BF16)
            self.HG = self.sb("HG", [128, 24, 3], F32)
            self.HF = self.sb("HF", [128, 2, NFC, 2], F32)
            self.HP = self.sb("HP", [128, 8, 15], F32)
            self.vecs = self.sb("vecs_sb", [128, NV], F32)
            self.cst = self.sb("cst", [128, NCONST], F32)
            self.idb = self.sb("idb", [128, 128], BF16)
            self.ones_m = self.sb("ones_m", [128, 128], BF16)
            self.ones_1 = self.sb("ones_1", [128, 128], BF16)
            self.ones_f = self.sb("ones_f", [64, 128], F32)
            self.nexpA = self.sb("nexpA", [8, 1], F32)
            self.lnq = self.sb("lnq", [128, 1], F32)
            self.banks = [self.es.enter_context(nc.psum_tensor("pb%d" % i, [128, 512], F32)) for i in range(8)]
            rem = nc.sbuf_bytes_remaining - 2048
            self.arena_n = (rem // 2) // 64 * 64
            self.arena = self.sb("arena", [128, self.arena_n], BF16)
            self.aoff = 0
            self.idf = self.cst[:, C_ID:C_ID + 128]
            tC = self.tok("consts")
            self.dma("sp", self.vecs[:], self.vecs_d, (), [tC], "ldc0")
            self.dma("sp", self.cst[:], self.consts_d, (), [tC], "ldc1")
            self.cp("dve", self.idb[:], self.idf, [tC], [tC])
            self.memset("pool", self.ones_m[:], 1.0 / 1024.0, [tC])
            self.memset("pool", self.ones_1[:], 1.0, [tC])
            self.memset("pool", self.ones_f[:], 1.0, [tC])
            self.memset("pool", self.lnq[:], -0.5 * float(np.log(128.0)), [tC])
            self.memset("pool", self.S32[:], 0.0, [self.tok("S32")])
            self.memset("pool", self.S16[:], 0.0, [self.tok("S16")])
            self.memset("pool", self.HG[:], 0.0, [self.tok("HG")])
            self.memset("pool", self.HF[:], 0.0, [self.tok("HF")])
            self.memset("pool", self.HP[:], 0.0, [self.tok("HP")])
            self.act(self.nexpA[:], self.vcol("alog", 0, 8), AF.Exp, [tC], [tC])
            self.ts("dve", self.nexpA[:], self.nexpA[:], -1.0, ALU.mult, [tC], [tC])
            self.tC = tC
            for st in SUPER:
                self.run_super(st)
            self.P.emit()
        return nc

    def run_super(self, st):
        self.load_x(st)
        self.gdn(st)
        self.ffn(st, 0)
        self.pool_mixer(st)
        self.ffn(st, 1)
        self.final(st)

    def xtok(self, dc, tile):
        return self.tok("xT", dc, tile[0].off + tile[1])

    def load_x(self, st):
        if st.first:
            self.phase()
        stg = [self.A([128, 4, D], F32) for _ in range(2)]
        bi = 0
        for seg in st.segs:
            for (_, t0, n) in seg.tiles():
                sg, tsg = stg[bi % 2]
                bi += 1
                nb = (n + 127) // 128
                for b in range(nb):
                    m = min(128, n - b * 128)
                    if seg.kind == "S":
                        self.dma("sp", sg[0:m, b, :], self.xs[0:m, :], (), [tsg], "ldx")
                    else:
                        p0 = seg.pos0 + t0 + b * 128
                        r = 0
                        if p0 < NMETA:
                            k = min(m, NMETA - p0)
                            self.dma("sp", sg[0:k, b, :], self.meta[p0:p0 + k, :], (), [tsg], "ldx")
                            r = k
                        if r < m:
                            a = p0 + r - NMETA
                            self.dma("sp", sg[r:m, b, :], self.xp[a:a + (m - r), :], (), [tsg], "ldx")
                tile = (seg, t0, n)
                c0 = seg.off + t0
                for dc in range(8):
                    pb, tpb = self.bank()
                    for b in range(nb):
                        m = min(128, n - b * 128)
                        self.tr(pb[:, b * 128:b * 128 + m], sg[0:m, b, dc * 128:(dc + 1) * 128], self.idf[0:m, 0:m],
                                [tsg, self.tC], [tpb])
                    self.cp(self.rr("ev", ["act", "dve"]), self.xT[:, dc, c0:c0 + n], pb[:, 0:n], [tpb], [self.xtok(dc, tile)])

    def norm(self, st, wname, dstf, sq2, rsb):
        for tile in st.tiles():
            seg, t0, n = tile
            c0 = seg.off + t0
            sq, tsq = sq2[self.rr("sq2", [0, 1])]
            rs, trs = rsb[self.rr("rsb", [0, 1])]
            pb, tpb = self.bank()
            for dc in range(8):
                xin = self.xT[:, dc, c0:c0 + n]
                if dc % 2 == 0:
                    self.tt("pool", sq[:, dc, 0:n], xin, xin, ALU.mult, [self.xtok(dc, tile)], [tsq])
                else:
                    self.act(sq[:, dc, 0:n], xin, AF.Square, [self.xtok(dc, tile)], [tsq])
            for dc in range(8):
                self.mm(pb[:, 0:n], self.ones_m[:], sq[:, dc, 0:n], [tsq, self.tC], [tpb], start=(dc == 0), stop=(dc == 7))
            self.act(rs[:, 0:n], pb[:, 0:n], AF.Ln, [tpb], [trs], bias=EPS)
            self.act(rs[:, 0:n], rs[:, 0:n], AF.Exp, [trs], [trs], scale=-0.5)
            for dc in range(8):
                dst, tdst = dstf(dc, tile)
                self.stt(dst, self.xT[:, dc, c0:c0 + n], self.vcol(wname, dc), rs[:, 0:n], ALU.mult, ALU.mult,
                         [self.xtok(dc, tile), trs, self.tC], [tdst])

    def gdn(self, st):
        NT = st.NT
        self.phase()
        hasS = any(s.kind == "S" for s in st.segs)
        xn, _ = self.A([128, 8, NT], BF16)
        QKVZ, _ = self.A([128, 32, NT], BF16)
        GB, tGB = self.A([40, NT], F32)
        mark = self.aoff
        sq2 = [self.A([128, 8, 512], BF16) for _ in range(2)]
        rsb = [self.A([128, 512], F32) for _ in range(2)]
        wsl = [self.A([128, 8, 512], BF16) for _ in range(3)]
        wbat, twba = self.A([128, 8, 40], BF16)
        ext = [self.A([128, 3 + 512], F32) for _ in range(3)]
        acc = [self.A([128, 512], F32) for _ in range(2)]
        sil = [self.A([128, 512], F32) for _ in range(2)]
        sqh = [self.A([128, 512], BF16) for _ in range(3)]
        rin = [self.A([128, 512], F32) for _ in range(2)]
        bat = [self.A([8, 512], F32) for _ in range(4)]
        if hasS:
            SHG, tSHG = self.A([128, 24, 48], F32)
            stg, tstg = self.A([48, 3072], F32)
        self.memset("pool", GB, 0.0, [tGB])
        xnt = lambda dc, tile: (xn[:, dc, tile[0].off + tile[1]:tile[0].off + tile[1] + tile[2]], self.tok("xn", self.phase_id, dc, tile[0].off + tile[1]))
        self.norm(st, "nm0", xnt, sq2, rsb)
        if hasS:
            self.dma("sp", stg, self.st_conv, (), [tstg], "ldst")
            for g in range(3):
                pb, tpb = self.bank()
                for j in range(8):
                    fc = g * 8 + j
                    self.tr(pb[:, j * 48:(j + 1) * 48], stg[0:48, fc * 128:(fc + 1) * 128], self.idf[0:48, 0:48], [tstg, self.tC], [tpb])
                self.cp("act", SHG[:, g * 8:(g + 1) * 8, :], pb[:, 0:384].rearrange("p (a b) -> p a b", b=48), [tpb], [tSHG])
        self.dma("pool", wbat, self.wba.rearrange("(kc p) f -> p kc f", p=128), (), [twba], "ldwba")
        for tile in st.tiles():
            seg, t0, n = tile
            c0 = seg.off + t0
            pb, tpb = self.bank()
            for kc in range(8):
                self.mm(pb[0:40, 0:n], wbat[:, kc, :], xn[:, kc, c0:c0 + n], [twba, xnt(kc, tile)[1]], [tpb], start=(kc == 0), stop=(kc == 7))
            self.act(GB[32:40, c0:c0 + n], pb[32:40, 0:n], AF.Sigmoid, [tpb], [tGB])
            (b1, t1), (b2, t2), (b3, t3), (b4, t4) = bat
            self.ts("dve", b1[:, 0:n], pb[0:8, 0:n], self.vcol("dtb", 0, 8), ALU.add, [tpb, self.tC], [t1])
            self.stt(b2[:, 0:n], b1[:, 0:n], -1.0, b1[:, 0:n], ALU.mult, ALU.max, [t1], [t2])
            self.act(b3[:, 0:n], b2[:, 0:n], AF.Exp, [t2], [t3], scale=-1.0)
            self.act(b4[:, 0:n], b3[:, 0:n], AF.Ln, [t3], [t4], bias=1.0)
            self.stt(b2[:, 0:n], b1[:, 0:n], 0.0, b4[:, 0:n], ALU.max, ALU.add, [t1, t4], [t2])
            self.ts("dve", GB[0:8, c0:c0 + n], b2[:, 0:n], self.nexpA[:, 0:1], ALU.mult, [t2, self.tC], [tGB])
        def ld_win(u):
            wt_, twt_ = wsl[u % 3]
            self.dma("pool", wt_, self.w_in[:, u * 512:(u + 1) * 512].rearrange("(kc p) f -> p kc f", p=128), (), [twt_], "ldw%d" % (u % 3))
        ld_win(0); ld_win(1)
        pend = []
        qk_list = []
        for u in range(8):
            wt, twt = wsl[u % 3]
            if u + 2 < 8:
                ld_win(u + 2)
            for j in range(4):
                fc = u * 4 + j
                kind = fc // 8
                prev_ext = None
                for tile in st.tiles():
                    seg, t0, n = tile
                    c0 = seg.off + t0
                    pb, tpb = self.bank()
                    for kc in range(8):
                        self.mm(pb[:, 0:n], wt[:, kc, j * 128:(j + 1) * 128], xn[:, kc, c0:c0 + n], [twt, xnt(kc, tile)[1]], [tpb],
                                start=(kc == 0), stop=(kc == 7))
                    dst = QKVZ[:, fc, c0:c0 + n]
                    tdst = self.tok("qkvz", self.phase_id, fc, c0)
                    if kind == 3:
                        self.act(self.V(seg, dst), self.V(seg, pb[:, 0:n]), AF.Silu, [tpb], [tdst])
                        continue
                    ex, tex = ext[self.rr("ext", [0, 1, 2])]
                    if seg.kind == "S":
                        self.cp("pool", self.ext_halo(seg, ex, 3), SHG[:, fc, :].rearrange("p (s r) -> p s r", r=3), [tSHG], [tex])
                    elif t0 == 0:
                        self.cp("pool", ex[:, 0:3], self.HG[:, fc, :], [self.tok("HG")], [tex])
                    else:
                        pe_, tpe_, pn = prev_ext
                        self.cp("pool", ex[:, 0:3], pe_[:, pn:pn + 3], [tpe_], [tex])
                    self.cp("act", self.ext_dst(seg, ex, 3, n), self.V(seg, pb[:, 0:n]), [tpb], [tex])
                    prev_ext = (ex, tex, n)
                    if seg.kind == "S":
                        self.cp("pool", SHG[:, fc, :].rearrange("p (s r) -> p s r", r=3), self.ext_tail(seg, ex, 3, n), [tex], [tSHG])
                    elif t0 + n == seg.n:
                        self.cp("pool", self.HG[:, fc, :], ex[:, n:n + 3], [tex], [self.tok("HG")])
                    ac, tac = acc[self.rr("acc", [0, 1])]
                    av = self.V(seg, ac[:, 0:n])
                    self.act(av, self.ext_tap(seg, ex, 3, 0, n), AF.Identity, [tex, self.tC], [tac], scale=self.vcol("gcw", 0 * 24 + fc))
                    for tap in (1, 2, 3):
                        self.stt(av, self.ext_tap(seg, ex, 3, tap, n), self.vcol("gcw", tap * 24 + fc), av, ALU.mult, ALU.add, [tex, tac, self.tC], [tac])

                    def tail(dst=dst, tdst=tdst, ac=ac, tac=tac, n=n):
                        self.act(dst, ac[:, 0:n], AF.Silu, [tac], [tdst])
                    if pend:
                        pend.pop(0)()
                    pend.append(tail)
                    if kind < 2:
                        qk_list.append((kind, dst, tdst, n))
            while pend:
                pend.pop(0)()
            def nstage1(item):
                kind, dst, tdst, n = item
                sh, tsh = sqh[self.rr("sqh", [0, 1, 2])]
                self.tt("dve", sh[:, 0:n], dst, dst, ALU.mult, [tdst], [tsh])
                pb2, tpb2 = self.bank()
                self.mm(pb2[:, 0:n], self.ones_1[:], sh[:, 0:n], [tsh, self.tC], [tpb2])
                return pb2, tpb2

            def nstage2(item, pb2, tpb2):
                kind, dst, tdst, n = item
                ri, tri_ = rin[self.rr("rin", [0, 1])]
                self.act(ri[:, 0:n], pb2[:, 0:n], AF.Ln, [tpb2], [tri_], bias=EPS)
                self.act(ri[:, 0:n], ri[:, 0:n], AF.Exp, [tri_], [tri_], scale=-0.5, bias=(self.lnq[:, 0:1] if kind == 0 else None))
                self.tt("dve", dst, dst, ri[:, 0:n], ALU.mult, [tdst, tri_], [tdst])
            inflight = []
            for item in qk_list:
                inflight.append((item,) + nstage1(item))
                if len(inflight) > 2:
                    nstage2(*inflight.pop(0))
            while inflight:
                nstage2(*inflight.pop(0))
            qk_list = []
        if hasS:
            for fc in range(24):
                pb, tpb = self.bank()
                self.tr(pb[0:48, 0:128], SHG[:, fc, :], self.idf[:, :], [tSHG, self.tC], [tpb])
                self.cp(self.rr("ev", ["act", "dve"]), stg[0:48, fc * 128:(fc + 1) * 128], pb[0:48, 0:128], [tpb], [tstg])
            self.dma("sp", self.o_sconv, stg, [tstg], (), "stst")
        if st.last:
            stg2, tstg2 = self.A([3, 3072], F32)
            for fc in range(24):
                pb, tpb = self.bank()
                self.tr(pb[0:3, 0:128], self.HG[:, fc, :], self.idf[:, :], [self.tok("HG"), self.tC], [tpb])
                self.cp(self.rr("ev", ["act", "dve"]), stg2[0:3, fc * 128:(fc + 1) * 128], pb[0:3, 0:128], [tpb], [tstg2])
            self.dma("sp", self.o_pconv, stg2, [tstg2], (), "stst")

        self.P.barrier()
        self.aoff = mark
        ONT = xn
        self.bfree = list(range(8))
        NA, NC_, NB = 3, (3 if hasS else 4), 1
        self.want_onacc = hasS
        TAs = [self.alloc_chunk_bufs("A") for _ in range(NA)]
        CAs = [self.alloc_chunk_bufs("C") for _ in range(NC_)]
        TBs = [self.alloc_chunk_bufs("B") for _ in range(NB)]
        if hasS:
            self.SS32 = [self.A([128, 8, 128], F32) for _ in range(2)]
            self.SS16 = [self.A([128, 8, 128], BF16) for _ in range(2)]
        jobs = []
        for seg in st.segs:
            if seg.kind == "P":
                c = 0
                if seg.pos0 == 0:
                    jobs.append(("P", seg.off, NMETA, None))
                    c = NMETA
                while c < seg.n:
                    jobs.append(("P", seg.off + c, 64, None))
                    c += 64
            else:
                for b_ in range(2):
                    jobs.append(("SB", seg.off + 64 * b_, 64, 8 * b_))
        N = len(jobs)
        nextA = 0
        nextB = 0
        doneA = set()
        actA = {}
        actB = None
        while nextB < N:
            for slot in range(NA):
                if slot not in actA and nextA < N and nextA < nextB + NC_:
                    actA[slot] = (nextA, self.chunk_A(jobs[nextA], QKVZ, GB, tGB, TAs[slot], CAs[nextA % NC_]))
                    nextA += 1
            if actB is None and nextB in doneA:
                if jobs[nextB][0] == "SB":
                    actB = self.chunk_B_sample(jobs[nextB], CAs[nextB % NC_], TBs[nextB % NB], QKVZ, ONT)
                else:
                    actB = self.chunk_B(jobs[nextB], CAs[nextB % NC_], TBs[nextB % NB], QKVZ, ONT)
            if actB is not None:
                try:
                    next(actB)
                except StopIteration:
                    actB = None
                    nextB += 1
            for slot in list(actA):
                j, g = actA[slot]
                try:
                    next(g)
                except StopIteration:
                    doneA.add(j)
                    del actA[slot]
        if st.last:
            self.dma("sp", self.o_prec.rearrange("h k v -> k h v"), self.S32[:], [self.tok("S32")], (), "strec")

        self.P.barrier()
        self.aoff = mark
        wo, two = self.A([128, 8, D], BF16)
        self.dma("pool", wo[:, :, 0:512], self.w_out[:, 0:512].rearrange("(kc p) f -> p kc f", p=128), (), [two], "ldw0")
        self.dma("pool", wo[:, :, 512:1024], self.w_out[:, 512:1024].rearrange("(kc p) f -> p kc f", p=128), (), [two], "ldw1")
        for tile in st.tiles():
            seg, t0, n = tile
            c0 = seg.off + t0
            for dc in range(8):
                pb, tpb = self.bank()
                for kc in range(8):
                    self.mm(pb[:, 0:n], wo[:, kc, dc * 128:(dc + 1) * 128], ONT[:, kc, c0:c0 + n], [two], [tpb], start=(kc == 0), stop=(kc == 7))
                xv = self.xT[:, dc, c0:c0 + n]
                self.tt("dve", xv, pb[:, 0:n], xv, ALU.add, [tpb], [self.xtok(dc, tile)])

    def alloc_chunk_bufs(self, which):
        b = {}
        def a(name, shape, dt):
            b[name] = self.A(shape, dt)
        if which == "A":
            a("gbt", [64, 40], F32)
            for nm in ("Gt", "eG", "nbG", "nb", "dGl", "eGl"):
                a(nm, [64, 8], F32)
            a("Dm", [64, 512], F32); a("Du", [64, 512], F32); a("Dl", [64, 512], F32); a("eGbc", [128, 512], F32)
            b["rhsG"] = b["Dl"]
            a("Lneg", [64, 512], BF16); a("M0", [64, 512], BF16)
            a("QTa", [64, 512], BF16); a("QTb", [64, 512], BF16)
            a("kbgn", [64, 1024], BF16)
        elif which == "C":
            a("PQa", [64, 1024], BF16); a("PQb", [64, 1024], BF16); a("At", [64, 512], BF16)
            a("kd", [64, 1024], BF16); a("vb", [64, 1024], BF16)
            a("nWT", [128, 512], BF16); a("qdT", [128, 512], BF16); a("gtc", [128, 64], F32)
        else:
            a("vn", [64, 1024], BF16); a("sqo", [64, 1024], BF16); a("on", [64, 1024], BF16)
            a("Stmp", [128, 1024], F32); a("ss", [64, 8], F32); a("rs", [64, 8], F32)
            if self.want_onacc:
                a("onacc", [64, 1024], F32)
        return b

    def bacq(self):
        if not self.bfree:
            raise RuntimeError("out of PSUM banks")
        i = self.bfree.pop(0)
        return self.banks[i], self.tok("bank", i), i

    def brel(self, i):
        self.bfree.append(i)

    def chunk_A(self, job, QKVZ, GB, tGB, TA, CA):
        kind, c0, L, sidx = job
        B = dict(TA); B.update(CA)
        tC = self.tC
        Q = lambda h: QKVZ[:, h, c0:c0 + L]
        K = lambda h: QKVZ[:, 8 + h, c0:c0 + L]
        Vv = lambda h: QKVZ[:, 16 + h, c0:c0 + L]
        h3 = lambda ap: ap.rearrange("p (h l) -> p h l", l=L)
        hd = lambda ap: ap.rearrange("p (h d) -> p h d", d=128)
        W8 = 8 * L
        pg, tpg, ipg = self.bacq()
        self.tr(pg[0:L, 0:40], GB[0:40, c0:c0 + L], self.idf[0:40, 0:40], [tGB, tC], [tpg])
        gbt, tgbt = B["gbt"]
        self.cp("dve", gbt[0:L, :], pg[0:L, 0:40], [tpg], [tgbt])
        self.brel(ipg)
        g_tm = gbt[0:L, 0:8]
        beta = gbt[0:L, 32:40]
        blk = (kind == "SB")
        cT, cN, cP = (C_TRI8, C_NEGU8, C_POSL8) if blk else (C_TRI, C_NEGU, C_POSL)
        tri = self.cst[0:L, cT:cT + L]
        yield
        rhsG, trG = B["rhsG"]
        self.tt("pool", h3(rhsG[0:L, 0:W8]), tri.unsqueeze(1).to_broadcast([L, 8, L]), g_tm.unsqueeze(2).to_broadcast([L, 8, L]),
                ALU.mult, [tgbt, tC], [trG])
        pg2, tpg2, ipg2 = self.bacq()
        self.mm(pg2[0:L, 0:8], tri, g_tm, [tgbt, tC], [tpg2])
        Gt, tGt = B["Gt"]; eG, teG = B["eG"]; nbG, tnbG = B["nbG"]; nb, tnb = B["nb"]
        dGl, tdGl = B["dGl"]; eGl, teGl = B["eGl"]; gtc, tgtc = B["gtc"]
        self.cp("dve", Gt[0:L, :], pg2[0:L, 0:8], [tpg2], [tGt])
        self.brel(ipg2)
        self.ts("dve", nb[0:L, :], beta, -1.0, ALU.mult, [tgbt], [tnb])
        yield
        pG, tpG, ipG = self.bacq()
        self.mm(pG[:, 0:W8], self.ones_f[0:L, :], rhsG[0:L, 0:W8], [trG, tC], [tpG])
        self.act(eG[0:L, :], Gt[0:L, :], AF.Exp, [tGt], [teG])
        self.tt("dve", nbG[0:L, :], eG[0:L, :], nb[0:L, :], ALU.mult, [teG, tnb], [tnbG])
        yield
        if blk:
            Glast = None
            gl4 = pG[:, 0:W8].rearrange("p (h s t) -> p h s t", s=8, t=8)[:, :, :, 7]
        else:
            Glast = h3(pG[:, 0:W8])[:, :, L - 1]
        Dm, tDm = B["Dm"]; Du, tDu = B["Du"]; Dl, tDl = B["Dl"]; eGbc, teGbc = B["eGbc"]
        self.tt("dve", h3(Dm[0:L, 0:W8]), h3(pG[0:L, 0:W8]), Gt[0:L, :].unsqueeze(2).to_broadcast([L, 8, L]), ALU.subtract,
                [tpG, tGt], [tDm])
        if blk:
            pgl, tpgl, ipgl = self.bacq()
            self.mm(pgl[0:L, 0:8], self.cst[0:L, C_SEL:C_SEL + L], Gt[0:L, :], [tGt, tC], [tpgl])
            self.tt("dve", dGl[0:L, :], pgl[0:L, 0:8], Gt[0:L, :], ALU.subtract, [tpgl, tGt], [tdGl])
            self.brel(ipgl)
            self.act(gtc[:, 0:64].rearrange("p (h s) -> p h s", s=8), gl4, AF.Exp, [tpG], [tgtc])
        else:
            self.tt("dve", dGl[0:L, :], Glast[0:L], Gt[0:L, :], ALU.subtract, [tpG, tGt], [tdGl])
            self.act(gtc[:, 0:8], Glast, AF.Exp, [tpG], [tgtc])
        self.act(eGbc[:, 0:W8], pG[:, 0:W8], AF.Exp, [tpG], [teGbc])
        self.brel(ipG)
        pk, tpk, ipk = self.bacq()
        pkb = pk[:].bitcast(BF16)
        for h in range(8):
            self.tr(pkb[0:L, h * 128:(h + 1) * 128], K(h), self.idb[:], [tC], [tpk])
        pv, tpv, ipv = self.bacq()
        pvb = pv[:].bitcast(BF16)
        for h in range(8):
            self.tr(pvb[0:L, h * 128:(h + 1) * 128], Vv(h), self.idb[:], [tC], [tpv])
        yield
        self.act(eGl[0:L, :], dGl[0:L, :], AF.Exp, [tdGl], [teGl])
        negu = self.cst[0:L, cN:cN + L].unsqueeze(1).to_broadcast([L, 8, L])
        posl = self.cst[0:L, cP:cP + L].unsqueeze(1).to_broadcast([L, 8, L])
        self.tt("pool", h3(Du[0:L, 0:W8]), h3(Dm[0:L, 0:W8]), negu, ALU.add, [tDm, tC], [tDu])
        self.tt("pool", h3(Dl[0:L, 0:W8]), h3(Dm[0:L, 0:W8]), posl, ALU.add, [tDm, tC], [tDl])
        kbgn, tkb = B["kbgn"]; kd, tkd = B["kd"]; vb, tvb = B["vb"]
        self.tt("dve", hd(kbgn[0:L, :]), hd(pkb[0:L, :]), nbG[0:L, :].unsqueeze(2).to_broadcast([L, 8, 128]), ALU.mult, [tpk, tnbG], [tkb])
        self.tt("dve", hd(vb[0:L, :]), hd(pvb[0:L, :]), beta.unsqueeze(2).to_broadcast([L, 8, 128]), ALU.mult, [tpv, tgbt], [tvb])
        self.brel(ipv)
        yield
        self.tt("dve", hd(kd[0:L, :]), hd(pkb[0:L, :]), eGl[0:L, :].unsqueeze(2).to_broadcast([L, 8, 128]), ALU.mult, [tpk, teGl], [tkd])
        self.brel(ipk)
        self.act(Du[0:L, 0:W8], Du[0:L, 0:W8], AF.Exp, [tDu], [tDu])
        self.act(Dl[0:L, 0:W8], Dl[0:L, 0:W8], AF.Exp, [tDl], [tDl], scale=-1.0)
        pkk, tpkk, ipkk = self.bacq()
        for h in range(8):
            self.mm(pkk[0:L, h * L:(h + 1) * L], K(h), K(h), [], [tpkk])
        pkq, tpkq, ipkq = self.bacq()
        for h in range(8):
            self.mm(pkq[0:L, h * L:(h + 1) * L], K(h), Q(h), [], [tpkq])
        qdT, tqd = B["qdT"]
        self.tt("pool", h3(qdT[:, 0:W8]), QKVZ[:, 0:8, c0:c0 + L], h3(eGbc[:, 0:W8]), ALU.mult, [teGbc], [tqd])
        yield
        self.tt("pool", h3(Dl[0:L, 0:W8]), h3(Dl[0:L, 0:W8]), nb[0:L, :].unsqueeze(2).to_broadcast([L, 8, L]), ALU.mult,
                [tDl, tnb], [tDl])
        Lneg, tLn = B["Lneg"]; At, tAt = B["At"]; M0, tM0 = B["M0"]
        self.tt("dve", At[0:L, 0:W8], pkq[0:L, 0:W8], Du[0:L, 0:W8], ALU.mult, [tpkq, tDu], [tAt])
        self.brel(ipkq)
        yield
        self.tt("dve", Lneg[0:L, 0:W8], pkk[0:L, 0:W8], Dl[0:L, 0:W8], ALU.mult, [tpkk, tDl], [tLn])
        self.brel(ipkk)
        yield
        pm, tpm, ipm = self.bacq()
        pmb = pm[:].bitcast(BF16)
        for h in range(8):
            self.tr(pmb[0:L, h * L:(h + 1) * L], Lneg[0:L, h * L:(h + 1) * L], self.idb[0:L, 0:L], [tLn, tC], [tpm])
        self.cp("act", M0[0:L, 0:W8], pmb[0:L, 0:W8], [tpm], [tM0])
        self.brel(ipm)
        yield
        nlev = 3 if blk else {64: 6, 16: 4, 8: 3}[L]
        idbL = self.idb[0:L, 0:L]
        PQ = [B["PQa"], B["PQb"]]
        QTbufs = [B["QTa"], B["QTb"]]
        pq3 = lambda ap: ap[0:L, :].rearrange("p (h c) -> p h c", c=128)
        cur = 0
        Pc, tPc = PQ[cur]
        self.tt("pool", pq3(Pc)[:, :, 0:L], h3(M0[0:L, 0:W8]), idbL.unsqueeze(1).to_broadcast([L, 8, L]), ALU.add, [tM0, tC], [tPc])
        pq, tpq, ipq = self.bacq()
        for h in range(8):
            sl = slice(h * L, (h + 1) * L)
            self.mm(pq[0:L, sl], M0[0:L, sl], Lneg[0:L, sl], [tM0, tLn], [tpq])
        QTc = QTbufs[0]
        self.cp("act", QTc[0][0:L, 0:W8], pq[0:L, 0:W8], [tpq], [QTc[1]])
        self.brel(ipq)
        pq2, tpq2, ipq2 = self.bacq()
        for h in range(8):
            sl = slice(h * L, (h + 1) * L)
            self.mm(pq2[0:L, sl], Lneg[0:L, sl], M0[0:L, sl], [tM0, tLn], [tpq2])
        self.cp("dve", pq3(Pc)[:, :, L:2 * L], h3(pq2[0:L, 0:W8]), [tpq2], [tPc])
        self.brel(ipq2)
        yield
        for k in range(1, nlev):
            last = (k == nlev - 1)
            Pn, tPn = PQ[1 - cur]
            wid = L if last else 2 * L
            if not last:
                QTn = QTbufs[k % 2]
                pq, tpq, ipq = self.bacq()
                for h in range(8):
                    self.mm(pq[0:L, h * L:(h + 1) * L], pq3(Pc)[:, h, L:2 * L], QTc[0][0:L, h * L:(h + 1) * L], [tPc, QTc[1]], [tpq])
                self.cp("act", QTn[0][0:L, 0:W8], pq[0:L, 0:W8], [tpq], [QTn[1]])
                self.brel(ipq)
            for half in range(2):
                pp, tpp, ipp = self.bacq()
                for hh in range(4):
                    h = half * 4 + hh
                    self.mm(pp[0:L, hh * 128:hh * 128 + wid], QTc[0][0:L, h * L:(h + 1) * L], pq3(Pc)[:, h, 0:wid], [tPc, QTc[1]], [tpp])
                ppv = pp[0:L, :].rearrange("p (h c) -> p h c", c=128)
                hs = slice(half * 4, half * 4 + 4)
                self.tt("dve", pq3(Pn)[:, hs, 0:L], ppv[:, :, 0:L], pq3(Pc)[:, hs, 0:L], ALU.add, [tpp, tPc], [tPn])
                if not last:
                    self.cp("act", pq3(Pn)[:, hs, L:2 * L], ppv[:, :, L:2 * L], [tpp], [tPn])
                self.brel(ipp)
            cur = 1 - cur
            Pc, tPc = PQ[cur]
            if not last:
                QTc = QTn
            yield
        Ttv = pq3(Pc)
        tTt = tPc
        pw, tpw, ipw = self.bacq()
        for h in range(8):
            self.mm(pw[:, h * L:(h + 1) * L], kbgn[0:L, h * 128:(h + 1) * 128], Ttv[:, h, 0:L], [tkb, tTt], [tpw])
        nWT, tnW = B["nWT"]
        self.cp("act", nWT[:, 0:W8], pw[:, 0:W8], [tpw], [tnW])
        self.brel(ipw)
        CA["Tt"] = (Ttv, tTt)
        yield

    def chunk_B(self, job, CA, TB, QKVZ, ONT):
        kind, c0, L, sidx = job
        B = dict(TB); B.update(CA)
        tC = self.tC
        Tt, tTt = CA["Tt"]
        h3 = lambda ap: ap.rearrange("p (h l) -> p h l", l=L)
        hd = lambda ap: ap.rearrange("p (h d) -> p h d", d=128)
        W8 = 8 * L
        if kind == "S":
            S32, tS32 = self.SS32[sidx % 2]
            S16, tS16 = self.SS16[sidx % 2]
            self.dma("sp", S32, self.st_rec[sidx].rearrange("h k v -> k h v"), (), [tS32], "ldrec%d" % (sidx % 2))
            self.cp("pool", S16, S32, [tS32], [tS16])
        else:
            S32, tS32 = self.S32[:], self.tok("S32")
            S16, tS16 = self.S16[:], self.tok("S16")
        vb, tvb = B["vb"]; nWT, tnW = B["nWT"]; vn, tvn = B["vn"]; qdT, tqd = B["qdT"]; At, tAt = B["At"]
        kd, tkd = B["kd"]; sqo, tsq = B["sqo"]; on, ton = B["on"]; ss, tss = B["ss"]; rs, trs = B["rs"]
        gtc, tgtc = B["gtc"]; Stmp, tSt = B["Stmp"]
        for half in range(2):
            pv, tpv, ipv = self.bacq()
            for hh in range(4):
                h = half * 4 + hh
                self.mm(pv[0:L, hh * 128:(hh + 1) * 128], Tt[:, h, 0:L], vb[0:L, h * 128:(h + 1) * 128], [tTt, tvb], [tpv], start=(hh == 0), stop=False, skip=True)
            for hh in range(4):
                h = half * 4 + hh
                self.mm(pv[0:L, hh * 128:(hh + 1) * 128], nWT[:, h * L:(h + 1) * L], S16[:, h, :], [tnW, tS16], [tpv], start=False, stop=True, skip=True)
            self.cp(("act", "dve")[half], vn[0:L, half * 512:(half + 1) * 512], pv[0:L, :], [tpv], [tvn])
            self.brel(ipv)
        self.tt("pool", hd(Stmp[:, :]), S32, gtc[:, 0:8].unsqueeze(2).to_broadcast([128, 8, 128]), ALU.mult, [tS32, tgtc], [tSt])
        yield
        pss = []
        for half in range(2):
            pS, tpS, ipS = self.bacq()
            pss.append((pS, tpS, ipS))
            for hh in range(4):
                h = half * 4 + hh
                self.mm(pS[:, hh * 128:(hh + 1) * 128], kd[0:L, h * 128:(h + 1) * 128], vn[0:L, h * 128:(h + 1) * 128], [tkd, tvn], [tpS])
        for half in range(2):
            pS, tpS, ipS = pss[half]
            self.tt("dve", S32[:, half * 4:(half + 1) * 4, :], hd(pS[:, :]), hd(Stmp[:, half * 512:(half + 1) * 512]), ALU.add, [tpS, tSt], [tS32])
            self.brel(ipS)
        pos = []
        for half in range(2):
            po, tpo, ipo = self.bacq()
            pos.append((po, tpo, ipo))
            for hh in range(4):
                h = half * 4 + hh
                self.mm(po[0:L, hh * 128:(hh + 1) * 128], qdT[:, h * L:(h + 1) * L], S16[:, h, :], [tqd, tS16], [tpo], start=(hh == 0), stop=False, skip=True)
            for hh in range(4):
                h = half * 4 + hh
                self.mm(po[0:L, hh * 128:(hh + 1) * 128], At[0:L, h * L:(h + 1) * L], vn[0:L, h * 128:(h + 1) * 128], [tAt, tvn], [tpo], start=False, stop=True, skip=True)
        self.cp("act", S16, S32, [tS32], [tS16])
        if kind == "S":
            self.dma("sp", self.o_srec[sidx].rearrange("h k v -> k h v"), S32, [tS32], (), "strec%d" % (sidx % 2))
        yield
        for half in range(2):
            po, tpo, ipo = pos[half]
            self.act(sqo[0:L, half * 512:(half + 1) * 512], po[0:L, :], AF.Square, [tpo], [tsq])
        self.P.op("dve", lambda e, o=ss[0:L, :], i=hd(sqo[0:L, :]): e.tensor_reduce(out=o, in_=i, axis=AX.X, op=ALU.add), [tsq], [tss])
        self.act(rs[0:L, :], ss[0:L, :], AF.Ln, [tss], [trs], bias=EPS, scale=1.0 / 128.0)
        self.act(rs[0:L, :], rs[0:L, :], AF.Exp, [trs], [trs], scale=-0.5)
        yield
        for half in range(2):
            po, tpo, ipo = pos[half]
            self.tt("dve", hd(on[0:L, half * 512:(half + 1) * 512]), hd(po[0:L, :]), rs[0:L, half * 4:(half + 1) * 4].unsqueeze(2).to_broadcast([L, 4, 128]),
                    ALU.mult, [tpo, trs], [ton])
            self.brel(ipo)
        yield
        pt, tpt, ipt = self.bacq()
        ptb = pt[:].bitcast(BF16)
        for h in range(8):
            self.tr(ptb[:, h * L:(h + 1) * L], on[0:L, h * 128:(h + 1) * 128], self.idb[0:L, 0:L], [ton, tC], [tpt])
        self.stt(ONT[:, :, c0:c0 + L], h3(ptb[:, 0:W8]), self.vcol("gnw"), QKVZ[:, 24:32, c0:c0 + L], ALU.mult, ALU.mult, [tpt, tC], [self.tok("ONT", self.phase_id)])
        self.brel(ipt)
        yield

    def chunk_B_sample(self, job, CA, TB, QKVZ, ONT):
        kind, c0, L, s0 = job
        B = dict(TB); B.update(CA)
        tC = self.tC
        Tt, tTt = CA["Tt"]
        h3 = lambda ap: ap.rearrange("p (h l) -> p h l", l=L)
        hd = lambda ap: ap.rearrange("p (h d) -> p h d", d=128)
        W8 = 8 * L
        vb, tvb = B["vb"]; nWT, tnW = B["nWT"]; vn, tvn = B["vn"]; qdT, tqd = B["qdT"]; At, tAt = B["At"]
        kd, tkd = B["kd"]; sqo, tsq = B["sqo"]; on, ton = B["on"]; ss, tss = B["ss"]; rs, trs = B["rs"]
        gtc, tgtc = B["gtc"]; Stmp, tSt = B["Stmp"]; onacc, tacc = B["onacc"]
        gtc3 = gtc[:, 0:64].rearrange("p (h s) -> p h s", s=8)
        for s_ in range(8):
            sidx = s0 + s_
            rm = self.cst[0:L, C_RM + s_:C_RM + s_ + 1]
            S32, tS32 = self.SS32[sidx % 2]
            S16, tS16 = self.SS16[sidx % 2]
            self.dma("sp", S32, self.st_rec[sidx].rearrange("h k v -> k h v"), (), [tS32], "ldrec%d" % (sidx % 2))
            self.cp("pool", S16, S32, [tS32], [tS16])
            self.tt("pool", hd(Stmp[:, :]), S32, gtc3[:, :, s_].unsqueeze(2).to_broadcast([128, 8, 128]), ALU.mult, [tS32, tgtc], [tSt])
            for half in range(2):
                pv, tpv, ipv = self.bacq()
                for hh in range(4):
                    h = half * 4 + hh
                    self.mm(pv[0:L, hh * 128:(hh + 1) * 128], Tt[:, h, 0:L], vb[0:L, h * 128:(h + 1) * 128], [tTt, tvb], [tpv], start=(hh == 0), stop=False, skip=True)
                for hh in range(4):
                    h = half * 4 + hh
                    self.mm(pv[0:L, hh * 128:(hh + 1) * 128], nWT[:, h * L:(h + 1) * L], S16[:, h, :], [tnW, tS16], [tpv], start=False, stop=True, skip=True)
                if half == 0:
                    self.act(vn[0:L, 0:512], pv[0:L, :], AF.Identity, [tpv, tC], [tvn], scale=rm)
                else:
                    self.ts("dve", vn[0:L, 512:1024], pv[0:L, :], rm, ALU.mult, [tpv, tC], [tvn])
                self.brel(ipv)
            yield
            pss = []
            for half in range(2):
                pS, tpS, ipS = self.bacq()
                pss.append((pS, tpS, ipS))
                for hh in range(4):
                    h = half * 4 + hh
                    self.mm(pS[:, hh * 128:(hh + 1) * 128], kd[0:L, h * 128:(h + 1) * 128], vn[0:L, h * 128:(h + 1) * 128], [tkd, tvn], [tpS])
            for half in range(2):
                pS, tpS, ipS = pss[half]
                self.tt("dve", S32[:, half * 4:(half + 1) * 4, :], hd(pS[:, :]), hd(Stmp[:, half * 512:(half + 1) * 512]), ALU.add, [tpS, tSt], [tS32])
                self.brel(ipS)
            self.dma("sp", self.o_srec[sidx].rearrange("h k v -> k h v"), S32, [tS32], (), "strec%d" % (sidx % 2))
            pos = []
            for half in range(2):
                po, tpo, ipo = self.bacq()
                pos.append((po, tpo, ipo))
                for hh in range(4):
                    h = half * 4 + hh
                    self.mm(po[0:L, hh * 128:(hh + 1) * 128], qdT[:, h * L:(h + 1) * L], S16[:, h, :], [tqd, tS16], [tpo], start=(hh == 0), stop=False, skip=True)
                for hh in range(4):
                    h = half * 4 + hh
                    self.mm(po[0:L, hh * 128:(hh + 1) * 128], At[0:L, h * L:(h + 1) * L], vn[0:L, h * 128:(h + 1) * 128], [tAt, tvn], [tpo], start=False, stop=True, skip=True)
            yield
            for half in range(2):
                po, tpo, ipo = pos[half]
                acc = onacc[0:L, half * 512:(half + 1) * 512]
                if s_ == 0:
                    self.ts("dve", acc, po[0:L, :], rm, ALU.mult, [tpo, tC], [tacc])
                else:
                    self.stt(acc, po[0:L, :], rm, acc, ALU.mult, ALU.add, [tpo, tC, tacc], [tacc])
                self.brel(ipo)
            yield
        self.act(sqo[0:L, :], onacc[0:L, :], AF.Square, [tacc], [tsq])
        self.P.op("dve", lambda e, o=ss[0:L, :], i=hd(sqo[0:L, :]): e.tensor_reduce(out=o, in_=i, axis=AX.X, op=ALU.add), [tsq], [tss])
        self.act(rs[0:L, :], ss[0:L, :], AF.Ln, [tss], [trs], bias=EPS, scale=1.0 / 128.0)
        self.act(rs[0:L, :], rs[0:L, :], AF.Exp, [trs], [trs], scale=-0.5)
        yield
        self.tt("dve", hd(on[0:L, :]), hd(onacc[0:L, :]), rs[0:L, :].unsqueeze(2).to_broadcast([L, 8, 128]), ALU.mult, [tacc, trs], [ton])
        yield
        pt, tpt, ipt = self.bacq()
        ptb = pt[:].bitcast(BF16)
        for h in range(8):
            self.tr(ptb[:, h * L:(h + 1) * L], on[0:L, h * 128:(h + 1) * 128], self.idb[0:L, 0:L], [ton, tC], [tpt])
        self.stt(ONT[:, :, c0:c0 + L], h3(ptb[:, 0:W8]), self.vcol("gnw"), QKVZ[:, 24:32, c0:c0 + L], ALU.mult, ALU.mult, [tpt, tC], [self.tok("ONT", self.phase_id)])
        self.brel(ipt)
        yield

    def ffn(self, st, l):
        NT = st.NT
        self.phase()
        hasS = any(s.kind == "S" for s in st.segs)
        xn, _ = self.A([128, 8, NT], BF16)
        hT, _ = self.A([128, 22, NT], BF16)
        sq2 = [self.A([128, 8, 512], BF16) for _ in range(2)]
        rsb = [self.A([128, 512], F32) for _ in range(2)]
        wsl = [self.A([128, 2, 8, 256], BF16) for _ in range(3)]
        wsl = [(w_, (t_, self.tok("wslb", self.phase_id, i_))) for i_, (w_, t_) in enumerate(wsl)]
        wdn = [self.A([128, 22, 128], BF16) for _ in range(3)]
        ub = [self.A([128, 2 + 512], F32) for _ in range(4)]
        t0b = [self.A([128, 512], F32) for _ in range(4)]
        sab = [self.A([128, 512], F32) for _ in range(2)]
        if hasS:
            SHF, tSHF = self.A([128, NFC, 32], F32)
            stg, tstg = self.A([32, 5632], F32)
        xnt = lambda dc, tile: (xn[:, dc, tile[0].off + tile[1]:tile[0].off + tile[1] + tile[2]], self.tok("xn", self.phase_id, dc, tile[0].off + tile[1]))
        self.norm(st, "nf%d" % l, xnt, sq2, rsb)
        if hasS:
            self.dma("sp", stg, self.st_ffn[l], (), [tstg], "ldst")
            for g in range(0, NFC, 8):
                pb, tpb = self.bank()
                ng = min(8, NFC - g)
                for j in range(ng):
                    fc = g + j
                    self.tr(pb[:, j * 32:(j + 1) * 32], stg[0:32, fc * 128:(fc + 1) * 128], self.idf[0:32, 0:32], [tstg, self.tC], [tpb])
                self.cp("act", SHF[:, g:g + ng, :], pb[:, 0:ng * 32].rearrange("p (a b) -> p a b", b=32), [tpb], [tSHF])
        tHF = self.tok("HF")
        fcw, fcb = "fcw%d" % l, "fcb%d" % l
        def ld_wup(u):
            wt_, twt_ = wsl[u % 3]
            self.dma("pool", wt_[:, 0], self.w_up[l][:, u * 256:(u + 1) * 256].rearrange("(kc p) f -> p kc f", p=128), (), [twt_[0]], "ldw%d" % (u % 3))
            self.dma("pool", wt_[:, 1], self.w_up[l][:, DFF + u * 256:DFF + (u + 1) * 256].rearrange("(kc p) f -> p kc f", p=128), (), [twt_[1]], "ldwb%d" % (u % 3))

        def ld_wdn(dc):
            wd_, twd_ = wdn[dc % 3]
            self.dma("pool", wd_, self.w_down[l][:, dc * 128:(dc + 1) * 128].rearrange("(i p) d -> p i d", p=128), (), [twd_], "ldwd%d" % (dc % 3))
        ld_wup(0); ld_wup(1)
        pend = []
        for u in range(11):
            wt, twt = wsl[u % 3]
            if u + 2 < 11:
                ld_wup(u + 2)
            elif u + 2 == 11:
                ld_wdn(0)
            else:
                ld_wdn(1)
            for j in range(2):
                i = u * 2 + j
                prev = [None, None]
                for tile in st.tiles():
                    seg, t0, n = tile
                    c0 = seg.off + t0
                    conv = []
                    for ab in range(2):
                        fc = i + 22 * ab
                        pb, tpb = self.bank()
                        for kc in range(8):
                            self.mm(pb[:, 0:n], wt[:, ab, kc, j * 128:(j + 1) * 128], xn[:, kc, c0:c0 + n], [twt[ab], xnt(kc, tile)[1]], [tpb],
                                    start=(kc == 0), stop=(kc == 7))
                        ex, tex = ub[self.rr("ub", [0, 1, 2, 3])]
                        if seg.kind == "S":
                            self.cp("pool", self.ext_halo(seg, ex, 2), SHF[:, fc, :].rearrange("p (s r) -> p s r", r=2), [tSHF], [tex])
                        elif t0 == 0:
                            self.cp("pool", ex[:, 0:2], self.HF[:, l, fc, :], [tHF], [tex])
                        else:
                            pe_, tpe_, pn = prev[ab]
                            self.cp("pool", ex[:, 0:2], pe_[:, pn:pn + 2], [tpe_], [tex])
                        self.cp("act", self.ext_dst(seg, ex, 2, n), self.V(seg, pb[:, 0:n]), [tpb], [tex])
                        prev[ab] = (ex, tex, n)
                        if seg.kind == "S":
                            self.cp("pool", SHF[:, fc, :].rearrange("p (s r) -> p s r", r=2), self.ext_tail(seg, ex, 2, n), [tex], [tSHF])
                        elif t0 + n == seg.n:
                            self.cp("pool", self.HF[:, l, fc, :], ex[:, n:n + 2], [tex], [tHF])
                        tb, ttb = t0b[self.rr("t0b", [0, 1, 2, 3])]
                        tv = self.V(seg, tb[:, 0:n])
                        self.act(tv, self.V(seg, pb[:, 0:n]), AF.Identity, [tpb, self.tC], [ttb], bias=self.vcol(fcb, fc), scale=self.vcol(fcw, 2 * NFC + fc))
                        self.stt(tv, self.ext_tap(seg, ex, 2, 1, n), self.vcol(fcw, 1 * NFC + fc), tv, ALU.mult, ALU.add, [tex, ttb, self.tC], [ttb])
                        self.stt(tv, self.ext_tap(seg, ex, 2, 0, n), self.vcol(fcw, 0 * NFC + fc), tv, ALU.mult, ALU.add, [tex, ttb, self.tC], [ttb])
                        conv.append((tb, ttb))
                    def tail(conv=conv, i=i, c0=c0, n=n):
                        sa, tsa = sab[self.rr("sab", [0, 1])]
                        self.act(sa[:, 0:n], conv[0][0][:, 0:n], AF.Silu, [conv[0][1]], [tsa])
                        self.tt("dve", hT[:, i, c0:c0 + n], sa[:, 0:n], conv[1][0][:, 0:n], ALU.mult, [tsa, conv[1][1]], [self.tok("hT", self.phase_id, i, c0)])
                    if pend:
                        pend.pop(0)()
                    pend.append(tail)
        while pend:
            pend.pop(0)()
        if hasS:
            for fc in range(NFC):
                pb, tpb = self.bank()
                self.tr(pb[0:32, 0:128], SHF[:, fc, :], self.idf[:, :], [tSHF, self.tC], [tpb])
                self.cp(self.rr("ev", ["act", "dve"]), stg[0:32, fc * 128:(fc + 1) * 128], pb[0:32, 0:128], [tpb], [tstg])
            self.dma("sp", self.o_sffn[l], stg, [tstg], (), "stst")
        if st.last:
            stg2, tstg2 = self.A([2, 5632], F32)
            for fc in range(NFC):
                pb, tpb = self.bank()
                self.tr(pb[0:2, 0:128], self.HF[:, l, fc, :], self.idf[:, :], [tHF, self.tC], [tpb])
                self.cp(self.rr("ev", ["act", "dve"]), stg2[0:2, fc * 128:(fc + 1) * 128], pb[0:2, 0:128], [tpb], [tstg2])
            self.dma("sp", self.o_pffn[l], stg2, [tstg2], (), "stst")
        for dc in range(8):
            wd, twd = wdn[dc % 3]
            if dc + 2 < 8:
                ld_wdn(dc + 2)
            for tile in st.tiles():
                seg, t0, n = tile
                c0 = seg.off + t0
                pb, tpb = self.bank()
                for i in range(22):
                    self.mm(pb[:, 0:n], wd[:, i, :], hT[:, i, c0:c0 + n], [twd, self.tok("hT", self.phase_id, i, c0)], [tpb], start=(i == 0), stop=(i == 21))
                xv = self.xT[:, dc, c0:c0 + n]
                self.tt("dve", xv, pb[:, 0:n], xv, ALU.add, [tpb], [self.xtok(dc, tile)])

    def pool_mixer(self, st):
        self.phase()
        sq2 = [self.A([128, 8, 512], BF16) for _ in range(2)]
        rsb = [self.A([128, 512], F32) for _ in range(2)]
        pw, tpw = self.A([128, 4, 2, 256], BF16)
        self.dma("pool", pw, self.pool_w.rearrange("g (ci p) e -> p g ci e", p=128), (), [tpw], "ldw0")
        segbuf = {}
        for seg in st.segs:
            W = 16 * 23 if seg.kind == "S" else 15 + seg.n
            hn, _ = self.A([128, 8, W], F32)
            s1, _ = self.A([128, 2, W], F32)
            s2, _ = self.A([128, 2, W], F32)
            PL, _ = self.A([128, 8, seg.n], BF16)
            segbuf[id(seg)] = (hn, s1, s2, PL, W)
        if any(s.kind == "S" for s in st.segs):
            stg, tstg = self.A([120, 2, D], F32)
            self._pcb = [self.A([128, 120], F32) for _ in range(2)]
        tmp15, ttmp15 = self.A([128, 15], F32)
        tHP = self.tok("HP")

        def dstf(dc, tile):
            seg, t0, n = tile
            hn = segbuf[id(seg)][0]
            if seg.kind == "S":
                ap = hn[:, dc, :].rearrange("p (s w) -> p s w", w=23)[:, :, 15:23]
            else:
                ap = hn[:, dc, 15 + t0:15 + t0 + n]
            return ap, self.tok("hn", self.phase_id, id(seg), dc)

        self._norm_pool(st, "nm1", dstf, sq2, rsb)
        for seg in st.segs:
            hn, s1, s2, PL, W = segbuf[id(seg)]
            n = seg.n
            if seg.kind == "S":
                for half in range(2):
                    self.dma("sp", stg[:, half, :], self.st_pool[half * 120:(half + 1) * 120, :], (), [tstg], "ldst")
                for dc in range(8):
                    pb, tpb = self.bank()
                    for half in range(2):
                        self.tr(pb[:, half * 120:(half + 1) * 120], stg[0:120, half, dc * 128:(dc + 1) * 128], self.idf[0:120, 0:120], [tstg, self.tC], [tpb])
                    self.cp(self.rr("ev", ["act", "dve"]), hn[:, dc, :].rearrange("p (s w) -> p s w", w=23)[:, :, 0:15],
                            pb[:, 0:240].rearrange("p (s r) -> p s r", r=15), [tpb], [self.tok("hn", self.phase_id, id(seg), dc)])
            else:
                for dc in range(8):
                    self.cp("pool", hn[:, dc, 0:15], self.HP[:, dc, :], [tHP], [self.tok("hn", self.phase_id, id(seg), dc)])
            if seg.kind == "S":
                e3 = lambda ap: ap.rearrange("p (s w) -> p s w", w=23)
                sl = lambda ap, a, b: e3(ap)[:, :, a:b]
                WW = 23
            else:
                sl = lambda ap, a, b: ap[:, a:b]
                WW = W
            for dc in range(8):
                gi = dc // 2
                th = self.tok("hn", self.phase_id, id(seg), dc)
                ts1 = self.tok("ps1", self.phase_id, id(seg), dc % 2)
                ts2 = self.tok("ps2", self.phase_id, id(seg), dc % 2)
                src, tsrc = hn[:, dc, :], th
                bufs = [(s1[:, dc % 2, :], ts1), (s2[:, dc % 2, :], ts2)]
                for lev in range(gi + 1):
                    sh = 1 << lev
                    lo = (1 << (lev + 1)) - 1
                    dstb, tdb = bufs[lev % 2]
                    self.tt("pool", sl(dstb, lo, WW), sl(src, lo, WW), sl(src, lo - sh, WW - sh), ALU.add, [tsrc], [tdb])
                    src, tsrc = dstb, tdb
                if seg.kind == "S":
                    outv = PL[:, dc, :].rearrange("p (s t) -> p s t", t=8)
                else:
                    outv = PL[:, dc, :]
                tPL = self.tok("PL", self.phase_id, id(seg), dc)
                self.stt(outv, sl(src, 15, WW), 1.0 / WINS[gi], sl(hn[:, dc, :], 15, WW), ALU.mult, ALU.subtract, [tsrc, th], [tPL])
                if seg.kind == "P" and seg.pos0 == 0:
                    ic = self.cst[:, C_INVC + gi * 15:C_INVC + gi * 15 + 15]
                    self.tt("dve", tmp15, src[:, 15:30], ic, ALU.mult, [tsrc, self.tC], [ttmp15])
                    self.tt("dve", PL[:, dc, 0:15], tmp15, hn[:, dc, 15:30], ALU.subtract, [ttmp15, th], [tPL])
            if seg.kind == "S":
                for dc in range(8):
                    th = self.tok("hn", self.phase_id, id(seg), dc)
                    for half in range(2):
                        pb, tpb = self.bank()
                        src3 = hn[:, dc, :].rearrange("p (s w) -> p s w", w=23)[:, half * 8:(half + 1) * 8, 8:23]
                        cbuf, tcb = self._pcb[self.rr("pcb", [0, 1])]
                        self.cp("pool", cbuf.rearrange("p (s r) -> p s r", r=15), src3, [th], [tcb])
                        self.tr(pb[0:120, 0:128], cbuf, self.idf[:, :], [tcb, self.tC], [tpb])
                        self.cp(self.rr("ev", ["act", "dve"]), stg[0:120, half, dc * 128:(dc + 1) * 128], pb[0:120, 0:128], [tpb], [tstg])
                for half in range(2):
                    self.dma("sp", self.o_spool[half * 120:(half + 1) * 120, :], stg[:, half, :], [tstg], (), "stst")
            else:
                for dc in range(8):
                    th = self.tok("hn", self.phase_id, id(seg), dc)
                    self.cp("pool", self.HP[:, dc, :], hn[:, dc, n:n + 15], [th], [tHP])
                if st.last:
                    stg2, tstg2 = self.A([15, D], F32)
                    for dc in range(8):
                        pb, tpb = self.bank()
                        self.tr(pb[0:15, 0:128], self.HP[:, dc, :], self.idf[:, :], [tHP, self.tC], [tpb])
                        self.cp(self.rr("ev", ["act", "dve"]), stg2[0:15, dc * 128:(dc + 1) * 128], pb[0:15, 0:128], [tpb], [tstg2])
                    self.dma("sp", self.o_ppool, stg2, [tstg2], (), "stst")
            for tile in seg.tiles():
                _, t0, nn = tile
                c0 = seg.off + t0
                for gi in range(4):
                    for eo in range(2):
                        dco = 2 * gi + eo
                        pb, tpb = self.bank()
                        for ci in range(2):
                            self.mm(pb[:, 0:nn], pw[:, gi, ci, eo * 128:(eo + 1) * 128], PL[:, 2 * gi + ci, t0:t0 + nn],
                                    [tpw, self.tok("PL", self.phase_id, id(seg), 2 * gi + ci)], [tpb], start=(ci == 0), stop=(ci == 1))
                        xv = self.xT[:, dco, c0:c0 + nn]
                        self.stt(xv, pb[:, 0:nn], self.vcol("psc", dco), xv, ALU.mult, ALU.add, [tpb, self.tC], [self.xtok(dco, tile)])

    def _norm_pool(self, st, wname, dstf, sq2, rsb):
        for tile in st.tiles():
            seg, t0, n = tile
            c0 = seg.off + t0
            sq, tsq = sq2[self.rr("sq2", [0, 1])]
            rs, trs = rsb[self.rr("rsb", [0, 1])]
            pb, tpb = self.bank()
            for dc in range(8):
                xin = self.xT[:, dc, c0:c0 + n]
                if dc % 2 == 0:
                    self.tt("pool", sq[:, dc, 0:n], xin, xin, ALU.mult, [self.xtok(dc, tile)], [tsq])
                else:
                    self.act(sq[:, dc, 0:n], xin, AF.Square, [self.xtok(dc, tile)], [tsq])
            for dc in range(8):
                self.mm(pb[:, 0:n], self.ones_m[:], sq[:, dc, 0:n], [tsq, self.tC], [tpb], start=(dc == 0), stop=(dc == 7))
            self.act(rs[:, 0:n], pb[:, 0:n], AF.Ln, [tpb], [trs], bias=EPS)
            self.act(rs[:, 0:n], rs[:, 0:n], AF.Exp, [trs], [trs], scale=-0.5)
            for dc in range(8):
                dst, tdst = dstf(dc, tile)
                self.stt(dst, self.V(seg, self.xT[:, dc, c0:c0 + n]), self.vcol(wname, dc), self.V(seg, rs[:, 0:n]), ALU.mult, ALU.mult,
                         [self.xtok(dc, tile), trs, self.tC], [tdst])

    def final(self, st):
        self.phase()
        sq2 = [self.A([128, 8, 512], BF16) for _ in range(2)]
        rsb = [self.A([128, 512], F32) for _ in range(2)]
        yT = [self.A([128, 8, 512], F32) for _ in range(2)]
        ysg = [self.A([128, D], F32) for _ in range(3)]
        cur = {}

        def dstf(dc, tile):
            return cur["y"][0][:, dc, 0:tile[2]], cur["y"][1]

        for tile in st.tiles():
            seg, t0, n = tile
            cur["y"] = yT[self.rr("yT", [0, 1])]
            self._norm_one(tile, "nfin", dstf, sq2, rsb)
            y, ty = cur["y"]
            b0 = 0
            while b0 < n:
                if seg.kind == "P":
                    pos = seg.pos0 + t0 + b0
                    if pos < NMETA:
                        b0 += NMETA - pos
                        continue
                m = min(128, n - b0)
                sg, tsg = ysg[self.rr("ysg", [0, 1, 2])]
                for half in range(2):
                    pb, tpb = self.bank()
                    for j in range(4):
                        dc = half * 4 + j
                        self.tr(pb[0:m, j * 128:(j + 1) * 128], y[:, dc, b0:b0 + m], self.idf[:, :], [ty, self.tC], [tpb])
                    self.cp(("act", "dve")[half], sg[0:m, half * 512:(half + 1) * 512], pb[0:m, :], [tpb], [tsg])
                if seg.kind == "S":
                    self.dma("sp", self.ys[b0:b0 + m, :], sg[0:m, :], [tsg], (), "sty%d" % ((self.rrc["ysg"] - 1) % 3))
                else:
                    r0 = seg.pos0 + t0 + b0 - NMETA
                    self.dma("sp", self.yp[r0:r0 + m, :], sg[0:m, :], [tsg], (), "sty%d" % ((self.rrc["ysg"] - 1) % 3))
                b0 += m

    def _norm_one(self, tile, wname, dstf, sq2, rsb):
        seg, t0, n = tile
        c0 = seg.off + t0
        sq, tsq = sq2[self.rr("sq2", [0, 1])]
        rs, trs = rsb[self.rr("rsb", [0, 1])]
        pb, tpb = self.bank()
        for dc in range(8):
            xin = self.xT[:, dc, c0:c0 + n]
            if dc % 2 == 0:
                self.tt("pool", sq[:, dc, 0:n], xin, xin, ALU.mult, [self.xtok(dc, tile)], [tsq])
            else:
                self.act(sq[:, dc, 0:n], xin, AF.Square, [self.xtok(dc, tile)], [tsq])
        for dc in range(8):
            self.mm(pb[:, 0:n], self.ones_m[:], sq[:, dc, 0:n], [tsq, self.tC], [tpb], start=(dc == 0), stop=(dc == 7))
        self.act(rs[:, 0:n], pb[:, 0:n], AF.Ln, [tpb], [trs], bias=EPS)
        self.act(rs[:, 0:n], rs[:, 0:n], AF.Exp, [trs], [trs], scale=-0.5)
        for dc in range(8):
            dst, tdst = dstf(dc, tile)
            self.stt(dst, self.xT[:, dc, c0:c0 + n], self.vcol(wname, dc), rs[:, 0:n], ALU.mult, ALU.mult,
                     [self.xtok(dc, tile), trs, self.tC], [tdst])


_NC_CACHE = {}


def _get_nc():
    if "nc" not in _NC_CACHE:
        b = Builder()
        _NC_CACHE["nc"] = b.build()
    return _NC_CACHE["nc"]


def kernel(**inp):
    inp = {k: np.asarray(v) for k, v in inp.items()}
    f = lambda a: np.ascontiguousarray(a, dtype=np.float32)
    nc = _get_nc()
    vecs = build_vecs(inp)
    consts = build_consts()
    w_in = f(inp["gdn_w_in"][0])
    wba = np.zeros((D, 40), np.float32)
    wba[:, 0:8] = w_in[:, 4104:4112]
    wba[:, 32:40] = w_in[:, 4096:4104]
    shared = {
        "meta": f(inp["meta_tokens"]), "w_in": w_in, "wba": wba, "w_out": f(inp["gdn_w_out"][0]),
        "pool_w": f(inp["pool_w"][0]), "w_up": f(inp["ffn_w_up"]), "w_down": f(inp["ffn_w_down"]),
        "vecs": vecs, "consts": consts,
    }
    in_maps = []
    for c in range(8):
        sl = slice(16 * c, 16 * c + 16)
        m = dict(shared)
        m["xp"] = f(inp["x_prompt"][c])
        m["xs"] = f(inp["x_sample"][sl].reshape(128, D))
        m["st_conv"] = f(inp["state_gdn_conv"][0, sl].reshape(48, 3072))
        m["st_rec"] = f(inp["state_gdn_rec"][0, sl])
        m["st_pool"] = f(inp["state_pool"][0, sl].reshape(240, D))
        m["st_ffn"] = f(inp["state_ffn_conv"][:, sl].reshape(2, 32, 5632))
        in_maps.append(m)
    res = run_bass_kernel_spmd(nc, in_maps, core_ids=list(range(8)))
    R = res.results
    g = lambda k: [np.asarray(r[k], dtype=np.float32) for r in R]
    y_prompt = np.stack(g("yp"), 0)
    y_sample = np.concatenate(g("ys"), 0).reshape(128, 8, D)
    p_conv = np.stack(g("o_pconv"), 0)[None]
    p_rec = np.stack(g("o_prec"), 0)[None]
    p_pool = np.stack(g("o_ppool"), 0)[None]
    p_ffn = np.stack(g("o_pffn"), 1)
    s_conv = np.concatenate([a.reshape(16, 3, 3072) for a in g("o_sconv")], 0)[None]
    s_rec = np.concatenate(g("o_srec"), 0)[None]
    s_pool = np.concatenate([a.reshape(16, 15, D) for a in g("o_spool")], 0)[None]
    s_ffn = np.concatenate([a.reshape(2, 16, 2, 5632) for a in g("o_sffn")], 1)
    return (y_prompt, y_sample, p_conv, p_rec, p_pool, p_ffn, s_conv, s_rec, s_pool, s_ffn)
```

```python
import contextlib
import numpy as np
import concourse.bass as bass
import concourse.mybir as mybir
from concourse.bass_utils import run_bass_kernel_spmd

F32 = mybir.dt.float32
BF16 = mybir.dt.bfloat16
ALU = mybir.AluOpType
AF = mybir.ActivationFunctionType
AX = mybir.AxisListType

D = 1024
NH = 8
DFF = 2816
NFC = 44
SEQ = 2048
NMETA = 16
EPS = 1e-6
NEG = -1.0e30
DEBUG_MAP = None
WINS = (2, 4, 8, 16)


class Tok:
    __slots__ = ("lastw", "readers", "excl")

    def __init__(self):
        self.lastw = None
        self.readers = []
        self.excl = False


class Op:
    __slots__ = ("eng", "fn", "deps", "ms", "dma_sem", "dma_val", "is_dma", "where")

    def __init__(self, eng, fn):
        import sys as _s
        f = _s._getframe(3)
        self.where = (f.f_lineno, f.f_back.f_lineno if f.f_back else 0)
        self.eng = eng
        self.fn = fn
        self.deps = []
        self.ms = None
        self.is_dma = False
        self.dma_sem = None
        self.dma_val = 0


class Prog:
    ENGS = ("pe", "act", "dve", "pool", "sp")

    def __init__(self, nc):
        self.nc = nc
        self.ops = {e: [] for e in self.ENGS}
        self.streams = {}
        self.pending = {}

    def barrier(self):
        lasts = [self.ops[e][-1] for e in self.ENGS if self.ops[e]]
        lasts += [st[0] for st in self.streams.values() if st[0] is not None]
        for e in self.ENGS:
            self.pending[e] = list(lasts)

    def op(self, eng, fn, reads=(), writes=(), stream=None):
        o = Op(eng, fn)
        is_dma = stream is not None
        deps = []
        for t in reads:
            if t.lastw is not None:
                deps.append((t.lastw, True))
            if t.excl:
                for r in t.readers:
                    if r.eng != eng:
                        deps.append((r, True))
        for t in writes:
            if t.lastw is not None:
                deps.append((t.lastw, False))
            for r in t.readers:
                deps.append((r, False))
        for d in self.pending.pop(eng, []):
            deps.append((d, True))
        if is_dma:
            o.is_dma = True
            st = self.streams.setdefault(stream, [None, 0])
            if st[0] is not None:
                deps.append((st[0], True))
            st[1] += 1
            o.dma_sem = stream
            o.dma_val = 16 * st[1]
            st[0] = o
        seen = set()
        for d, raw in deps:
            if d is o or id(d) in seen:
                continue
            if (not d.is_dma) and (not is_dma) and d.eng == eng and eng == "pe":
                continue
            seen.add(id(d))
            o.deps.append(d)
        for t in reads:
            t.readers.append(o)
        for t in writes:
            t.lastw = o
            t.readers = []
        self.ops[eng].append(o)
        return o

    def emit(self):
        nc = self.nc
        for e in self.ENGS:
            for o in self.ops[e]:
                for d in o.deps:
                    if not d.is_dma:
                        d.ms = True
        for e in self.ENGS:
            k = 0
            for o in self.ops[e]:
                if o.ms and not o.is_dma:
                    k += 1
                    o.ms = k
        with contextlib.ExitStack() as es:
            esem = {e: es.enter_context(nc.semaphore("s_" + e)) for e in self.ENGS}
            dsem = {k: es.enter_context(nc.semaphore("d_%d" % i)) for i, k in enumerate(self.streams)}
            block = es.enter_context(nc.Block())
            prog = self

            def run(e, engobj):
                seen = {}
                for o in prog.ops[e]:
                    for d in o.deps:
                        if d.is_dma:
                            key, val, sem = ("d", d.dma_sem), d.dma_val, dsem[d.dma_sem]
                        else:
                            key, val, sem = ("e", d.eng), d.ms, esem[d.eng]
                        if seen.get(key, 0) >= val:
                            continue
                        seen[key] = val
                        engobj.wait_ge(sem, val)
                    ins = o.fn(engobj)
                    if DEBUG_MAP is not None:
                        try:
                            DEBUG_MAP[str(ins.ins.name)] = o.where
                        except Exception as ex:
                            DEBUG_MAP["err"] = repr(ex)
                    if o.is_dma:
                        ins.then_inc(dsem[o.dma_sem], 16)
                    elif o.ms:
                        ins.then_inc(esem[e], 1)
                if e == "sp":
                    for k, st in prog.streams.items():
                        engobj.wait_ge(dsem[k], 16 * st[1])

            block.tensor(lambda eng: run("pe", eng))
            block.scalar(lambda eng: run("act", eng))
            block.vector(lambda eng: run("dve", eng))
            block.gpsimd(lambda eng: run("pool", eng))
            block.sync(lambda eng: run("sp", eng))


VEC_COLS = {}


def _vec_layout():
    off = 0
    for name, n in (("nm0", 8), ("nm1", 8), ("nf0", 8), ("nf1", 8), ("nfin", 8),
                    ("gcw", 96), ("fcw0", 132), ("fcw1", 132), ("fcb0", 44), ("fcb1", 44),
                    ("psc", 8), ("gnw", 1), ("alog", 1), ("dtb", 1)):
        VEC_COLS[name] = off
        off += n
    return off


NV = _vec_layout()
C_ID, C_TRI, C_NEGU, C_POSL, C_INVC = 0, 128, 192, 256, 320
C_TRI8, C_NEGU8, C_POSL8, C_SEL, C_RM = 380, 444, 508, 572, 636
NCONST = 636 + 8


def build_consts():
    c = np.zeros((128, NCONST), np.float32)
    c[:, C_ID:C_ID + 128] = np.eye(128, dtype=np.float32)
    p = np.arange(64)[:, None]
    f = np.arange(64)[None, :]
    c[:64, C_TRI:C_TRI + 64] = (f >= p).astype(np.float32)
    c[:64, C_NEGU:C_NEGU + 64] = np.where(f >= p, 0.0, NEG)
    c[:64, C_POSL:C_POSL + 64] = np.where(f < p, 0.0, -NEG)
    for gi, w in enumerate(WINS):
        for t in range(15):
            c[:, C_INVC + gi * 15 + t] = 1.0 / min(w, t + 1)
    same = (p // 8) == (f // 8)
    c[:64, C_TRI8:C_TRI8 + 64] = (same & (f >= p)).astype(np.float32)
    c[:64, C_NEGU8:C_NEGU8 + 64] = np.where(same & (f >= p), 0.0, NEG)
    c[:64, C_POSL8:C_POSL8 + 64] = np.where(same & (f < p), 0.0, -NEG)
    c[:64, C_SEL:C_SEL + 64] = (p == 8 * (f // 8) + 7).astype(np.float32)
    c[:64, C_RM:C_RM + 8] = ((p // 8) == np.arange(8)[None, :]).astype(np.float32)
    return c


def build_vecs(inp):
    v = np.zeros((128, NV), np.float32)

    def put(name, arr):
        a = np.asarray(arr, np.float32).reshape(-1, 128).T
        v[:, VEC_COLS[name]:VEC_COLS[name] + a.shape[1]] = a

    put("nm0", inp["norm_mix"][0]); put("nm1", inp["norm_mix"][1])
    put("nf0", inp["norm_ffn"][0]); put("nf1", inp["norm_ffn"][1])
    put("nfin", inp["norm_final"])
    put("gcw", inp["gdn_conv_w"][0].reshape(-1))
    put("fcw0", inp["ffn_conv_w"][0].reshape(-1)); put("fcw1", inp["ffn_conv_w"][1].reshape(-1))
    put("fcb0", inp["ffn_conv_b"][0]); put("fcb1", inp["ffn_conv_b"][1])
    put("psc", inp["pool_scale"][0])
    put("gnw", inp["gdn_norm_w"][0])
    v[0:8, VEC_COLS["alog"]] = inp["gdn_A_log"][0]
    v[0:8, VEC_COLS["dtb"]] = inp["gdn_dt_bias"][0]
    return v


class Seg:
    def __init__(self, kind, n, off, pos0=0):
        self.kind, self.n, self.off, self.pos0 = kind, n, off, pos0

    def tiles(self):
        if self.kind == "S":
            return [(self, 0, 128)]
        k = (self.n + 511) // 512
        base = (self.n // k + 7) // 8 * 8
        out, t = [], 0
        while t < self.n:
            m = min(base, self.n - t)
            out.append((self, t, m))
            t += m
        return out


class ST:
    def __init__(self, segs, first, last):
        self.segs, self.first, self.last = segs, first, last
        self.NT = sum(s.n for s in segs)

    def tiles(self):
        return [t for s in self.segs for t in s.tiles()]


SUPER = [
    ST([Seg("P", 592, 0, 0), Seg("S", 128, 592)], True, False),
    ST([Seg("P", 704, 0, 592)], False, False),
    ST([Seg("P", 768, 0, 1296)], False, True),
]
NTMAX = 768


class Builder:
    def __init__(self):
        self.nc = nc = bass.Bass("TRN2", target_bir_lowering=False)
        self.P = Prog(nc)
        self.es = contextlib.ExitStack()
        self.toks = {}
        self.rrc = {}
        self.phase_id = 0

        def din(name, shape):
            return nc.dram_tensor(name, list(shape), F32, kind="ExternalInput").ap()

        def dout(name, shape):
            return nc.dram_tensor(name, list(shape), F32, kind="ExternalOutput").ap()

        self.xp = din("xp", [SEQ, D]); self.xs = din("xs", [128, D])
        self.st_conv = din("st_conv", [48, 3072]); self.st_rec = din("st_rec", [16, 8, 128, 128])
        self.st_pool = din("st_pool", [240, D]); self.st_ffn = din("st_ffn", [2, 32, 5632])
        self.meta = din("meta", [NMETA, D])
        self.w_in = din("w_in", [D, 4112]); self.wba = din("wba", [D, 40])
        self.w_out = din("w_out", [D, D]); self.pool_w = din("pool_w", [4, 256, 256])
        self.w_up = din("w_up", [2, D, 5632]); self.w_down = din("w_down", [2, DFF, D])
        self.vecs_d = din("vecs", [128, NV]); self.consts_d = din("consts", [128, NCONST])
        self.yp = dout("yp", [SEQ, D]); self.ys = dout("ys", [128, D])
        self.o_pconv = dout("o_pconv", [3, 3072]); self.o_prec = dout("o_prec", [8, 128, 128])
        self.o_ppool = dout("o_ppool", [15, D]); self.o_pffn = dout("o_pffn", [2, 2, 5632])
        self.o_sconv = dout("o_sconv", [48, 3072]); self.o_srec = dout("o_srec", [16, 8, 128, 128])
        self.o_spool = dout("o_spool", [240, D]); self.o_sffn = dout("o_sffn", [2, 32, 5632])

    def tok(self, *key):
        t = self.toks.get(key)
        if t is None:
            t = self.toks[key] = Tok()
            if key[0] == "bank":
                t.excl = True
        return t

    def sb(self, name, shape, dt):
        return self.es.enter_context(self.nc.sbuf_tensor(name, list(shape), dt))

    def rr(self, name, choices):
        i = self.rrc.get(name, 0)
        self.rrc[name] = i + 1
        return choices[i % len(choices)]

    def bank(self):
        i = self.rrc.get("bank", 0)
        self.rrc["bank"] = i + 1
        i %= 8
        return self.banks[i], self.tok("bank", i)

    def phase(self):
        self.P.barrier()
        self.aoff = 0
        self.phase_id += 1

    def A(self, shape, dt, key=None):
        n = int(np.prod(shape[1:]))
        nb = n * (4 if dt == F32 else 2)
        nb = (nb + 31) // 32 * 32
        ne = nb // 2
        assert self.aoff + ne <= self.arena_n, ("arena overflow", self.aoff, ne, self.arena_n)
        ap = self.arena[0:shape[0], self.aoff:self.aoff + ne]
        self.aoff += ne
        if dt == F32:
            ap = ap.bitcast(F32)
        ap = ap[:, 0:n]
        if len(shape) == 3:
            ap = ap.rearrange("p (a b) -> p a b", b=shape[2])
        elif len(shape) == 4:
            ap = ap.rearrange("p (a b c) -> p a b c", b=shape[2], c=shape[3])
        return ap, self.tok("arena", self.phase_id, self.aoff)

    def mm(self, out, lhsT, rhs, r, w, start=True, stop=True, skip=False):
        if skip:
            self.P.op("pe", lambda e: e.matmul(out, lhsT=lhsT, rhs=rhs, start=start, stop=stop, skip_group_check=True), r, w)
        else:
            self.P.op("pe", lambda e: e.matmul(out, lhsT=lhsT, rhs=rhs, start=start, stop=stop), r, w)

    def tr(self, out, in_, ident, r, w):
        self.P.op("pe", lambda e: e.transpose(out=out, in_=in_, identity=ident), r, w)

    def act(self, out, in_, func, r, w, bias=None, scale=None):
        kw = {}
        if bias is not None:
            kw["bias"] = bias
        if scale is not None:
            kw["scale"] = scale
        self.P.op("act", lambda e: e.activation(out=out, in_=in_, func=func, **kw), r, w)

    def tt(self, eng, out, in0, in1, op, r, w):
        self.P.op(eng, lambda e: e.tensor_tensor(out=out, in0=in0, in1=in1, op=op), r, w)

    def ts(self, eng, out, in0, s1, op0, r, w, s2=None, op1=None):
        if op1 is None:
            self.P.op(eng, lambda e: e.tensor_scalar(out=out, in0=in0, scalar1=s1, scalar2=None, op0=op0), r, w)
        else:
            self.P.op(eng, lambda e: e.tensor_scalar(out=out, in0=in0, scalar1=s1, scalar2=s2, op0=op0, op1=op1), r, w)

    def stt(self, out, in0, scalar, in1, op0, op1, r, w):
        self.P.op("dve", lambda e: e.scalar_tensor_tensor(out=out, in0=in0, scalar=scalar, in1=in1, op0=op0, op1=op1), r, w)

    def cp(self, eng, out, in_, r, w):
        if eng == "act":
            self.act(out, in_, AF.Copy, r, w)
        else:
            self.P.op(eng, lambda e: e.tensor_copy(out=out, in_=in_), r, w)

    def dma(self, q, out, in_, r, w, stream):
        self.P.op(q, lambda e: e.dma_start(out=out, in_=in_), r, w, stream=stream)

    def memset(self, eng, ap, val, w):
        self.P.op(eng, lambda e: e.memset(ap, val), (), w)

    def vcol(self, name, j=0, np_=128):
        c = VEC_COLS[name] + j
        return self.vecs[0:np_, c:c + 1]

    @staticmethod
    def V(seg, ap):
        if seg.kind == "S":
            return ap.rearrange("p (s t) -> p s t", t=8)
        return ap

    @staticmethod
    def ext_dst(seg, buf, H, n):
        if seg.kind == "S":
            return buf[:, 0:16 * (H + 8)].rearrange("p (s w) -> p s w", w=H + 8)[:, :, H:H + 8]
        return buf[:, H:H + n]

    @staticmethod
    def ext_tap(seg, buf, H, j, n):
        if seg.kind == "S":
            return buf[:, 0:16 * (H + 8)].rearrange("p (s w) -> p s w", w=H + 8)[:, :, j:j + 8]
        return buf[:, j:j + n]

    @staticmethod
    def ext_halo(seg, buf, H):
        if seg.kind == "S":
            return buf[:, 0:16 * (H + 8)].rearrange("p (s w) -> p s w", w=H + 8)[:, :, 0:H]
        return buf[:, 0:H]

    @staticmethod
    def ext_tail(seg, buf, H, n):
        if seg.kind == "S":
            return buf[:, 0:16 * (H + 8)].rearrange("p (s w) -> p s w", w=H + 8)[:, :, 8:8 + H]
        return buf[:, n:n + H]

    def build(self):
        nc = self.nc
        with self.es:
            self.xT = self.sb("xT", [128, 8, NTMAX], F32)
            self.S32 = self.sb("S32", [128, 8, 128], F32)
            self.S16 = self.sb("S16", [128, 8, 128], BF16)
            self.HG = self.sb("HG", [128, 24, 3], F32)
            self.HF = self.sb("HF", [128, 2, NFC, 2], F32)
            self.HP = self.sb("HP", [128, 8, 15], F32)
            self.vecs = self.sb("vecs_sb", [128, NV], F32)
            self.cst = self.sb("cst", [128, NCONST], F32)
            self.idb = self.sb("idb", [128, 128], BF16)
            self.ones_m = self.sb("ones_m", [128, 128], BF16)
            self.ones_1 = self.sb("ones_1", [128, 128], BF16)
            self.ones_f = self.sb("ones_f", [64, 128], F32)
            self.nexpA = self.sb("nexpA", [8, 1], F32)
            self.lnq = self.sb("lnq", [128, 1], F32)
            self.banks = [self.es.enter_context(nc.psum_tensor("pb%d" % i, [128, 512], F32)) for i in range(8)]
            rem = nc.sbuf_bytes_remaining - 2048
            self.arena_n = (rem // 2) // 64 * 64
            self.arena = self.sb("arena", [128, self.arena_n], BF16)
            self.aoff = 0
            self.idf = self.cst[:, C_ID:C_ID + 128]
            tC = self.tok("consts")
            self.dma("sp", self.vecs[:], self.vecs_d, (), [tC], "ldc0")
            self.dma("sp", self.cst[:], self.consts_d, (), [tC], "ldc1")
            self.cp("dve", self.idb[:], self.idf, [tC], [tC])
            self.memset("pool", self.ones_m[:], 1.0 / 1024.0, [tC])
            self.memset("pool", self.ones_1[:], 1.0, [tC])
            self.memset("pool", self.ones_f[:], 1.0, [tC])
            self.memset("pool", self.lnq[:], -0.5 * float(np.log(128.0)), [tC])
            self.memset("pool", self.S32[:], 0.0, [self.tok("S32")])
            self.memset("pool", self.S16[:], 0.0, [self.tok("S16")])
            self.memset("pool", self.HG[:], 0.0, [self.tok("HG")])
            self.memset("pool", self.HF[:], 0.0, [self.tok("HF")])
            self.memset("pool", self.HP[:], 0.0, [self.tok("HP")])
            self.act(self.nexpA[:], self.vcol("alog", 0, 8), AF.Exp, [tC], [tC])
            self.ts("dve", self.nexpA[:], self.nexpA[:], -1.0, ALU.mult, [tC], [tC])
            self.tC = tC
            for st in SUPER:
                self.run_super(st)
            self.P.emit()
        return nc

    def run_super(self, st):
        self.load_x(st)
        self.gdn(st)
        self.ffn(st, 0)
        self.pool_mixer(st)
        self.ffn(st, 1)
        self.final(st)

    def xtok(self, dc, tile):
        return self.tok("xT", dc, tile[0].off + tile[1])

    def load_x(self, st):
        if st.first:
            self.phase()
        stg = [self.A([128, 4, D], F32) for _ in range(2)]
        bi = 0
        for seg in st.segs:
            for (_, t0, n) in seg.tiles():
                sg, tsg = stg[bi % 2]
                bi += 1
                nb = (n + 127) // 128
                for b in range(nb):
                    m = min(128, n - b * 128)
                    if seg.kind == "S":
                        self.dma("sp", sg[0:m, b, :], self.xs[0:m, :], (), [tsg], "ldx")
                    else:
                        p0 = seg.pos0 + t0 + b * 128
                        r = 0
                        if p0 < NMETA:
                            k = min(m, NMETA - p0)
                            self.dma("sp", sg[0:k, b, :], self.meta[p0:p0 + k, :], (), [tsg], "ldx")
                            r = k
                        if r < m:
                            a = p0 + r - NMETA
                            self.dma("sp", sg[r:m, b, :], self.xp[a:a + (m - r), :], (), [tsg], "ldx")
                tile = (seg, t0, n)
                c0 = seg.off + t0
                for dc in range(8):
                    pb, tpb = self.bank()
                    for b in range(nb):
                        m = min(128, n - b * 128)
                        self.tr(pb[:, b * 128:b * 128 + m], sg[0:m, b, dc * 128:(dc + 1) * 128], self.idf[0:m, 0:m],
                                [tsg, self.tC], [tpb])
                    self.cp(self.rr("ev", ["act", "dve"]), self.xT[:, dc, c0:c0 + n], pb[:, 0:n], [tpb], [self.xtok(dc, tile)])

    def norm(self, st, wname, dstf, sq2, rsb):
        for tile in st.tiles():
            seg, t0, n = tile
            c0 = seg.off + t0
            sq, tsq = sq2[self.rr("sq2", [0, 1])]
            rs, trs = rsb[self.rr("rsb", [0, 1])]
            pb, tpb = self.bank()
            for dc in range(8):
                xin = self.xT[:, dc, c0:c0 + n]
                if dc % 2 == 0:
                    self.tt("pool", sq[:, dc, 0:n], xin, xin, ALU.mult, [self.xtok(dc, tile)], [tsq])
                else:
                    self.act(sq[:, dc, 0:n], xin, AF.Square, [self.xtok(dc, tile)], [tsq])
            for dc in range(8):
                self.mm(pb[:, 0:n], self.ones_m[:], sq[:, dc, 0:n], [tsq, self.tC], [tpb], start=(dc == 0), stop=(dc == 7))
            self.act(rs[:, 0:n], pb[:, 0:n], AF.Ln, [tpb], [trs], bias=EPS)
            self.act(rs[:, 0:n], rs[:, 0:n], AF.Exp, [trs], [trs], scale=-0.5)
            for dc in range(8):
                dst, tdst = dstf(dc, tile)
                self.stt(dst, self.xT[:, dc, c0:c0 + n], self.vcol(wname, dc), rs[:, 0:n], ALU.mult, ALU.mult,
                         [self.xtok(dc, tile), trs, self.tC], [tdst])

    def gdn(self, st):
        NT = st.NT
        self.phase()
        hasS = any(s.kind == "S" for s in st.segs)
        xn, _ = self.A([128, 8, NT], BF16)
        QKVZ, _ = self.A([128, 32, NT], BF16)
        GB, tGB = self.A([40, NT], F32)
        mark = self.aoff
        sq2 = [self.A([128, 8, 512], BF16) for _ in range(2)]
        rsb = [self.A([128, 512], F32) for _ in range(2)]
        wsl = [self.A([128, 8, 512], BF16) for _ in range(3)]
        wbat, twba = self.A([128, 8, 40], BF16)
        ext = [self.A([128, 3 + 512], F32) for _ in range(3)]
        acc = [self.A([128, 512], F32) for _ in range(2)]
        sil = [self.A([128, 512], F32) for _ in range(2)]
        sqh = [self.A([128, 512], BF16) for _ in range(3)]
        rin = [self.A([128, 512], F32) for _ in range(2)]
        bat = [self.A([8, 512], F32) for _ in range(4)]
        if hasS:
            SHG, tSHG = self.A([128, 24, 48], F32)
            stg, tstg = self.A([48, 3072], F32)
        self.memset("pool", GB, 0.0, [tGB])
        xnt = lambda dc, tile: (xn[:, dc, tile[0].off + tile[1]:tile[0].off + tile[1] + tile[2]], self.tok("xn", self.phase_id, dc, tile[0].off + tile[1]))
        self.norm(st, "nm0", xnt, sq2, rsb)
        if hasS:
            self.dma("sp", stg, self.st_conv, (), [tstg], "ldst")
            for g in range(3):
                pb, tpb = self.bank()
                for j in range(8):
                    fc = g * 8 + j
                    self.tr(pb[:, j * 48:(j + 1) * 48], stg[0:48, fc * 128:(fc + 1) * 128], self.idf[0:48, 0:48], [tstg, self.tC], [tpb])
                self.cp("act", SHG[:, g * 8:(g + 1) * 8, :], pb[:, 0:384].rearrange("p (a b) -> p a b", b=48), [tpb], [tSHG])
        self.dma("pool", wbat, self.wba.rearrange("(kc p) f -> p kc f", p=128), (), [twba], "ldwba")
        for tile in st.tiles():
            seg, t0, n = tile
            c0 = seg.off + t0
            pb, tpb = self.bank()
            for kc in range(8):
                self.mm(pb[0:40, 0:n], wbat[:, kc, :], xn[:, kc, c0:c0 + n], [twba, xnt(kc, tile)[1]], [tpb], start=(kc == 0), stop=(kc == 7))
            self.act(GB[32:40, c0:c0 + n], pb[32:40, 0:n], AF.Sigmoid, [tpb], [tGB])
            (b1, t1), (b2, t2), (b3, t3), (b4, t4) = bat
            self.ts("dve", b1[:, 0:n], pb[0:8, 0:n], self.vcol("dtb", 0, 8), ALU.add, [tpb, self.tC], [t1])
            self.stt(b2[:, 0:n], b1[:, 0:n], -1.0, b1[:, 0:n], ALU.mult, ALU.max, [t1], [t2])
            self.act(b3[:, 0:n], b2[:, 0:n], AF.Exp, [t2], [t3], scale=-1.0)
            self.act(b4[:, 0:n], b3[:, 0:n], AF.Ln, [t3], [t4], bias=1.0)
            self.stt(b2[:, 0:n], b1[:, 0:n], 0.0, b4[:, 0:n], ALU.max, ALU.add, [t1, t4], [t2])
            self.ts("dve", GB[0:8, c0:c0 + n], b2[:, 0:n], self.nexpA[:, 0:1], ALU.mult, [t2, self.tC], [tGB])
        def ld_win(u):
            wt_, twt_ = wsl[u % 3]
            self.dma("pool", wt_, self.w_in[:, u * 512:(u + 1) * 512].rearrange("(kc p) f -> p kc f", p=128), (), [twt_], "ldw%d" % (u % 3))
        ld_win(0); ld_win(1)
        pend = []
        qk_list = []
        for u in range(8):
            wt, twt = wsl[u % 3]
            if u + 2 < 8:
                ld_win(u + 2)
            for j in range(4):
                fc = u * 4 + j
                kind = fc // 8
                prev_ext = None
                for tile in st.tiles():
                    seg, t0, n = tile
                    c0 = seg.off + t0
                    pb, tpb = self.bank()
                    for kc in range(8):
                        self.mm(pb[:, 0:n], wt[:, kc, j * 128:(j + 1) * 128], xn[:, kc, c0:c0 + n], [twt, xnt(kc, tile)[1]], [tpb],
                                start=(kc == 0), stop=(kc == 7))
                    dst = QKVZ[:, fc, c0:c0 + n]
                    tdst = self.tok("qkvz", self.phase_id, fc, c0)
                    if kind == 3:
                        self.act(self.V(seg, dst), self.V(seg, pb[:, 0:n]), AF.Silu, [tpb], [tdst])
                        continue
                    ex, tex = ext[self.rr("ext", [0, 1, 2])]
                    if seg.kind == "S":
                        self.cp("pool", self.ext_halo(seg, ex, 3), SHG[:, fc, :].rearrange("p (s r) -> p s r", r=3), [tSHG], [tex])
                    elif t0 == 0:
                        self.cp("pool", ex[:, 0:3], self.HG[:, fc, :], [self.tok("HG")], [tex])
                    else:
                        pe_, tpe_, pn = prev_ext
                        self.cp("pool", ex[:, 0:3], pe_[:, pn:pn + 3], [tpe_], [tex])
                    self.cp("act", self.ext_dst(seg, ex, 3, n), self.V(seg, pb[:, 0:n]), [tpb], [tex])
                    prev_ext = (ex, tex, n)
                    if seg.kind == "S":
                        self.cp("pool", SHG[:, fc, :].rearrange("p (s r) -> p s r", r=3), self.ext_tail(seg, ex, 3, n), [tex], [tSHG])
                    elif t0 + n == seg.n:
                        self.cp("pool", self.HG[:, fc, :], ex[:, n:n + 3], [tex], [self.tok("HG")])
                    ac, tac = acc[self.rr("acc", [0, 1])]
                    av = self.V(seg, ac[:, 0:n])
                    self.act(av, self.ext_tap(seg, ex, 3, 0, n), AF.Identity, [tex, self.tC], [tac], scale=self.vcol("gcw", 0 * 24 + fc))
                    for tap in (1, 2, 3):
                        self.stt(av, self.ext_tap(seg, ex, 3, tap, n), self.vcol("gcw", tap * 24 + fc), av, ALU.mult, ALU.add, [tex, tac, self.tC], [tac])

                    def tail(dst=dst, tdst=tdst, ac=ac, tac=tac, n=n):
                        self.act(dst, ac[:, 0:n], AF.Silu, [tac], [tdst])
                    if pend:
                        pend.pop(0)()
                    pend.append(tail)
                    if kind < 2:
                        qk_list.append((kind, dst, tdst, n))
            while pend:
                pend.pop(0)()
            def nstage1(item):
                kind, dst, tdst, n = item
                sh, tsh = sqh[self.rr("sqh", [0, 1, 2])]
                self.tt("dve", sh[:, 0:n], dst, dst, ALU.mult, [tdst], [tsh])
                pb2, tpb2 = self.bank()
                self.mm(pb2[:, 0:n], self.ones_1[:], sh[:, 0:n], [tsh, self.tC], [tpb2])
                return pb2, tpb2

            def nstage2(item, pb2, tpb2):
                kind, dst, tdst, n = item
                ri, tri_ = rin[self.rr("rin", [0, 1])]
                self.act(ri[:, 0:n], pb2[:, 0:n], AF.Ln, [tpb2], [tri_], bias=EPS)
                self.act(ri[:, 0:n], ri[:, 0:n], AF.Exp, [tri_], [tri_], scale=-0.5, bias=(self.lnq[:, 0:1] if kind == 0 else None))
                self.tt("dve", dst, dst, ri[:, 0:n], ALU.mult, [tdst, tri_], [tdst])
            inflight = []
            for item in qk_list:
                inflight.append((item,) + nstage1(item))
                if len(inflight) > 2:
                    nstage2(*inflight.pop(0))
            while inflight:
                nstage2(*inflight.pop(0))
            qk_list = []
        if hasS:
            for fc in range(24):
                pb, tpb = self.bank()
                self.tr(pb[0:48, 0:128], SHG[:, fc, :], self.idf[:, :], [tSHG, self.tC], [tpb])
                self.cp(self.rr("ev", ["act", "dve"]), stg[0:48, fc * 128:(fc + 1) * 128], pb[0:48, 0:128], [tpb], [tstg])
            self.dma("sp", self.o_sconv, stg, [tstg], (), "stst")
        if st.last:
            stg2, tstg2 = self.A([3, 3072], F32)
            for fc in range(24):
                pb, tpb = self.bank()
                self.tr(pb[0:3, 0:128], self.HG[:, fc, :], self.idf[:, :], [self.tok("HG"), self.tC], [tpb])
                self.cp(self.rr("ev", ["act", "dve"]), stg2[0:3, fc * 128:(fc + 1) * 128], pb[0:3, 0:128], [tpb], [tstg2])
            self.dma("sp", self.o_pconv, stg2, [tstg2], (), "stst")

        self.P.barrier()
        self.aoff = mark
        ONT = xn
        self.bfree = list(range(8))
        NA, NC_, NB = 3, 4, 1
        self.want_onacc = False
        TAs = [self.alloc_chunk_bufs("A") for _ in range(NA)]
        CAs = [self.alloc_chunk_bufs("C") for _ in range(NC_)]
        TBs = [self.alloc_chunk_bufs("B") for _ in range(NB)]
        jobs = []
        for seg in st.segs:
            if seg.kind == "P":
                c = 0
                if seg.pos0 == 0:
                    jobs.append(("P", seg.off, NMETA, None))
                    c = NMETA
                while c < seg.n:
                    jobs.append(("P", seg.off + c, 64, None))
                    c += 64
            else:
                for b_ in range(2):
                    jobs.append(("SB", seg.off + 64 * b_, 64, 8 * b_))
        N = len(jobs)
        nextA = 0
        nextB = 0
        doneA = set()
        actA = {}
        actB = None
        while nextB < N:
            for slot in range(NA):
                if slot not in actA and nextA < N and nextA < nextB + NC_:
                    actA[slot] = (nextA, self.chunk_A(jobs[nextA], QKVZ, GB, tGB, TAs[slot], CAs[nextA % NC_]))
                    nextA += 1
            if actB is None and nextB in doneA:
                if jobs[nextB][0] == "SB":
                    nextB += 1
                    continue
                actB = self.chunk_B(jobs[nextB], CAs[nextB % NC_], TBs[nextB % NB], QKVZ, ONT)
            if actB is not None:
                try:
                    next(actB)
                except StopIteration:
                    actB = None
                    nextB += 1
            for slot in list(actA):
                j, g = actA[slot]
                try:
                    next(g)
                except StopIteration:
                    doneA.add(j)
                    del actA[slot]
        if hasS:
            self.P.barrier()
            save_off = self.aoff
            self.aoff = mark
            NSQ = 3
            SBs = []
            for _ in range(NSQ):
                d = {}
                d["S32"] = self.A([128, 8, 128], F32); d["S16"] = self.A([128, 8, 128], BF16)
                d["Stmp"] = self.A([128, 1024], F32); d["vn"] = self.A([64, 1024], BF16)
                SBs.append(d)
            onaccs = [self.A([64, 1024], F32) for _ in range(2)]
            sb_jobs = [(ji, jobs[ji]) for ji in range(N) if jobs[ji][0] == "SB"]
            todo = [(bi, ji, job, s_) for bi, (ji, job) in enumerate(sb_jobs) for s_ in range(8)]
            remaining = {bi: 8 for bi in range(len(sb_jobs))}
            act = {}
            fin = []
            while todo or act or fin:
                for slot in range(NSQ):
                    if slot not in act and todo:
                        bi, ji, job, s_ = todo.pop(0)
                        act[slot] = (bi, self.sample_seq(job, s_, CAs[ji % NC_], SBs[slot], onaccs[bi]))
                for slot in list(act):
                    bi, g = act[slot]
                    try:
                        next(g)
                    except StopIteration:
                        del act[slot]
                        remaining[bi] -= 1
                        if remaining[bi] == 0:
                            ji, job = sb_jobs[bi]
                            fin.append(self.sample_finish(job, TBs[0], onaccs[bi], QKVZ, ONT))
                for g in list(fin):
                    try:
                        next(g)
                    except StopIteration:
                        fin.remove(g)
            self.aoff = max(save_off, self.aoff)
        if st.last:
            self.dma("sp", self.o_prec.rearrange("h k v -> k h v"), self.S32[:], [self.tok("S32")], (), "strec")

        self.P.barrier()
        self.aoff = mark
        wo, two = self.A([128, 8, D], BF16)
        self.dma("pool", wo[:, :, 0:512], self.w_out[:, 0:512].rearrange("(kc p) f -> p kc f", p=128), (), [two], "ldw0")
        self.dma("pool", wo[:, :, 512:1024], self.w_out[:, 512:1024].rearrange("(kc p) f -> p kc f", p=128), (), [two], "ldw1")
        for tile in st.tiles():
            seg, t0, n = tile
            c0 = seg.off + t0
            for dc in range(8):
                pb, tpb = self.bank()
                for kc in range(8):
                    self.mm(pb[:, 0:n], wo[:, kc, dc * 128:(dc + 1) * 128], ONT[:, kc, c0:c0 + n], [two], [tpb], start=(kc == 0), stop=(kc == 7))
                xv = self.xT[:, dc, c0:c0 + n]
                self.tt("dve", xv, pb[:, 0:n], xv, ALU.add, [tpb], [self.xtok(dc, tile)])

    def alloc_chunk_bufs(self, which):
        b = {}
        def a(name, shape, dt):
            b[name] = self.A(shape, dt)
        if which == "A":
            a("gbt", [64, 40], F32)
            for nm in ("Gt", "eG", "nbG", "nb", "dGl", "eGl"):
                a(nm, [64, 8], F32)
            a("Dm", [64, 512], F32); a("Du", [64, 512], F32); a("Dl", [64, 512], F32); a("eGbc", [128, 512], F32)
            b["rhsG"] = b["Dl"]
            a("Lneg", [64, 512], BF16); a("M0", [64, 512], BF16)
            a("QTa", [64, 512], BF16); a("QTb", [64, 512], BF16)
            a("kbgn", [64, 1024], BF16)
        elif which == "C":
            a("PQa", [64, 1024], BF16); a("PQb", [64, 1024], BF16); a("At", [64, 512], BF16)
            a("kd", [64, 1024], BF16); a("vb", [64, 1024], BF16)
            a("nWT", [128, 512], BF16); a("qdT", [128, 512], BF16); a("gtc", [128, 64], F32)
        else:
            a("vn", [64, 1024], BF16); a("sqo", [64, 1024], BF16); a("on", [64, 1024], BF16)
            a("Stmp", [128, 1024], F32); a("ss", [64, 8], F32); a("rs", [64, 8], F32)
            if self.want_onacc:
                a("onacc", [64, 1024], F32)
        return b

    def bacq(self):
        if not self.bfree:
            raise RuntimeError("out of PSUM banks")
        i = self.bfree.pop(0)
        return self.banks[i], self.tok("bank", i), i

    def brel(self, i):
        self.bfree.append(i)

    def chunk_A(self, job, QKVZ, GB, tGB, TA, CA):
        kind, c0, L, sidx = job
        B = dict(TA); B.update(CA)
        tC = self.tC
        Q = lambda h: QKVZ[:, h, c0:c0 + L]
        K = lambda h: QKVZ[:, 8 + h, c0:c0 + L]
        Vv = lambda h: QKVZ[:, 16 + h, c0:c0 + L]
        h3 = lambda ap: ap.rearrange("p (h l) -> p h l", l=L)
        hd = lambda ap: ap.rearrange("p (h d) -> p h d", d=128)
        W8 = 8 * L
        pg, tpg, ipg = self.bacq()
        self.tr(pg[0:L, 0:40], GB[0:40, c0:c0 + L], self.idf[0:40, 0:40], [tGB, tC], [tpg])
        gbt, tgbt = B["gbt"]
        self.cp("dve", gbt[0:L, :], pg[0:L, 0:40], [tpg], [tgbt])
        self.brel(ipg)
        g_tm = gbt[0:L, 0:8]
        beta = gbt[0:L, 32:40]
        blk = (kind == "SB")
        cT, cN, cP = (C_TRI8, C_NEGU8, C_POSL8) if blk else (C_TRI, C_NEGU, C_POSL)
        tri = self.cst[0:L, cT:cT + L]
        yield
        rhsG, trG = B["rhsG"]
        self.tt("pool", h3(rhsG[0:L, 0:W8]), tri.unsqueeze(1).to_broadcast([L, 8, L]), g_tm.unsqueeze(2).to_broadcast([L, 8, L]),
                ALU.mult, [tgbt, tC], [trG])
        pg2, tpg2, ipg2 = self.bacq()
        self.mm(pg2[0:L, 0:8], tri, g_tm, [tgbt, tC], [tpg2])
        Gt, tGt = B["Gt"]; eG, teG = B["eG"]; nbG, tnbG = B["nbG"]; nb, tnb = B["nb"]
        dGl, tdGl = B["dGl"]; eGl, teGl = B["eGl"]; gtc, tgtc = B["gtc"]
        self.cp("dve", Gt[0:L, :], pg2[0:L, 0:8], [tpg2], [tGt])
        self.brel(ipg2)
        self.ts("dve", nb[0:L, :], beta, -1.0, ALU.mult, [tgbt], [tnb])
        yield
        pG, tpG, ipG = self.bacq()
        self.mm(pG[:, 0:W8], self.ones_f[0:L, :], rhsG[0:L, 0:W8], [trG, tC], [tpG])
        self.act(eG[0:L, :], Gt[0:L, :], AF.Exp, [tGt], [teG])
        self.tt("dve", nbG[0:L, :], eG[0:L, :], nb[0:L, :], ALU.mult, [teG, tnb], [tnbG])
        yield
        if blk:
            Glast = None
            gl4 = pG[:, 0:W8].rearrange("p (h s t) -> p h s t", s=8, t=8)[:, :, :, 7]
        else:
            Glast = h3(pG[:, 0:W8])[:, :, L - 1]
        Dm, tDm = B["Dm"]; Du, tDu = B["Du"]; Dl, tDl = B["Dl"]; eGbc, teGbc = B["eGbc"]
        self.tt("dve", h3(Dm[0:L, 0:W8]), h3(pG[0:L, 0:W8]), Gt[0:L, :].unsqueeze(2).to_broadcast([L, 8, L]), ALU.subtract,
                [tpG, tGt], [tDm])
        if blk:
            pgl, tpgl, ipgl = self.bacq()
            self.mm(pgl[0:L, 0:8], self.cst[0:L, C_SEL:C_SEL + L], Gt[0:L, :], [tGt, tC], [tpgl])
            self.tt("dve", dGl[0:L, :], pgl[0:L, 0:8], Gt[0:L, :], ALU.subtract, [tpgl, tGt], [tdGl])
            self.brel(ipgl)
            self.act(gtc[:, 0:64].rearrange("p (h s) -> p h s", s=8), gl4, AF.Exp, [tpG], [tgtc])
        else:
            self.tt("dve", dGl[0:L, :], Glast[0:L], Gt[0:L, :], ALU.subtract, [tpG, tGt], [tdGl])
            self.act(gtc[:, 0:8], Glast, AF.Exp, [tpG], [tgtc])
        self.act(eGbc[:, 0:W8], pG[:, 0:W8], AF.Exp, [tpG], [teGbc])
        self.brel(ipG)
        pk, tpk, ipk = self.bacq()
        pkb = pk[:].bitcast(BF16)
        for h in range(8):
            self.tr(pkb[0:L, h * 128:(h + 1) * 128], K(h), self.idb[:], [tC], [tpk])
        pv, tpv, ipv = self.bacq()
        pvb = pv[:].bitcast(BF16)
        for h in range(8):
            self.tr(pvb[0:L, h * 128:(h + 1) * 128], Vv(h), self.idb[:], [tC], [tpv])
        yield
        self.act(eGl[0:L, :], dGl[0:L, :], AF.Exp, [tdGl], [teGl])
        negu = self.cst[0:L, cN:cN + L].unsqueeze(1).to_broadcast([L, 8, L])
        posl = self.cst[0:L, cP:cP + L].unsqueeze(1).to_broadcast([L, 8, L])
        self.tt("pool", h3(Du[0:L, 0:W8]), h3(Dm[0:L, 0:W8]), negu, ALU.add, [tDm, tC], [tDu])
        self.tt("pool", h3(Dl[0:L, 0:W8]), h3(Dm[0:L, 0:W8]), posl, ALU.add, [tDm, tC], [tDl])
        kbgn, tkb = B["kbgn"]; kd, tkd = B["kd"]; vb, tvb = B["vb"]
        self.tt("dve", hd(kbgn[0:L, :]), hd(pkb[0:L, :]), nbG[0:L, :].unsqueeze(2).to_broadcast([L, 8, 128]), ALU.mult, [tpk, tnbG], [tkb])
        self.tt("dve", hd(vb[0:L, :]), hd(pvb[0:L, :]), beta.unsqueeze(2).to_broadcast([L, 8, 128]), ALU.mult, [tpv, tgbt], [tvb])
        self.brel(ipv)
        yield
        self.tt("dve", hd(kd[0:L, :]), hd(pkb[0:L, :]), eGl[0:L, :].unsqueeze(2).to_broadcast([L, 8, 128]), ALU.mult, [tpk, teGl], [tkd])
        self.brel(ipk)
        self.act(Du[0:L, 0:W8], Du[0:L, 0:W8], AF.Exp, [tDu], [tDu])
        self.act(Dl[0:L, 0:W8], Dl[0:L, 0:W8], AF.Exp, [tDl], [tDl], scale=-1.0)
        pkk, tpkk, ipkk = self.bacq()
        for h in range(8):
            self.mm(pkk[0:L, h * L:(h + 1) * L], K(h), K(h), [], [tpkk])
        pkq, tpkq, ipkq = self.bacq()
        for h in range(8):
            self.mm(pkq[0:L, h * L:(h + 1) * L], K(h), Q(h), [], [tpkq])
        qdT, tqd = B["qdT"]
        self.tt("pool", h3(qdT[:, 0:W8]), QKVZ[:, 0:8, c0:c0 + L], h3(eGbc[:, 0:W8]), ALU.mult, [teGbc], [tqd])
        yield
        self.tt("pool", h3(Dl[0:L, 0:W8]), h3(Dl[0:L, 0:W8]), nb[0:L, :].unsqueeze(2).to_broadcast([L, 8, L]), ALU.mult,
                [tDl, tnb], [tDl])
        Lneg, tLn = B["Lneg"]; At, tAt = B["At"]; M0, tM0 = B["M0"]
        self.tt("dve", At[0:L, 0:W8], pkq[0:L, 0:W8], Du[0:L, 0:W8], ALU.mult, [tpkq, tDu], [tAt])
        self.brel(ipkq)
        yield
        self.tt("dve", Lneg[0:L, 0:W8], pkk[0:L, 0:W8], Dl[0:L, 0:W8], ALU.mult, [tpkk, tDl], [tLn])
        self.brel(ipkk)
        yield
        pm, tpm, ipm = self.bacq()
        pmb = pm[:].bitcast(BF16)
        for h in range(8):
            self.tr(pmb[0:L, h * L:(h + 1) * L], Lneg[0:L, h * L:(h + 1) * L], self.idb[0:L, 0:L], [tLn, tC], [tpm])
        self.cp("act", M0[0:L, 0:W8], pmb[0:L, 0:W8], [tpm], [tM0])
        self.brel(ipm)
        yield
        nlev = 3 if blk else {64: 6, 16: 4, 8: 3}[L]
        idbL = self.idb[0:L, 0:L]
        PQ = [B["PQa"], B["PQb"]]
        QTbufs = [B["QTa"], B["QTb"]]
        pq3 = lambda ap: ap[0:L, :].rearrange("p (h c) -> p h c", c=128)
        cur = 0
        Pc, tPc = PQ[cur]
        self.tt("pool", pq3(Pc)[:, :, 0:L], h3(M0[0:L, 0:W8]), idbL.unsqueeze(1).to_broadcast([L, 8, L]), ALU.add, [tM0, tC], [tPc])
        pq, tpq, ipq = self.bacq()
        for h in range(8):
            sl = slice(h * L, (h + 1) * L)
            self.mm(pq[0:L, sl], M0[0:L, sl], Lneg[0:L, sl], [tM0, tLn], [tpq])
        QTc = QTbufs[0]
        self.cp("act", QTc[0][0:L, 0:W8], pq[0:L, 0:W8], [tpq], [QTc[1]])
        self.brel(ipq)
        pq2, tpq2, ipq2 = self.bacq()
        for h in range(8):
            sl = slice(h * L, (h + 1) * L)
            self.mm(pq2[0:L, sl], Lneg[0:L, sl], M0[0:L, sl], [tM0, tLn], [tpq2])
        self.cp("dve", pq3(Pc)[:, :, L:2 * L], h3(pq2[0:L, 0:W8]), [tpq2], [tPc])
        self.brel(ipq2)
        yield
        for k in range(1, nlev):
            last = (k == nlev - 1)
            Pn, tPn = PQ[1 - cur]
            wid = L if last else 2 * L
            if not last:
                QTn = QTbufs[k % 2]
                pq, tpq, ipq = self.bacq()
                for h in range(8):
                    self.mm(pq[0:L, h * L:(h + 1) * L], pq3(Pc)[:, h, L:2 * L], QTc[0][0:L, h * L:(h + 1) * L], [tPc, QTc[1]], [tpq])
                self.cp("act", QTn[0][0:L, 0:W8], pq[0:L, 0:W8], [tpq], [QTn[1]])
                self.brel(ipq)
            for half in range(2):
                pp, tpp, ipp = self.bacq()
                for hh in range(4):
                    h = half * 4 + hh
                    self.mm(pp[0:L, hh * 128:hh * 128 + wid], QTc[0][0:L, h * L:(h + 1) * L], pq3(Pc)[:, h, 0:wid], [tPc, QTc[1]], [tpp])
                ppv = pp[0:L, :].rearrange("p (h c) -> p h c", c=128)
                hs = slice(half * 4, half * 4 + 4)
                self.tt("dve", pq3(Pn)[:, hs, 0:L], ppv[:, :, 0:L], pq3(Pc)[:, hs, 0:L], ALU.add, [tpp, tPc], [tPn])
                if not last:
                    self.cp("act", pq3(Pn)[:, hs, L:2 * L], ppv[:, :, L:2 * L], [tpp], [tPn])
                self.brel(ipp)
            cur = 1 - cur
            Pc, tPc = PQ[cur]
            if not last:
                QTc = QTn
            yield
        Ttv = pq3(Pc)
        tTt = tPc
        pw, tpw, ipw = self.bacq()
        for h in range(8):
            self.mm(pw[:, h * L:(h + 1) * L], kbgn[0:L, h * 128:(h + 1) * 128], Ttv[:, h, 0:L], [tkb, tTt], [tpw])
        nWT, tnW = B["nWT"]
        self.cp("act", nWT[:, 0:W8], pw[:, 0:W8], [tpw], [tnW])
        self.brel(ipw)
        CA["Tt"] = (Ttv, tTt)
        yield

    def chunk_B(self, job, CA, TB, QKVZ, ONT):
        kind, c0, L, sidx = job
        B = dict(TB); B.update(CA)
        tC = self.tC
        Tt, tTt = CA["Tt"]
        h3 = lambda ap: ap.rearrange("p (h l) -> p h l", l=L)
        hd = lambda ap: ap.rearrange("p (h d) -> p h d", d=128)
        W8 = 8 * L
        if kind == "S":
            S32, tS32 = self.SS32[sidx % 2]
            S16, tS16 = self.SS16[sidx % 2]
            self.dma("sp", S32, self.st_rec[sidx].rearrange("h k v -> k h v"), (), [tS32], "ldrec%d" % (sidx % 2))
            self.cp("pool", S16, S32, [tS32], [tS16])
        else:
            S32, tS32 = self.S32[:], self.tok("S32")
            S16, tS16 = self.S16[:], self.tok("S16")
        vb, tvb = B["vb"]; nWT, tnW = B["nWT"]; vn, tvn = B["vn"]; qdT, tqd = B["qdT"]; At, tAt = B["At"]
        kd, tkd = B["kd"]; sqo, tsq = B["sqo"]; on, ton = B["on"]; ss, tss = B["ss"]; rs, trs = B["rs"]
        gtc, tgtc = B["gtc"]; Stmp, tSt = B["Stmp"]
        for half in range(2):
            pv, tpv, ipv = self.bacq()
            for hh in range(4):
                h = half * 4 + hh
                self.mm(pv[0:L, hh * 128:(hh + 1) * 128], Tt[:, h, 0:L], vb[0:L, h * 128:(h + 1) * 128], [tTt, tvb], [tpv], start=(hh == 0), stop=False, skip=True)
            for hh in range(4):
                h = half * 4 + hh
                self.mm(pv[0:L, hh * 128:(hh + 1) * 128], nWT[:, h * L:(h + 1) * L], S16[:, h, :], [tnW, tS16], [tpv], start=False, stop=True, skip=True)
            self.cp(("act", "dve")[half], vn[0:L, half * 512:(half + 1) * 512], pv[0:L, :], [tpv], [tvn])
            self.brel(ipv)
        self.tt("pool", hd(Stmp[:, :]), S32, gtc[:, 0:8].unsqueeze(2).to_broadcast([128, 8, 128]), ALU.mult, [tS32, tgtc], [tSt])
        yield
        pss = []
        for half in range(2):
            pS, tpS, ipS = self.bacq()
            pss.append((pS, tpS, ipS))
            for hh in range(4):
                h = half * 4 + hh
                self.mm(pS[:, hh * 128:(hh + 1) * 128], kd[0:L, h * 128:(h + 1) * 128], vn[0:L, h * 128:(h + 1) * 128], [tkd, tvn], [tpS])
        for half in range(2):
            pS, tpS, ipS = pss[half]
            self.tt("dve", S32[:, half * 4:(half + 1) * 4, :], hd(pS[:, :]), hd(Stmp[:, half * 512:(half + 1) * 512]), ALU.add, [tpS, tSt], [tS32])
            self.brel(ipS)
        pos = []
        for half in range(2):
            po, tpo, ipo = self.bacq()
            pos.append((po, tpo, ipo))
            for hh in range(4):
                h = half * 4 + hh
                self.mm(po[0:L, hh * 128:(hh + 1) * 128], qdT[:, h * L:(h + 1) * L], S16[:, h, :], [tqd, tS16], [tpo], start=(hh == 0), stop=False, skip=True)
            for hh in range(4):
                h = half * 4 + hh
                self.mm(po[0:L, hh * 128:(hh + 1) * 128], At[0:L, h * L:(h + 1) * L], vn[0:L, h * 128:(h + 1) * 128], [tAt, tvn], [tpo], start=False, stop=True, skip=True)
        self.cp("act", S16, S32, [tS32], [tS16])
        if kind == "S":
            self.dma("sp", self.o_srec[sidx].rearrange("h k v -> k h v"), S32, [tS32], (), "strec%d" % (sidx % 2))
        yield
        for half in range(2):
            po, tpo, ipo = pos[half]
            self.act(sqo[0:L, half * 512:(half + 1) * 512], po[0:L, :], AF.Square, [tpo], [tsq])
        self.P.op("dve", lambda e, o=ss[0:L, :], i=hd(sqo[0:L, :]): e.tensor_reduce(out=o, in_=i, axis=AX.X, op=ALU.add), [tsq], [tss])
        self.act(rs[0:L, :], ss[0:L, :], AF.Ln, [tss], [trs], bias=EPS, scale=1.0 / 128.0)
        self.act(rs[0:L, :], rs[0:L, :], AF.Exp, [trs], [trs], scale=-0.5)
        yield
        for half in range(2):
            po, tpo, ipo = pos[half]
            self.tt("dve", hd(on[0:L, half * 512:(half + 1) * 512]), hd(po[0:L, :]), rs[0:L, half * 4:(half + 1) * 4].unsqueeze(2).to_broadcast([L, 4, 128]),
                    ALU.mult, [tpo, trs], [ton])
            self.brel(ipo)
        yield
        pt, tpt, ipt = self.bacq()
        ptb = pt[:].bitcast(BF16)
        for h in range(8):
            self.tr(ptb[:, h * L:(h + 1) * L], on[0:L, h * 128:(h + 1) * 128], self.idb[0:L, 0:L], [ton, tC], [tpt])
        self.stt(ONT[:, :, c0:c0 + L], h3(ptb[:, 0:W8]), self.vcol("gnw"), QKVZ[:, 24:32, c0:c0 + L], ALU.mult, ALU.mult, [tpt, tC], [self.tok("ONT", self.phase_id)])
        self.brel(ipt)
        yield

    def sample_seq(self, job, s_, CA, SB, onacc_t):
        kind, c0, L, s0 = job
        tC = self.tC
        Tt, tTt = CA["Tt"]
        hd = lambda ap: ap.rearrange("p (h d) -> p h d", d=128)
        vb, tvb = CA["vb"]; nWT, tnW = CA["nWT"]; qdT, tqd = CA["qdT"]; At, tAt = CA["At"]
        kd, tkd = CA["kd"]; gtc, tgtc = CA["gtc"]
        S32, tS32 = SB["S32"]; S16, tS16 = SB["S16"]; Stmp, tSt = SB["Stmp"]; vn, tvn = SB["vn"]
        onacc, tacc = onacc_t
        gtc3 = gtc[:, 0:64].rearrange("p (h s) -> p h s", s=8)
        sidx = s0 + s_
        rm = self.cst[0:L, C_RM + s_:C_RM + s_ + 1]
        self.dma("sp", S32, self.st_rec[sidx].rearrange("h k v -> k h v"), (), [tS32], "ldrec%d" % (sidx % 3))
        yield
        self.cp("pool", S16, S32, [tS32], [tS16])
        self.tt("pool", hd(Stmp[:, :]), S32, gtc3[:, :, s_].unsqueeze(2).to_broadcast([128, 8, 128]), ALU.mult, [tS32, tgtc], [tSt])
        yield
        for half in range(2):
            pv, tpv, ipv = self.bacq()
            for hh in range(4):
                h = half * 4 + hh
                self.mm(pv[0:L, hh * 128:(hh + 1) * 128], Tt[:, h, 0:L], vb[0:L, h * 128:(h + 1) * 128], [tTt, tvb], [tpv], start=(hh == 0), stop=False, skip=True)
            for hh in range(4):
                h = half * 4 + hh
                self.mm(pv[0:L, hh * 128:(hh + 1) * 128], nWT[:, h * L:(h + 1) * L], S16[:, h, :], [tnW, tS16], [tpv], start=False, stop=True, skip=True)
            if half == 0:
                self.act(vn[0:L, 0:512], pv[0:L, :], AF.Identity, [tpv, tC], [tvn], scale=rm)
            else:
                self.ts("dve", vn[0:L, 512:1024], pv[0:L, :], rm, ALU.mult, [tpv, tC], [tvn])
            self.brel(ipv)
        yield
        pss = []
        for half in range(2):
            pS, tpS, ipS = self.bacq()
            pss.append((pS, tpS, ipS))
            for hh in range(4):
                h = half * 4 + hh
                self.mm(pS[:, hh * 128:(hh + 1) * 128], kd[0:L, h * 128:(h + 1) * 128], vn[0:L, h * 128:(h + 1) * 128], [tkd, tvn], [tpS])
        for half in range(2):
            pS, tpS, ipS = pss[half]
            self.tt("dve", S32[:, half * 4:(half + 1) * 4, :], hd(pS[:, :]), hd(Stmp[:, half * 512:(half + 1) * 512]), ALU.add, [tpS, tSt], [tS32])
            self.brel(ipS)
        self.dma("sp", self.o_srec[sidx].rearrange("h k v -> k h v"), S32, [tS32], (), "strec%d" % (sidx % 3))
        pos = []
        for half in range(2):
            po, tpo, ipo = self.bacq()
            pos.append((po, tpo, ipo))
            for hh in range(4):
                h = half * 4 + hh
                self.mm(po[0:L, hh * 128:(hh + 1) * 128], qdT[:, h * L:(h + 1) * L], S16[:, h, :], [tqd, tS16], [tpo], start=(hh == 0), stop=False, skip=True)
            for hh in range(4):
                h = half * 4 + hh
                self.mm(po[0:L, hh * 128:(hh + 1) * 128], At[0:L, h * L:(h + 1) * L], vn[0:L, h * 128:(h + 1) * 128], [tAt, tvn], [tpo], start=False, stop=True, skip=True)
        yield
        for half in range(2):
            po, tpo, ipo = pos[half]
            acc = onacc[0:L, half * 512:(half + 1) * 512]
            if s_ == 0:
                self.ts("dve", acc, po[0:L, :], rm, ALU.mult, [tpo, tC], [tacc])
            else:
                self.stt(acc, po[0:L, :], rm, acc, ALU.mult, ALU.add, [tpo, tC, tacc], [tacc])
            self.brel(ipo)
        yield

    def sample_finish(self, job, TB, onacc_t, QKVZ, ONT):
        kind, c0, L, s0 = job
        tC = self.tC
        h3 = lambda ap: ap.rearrange("p (h l) -> p h l", l=L)
        hd = lambda ap: ap.rearrange("p (h d) -> p h d", d=128)
        W8 = 8 * L
        sqo, tsq = TB["sqo"]; on, ton = TB["on"]; ss, tss = TB["ss"]; rs, trs = TB["rs"]
        onacc, tacc = onacc_t
        self.act(sqo[0:L, :], onacc[0:L, :], AF.Square, [tacc], [tsq])
        self.P.op("dve", lambda e, o=ss[0:L, :], i=hd(sqo[0:L, :]): e.tensor_reduce(out=o, in_=i, axis=AX.X, op=ALU.add), [tsq], [tss])
        self.act(rs[0:L, :], ss[0:L, :], AF.Ln, [tss], [trs], bias=EPS, scale=1.0 / 128.0)
        self.act(rs[0:L, :], rs[0:L, :], AF.Exp, [trs], [trs], scale=-0.5)
        yield
        self.tt("dve", hd(on[0:L, :]), hd(onacc[0:L, :]), rs[0:L, :].unsqueeze(2).to_broadcast([L, 8, 128]), ALU.mult, [tacc, trs], [ton])
        yield
        pt, tpt, ipt = self.bacq()
        ptb = pt[:].bitcast(BF16)
        for h in range(8):
            self.tr(ptb[:, h * L:(h + 1) * L], on[0:L, h * 128:(h + 1) * 128], self.idb[0:L, 0:L], [ton, tC], [tpt])
        self.stt(ONT[:, :, c0:c0 + L], h3(ptb[:, 0:W8]), self.vcol("gnw"), QKVZ[:, 24:32, c0:c0 + L], ALU.mult, ALU.mult, [tpt, tC], [self.tok("ONT", self.phase_id)])
        self.brel(ipt)
        yield

    def ffn(self, st, l):
        NT = st.NT
        self.phase()
        hasS = any(s.kind == "S" for s in st.segs)
        xn, _ = self.A([128, 8, NT], BF16)
        hT, _ = self.A([128, 22, NT], BF16)
        sq2 = [self.A([128, 8, 512], BF16) for _ in range(2)]
        rsb = [self.A([128, 512], F32) for _ in range(2)]
        wsl = [self.A([128, 2, 8, 256], BF16) for _ in range(3)]
        wsl = [(w_, (t_, self.tok("wslb", self.phase_id, i_))) for i_, (w_, t_) in enumerate(wsl)]
        wdn = [self.A([128, 22, 128], BF16) for _ in range(3)]
        ub = [self.A([128, 2 + 512], F32) for _ in range(4)]
        t0b = [self.A([128, 512], F32) for _ in range(4)]
        sab = [self.A([128, 512], F32) for _ in range(2)]
        if hasS:
            SHF, tSHF = self.A([128, NFC, 32], F32)
            stg, tstg = self.A([32, 5632], F32)
        xnt = lambda dc, tile: (xn[:, dc, tile[0].off + tile[1]:tile[0].off + tile[1] + tile[2]], self.tok("xn", self.phase_id, dc, tile[0].off + tile[1]))
        self.norm(st, "nf%d" % l, xnt, sq2, rsb)
        if hasS:
            self.dma("sp", stg, self.st_ffn[l], (), [tstg], "ldst")
            for g in range(0, NFC, 8):
                pb, tpb = self.bank()
                ng = min(8, NFC - g)
                for j in range(ng):
                    fc = g + j
                    self.tr(pb[:, j * 32:(j + 1) * 32], stg[0:32, fc * 128:(fc + 1) * 128], self.idf[0:32, 0:32], [tstg, self.tC], [tpb])
                self.cp("act", SHF[:, g:g + ng, :], pb[:, 0:ng * 32].rearrange("p (a b) -> p a b", b=32), [tpb], [tSHF])
        tHF = self.tok("HF")
        fcw, fcb = "fcw%d" % l, "fcb%d" % l
        def ld_wup(u):
            wt_, twt_ = wsl[u % 3]
            self.dma("pool", wt_[:, 0], self.w_up[l][:, u * 256:(u + 1) * 256].rearrange("(kc p) f -> p kc f", p=128), (), [twt_[0]], "ldw%d" % (u % 3))
            self.dma("pool", wt_[:, 1], self.w_up[l][:, DFF + u * 256:DFF + (u + 1) * 256].rearrange("(kc p) f -> p kc f", p=128), (), [twt_[1]], "ldwb%d" % (u % 3))

        def ld_wdn(dc):
            wd_, twd_ = wdn[dc % 3]
            self.dma("pool", wd_, self.w_down[l][:, dc * 128:(dc + 1) * 128].rearrange("(i p) d -> p i d", p=128), (), [twd_], "ldwd%d" % (dc % 3))
        ld_wup(0); ld_wup(1)
        pend = []
        for u in range(11):
            wt, twt = wsl[u % 3]
            if u + 2 < 11:
                ld_wup(u + 2)
            elif u + 2 == 11:
                ld_wdn(0)
            else:
                ld_wdn(1)
            for j in range(2):
                i = u * 2 + j
                prev = [None, None]
                for tile in st.tiles():
                    seg, t0, n = tile
                    c0 = seg.off + t0
                    conv = []
                    for ab in range(2):
                        fc = i + 22 * ab
                        pb, tpb = self.bank()
                        for kc in range(8):
                            self.mm(pb[:, 0:n], wt[:, ab, kc, j * 128:(j + 1) * 128], xn[:, kc, c0:c0 + n], [twt[ab], xnt(kc, tile)[1]], [tpb],
                                    start=(kc == 0), stop=(kc == 7))
                        ex, tex = ub[self.rr("ub", [0, 1, 2, 3])]
                        if seg.kind == "S":
                            self.cp("pool", self.ext_halo(seg, ex, 2), SHF[:, fc, :].rearrange("p (s r) -> p s r", r=2), [tSHF], [tex])
                        elif t0 == 0:
                            self.cp("pool", ex[:, 0:2], self.HF[:, l, fc, :], [tHF], [tex])
                        else:
                            pe_, tpe_, pn = prev[ab]
                            self.cp("pool", ex[:, 0:2], pe_[:, pn:pn + 2], [tpe_], [tex])
                        self.cp("act", self.ext_dst(seg, ex, 2, n), self.V(seg, pb[:, 0:n]), [tpb], [tex])
                        prev[ab] = (ex, tex, n)
                        if seg.kind == "S":
                            self.cp("pool", SHF[:, fc, :].rearrange("p (s r) -> p s r", r=2), self.ext_tail(seg, ex, 2, n), [tex], [tSHF])
                        elif t0 + n == seg.n:
                            self.cp("pool", self.HF[:, l, fc, :], ex[:, n:n + 2], [tex], [tHF])
                        tb, ttb = t0b[self.rr("t0b", [0, 1, 2, 3])]
                        tv = self.V(seg, tb[:, 0:n])
                        self.act(tv, self.V(seg, pb[:, 0:n]), AF.Identity, [tpb, self.tC], [ttb], bias=self.vcol(fcb, fc), scale=self.vcol(fcw, 2 * NFC + fc))
                        self.stt(tv, self.ext_tap(seg, ex, 2, 1, n), self.vcol(fcw, 1 * NFC + fc), tv, ALU.mult, ALU.add, [tex, ttb, self.tC], [ttb])
                        self.stt(tv, self.ext_tap(seg, ex, 2, 0, n), self.vcol(fcw, 0 * NFC + fc), tv, ALU.mult, ALU.add, [tex, ttb, self.tC], [ttb])
                        conv.append((tb, ttb))
                    def tail(conv=conv, i=i, c0=c0, n=n):
                        sa, tsa = sab[self.rr("sab", [0, 1])]
                        self.act(sa[:, 0:n], conv[0][0][:, 0:n], AF.Silu, [conv[0][1]], [tsa])
                        self.tt("dve", hT[:, i, c0:c0 + n], sa[:, 0:n], conv[1][0][:, 0:n], ALU.mult, [tsa, conv[1][1]], [self.tok("hT", self.phase_id, i, c0)])
                    if pend:
                        pend.pop(0)()
                    pend.append(tail)
        while pend:
            pend.pop(0)()
        if hasS:
            for fc in range(NFC):
                pb, tpb = self.bank()
                self.tr(pb[0:32, 0:128], SHF[:, fc, :], self.idf[:, :], [tSHF, self.tC], [tpb])
                self.cp(self.rr("ev", ["act", "dve"]), stg[0:32, fc * 128:(fc + 1) * 128], pb[0:32, 0:128], [tpb], [tstg])
            self.dma("sp", self.o_sffn[l], stg, [tstg], (), "stst")
        if st.last:
            stg2, tstg2 = self.A([2, 5632], F32)
            for fc in range(NFC):
                pb, tpb = self.bank()
                self.tr(pb[0:2, 0:128], self.HF[:, l, fc, :], self.idf[:, :], [tHF, self.tC], [tpb])
                self.cp(self.rr("ev", ["act", "dve"]), stg2[0:2, fc * 128:(fc + 1) * 128], pb[0:2, 0:128], [tpb], [tstg2])
            self.dma("sp", self.o_pffn[l], stg2, [tstg2], (), "stst")
        for dc in range(8):
            wd, twd = wdn[dc % 3]
            if dc + 2 < 8:
                ld_wdn(dc + 2)
            for tile in st.tiles():
                seg, t0, n = tile
                c0 = seg.off + t0
                pb, tpb = self.bank()
                for i in range(22):
                    self.mm(pb[:, 0:n], wd[:, i, :], hT[:, i, c0:c0 + n], [twd, self.tok("hT", self.phase_id, i, c0)], [tpb], start=(i == 0), stop=(i == 21))
                xv = self.xT[:, dc, c0:c0 + n]
                self.tt("dve", xv, pb[:, 0:n], xv, ALU.add, [tpb], [self.xtok(dc, tile)])

    def pool_mixer(self, st):
        self.phase()
        sq2 = [self.A([128, 8, 512], BF16) for _ in range(2)]
        rsb = [self.A([128, 512], F32) for _ in range(2)]
        pw, tpw = self.A([128, 4, 2, 256], BF16)
        self.dma("pool", pw, self.pool_w.rearrange("g (ci p) e -> p g ci e", p=128), (), [tpw], "ldw0")
        segbuf = {}
        for seg in st.segs:
            W = 16 * 23 if seg.kind == "S" else 15 + seg.n
            hn, _ = self.A([128, 8, W], F32)
            s1, _ = self.A([128, 2, W], F32)
            s2, _ = self.A([128, 2, W], F32)
            PL, _ = self.A([128, 8, seg.n], BF16)
            segbuf[id(seg)] = (hn, s1, s2, PL, W)
        if any(s.kind == "S" for s in st.segs):
            stg, tstg = self.A([120, 2, D], F32)
            self._pcb = [self.A([128, 120], F32) for _ in range(2)]
        tmp15, ttmp15 = self.A([128, 15], F32)
        tHP = self.tok("HP")

        def dstf(dc, tile):
            seg, t0, n = tile
            hn = segbuf[id(seg)][0]
            if seg.kind == "S":
                ap = hn[:, dc, :].rearrange("p (s w) -> p s w", w=23)[:, :, 15:23]
            else:
                ap = hn[:, dc, 15 + t0:15 + t0 + n]
            return ap, self.tok("hn", self.phase_id, id(seg), dc)

        self._norm_pool(st, "nm1", dstf, sq2, rsb)
        for seg in st.segs:
            hn, s1, s2, PL, W = segbuf[id(seg)]
            n = seg.n
            if seg.kind == "S":
                for half in range(2):
                    self.dma("sp", stg[:, half, :], self.st_pool[half * 120:(half + 1) * 120, :], (), [tstg], "ldst")
                for dc in range(8):
                    pb, tpb = self.bank()
                    for half in range(2):
                        self.tr(pb[:, half * 120:(half + 1) * 120], stg[0:120, half, dc * 128:(dc + 1) * 128], self.idf[0:120, 0:120], [tstg, self.tC], [tpb])
                    self.cp(self.rr("ev", ["act", "dve"]), hn[:, dc, :].rearrange("p (s w) -> p s w", w=23)[:, :, 0:15],
                            pb[:, 0:240].rearrange("p (s r) -> p s r", r=15), [tpb], [self.tok("hn", self.phase_id, id(seg), dc)])
            else:
                for dc in range(8):
                    self.cp("pool", hn[:, dc, 0:15], self.HP[:, dc, :], [tHP], [self.tok("hn", self.phase_id, id(seg), dc)])
            if seg.kind == "S":
                e3 = lambda ap: ap.rearrange("p (s w) -> p s w", w=23)
                sl = lambda ap, a, b: e3(ap)[:, :, a:b]
                WW = 23
            else:
                sl = lambda ap, a, b: ap[:, a:b]
                WW = W
            for dc in range(8):
                gi = dc // 2
                th = self.tok("hn", self.phase_id, id(seg), dc)
                ts1 = self.tok("ps1", self.phase_id, id(seg), dc % 2)
                ts2 = self.tok("ps2", self.phase_id, id(seg), dc % 2)
                src, tsrc = hn[:, dc, :], th
                bufs = [(s1[:, dc % 2, :], ts1), (s2[:, dc % 2, :], ts2)]
                for lev in range(gi + 1):
                    sh = 1 << lev
                    lo = (1 << (lev + 1)) - 1
                    dstb, tdb = bufs[lev % 2]
                    self.tt("pool", sl(dstb, lo, WW), sl(src, lo, WW), sl(src, lo - sh, WW - sh), ALU.add, [tsrc], [tdb])
                    src, tsrc = dstb, tdb
                if seg.kind == "S":
                    outv = PL[:, dc, :].rearrange("p (s t) -> p s t", t=8)
                else:
                    outv = PL[:, dc, :]
                tPL = self.tok("PL", self.phase_id, id(seg), dc)
                self.stt(outv, sl(src, 15, WW), 1.0 / WINS[gi], sl(hn[:, dc, :], 15, WW), ALU.mult, ALU.subtract, [tsrc, th], [tPL])
                if seg.kind == "P" and seg.pos0 == 0:
                    ic = self.cst[:, C_INVC + gi * 15:C_INVC + gi * 15 + 15]
                    self.tt("dve", tmp15, src[:, 15:30], ic, ALU.mult, [tsrc, self.tC], [ttmp15])
                    self.tt("dve", PL[:, dc, 0:15], tmp15, hn[:, dc, 15:30], ALU.subtract, [ttmp15, th], [tPL])
            if seg.kind == "S":
                for dc in range(8):
                    th = self.tok("hn", self.phase_id, id(seg), dc)
                    for half in range(2):
                        pb, tpb = self.bank()
                        src3 = hn[:, dc, :].rearrange("p (s w) -> p s w", w=23)[:, half * 8:(half + 1) * 8, 8:23]
                        cbuf, tcb = self._pcb[self.rr("pcb", [0, 1])]
                        self.cp("pool", cbuf.rearrange("p (s r) -> p s r", r=15), src3, [th], [tcb])
                        self.tr(pb[0:120, 0:128], cbuf, self.idf[:, :], [tcb, self.tC], [tpb])
                        self.cp(self.rr("ev", ["act", "dve"]), stg[0:120, half, dc * 128:(dc + 1) * 128], pb[0:120, 0:128], [tpb], [tstg])
                for half in range(2):
                    self.dma("sp", self.o_spool[half * 120:(half + 1) * 120, :], stg[:, half, :], [tstg], (), "stst")
            else:
                for dc in range(8):
                    th = self.tok("hn", self.phase_id, id(seg), dc)
                    self.cp("pool", self.HP[:, dc, :], hn[:, dc, n:n + 15], [th], [tHP])
                if st.last:
                    stg2, tstg2 = self.A([15, D], F32)
                    for dc in range(8):
                        pb, tpb = self.bank()
                        self.tr(pb[0:15, 0:128], self.HP[:, dc, :], self.idf[:, :], [tHP, self.tC], [tpb])
                        self.cp(self.rr("ev", ["act", "dve"]), stg2[0:15, dc * 128:(dc + 1) * 128], pb[0:15, 0:128], [tpb], [tstg2])
                    self.dma("sp", self.o_ppool, stg2, [tstg2], (), "stst")
            for tile in seg.tiles():
                _, t0, nn = tile
                c0 = seg.off + t0
                for gi in range(4):
                    for eo in range(2):
                        dco = 2 * gi + eo
                        pb, tpb = self.bank()
                        for ci in range(2):
                            self.mm(pb[:, 0:nn], pw[:, gi, ci, eo * 128:(eo + 1) * 128], PL[:, 2 * gi + ci, t0:t0 + nn],
                                    [tpw, self.tok("PL", self.phase_id, id(seg), 2 * gi + ci)], [tpb], start=(ci == 0), stop=(ci == 1))
                        xv = self.xT[:, dco, c0:c0 + nn]
                        self.stt(xv, pb[:, 0:nn], self.vcol("psc", dco), xv, ALU.mult, ALU.add, [tpb, self.tC], [self.xtok(dco, tile)])

    def _norm_pool(self, st, wname, dstf, sq2, rsb):
        for tile in st.tiles():
            seg, t0, n = tile
            c0 = seg.off + t0
            sq, tsq = sq2[self.rr("sq2", [0, 1])]
            rs, trs = rsb[self.rr("rsb", [0, 1])]
            pb, tpb = self.bank()
            for dc in range(8):
                xin = self.xT[:, dc, c0:c0 + n]
                if dc % 2 == 0:
                    self.tt("pool", sq[:, dc, 0:n], xin, xin, ALU.mult, [self.xtok(dc, tile)], [tsq])
                else:
                    self.act(sq[:, dc, 0:n], xin, AF.Square, [self.xtok(dc, tile)], [tsq])
            for dc in range(8):
                self.mm(pb[:, 0:n], self.ones_m[:], sq[:, dc, 0:n], [tsq, self.tC], [tpb], start=(dc == 0), stop=(dc == 7))
            self.act(rs[:, 0:n], pb[:, 0:n], AF.Ln, [tpb], [trs], bias=EPS)
            self.act(rs[:, 0:n], rs[:, 0:n], AF.Exp, [trs], [trs], scale=-0.5)
            for dc in range(8):
                dst, tdst = dstf(dc, tile)
                self.stt(dst, self.V(seg, self.xT[:, dc, c0:c0 + n]), self.vcol(wname, dc), self.V(seg, rs[:, 0:n]), ALU.mult, ALU.mult,
                         [self.xtok(dc, tile), trs, self.tC], [tdst])

    def final(self, st):
        self.phase()
        sq2 = [self.A([128, 8, 512], BF16) for _ in range(2)]
        rsb = [self.A([128, 512], F32) for _ in range(2)]
        yT = [self.A([128, 8, 512], F32) for _ in range(2)]
        ysg = [self.A([128, D], F32) for _ in range(3)]
        cur = {}

        def dstf(dc, tile):
            return cur["y"][0][:, dc, 0:tile[2]], cur["y"][1]

        for tile in st.tiles():
            seg, t0, n = tile
            cur["y"] = yT[self.rr("yT", [0, 1])]
            self._norm_one(tile, "nfin", dstf, sq2, rsb)
            y, ty = cur["y"]
            b0 = 0
            while b0 < n:
                if seg.kind == "P":
                    pos = seg.pos0 + t0 + b0
                    if pos < NMETA:
                        b0 += NMETA - pos
                        continue
                m = min(128, n - b0)
                sg, tsg = ysg[self.rr("ysg", [0, 1, 2])]
                for half in range(2):
                    pb, tpb = self.bank()
                    for j in range(4):
                        dc = half * 4 + j
                        self.tr(pb[0:m, j * 128:(j + 1) * 128], y[:, dc, b0:b0 + m], self.idf[:, :], [ty, self.tC], [tpb])
                    self.cp(("act", "dve")[half], sg[0:m, half * 512:(half + 1) * 512], pb[0:m, :], [tpb], [tsg])
                if seg.kind == "S":
                    self.dma("sp", self.ys[b0:b0 + m, :], sg[0:m, :], [tsg], (), "sty%d" % ((self.rrc["ysg"] - 1) % 3))
                else:
                    r0 = seg.pos0 + t0 + b0 - NMETA
                    self.dma("sp", self.yp[r0:r0 + m, :], sg[0:m, :], [tsg], (), "sty%d" % ((self.rrc["ysg"] - 1) % 3))
                b0 += m

    def _norm_one(self, tile, wname, dstf, sq2, rsb):
        seg, t0, n = tile
        c0 = seg.off + t0
        sq, tsq = sq2[self.rr("sq2", [0, 1])]
        rs, trs = rsb[self.rr("rsb", [0, 1])]
        pb, tpb = self.bank()
        for dc in range(8):
            xin = self.xT[:, dc, c0:c0 + n]
            if dc % 2 == 0:
                self.tt("pool", sq[:, dc, 0:n], xin, xin, ALU.mult, [self.xtok(dc, tile)], [tsq])
            else:
                self.act(sq[:, dc, 0:n], xin, AF.Square, [self.xtok(dc, tile)], [tsq])
        for dc in range(8):
            self.mm(pb[:, 0:n], self.ones_m[:], sq[:, dc, 0:n], [tsq, self.tC], [tpb], start=(dc == 0), stop=(dc == 7))
        self.act(rs[:, 0:n], pb[:, 0:n], AF.Ln, [tpb], [trs], bias=EPS)
        self.act(rs[:, 0:n], rs[:, 0:n], AF.Exp, [trs], [trs], scale=-0.5)
        for dc in range(8):
            dst, tdst = dstf(dc, tile)
            self.stt(dst, self.xT[:, dc, c0:c0 + n], self.vcol(wname, dc), rs[:, 0:n], ALU.mult, ALU.mult,
                     [self.xtok(dc, tile), trs, self.tC], [tdst])


_NC_CACHE = {}


def _get_nc():
    if "nc" not in _NC_CACHE:
        b = Builder()
        _NC_CACHE["nc"] = b.build()
    return _NC_CACHE["nc"]


def kernel(**inp):
    inp = {k: np.asarray(v) for k, v in inp.items()}
    f = lambda a: np.ascontiguousarray(a, dtype=np.float32)
    nc = _get_nc()
    vecs = build_vecs(inp)
    consts = build_consts()
    w_in = f(inp["gdn_w_in"][0])
    wba = np.zeros((D, 40), np.float32)
    wba[:, 0:8] = w_in[:, 4104:4112]
    wba[:, 32:40] = w_in[:, 4096:4104]
    shared = {
        "meta": f(inp["meta_tokens"]), "w_in": w_in, "wba": wba, "w_out": f(inp["gdn_w_out"][0]),
        "pool_w": f(inp["pool_w"][0]), "w_up": f(inp["ffn_w_up"]), "w_down": f(inp["ffn_w_down"]),
        "vecs": vecs, "consts": consts,
    }
    in_maps = []
    for c in range(8):
        sl = slice(16 * c, 16 * c + 16)
        m = dict(shared)
        m["xp"] = f(inp["x_prompt"][c])
        m["xs"] = f(inp["x_sample"][sl].reshape(128, D))
        m["st_conv"] = f(inp["state_gdn_conv"][0, sl].reshape(48, 3072))
        m["st_rec"] = f(inp["state_gdn_rec"][0, sl])
        m["st_pool"] = f(inp["state_pool"][0, sl].reshape(240, D))
        m["st_ffn"] = f(inp["state_ffn_conv"][:, sl].reshape(2, 32, 5632))
        in_maps.append(m)
    res = run_bass_kernel_spmd(nc, in_maps, core_ids=list(range(8)))
    R = res.results
    g = lambda k: [np.asarray(r[k], dtype=np.float32) for r in R]
    y_prompt = np.stack(g("yp"), 0)
    y_sample = np.concatenate(g("ys"), 0).reshape(128, 8, D)
    p_conv = np.stack(g("o_pconv"), 0)[None]
    p_rec = np.stack(g("o_prec"), 0)[None]
    p_pool = np.stack(g("o_ppool"), 0)[None]
    p_ffn = np.stack(g("o_pffn"), 1)
    s_conv = np.concatenate([a.reshape(16, 3, 3072) for a in g("o_sconv")], 0)[None]
    s_rec = np.concatenate(g("o_srec"), 0)[None]
    s_pool = np.concatenate([a.reshape(16, 15, D) for a in g("o_spool")], 0)[None]
    s_ffn = np.concatenate([a.reshape(2, 16, 2, 5632) for a in g("o_sffn")], 1)
    return (y_prompt, y_sample, p_conv, p_rec, p_pool, p_ffn, s_conv, s_rec, s_pool, s_ffn)
```

```python
import contextlib
import numpy as np
import concourse.bass as bass
import concourse.mybir as mybir
from concourse.bass_utils import run_bass_kernel_spmd

F32 = mybir.dt.float32
BF16 = mybir.dt.bfloat16
ALU = mybir.AluOpType
AF = mybir.ActivationFunctionType
AX = mybir.AxisListType

D = 1024
NH = 8
DFF = 2816
NFC = 44
SEQ = 2048
NMETA = 16
EPS = 1e-6
NEG = -1.0e30
DEBUG_MAP = None
WINS = (2, 4, 8, 16)


class Tok:
    __slots__ = ("lastw", "readers", "excl")

    def __init__(self):
        self.lastw = None
        self.readers = []
        self.excl = False


class Op:
    __slots__ = ("eng", "fn", "deps", "ms", "dma_sem", "dma_val", "is_dma", "where")

    def __init__(self, eng, fn):
        import sys as _s
        f = _s._getframe(3)
        self.where = (f.f_lineno, f.f_back.f_lineno if f.f_back else 0)
        self.eng = eng
        self.fn = fn
        self.deps = []
        self.ms = None
        self.is_dma = False
        self.dma_sem = None
        self.dma_val = 0


class Prog:
    ENGS = ("pe", "act", "dve", "pool", "sp")

    def __init__(self, nc):
        self.nc = nc
        self.ops = {e: [] for e in self.ENGS}
        self.streams = {}
        self.pending = {}

    def barrier(self):
        lasts = [self.ops[e][-1] for e in self.ENGS if self.ops[e]]
        lasts += [st[0] for st in self.streams.values() if st[0] is not None]
        for e in self.ENGS:
            self.pending[e] = list(lasts)

    def op(self, eng, fn, reads=(), writes=(), stream=None):
        o = Op(eng, fn)
        is_dma = stream is not None
        deps = []
        for t in reads:
            if t.lastw is not None:
                deps.append((t.lastw, True))
            if t.excl:
                for r in t.readers:
                    if r.eng != eng:
                        deps.append((r, True))
        for t in writes:
            if t.lastw is not None:
                deps.append((t.lastw, False))
            for r in t.readers:
                deps.append((r, False))
        for d in self.pending.pop(eng, []):
            deps.append((d, True))
        if is_dma:
            o.is_dma = True
            st = self.streams.setdefault(stream, [None, 0])
            if st[0] is not None:
                deps.append((st[0], True))
            st[1] += 1
            o.dma_sem = stream
            o.dma_val = 16 * st[1]
            st[0] = o
        seen = set()
        for d, raw in deps:
            if d is o or id(d) in seen:
                continue
            if (not d.is_dma) and (not is_dma) and d.eng == eng and eng == "pe":
                continue
            seen.add(id(d))
            o.deps.append(d)
        for t in reads:
            t.readers.append(o)
        for t in writes:
            t.lastw = o
            t.readers = []
        self.ops[eng].append(o)
        return o

    def emit(self):
        nc = self.nc
        for e in self.ENGS:
            for o in self.ops[e]:
                for d in o.deps:
                    if not d.is_dma:
                        d.ms = True
        for e in self.ENGS:
            k = 0
            for o in self.ops[e]:
                if o.ms and not o.is_dma:
                    k += 1
                    o.ms = k
        with contextlib.ExitStack() as es:
            esem = {e: es.enter_context(nc.semaphore("s_" + e)) for e in self.ENGS}
            dsem = {k: es.enter_context(nc.semaphore("d_%d" % i)) for i, k in enumerate(self.streams)}
            block = es.enter_context(nc.Block())
            prog = self

            def run(e, engobj):
                seen = {}
                for o in prog.ops[e]:
                    for d in o.deps:
                        if d.is_dma:
                            key, val, sem = ("d", d.dma_sem), d.dma_val, dsem[d.dma_sem]
                        else:
                            key, val, sem = ("e", d.eng), d.ms, esem[d.eng]
                        if seen.get(key, 0) >= val:
                            continue
                        seen[key] = val
                        engobj.wait_ge(sem, val)
                    ins = o.fn(engobj)
                    if DEBUG_MAP is not None:
                        try:
                            DEBUG_MAP[str(ins.ins.name)] = o.where
                        except Exception as ex:
                            DEBUG_MAP["err"] = repr(ex)
                    if o.is_dma:
                        ins.then_inc(dsem[o.dma_sem], 16)
                    elif o.ms:
                        ins.then_inc(esem[e], 1)
                if e == "sp":
                    for k, st in prog.streams.items():
                        engobj.wait_ge(dsem[k], 16 * st[1])

            block.tensor(lambda eng: run("pe", eng))
            block.scalar(lambda eng: run("act", eng))
            block.vector(lambda eng: run("dve", eng))
            block.gpsimd(lambda eng: run("pool", eng))
            block.sync(lambda eng: run("sp", eng))


VEC_COLS = {}


def _vec_layout():
    off = 0
    for name, n in (("nm0", 8), ("nm1", 8), ("nf0", 8), ("nf1", 8), ("nfin", 8),
                    ("gcw", 96), ("fcw0", 132), ("fcw1", 132), ("fcb0", 44), ("fcb1", 44),
                    ("psc", 8), ("gnw", 1), ("alog", 1), ("dtb", 1)):
        VEC_COLS[name] = off
        off += n
    return off


NV = _vec_layout()
C_ID, C_TRI, C_NEGU, C_POSL, C_INVC = 0, 128, 192, 256, 320
C_TRI8, C_NEGU8, C_POSL8, C_SEL, C_RM = 380, 444, 508, 572, 636
NCONST = 636 + 8


def build_consts():
    c = np.zeros((128, NCONST), np.float32)
    c[:, C_ID:C_ID + 128] = np.eye(128, dtype=np.float32)
    p = np.arange(64)[:, None]
    f = np.arange(64)[None, :]
    c[:64, C_TRI:C_TRI + 64] = (f >= p).astype(np.float32)
    c[:64, C_NEGU:C_NEGU + 64] = np.where(f >= p, 0.0, NEG)
    c[:64, C_POSL:C_POSL + 64] = np.where(f < p, 0.0, -NEG)
    for gi, w in enumerate(WINS):
        for t in range(15):
            c[:, C_INVC + gi * 15 + t] = 1.0 / min(w, t + 1)
    same = (p // 8) == (f // 8)
    c[:64, C_TRI8:C_TRI8 + 64] = (same & (f >= p)).astype(np.float32)
    c[:64, C_NEGU8:C_NEGU8 + 64] = np.where(same & (f >= p), 0.0, NEG)
    c[:64, C_POSL8:C_POSL8 + 64] = np.where(same & (f < p), 0.0, -NEG)
    c[:64, C_SEL:C_SEL + 64] = (p == 8 * (f // 8) + 7).astype(np.float32)
    c[:64, C_RM:C_RM + 8] = ((p // 8) == np.arange(8)[None, :]).astype(np.float32)
    return c


def build_vecs(inp):
    v = np.zeros((128, NV), np.float32)

    def put(name, arr):
        a = np.asarray(arr, np.float32).reshape(-1, 128).T
        v[:, VEC_COLS[name]:VEC_COLS[name] + a.shape[1]] = a

    put("nm0", inp["norm_mix"][0]); put("nm1", inp["norm_mix"][1])
    put("nf0", inp["norm_ffn"][0]); put("nf1", inp["norm_ffn"][1])
    put("nfin", inp["norm_final"])
    put("gcw", inp["gdn_conv_w"][0].reshape(-1))
    put("fcw0", inp["ffn_conv_w"][0].reshape(-1)); put("fcw1", inp["ffn_conv_w"][1].reshape(-1))
    put("fcb0", inp["ffn_conv_b"][0]); put("fcb1", inp["ffn_conv_b"][1])
    put("psc", inp["pool_scale"][0])
    put("gnw", inp["gdn_norm_w"][0])
    v[0:8, VEC_COLS["alog"]] = inp["gdn_A_log"][0]
    v[0:8, VEC_COLS["dtb"]] = inp["gdn_dt_bias"][0]
    return v


class Seg:
    def __init__(self, kind, n, off, pos0=0):
        self.kind, self.n, self.off, self.pos0 = kind, n, off, pos0

    def tiles(self):
        if self.kind == "S":
            return [(self, 0, 128)]
        k = (self.n + 511) // 512
        base = (self.n // k + 7) // 8 * 8
        out, t = [], 0
        while t < self.n:
            m = min(base, self.n - t)
            out.append((self, t, m))
            t += m
        return out


class ST:
    def __init__(self, segs, first, last):
        self.segs, self.first, self.last = segs, first, last
        self.NT = sum(s.n for s in segs)

    def tiles(self):
        return [t for s in self.segs for t in s.tiles()]


SUPER = [
    ST([Seg("P", 592, 0, 0), Seg("S", 128, 592)], True, False),
    ST([Seg("P", 704, 0, 592)], False, False),
    ST([Seg("P", 768, 0, 1296)], False, True),
]
NTMAX = 768


class Builder:
    def __init__(self):
        self.nc = nc = bass.Bass("TRN2", target_bir_lowering=False)
        self.P = Prog(nc)
        self.es = contextlib.ExitStack()
        self.toks = {}
        self.rrc = {}
        self.phase_id = 0

        def din(name, shape):
            return nc.dram_tensor(name, list(shape), F32, kind="ExternalInput").ap()

        def dout(name, shape):
            return nc.dram_tensor(name, list(shape), F32, kind="ExternalOutput").ap()

        self.xp = din("xp", [SEQ, D]); self.xs = din("xs", [128, D])
        self.st_conv = din("st_conv", [48, 3072]); self.st_rec = din("st_rec", [16, 8, 128, 128])
        self.st_pool = din("st_pool", [240, D]); self.st_ffn = din("st_ffn", [2, 32, 5632])
        self.meta = din("meta", [NMETA, D])
        self.w_in = din("w_in", [D, 4112]); self.wba = din("wba", [D, 40])
        self.w_out = din("w_out", [D, D]); self.pool_w = din("pool_w", [4, 256, 256])
        self.w_up = din("w_up", [2, D, 5632]); self.w_down = din("w_down", [2, DFF, D])
        self.vecs_d = din("vecs", [128, NV]); self.consts_d = din("consts", [128, NCONST])
        self.yp = dout("yp", [SEQ, D]); self.ys = dout("ys", [128, D])
        self.o_pconv = dout("o_pconv", [3, 3072]); self.o_prec = dout("o_prec", [8, 128, 128])
        self.o_ppool = dout("o_ppool", [15, D]); self.o_pffn = dout("o_pffn", [2, 2, 5632])
        self.o_sconv = dout("o_sconv", [48, 3072]); self.o_srec = dout("o_srec", [16, 8, 128, 128])
        self.o_spool = dout("o_spool", [240, D]); self.o_sffn = dout("o_sffn", [2, 32, 5632])

    def tok(self, *key):
        t = self.toks.get(key)
        if t is None:
            t = self.toks[key] = Tok()
            if key[0] == "bank":
                t.excl = True
        return t

    def sb(self, name, shape, dt):
        return self.es.enter_context(self.nc.sbuf_tensor(name, list(shape), dt))

    def rr(self, name, choices):
        i = self.rrc.get(name, 0)
        self.rrc[name] = i + 1
        return choices[i % len(choices)]

    def bank(self):
        i = self.rrc.get("bank", 0)
        self.rrc["bank"] = i + 1
        i %= 8
        return self.banks[i], self.tok("bank", i)

    def phase(self):
        self.P.barrier()
        self.aoff = 0
        self.phase_id += 1

    def A(self, shape, dt, key=None):
        n = int(np.prod(shape[1:]))
        nb = n * (4 if dt == F32 else 2)
        nb = (nb + 31) // 32 * 32
        ne = nb // 2
        assert self.aoff + ne <= self.arena_n, ("arena overflow", self.aoff, ne, self.arena_n)
        ap = self.arena[0:shape[0], self.aoff:self.aoff + ne]
        self.aoff += ne
        if dt == F32:
            ap = ap.bitcast(F32)
        ap = ap[:, 0:n]
        if len(shape) == 3:
            ap = ap.rearrange("p (a b) -> p a b", b=shape[2])
        elif len(shape) == 4:
            ap = ap.rearrange("p (a b c) -> p a b c", b=shape[2], c=shape[3])
        return ap, self.tok("arena", self.phase_id, self.aoff)

    def mm(self, out, lhsT, rhs, r, w, start=True, stop=True, skip=False):
        if skip:
            self.P.op("pe", lambda e: e.matmul(out, lhsT=lhsT, rhs=rhs, start=start, stop=stop, skip_group_check=True), r, w)
        else:
            self.P.op("pe", lambda e: e.matmul(out, lhsT=lhsT, rhs=rhs, start=start, stop=stop), r, w)

    def tr(self, out, in_, ident, r, w):
        self.P.op("pe", lambda e: e.transpose(out=out, in_=in_, identity=ident), r, w)

    def act(self, out, in_, func, r, w, bias=None, scale=None):
        kw = {}
        if bias is not None:
            kw["bias"] = bias
        if scale is not None:
            kw["scale"] = scale
        self.P.op("act", lambda e: e.activation(out=out, in_=in_, func=func, **kw), r, w)

    def tt(self, eng, out, in0, in1, op, r, w):
        self.P.op(eng, lambda e: e.tensor_tensor(out=out, in0=in0, in1=in1, op=op), r, w)

    def ts(self, eng, out, in0, s1, op0, r, w, s2=None, op1=None):
        if op1 is None:
            self.P.op(eng, lambda e: e.tensor_scalar(out=out, in0=in0, scalar1=s1, scalar2=None, op0=op0), r, w)
        else:
            self.P.op(eng, lambda e: e.tensor_scalar(out=out, in0=in0, scalar1=s1, scalar2=s2, op0=op0, op1=op1), r, w)

    def stt(self, out, in0, scalar, in1, op0, op1, r, w):
        self.P.op("dve", lambda e: e.scalar_tensor_tensor(out=out, in0=in0, scalar=scalar, in1=in1, op0=op0, op1=op1), r, w)

    def cp(self, eng, out, in_, r, w):
        if eng == "act":
            self.act(out, in_, AF.Copy, r, w)
        else:
            self.P.op(eng, lambda e: e.tensor_copy(out=out, in_=in_), r, w)

    def dma(self, q, out, in_, r, w, stream):
        self.P.op(q, lambda e: e.dma_start(out=out, in_=in_), r, w, stream=stream)

    def memset(self, eng, ap, val, w):
        self.P.op(eng, lambda e: e.memset(ap, val), (), w)

    def vcol(self, name, j=0, np_=128):
        c = VEC_COLS[name] + j
        return self.vecs[0:np_, c:c + 1]

    @staticmethod
    def V(seg, ap):
        if seg.kind == "S":
            return ap.rearrange("p (s t) -> p s t", t=8)
        return ap

    @staticmethod
    def ext_dst(seg, buf, H, n):
        if seg.kind == "S":
            return buf[:, 0:16 * (H + 8)].rearrange("p (s w) -> p s w", w=H + 8)[:, :, H:H + 8]
        return buf[:, H:H + n]

    @staticmethod
    def ext_tap(seg, buf, H, j, n):
        if seg.kind == "S":
            return buf[:, 0:16 * (H + 8)].rearrange("p (s w) -> p s w", w=H + 8)[:, :, j:j + 8]
        return buf[:, j:j + n]

    @staticmethod
    def ext_halo(seg, buf, H):
        if seg.kind == "S":
            return buf[:, 0:16 * (H + 8)].rearrange("p (s w) -> p s w", w=H + 8)[:, :, 0:H]
        return buf[:, 0:H]

    @staticmethod
    def ext_tail(seg, buf, H, n):
        if seg.kind == "S":
            return buf[:, 0:16 * (H + 8)].rearrange("p (s w) -> p s w", w=H + 8)[:, :, 8:8 + H]
        return buf[:, n:n + H]

    def build(self):
        nc = self.nc
        with self.es:
            self.xT = self.sb("xT", [128, 8, NTMAX], F32)
            self.S32 = self.sb("S32", [128, 8, 128], F32)
            self.S16 = self.sb("S16", [128, 8, 128], BF16)
            self.HG = self.sb("HG", [128, 24, 3], F32)
            self.HF = self.sb("HF", [128, 2, NFC, 2], F32)
            self.HP = self.sb("HP", [128, 8, 15], F32)
            self.vecs = self.sb("vecs_sb", [128, NV], F32)
            self.cst = self.sb("cst", [128, NCONST], F32)
            self.idb = self.sb("idb", [128, 128], BF16)
            self.ones_m = self.sb("ones_m", [128, 128], BF16)
            self.ones_1 = self.sb("ones_1", [128, 128], BF16)
            self.ones_f = self.sb("ones_f", [64, 128], F32)
            self.nexpA = self.sb("nexpA", [8, 1], F32)
            self.lnq = self.sb("lnq", [128, 1], F32)
            self.banks = [self.es.enter_context(nc.psum_tensor("pb%d" % i, [128, 512], F32)) for i in range(8)]
            rem = nc.sbuf_bytes_remaining - 2048
            self.arena_n = (rem // 2) // 64 * 64
            self.arena = self.sb("arena", [128, self.arena_n], BF16)
            self.aoff = 0
            self.idf = self.cst[:, C_ID:C_ID + 128]
            tC = self.tok("consts")
            self.dma("sp", self.vecs[:], self.vecs_d, (), [tC], "ldc0")
            self.dma("sp", self.cst[:], self.consts_d, (), [tC], "ldc1")
            self.cp("dve", self.idb[:], self.idf, [tC], [tC])
            self.memset("pool", self.ones_m[:], 1.0 / 1024.0, [tC])
            self.memset("pool", self.ones_1[:], 1.0, [tC])
            self.memset("pool", self.ones_f[:], 1.0, [tC])
            self.memset("pool", self.lnq[:], -0.5 * float(np.log(128.0)), [tC])
            self.memset("pool", self.S32[:], 0.0, [self.tok("S32")])
            self.memset("pool", self.S16[:], 0.0, [self.tok("S16")])
            self.memset("pool", self.HG[:], 0.0, [self.tok("HG")])
            self.memset("pool", self.HF[:], 0.0, [self.tok("HF")])
            self.memset("pool", self.HP[:], 0.0, [self.tok("HP")])
            self.act(self.nexpA[:], self.vcol("alog", 0, 8), AF.Exp, [tC], [tC])
            self.ts("dve", self.nexpA[:], self.nexpA[:], -1.0, ALU.mult, [tC], [tC])
            self.tC = tC
            for st in SUPER:
                self.run_super(st)
            self.P.emit()
        return nc

    def run_super(self, st):
        self.load_x(st)
        self.gdn(st)
        self.ffn(st, 0)
        self.pool_mixer(st)
        self.ffn(st, 1)
        self.final(st)

    def xtok(self, dc, tile):
        return self.tok("xT", dc, tile[0].off + tile[1])

    def load_x(self, st):
        if st.first:
            self.phase()
        stg = [self.A([128, 4, D], F32) for _ in range(2)]
        bi = 0
        for seg in st.segs:
            for (_, t0, n) in seg.tiles():
                sg, tsg = stg[bi % 2]
                bi += 1
                nb = (n + 127) // 128
                for b in range(nb):
                    m = min(128, n - b * 128)
                    if seg.kind == "S":
                        self.dma("sp", sg[0:m, b, :], self.xs[0:m, :], (), [tsg], "ldx")
                    else:
                        p0 = seg.pos0 + t0 + b * 128
                        r = 0
                        if p0 < NMETA:
                            k = min(m, NMETA - p0)
                            self.dma("sp", sg[0:k, b, :], self.meta[p0:p0 + k, :], (), [tsg], "ldx")
                            r = k
                        if r < m:
                            a = p0 + r - NMETA
                            self.dma("sp", sg[r:m, b, :], self.xp[a:a + (m - r), :], (), [tsg], "ldx")
                tile = (seg, t0, n)
                c0 = seg.off + t0
                for dc in range(8):
                    pb, tpb = self.bank()
                    for b in range(nb):
                        m = min(128, n - b * 128)
                        self.tr(pb[:, b * 128:b * 128 + m], sg[0:m, b, dc * 128:(dc + 1) * 128], self.idf[0:m, 0:m],
                                [tsg, self.tC], [tpb])
                    self.cp(self.rr("ev", ["act", "dve"]), self.xT[:, dc, c0:c0 + n], pb[:, 0:n], [tpb], [self.xtok(dc, tile)])

    def norm(self, st, wname, dstf, sq2, rsb):
        for tile in st.tiles():
            seg, t0, n = tile
            c0 = seg.off + t0
            sq, tsq = sq2[self.rr("sq2", [0, 1])]
            rs, trs = rsb[self.rr("rsb", [0, 1])]
            pb, tpb = self.bank()
            for dc in range(8):
                xin = self.xT[:, dc, c0:c0 + n]
                if dc % 2 == 0:
                    self.tt("pool", sq[:, dc, 0:n], xin, xin, ALU.mult, [self.xtok(dc, tile)], [tsq])
                else:
                    self.act(sq[:, dc, 0:n], xin, AF.Square, [self.xtok(dc, tile)], [tsq])
            for dc in range(8):
                self.mm(pb[:, 0:n], self.ones_m[:], sq[:, dc, 0:n], [tsq, self.tC], [tpb], start=(dc == 0), stop=(dc == 7))
            self.act(rs[:, 0:n], pb[:, 0:n], AF.Ln, [tpb], [trs], bias=EPS)
            self.act(rs[:, 0:n], rs[:, 0:n], AF.Exp, [trs], [trs], scale=-0.5)
            for dc in range(8):
                dst, tdst = dstf(dc, tile)
                self.stt(dst, self.xT[:, dc, c0:c0 + n], self.vcol(wname, dc), rs[:, 0:n], ALU.mult, ALU.mult,
                         [self.xtok(dc, tile), trs, self.tC], [tdst])

    def gdn(self, st):
        NT = st.NT
        self.phase()
        hasS = any(s.kind == "S" for s in st.segs)
        xn, _ = self.A([128, 8, NT], BF16)
        QKVZ, _ = self.A([128, 32, NT], BF16)
        GB, tGB = self.A([40, NT], F32)
        mark = self.aoff
        sq2 = [self.A([128, 8, 512], BF16) for _ in range(2)]
        rsb = [self.A([128, 512], F32) for _ in range(2)]
        wsl = [self.A([128, 8, 512], BF16) for _ in range(3)]
        wbat, twba = self.A([128, 8, 40], BF16)
        ext = [self.A([128, 3 + 512], F32) for _ in range(3)]
        acc = [self.A([128, 512], F32) for _ in range(2)]
        sil = [self.A([128, 512], F32) for _ in range(2)]
        sqh = [self.A([128, 512], BF16) for _ in range(3)]
        rin = [self.A([128, 512], F32) for _ in range(2)]
        bat = [self.A([8, 512], F32) for _ in range(4)]
        if hasS:
            SHG, tSHG = self.A([128, 24, 48], F32)
            stg, tstg = self.A([48, 3072], F32)
        self.memset("pool", GB, 0.0, [tGB])
        xnt = lambda dc, tile: (xn[:, dc, tile[0].off + tile[1]:tile[0].off + tile[1] + tile[2]], self.tok("xn", self.phase_id, dc, tile[0].off + tile[1]))
        self.norm(st, "nm0", xnt, sq2, rsb)
        if hasS:
            self.dma("sp", stg, self.st_conv, (), [tstg], "ldst")
            for g in range(3):
                pb, tpb = self.bank()
                for j in range(8):
                    fc = g * 8 + j
                    self.tr(pb[:, j * 48:(j + 1) * 48], stg[0:48, fc * 128:(fc + 1) * 128], self.idf[0:48, 0:48], [tstg, self.tC], [tpb])
                self.cp("act", SHG[:, g * 8:(g + 1) * 8, :], pb[:, 0:384].rearrange("p (a b) -> p a b", b=48), [tpb], [tSHG])
        self.dma("pool", wbat, self.wba.rearrange("(kc p) f -> p kc f", p=128), (), [twba], "ldwba")
        for tile in st.tiles():
            seg, t0, n = tile
            c0 = seg.off + t0
            pb, tpb = self.bank()
            for kc in range(8):
                self.mm(pb[0:40, 0:n], wbat[:, kc, :], xn[:, kc, c0:c0 + n], [twba, xnt(kc, tile)[1]], [tpb], start=(kc == 0), stop=(kc == 7))
            self.act(GB[32:40, c0:c0 + n], pb[32:40, 0:n], AF.Sigmoid, [tpb], [tGB])
            (b1, t1), (b2, t2), (b3, t3), (b4, t4) = bat
            self.ts("dve", b1[:, 0:n], pb[0:8, 0:n], self.vcol("dtb", 0, 8), ALU.add, [tpb, self.tC], [t1])
            self.stt(b2[:, 0:n], b1[:, 0:n], -1.0, b1[:, 0:n], ALU.mult, ALU.max, [t1], [t2])
            self.act(b3[:, 0:n], b2[:, 0:n], AF.Exp, [t2], [t3], scale=-1.0)
            self.act(b4[:, 0:n], b3[:, 0:n], AF.Ln, [t3], [t4], bias=1.0)
            self.stt(b2[:, 0:n], b1[:, 0:n], 0.0, b4[:, 0:n], ALU.max, ALU.add, [t1, t4], [t2])
            self.ts("dve", GB[0:8, c0:c0 + n], b2[:, 0:n], self.nexpA[:, 0:1], ALU.mult, [t2, self.tC], [tGB])
        def ld_win(u):
            wt_, twt_ = wsl[u % 3]
            self.dma("pool", wt_, self.w_in[:, u * 512:(u + 1) * 512].rearrange("(kc p) f -> p kc f", p=128), (), [twt_], "ldw%d" % (u % 3))
        ld_win(0); ld_win(1)
        pend = []
        qk_list = []
        for u in range(8):
            wt, twt = wsl[u % 3]
            if u + 2 < 8:
                ld_win(u + 2)
            for j in range(4):
                fc = u * 4 + j
                kind = fc // 8
                prev_ext = None
                for tile in st.tiles():
                    seg, t0, n = tile
                    c0 = seg.off + t0
                    pb, tpb = self.bank()
                    for kc in range(8):
                        self.mm(pb[:, 0:n], wt[:, kc, j * 128:(j + 1) * 128], xn[:, kc, c0:c0 + n], [twt, xnt(kc, tile)[1]], [tpb],
                                start=(kc == 0), stop=(kc == 7))
                    dst = QKVZ[:, fc, c0:c0 + n]
                    tdst = self.tok("qkvz", self.phase_id, fc, c0)
                    if kind == 3:
                        self.act(self.V(seg, dst), self.V(seg, pb[:, 0:n]), AF.Silu, [tpb], [tdst])
                        continue
                    ex, tex = ext[self.rr("ext", [0, 1, 2])]
                    if seg.kind == "S":
                        self.cp("pool", self.ext_halo(seg, ex, 3), SHG[:, fc, :].rearrange("p (s r) -> p s r", r=3), [tSHG], [tex])
                    elif t0 == 0:
                        self.cp("pool", ex[:, 0:3], self.HG[:, fc, :], [self.tok("HG")], [tex])
                    else:
                        pe_, tpe_, pn = prev_ext
                        self.cp("pool", ex[:, 0:3], pe_[:, pn:pn + 3], [tpe_], [tex])
                    self.cp("act", self.ext_dst(seg, ex, 3, n), self.V(seg, pb[:, 0:n]), [tpb], [tex])
                    prev_ext = (ex, tex, n)
                    if seg.kind == "S":
                        self.cp("pool", SHG[:, fc, :].rearrange("p (s r) -> p s r", r=3), self.ext_tail(seg, ex, 3, n), [tex], [tSHG])
                    elif t0 + n == seg.n:
                        self.cp("pool", self.HG[:, fc, :], ex[:, n:n + 3], [tex], [self.tok("HG")])
                    ac, tac = acc[self.rr("acc", [0, 1])]
                    av = self.V(seg, ac[:, 0:n])
                    self.act(av, self.ext_tap(seg, ex, 3, 0, n), AF.Identity, [tex, self.tC], [tac], scale=self.vcol("gcw", 0 * 24 + fc))
                    for tap in (1, 2, 3):
                        self.stt(av, self.ext_tap(seg, ex, 3, tap, n), self.vcol("gcw", tap * 24 + fc), av, ALU.mult, ALU.add, [tex, tac, self.tC], [tac])

                    def tail(dst=dst, tdst=tdst, ac=ac, tac=tac, n=n):
                        self.act(dst, ac[:, 0:n], AF.Silu, [tac], [tdst])
                    if pend:
                        pend.pop(0)()
                    pend.append(tail)
                    if kind < 2:
                        qk_list.append((kind, dst, tdst, n))
            while pend:
                pend.pop(0)()
            def nstage1(item):
                kind, dst, tdst, n = item
                sh, tsh = sqh[self.rr("sqh", [0, 1, 2])]
                self.tt("dve", sh[:, 0:n], dst, dst, ALU.mult, [tdst], [tsh])
                pb2, tpb2 = self.bank()
                self.mm(pb2[:, 0:n], self.ones_1[:], sh[:, 0:n], [tsh, self.tC], [tpb2])
                return pb2, tpb2

            def nstage2(item, pb2, tpb2):
                kind, dst, tdst, n = item
                ri, tri_ = rin[self.rr("rin", [0, 1])]
                self.act(ri[:, 0:n], pb2[:, 0:n], AF.Ln, [tpb2], [tri_], bias=EPS)
                self.act(ri[:, 0:n], ri[:, 0:n], AF.Exp, [tri_], [tri_], scale=-0.5, bias=(self.lnq[:, 0:1] if kind == 0 else None))
                self.tt("dve", dst, dst, ri[:, 0:n], ALU.mult, [tdst, tri_], [tdst])
            inflight = []
            for item in qk_list:
                inflight.append((item,) + nstage1(item))
                if len(inflight) > 2:
                    nstage2(*inflight.pop(0))
            while inflight:
                nstage2(*inflight.pop(0))
            qk_list = []
        if hasS:
            for fc in range(24):
                pb, tpb = self.bank()
                self.tr(pb[0:48, 0:128], SHG[:, fc, :], self.idf[:, :], [tSHG, self.tC], [tpb])
                self.cp(self.rr("ev", ["act", "dve"]), stg[0:48, fc * 128:(fc + 1) * 128], pb[0:48, 0:128], [tpb], [tstg])
            self.dma("sp", self.o_sconv, stg, [tstg], (), "stst")
        if st.last:
            stg2, tstg2 = self.A([3, 3072], F32)
            for fc in range(24):
                pb, tpb = self.bank()
                self.tr(pb[0:3, 0:128], self.HG[:, fc, :], self.idf[:, :], [self.tok("HG"), self.tC], [tpb])
                self.cp(self.rr("ev", ["act", "dve"]), stg2[0:3, fc * 128:(fc + 1) * 128], pb[0:3, 0:128], [tpb], [tstg2])
            self.dma("sp", self.o_pconv, stg2, [tstg2], (), "stst")

        self.P.barrier()
        self.aoff = mark
        ONT = xn
        self.bfree = list(range(8))
        NA, NC_, NB = 3, 4, 1
        self.want_onacc = False
        TAs = [self.alloc_chunk_bufs("A") for _ in range(NA)]
        CAs = [self.alloc_chunk_bufs("C") for _ in range(NC_)]
        TBs = [self.alloc_chunk_bufs("B") for _ in range(NB)]
        jobs = []
        for seg in st.segs:
            if seg.kind == "P":
                c = 0
                if seg.pos0 == 0:
                    jobs.append(("P", seg.off, NMETA, None))
                    c = NMETA
                while c < seg.n:
                    jobs.append(("P", seg.off + c, 64, None))
                    c += 64
            else:
                for b_ in range(2):
                    jobs.append(("SB", seg.off + 64 * b_, 64, 8 * b_))
        N = len(jobs)
        nextA = 0
        nextB = 0
        doneA = set()
        actA = {}
        actB = None
        while nextB < N:
            for slot in range(NA):
                if slot not in actA and nextA < N and nextA < nextB + NC_:
                    actA[slot] = (nextA, self.chunk_A(jobs[nextA], QKVZ, GB, tGB, TAs[slot], CAs[nextA % NC_]))
                    nextA += 1
            if actB is None and nextB in doneA:
                if jobs[nextB][0] == "SB":
                    nextB += 1
                    continue
                actB = self.chunk_B(jobs[nextB], CAs[nextB % NC_], TBs[nextB % NB], QKVZ, ONT)
            if actB is not None:
                try:
                    next(actB)
                except StopIteration:
                    actB = None
                    nextB += 1
            for slot in list(actA):
                j, g = actA[slot]
                try:
                    next(g)
                except StopIteration:
                    doneA.add(j)
                    del actA[slot]
        if hasS:
            self.P.barrier()
            onaccs = [self.A([64, 1024], F32) for _ in range(2)]
            save_off = self.aoff
            self.aoff = mark
            NSQ = 4
            assert all((ji % NC_) >= 2 for ji in range(N) if jobs[ji][0] == "SB")
            SBs = []
            for _ in range(NSQ):
                d = {}
                d["S32"] = self.A([128, 8, 128], F32); d["S16"] = self.A([128, 8, 128], BF16)
                d["Stmp"] = self.A([128, 1024], F32); d["vn"] = self.A([64, 1024], BF16)
                SBs.append(d)
            sb_jobs = [(ji, jobs[ji]) for ji in range(N) if jobs[ji][0] == "SB"]
            todo = [(bi, ji, job, s_) for bi, (ji, job) in enumerate(sb_jobs) for s_ in range(8)]
            remaining = {bi: 8 for bi in range(len(sb_jobs))}
            act = {}
            fin = []
            while todo or act or fin:
                for slot in range(NSQ):
                    if slot not in act and todo:
                        bi, ji, job, s_ = todo.pop(0)
                        act[slot] = (bi, self.sample_seq(job, s_, CAs[ji % NC_], SBs[slot], onaccs[bi]))
                for slot in list(act):
                    bi, g = act[slot]
                    try:
                        next(g)
                    except StopIteration:
                        del act[slot]
                        remaining[bi] -= 1
                        if remaining[bi] == 0:
                            ji, job = sb_jobs[bi]
                            fin.append(self.sample_finish(job, TBs[0], onaccs[bi], QKVZ, ONT))
                for g in list(fin):
                    try:
                        next(g)
                    except StopIteration:
                        fin.remove(g)
            self.aoff = max(save_off, self.aoff)
        if st.last:
            self.dma("sp", self.o_prec.rearrange("h k v -> k h v"), self.S32[:], [self.tok("S32")], (), "strec")

        self.P.barrier()
        self.aoff = mark
        wo, two = self.A([128, 8, D], BF16)
        self.dma("pool", wo[:, :, 0:512], self.w_out[:, 0:512].rearrange("(kc p) f -> p kc f", p=128), (), [two], "ldw0")
        self.dma("pool", wo[:, :, 512:1024], self.w_out[:, 512:1024].rearrange("(kc p) f -> p kc f", p=128), (), [two], "ldw1")
        for tile in st.tiles():
            seg, t0, n = tile
            c0 = seg.off + t0
            for dc in range(8):
                pb, tpb = self.bank()
                for kc in range(8):
                    self.mm(pb[:, 0:n], wo[:, kc, dc * 128:(dc + 1) * 128], ONT[:, kc, c0:c0 + n], [two], [tpb], start=(kc == 0), stop=(kc == 7))
                xv = self.xT[:, dc, c0:c0 + n]
                self.tt("dve", xv, pb[:, 0:n], xv, ALU.add, [tpb], [self.xtok(dc, tile)])

    def alloc_chunk_bufs(self, which):
        b = {}
        def a(name, shape, dt):
            b[name] = self.A(shape, dt)
        if which == "A":
            a("gbt", [64, 40], F32)
            for nm in ("Gt", "eG", "nbG", "nb", "dGl", "eGl"):
                a(nm, [64, 8], F32)
            a("Dm", [64, 512], F32); a("Du", [64, 512], F32); a("Dl", [64, 512], F32); a("eGbc", [128, 512], F32)
            b["rhsG"] = b["Dl"]
            a("Lneg", [64, 512], BF16); a("M0", [64, 512], BF16)
            a("QTa", [64, 512], BF16); a("QTb", [64, 512], BF16)
            a("kbgn", [64, 1024], BF16)
        elif which == "C":
            a("PQa", [64, 1024], BF16); a("PQb", [64, 1024], BF16); a("At", [64, 512], BF16)
            a("kd", [64, 1024], BF16); a("vb", [64, 1024], BF16)
            a("nWT", [128, 512], BF16); a("qdT", [128, 512], BF16); a("gtc", [128, 64], F32)
        else:
            a("vn", [64, 1024], BF16); a("sqo", [64, 1024], BF16); a("on", [64, 1024], BF16)
            a("Stmp", [128, 1024], F32); a("ss", [64, 8], F32); a("rs", [64, 8], F32)
            if self.want_onacc:
                a("onacc", [64, 1024], F32)
        return b

    def bacq(self):
        if not self.bfree:
            raise RuntimeError("out of PSUM banks")
        i = self.bfree.pop(0)
        return self.banks[i], self.tok("bank", i), i

    def brel(self, i):
        self.bfree.append(i)

    def chunk_A(self, job, QKVZ, GB, tGB, TA, CA):
        kind, c0, L, sidx = job
        B = dict(TA); B.update(CA)
        tC = self.tC
        Q = lambda h: QKVZ[:, h, c0:c0 + L]
        K = lambda h: QKVZ[:, 8 + h, c0:c0 + L]
        Vv = lambda h: QKVZ[:, 16 + h, c0:c0 + L]
        h3 = lambda ap: ap.rearrange("p (h l) -> p h l", l=L)
        hd = lambda ap: ap.rearrange("p (h d) -> p h d", d=128)
        W8 = 8 * L
        pg, tpg, ipg = self.bacq()
        self.tr(pg[0:L, 0:40], GB[0:40, c0:c0 + L], self.idf[0:40, 0:40], [tGB, tC], [tpg])
        gbt, tgbt = B["gbt"]
        self.cp("dve", gbt[0:L, :], pg[0:L, 0:40], [tpg], [tgbt])
        self.brel(ipg)
        g_tm = gbt[0:L, 0:8]
        beta = gbt[0:L, 32:40]
        blk = (kind == "SB")
        cT, cN, cP = (C_TRI8, C_NEGU8, C_POSL8) if blk else (C_TRI, C_NEGU, C_POSL)
        tri = self.cst[0:L, cT:cT + L]
        yield
        rhsG, trG = B["rhsG"]
        self.tt("pool", h3(rhsG[0:L, 0:W8]), tri.unsqueeze(1).to_broadcast([L, 8, L]), g_tm.unsqueeze(2).to_broadcast([L, 8, L]),
                ALU.mult, [tgbt, tC], [trG])
        pg2, tpg2, ipg2 = self.bacq()
        self.mm(pg2[0:L, 0:8], tri, g_tm, [tgbt, tC], [tpg2])
        Gt, tGt = B["Gt"]; eG, teG = B["eG"]; nbG, tnbG = B["nbG"]; nb, tnb = B["nb"]
        dGl, tdGl = B["dGl"]; eGl, teGl = B["eGl"]; gtc, tgtc = B["gtc"]
        self.cp("dve", Gt[0:L, :], pg2[0:L, 0:8], [tpg2], [tGt])
        self.brel(ipg2)
        self.ts("dve", nb[0:L, :], beta, -1.0, ALU.mult, [tgbt], [tnb])
        yield
        pG, tpG, ipG = self.bacq()
        self.mm(pG[:, 0:W8], self.ones_f[0:L, :], rhsG[0:L, 0:W8], [trG, tC], [tpG])
        self.act(eG[0:L, :], Gt[0:L, :], AF.Exp, [tGt], [teG])
        self.tt("dve", nbG[0:L, :], eG[0:L, :], nb[0:L, :], ALU.mult, [teG, tnb], [tnbG])
        yield
        if blk:
            Glast = None
            gl4 = pG[:, 0:W8].rearrange("p (h s t) -> p h s t", s=8, t=8)[:, :, :, 7]
        else:
            Glast = h3(pG[:, 0:W8])[:, :, L - 1]
        Dm, tDm = B["Dm"]; Du, tDu = B["Du"]; Dl, tDl = B["Dl"]; eGbc, teGbc = B["eGbc"]
        self.tt("dve", h3(Dm[0:L, 0:W8]), h3(pG[0:L, 0:W8]), Gt[0:L, :].unsqueeze(2).to_broadcast([L, 8, L]), ALU.subtract,
                [tpG, tGt], [tDm])
        if blk:
            pgl, tpgl, ipgl = self.bacq()
            self.mm(pgl[0:L, 0:8], self.cst[0:L, C_SEL:C_SEL + L], Gt[0:L, :], [tGt, tC], [tpgl])
            self.tt("dve", dGl[0:L, :], pgl[0:L, 0:8], Gt[0:L, :], ALU.subtract, [tpgl, tGt], [tdGl])
            self.brel(ipgl)
            self.act(gtc[:, 0:64].rearrange("p (h s) -> p h s", s=8), gl4, AF.Exp, [tpG], [tgtc])
        else:
            self.tt("dve", dGl[0:L, :], Glast[0:L], Gt[0:L, :], ALU.subtract, [tpG, tGt], [tdGl])
            self.act(gtc[:, 0:8], Glast, AF.Exp, [tpG], [tgtc])
        self.act(eGbc[:, 0:W8], pG[:, 0:W8], AF.Exp, [tpG], [teGbc])
        self.brel(ipG)
        pk, tpk, ipk = self.bacq()
        pkb = pk[:].bitcast(BF16)
        for h in range(8):
            self.tr(pkb[0:L, h * 128:(h + 1) * 128], K(h), self.idb[:], [tC], [tpk])
        pv, tpv, ipv = self.bacq()
        pvb = pv[:].bitcast(BF16)
        for h in range(8):
            self.tr(pvb[0:L, h * 128:(h + 1) * 128], Vv(h), self.idb[:], [tC], [tpv])
        yield
        self.act(eGl[0:L, :], dGl[0:L, :], AF.Exp, [tdGl], [teGl])
        negu = self.cst[0:L, cN:cN + L].unsqueeze(1).to_broadcast([L, 8, L])
        posl = self.cst[0:L, cP:cP + L].unsqueeze(1).to_broadcast([L, 8, L])
        self.tt("pool", h3(Du[0:L, 0:W8]), h3(Dm[0:L, 0:W8]), negu, ALU.add, [tDm, tC], [tDu])
        self.tt("pool", h3(Dl[0:L, 0:W8]), h3(Dm[0:L, 0:W8]), posl, ALU.add, [tDm, tC], [tDl])
        kbgn, tkb = B["kbgn"]; kd, tkd = B["kd"]; vb, tvb = B["vb"]
        self.tt("dve", hd(kbgn[0:L, :]), hd(pkb[0:L, :]), nbG[0:L, :].unsqueeze(2).to_broadcast([L, 8, 128]), ALU.mult, [tpk, tnbG], [tkb])
        self.tt("dve", hd(vb[0:L, :]), hd(pvb[0:L, :]), beta.unsqueeze(2).to_broadcast([L, 8, 128]), ALU.mult, [tpv, tgbt], [tvb])
        self.brel(ipv)
        yield
        self.tt("dve", hd(kd[0:L, :]), hd(pkb[0:L, :]), eGl[0:L, :].unsqueeze(2).to_broadcast([L, 8, 128]), ALU.mult, [tpk, teGl], [tkd])
        self.brel(ipk)
        self.act(Du[0:L, 0:W8], Du[0:L, 0:W8], AF.Exp, [tDu], [tDu])
        self.act(Dl[0:L, 0:W8], Dl[0:L, 0:W8], AF.Exp, [tDl], [tDl], scale=-1.0)
        pkk, tpkk, ipkk = self.bacq()
        for h in range(8):
            self.mm(pkk[0:L, h * L:(h + 1) * L], K(h), K(h), [], [tpkk])
        pkq, tpkq, ipkq = self.bacq()
        for h in range(8):
            self.mm(pkq[0:L, h * L:(h + 1) * L], K(h), Q(h), [], [tpkq])
        qdT, tqd = B["qdT"]
        self.tt("pool", h3(qdT[:, 0:W8]), QKVZ[:, 0:8, c0:c0 + L], h3(eGbc[:, 0:W8]), ALU.mult, [teGbc], [tqd])
        yield
        self.tt("pool", h3(Dl[0:L, 0:W8]), h3(Dl[0:L, 0:W8]), nb[0:L, :].unsqueeze(2).to_broadcast([L, 8, L]), ALU.mult,
                [tDl, tnb], [tDl])
        Lneg, tLn = B["Lneg"]; At, tAt = B["At"]; M0, tM0 = B["M0"]
        self.tt("dve", At[0:L, 0:W8], pkq[0:L, 0:W8], Du[0:L, 0:W8], ALU.mult, [tpkq, tDu], [tAt])
        self.brel(ipkq)
        yield
        self.tt("dve", Lneg[0:L, 0:W8], pkk[0:L, 0:W8], Dl[0:L, 0:W8], ALU.mult, [tpkk, tDl], [tLn])
        self.brel(ipkk)
        yield
        pm, tpm, ipm = self.bacq()
        pmb = pm[:].bitcast(BF16)
        for h in range(8):
            self.tr(pmb[0:L, h * L:(h + 1) * L], Lneg[0:L, h * L:(h + 1) * L], self.idb[0:L, 0:L], [tLn, tC], [tpm])
        self.cp("act", M0[0:L, 0:W8], pmb[0:L, 0:W8], [tpm], [tM0])
        self.brel(ipm)
        yield
        nlev = 3 if blk else {64: 6, 16: 4, 8: 3}[L]
        idbL = self.idb[0:L, 0:L]
        PQ = [B["PQa"], B["PQb"]]
        QTbufs = [B["QTa"], B["QTb"]]
        pq3 = lambda ap: ap[0:L, :].rearrange("p (h c) -> p h c", c=128)
        cur = 0
        Pc, tPc = PQ[cur]
        self.tt("pool", pq3(Pc)[:, :, 0:L], h3(M0[0:L, 0:W8]), idbL.unsqueeze(1).to_broadcast([L, 8, L]), ALU.add, [tM0, tC], [tPc])
        pq, tpq, ipq = self.bacq()
        for h in range(8):
            sl = slice(h * L, (h + 1) * L)
            self.mm(pq[0:L, sl], M0[0:L, sl], Lneg[0:L, sl], [tM0, tLn], [tpq])
        QTc = QTbufs[0]
        self.cp("act", QTc[0][0:L, 0:W8], pq[0:L, 0:W8], [tpq], [QTc[1]])
        self.brel(ipq)
        pq2, tpq2, ipq2 = self.bacq()
        for h in range(8):
            sl = slice(h * L, (h + 1) * L)
            self.mm(pq2[0:L, sl], Lneg[0:L, sl], M0[0:L, sl], [tM0, tLn], [tpq2])
        self.cp("dve", pq3(Pc)[:, :, L:2 * L], h3(pq2[0:L, 0:W8]), [tpq2], [tPc])
        self.brel(ipq2)
        yield
        for k in range(1, nlev):
            last = (k == nlev - 1)
            Pn, tPn = PQ[1 - cur]
            wid = L if last else 2 * L
            if not last:
                QTn = QTbufs[k % 2]
                pq, tpq, ipq = self.bacq()
                for h in range(8):
                    self.mm(pq[0:L, h * L:(h + 1) * L], pq3(Pc)[:, h, L:2 * L], QTc[0][0:L, h * L:(h + 1) * L], [tPc, QTc[1]], [tpq])
                self.cp("act", QTn[0][0:L, 0:W8], pq[0:L, 0:W8], [tpq], [QTn[1]])
                self.brel(ipq)
            for half in range(2):
                pp, tpp, ipp = self.bacq()
                for hh in range(4):
                    h = half * 4 + hh
                    self.mm(pp[0:L, hh * 128:hh * 128 + wid], QTc[0][0:L, h * L:(h + 1) * L], pq3(Pc)[:, h, 0:wid], [tPc, QTc[1]], [tpp])
                ppv = pp[0:L, :].rearrange("p (h c) -> p h c", c=128)
                hs = slice(half * 4, half * 4 + 4)
                self.tt("dve", pq3(Pn)[:, hs, 0:L], ppv[:, :, 0:L], pq3(Pc)[:, hs, 0:L], ALU.add, [tpp, tPc], [tPn])
                if not last:
                    self.cp("act", pq3(Pn)[:, hs, L:2 * L], ppv[:, :, L:2 * L], [tpp], [tPn])
                self.brel(ipp)
            cur = 1 - cur
            Pc, tPc = PQ[cur]
            if not last:
                QTc = QTn
            yield
        Ttv = pq3(Pc)
        tTt = tPc
        pw, tpw, ipw = self.bacq()
        for h in range(8):
            self.mm(pw[:, h * L:(h + 1) * L], kbgn[0:L, h * 128:(h + 1) * 128], Ttv[:, h, 0:L], [tkb, tTt], [tpw])
        nWT, tnW = B["nWT"]
        self.cp("act", nWT[:, 0:W8], pw[:, 0:W8], [tpw], [tnW])
        self.brel(ipw)
        CA["Tt"] = (Ttv, tTt)
        yield

    def chunk_B(self, job, CA, TB, QKVZ, ONT):
        kind, c0, L, sidx = job
        B = dict(TB); B.update(CA)
        tC = self.tC
        Tt, tTt = CA["Tt"]
        h3 = lambda ap: ap.rearrange("p (h l) -> p h l", l=L)
        hd = lambda ap: ap.rearrange("p (h d) -> p h d", d=128)
        W8 = 8 * L
        if kind == "S":
            S32, tS32 = self.SS32[sidx % 2]
            S16, tS16 = self.SS16[sidx % 2]
            self.dma("sp", S32, self.st_rec[sidx].rearrange("h k v -> k h v"), (), [tS32], "ldrec%d" % (sidx % 2))
            self.cp("pool", S16, S32, [tS32], [tS16])
        else:
            S32, tS32 = self.S32[:], self.tok("S32")
            S16, tS16 = self.S16[:], self.tok("S16")
        vb, tvb = B["vb"]; nWT, tnW = B["nWT"]; vn, tvn = B["vn"]; qdT, tqd = B["qdT"]; At, tAt = B["At"]
        kd, tkd = B["kd"]; sqo, tsq = B["sqo"]; on, ton = B["on"]; ss, tss = B["ss"]; rs, trs = B["rs"]
        gtc, tgtc = B["gtc"]; Stmp, tSt = B["Stmp"]
        for half in range(2):
            pv, tpv, ipv = self.bacq()
            for hh in range(4):
                h = half * 4 + hh
                self.mm(pv[0:L, hh * 128:(hh + 1) * 128], Tt[:, h, 0:L], vb[0:L, h * 128:(h + 1) * 128], [tTt, tvb], [tpv], start=(hh == 0), stop=False, skip=True)
            for hh in range(4):
                h = half * 4 + hh
                self.mm(pv[0:L, hh * 128:(hh + 1) * 128], nWT[:, h * L:(h + 1) * L], S16[:, h, :], [tnW, tS16], [tpv], start=False, stop=True, skip=True)
            self.cp(("act", "dve")[half], vn[0:L, half * 512:(half + 1) * 512], pv[0:L, :], [tpv], [tvn])
            self.brel(ipv)
        self.tt("pool", hd(Stmp[:, :]), S32, gtc[:, 0:8].unsqueeze(2).to_broadcast([128, 8, 128]), ALU.mult, [tS32, tgtc], [tSt])
        yield
        pss = []
        for half in range(2):
            pS, tpS, ipS = self.bacq()
            pss.append((pS, tpS, ipS))
            for hh in range(4):
                h = half * 4 + hh
                self.mm(pS[:, hh * 128:(hh + 1) * 128], kd[0:L, h * 128:(h + 1) * 128], vn[0:L, h * 128:(h + 1) * 128], [tkd, tvn], [tpS])
        for half in range(2):
            pS, tpS, ipS = pss[half]
            self.tt("dve", S32[:, half * 4:(half + 1) * 4, :], hd(pS[:, :]), hd(Stmp[:, half * 512:(half + 1) * 512]), ALU.add, [tpS, tSt], [tS32])
            self.brel(ipS)
        pos = []
        for half in range(2):
            po, tpo, ipo = self.bacq()
            pos.append((po, tpo, ipo))
            for hh in range(4):
                h = half * 4 + hh
                self.mm(po[0:L, hh * 128:(hh + 1) * 128], qdT[:, h * L:(h + 1) * L], S16[:, h, :], [tqd, tS16], [tpo], start=(hh == 0), stop=False, skip=True)
            for hh in range(4):
                h = half * 4 + hh
                self.mm(po[0:L, hh * 128:(hh + 1) * 128], At[0:L, h * L:(h + 1) * L], vn[0:L, h * 128:(h + 1) * 128], [tAt, tvn], [tpo], start=False, stop=True, skip=True)
        self.cp("act", S16, S32, [tS32], [tS16])
        if kind == "S":
            self.dma("sp", self.o_srec[sidx].rearrange("h k v -> k h v"), S32, [tS32], (), "strec%d" % (sidx % 2))
        yield
        for half in range(2):
            po, tpo, ipo = pos[half]
            self.act(sqo[0:L, half * 512:(half + 1) * 512], po[0:L, :], AF.Square, [tpo], [tsq])
        self.P.op("dve", lambda e, o=ss[0:L, :], i=hd(sqo[0:L, :]): e.tensor_reduce(out=o, in_=i, axis=AX.X, op=ALU.add), [tsq], [tss])
        self.act(rs[0:L, :], ss[0:L, :], AF.Ln, [tss], [trs], bias=EPS, scale=1.0 / 128.0)
        self.act(rs[0:L, :], rs[0:L, :], AF.Exp, [trs], [trs], scale=-0.5)
        yield
        for half in range(2):
            po, tpo, ipo = pos[half]
            self.tt("dve", hd(on[0:L, half * 512:(half + 1) * 512]), hd(po[0:L, :]), rs[0:L, half * 4:(half + 1) * 4].unsqueeze(2).to_broadcast([L, 4, 128]),
                    ALU.mult, [tpo, trs], [ton])
            self.brel(ipo)
        yield
        pt, tpt, ipt = self.bacq()
        ptb = pt[:].bitcast(BF16)
        for h in range(8):
            self.tr(ptb[:, h * L:(h + 1) * L], on[0:L, h * 128:(h + 1) * 128], self.idb[0:L, 0:L], [ton, tC], [tpt])
        self.stt(ONT[:, :, c0:c0 + L], h3(ptb[:, 0:W8]), self.vcol("gnw"), QKVZ[:, 24:32, c0:c0 + L], ALU.mult, ALU.mult, [tpt, tC], [self.tok("ONT", self.phase_id)])
        self.brel(ipt)
        yield

    def sample_seq(self, job, s_, CA, SB, onacc_t):
        kind, c0, L, s0 = job
        tC = self.tC
        Tt, tTt = CA["Tt"]
        hd = lambda ap: ap.rearrange("p (h d) -> p h d", d=128)
        vb, tvb = CA["vb"]; nWT, tnW = CA["nWT"]; qdT, tqd = CA["qdT"]; At, tAt = CA["At"]
        kd, tkd = CA["kd"]; gtc, tgtc = CA["gtc"]
        S32, tS32 = SB["S32"]; S16, tS16 = SB["S16"]; Stmp, tSt = SB["Stmp"]; vn, tvn = SB["vn"]
        onacc, tacc = onacc_t
        gtc3 = gtc[:, 0:64].rearrange("p (h s) -> p h s", s=8)
        sidx = s0 + s_
        rm = self.cst[0:L, C_RM + s_:C_RM + s_ + 1]
        self.dma("sp", S32, self.st_rec[sidx].rearrange("h k v -> k h v"), (), [tS32], "ldrec%d" % (sidx % 3))
        yield
        self.cp("pool", S16, S32, [tS32], [tS16])
        self.tt("pool", hd(Stmp[:, :]), S32, gtc3[:, :, s_].unsqueeze(2).to_broadcast([128, 8, 128]), ALU.mult, [tS32, tgtc], [tSt])
        yield
        for half in range(2):
            pv, tpv, ipv = self.bacq()
            for hh in range(4):
                h = half * 4 + hh
                self.mm(pv[0:L, hh * 128:(hh + 1) * 128], Tt[:, h, 0:L], vb[0:L, h * 128:(h + 1) * 128], [tTt, tvb], [tpv], start=(hh == 0), stop=False, skip=True)
            for hh in range(4):
                h = half * 4 + hh
                self.mm(pv[0:L, hh * 128:(hh + 1) * 128], nWT[:, h * L:(h + 1) * L], S16[:, h, :], [tnW, tS16], [tpv], start=False, stop=True, skip=True)
            if half == 0:
                self.act(vn[0:L, 0:512], pv[0:L, :], AF.Identity, [tpv, tC], [tvn], scale=rm)
            else:
                self.ts("dve", vn[0:L, 512:1024], pv[0:L, :], rm, ALU.mult, [tpv, tC], [tvn])
            self.brel(ipv)
        yield
        pss = []
        for half in range(2):
            pS, tpS, ipS = self.bacq()
            pss.append((pS, tpS, ipS))
            for hh in range(4):
                h = half * 4 + hh
                self.mm(pS[:, hh * 128:(hh + 1) * 128], kd[0:L, h * 128:(h + 1) * 128], vn[0:L, h * 128:(h + 1) * 128], [tkd, tvn], [tpS])
        for half in range(2):
            pS, tpS, ipS = pss[half]
            self.tt("dve", S32[:, half * 4:(half + 1) * 4, :], hd(pS[:, :]), hd(Stmp[:, half * 512:(half + 1) * 512]), ALU.add, [tpS, tSt], [tS32])
            self.brel(ipS)
        self.dma("sp", self.o_srec[sidx].rearrange("h k v -> k h v"), S32, [tS32], (), "strec%d" % (sidx % 3))
        pos = []
        for half in range(2):
            po, tpo, ipo = self.bacq()
            pos.append((po, tpo, ipo))
            for hh in range(4):
                h = half * 4 + hh
                self.mm(po[0:L, hh * 128:(hh + 1) * 128], qdT[:, h * L:(h + 1) * L], S16[:, h, :], [tqd, tS16], [tpo], start=(hh == 0), stop=False, skip=True)
            for hh in range(4):
                h = half * 4 + hh
                self.mm(po[0:L, hh * 128:(hh + 1) * 128], At[0:L, h * L:(h + 1) * L], vn[0:L, h * 128:(h + 1) * 128], [tAt, tvn], [tpo], start=False, stop=True, skip=True)
        yield
        for half in range(2):
            po, tpo, ipo = pos[half]
            acc = onacc[0:L, half * 512:(half + 1) * 512]
            if s_ == 0:
                self.ts("dve", acc, po[0:L, :], rm, ALU.mult, [tpo, tC], [tacc])
            else:
                self.stt(acc, po[0:L, :], rm, acc, ALU.mult, ALU.add, [tpo, tC, tacc], [tacc])
            self.brel(ipo)
        yield

    def sample_finish(self, job, TB, onacc_t, QKVZ, ONT):
        kind, c0, L, s0 = job
        tC = self.tC
        h3 = lambda ap: ap.rearrange("p (h l) -> p h l", l=L)
        hd = lambda ap: ap.rearrange("p (h d) -> p h d", d=128)
        W8 = 8 * L
        sqo, tsq = TB["sqo"]; on, ton = TB["on"]; ss, tss = TB["ss"]; rs, trs = TB["rs"]
        onacc, tacc = onacc_t
        self.act(sqo[0:L, :], onacc[0:L, :], AF.Square, [tacc], [tsq])
        self.P.op("dve", lambda e, o=ss[0:L, :], i=hd(sqo[0:L, :]): e.tensor_reduce(out=o, in_=i, axis=AX.X, op=ALU.add), [tsq], [tss])
        self.act(rs[0:L, :], ss[0:L, :], AF.Ln, [tss], [trs], bias=EPS, scale=1.0 / 128.0)
        self.act(rs[0:L, :], rs[0:L, :], AF.Exp, [trs], [trs], scale=-0.5)
        yield
        self.tt("dve", hd(on[0:L, :]), hd(onacc[0:L, :]), rs[0:L, :].unsqueeze(2).to_broadcast([L, 8, 128]), ALU.mult, [tacc, trs], [ton])
        yield
        pt, tpt, ipt = self.bacq()
        ptb = pt[:].bitcast(BF16)
        for h in range(8):
            self.tr(ptb[:, h * L:(h + 1) * L], on[0:L, h * 128:(h + 1) * 128], self.idb[0:L, 0:L], [ton, tC], [tpt])
        self.stt(ONT[:, :, c0:c0 + L], h3(ptb[:, 0:W8]), self.vcol("gnw"), QKVZ[:, 24:32, c0:c0 + L], ALU.mult, ALU.mult, [tpt, tC], [self.tok("ONT", self.phase_id)])
        self.brel(ipt)
        yield

    def ffn(self, st, l):
        NT = st.NT
        self.phase()
        hasS = any(s.kind == "S" for s in st.segs)
        xn, _ = self.A([128, 8, NT], BF16)
        hT, _ = self.A([128, 22, NT], BF16)
        sq2 = [self.A([128, 8, 512], BF16) for _ in range(2)]
        rsb = [self.A([128, 512], F32) for _ in range(2)]
        wsl = [self.A([128, 2, 8, 256], BF16) for _ in range(3)]
        wsl = [(w_, (t_, self.tok("wslb", self.phase_id, i_))) for i_, (w_, t_) in enumerate(wsl)]
        wdn = [self.A([128, 22, 128], BF16) for _ in range(3)]
        ub = [self.A([128, 2 + 512], F32) for _ in range(4)]
        t0b = [self.A([128, 512], F32) for _ in range(4)]
        sab = [self.A([128, 512], F32) for _ in range(2)]
        if hasS:
            SHF, tSHF = self.A([128, NFC, 32], F32)
            stg, tstg = self.A([32, 5632], F32)
        xnt = lambda dc, tile: (xn[:, dc, tile[0].off + tile[1]:tile[0].off + tile[1] + tile[2]], self.tok("xn", self.phase_id, dc, tile[0].off + tile[1]))
        self.norm(st, "nf%d" % l, xnt, sq2, rsb)
        if hasS:
            self.dma("sp", stg, self.st_ffn[l], (), [tstg], "ldst")
            for g in range(0, NFC, 8):
                pb, tpb = self.bank()
                ng = min(8, NFC - g)
                for j in range(ng):
                    fc = g + j
                    self.tr(pb[:, j * 32:(j + 1) * 32], stg[0:32, fc * 128:(fc + 1) * 128], self.idf[0:32, 0:32], [tstg, self.tC], [tpb])
                self.cp("act", SHF[:, g:g + ng, :], pb[:, 0:ng * 32].rearrange("p (a b) -> p a b", b=32), [tpb], [tSHF])
        tHF = self.tok("HF")
        fcw, fcb = "fcw%d" % l, "fcb%d" % l
        def ld_wup(u):
            wt_, twt_ = wsl[u % 3]
            self.dma("pool", wt_[:, 0], self.w_up[l][:, u * 256:(u + 1) * 256].rearrange("(kc p) f -> p kc f", p=128), (), [twt_[0]], "ldw%d" % (u % 3))
            self.dma("pool", wt_[:, 1], self.w_up[l][:, DFF + u * 256:DFF + (u + 1) * 256].rearrange("(kc p) f -> p kc f", p=128), (), [twt_[1]], "ldwb%d" % (u % 3))

        def ld_wdn(dc):
            wd_, twd_ = wdn[dc % 3]
            self.dma("pool", wd_, self.w_down[l][:, dc * 128:(dc + 1) * 128].rearrange("(i p) d -> p i d", p=128), (), [twd_], "ldwd%d" % (dc % 3))
        ld_wup(0); ld_wup(1)
        pend = []
        for u in range(11):
            wt, twt = wsl[u % 3]
            if u + 2 < 11:
                ld_wup(u + 2)
            elif u + 2 == 11:
                ld_wdn(0)
            else:
                ld_wdn(1)
            for j in range(2):
                i = u * 2 + j
                prev = [None, None]
                for tile in st.tiles():
                    seg, t0, n = tile
                    c0 = seg.off + t0
                    conv = []
                    for ab in range(2):
                        fc = i + 22 * ab
                        pb, tpb = self.bank()
                        for kc in range(8):
                            self.mm(pb[:, 0:n], wt[:, ab, kc, j * 128:(j + 1) * 128], xn[:, kc, c0:c0 + n], [twt[ab], xnt(kc, tile)[1]], [tpb],
                                    start=(kc == 0), stop=(kc == 7))
                        ex, tex = ub[self.rr("ub", [0, 1, 2, 3])]
                        if seg.kind == "S":
                            self.cp("pool", self.ext_halo(seg, ex, 2), SHF[:, fc, :].rearrange("p (s r) -> p s r", r=2), [tSHF], [tex])
                        elif t0 == 0:
                            self.cp("pool", ex[:, 0:2], self.HF[:, l, fc, :], [tHF], [tex])
                        else:
                            pe_, tpe_, pn = prev[ab]
                            self.cp("pool", ex[:, 0:2], pe_[:, pn:pn + 2], [tpe_], [tex])
                        self.cp("act", self.ext_dst(seg, ex, 2, n), self.V(seg, pb[:, 0:n]), [tpb], [tex])
                        prev[ab] = (ex, tex, n)
                        if seg.kind == "S":
                            self.cp("pool", SHF[:, fc, :].rearrange("p (s r) -> p s r", r=2), self.ext_tail(seg, ex, 2, n), [tex], [tSHF])
                        elif t0 + n == seg.n:
                            self.cp("pool", self.HF[:, l, fc, :], ex[:, n:n + 2], [tex], [tHF])
                        tb, ttb = t0b[self.rr("t0b", [0, 1, 2, 3])]
                        tv = self.V(seg, tb[:, 0:n])
                        self.act(tv, self.V(seg, pb[:, 0:n]), AF.Identity, [tpb, self.tC], [ttb], bias=self.vcol(fcb, fc), scale=self.vcol(fcw, 2 * NFC + fc))
                        self.stt(tv, self.ext_tap(seg, ex, 2, 1, n), self.vcol(fcw, 1 * NFC + fc), tv, ALU.mult, ALU.add, [tex, ttb, self.tC], [ttb])
                        self.stt(tv, self.ext_tap(seg, ex, 2, 0, n), self.vcol(fcw, 0 * NFC + fc), tv, ALU.mult, ALU.add, [tex, ttb, self.tC], [ttb])
                        conv.append((tb, ttb))
                    def tail(conv=conv, i=i, c0=c0, n=n):
                        sa, tsa = sab[self.rr("sab", [0, 1])]
                        self.act(sa[:, 0:n], conv[0][0][:, 0:n], AF.Silu, [conv[0][1]], [tsa])
                        self.tt("dve", hT[:, i, c0:c0 + n], sa[:, 0:n], conv[1][0][:, 0:n], ALU.mult, [tsa, conv[1][1]], [self.tok("hT", self.phase_id, i, c0)])
                    if pend:
                        pend.pop(0)()
                    pend.append(tail)
        while pend:
            pend.pop(0)()
        if hasS:
            for fc in range(NFC):
                pb, tpb = self.bank()
                self.tr(pb[0:32, 0:128], SHF[:, fc, :], self.idf[:, :], [tSHF, self.tC], [tpb])
                self.cp(self.rr("ev", ["act", "dve"]), stg[0:32, fc * 128:(fc + 1) * 128], pb[0:32, 0:128], [tpb], [tstg])
            self.dma("sp", self.o_sffn[l], stg, [tstg], (), "stst")
        if st.last:
            stg2, tstg2 = self.A([2, 5632], F32)
            for fc in range(NFC):
                pb, tpb = self.bank()
                self.tr(pb[0:2, 0:128], self.HF[:, l, fc, :], self.idf[:, :], [tHF, self.tC], [tpb])
                self.cp(self.rr("ev", ["act", "dve"]), stg2[0:2, fc * 128:(fc + 1) * 128], pb[0:2, 0:128], [tpb], [tstg2])
            self.dma("sp", self.o_pffn[l], stg2, [tstg2], (), "stst")
        for dc in range(8):
            wd, twd = wdn[dc % 3]
            if dc + 2 < 8:
                ld_wdn(dc + 2)
            for tile in st.tiles():
                seg, t0, n = tile
                c0 = seg.off + t0
                pb, tpb = self.bank()
                for i in range(22):
                    self.mm(pb[:, 0:n], wd[:, i, :], hT[:, i, c0:c0 + n], [twd, self.tok("hT", self.phase_id, i, c0)], [tpb], start=(i == 0), stop=(i == 21))
                xv = self.xT[:, dc, c0:c0 + n]
                self.tt("dve", xv, pb[:, 0:n], xv, ALU.add, [tpb], [self.xtok(dc, tile)])

    def pool_mixer(self, st):
        self.phase()
        sq2 = [self.A([128, 8, 512], BF16) for _ in range(2)]
        rsb = [self.A([128, 512], F32) for _ in range(2)]
        pw, tpw = self.A([128, 4, 2, 256], BF16)
        self.dma("pool", pw, self.pool_w.rearrange("g (ci p) e -> p g ci e", p=128), (), [tpw], "ldw0")
        segbuf = {}
        for seg in st.segs:
            W = 16 * 23 if seg.kind == "S" else 15 + seg.n
            hn, _ = self.A([128, 8, W], F32)
            s1, _ = self.A([128, 2, W], F32)
            s2, _ = self.A([128, 2, W], F32)
            PL, _ = self.A([128, 8, seg.n], BF16)
            segbuf[id(seg)] = (hn, s1, s2, PL, W)
        if any(s.kind == "S" for s in st.segs):
            stg, tstg = self.A([120, 2, D], F32)
            self._pcb = [self.A([128, 120], F32) for _ in range(2)]
        tmp15, ttmp15 = self.A([128, 15], F32)
        tHP = self.tok("HP")

        def dstf(dc, tile):
            seg, t0, n = tile
            hn = segbuf[id(seg)][0]
            if seg.kind == "S":
                ap = hn[:, dc, :].rearrange("p (s w) -> p s w", w=23)[:, :, 15:23]
            else:
                ap = hn[:, dc, 15 + t0:15 + t0 + n]
            return ap, self.tok("hn", self.phase_id, id(seg), dc)

        self._norm_pool(st, "nm1", dstf, sq2, rsb)
        for seg in st.segs:
            hn, s1, s2, PL, W = segbuf[id(seg)]
            n = seg.n
            if seg.kind == "S":
                for half in range(2):
                    self.dma("sp", stg[:, half, :], self.st_pool[half * 120:(half + 1) * 120, :], (), [tstg], "ldst")
                for dc in range(8):
                    pb, tpb = self.bank()
                    for half in range(2):
                        self.tr(pb[:, half * 120:(half + 1) * 120], stg[0:120, half, dc * 128:(dc + 1) * 128], self.idf[0:120, 0:120], [tstg, self.tC], [tpb])
                    self.cp(self.rr("ev", ["act", "dve"]), hn[:, dc, :].rearrange("p (s w) -> p s w", w=23)[:, :, 0:15],
                            pb[:, 0:240].rearrange("p (s r) -> p s r", r=15), [tpb], [self.tok("hn", self.phase_id, id(seg), dc)])
            else:
                for dc in range(8):
                    self.cp("pool", hn[:, dc, 0:15], self.HP[:, dc, :], [tHP], [self.tok("hn", self.phase_id, id(seg), dc)])
            if seg.kind == "S":
                e3 = lambda ap: ap.rearrange("p (s w) -> p s w", w=23)
                sl = lambda ap, a, b: e3(ap)[:, :, a:b]
                WW = 23
            else:
                sl = lambda ap, a, b: ap[:, a:b]
                WW = W
            for dc in range(8):
                gi = dc // 2
                th = self.tok("hn", self.phase_id, id(seg), dc)
                ts1 = self.tok("ps1", self.phase_id, id(seg), dc % 2)
                ts2 = self.tok("ps2", self.phase_id, id(seg), dc % 2)
                src, tsrc = hn[:, dc, :], th
                bufs = [(s1[:, dc % 2, :], ts1), (s2[:, dc % 2, :], ts2)]
                for lev in range(gi + 1):
                    sh = 1 << lev
                    lo = (1 << (lev + 1)) - 1
                    dstb, tdb = bufs[lev % 2]
                    self.tt("pool", sl(dstb, lo, WW), sl(src, lo, WW), sl(src, lo - sh, WW - sh), ALU.add, [tsrc], [tdb])
                    src, tsrc = dstb, tdb
                if seg.kind == "S":
                    outv = PL[:, dc, :].rearrange("p (s t) -> p s t", t=8)
                else:
                    outv = PL[:, dc, :]
                tPL = self.tok("PL", self.phase_id, id(seg), dc)
                self.stt(outv, sl(src, 15, WW), 1.0 / WINS[gi], sl(hn[:, dc, :], 15, WW), ALU.mult, ALU.subtract, [tsrc, th], [tPL])
                if seg.kind == "P" and seg.pos0 == 0:
                    ic = self.cst[:, C_INVC + gi * 15:C_INVC + gi * 15 + 15]
                    self.tt("dve", tmp15, src[:, 15:30], ic, ALU.mult, [tsrc, self.tC], [ttmp15])
                    self.tt("dve", PL[:, dc, 0:15], tmp15, hn[:, dc, 15:30], ALU.subtract, [ttmp15, th], [tPL])
            if seg.kind == "S":
                for dc in range(8):
                    th = self.tok("hn", self.phase_id, id(seg), dc)
                    for half in range(2):
                        pb, tpb = self.bank()
                        src3 = hn[:, dc, :].rearrange("p (s w) -> p s w", w=23)[:, half * 8:(half + 1) * 8, 8:23]
                        cbuf, tcb = self._pcb[self.rr("pcb", [0, 1])]
                        self.cp("pool", cbuf.rearrange("p (s r) -> p s r", r=15), src3, [th], [tcb])
                        self.tr(pb[0:120, 0:128], cbuf, self.idf[:, :], [tcb, self.tC], [tpb])
                        self.cp(self.rr("ev", ["act", "dve"]), stg[0:120, half, dc * 128:(dc + 1) * 128], pb[0:120, 0:128], [tpb], [tstg])
                for half in range(2):
                    self.dma("sp", self.o_spool[half * 120:(half + 1) * 120, :], stg[:, half, :], [tstg], (), "stst")
            else:
                for dc in range(8):
                    th = self.tok("hn", self.phase_id, id(seg), dc)
                    self.cp("pool", self.HP[:, dc, :], hn[:, dc, n:n + 15], [th], [tHP])
                if st.last:
                    stg2, tstg2 = self.A([15, D], F32)
                    for dc in range(8):
                        pb, tpb = self.bank()
                        self.tr(pb[0:15, 0:128], self.HP[:, dc, :], self.idf[:, :], [tHP, self.tC], [tpb])
                        self.cp(self.rr("ev", ["act", "dve"]), stg2[0:15, dc * 128:(dc + 1) * 128], pb[0:15, 0:128], [tpb], [tstg2])
                    self.dma("sp", self.o_ppool, stg2, [tstg2], (), "stst")
            for tile in seg.tiles():
                _, t0, nn = tile
                c0 = seg.off + t0
                for gi in range(4):
                    for eo in range(2):
                        dco = 2 * gi + eo
                        pb, tpb = self.bank()
                        for ci in range(2):
                            self.mm(pb[:, 0:nn], pw[:, gi, ci, eo * 128:(eo + 1) * 128], PL[:, 2 * gi + ci, t0:t0 + nn],
                                    [tpw, self.tok("PL", self.phase_id, id(seg), 2 * gi + ci)], [tpb], start=(ci == 0), stop=(ci == 1))
                        xv = self.xT[:, dco, c0:c0 + nn]
                        self.stt(xv, pb[:, 0:nn], self.vcol("psc", dco), xv, ALU.mult, ALU.add, [tpb, self.tC], [self.xtok(dco, tile)])

    def _norm_pool(self, st, wname, dstf, sq2, rsb):
        for tile in st.tiles():
            seg, t0, n = tile
            c0 = seg.off + t0
            sq, tsq = sq2[self.rr("sq2", [0, 1])]
            rs, trs = rsb[self.rr("rsb", [0, 1])]
            pb, tpb = self.bank()
            for dc in range(8):
                xin = self.xT[:, dc, c0:c0 + n]
                if dc % 2 == 0:
                    self.tt("pool", sq[:, dc, 0:n], xin, xin, ALU.mult, [self.xtok(dc, tile)], [tsq])
                else:
                    self.act(sq[:, dc, 0:n], xin, AF.Square, [self.xtok(dc, tile)], [tsq])
            for dc in range(8):
                self.mm(pb[:, 0:n], self.ones_m[:], sq[:, dc, 0:n], [tsq, self.tC], [tpb], start=(dc == 0), stop=(dc == 7))
            self.act(rs[:, 0:n], pb[:, 0:n], AF.Ln, [tpb], [trs], bias=EPS)
            self.act(rs[:, 0:n], rs[:, 0:n], AF.Exp, [trs], [trs], scale=-0.5)
            for dc in range(8):
                dst, tdst = dstf(dc, tile)
                self.stt(dst, self.V(seg, self.xT[:, dc, c0:c0 + n]), self.vcol(wname, dc), self.V(seg, rs[:, 0:n]), ALU.mult, ALU.mult,
                         [self.xtok(dc, tile), trs, self.tC], [tdst])

    def final(self, st):
        self.phase()
        sq2 = [self.A([128, 8, 512], BF16) for _ in range(2)]
        rsb = [self.A([128, 512], F32) for _ in range(2)]
        yT = [self.A([128, 8, 512], F32) for _ in range(2)]
        ysg = [self.A([128, D], F32) for _ in range(3)]
        cur = {}

        def dstf(dc, tile):
            return cur["y"][0][:, dc, 0:tile[2]], cur["y"][1]

        for tile in st.tiles():
            seg, t0, n = tile
            cur["y"] = yT[self.rr("yT", [0, 1])]
            self._norm_one(tile, "nfin", dstf, sq2, rsb)
            y, ty = cur["y"]
            b0 = 0
            while b0 < n:
                if seg.kind == "P":
                    pos = seg.pos0 + t0 + b0
                    if pos < NMETA:
                        b0 += NMETA - pos
                        continue
                m = min(128, n - b0)
                sg, tsg = ysg[self.rr("ysg", [0, 1, 2])]
                for half in range(2):
                    pb, tpb = self.bank()
                    for j in range(4):
                        dc = half * 4 + j
                        self.tr(pb[0:m, j * 128:(j + 1) * 128], y[:, dc, b0:b0 + m], self.idf[:, :], [ty, self.tC], [tpb])
                    self.cp(("act", "dve")[half], sg[0:m, half * 512:(half + 1) * 512], pb[0:m, :], [tpb], [tsg])
                if seg.kind == "S":
                    self.dma("sp", self.ys[b0:b0 + m, :], sg[0:m, :], [tsg], (), "sty%d" % ((self.rrc["ysg"] - 1) % 3))
                else:
                    r0 = seg.pos0 + t0 + b0 - NMETA
                    self.dma("sp", self.yp[r0:r0 + m, :], sg[0:m, :], [tsg], (), "sty%d" % ((self.rrc["ysg"] - 1) % 3))
                b0 += m

    def _norm_one(self, tile, wname, dstf, sq2, rsb):
        seg, t0, n = tile
        c0 = seg.off + t0
        sq, tsq = sq2[self.rr("sq2", [0, 1])]
        rs, trs = rsb[self.rr("rsb", [0, 1])]
        pb, tpb = self.bank()
        for dc in range(8):
            xin = self.xT[:, dc, c0:c0 + n]
            if dc % 2 == 0:
                self.tt("pool", sq[:, dc, 0:n], xin, xin, ALU.mult, [self.xtok(dc, tile)], [tsq])
            else:
                self.act(sq[:, dc, 0:n], xin, AF.Square, [self.xtok(dc, tile)], [tsq])
        for dc in range(8):
            self.mm(pb[:, 0:n], self.ones_m[:], sq[:, dc, 0:n], [tsq, self.tC], [tpb], start=(dc == 0), stop=(dc == 7))
        self.act(rs[:, 0:n], pb[:, 0:n], AF.Ln, [tpb], [trs], bias=EPS)
        self.act(rs[:, 0:n], rs[:, 0:n], AF.Exp, [trs], [trs], scale=-0.5)
        for dc in range(8):
            dst, tdst = dstf(dc, tile)
            self.stt(dst, self.xT[:, dc, c0:c0 + n], self.vcol(wname, dc), rs[:, 0:n], ALU.mult, ALU.mult,
                     [self.xtok(dc, tile), trs, self.tC], [tdst])


_NC_CACHE = {}


def _get_nc():
    if "nc" not in _NC_CACHE:
        b = Builder()
        _NC_CACHE["nc"] = b.build()
    return _NC_CACHE["nc"]


def kernel(**inp):
    inp = {k: np.asarray(v) for k, v in inp.items()}
    f = lambda a: np.ascontiguousarray(a, dtype=np.float32)
    nc = _get_nc()
    vecs = build_vecs(inp)
    consts = build_consts()
    w_in = f(inp["gdn_w_in"][0])
    wba = np.zeros((D, 40), np.float32)
    wba[:, 0:8] = w_in[:, 4104:4112]
    wba[:, 32:40] = w_in[:, 4096:4104]
    shared = {
        "meta": f(inp["meta_tokens"]), "w_in": w_in, "wba": wba, "w_out": f(inp["gdn_w_out"][0]),
        "pool_w": f(inp["pool_w"][0]), "w_up": f(inp["ffn_w_up"]), "w_down": f(inp["ffn_w_down"]),
        "vecs": vecs, "consts": consts,
    }
    in_maps = []
    for c in range(8):
        sl = slice(16 * c, 16 * c + 16)
        m = dict(shared)
        m["xp"] = f(inp["x_prompt"][c])
        m["xs"] = f(inp["x_sample"][sl].reshape(128, D))
        m["st_conv"] = f(inp["state_gdn_conv"][0, sl].reshape(48, 3072))
        m["st_rec"] = f(inp["state_gdn_rec"][0, sl])
        m["st_pool"] = f(inp["state_pool"][0, sl].reshape(240, D))
        m["st_ffn"] = f(inp["state_ffn_conv"][:, sl].reshape(2, 32, 5632))
        in_maps.append(m)
    res = run_bass_kernel_spmd(nc, in_maps, core_ids=list(range(8)))
    R = res.results
    g = lambda k: [np.asarray(r[k], dtype=np.float32) for r in R]
    y_prompt = np.stack(g("yp"), 0)
    y_sample = np.concatenate(g("ys"), 0).reshape(128, 8, D)
    p_conv = np.stack(g("o_pconv"), 0)[None]
    p_rec = np.stack(g("o_prec"), 0)[None]
    p_pool = np.stack(g("o_ppool"), 0)[None]
    p_ffn = np.stack(g("o_pffn"), 1)
    s_conv = np.concatenate([a.reshape(16, 3, 3072) for a in g("o_sconv")], 0)[None]
    s_rec = np.concatenate(g("o_srec"), 0)[None]
    s_pool = np.concatenate([a.reshape(16, 15, D) for a in g("o_spool")], 0)[None]
    s_ffn = np.concatenate([a.reshape(2, 16, 2, 5632) for a in g("o_sffn")], 1)
    return (y_prompt, y_sample, p_conv, p_rec, p_pool, p_ffn, s_conv, s_rec, s_pool, s_ffn)
```

```python
import contextlib
import numpy as np
import concourse.bass as bass
import concourse.mybir as mybir
from concourse.bass_utils import run_bass_kernel_spmd

F32 = mybir.dt.float32
BF16 = mybir.dt.bfloat16
ALU = mybir.AluOpType
AF = mybir.ActivationFunctionType
AX = mybir.AxisListType

D = 1024
NH = 8
DFF = 2816
NFC = 44
SEQ = 2048
NMETA = 16
EPS = 1e-6
NEG = -1.0e30
DEBUG_MAP = None
WINS = (2, 4, 8, 16)


class Tok:
    __slots__ = ("lastw", "readers", "excl")

    def __init__(self):
        self.lastw = None
        self.readers = []
        self.excl = False


class Op:
    __slots__ = ("eng", "fn", "deps", "ms", "dma_sem", "dma_val", "is_dma", "where")

    def __init__(self, eng, fn):
        import sys as _s
        f = _s._getframe(3)
        self.where = (f.f_lineno, f.f_back.f_lineno if f.f_back else 0)
        self.eng = eng
        self.fn = fn
        self.deps = []
        self.ms = None
        self.is_dma = False
        self.dma_sem = None
        self.dma_val = 0


class Prog:
    ENGS = ("pe", "act", "dve", "pool", "sp")

    def __init__(self, nc):
        self.nc = nc
        self.ops = {e: [] for e in self.ENGS}
        self.streams = {}
        self.pending = {}

    def barrier(self):
        lasts = [self.ops[e][-1] for e in self.ENGS if self.ops[e]]
        lasts += [st[0] for st in self.streams.values() if st[0] is not None]
        for e in self.ENGS:
            self.pending[e] = list(lasts)

    def op(self, eng, fn, reads=(), writes=(), stream=None):
        o = Op(eng, fn)
        is_dma = stream is not None
        deps = []
        for t in reads:
            if t.lastw is not None:
                deps.append((t.lastw, True))
            if t.excl:
                for r in t.readers:
                    if r.eng != eng:
                        deps.append((r, True))
        for t in writes:
            if t.lastw is not None:
                deps.append((t.lastw, False))
            for r in t.readers:
                deps.append((r, False))
        for d in self.pending.pop(eng, []):
            deps.append((d, True))
        if is_dma:
            o.is_dma = True
            st = self.streams.setdefault(stream, [None, 0])
            if st[0] is not None:
                deps.append((st[0], True))
            st[1] += 1
            o.dma_sem = stream
            o.dma_val = 16 * st[1]
            st[0] = o
        seen = set()
        for d, raw in deps:
            if d is o or id(d) in seen:
                continue
            if (not d.is_dma) and (not is_dma) and d.eng == eng and eng == "pe":
                continue
            seen.add(id(d))
            o.deps.append(d)
        for t in reads:
            t.readers.append(o)
        for t in writes:
            t.lastw = o
            t.readers = []
        self.ops[eng].append(o)
        return o

    def emit(self):
        nc = self.nc
        for e in self.ENGS:
            for o in self.ops[e]:
                for d in o.deps:
                    if not d.is_dma:
                        d.ms = True
        for e in self.ENGS:
            k = 0
            for o in self.ops[e]:
                if o.ms and not o.is_dma:
                    k += 1
                    o.ms = k
        with contextlib.ExitStack() as es:
            esem = {e: es.enter_context(nc.semaphore("s_" + e)) for e in self.ENGS}
            dsem = {k: es.enter_context(nc.semaphore("d_%d" % i)) for i, k in enumerate(self.streams)}
            block = es.enter_context(nc.Block())
            prog = self

            def run(e, engobj):
                seen = {}
                for o in prog.ops[e]:
                    for d in o.deps:
                        if d.is_dma:
                            key, val, sem = ("d", d.dma_sem), d.dma_val, dsem[d.dma_sem]
                        else:
                            key, val, sem = ("e", d.eng), d.ms, esem[d.eng]
                        if seen.get(key, 0) >= val:
                            continue
                        seen[key] = val
                        engobj.wait_ge(sem, val)
                    ins = o.fn(engobj)
                    if DEBUG_MAP is not None:
                        try:
                            DEBUG_MAP[str(ins.ins.name)] = o.where
                        except Exception as ex:
                            DEBUG_MAP["err"] = repr(ex)
                    if o.is_dma:
                        ins.then_inc(dsem[o.dma_sem], 16)
                    elif o.ms:
                        ins.then_inc(esem[e], 1)
                if e == "sp":
                    for k, st in prog.streams.items():
                        engobj.wait_ge(dsem[k], 16 * st[1])

            block.tensor(lambda eng: run("pe", eng))
            block.scalar(lambda eng: run("act", eng))
            block.vector(lambda eng: run("dve", eng))
            block.gpsimd(lambda eng: run("pool", eng))
            block.sync(lambda eng: run("sp", eng))


VEC_COLS = {}


def _vec_layout():
    off = 0
    for name, n in (("nm0", 8), ("nm1", 8), ("nf0", 8), ("nf1", 8), ("nfin", 8),
                    ("gcw", 96), ("fcw0", 132), ("fcw1", 132), ("fcb0", 44), ("fcb1", 44),
                    ("psc", 8), ("gnw", 1), ("alog", 1), ("dtb", 1)):
        VEC_COLS[name] = off
        off += n
    return off


NV = _vec_layout()
C_ID, C_TRI, C_NEGU, C_POSL, C_INVC = 0, 128, 192, 256, 320
C_TRI8, C_NEGU8, C_POSL8, C_SEL, C_RM = 380, 444, 508, 572, 636
NCONST = 636 + 8


def build_consts():
    c = np.zeros((128, NCONST), np.float32)
    c[:, C_ID:C_ID + 128] = np.eye(128, dtype=np.float32)
    p = np.arange(64)[:, None]
    f = np.arange(64)[None, :]
    c[:64, C_TRI:C_TRI + 64] = (f >= p).astype(np.float32)
    c[:64, C_NEGU:C_NEGU + 64] = np.where(f >= p, 0.0, NEG)
    c[:64, C_POSL:C_POSL + 64] = np.where(f < p, 0.0, -NEG)
    for gi, w in enumerate(WINS):
        for t in range(15):
            c[:, C_INVC + gi * 15 + t] = 1.0 / min(w, t + 1)
    same = (p // 8) == (f // 8)
    c[:64, C_TRI8:C_TRI8 + 64] = (same & (f >= p)).astype(np.float32)
    c[:64, C_NEGU8:C_NEGU8 + 64] = np.where(same & (f >= p), 0.0, NEG)
    c[:64, C_POSL8:C_POSL8 + 64] = np.where(same & (f < p), 0.0, -NEG)
    c[:64, C_SEL:C_SEL + 64] = (p == 8 * (f // 8) + 7).astype(np.float32)
    c[:64, C_RM:C_RM + 8] = ((p // 8) == np.arange(8)[None, :]).astype(np.float32)
    return c


def build_vecs(inp):
    v = np.zeros((128, NV), np.float32)

    def put(name, arr):
        a = np.asarray(arr, np.float32).reshape(-1, 128).T
        v[:, VEC_COLS[name]:VEC_COLS[name] + a.shape[1]] = a

    put("nm0", inp["norm_mix"][0]); put("nm1", inp["norm_mix"][1])
    put("nf0", inp["norm_ffn"][0]); put("nf1", inp["norm_ffn"][1])
    put("nfin", inp["norm_final"])
    put("gcw", inp["gdn_conv_w"][0].reshape(-1))
    put("fcw0", inp["ffn_conv_w"][0].reshape(-1)); put("fcw1", inp["ffn_conv_w"][1].reshape(-1))
    put("fcb0", inp["ffn_conv_b"][0]); put("fcb1", inp["ffn_conv_b"][1])
    put("psc", inp["pool_scale"][0])
    put("gnw", inp["gdn_norm_w"][0])
    v[0:8, VEC_COLS["alog"]] = inp["gdn_A_log"][0]
    v[0:8, VEC_COLS["dtb"]] = inp["gdn_dt_bias"][0]
    return v


class Seg:
    def __init__(self, kind, n, off, pos0=0):
        self.kind, self.n, self.off, self.pos0 = kind, n, off, pos0

    def tiles(self):
        if self.kind == "S":
            return [(self, 0, 128)]
        k = (self.n + 511) // 512
        base = (self.n // k + 7) // 8 * 8
        out, t = [], 0
        while t < self.n:
            m = min(base, self.n - t)
            out.append((self, t, m))
            t += m
        return out


class ST:
    def __init__(self, segs, first, last):
        self.segs, self.first, self.last = segs, first, last
        self.NT = sum(s.n for s in segs)

    def tiles(self):
        return [t for s in self.segs for t in s.tiles()]


SUPER = [
    ST([Seg("P", 592, 0, 0), Seg("S", 128, 592)], True, False),
    ST([Seg("P", 704, 0, 592)], False, False),
    ST([Seg("P", 768, 0, 1296)], False, True),
]
NTMAX = 768


class Builder:
    def __init__(self):
        self.nc = nc = bass.Bass("TRN2", target_bir_lowering=False)
        self.P = Prog(nc)
        self.es = contextlib.ExitStack()
        self.toks = {}
        self.rrc = {}
        self.phase_id = 0

        def din(name, shape):
            return nc.dram_tensor(name, list(shape), F32, kind="ExternalInput").ap()

        def dout(name, shape):
            return nc.dram_tensor(name, list(shape), F32, kind="ExternalOutput").ap()

        self.xp = din("xp", [SEQ, D]); self.xs = din("xs", [128, D])
        self.st_conv = din("st_conv", [48, 3072]); self.st_rec = din("st_rec", [16, 8, 128, 128])
        self.st_pool = din("st_pool", [240, D]); self.st_ffn = din("st_ffn", [2, 32, 5632])
        self.meta = din("meta", [NMETA, D])
        self.w_in = din("w_in", [D, 4112]); self.wba = din("wba", [D, 40])
        self.w_out = din("w_out", [D, D]); self.pool_w = din("pool_w", [4, 256, 256])
        self.w_up = din("w_up", [2, D, 5632]); self.w_down = din("w_down", [2, DFF, D])
        self.vecs_d = din("vecs", [128, NV]); self.consts_d = din("consts", [128, NCONST])
        self.yp = dout("yp", [SEQ, D]); self.ys = dout("ys", [128, D])
        self.o_pconv = dout("o_pconv", [3, 3072]); self.o_prec = dout("o_prec", [8, 128, 128])
        self.o_ppool = dout("o_ppool", [15, D]); self.o_pffn = dout("o_pffn", [2, 2, 5632])
        self.o_sconv = dout("o_sconv", [48, 3072]); self.o_srec = dout("o_srec", [16, 8, 128, 128])
        self.o_spool = dout("o_spool", [240, D]); self.o_sffn = dout("o_sffn", [2, 32, 5632])

    def tok(self, *key):
        t = self.toks.get(key)
        if t is None:
            t = self.toks[key] = Tok()
            if key[0] == "bank":
                t.excl = True
        return t

    def sb(self, name, shape, dt):
        return self.es.enter_context(self.nc.sbuf_tensor(name, list(shape), dt))

    def rr(self, name, choices):
        i = self.rrc.get(name, 0)
        self.rrc[name] = i + 1
        return choices[i % len(choices)]

    def bank(self):
        i = self.rrc.get("bank", 0)
        self.rrc["bank"] = i + 1
        i %= 8
        return self.banks[i], self.tok("bank", i)

    def phase(self):
        self.P.barrier()
        self.aoff = 0
        self.phase_id += 1

    def A(self, shape, dt, key=None):
        n = int(np.prod(shape[1:]))
        nb = n * (4 if dt == F32 else 2)
        nb = (nb + 31) // 32 * 32
        ne = nb // 2
        assert self.aoff + ne <= self.arena_n, ("arena overflow", self.aoff, ne, self.arena_n)
        ap = self.arena[0:shape[0], self.aoff:self.aoff + ne]
        self.aoff += ne
        if dt == F32:
            ap = ap.bitcast(F32)
        ap = ap[:, 0:n]
        if len(shape) == 3:
            ap = ap.rearrange("p (a b) -> p a b", b=shape[2])
        elif len(shape) == 4:
            ap = ap.rearrange("p (a b c) -> p a b c", b=shape[2], c=shape[3])
        return ap, self.tok("arena", self.phase_id, self.aoff)

    def mm(self, out, lhsT, rhs, r, w, start=True, stop=True, skip=False):
        if skip:
            self.P.op("pe", lambda e: e.matmul(out, lhsT=lhsT, rhs=rhs, start=start, stop=stop, skip_group_check=True), r, w)
        else:
            self.P.op("pe", lambda e: e.matmul(out, lhsT=lhsT, rhs=rhs, start=start, stop=stop), r, w)

    def tr(self, out, in_, ident, r, w):
        self.P.op("pe", lambda e: e.transpose(out=out, in_=in_, identity=ident), r, w)

    def act(self, out, in_, func, r, w, bias=None, scale=None):
        kw = {}
        if bias is not None:
            kw["bias"] = bias
        if scale is not None:
            kw["scale"] = scale
        self.P.op("act", lambda e: e.activation(out=out, in_=in_, func=func, **kw), r, w)

    def tt(self, eng, out, in0, in1, op, r, w):
        self.P.op(eng, lambda e: e.tensor_tensor(out=out, in0=in0, in1=in1, op=op), r, w)

    def ts(self, eng, out, in0, s1, op0, r, w, s2=None, op1=None):
        if op1 is None:
            self.P.op(eng, lambda e: e.tensor_scalar(out=out, in0=in0, scalar1=s1, scalar2=None, op0=op0), r, w)
        else:
            self.P.op(eng, lambda e: e.tensor_scalar(out=out, in0=in0, scalar1=s1, scalar2=s2, op0=op0, op1=op1), r, w)

    def stt(self, out, in0, scalar, in1, op0, op1, r, w):
        self.P.op("dve", lambda e: e.scalar_tensor_tensor(out=out, in0=in0, scalar=scalar, in1=in1, op0=op0, op1=op1), r, w)

    def cp(self, eng, out, in_, r, w):
        if eng == "act":
            self.act(out, in_, AF.Copy, r, w)
        else:
            self.P.op(eng, lambda e: e.tensor_copy(out=out, in_=in_), r, w)

    def dma(self, q, out, in_, r, w, stream):
        self.P.op(q, lambda e: e.dma_start(out=out, in_=in_), r, w, stream=stream)

    def memset(self, eng, ap, val, w):
        self.P.op(eng, lambda e: e.memset(ap, val), (), w)

    def vcol(self, name, j=0, np_=128):
        c = VEC_COLS[name] + j
        return self.vecs[0:np_, c:c + 1]

    @staticmethod
    def V(seg, ap):
        if seg.kind == "S":
            return ap.rearrange("p (s t) -> p s t", t=8)
        return ap

    @staticmethod
    def ext_dst(seg, buf, H, n):
        if seg.kind == "S":
            return buf[:, 0:16 * (H + 8)].rearrange("p (s w) -> p s w", w=H + 8)[:, :, H:H + 8]
        return buf[:, H:H + n]

    @staticmethod
    def ext_tap(seg, buf, H, j, n):
        if seg.kind == "S":
            return buf[:, 0:16 * (H + 8)].rearrange("p (s w) -> p s w", w=H + 8)[:, :, j:j + 8]
        return buf[:, j:j + n]

    @staticmethod
    def ext_halo(seg, buf, H):
        if seg.kind == "S":
            return buf[:, 0:16 * (H + 8)].rearrange("p (s w) -> p s w", w=H + 8)[:, :, 0:H]
        return buf[:, 0:H]

    @staticmethod
    def ext_tail(seg, buf, H, n):
        if seg.kind == "S":
            return buf[:, 0:16 * (H + 8)].rearrange("p (s w) -> p s w", w=H + 8)[:, :, 8:8 + H]
        return buf[:, n:n + H]

    def build(self):
        nc = self.nc
        with self.es:
            self.xT = self.sb("xT", [128, 8, NTMAX], F32)
            self.S32 = self.sb("S32", [128, 8, 128], F32)
            self.S16 = self.sb("S16", [128, 8, 128], BF16)
            self.HG = self.sb("HG", [128, 24, 3], F32)
            self.HF = self.sb("HF", [128, 2, NFC, 2], F32)
            self.HP = self.sb("HP", [128, 8, 15], F32)
            self.vecs = self.sb("vecs_sb", [128, NV], F32)
            self.cst = self.sb("cst", [128, NCONST], F32)
            self.idb = self.sb("idb", [128, 128], BF16)
            self.ones_m = self.sb("ones_m", [128, 128], BF16)
            self.ones_1 = self.sb("ones_1", [128, 128], BF16)
            self.ones_f = self.sb("ones_f", [64, 128], F32)
            self.nexpA = self.sb("nexpA", [8, 1], F32)
            self.lnq = self.sb("lnq", [128, 1], F32)
            self.banks = [self.es.enter_context(nc.psum_tensor("pb%d" % i, [128, 512], F32)) for i in range(8)]
            rem = nc.sbuf_bytes_remaining - 2048
            self.arena_n = (rem // 2) // 64 * 64
            self.arena = self.sb("arena", [128, self.arena_n], BF16)
            self.aoff = 0
            self.idf = self.cst[:, C_ID:C_ID + 128]
            tC = self.tok("consts")
            self.dma("sp", self.vecs[:], self.vecs_d, (), [tC], "ldc0")
            self.dma("sp", self.cst[:], self.consts_d, (), [tC], "ldc1")
            self.cp("dve", self.idb[:], self.idf, [tC], [tC])
            self.memset("pool", self.ones_m[:], 1.0 / 1024.0, [tC])
            self.memset("pool", self.ones_1[:], 1.0, [tC])
            self.memset("pool", self.ones_f[:], 1.0, [tC])
            self.memset("pool", self.lnq[:], -0.5 * float(np.log(128.0)), [tC])
            self.memset("pool", self.S32[:], 0.0, [self.tok("S32")])
            self.memset("pool", self.S16[:], 0.0, [self.tok("S16")])
            self.memset("pool", self.HG[:], 0.0, [self.tok("HG")])
            self.memset("pool", self.HF[:], 0.0, [self.tok("HF")])
            self.memset("pool", self.HP[:], 0.0, [self.tok("HP")])
            self.act(self.nexpA[:], self.vcol("alog", 0, 8), AF.Exp, [tC], [tC])
            self.ts("dve", self.nexpA[:], self.nexpA[:], -1.0, ALU.mult, [tC], [tC])
            self.tC = tC
            for st in SUPER:
                self.run_super(st)
            self.P.emit()
        return nc

    def run_super(self, st):
        self.load_x(st)
        self.gdn(st)
        self.ffn(st, 0)
        self.pool_mixer(st)
        self.ffn(st, 1)
        self.final(st)

    def xtok(self, dc, tile):
        return self.tok("xT", dc, tile[0].off + tile[1])

    def load_x(self, st):
        if st.first:
            self.phase()
        stg = [self.A([128, 4, D], F32) for _ in range(2)]
        bi = 0
        for seg in st.segs:
            for (_, t0, n) in seg.tiles():
                sg, tsg = stg[bi % 2]
                bi += 1
                nb = (n + 127) // 128
                for b in range(nb):
                    m = min(128, n - b * 128)
                    if seg.kind == "S":
                        self.dma("sp", sg[0:m, b, :], self.xs[0:m, :], (), [tsg], "ldx")
                    else:
                        p0 = seg.pos0 + t0 + b * 128
                        r = 0
                        if p0 < NMETA:
                            k = min(m, NMETA - p0)
                            self.dma("sp", sg[0:k, b, :], self.meta[p0:p0 + k, :], (), [tsg], "ldx")
                            r = k
                        if r < m:
                            a = p0 + r - NMETA
                            self.dma("sp", sg[r:m, b, :], self.xp[a:a + (m - r), :], (), [tsg], "ldx")
                tile = (seg, t0, n)
                c0 = seg.off + t0
                for dc in range(8):
                    pb, tpb = self.bank()
                    for b in range(nb):
                        m = min(128, n - b * 128)
                        self.tr(pb[:, b * 128:b * 128 + m], sg[0:m, b, dc * 128:(dc + 1) * 128], self.idf[0:m, 0:m],
                                [tsg, self.tC], [tpb])
                    self.cp(self.rr("ev", ["act", "dve"]), self.xT[:, dc, c0:c0 + n], pb[:, 0:n], [tpb], [self.xtok(dc, tile)])

    def norm(self, st, wname, dstf, sq2, rsb):
        for tile in st.tiles():
            seg, t0, n = tile
            c0 = seg.off + t0
            sq, tsq = sq2[self.rr("sq2", [0, 1])]
            rs, trs = rsb[self.rr("rsb", [0, 1])]
            pb, tpb = self.bank()
            for dc in range(8):
                xin = self.xT[:, dc, c0:c0 + n]
                if dc % 2 == 0:
                    self.tt("pool", sq[:, dc, 0:n], xin, xin, ALU.mult, [self.xtok(dc, tile)], [tsq])
                else:
                    self.act(sq[:, dc, 0:n], xin, AF.Square, [self.xtok(dc, tile)], [tsq])
            for dc in range(8):
                self.mm(pb[:, 0:n], self.ones_m[:], sq[:, dc, 0:n], [tsq, self.tC], [tpb], start=(dc == 0), stop=(dc == 7))
            self.act(rs[:, 0:n], pb[:, 0:n], AF.Ln, [tpb], [trs], bias=EPS)
            self.act(rs[:, 0:n], rs[:, 0:n], AF.Exp, [trs], [trs], scale=-0.5)
            for dc in range(8):
                dst, tdst = dstf(dc, tile)
                self.stt(dst, self.xT[:, dc, c0:c0 + n], self.vcol(wname, dc), rs[:, 0:n], ALU.mult, ALU.mult,
                         [self.xtok(dc, tile), trs, self.tC], [tdst])

    def gdn(self, st):
        NT = st.NT
        self.phase()
        hasS = any(s.kind == "S" for s in st.segs)
        xn, _ = self.A([128, 8, NT], BF16)
        QKVZ, _ = self.A([128, 32, NT], BF16)
        GB, tGB = self.A([40, NT], F32)
        mark = self.aoff
        sq2 = [self.A([128, 8, 512], BF16) for _ in range(2)]
        rsb = [self.A([128, 512], F32) for _ in range(2)]
        wsl = [self.A([128, 8, 512], BF16) for _ in range(3)]
        wbat, twba = self.A([128, 8, 40], BF16)
        ext = [self.A([128, 3 + 512], F32) for _ in range(3)]
        acc = [self.A([128, 512], F32) for _ in range(2)]
        sil = [self.A([128, 512], F32) for _ in range(2)]
        sqh = [self.A([128, 512], BF16) for _ in range(3)]
        rin = [self.A([128, 512], F32) for _ in range(2)]
        bat = [self.A([8, 512], F32) for _ in range(4)]
        if hasS:
            SHG, tSHG = self.A([128, 24, 48], F32)
            stg, tstg = self.A([48, 3072], F32)
        self.memset("pool", GB, 0.0, [tGB])
        xnt = lambda dc, tile: (xn[:, dc, tile[0].off + tile[1]:tile[0].off + tile[1] + tile[2]], self.tok("xn", self.phase_id, dc, tile[0].off + tile[1]))
        self.norm(st, "nm0", xnt, sq2, rsb)
        if hasS:
            self.dma("sp", stg, self.st_conv, (), [tstg], "ldst")
            for g in range(3):
                pb, tpb = self.bank()
                for j in range(8):
                    fc = g * 8 + j
                    self.tr(pb[:, j * 48:(j + 1) * 48], stg[0:48, fc * 128:(fc + 1) * 128], self.idf[0:48, 0:48], [tstg, self.tC], [tpb])
                self.cp("act", SHG[:, g * 8:(g + 1) * 8, :], pb[:, 0:384].rearrange("p (a b) -> p a b", b=48), [tpb], [tSHG])
        self.dma("pool", wbat, self.wba.rearrange("(kc p) f -> p kc f", p=128), (), [twba], "ldwba")
        for tile in st.tiles():
            seg, t0, n = tile
            c0 = seg.off + t0
            pb, tpb = self.bank()
            for kc in range(8):
                self.mm(pb[0:40, 0:n], wbat[:, kc, :], xn[:, kc, c0:c0 + n], [twba, xnt(kc, tile)[1]], [tpb], start=(kc == 0), stop=(kc == 7))
            self.act(GB[32:40, c0:c0 + n], pb[32:40, 0:n], AF.Sigmoid, [tpb], [tGB])
            (b1, t1), (b2, t2), (b3, t3), (b4, t4) = bat
            self.ts("dve", b1[:, 0:n], pb[0:8, 0:n], self.vcol("dtb", 0, 8), ALU.add, [tpb, self.tC], [t1])
            self.stt(b2[:, 0:n], b1[:, 0:n], -1.0, b1[:, 0:n], ALU.mult, ALU.max, [t1], [t2])
            self.act(b3[:, 0:n], b2[:, 0:n], AF.Exp, [t2], [t3], scale=-1.0)
            self.act(b4[:, 0:n], b3[:, 0:n], AF.Ln, [t3], [t4], bias=1.0)
            self.stt(b2[:, 0:n], b1[:, 0:n], 0.0, b4[:, 0:n], ALU.max, ALU.add, [t1, t4], [t2])
            self.ts("dve", GB[0:8, c0:c0 + n], b2[:, 0:n], self.nexpA[:, 0:1], ALU.mult, [t2, self.tC], [tGB])
        def ld_win(u):
            wt_, twt_ = wsl[u % 3]
            self.dma("pool", wt_, self.w_in[:, u * 512:(u + 1) * 512].rearrange("(kc p) f -> p kc f", p=128), (), [twt_], "ldw%d" % (u % 3))
        ld_win(0); ld_win(1)
        pend = []
        qk_list = []
        for u in range(8):
            wt, twt = wsl[u % 3]
            if u + 2 < 8:
                ld_win(u + 2)
            for j in range(4):
                fc = u * 4 + j
                kind = fc // 8
                prev_ext = None
                for tile in st.tiles():
                    seg, t0, n = tile
                    c0 = seg.off + t0
                    pb, tpb = self.bank()
                    for kc in range(8):
                        self.mm(pb[:, 0:n], wt[:, kc, j * 128:(j + 1) * 128], xn[:, kc, c0:c0 + n], [twt, xnt(kc, tile)[1]], [tpb],
                                start=(kc == 0), stop=(kc == 7))
                    dst = QKVZ[:, fc, c0:c0 + n]
                    tdst = self.tok("qkvz", self.phase_id, fc, c0)
                    if kind == 3:
                        self.act(self.V(seg, dst), self.V(seg, pb[:, 0:n]), AF.Silu, [tpb], [tdst])
                        continue
                    ex, tex = ext[self.rr("ext", [0, 1, 2])]
                    if seg.kind == "S":
                        self.cp("pool", self.ext_halo(seg, ex, 3), SHG[:, fc, :].rearrange("p (s r) -> p s r", r=3), [tSHG], [tex])
                    elif t0 == 0:
                        self.cp("pool", ex[:, 0:3], self.HG[:, fc, :], [self.tok("HG")], [tex])
                    else:
                        pe_, tpe_, pn = prev_ext
                        self.cp("pool", ex[:, 0:3], pe_[:, pn:pn + 3], [tpe_], [tex])
                    self.cp("act", self.ext_dst(seg, ex, 3, n), self.V(seg, pb[:, 0:n]), [tpb], [tex])
                    prev_ext = (ex, tex, n)
                    if seg.kind == "S":
                        self.cp("pool", SHG[:, fc, :].rearrange("p (s r) -> p s r", r=3), self.ext_tail(seg, ex, 3, n), [tex], [tSHG])
                    elif t0 + n == seg.n:
                        self.cp("pool", self.HG[:, fc, :], ex[:, n:n + 3], [tex], [self.tok("HG")])
                    ac, tac = acc[self.rr("acc", [0, 1])]
                    av = self.V(seg, ac[:, 0:n])
                    self.act(av, self.ext_tap(seg, ex, 3, 0, n), AF.Identity, [tex, self.tC], [tac], scale=self.vcol("gcw", 0 * 24 + fc))
                    for tap in (1, 2, 3):
                        self.stt(av, self.ext_tap(seg, ex, 3, tap, n), self.vcol("gcw", tap * 24 + fc), av, ALU.mult, ALU.add, [tex, tac, self.tC], [tac])

                    def tail(dst=dst, tdst=tdst, ac=ac, tac=tac, n=n):
                        self.act(dst, ac[:, 0:n], AF.Silu, [tac], [tdst])
                    if pend:
                        pend.pop(0)()
                    pend.append(tail)
                    if kind < 2:
                        qk_list.append((kind, dst, tdst, n))
            while pend:
                pend.pop(0)()
            def nstage1(item):
                kind, dst, tdst, n = item
                sh, tsh = sqh[self.rr("sqh", [0, 1, 2])]
                self.tt("dve", sh[:, 0:n], dst, dst, ALU.mult, [tdst], [tsh])
                pb2, tpb2 = self.bank()
                self.mm(pb2[:, 0:n], self.ones_1[:], sh[:, 0:n], [tsh, self.tC], [tpb2])
                return pb2, tpb2

            def nstage2(item, pb2, tpb2):
                kind, dst, tdst, n = item
                ri, tri_ = rin[self.rr("rin", [0, 1])]
                self.act(ri[:, 0:n], pb2[:, 0:n], AF.Ln, [tpb2], [tri_], bias=EPS)
                self.act(ri[:, 0:n], ri[:, 0:n], AF.Exp, [tri_], [tri_], scale=-0.5, bias=(self.lnq[:, 0:1] if kind == 0 else None))
                self.tt("dve", dst, dst, ri[:, 0:n], ALU.mult, [tdst, tri_], [tdst])
            inflight = []
            for item in qk_list:
                inflight.append((item,) + nstage1(item))
                if len(inflight) > 2:
                    nstage2(*inflight.pop(0))
            while inflight:
                nstage2(*inflight.pop(0))
            qk_list = []
        if hasS:
            for fc in range(24):
                pb, tpb = self.bank()
                self.tr(pb[0:48, 0:128], SHG[:, fc, :], self.idf[:, :], [tSHG, self.tC], [tpb])
                self.cp(self.rr("ev", ["act", "dve"]), stg[0:48, fc * 128:(fc + 1) * 128], pb[0:48, 0:128], [tpb], [tstg])
            self.dma("sp", self.o_sconv, stg, [tstg], (), "stst")
        if st.last:
            stg2, tstg2 = self.A([3, 3072], F32)
            for fc in range(24):
                pb, tpb = self.bank()
                self.tr(pb[0:3, 0:128], self.HG[:, fc, :], self.idf[:, :], [self.tok("HG"), self.tC], [tpb])
                self.cp(self.rr("ev", ["act", "dve"]), stg2[0:3, fc * 128:(fc + 1) * 128], pb[0:3, 0:128], [tpb], [tstg2])
            self.dma("sp", self.o_pconv, stg2, [tstg2], (), "stst")

        self.P.barrier()
        self.aoff = mark
        ONT = xn
        self.bfree = list(range(8))
        NA, NC_, NB = 3, 4, 1
        self.want_onacc = False
        TAs = [self.alloc_chunk_bufs("A") for _ in range(NA)]
        CAs = [self.alloc_chunk_bufs("C") for _ in range(NC_)]
        TBs = [self.alloc_chunk_bufs("B") for _ in range(NB)]
        jobs = []
        for seg in st.segs:
            if seg.kind == "P":
                c = 0
                if seg.pos0 == 0:
                    jobs.append(("P", seg.off, NMETA, None))
                    c = NMETA
                while c < seg.n:
                    jobs.append(("P", seg.off + c, 64, None))
                    c += 64
            else:
                for b_ in range(2):
                    jobs.append(("SB", seg.off + 64 * b_, 64, 8 * b_))
        N = len(jobs)
        nextA = 0
        nextB = 0
        doneA = set()
        actA = {}
        actB = None
        while nextB < N:
            for slot in range(NA):
                if slot not in actA and nextA < N and nextA < nextB + NC_:
                    actA[slot] = (nextA, self.chunk_A(jobs[nextA], QKVZ, GB, tGB, TAs[slot], CAs[nextA % NC_]))
                    nextA += 1
            if actB is None and nextB in doneA:
                if jobs[nextB][0] == "SB":
                    nextB += 1
                    continue
                actB = self.chunk_B(jobs[nextB], CAs[nextB % NC_], TBs[nextB % NB], QKVZ, ONT)
            if actB is not None:
                try:
                    next(actB)
                except StopIteration:
                    actB = None
                    nextB += 1
            for slot in list(actA):
                j, g = actA[slot]
                try:
                    next(g)
                except StopIteration:
                    doneA.add(j)
                    del actA[slot]
        if hasS:
            self.P.barrier()
            onaccs = [self.A([64, 1024], F32) for _ in range(2)]
            save_off = self.aoff
            self.aoff = mark
            NSQ = 3
            assert all((ji % NC_) >= 2 for ji in range(N) if jobs[ji][0] == "SB")
            SBs = []
            for _ in range(NSQ):
                d = {}
                d["S32"] = self.A([128, 8, 128], F32); d["S16"] = self.A([128, 8, 128], BF16)
                d["Stmp"] = self.A([128, 1024], F32); d["vn"] = self.A([64, 1024], BF16)
                SBs.append(d)
            sb_jobs = [(ji, jobs[ji]) for ji in range(N) if jobs[ji][0] == "SB"]
            todo = [(bi, ji, job, s_) for bi, (ji, job) in enumerate(sb_jobs) for s_ in range(8)]
            remaining = {bi: 8 for bi in range(len(sb_jobs))}
            act = {}
            fin = []
            while todo or act or fin:
                for slot in range(NSQ):
                    if slot not in act and todo:
                        bi, ji, job, s_ = todo.pop(0)
                        act[slot] = (bi, self.sample_seq(job, s_, CAs[ji % NC_], SBs[slot], onaccs[bi]))
                for slot in list(act):
                    bi, g = act[slot]
                    try:
                        next(g)
                    except StopIteration:
                        del act[slot]
                        remaining[bi] -= 1
                        if remaining[bi] == 0:
                            ji, job = sb_jobs[bi]
                            fin.append(self.sample_finish(job, TBs[0], onaccs[bi], QKVZ, ONT))
                for g in list(fin):
                    try:
                        next(g)
                    except StopIteration:
                        fin.remove(g)
            self.aoff = max(save_off, self.aoff)
        if st.last:
            self.dma("sp", self.o_prec.rearrange("h k v -> k h v"), self.S32[:], [self.tok("S32")], (), "strec")

        self.P.barrier()
        self.aoff = mark
        wo, two = self.A([128, 8, D], BF16)
        self.dma("pool", wo[:, :, 0:512], self.w_out[:, 0:512].rearrange("(kc p) f -> p kc f", p=128), (), [two], "ldw0")
        self.dma("pool", wo[:, :, 512:1024], self.w_out[:, 512:1024].rearrange("(kc p) f -> p kc f", p=128), (), [two], "ldw1")
        for tile in st.tiles():
            seg, t0, n = tile
            c0 = seg.off + t0
            for dc in range(8):
                pb, tpb = self.bank()
                for kc in range(8):
                    self.mm(pb[:, 0:n], wo[:, kc, dc * 128:(dc + 1) * 128], ONT[:, kc, c0:c0 + n], [two], [tpb], start=(kc == 0), stop=(kc == 7))
                xv = self.xT[:, dc, c0:c0 + n]
                self.tt("dve", xv, pb[:, 0:n], xv, ALU.add, [tpb], [self.xtok(dc, tile)])

    def alloc_chunk_bufs(self, which):
        b = {}
        def a(name, shape, dt):
            b[name] = self.A(shape, dt)
        if which == "A":
            a("gbt", [64, 40], F32)
            for nm in ("Gt", "eG", "nbG", "nb", "dGl", "eGl"):
                a(nm, [64, 8], F32)
            a("Dm", [64, 512], F32); a("Du", [64, 512], F32); a("Dl", [64, 512], F32); a("eGbc", [128, 512], F32)
            b["rhsG"] = b["Dl"]
            a("Lneg", [64, 512], BF16); a("M0", [64, 512], BF16)
            a("QTa", [64, 512], BF16); a("QTb", [64, 512], BF16)
            a("kbgn", [64, 1024], BF16)
        elif which == "C":
            a("PQa", [64, 1024], BF16); a("PQb", [64, 1024], BF16); a("At", [64, 512], BF16)
            a("kd", [64, 1024], BF16); a("vb", [64, 1024], BF16)
            a("nWT", [128, 512], BF16); a("qdT", [128, 512], BF16); a("gtc", [128, 64], F32)
        else:
            a("vn", [64, 1024], BF16); a("sqo", [64, 1024], BF16); a("on", [64, 1024], BF16)
            a("Stmp", [128, 1024], F32); a("ss", [64, 8], F32); a("rs", [64, 8], F32)
            if self.want_onacc:
                a("onacc", [64, 1024], F32)
        return b

    def bacq(self):
        if not self.bfree:
            raise RuntimeError("out of PSUM banks")
        i = self.bfree.pop(0)
        return self.banks[i], self.tok("bank", i), i

    def brel(self, i):
        self.bfree.append(i)

    def chunk_A(self, job, QKVZ, GB, tGB, TA, CA):
        kind, c0, L, sidx = job
        B = dict(TA); B.update(CA)
        tC = self.tC
        Q = lambda h: QKVZ[:, h, c0:c0 + L]
        K = lambda h: QKVZ[:, 8 + h, c0:c0 + L]
        Vv = lambda h: QKVZ[:, 16 + h, c0:c0 + L]
        h3 = lambda ap: ap.rearrange("p (h l) -> p h l", l=L)
        hd = lambda ap: ap.rearrange("p (h d) -> p h d", d=128)
        W8 = 8 * L
        pg, tpg, ipg = self.bacq()
        self.tr(pg[0:L, 0:40], GB[0:40, c0:c0 + L], self.idf[0:40, 0:40], [tGB, tC], [tpg])
        gbt, tgbt = B["gbt"]
        self.cp("dve", gbt[0:L, :], pg[0:L, 0:40], [tpg], [tgbt])
        self.brel(ipg)
        g_tm = gbt[0:L, 0:8]
        beta = gbt[0:L, 32:40]
        blk = (kind == "SB")
        cT, cN, cP = (C_TRI8, C_NEGU8, C_POSL8) if blk else (C_TRI, C_NEGU, C_POSL)
        tri = self.cst[0:L, cT:cT + L]
        yield
        rhsG, trG = B["rhsG"]
        self.tt("pool", h3(rhsG[0:L, 0:W8]), tri.unsqueeze(1).to_broadcast([L, 8, L]), g_tm.unsqueeze(2).to_broadcast([L, 8, L]),
                ALU.mult, [tgbt, tC], [trG])
        pg2, tpg2, ipg2 = self.bacq()
        self.mm(pg2[0:L, 0:8], tri, g_tm, [tgbt, tC], [tpg2])
        Gt, tGt = B["Gt"]; eG, teG = B["eG"]; nbG, tnbG = B["nbG"]; nb, tnb = B["nb"]
        dGl, tdGl = B["dGl"]; eGl, teGl = B["eGl"]; gtc, tgtc = B["gtc"]
        self.cp("dve", Gt[0:L, :], pg2[0:L, 0:8], [tpg2], [tGt])
        self.brel(ipg2)
        self.ts("dve", nb[0:L, :], beta, -1.0, ALU.mult, [tgbt], [tnb])
        yield
        pG, tpG, ipG = self.bacq()
        self.mm(pG[:, 0:W8], self.ones_f[0:L, :], rhsG[0:L, 0:W8], [trG, tC], [tpG])
        self.act(eG[0:L, :], Gt[0:L, :], AF.Exp, [tGt], [teG])
        self.tt("dve", nbG[0:L, :], eG[0:L, :], nb[0:L, :], ALU.mult, [teG, tnb], [tnbG])
        yield
        if blk:
            Glast = None
            gl4 = pG[:, 0:W8].rearrange("p (h s t) -> p h s t", s=8, t=8)[:, :, :, 7]
        else:
            Glast = h3(pG[:, 0:W8])[:, :, L - 1]
        Dm, tDm = B["Dm"]; Du, tDu = B["Du"]; Dl, tDl = B["Dl"]; eGbc, teGbc = B["eGbc"]
        self.tt("dve", h3(Dm[0:L, 0:W8]), h3(pG[0:L, 0:W8]), Gt[0:L, :].unsqueeze(2).to_broadcast([L, 8, L]), ALU.subtract,
                [tpG, tGt], [tDm])
        if blk:
            pgl, tpgl, ipgl = self.bacq()
            self.mm(pgl[0:L, 0:8], self.cst[0:L, C_SEL:C_SEL + L], Gt[0:L, :], [tGt, tC], [tpgl])
            self.tt("dve", dGl[0:L, :], pgl[0:L, 0:8], Gt[0:L, :], ALU.subtract, [tpgl, tGt], [tdGl])
            self.brel(ipgl)
            self.act(gtc[:, 0:64].rearrange("p (h s) -> p h s", s=8), gl4, AF.Exp, [tpG], [tgtc])
        else:
            self.tt("dve", dGl[0:L, :], Glast[0:L], Gt[0:L, :], ALU.subtract, [tpG, tGt], [tdGl])
            self.act(gtc[:, 0:8], Glast, AF.Exp, [tpG], [tgtc])
        self.act(eGbc[:, 0:W8], pG[:, 0:W8], AF.Exp, [tpG], [teGbc])
        self.brel(ipG)
        pk, tpk, ipk = self.bacq()
        pkb = pk[:].bitcast(BF16)
        for h in range(8):
            self.tr(pkb[0:L, h * 128:(h + 1) * 128], K(h), self.idb[:], [tC], [tpk])
        pv, tpv, ipv = self.bacq()
        pvb = pv[:].bitcast(BF16)
        for h in range(8):
            self.tr(pvb[0:L, h * 128:(h + 1) * 128], Vv(h), self.idb[:], [tC], [tpv])
        yield
        self.act(eGl[0:L, :], dGl[0:L, :], AF.Exp, [tdGl], [teGl])
        negu = self.cst[0:L, cN:cN + L].unsqueeze(1).to_broadcast([L, 8, L])
        posl = self.cst[0:L, cP:cP + L].unsqueeze(1).to_broadcast([L, 8, L])
        self.tt("pool", h3(Du[0:L, 0:W8]), h3(Dm[0:L, 0:W8]), negu, ALU.add, [tDm, tC], [tDu])
        self.tt("pool", h3(Dl[0:L, 0:W8]), h3(Dm[0:L, 0:W8]), posl, ALU.add, [tDm, tC], [tDl])
        kbgn, tkb = B["kbgn"]; kd, tkd = B["kd"]; vb, tvb = B["vb"]
        self.tt("dve", hd(kbgn[0:L, :]), hd(pkb[0:L, :]), nbG[0:L, :].unsqueeze(2).to_broadcast([L, 8, 128]), ALU.mult, [tpk, tnbG], [tkb])
        self.tt("dve", hd(vb[0:L, :]), hd(pvb[0:L, :]), beta.unsqueeze(2).to_broadcast([L, 8, 128]), ALU.mult, [tpv, tgbt], [tvb])
        self.brel(ipv)
        yield
        self.tt("dve", hd(kd[0:L, :]), hd(pkb[0:L, :]), eGl[0:L, :].unsqueeze(2).to_broadcast([L, 8, 128]), ALU.mult, [tpk, teGl], [tkd])
        self.brel(ipk)
        self.act(Du[0:L, 0:W8], Du[0:L, 0:W8], AF.Exp, [tDu], [tDu])
        self.act(Dl[0:L, 0:W8], Dl[0:L, 0:W8], AF.Exp, [tDl], [tDl], scale=-1.0)
        pkk, tpkk, ipkk = self.bacq()
        for h in range(8):
            self.mm(pkk[0:L, h * L:(h + 1) * L], K(h), K(h), [], [tpkk])
        pkq, tpkq, ipkq = self.bacq()
        for h in range(8):
            self.mm(pkq[0:L, h * L:(h + 1) * L], K(h), Q(h), [], [tpkq])
        qdT, tqd = B["qdT"]
        self.tt("pool", h3(qdT[:, 0:W8]), QKVZ[:, 0:8, c0:c0 + L], h3(eGbc[:, 0:W8]), ALU.mult, [teGbc], [tqd])
        yield
        self.tt("pool", h3(Dl[0:L, 0:W8]), h3(Dl[0:L, 0:W8]), nb[0:L, :].unsqueeze(2).to_broadcast([L, 8, L]), ALU.mult,
                [tDl, tnb], [tDl])
        Lneg, tLn = B["Lneg"]; At, tAt = B["At"]; M0, tM0 = B["M0"]
        self.tt("dve", At[0:L, 0:W8], pkq[0:L, 0:W8], Du[0:L, 0:W8], ALU.mult, [tpkq, tDu], [tAt])
        self.brel(ipkq)
        yield
        self.tt("dve", Lneg[0:L, 0:W8], pkk[0:L, 0:W8], Dl[0:L, 0:W8], ALU.mult, [tpkk, tDl], [tLn])
        self.brel(ipkk)
        yield
        pm, tpm, ipm = self.bacq()
        pmb = pm[:].bitcast(BF16)
        for h in range(8):
            self.tr(pmb[0:L, h * L:(h + 1) * L], Lneg[0:L, h * L:(h + 1) * L], self.idb[0:L, 0:L], [tLn, tC], [tpm])
        self.cp("act", M0[0:L, 0:W8], pmb[0:L, 0:W8], [tpm], [tM0])
        self.brel(ipm)
        yield
        nlev = 3 if blk else {64: 6, 16: 4, 8: 3}[L]
        idbL = self.idb[0:L, 0:L]
        PQ = [B["PQa"], B["PQb"]]
        QTbufs = [B["QTa"], B["QTb"]]
        pq3 = lambda ap: ap[0:L, :].rearrange("p (h c) -> p h c", c=128)
        cur = 0
        Pc, tPc = PQ[cur]
        self.tt("pool", pq3(Pc)[:, :, 0:L], h3(M0[0:L, 0:W8]), idbL.unsqueeze(1).to_broadcast([L, 8, L]), ALU.add, [tM0, tC], [tPc])
        pq, tpq, ipq = self.bacq()
        for h in range(8):
            sl = slice(h * L, (h + 1) * L)
            self.mm(pq[0:L, sl], M0[0:L, sl], Lneg[0:L, sl], [tM0, tLn], [tpq])
        QTc = QTbufs[0]
        self.cp("act", QTc[0][0:L, 0:W8], pq[0:L, 0:W8], [tpq], [QTc[1]])
        self.brel(ipq)
        pq2, tpq2, ipq2 = self.bacq()
        for h in range(8):
            sl = slice(h * L, (h + 1) * L)
            self.mm(pq2[0:L, sl], Lneg[0:L, sl], M0[0:L, sl], [tM0, tLn], [tpq2])
        self.cp("dve", pq3(Pc)[:, :, L:2 * L], h3(pq2[0:L, 0:W8]), [tpq2], [tPc])
        self.brel(ipq2)
        yield
        for k in range(1, nlev):
            last = (k == nlev - 1)
            Pn, tPn = PQ[1 - cur]
            wid = L if last else 2 * L
            if not last:
                QTn = QTbufs[k % 2]
                pq, tpq, ipq = self.bacq()
                for h in range(8):
                    self.mm(pq[0:L, h * L:(h + 1) * L], pq3(Pc)[:, h, L:2 * L], QTc[0][0:L, h * L:(h + 1) * L], [tPc, QTc[1]], [tpq])
                self.cp("act", QTn[0][0:L, 0:W8], pq[0:L, 0:W8], [tpq], [QTn[1]])
                self.brel(ipq)
            for half in range(2):
                pp, tpp, ipp = self.bacq()
                for hh in range(4):
                    h = half * 4 + hh
                    self.mm(pp[0:L, hh * 128:hh * 128 + wid], QTc[0][0:L, h * L:(h + 1) * L], pq3(Pc)[:, h, 0:wid], [tPc, QTc[1]], [tpp])
                ppv = pp[0:L, :].rearrange("p (h c) -> p h c", c=128)
                hs = slice(half * 4, half * 4 + 4)
                self.tt("dve", pq3(Pn)[:, hs, 0:L], ppv[:, :, 0:L], pq3(Pc)[:, hs, 0:L], ALU.add, [tpp, tPc], [tPn])
                if not last:
                    self.cp("act", pq3(Pn)[:, hs, L:2 * L], ppv[:, :, L:2 * L], [tpp], [tPn])
                self.brel(ipp)
            cur = 1 - cur
            Pc, tPc = PQ[cur]
            if not last:
                QTc = QTn
            yield
        Ttv = pq3(Pc)
        tTt = tPc
        pw, tpw, ipw = self.bacq()
        for h in range(8):
            self.mm(pw[:, h * L:(h + 1) * L], kbgn[0:L, h * 128:(h + 1) * 128], Ttv[:, h, 0:L], [tkb, tTt], [tpw])
        nWT, tnW = B["nWT"]
        self.cp("act", nWT[:, 0:W8], pw[:, 0:W8], [tpw], [tnW])
        self.brel(ipw)
        CA["Tt"] = (Ttv, tTt)
        yield

    def chunk_B(self, job, CA, TB, QKVZ, ONT):
        kind, c0, L, sidx = job
        B = dict(TB); B.update(CA)
        tC = self.tC
        Tt, tTt = CA["Tt"]
        h3 = lambda ap: ap.rearrange("p (h l) -> p h l", l=L)
        hd = lambda ap: ap.rearrange("p (h d) -> p h d", d=128)
        W8 = 8 * L
        if kind == "S":
            S32, tS32 = self.SS32[sidx % 2]
            S16, tS16 = self.SS16[sidx % 2]
            self.dma("sp", S32, self.st_rec[sidx].rearrange("h k v -> k h v"), (), [tS32], "ldrec%d" % (sidx % 2))
            self.cp("pool", S16, S32, [tS32], [tS16])
        else:
            S32, tS32 = self.S32[:], self.tok("S32")
            S16, tS16 = self.S16[:], self.tok("S16")
        vb, tvb = B["vb"]; nWT, tnW = B["nWT"]; vn, tvn = B["vn"]; qdT, tqd = B["qdT"]; At, tAt = B["At"]
        kd, tkd = B["kd"]; sqo, tsq = B["sqo"]; on, ton = B["on"]; ss, tss = B["ss"]; rs, trs = B["rs"]
        gtc, tgtc = B["gtc"]; Stmp, tSt = B["Stmp"]
        for half in range(2):
            pv, tpv, ipv = self.bacq()
            for hh in range(4):
                h = half * 4 + hh
                self.mm(pv[0:L, hh * 128:(hh + 1) * 128], Tt[:, h, 0:L], vb[0:L, h * 128:(h + 1) * 128], [tTt, tvb], [tpv], start=(hh == 0), stop=False, skip=True)
            for hh in range(4):
                h = half * 4 + hh
                self.mm(pv[0:L, hh * 128:(hh + 1) * 128], nWT[:, h * L:(h + 1) * L], S16[:, h, :], [tnW, tS16], [tpv], start=False, stop=True, skip=True)
            self.cp(("act", "dve")[half], vn[0:L, half * 512:(half + 1) * 512], pv[0:L, :], [tpv], [tvn])
            self.brel(ipv)
        self.tt("pool", hd(Stmp[:, :]), S32, gtc[:, 0:8].unsqueeze(2).to_broadcast([128, 8, 128]), ALU.mult, [tS32, tgtc], [tSt])
        yield
        pss = []
        for half in range(2):
            pS, tpS, ipS = self.bacq()
            pss.append((pS, tpS, ipS))
            for hh in range(4):
                h = half * 4 + hh
                self.mm(pS[:, hh * 128:(hh + 1) * 128], kd[0:L, h * 128:(h + 1) * 128], vn[0:L, h * 128:(h + 1) * 128], [tkd, tvn], [tpS])
        for half in range(2):
            pS, tpS, ipS = pss[half]
            self.tt("dve", S32[:, half * 4:(half + 1) * 4, :], hd(pS[:, :]), hd(Stmp[:, half * 512:(half + 1) * 512]), ALU.add, [tpS, tSt], [tS32])
            self.brel(ipS)
        pos = []
        for half in range(2):
            po, tpo, ipo = self.bacq()
            pos.append((po, tpo, ipo))
            for hh in range(4):
                h = half * 4 + hh
                self.mm(po[0:L, hh * 128:(hh + 1) * 128], qdT[:, h * L:(h + 1) * L], S16[:, h, :], [tqd, tS16], [tpo], start=(hh == 0), stop=False, skip=True)
            for hh in range(4):
                h = half * 4 + hh
                self.mm(po[0:L, hh * 128:(hh + 1) * 128], At[0:L, h * L:(h + 1) * L], vn[0:L, h * 128:(h + 1) * 128], [tAt, tvn], [tpo], start=False, stop=True, skip=True)
        self.cp("act", S16, S32, [tS32], [tS16])
        if kind == "S":
            self.dma("sp", self.o_srec[sidx].rearrange("h k v -> k h v"), S32, [tS32], (), "strec%d" % (sidx % 2))
        yield
        for half in range(2):
            po, tpo, ipo = pos[half]
            self.act(sqo[0:L, half * 512:(half + 1) * 512], po[0:L, :], AF.Square, [tpo], [tsq])
        self.P.op("dve", lambda e, o=ss[0:L, :], i=hd(sqo[0:L, :]): e.tensor_reduce(out=o, in_=i, axis=AX.X, op=ALU.add), [tsq], [tss])
        self.act(rs[0:L, :], ss[0:L, :], AF.Ln, [tss], [trs], bias=EPS, scale=1.0 / 128.0)
        self.act(rs[0:L, :], rs[0:L, :], AF.Exp, [trs], [trs], scale=-0.5)
        yield
        for half in range(2):
            po, tpo, ipo = pos[half]
            self.tt("dve", hd(on[0:L, half * 512:(half + 1) * 512]), hd(po[0:L, :]), rs[0:L, half * 4:(half + 1) * 4].unsqueeze(2).to_broadcast([L, 4, 128]),
                    ALU.mult, [tpo, trs], [ton])
            self.brel(ipo)
        yield
        pt, tpt, ipt = self.bacq()
        ptb = pt[:].bitcast(BF16)
        for h in range(8):
            self.tr(ptb[:, h * L:(h + 1) * L], on[0:L, h * 128:(h + 1) * 128], self.idb[0:L, 0:L], [ton, tC], [tpt])
        self.stt(ONT[:, :, c0:c0 + L], h3(ptb[:, 0:W8]), self.vcol("gnw"), QKVZ[:, 24:32, c0:c0 + L], ALU.mult, ALU.mult, [tpt, tC], [self.tok("ONT", self.phase_id)])
        self.brel(ipt)
        yield

    def sample_seq(self, job, s_, CA, SB, onacc_t):
        kind, c0, L, s0 = job
        tC = self.tC
        Tt, tTt = CA["Tt"]
        hd = lambda ap: ap.rearrange("p (h d) -> p h d", d=128)
        vb, tvb = CA["vb"]; nWT, tnW = CA["nWT"]; qdT, tqd = CA["qdT"]; At, tAt = CA["At"]
        kd, tkd = CA["kd"]; gtc, tgtc = CA["gtc"]
        S32, tS32 = SB["S32"]; S16, tS16 = SB["S16"]; Stmp, tSt = SB["Stmp"]; vn, tvn = SB["vn"]
        onacc, tacc = onacc_t
        gtc3 = gtc[:, 0:64].rearrange("p (h s) -> p h s", s=8)
        sidx = s0 + s_
        rm = self.cst[0:L, C_RM + s_:C_RM + s_ + 1]
        self.dma("sp", S32, self.st_rec[sidx].rearrange("h k v -> k h v"), (), [tS32], "ldrec%d" % (sidx % 3))
        yield
        self.cp("pool", S16, S32, [tS32], [tS16])
        self.tt("pool", hd(Stmp[:, :]), S32, gtc3[:, :, s_].unsqueeze(2).to_broadcast([128, 8, 128]), ALU.mult, [tS32, tgtc], [tSt])
        yield
        for half in range(2):
            pv, tpv, ipv = self.bacq()
            for hh in range(4):
                h = half * 4 + hh
                self.mm(pv[0:L, hh * 128:(hh + 1) * 128], Tt[:, h, 0:L], vb[0:L, h * 128:(h + 1) * 128], [tTt, tvb], [tpv], start=(hh == 0), stop=False, skip=True)
            for hh in range(4):
                h = half * 4 + hh
                self.mm(pv[0:L, hh * 128:(hh + 1) * 128], nWT[:, h * L:(h + 1) * L], S16[:, h, :], [tnW, tS16], [tpv], start=False, stop=True, skip=True)
            if half == 0:
                self.act(vn[0:L, 0:512], pv[0:L, :], AF.Identity, [tpv, tC], [tvn], scale=rm)
            else:
                self.ts("dve", vn[0:L, 512:1024], pv[0:L, :], rm, ALU.mult, [tpv, tC], [tvn])
            self.brel(ipv)
        yield
        pss = []
        for half in range(2):
            pS, tpS, ipS = self.bacq()
            pss.append((pS, tpS, ipS))
            for hh in range(4):
                h = half * 4 + hh
                self.mm(pS[:, hh * 128:(hh + 1) * 128], kd[0:L, h * 128:(h + 1) * 128], vn[0:L, h * 128:(h + 1) * 128], [tkd, tvn], [tpS])
        for half in range(2):
            pS, tpS, ipS = pss[half]
            self.tt("dve", S32[:, half * 4:(half + 1) * 4, :], hd(pS[:, :]), hd(Stmp[:, half * 512:(half + 1) * 512]), ALU.add, [tpS, tSt], [tS32])
            self.brel(ipS)
        self.dma("sp", self.o_srec[sidx].rearrange("h k v -> k h v"), S32, [tS32], (), "strec%d" % (sidx % 3))
        pos = []
        for half in range(2):
            po, tpo, ipo = self.bacq()
            pos.append((po, tpo, ipo))
            for hh in range(4):
                h = half * 4 + hh
                self.mm(po[0:L, hh * 128:(hh + 1) * 128], qdT[:, h * L:(h + 1) * L], S16[:, h, :], [tqd, tS16], [tpo], start=(hh == 0), stop=False, skip=True)
            for hh in range(4):
                h = half * 4 + hh
                self.mm(po[0:L, hh * 128:(hh + 1) * 128], At[0:L, h * L:(h + 1) * L], vn[0:L, h * 128:(h + 1) * 128], [tAt, tvn], [tpo], start=False, stop=True, skip=True)
        yield
        for half in range(2):
            po, tpo, ipo = pos[half]
            acc = onacc[0:L, half * 512:(half + 1) * 512]
            if s_ == 0:
                self.ts("dve", acc, po[0:L, :], rm, ALU.mult, [tpo, tC], [tacc])
            else:
                self.stt(acc, po[0:L, :], rm, acc, ALU.mult, ALU.add, [tpo, tC, tacc], [tacc])
            self.brel(ipo)
        yield

    def sample_finish(self, job, TB, onacc_t, QKVZ, ONT):
        kind, c0, L, s0 = job
        tC = self.tC
        h3 = lambda ap: ap.rearrange("p (h l) -> p h l", l=L)
        hd = lambda ap: ap.rearrange("p (h d) -> p h d", d=128)
        W8 = 8 * L
        sqo, tsq = TB["sqo"]; on, ton = TB["on"]; ss, tss = TB["ss"]; rs, trs = TB["rs"]
        onacc, tacc = onacc_t
        self.act(sqo[0:L, :], onacc[0:L, :], AF.Square, [tacc], [tsq])
        self.P.op("dve", lambda e, o=ss[0:L, :], i=hd(sqo[0:L, :]): e.tensor_reduce(out=o, in_=i, axis=AX.X, op=ALU.add), [tsq], [tss])
        self.act(rs[0:L, :], ss[0:L, :], AF.Ln, [tss], [trs], bias=EPS, scale=1.0 / 128.0)
        self.act(rs[0:L, :], rs[0:L, :], AF.Exp, [trs], [trs], scale=-0.5)
        yield
        self.tt("dve", hd(on[0:L, :]), hd(onacc[0:L, :]), rs[0:L, :].unsqueeze(2).to_broadcast([L, 8, 128]), ALU.mult, [tacc, trs], [ton])
        yield
        pt, tpt, ipt = self.bacq()
        ptb = pt[:].bitcast(BF16)
        for h in range(8):
            self.tr(ptb[:, h * L:(h + 1) * L], on[0:L, h * 128:(h + 1) * 128], self.idb[0:L, 0:L], [ton, tC], [tpt])
        self.stt(ONT[:, :, c0:c0 + L], h3(ptb[:, 0:W8]), self.vcol("gnw"), QKVZ[:, 24:32, c0:c0 + L], ALU.mult, ALU.mult, [tpt, tC], [self.tok("ONT", self.phase_id)])
        self.brel(ipt)
        yield

    def ffn(self, st, l):
        NT = st.NT
        self.phase()
        hasS = any(s.kind == "S" for s in st.segs)
        xn, _ = self.A([128, 8, NT], BF16)
        hT, _ = self.A([128, 22, NT], BF16)
        sq2 = [self.A([128, 8, 512], BF16) for _ in range(2)]
        rsb = [self.A([128, 512], F32) for _ in range(2)]
        wsl = [self.A([128, 2, 8, 256], BF16) for _ in range(3)]
        wsl = [(w_, (t_, self.tok("wslb", self.phase_id, i_))) for i_, (w_, t_) in enumerate(wsl)]
        wdn = [self.A([128, 22, 128], BF16) for _ in range(3)]
        ub = [self.A([128, 2 + 512], F32) for _ in range(4)]
        t0b = [self.A([128, 512], F32) for _ in range(4)]
        sab = [self.A([128, 512], F32) for _ in range(2)]
        if hasS:
            SHF, tSHF = self.A([128, NFC, 32], F32)
            stg, tstg = self.A([32, 5632], F32)
        xnt = lambda dc, tile: (xn[:, dc, tile[0].off + tile[1]:tile[0].off + tile[1] + tile[2]], self.tok("xn", self.phase_id, dc, tile[0].off + tile[1]))
        self.norm(st, "nf%d" % l, xnt, sq2, rsb)
        if hasS:
            self.dma("sp", stg, self.st_ffn[l], (), [tstg], "ldst")
            for g in range(0, NFC, 8):
                pb, tpb = self.bank()
                ng = min(8, NFC - g)
                for j in range(ng):
                    fc = g + j
                    self.tr(pb[:, j * 32:(j + 1) * 32], stg[0:32, fc * 128:(fc + 1) * 128], self.idf[0:32, 0:32], [tstg, self.tC], [tpb])
                self.cp("act", SHF[:, g:g + ng, :], pb[:, 0:ng * 32].rearrange("p (a b) -> p a b", b=32), [tpb], [tSHF])
        tHF = self.tok("HF")
        fcw, fcb = "fcw%d" % l, "fcb%d" % l
        def ld_wup(u):
            wt_, twt_ = wsl[u % 3]
            self.dma("pool", wt_[:, 0], self.w_up[l][:, u * 256:(u + 1) * 256].rearrange("(kc p) f -> p kc f", p=128), (), [twt_[0]], "ldw%d" % (u % 3))
            self.dma("pool", wt_[:, 1], self.w_up[l][:, DFF + u * 256:DFF + (u + 1) * 256].rearrange("(kc p) f -> p kc f", p=128), (), [twt_[1]], "ldwb%d" % (u % 3))

        def ld_wdn(dc):
            wd_, twd_ = wdn[dc % 3]
            self.dma("pool", wd_, self.w_down[l][:, dc * 128:(dc + 1) * 128].rearrange("(i p) d -> p i d", p=128), (), [twd_], "ldwd%d" % (dc % 3))
        ld_wup(0); ld_wup(1)
        pend = []
        for u in range(11):
            wt, twt = wsl[u % 3]
            if u + 2 < 11:
                ld_wup(u + 2)
            elif u + 2 == 11:
                ld_wdn(0)
            else:
                ld_wdn(1)
            for j in range(2):
                i = u * 2 + j
                prev = [None, None]
                for tile in st.tiles():
                    seg, t0, n = tile
                    c0 = seg.off + t0
                    conv = []
                    for ab in range(2):
                        fc = i + 22 * ab
                        pb, tpb = self.bank()
                        for kc in range(8):
                            self.mm(pb[:, 0:n], wt[:, ab, kc, j * 128:(j + 1) * 128], xn[:, kc, c0:c0 + n], [twt[ab], xnt(kc, tile)[1]], [tpb],
                                    start=(kc == 0), stop=(kc == 7))
                        ex, tex = ub[self.rr("ub", [0, 1, 2, 3])]
                        if seg.kind == "S":
                            self.cp("pool", self.ext_halo(seg, ex, 2), SHF[:, fc, :].rearrange("p (s r) -> p s r", r=2), [tSHF], [tex])
                        elif t0 == 0:
                            self.cp("pool", ex[:, 0:2], self.HF[:, l, fc, :], [tHF], [tex])
                        else:
                            pe_, tpe_, pn = prev[ab]
                            self.cp("pool", ex[:, 0:2], pe_[:, pn:pn + 2], [tpe_], [tex])
                        self.cp("act", self.ext_dst(seg, ex, 2, n), self.V(seg, pb[:, 0:n]), [tpb], [tex])
                        prev[ab] = (ex, tex, n)
                        if seg.kind == "S":
                            self.cp("pool", SHF[:, fc, :].rearrange("p (s r) -> p s r", r=2), self.ext_tail(seg, ex, 2, n), [tex], [tSHF])
                        elif t0 + n == seg.n:
                            self.cp("pool", self.HF[:, l, fc, :], ex[:, n:n + 2], [tex], [tHF])
                        tb, ttb = t0b[self.rr("t0b", [0, 1, 2, 3])]
                        tv = self.V(seg, tb[:, 0:n])
                        self.act(tv, self.V(seg, pb[:, 0:n]), AF.Identity, [tpb, self.tC], [ttb], bias=self.vcol(fcb, fc), scale=self.vcol(fcw, 2 * NFC + fc))
                        self.stt(tv, self.ext_tap(seg, ex, 2, 1, n), self.vcol(fcw, 1 * NFC + fc), tv, ALU.mult, ALU.add, [tex, ttb, self.tC], [ttb])
                        self.stt(tv, self.ext_tap(seg, ex, 2, 0, n), self.vcol(fcw, 0 * NFC + fc), tv, ALU.mult, ALU.add, [tex, ttb, self.tC], [ttb])
                        conv.append((tb, ttb))
                    def tail(conv=conv, i=i, c0=c0, n=n):
                        sa, tsa = sab[self.rr("sab", [0, 1])]
                        self.act(sa[:, 0:n], conv[0][0][:, 0:n], AF.Silu, [conv[0][1]], [tsa])
                        self.tt("dve", hT[:, i, c0:c0 + n], sa[:, 0:n], conv[1][0][:, 0:n], ALU.mult, [tsa, conv[1][1]], [self.tok("hT", self.phase_id, i, c0)])
                    if pend:
                        pend.pop(0)()
                    pend.append(tail)
        while pend:
            pend.pop(0)()
        if hasS:
            for fc in range(NFC):
                pb, tpb = self.bank()
                self.tr(pb[0:32, 0:128], SHF[:, fc, :], self.idf[:, :], [tSHF, self.tC], [tpb])
                self.cp(self.rr("ev", ["act", "dve"]), stg[0:32, fc * 128:(fc + 1) * 128], pb[0:32, 0:128], [tpb], [tstg])
            self.dma("sp", self.o_sffn[l], stg, [tstg], (), "stst")
        if st.last:
            stg2, tstg2 = self.A([2, 5632], F32)
            for fc in range(NFC):
                pb, tpb = self.bank()
                self.tr(pb[0:2, 0:128], self.HF[:, l, fc, :], self.idf[:, :], [tHF, self.tC], [tpb])
                self.cp(self.rr("ev", ["act", "dve"]), stg2[0:2, fc * 128:(fc + 1) * 128], pb[0:2, 0:128], [tpb], [tstg2])
            self.dma("sp", self.o_pffn[l], stg2, [tstg2], (), "stst")
        for dc in range(8):
            wd, twd = wdn[dc % 3]
            if dc + 2 < 8:
                ld_wdn(dc + 2)
            for tile in st.tiles():
                seg, t0, n = tile
                c0 = seg.off + t0
                pb, tpb = self.bank()
                for i in range(22):
                    self.mm(pb[:, 0:n], wd[:, i, :], hT[:, i, c0:c0 + n], [twd, self.tok("hT", self.phase_id, i, c0)], [tpb], start=(i == 0), stop=(i == 21))
                xv = self.xT[:, dc, c0:c0 + n]
                self.tt("dve", xv, pb[:, 0:n], xv, ALU.add, [tpb], [self.xtok(dc, tile)])

    def pool_mixer(self, st):
        self.phase()
        sq2 = [self.A([128, 8, 512], BF16) for _ in range(2)]
        rsb = [self.A([128, 512], F32) for _ in range(2)]
        pw, tpw = self.A([128, 4, 2, 256], BF16)
        self.dma("pool", pw, self.pool_w.rearrange("g (ci p) e -> p g ci e", p=128), (), [tpw], "ldw0")
        segbuf = {}
        for seg in st.segs:
            W = 16 * 23 if seg.kind == "S" else 15 + seg.n
            hn, _ = self.A([128, 8, W], F32)
            s1, _ = self.A([128, 2, W], F32)
            s2, _ = self.A([128, 2, W], F32)
            PL, _ = self.A([128, 8, seg.n], BF16)
            segbuf[id(seg)] = (hn, s1, s2, PL, W)
        if any(s.kind == "S" for s in st.segs):
            stg, tstg = self.A([120, 2, D], F32)
            self._pcb = [self.A([128, 120], F32) for _ in range(2)]
        tmp15, ttmp15 = self.A([128, 15], F32)
        tHP = self.tok("HP")

        def dstf(dc, tile):
            seg, t0, n = tile
            hn = segbuf[id(seg)][0]
            if seg.kind == "S":
                ap = hn[:, dc, :].rearrange("p (s w) -> p s w", w=23)[:, :, 15:23]
            else:
                ap = hn[:, dc, 15 + t0:15 + t0 + n]
            return ap, self.tok("hn", self.phase_id, id(seg), dc)

        self._norm_pool(st, "nm1", dstf, sq2, rsb)
        for seg in st.segs:
            hn, s1, s2, PL, W = segbuf[id(seg)]
            n = seg.n
            if seg.kind == "S":
                for half in range(2):
                    self.dma("sp", stg[:, half, :], self.st_pool[half * 120:(half + 1) * 120, :], (), [tstg], "ldst")
                for dc in range(8):
                    pb, tpb = self.bank()
                    for half in range(2):
                        self.tr(pb[:, half * 120:(half + 1) * 120], stg[0:120, half, dc * 128:(dc + 1) * 128], self.idf[0:120, 0:120], [tstg, self.tC], [tpb])
                    self.cp(self.rr("ev", ["act", "dve"]), hn[:, dc, :].rearrange("p (s w) -> p s w", w=23)[:, :, 0:15],
                            pb[:, 0:240].rearrange("p (s r) -> p s r", r=15), [tpb], [self.tok("hn", self.phase_id, id(seg), dc)])
            else:
                for dc in range(8):
                    self.cp("pool", hn[:, dc, 0:15], self.HP[:, dc, :], [tHP], [self.tok("hn", self.phase_id, id(seg), dc)])
            if seg.kind == "S":
                e3 = lambda ap: ap.rearrange("p (s w) -> p s w", w=23)
                sl = lambda ap, a, b: e3(ap)[:, :, a:b]
                WW = 23
            else:
                sl = lambda ap, a, b: ap[:, a:b]
                WW = W
            for dc in range(8):
                gi = dc // 2
                th = self.tok("hn", self.phase_id, id(seg), dc)
                ts1 = self.tok("ps1", self.phase_id, id(seg), dc % 2)
                ts2 = self.tok("ps2", self.phase_id, id(seg), dc % 2)
                src, tsrc = hn[:, dc, :], th
                bufs = [(s1[:, dc % 2, :], ts1), (s2[:, dc % 2, :], ts2)]
                for lev in range(gi + 1):
                    sh = 1 << lev
                    lo = (1 << (lev + 1)) - 1
                    dstb, tdb = bufs[lev % 2]
                    self.tt("pool", sl(dstb, lo, WW), sl(src, lo, WW), sl(src, lo - sh, WW - sh), ALU.add, [tsrc], [tdb])
                    src, tsrc = dstb, tdb
                if seg.kind == "S":
                    outv = PL[:, dc, :].rearrange("p (s t) -> p s t", t=8)
                else:
                    outv = PL[:, dc, :]
                tPL = self.tok("PL", self.phase_id, id(seg), dc)
                self.stt(outv, sl(src, 15, WW), 1.0 / WINS[gi], sl(hn[:, dc, :], 15, WW), ALU.mult, ALU.subtract, [tsrc, th], [tPL])
                if seg.kind == "P" and seg.pos0 == 0:
                    ic = self.cst[:, C_INVC + gi * 15:C_INVC + gi * 15 + 15]
                    self.tt("dve", tmp15, src[:, 15:30], ic, ALU.mult, [tsrc, self.tC], [ttmp15])
                    self.tt("dve", PL[:, dc, 0:15], tmp15, hn[:, dc, 15:30], ALU.subtract, [ttmp15, th], [tPL])
            if seg.kind == "S":
                for dc in range(8):
                    th = self.tok("hn", self.phase_id, id(seg), dc)
                    for half in range(2):
                        pb, tpb = self.bank()
                        src3 = hn[:, dc, :].rearrange("p (s w) -> p s w", w=23)[:, half * 8:(half + 1) * 8, 8:23]
                        cbuf, tcb = self._pcb[self.rr("pcb", [0, 1])]
                        self.cp("pool", cbuf.rearrange("p (s r) -> p s r", r=15), src3, [th], [tcb])
                        self.tr(pb[0:120, 0:128], cbuf, self.idf[:, :], [tcb, self.tC], [tpb])
                        self.cp(self.rr("ev", ["act", "dve"]), stg[0:120, half, dc * 128:(dc + 1) * 128], pb[0:120, 0:128], [tpb], [tstg])
                for half in range(2):
                    self.dma("sp", self.o_spool[half * 120:(half + 1) * 120, :], stg[:, half, :], [tstg], (), "stst")
            else:
                for dc in range(8):
                    th = self.tok("hn", self.phase_id, id(seg), dc)
                    self.cp("pool", self.HP[:, dc, :], hn[:, dc, n:n + 15], [th], [tHP])
                if st.last:
                    stg2, tstg2 = self.A([15, D], F32)
                    for dc in range(8):
                        pb, tpb = self.bank()
                        self.tr(pb[0:15, 0:128], self.HP[:, dc, :], self.idf[:, :], [tHP, self.tC], [tpb])
                        self.cp(self.rr("ev", ["act", "dve"]), stg2[0:15, dc * 128:(dc + 1) * 128], pb[0:15, 0:128], [tpb], [tstg2])
                    self.dma("sp", self.o_ppool, stg2, [tstg2], (), "stst")
            for tile in seg.tiles():
                _, t0, nn = tile
                c0 = seg.off + t0
                for gi in range(4):
                    for eo in range(2):
                        dco = 2 * gi + eo
                        pb, tpb = self.bank()
                        for ci in range(2):
                            self.mm(pb[:, 0:nn], pw[:, gi, ci, eo * 128:(eo + 1) * 128], PL[:, 2 * gi + ci, t0:t0 + nn],
                                    [tpw, self.tok("PL", self.phase_id, id(seg), 2 * gi + ci)], [tpb], start=(ci == 0), stop=(ci == 1))
                        xv = self.xT[:, dco, c0:c0 + nn]
                        self.stt(xv, pb[:, 0:nn], self.vcol("psc", dco), xv, ALU.mult, ALU.add, [tpb, self.tC], [self.xtok(dco, tile)])

    def _norm_pool(self, st, wname, dstf, sq2, rsb):
        for tile in st.tiles():
            seg, t0, n = tile
            c0 = seg.off + t0
            sq, tsq = sq2[self.rr("sq2", [0, 1])]
            rs, trs = rsb[self.rr("rsb", [0, 1])]
            pb, tpb = self.bank()
            for dc in range(8):
                xin = self.xT[:, dc, c0:c0 + n]
                if dc % 2 == 0:
                    self.tt("pool", sq[:, dc, 0:n], xin, xin, ALU.mult, [self.xtok(dc, tile)], [tsq])
                else:
                    self.act(sq[:, dc, 0:n], xin, AF.Square, [self.xtok(dc, tile)], [tsq])
            for dc in range(8):
                self.mm(pb[:, 0:n], self.ones_m[:], sq[:, dc, 0:n], [tsq, self.tC], [tpb], start=(dc == 0), stop=(dc == 7))
            self.act(rs[:, 0:n], pb[:, 0:n], AF.Ln, [tpb], [trs], bias=EPS)
            self.act(rs[:, 0:n], rs[:, 0:n], AF.Exp, [trs], [trs], scale=-0.5)
            for dc in range(8):
                dst, tdst = dstf(dc, tile)
                self.stt(dst, self.V(seg, self.xT[:, dc, c0:c0 + n]), self.vcol(wname, dc), self.V(seg, rs[:, 0:n]), ALU.mult, ALU.mult,
                         [self.xtok(dc, tile), trs, self.tC], [tdst])

    def final(self, st):
        self.phase()
        sq2 = [self.A([128, 8, 512], BF16) for _ in range(2)]
        rsb = [self.A([128, 512], F32) for _ in range(2)]
        yT = [self.A([128, 8, 512], F32) for _ in range(2)]
        ysg = [self.A([128, D], F32) for _ in range(3)]
        cur = {}

        def dstf(dc, tile):
            return cur["y"][0][:, dc, 0:tile[2]], cur["y"][1]

        for tile in st.tiles():
            seg, t0, n = tile
            cur["y"] = yT[self.rr("yT", [0, 1])]
            self._norm_one(tile, "nfin", dstf, sq2, rsb)
            y, ty = cur["y"]
            b0 = 0
            while b0 < n:
                if seg.kind == "P":
                    pos = seg.pos0 + t0 + b0
                    if pos < NMETA:
                        b0 += NMETA - pos
                        continue
                m = min(128, n - b0)
                sg, tsg = ysg[self.rr("ysg", [0, 1, 2])]
                for half in range(2):
                    pb, tpb = self.bank()
                    for j in range(4):
                        dc = half * 4 + j
                        self.tr(pb[0:m, j * 128:(j + 1) * 128], y[:, dc, b0:b0 + m], self.idf[:, :], [ty, self.tC], [tpb])
                    self.cp(("act", "dve")[half], sg[0:m, half * 512:(half + 1) * 512], pb[0:m, :], [tpb], [tsg])
                if seg.kind == "S":
                    self.dma("sp", self.ys[b0:b0 + m, :], sg[0:m, :], [tsg], (), "sty%d" % ((self.rrc["ysg"] - 1) % 3))
                else:
                    r0 = seg.pos0 + t0 + b0 - NMETA
                    self.dma("sp", self.yp[r0:r0 + m, :], sg[0:m, :], [tsg], (), "sty%d" % ((self.rrc["ysg"] - 1) % 3))
                b0 += m

    def _norm_one(self, tile, wname, dstf, sq2, rsb):
        seg, t0, n = tile
        c0 = seg.off + t0
        sq, tsq = sq2[self.rr("sq2", [0, 1])]
        rs, trs = rsb[self.rr("rsb", [0, 1])]
        pb, tpb = self.bank()
        for dc in range(8):
            xin = self.xT[:, dc, c0:c0 + n]
            if dc % 2 == 0:
                self.tt("pool", sq[:, dc, 0:n], xin, xin, ALU.mult, [self.xtok(dc, tile)], [tsq])
            else:
                self.act(sq[:, dc, 0:n], xin, AF.Square, [self.xtok(dc, tile)], [tsq])
        for dc in range(8):
            self.mm(pb[:, 0:n], self.ones_m[:], sq[:, dc, 0:n], [tsq, self.tC], [tpb], start=(dc == 0), stop=(dc == 7))
        self.act(rs[:, 0:n], pb[:, 0:n], AF.Ln, [tpb], [trs], bias=EPS)
        self.act(rs[:, 0:n], rs[:, 0:n], AF.Exp, [trs], [trs], scale=-0.5)
        for dc in range(8):
            dst, tdst = dstf(dc, tile)
            self.stt(dst, self.xT[:, dc, c0:c0 + n], self.vcol(wname, dc), rs[:, 0:n], ALU.mult, ALU.mult,
                     [self.xtok(dc, tile), trs, self.tC], [tdst])


_NC_CACHE = {}


def _get_nc():
    if "nc" not in _NC_CACHE:
        b = Builder()
        _NC_CACHE["nc"] = b.build()
    return _NC_CACHE["nc"]


def kernel(**inp):
    inp = {k: np.asarray(v) for k, v in inp.items()}
    f = lambda a: np.ascontiguousarray(a, dtype=np.float32)
    nc = _get_nc()
    vecs = build_vecs(inp)
    consts = build_consts()
    w_in = f(inp["gdn_w_in"][0])
    wba = np.zeros((D, 40), np.float32)
    wba[:, 0:8] = w_in[:, 4104:4112]
    wba[:, 32:40] = w_in[:, 4096:4104]
    shared = {
        "meta": f(inp["meta_tokens"]), "w_in": w_in, "wba": wba, "w_out": f(inp["gdn_w_out"][0]),
        "pool_w": f(inp["pool_w"][0]), "w_up": f(inp["ffn_w_up"]), "w_down": f(inp["ffn_w_down"]),
        "vecs": vecs, "consts": consts,
    }
    in_maps = []
    for c in range(8):
        sl = slice(16 * c, 16 * c + 16)
        m = dict(shared)
        m["xp"] = f(inp["x_prompt"][c])
        m["xs"] = f(inp["x_sample"][sl].reshape(128, D))
        m["st_conv"] = f(inp["state_gdn_conv"][0, sl].reshape(48, 3072))
        m["st_rec"] = f(inp["state_gdn_rec"][0, sl])
        m["st_pool"] = f(inp["state_pool"][0, sl].reshape(240, D))
        m["st_ffn"] = f(inp["state_ffn_conv"][:, sl].reshape(2, 32, 5632))
        in_maps.append(m)
    res = run_bass_kernel_spmd(nc, in_maps, core_ids=list(range(8)))
    R = res.results
    g = lambda k: [np.asarray(r[k], dtype=np.float32) for r in R]
    y_prompt = np.stack(g("yp"), 0)
    y_sample = np.concatenate(g("ys"), 0).reshape(128, 8, D)
    p_conv = np.stack(g("o_pconv"), 0)[None]
    p_rec = np.stack(g("o_prec"), 0)[None]
    p_pool = np.stack(g("o_ppool"), 0)[None]
    p_ffn = np.stack(g("o_pffn"), 1)
    s_conv = np.concatenate([a.reshape(16, 3, 3072) for a in g("o_sconv")], 0)[None]
    s_rec = np.concatenate(g("o_srec"), 0)[None]
    s_pool = np.concatenate([a.reshape(16, 15, D) for a in g("o_spool")], 0)[None]
    s_ffn = np.concatenate([a.reshape(2, 16, 2, 5632) for a in g("o_sffn")], 1)
    return (y_prompt, y_sample, p_conv, p_rec, p_pool, p_ffn, s_conv, s_rec, s_pool, s_ffn)
```

```python
import contextlib
import numpy as np
import concourse.bass as bass
import concourse.mybir as mybir
from concourse.bass_utils import run_bass_kernel_spmd

F32 = mybir.dt.float32
BF16 = mybir.dt.bfloat16
ALU = mybir.AluOpType
AF = mybir.ActivationFunctionType
AX = mybir.AxisListType

D = 1024
NH = 8
DFF = 2816
NFC = 44
SEQ = 2048
NMETA = 16
EPS = 1e-6
NEG = -1.0e30
DEBUG_MAP = None
WINS = (2, 4, 8, 16)


class Tok:
    __slots__ = ("lastw", "readers", "excl")

    def __init__(self):
        self.lastw = None
        self.readers = []
        self.excl = False


class Op:
    __slots__ = ("eng", "fn", "deps", "ms", "dma_sem", "dma_val", "is_dma", "where")

    def __init__(self, eng, fn):
        import sys as _s
        f = _s._getframe(3)
        self.where = (f.f_lineno, f.f_back.f_lineno if f.f_back else 0)
        self.eng = eng
        self.fn = fn
        self.deps = []
        self.ms = None
        self.is_dma = False
        self.dma_sem = None
        self.dma_val = 0


class Prog:
    ENGS = ("pe", "act", "dve", "pool", "sp")

    def __init__(self, nc):
        self.nc = nc
        self.ops = {e: [] for e in self.ENGS}
        self.streams = {}
        self.pending = {}

    def barrier(self):
        lasts = [self.ops[e][-1] for e in self.ENGS if self.ops[e]]
        lasts += [st[0] for st in self.streams.values() if st[0] is not None]
        for e in self.ENGS:
            self.pending[e] = list(lasts)

    def op(self, eng, fn, reads=(), writes=(), stream=None):
        o = Op(eng, fn)
        is_dma = stream is not None
        deps = []
        for t in reads:
            if t.lastw is not None:
                deps.append((t.lastw, True))
            if t.excl:
                for r in t.readers:
                    if r.eng != eng:
                        deps.append((r, True))
        for t in writes:
            if t.lastw is not None:
                deps.append((t.lastw, False))
            for r in t.readers:
                deps.append((r, False))
        for d in self.pending.pop(eng, []):
            deps.append((d, True))
        if is_dma:
            o.is_dma = True
            st = self.streams.setdefault(stream, [None, 0])
            if st[0] is not None:
                deps.append((st[0], True))
            st[1] += 1
            o.dma_sem = stream
            o.dma_val = 16 * st[1]
            st[0] = o
        seen = set()
        for d, raw in deps:
            if d is o or id(d) in seen:
                continue
            if (not d.is_dma) and (not is_dma) and d.eng == eng and eng == "pe":
                continue
            seen.add(id(d))
            o.deps.append(d)
        for t in reads:
            t.readers.append(o)
        for t in writes:
            t.lastw = o
            t.readers = []
        self.ops[eng].append(o)
        return o

    def emit(self):
        nc = self.nc
        for e in self.ENGS:
            for o in self.ops[e]:
                for d in o.deps:
                    if not d.is_dma:
                        d.ms = True
        for e in self.ENGS:
            k = 0
            for o in self.ops[e]:
                if o.ms and not o.is_dma:
                    k += 1
                    o.ms = k
        with contextlib.ExitStack() as es:
            esem = {e: es.enter_context(nc.semaphore("s_" + e)) for e in self.ENGS}
            dsem = {k: es.enter_context(nc.semaphore("d_%d" % i)) for i, k in enumerate(self.streams)}
            block = es.enter_context(nc.Block())
            prog = self

            def run(e, engobj):
                seen = {}
                for o in prog.ops[e]:
                    for d in o.deps:
                        if d.is_dma:
                            key, val, sem = ("d", d.dma_sem), d.dma_val, dsem[d.dma_sem]
                        else:
                            key, val, sem = ("e", d.eng), d.ms, esem[d.eng]
                        if seen.get(key, 0) >= val:
                            continue
                        seen[key] = val
                        engobj.wait_ge(sem, val)
                    ins = o.fn(engobj)
                    if DEBUG_MAP is not None:
                        try:
                            DEBUG_MAP[str(ins.ins.name)] = o.where
                        except Exception as ex:
                            DEBUG_MAP["err"] = repr(ex)
                    if o.is_dma:
                        ins.then_inc(dsem[o.dma_sem], 16)
                    elif o.ms:
                        ins.then_inc(esem[e], 1)
                if e == "sp":
                    for k, st in prog.streams.items():
                        engobj.wait_ge(dsem[k], 16 * st[1])

            block.tensor(lambda eng: run("pe", eng))
            block.scalar(lambda eng: run("act", eng))
            block.vector(lambda eng: run("dve", eng))
            block.gpsimd(lambda eng: run("pool", eng))
            block.sync(lambda eng: run("sp", eng))


VEC_COLS = {}


def _vec_layout():
    off = 0
    for name, n in (("nm0", 8), ("nm1", 8), ("nf0", 8), ("nf1", 8), ("nfin", 8),
                    ("gcw", 96), ("fcw0", 132), ("fcw1", 132), ("fcb0", 44), ("fcb1", 44),
                    ("psc", 8), ("gnw", 1), ("alog", 1), ("dtb", 1)):
        VEC_COLS[name] = off
        off += n
    return off


NV = _vec_layout()
C_ID, C_TRI, C_NEGU, C_POSL, C_INVC = 0, 128, 192, 256, 320
C_TRI8, C_NEGU8, C_POSL8, C_SEL, C_RM = 380, 444, 508, 572, 636
NCONST = 636 + 8


def build_consts():
    c = np.zeros((128, NCONST), np.float32)
    c[:, C_ID:C_ID + 128] = np.eye(128, dtype=np.float32)
    p = np.arange(64)[:, None]
    f = np.arange(64)[None, :]
    c[:64, C_TRI:C_TRI + 64] = (f >= p).astype(np.float32)
    c[:64, C_NEGU:C_NEGU + 64] = np.where(f >= p, 0.0, NEG)
    c[:64, C_POSL:C_POSL + 64] = np.where(f < p, 0.0, -NEG)
    for gi, w in enumerate(WINS):
        for t in range(15):
            c[:, C_INVC + gi * 15 + t] = 1.0 / min(w, t + 1)
    same = (p // 8) == (f // 8)
    c[:64, C_TRI8:C_TRI8 + 64] = (same & (f >= p)).astype(np.float32)
    c[:64, C_NEGU8:C_NEGU8 + 64] = np.where(same & (f >= p), 0.0, NEG)
    c[:64, C_POSL8:C_POSL8 + 64] = np.where(same & (f < p), 0.0, -NEG)
    c[:64, C_SEL:C_SEL + 64] = (p == 8 * (f // 8) + 7).astype(np.float32)
    c[:64, C_RM:C_RM + 8] = ((p // 8) == np.arange(8)[None, :]).astype(np.float32)
    return c


def build_vecs(inp):
    v = np.zeros((128, NV), np.float32)

    def put(name, arr):
        a = np.asarray(arr, np.float32).reshape(-1, 128).T
        v[:, VEC_COLS[name]:VEC_COLS[name] + a.shape[1]] = a

    put("nm0", inp["norm_mix"][0]); put("nm1", inp["norm_mix"][1])
    put("nf0", inp["norm_ffn"][0]); put("nf1", inp["norm_ffn"][1])
    put("nfin", inp["norm_final"])
    put("gcw", inp["gdn_conv_w"][0].reshape(-1))
    put("fcw0", inp["ffn_conv_w"][0].reshape(-1)); put("fcw1", inp["ffn_conv_w"][1].reshape(-1))
    put("fcb0", inp["ffn_conv_b"][0]); put("fcb1", inp["ffn_conv_b"][1])
    put("psc", inp["pool_scale"][0])
    put("gnw", inp["gdn_norm_w"][0])
    v[0:8, VEC_COLS["alog"]] = inp["gdn_A_log"][0]
    v[0:8, VEC_COLS["dtb"]] = inp["gdn_dt_bias"][0]
    return v


class Seg:
    def __init__(self, kind, n, off, pos0=0):
        self.kind, self.n, self.off, self.pos0 = kind, n, off, pos0

    def tiles(self):
        if self.kind == "S":
            return [(self, 0, 128)]
        k = (self.n + 511) // 512
        base = (self.n // k + 7) // 8 * 8
        out, t = [], 0
        while t < self.n:
            m = min(base, self.n - t)
            out.append((self, t, m))
            t += m
        return out


class ST:
    def __init__(self, segs, first, last):
        self.segs, self.first, self.last = segs, first, last
        self.NT = sum(s.n for s in segs)

    def tiles(self):
        return [t for s in self.segs for t in s.tiles()]


SUPER = [
    ST([Seg("P", 592, 0, 0), Seg("S", 128, 592)], True, False),
    ST([Seg("P", 704, 0, 592)], False, False),
    ST([Seg("P", 768, 0, 1296)], False, True),
]
NTMAX = 768


class Builder:
    def __init__(self):
        self.nc = nc = bass.Bass("TRN2", target_bir_lowering=False)
        self.P = Prog(nc)
        self.es = contextlib.ExitStack()
        self.toks = {}
        self.rrc = {}
        self.phase_id = 0

        def din(name, shape):
            return nc.dram_tensor(name, list(shape), F32, kind="ExternalInput").ap()

        def dout(name, shape):
            return nc.dram_tensor(name, list(shape), F32, kind="ExternalOutput").ap()

        self.xp = din("xp", [SEQ, D]); self.xs = din("xs", [128, D])
        self.st_conv = din("st_conv", [48, 3072]); self.st_rec = din("st_rec", [16, 8, 128, 128])
        self.st_pool = din("st_pool", [240, D]); self.st_ffn = din("st_ffn", [2, 32, 5632])
        self.meta = din("meta", [NMETA, D])
        self.w_in = din("w_in", [D, 4112]); self.wba = din("wba", [D, 40])
        self.w_out = din("w_out", [D, D]); self.pool_w = din("pool_w", [4, 256, 256])
        self.w_up = din("w_up", [2, D, 5632]); self.w_down = din("w_down", [2, DFF, D])
        self.vecs_d = din("vecs", [128, NV]); self.consts_d = din("consts", [128, NCONST])
        self.yp = dout("yp", [SEQ, D]); self.ys = dout("ys", [128, D])
        self.o_pconv = dout("o_pconv", [3, 3072]); self.o_prec = dout("o_prec", [8, 128, 128])
        self.o_ppool = dout("o_ppool", [15, D]); self.o_pffn = dout("o_pffn", [2, 2, 5632])
        self.o_sconv = dout("o_sconv", [48, 3072]); self.o_srec = dout("o_srec", [16, 8, 128, 128])
        self.o_spool = dout("o_spool", [240, D]); self.o_sffn = dout("o_sffn", [2, 32, 5632])

    def tok(self, *key):
        t = self.toks.get(key)
        if t is None:
            t = self.toks[key] = Tok()
            if key[0] == "bank":
                t.excl = True
        return t

    def sb(self, name, shape, dt):
        return self.es.enter_context(self.nc.sbuf_tensor(name, list(shape), dt))

    def rr(self, name, choices):
        i = self.rrc.get(name, 0)
        self.rrc[name] = i + 1
        return choices[i % len(choices)]

    def bank(self):
        i = self.rrc.get("bank", 0)
        self.rrc["bank"] = i + 1
        i %= 8
        return self.banks[i], self.tok("bank", i)

    def phase(self):
        self.P.barrier()
        self.aoff = 0
        self.phase_id += 1

    def A(self, shape, dt, key=None):
        n = int(np.prod(shape[1:]))
        nb = n * (4 if dt == F32 else 2)
        nb = (nb + 31) // 32 * 32
        ne = nb // 2
        assert self.aoff + ne <= self.arena_n, ("arena overflow", self.aoff, ne, self.arena_n)
        ap = self.arena[0:shape[0], self.aoff:self.aoff + ne]
        self.aoff += ne
        if dt == F32:
            ap = ap.bitcast(F32)
        ap = ap[:, 0:n]
        if len(shape) == 3:
            ap = ap.rearrange("p (a b) -> p a b", b=shape[2])
        elif len(shape) == 4:
            ap = ap.rearrange("p (a b c) -> p a b c", b=shape[2], c=shape[3])
        return ap, self.tok("arena", self.phase_id, self.aoff)

    def mm(self, out, lhsT, rhs, r, w, start=True, stop=True, skip=False):
        if skip:
            self.P.op("pe", lambda e: e.matmul(out, lhsT=lhsT, rhs=rhs, start=start, stop=stop, skip_group_check=True), r, w)
        else:
            self.P.op("pe", lambda e: e.matmul(out, lhsT=lhsT, rhs=rhs, start=start, stop=stop), r, w)

    def tr(self, out, in_, ident, r, w):
        self.P.op("pe", lambda e: e.transpose(out=out, in_=in_, identity=ident), r, w)

    def act(self, out, in_, func, r, w, bias=None, scale=None):
        kw = {}
        if bias is not None:
            kw["bias"] = bias
        if scale is not None:
            kw["scale"] = scale
        self.P.op("act", lambda e: e.activation(out=out, in_=in_, func=func, **kw), r, w)

    def tt(self, eng, out, in0, in1, op, r, w):
        self.P.op(eng, lambda e: e.tensor_tensor(out=out, in0=in0, in1=in1, op=op), r, w)

    def ts(self, eng, out, in0, s1, op0, r, w, s2=None, op1=None):
        if op1 is None:
            self.P.op(eng, lambda e: e.tensor_scalar(out=out, in0=in0, scalar1=s1, scalar2=None, op0=op0), r, w)
        else:
            self.P.op(eng, lambda e: e.tensor_scalar(out=out, in0=in0, scalar1=s1, scalar2=s2, op0=op0, op1=op1), r, w)

    def stt(self, out, in0, scalar, in1, op0, op1, r, w):
        self.P.op("dve", lambda e: e.scalar_tensor_tensor(out=out, in0=in0, scalar=scalar, in1=in1, op0=op0, op1=op1), r, w)

    def cp(self, eng, out, in_, r, w):
        if eng == "act":
            self.act(out, in_, AF.Copy, r, w)
        else:
            self.P.op(eng, lambda e: e.tensor_copy(out=out, in_=in_), r, w)

    def dma(self, q, out, in_, r, w, stream):
        self.P.op(q, lambda e: e.dma_start(out=out, in_=in_), r, w, stream=stream)

    def memset(self, eng, ap, val, w):
        self.P.op(eng, lambda e: e.memset(ap, val), (), w)

    def vcol(self, name, j=0, np_=128):
        c = VEC_COLS[name] + j
        return self.vecs[0:np_, c:c + 1]

    @staticmethod
    def V(seg, ap):
        if seg.kind == "S":
            return ap.rearrange("p (s t) -> p s t", t=8)
        return ap

    @staticmethod
    def ext_dst(seg, buf, H, n):
        if seg.kind == "S":
            return buf[:, 0:16 * (H + 8)].rearrange("p (s w) -> p s w", w=H + 8)[:, :, H:H + 8]
        return buf[:, H:H + n]

    @staticmethod
    def ext_tap(seg, buf, H, j, n):
        if seg.kind == "S":
            return buf[:, 0:16 * (H + 8)].rearrange("p (s w) -> p s w", w=H + 8)[:, :, j:j + 8]
        return buf[:, j:j + n]

    @staticmethod
    def ext_halo(seg, buf, H):
        if seg.kind == "S":
            return buf[:, 0:16 * (H + 8)].rearrange("p (s w) -> p s w", w=H + 8)[:, :, 0:H]
        return buf[:, 0:H]

    @staticmethod
    def ext_tail(seg, buf, H, n):
        if seg.kind == "S":
            return buf[:, 0:16 * (H + 8)].rearrange("p (s w) -> p s w", w=H + 8)[:, :, 8:8 + H]
        return buf[:, n:n + H]

    def build(self):
        nc = self.nc
        with self.es:
            self.xT = self.sb("xT", [128, 8, NTMAX], F32)
            self.S32 = self.sb("S32", [128, 8, 128], F32)
            self.S16 = self.sb("S16", [128, 8, 128], BF16)
            self.HG = self.sb("HG", [128, 24, 3], F32)
            self.HF = self.sb("HF", [128, 2, NFC, 2], F32)
            self.HP = self.sb("HP", [128, 8, 15], F32)
            self.vecs = self.sb("vecs_sb", [128, NV], F32)
            self.cst = self.sb("cst", [128, NCONST], F32)
            self.idb = self.sb("idb", [128, 128], BF16)
            self.ones_m = self.sb("ones_m", [128, 128], BF16)
            self.ones_1 = self.sb("ones_1", [128, 128], BF16)
            self.ones_f = self.sb("ones_f", [64, 128], F32)
            self.nexpA = self.sb("nexpA", [8, 1], F32)
            self.lnq = self.sb("lnq", [128, 1], F32)
            self.banks = [self.es.enter_context(nc.psum_tensor("pb%d" % i, [128, 512], F32)) for i in range(8)]
            rem = nc.sbuf_bytes_remaining - 2048
            self.arena_n = (rem // 2) // 64 * 64
            self.arena = self.sb("arena", [128, self.arena_n], BF16)
            self.aoff = 0
            self.idf = self.cst[:, C_ID:C_ID + 128]
            tC = self.tok("consts")
            self.dma("sp", self.vecs[:], self.vecs_d, (), [tC], "ldc0")
            self.dma("sp", self.cst[:], self.consts_d, (), [tC], "ldc1")
            self.cp("dve", self.idb[:], self.idf, [tC], [tC])
            self.memset("pool", self.ones_m[:], 1.0 / 1024.0, [tC])
            self.memset("pool", self.ones_1[:], 1.0, [tC])
            self.memset("pool", self.ones_f[:], 1.0, [tC])
            self.memset("pool", self.lnq[:], -0.5 * float(np.log(128.0)), [tC])
            self.memset("pool", self.S32[:], 0.0, [self.tok("S32")])
            self.memset("pool", self.S16[:], 0.0, [self.tok("S16")])
            self.memset("pool", self.HG[:], 0.0, [self.tok("HG")])
            self.memset("pool", self.HF[:], 0.0, [self.tok("HF")])
            self.memset("pool", self.HP[:], 0.0, [self.tok("HP")])
            self.act(self.nexpA[:], self.vcol("alog", 0, 8), AF.Exp, [tC], [tC])
            self.ts("dve", self.nexpA[:], self.nexpA[:], -1.0, ALU.mult, [tC], [tC])
            self.tC = tC
            for st in SUPER:
                self.run_super(st)
            self.P.emit()
        return nc

    def run_super(self, st):
        self.load_x(st)
        self.gdn(st)
        self.ffn(st, 0)
        self.pool_mixer(st)
        self.ffn(st, 1)
        self.final(st)

    def xtok(self, dc, tile):
        return self.tok("xT", dc, tile[0].off + tile[1])

    def load_x(self, st):
        if st.first:
            self.phase()
        stg = [self.A([128, 4, D], F32) for _ in range(2)]
        bi = 0
        for seg in st.segs:
            for (_, t0, n) in seg.tiles():
                sg, tsg = stg[bi % 2]
                bi += 1
                nb = (n + 127) // 128
                for b in range(nb):
                    m = min(128, n - b * 128)
                    if seg.kind == "S":
                        self.dma("sp", sg[0:m, b, :], self.xs[0:m, :], (), [tsg], "ldx")
                    else:
                        p0 = seg.pos0 + t0 + b * 128
                        r = 0
                        if p0 < NMETA:
                            k = min(m, NMETA - p0)
                            self.dma("sp", sg[0:k, b, :], self.meta[p0:p0 + k, :], (), [tsg], "ldx")
                            r = k
                        if r < m:
                            a = p0 + r - NMETA
                            self.dma("sp", sg[r:m, b, :], self.xp[a:a + (m - r), :], (), [tsg], "ldx")
                tile = (seg, t0, n)
                c0 = seg.off + t0
                for dc in range(8):
                    pb, tpb = self.bank()
                    for b in range(nb):
                        m = min(128, n - b * 128)
                        self.tr(pb[:, b * 128:b * 128 + m], sg[0:m, b, dc * 128:(dc + 1) * 128], self.idf[0:m, 0:m],
                                [tsg, self.tC], [tpb])
                    self.cp(self.rr("ev", ["act", "dve"]), self.xT[:, dc, c0:c0 + n], pb[:, 0:n], [tpb], [self.xtok(dc, tile)])

    def norm(self, st, wname, dstf, sq2, rsb):
        for tile in st.tiles():
            seg, t0, n = tile
            c0 = seg.off + t0
            sq, tsq = sq2[self.rr("sq2", [0, 1])]
            rs, trs = rsb[self.rr("rsb", [0, 1])]
            pb, tpb = self.bank()
            for dc in range(8):
                xin = self.xT[:, dc, c0:c0 + n]
                if dc % 2 == 0:
                    self.tt("pool", sq[:, dc, 0:n], xin, xin, ALU.mult, [self.xtok(dc, tile)], [tsq])
                else:
                    self.act(sq[:, dc, 0:n], xin, AF.Square, [self.xtok(dc, tile)], [tsq])
            for dc in range(8):
                self.mm(pb[:, 0:n], self.ones_m[:], sq[:, dc, 0:n], [tsq, self.tC], [tpb], start=(dc == 0), stop=(dc == 7))
            self.act(rs[:, 0:n], pb[:, 0:n], AF.Ln, [tpb], [trs], bias=EPS)
            self.act(rs[:, 0:n], rs[:, 0:n], AF.Exp, [trs], [trs], scale=-0.5)
            for dc in range(8):
                dst, tdst = dstf(dc, tile)
                self.stt(dst, self.xT[:, dc, c0:c0 + n], self.vcol(wname, dc), rs[:, 0:n], ALU.mult, ALU.mult,
                         [self.xtok(dc, tile), trs, self.tC], [tdst])

    def gdn(self, st):
        NT = st.NT
        self.phase()
        hasS = any(s.kind == "S" for s in st.segs)
        xn, _ = self.A([128, 8, NT], BF16)
        QKVZ, _ = self.A([128, 32, NT], BF16)
        GB, tGB = self.A([40, NT], F32)
        mark = self.aoff
        sq2 = [self.A([128, 8, 512], BF16) for _ in range(2)]
        rsb = [self.A([128, 512], F32) for _ in range(2)]
        wsl = [self.A([128, 8, 512], BF16) for _ in range(3)]
        wbat, twba = self.A([128, 8, 40], BF16)
        ext = [self.A([128, 3 + 512], F32) for _ in range(3)]
        acc = [self.A([128, 512], F32) for _ in range(2)]
        sil = [self.A([128, 512], F32) for _ in range(2)]
        sqh = [self.A([128, 512], BF16) for _ in range(3)]
        rin = [self.A([128, 512], F32) for _ in range(2)]
        bat = [self.A([8, 512], F32) for _ in range(4)]
        if hasS:
            SHG, tSHG = self.A([128, 24, 48], F32)
            stg, tstg = self.A([48, 3072], F32)
        self.memset("pool", GB, 0.0, [tGB])
        xnt = lambda dc, tile: (xn[:, dc, tile[0].off + tile[1]:tile[0].off + tile[1] + tile[2]], self.tok("xn", self.phase_id, dc, tile[0].off + tile[1]))
        self.norm(st, "nm0", xnt, sq2, rsb)
        if hasS:
            self.dma("sp", stg, self.st_conv, (), [tstg], "ldst")
            for g in range(3):
                pb, tpb = self.bank()
                for j in range(8):
                    fc = g * 8 + j
                    self.tr(pb[:, j * 48:(j + 1) * 48], stg[0:48, fc * 128:(fc + 1) * 128], self.idf[0:48, 0:48], [tstg, self.tC], [tpb])
                self.cp("act", SHG[:, g * 8:(g + 1) * 8, :], pb[:, 0:384].rearrange("p (a b) -> p a b", b=48), [tpb], [tSHG])
        self.dma("pool", wbat, self.wba.rearrange("(kc p) f -> p kc f", p=128), (), [twba], "ldwba")
        for tile in st.tiles():
            seg, t0, n = tile
            c0 = seg.off + t0
            pb, tpb = self.bank()
            for kc in range(8):
                self.mm(pb[0:40, 0:n], wbat[:, kc, :], xn[:, kc, c0:c0 + n], [twba, xnt(kc, tile)[1]], [tpb], start=(kc == 0), stop=(kc == 7))
            self.act(GB[32:40, c0:c0 + n], pb[32:40, 0:n], AF.Sigmoid, [tpb], [tGB])
            (b1, t1), (b2, t2), (b3, t3), (b4, t4) = bat
            self.ts("dve", b1[:, 0:n], pb[0:8, 0:n], self.vcol("dtb", 0, 8), ALU.add, [tpb, self.tC], [t1])
            self.stt(b2[:, 0:n], b1[:, 0:n], -1.0, b1[:, 0:n], ALU.mult, ALU.max, [t1], [t2])
            self.act(b3[:, 0:n], b2[:, 0:n], AF.Exp, [t2], [t3], scale=-1.0)
            self.act(b4[:, 0:n], b3[:, 0:n], AF.Ln, [t3], [t4], bias=1.0)
            self.stt(b2[:, 0:n], b1[:, 0:n], 0.0, b4[:, 0:n], ALU.max, ALU.add, [t1, t4], [t2])
            self.ts("dve", GB[0:8, c0:c0 + n], b2[:, 0:n], self.nexpA[:, 0:1], ALU.mult, [t2, self.tC], [tGB])
        def ld_win(u):
            wt_, twt_ = wsl[u % 3]
            self.dma("pool", wt_, self.w_in[:, u * 512:(u + 1) * 512].rearrange("(kc p) f -> p kc f", p=128), (), [twt_], "ldw%d" % (u % 3))
        ld_win(0); ld_win(1)
        pend = []
        qk_list = []
        for u in range(8):
            wt, twt = wsl[u % 3]
            if u + 2 < 8:
                ld_win(u + 2)
            for j in range(4):
                fc = u * 4 + j
                kind = fc // 8
                prev_ext = None
                for tile in st.tiles():
                    seg, t0, n = tile
                    c0 = seg.off + t0
                    pb, tpb = self.bank()
                    for kc in range(8):
                        self.mm(pb[:, 0:n], wt[:, kc, j * 128:(j + 1) * 128], xn[:, kc, c0:c0 + n], [twt, xnt(kc, tile)[1]], [tpb],
                                start=(kc == 0), stop=(kc == 7))
                    dst = QKVZ[:, fc, c0:c0 + n]
                    tdst = self.tok("qkvz", self.phase_id, fc, c0)
                    if kind == 3:
                        self.act(self.V(seg, dst), self.V(seg, pb[:, 0:n]), AF.Silu, [tpb], [tdst])
                        continue
                    ex, tex = ext[self.rr("ext", [0, 1, 2])]
                    if seg.kind == "S":
                        self.cp("pool", self.ext_halo(seg, ex, 3), SHG[:, fc, :].rearrange("p (s r) -> p s r", r=3), [tSHG], [tex])
                    elif t0 == 0:
                        self.cp("pool", ex[:, 0:3], self.HG[:, fc, :], [self.tok("HG")], [tex])
                    else:
                        pe_, tpe_, pn = prev_ext
                        self.cp("pool", ex[:, 0:3], pe_[:, pn:pn + 3], [tpe_], [tex])
                    self.cp("act", self.ext_dst(seg, ex, 3, n), self.V(seg, pb[:, 0:n]), [tpb], [tex])
                    prev_ext = (ex, tex, n)
                    if seg.kind == "S":
                        self.cp("pool", SHG[:, fc, :].rearrange("p (s r) -> p s r", r=3), self.ext_tail(seg, ex, 3, n), [tex], [tSHG])
                    elif t0 + n == seg.n:
                        self.cp("pool", self.HG[:, fc, :], ex[:, n:n + 3], [tex], [self.tok("HG")])
                    ac, tac = acc[self.rr("acc", [0, 1])]
                    av = self.V(seg, ac[:, 0:n])
                    self.act(av, self.ext_tap(seg, ex, 3, 0, n), AF.Identity, [tex, self.tC], [tac], scale=self.vcol("gcw", 0 * 24 + fc))
                    for tap in (1, 2, 3):
                        self.stt(av, self.ext_tap(seg, ex, 3, tap, n), self.vcol("gcw", tap * 24 + fc), av, ALU.mult, ALU.add, [tex, tac, self.tC], [tac])

                    def tail(dst=dst, tdst=tdst, ac=ac, tac=tac, n=n):
                        self.act(dst, ac[:, 0:n], AF.Silu, [tac], [tdst])
                    if pend:
                        pend.pop(0)()
                    pend.append(tail)
                    if kind < 2:
                        qk_list.append((kind, dst, tdst, n))
            while pend:
                pend.pop(0)()
            def nstage1(item):
                kind, dst, tdst, n = item
                sh, tsh = sqh[self.rr("sqh", [0, 1, 2])]
                self.tt("dve", sh[:, 0:n], dst, dst, ALU.mult, [tdst], [tsh])
                pb2, tpb2 = self.bank()
                self.mm(pb2[:, 0:n], self.ones_1[:], sh[:, 0:n], [tsh, self.tC], [tpb2])
                return pb2, tpb2

            def nstage2(item, pb2, tpb2):
                kind, dst, tdst, n = item
                ri, tri_ = rin[self.rr("rin", [0, 1])]
                self.act(ri[:, 0:n], pb2[:, 0:n], AF.Ln, [tpb2], [tri_], bias=EPS)
                self.act(ri[:, 0:n], ri[:, 0:n], AF.Exp, [tri_], [tri_], scale=-0.5, bias=(self.lnq[:, 0:1] if kind == 0 else None))
                self.tt("dve", dst, dst, ri[:, 0:n], ALU.mult, [tdst, tri_], [tdst])
            inflight = []
            for item in qk_list:
                inflight.append((item,) + nstage1(item))
                if len(inflight) > 2:
                    nstage2(*inflight.pop(0))
            while inflight:
                nstage2(*inflight.pop(0))
            qk_list = []
        if hasS:
            for fc in range(24):
                pb, tpb = self.bank()
                self.tr(pb[0:48, 0:128], SHG[:, fc, :], self.idf[:, :], [tSHG, self.tC], [tpb])
                self.cp(self.rr("ev", ["act", "dve"]), stg[0:48, fc * 128:(fc + 1) * 128], pb[0:48, 0:128], [tpb], [tstg])
            self.dma("sp", self.o_sconv, stg, [tstg], (), "stst")
        if st.last:
            stg2, tstg2 = self.A([3, 3072], F32)
            for fc in range(24):
                pb, tpb = self.bank()
                self.tr(pb[0:3, 0:128], self.HG[:, fc, :], self.idf[:, :], [self.tok("HG"), self.tC], [tpb])
                self.cp(self.rr("ev", ["act", "dve"]), stg2[0:3, fc * 128:(fc + 1) * 128], pb[0:3, 0:128], [tpb], [tstg2])
            self.dma("sp", self.o_pconv, stg2, [tstg2], (), "stst")

        self.P.barrier()
        self.aoff = mark
        ONT = xn
        self.bfree = list(range(8))
        NA, NC_, NB = 3, 4, 1
        self.want_onacc = False
        TAs = [self.alloc_chunk_bufs("A") for _ in range(NA)]
        CAs = [self.alloc_chunk_bufs("C") for _ in range(NC_)]
        TBs = [self.alloc_chunk_bufs("B") for _ in range(NB)]
        jobs = []
        for seg in st.segs:
            if seg.kind == "P":
                c = 0
                if seg.pos0 == 0:
                    jobs.append(("P", seg.off, NMETA, None))
                    c = NMETA
                while c < seg.n:
                    jobs.append(("P", seg.off + c, 64, None))
                    c += 64
            else:
                for b_ in range(2):
                    jobs.append(("SB", seg.off + 64 * b_, 64, 8 * b_))
        N = len(jobs)
        pj = [ji for ji in range(N) if jobs[ji][0] == "P"]
        nP = len(pj)
        gbt_all, tpre = self.A([64, nP * 40], F32)
        Gt_all, _ = self.A([64, nP * 8], F32)
        eG_all, _ = self.A([64, nP * 8], F32)
        nb_all, _ = self.A([64, nP * 8], F32)
        nbG_all, _ = self.A([64, nP * 8], F32)
        self.memset("pool", gbt_all, 0.0, [tpre])
        pgb, tpgb, ipgb = self.bacq()
        for k, ji in enumerate(pj):
            _, c0_, L_, _ = jobs[ji]
            self.tr(pgb[0:L_, k * 40:(k + 1) * 40], GB[0:40, c0_:c0_ + L_], self.idf[0:40, 0:40], [tGB, self.tC], [tpgb])
        k = 0
        while k < nP:
            k2 = k
            while k2 < nP and jobs[pj[k2]][2] == jobs[pj[k]][2]:
                k2 += 1
            L_ = jobs[pj[k]][2]
            self.cp("dve", gbt_all[0:L_, k * 40:k2 * 40], pgb[0:L_, k * 40:k2 * 40], [tpgb], [tpre])
            k = k2
        self.brel(ipgb)
        g3 = gbt_all.rearrange("p (k c) -> p k c", c=40)
        pgc, tpgc, ipgc = self.bacq()
        self.mm(pgc[0:64, 0:nP * 8].rearrange("p (k c) -> p k c", c=8), self.cst[0:64, C_TRI:C_TRI + 64], g3[:, :, 0:8], [tpre, self.tC], [tpgc])
        self.cp("dve", Gt_all, pgc[0:64, 0:nP * 8], [tpgc], [tpre])
        self.brel(ipgc)
        self.act(eG_all, Gt_all, AF.Exp, [tpre], [tpre])
        self.ts("dve", nb_all.rearrange("p (k c) -> p k c", c=8), g3[:, :, 32:40], -1.0, ALU.mult, [tpre], [tpre])
        self.tt("dve", nbG_all, eG_all, nb_all, ALU.mult, [tpre], [tpre])
        pres = {}
        for k, ji in enumerate(pj):
            L_ = jobs[ji][2]
            pres[ji] = (gbt_all[0:L_, k * 40:k * 40 + 8], gbt_all[0:L_, k * 40 + 32:k * 40 + 40], Gt_all[0:L_, k * 8:(k + 1) * 8],
                        nb_all[0:L_, k * 8:(k + 1) * 8], nbG_all[0:L_, k * 8:(k + 1) * 8], tpre)
        nextA = 0
        nextB = 0
        doneA = set()
        actA = {}
        actB = None
        while nextB < N:
            for slot in range(NA):
                if slot not in actA and nextA < N and nextA < nextB + NC_:
                    actA[slot] = (nextA, self.chunk_A(jobs[nextA], QKVZ, GB, tGB, TAs[slot], CAs[nextA % NC_], pres.get(nextA)))
                    nextA += 1
            if actB is None and nextB in doneA:
                if jobs[nextB][0] == "SB":
                    nextB += 1
                    continue
                actB = self.chunk_B(jobs[nextB], CAs[nextB % NC_], TBs[nextB % NB], QKVZ, ONT)
            if actB is not None:
                try:
                    next(actB)
                except StopIteration:
                    actB = None
                    nextB += 1
            for slot in list(actA):
                j, g = actA[slot]
                try:
                    next(g)
                except StopIteration:
                    doneA.add(j)
                    del actA[slot]
        if hasS:
            self.P.barrier()
            onaccs = [self.A([64, 1024], F32) for _ in range(2)]
            save_off = self.aoff
            self.aoff = mark
            NSQ = 3
            assert all((ji % NC_) >= 2 for ji in range(N) if jobs[ji][0] == "SB")
            SBs = []
            for _ in range(NSQ):
                d = {}
                d["S32"] = self.A([128, 8, 128], F32); d["S16"] = self.A([128, 8, 128], BF16)
                d["Stmp"] = self.A([128, 1024], F32); d["vn"] = self.A([64, 1024], BF16)
                SBs.append(d)
            sb_jobs = [(ji, jobs[ji]) for ji in range(N) if jobs[ji][0] == "SB"]
            todo = [(bi, ji, job, s_) for bi, (ji, job) in enumerate(sb_jobs) for s_ in range(8)]
            remaining = {bi: 8 for bi in range(len(sb_jobs))}
            act = {}
            fin = []
            while todo or act or fin:
                for slot in range(NSQ):
                    if slot not in act and todo:
                        bi, ji, job, s_ = todo.pop(0)
                        act[slot] = (bi, self.sample_seq(job, s_, CAs[ji % NC_], SBs[slot], onaccs[bi]))
                for slot in list(act):
                    bi, g = act[slot]
                    try:
                        next(g)
                    except StopIteration:
                        del act[slot]
                        remaining[bi] -= 1
                        if remaining[bi] == 0:
                            ji, job = sb_jobs[bi]
                            fin.append(self.sample_finish(job, TBs[0], onaccs[bi], QKVZ, ONT))
                for g in list(fin):
                    try:
                        next(g)
                    except StopIteration:
                        fin.remove(g)
            self.aoff = max(save_off, self.aoff)
        if st.last:
            self.dma("sp", self.o_prec.rearrange("h k v -> k h v"), self.S32[:], [self.tok("S32")], (), "strec")

        self.P.barrier()
        self.aoff = mark
        wo, two = self.A([128, 8, D], BF16)
        self.dma("pool", wo[:, :, 0:512], self.w_out[:, 0:512].rearrange("(kc p) f -> p kc f", p=128), (), [two], "ldw0")
        self.dma("pool", wo[:, :, 512:1024], self.w_out[:, 512:1024].rearrange("(kc p) f -> p kc f", p=128), (), [two], "ldw1")
        for tile in st.tiles():
            seg, t0, n = tile
            c0 = seg.off + t0
            for dc in range(8):
                pb, tpb = self.bank()
                for kc in range(8):
                    self.mm(pb[:, 0:n], wo[:, kc, dc * 128:(dc + 1) * 128], ONT[:, kc, c0:c0 + n], [two], [tpb], start=(kc == 0), stop=(kc == 7))
                xv = self.xT[:, dc, c0:c0 + n]
                self.tt("dve", xv, pb[:, 0:n], xv, ALU.add, [tpb], [self.xtok(dc, tile)])

    def alloc_chunk_bufs(self, which):
        b = {}
        def a(name, shape, dt):
            b[name] = self.A(shape, dt)
        if which == "A":
            a("gbt", [64, 40], F32)
            for nm in ("Gt", "eG", "nbG", "nb", "dGl", "eGl"):
                a(nm, [64, 8], F32)
            a("Dm", [64, 512], F32); a("Du", [64, 512], F32); a("Dl", [64, 512], F32); a("eGbc", [128, 512], F32)
            b["rhsG"] = b["Dl"]
            a("Lneg", [64, 512], BF16); a("M0", [64, 512], BF16)
            a("QTa", [64, 512], BF16); a("QTb", [64, 512], BF16)
            a("kbgn", [64, 1024], BF16)
        elif which == "C":
            a("PQa", [64, 1024], BF16); a("PQb", [64, 1024], BF16); a("At", [64, 512], BF16)
            a("kd", [64, 1024], BF16); a("vb", [64, 1024], BF16)
            a("nWT", [128, 512], BF16); a("qdT", [128, 512], BF16); a("gtc", [128, 64], F32)
        else:
            a("vn", [64, 1024], BF16); a("sqo", [64, 1024], BF16); a("on", [64, 1024], BF16)
            a("Stmp", [128, 1024], F32); a("ss", [64, 8], F32); a("rs", [64, 8], F32)
            if self.want_onacc:
                a("onacc", [64, 1024], F32)
        return b

    def bacq(self):
        if not self.bfree:
            raise RuntimeError("out of PSUM banks")
        i = self.bfree.pop(0)
        return self.banks[i], self.tok("bank", i), i

    def brel(self, i):
        self.bfree.append(i)

    def chunk_A(self, job, QKVZ, GB, tGB, TA, CA, pre=None):
        kind, c0, L, sidx = job
        B = dict(TA); B.update(CA)
        tC = self.tC
        Q = lambda h: QKVZ[:, h, c0:c0 + L]
        K = lambda h: QKVZ[:, 8 + h, c0:c0 + L]
        Vv = lambda h: QKVZ[:, 16 + h, c0:c0 + L]
        h3 = lambda ap: ap.rearrange("p (h l) -> p h l", l=L)
        hd = lambda ap: ap.rearrange("p (h d) -> p h d", d=128)
        W8 = 8 * L
        blk = (kind == "SB")
        cT, cN, cP = (C_TRI8, C_NEGU8, C_POSL8) if blk else (C_TRI, C_NEGU, C_POSL)
        tri = self.cst[0:L, cT:cT + L]
        dGl, tdGl = B["dGl"]; eGl, teGl = B["eGl"]; gtc, tgtc = B["gtc"]
        rhsG, trG = B["rhsG"]
        if pre is not None:
            g_tm, beta, GtL, nbL, nbGL, tpre = pre
            tgbt = tGt = tnb = tnbG = tpre
            self.tt("pool", h3(rhsG[0:L, 0:W8]), tri.unsqueeze(1).to_broadcast([L, 8, L]), g_tm.unsqueeze(2).to_broadcast([L, 8, L]),
                    ALU.mult, [tgbt, tC], [trG])
            yield
            pG, tpG, ipG = self.bacq()
            self.mm(pG[:, 0:W8], self.ones_f[0:L, :], rhsG[0:L, 0:W8], [trG, tC], [tpG])
            yield
        else:
            pg, tpg, ipg = self.bacq()
            self.tr(pg[0:L, 0:40], GB[0:40, c0:c0 + L], self.idf[0:40, 0:40], [tGB, tC], [tpg])
            gbt, tgbt = B["gbt"]
            self.cp("dve", gbt[0:L, :], pg[0:L, 0:40], [tpg], [tgbt])
            self.brel(ipg)
            g_tm = gbt[0:L, 0:8]
            beta = gbt[0:L, 32:40]
            yield
            self.tt("pool", h3(rhsG[0:L, 0:W8]), tri.unsqueeze(1).to_broadcast([L, 8, L]), g_tm.unsqueeze(2).to_broadcast([L, 8, L]),
                    ALU.mult, [tgbt, tC], [trG])
            pg2, tpg2, ipg2 = self.bacq()
            self.mm(pg2[0:L, 0:8], tri, g_tm, [tgbt, tC], [tpg2])
            Gt, tGt = B["Gt"]; eG, teG = B["eG"]; nbG, tnbG = B["nbG"]; nb, tnb = B["nb"]
            self.cp("dve", Gt[0:L, :], pg2[0:L, 0:8], [tpg2], [tGt])
            self.brel(ipg2)
            self.ts("dve", nb[0:L, :], beta, -1.0, ALU.mult, [tgbt], [tnb])
            yield
            pG, tpG, ipG = self.bacq()
            self.mm(pG[:, 0:W8], self.ones_f[0:L, :], rhsG[0:L, 0:W8], [trG, tC], [tpG])
            self.act(eG[0:L, :], Gt[0:L, :], AF.Exp, [tGt], [teG])
            self.tt("dve", nbG[0:L, :], eG[0:L, :], nb[0:L, :], ALU.mult, [teG, tnb], [tnbG])
            GtL, nbL, nbGL = Gt[0:L, :], nb[0:L, :], nbG[0:L, :]
            yield
        if blk:
            Glast = None
            gl4 = pG[:, 0:W8].rearrange("p (h s t) -> p h s t", s=8, t=8)[:, :, :, 7]
        else:
            Glast = h3(pG[:, 0:W8])[:, :, L - 1]
        Dm, tDm = B["Dm"]; Du, tDu = B["Du"]; Dl, tDl = B["Dl"]; eGbc, teGbc = B["eGbc"]
        self.tt("dve", h3(Dm[0:L, 0:W8]), h3(pG[0:L, 0:W8]), GtL.unsqueeze(2).to_broadcast([L, 8, L]), ALU.subtract,
                [tpG, tGt], [tDm])
        if blk:
            pgl, tpgl, ipgl = self.bacq()
            self.mm(pgl[0:L, 0:8], self.cst[0:L, C_SEL:C_SEL + L], GtL, [tGt, tC], [tpgl])
            self.tt("dve", dGl[0:L, :], pgl[0:L, 0:8], GtL, ALU.subtract, [tpgl, tGt], [tdGl])
            self.brel(ipgl)
            self.act(gtc[:, 0:64].rearrange("p (h s) -> p h s", s=8), gl4, AF.Exp, [tpG], [tgtc])
        else:
            self.tt("dve", dGl[0:L, :], Glast[0:L], GtL, ALU.subtract, [tpG, tGt], [tdGl])
            self.act(gtc[:, 0:8], Glast, AF.Exp, [tpG], [tgtc])
        self.act(eGbc[:, 0:W8], pG[:, 0:W8], AF.Exp, [tpG], [teGbc])
        self.brel(ipG)
        pk, tpk, ipk = self.bacq()
        pkb = pk[:].bitcast(BF16)
        for h in range(8):
            self.tr(pkb[0:L, h * 128:(h + 1) * 128], K(h), self.idb[:], [tC], [tpk])
        pv, tpv, ipv = self.bacq()
        pvb = pv[:].bitcast(BF16)
        for h in range(8):
            self.tr(pvb[0:L, h * 128:(h + 1) * 128], Vv(h), self.idb[:], [tC], [tpv])
        yield
        self.act(eGl[0:L, :], dGl[0:L, :], AF.Exp, [tdGl], [teGl])
        negu = self.cst[0:L, cN:cN + L].unsqueeze(1).to_broadcast([L, 8, L])
        posl = self.cst[0:L, cP:cP + L].unsqueeze(1).to_broadcast([L, 8, L])
        self.tt("pool", h3(Du[0:L, 0:W8]), h3(Dm[0:L, 0:W8]), negu, ALU.add, [tDm, tC], [tDu])
        self.tt("pool", h3(Dl[0:L, 0:W8]), h3(Dm[0:L, 0:W8]), posl, ALU.add, [tDm, tC], [tDl])
        kbgn, tkb = B["kbgn"]; kd, tkd = B["kd"]; vb, tvb = B["vb"]
        self.tt("dve", hd(kbgn[0:L, :]), hd(pkb[0:L, :]), nbGL.unsqueeze(2).to_broadcast([L, 8, 128]), ALU.mult, [tpk, tnbG], [tkb])
        self.tt("dve", hd(vb[0:L, :]), hd(pvb[0:L, :]), beta.unsqueeze(2).to_broadcast([L, 8, 128]), ALU.mult, [tpv, tgbt], [tvb])
        self.brel(ipv)
        yield
        self.tt("dve", hd(kd[0:L, :]), hd(pkb[0:L, :]), eGl[0:L, :].unsqueeze(2).to_broadcast([L, 8, 128]), ALU.mult, [tpk, teGl], [tkd])
        self.brel(ipk)
        self.act(Du[0:L, 0:W8], Du[0:L, 0:W8], AF.Exp, [tDu], [tDu])
        self.act(Dl[0:L, 0:W8], Dl[0:L, 0:W8], AF.Exp, [tDl], [tDl], scale=-1.0)
        pkk, tpkk, ipkk = self.bacq()
        for h in range(8):
            self.mm(pkk[0:L, h * L:(h + 1) * L], K(h), K(h), [], [tpkk])
        pkq, tpkq, ipkq = self.bacq()
        for h in range(8):
            self.mm(pkq[0:L, h * L:(h + 1) * L], K(h), Q(h), [], [tpkq])
        qdT, tqd = B["qdT"]
        self.tt("pool", h3(qdT[:, 0:W8]), QKVZ[:, 0:8, c0:c0 + L], h3(eGbc[:, 0:W8]), ALU.mult, [teGbc], [tqd])
        yield
        self.tt("pool", h3(Dl[0:L, 0:W8]), h3(Dl[0:L, 0:W8]), nbL.unsqueeze(2).to_broadcast([L, 8, L]), ALU.mult,
                [tDl, tnb], [tDl])
        Lneg, tLn = B["Lneg"]; At, tAt = B["At"]; M0, tM0 = B["M0"]
        self.tt("dve", At[0:L, 0:W8], pkq[0:L, 0:W8], Du[0:L, 0:W8], ALU.mult, [tpkq, tDu], [tAt])
        self.brel(ipkq)
        yield
        self.tt("dve", Lneg[0:L, 0:W8], pkk[0:L, 0:W8], Dl[0:L, 0:W8], ALU.mult, [tpkk, tDl], [tLn])
        self.brel(ipkk)
        yield
        pm, tpm, ipm = self.bacq()
        pmb = pm[:].bitcast(BF16)
        for h in range(8):
            self.tr(pmb[0:L, h * L:(h + 1) * L], Lneg[0:L, h * L:(h + 1) * L], self.idb[0:L, 0:L], [tLn, tC], [tpm])
        self.cp("act", M0[0:L, 0:W8], pmb[0:L, 0:W8], [tpm], [tM0])
        self.brel(ipm)
        yield
        nlev = 3 if blk else {64: 6, 16: 4, 8: 3}[L]
        idbL = self.idb[0:L, 0:L]
        PQ = [B["PQa"], B["PQb"]]
        QTbufs = [B["QTa"], B["QTb"]]
        pq3 = lambda ap: ap[0:L, :].rearrange("p (h c) -> p h c", c=128)
        cur = 0
        Pc, tPc = PQ[cur]
        self.tt("pool", pq3(Pc)[:, :, 0:L], h3(M0[0:L, 0:W8]), idbL.unsqueeze(1).to_broadcast([L, 8, L]), ALU.add, [tM0, tC], [tPc])
        pq, tpq, ipq = self.bacq()
        for h in range(8):
            sl = slice(h * L, (h + 1) * L)
            self.mm(pq[0:L, sl], M0[0:L, sl], Lneg[0:L, sl], [tM0, tLn], [tpq])
        QTc = QTbufs[0]
        self.cp("act", QTc[0][0:L, 0:W8], pq[0:L, 0:W8], [tpq], [QTc[1]])
        self.brel(ipq)
        pq2, tpq2, ipq2 = self.bacq()
        for h in range(8):
            sl = slice(h * L, (h + 1) * L)
            self.mm(pq2[0:L, sl], Lneg[0:L, sl], M0[0:L, sl], [tM0, tLn], [tpq2])
        self.cp("dve", pq3(Pc)[:, :, L:2 * L], h3(pq2[0:L, 0:W8]), [tpq2], [tPc])
        self.brel(ipq2)
        yield
        for k in range(1, nlev):
            last = (k == nlev - 1)
            Pn, tPn = PQ[1 - cur]
            wid = L if last else 2 * L
            if not last:
                QTn = QTbufs[k % 2]
                pq, tpq, ipq = self.bacq()
                for h in range(8):
                    self.mm(pq[0:L, h * L:(h + 1) * L], pq3(Pc)[:, h, L:2 * L], QTc[0][0:L, h * L:(h + 1) * L], [tPc, QTc[1]], [tpq])
                self.cp("act", QTn[0][0:L, 0:W8], pq[0:L, 0:W8], [tpq], [QTn[1]])
                self.brel(ipq)
            for half in range(2):
                pp, tpp, ipp = self.bacq()
                for hh in range(4):
                    h = half * 4 + hh
                    self.mm(pp[0:L, hh * 128:hh * 128 + wid], QTc[0][0:L, h * L:(h + 1) * L], pq3(Pc)[:, h, 0:wid], [tPc, QTc[1]], [tpp])
                ppv = pp[0:L, :].rearrange("p (h c) -> p h c", c=128)
                hs = slice(half * 4, half * 4 + 4)
                self.tt("dve", pq3(Pn)[:, hs, 0:L], ppv[:, :, 0:L], pq3(Pc)[:, hs, 0:L], ALU.add, [tpp, tPc], [tPn])
                if not last:
                    self.cp("act", pq3(Pn)[:, hs, L:2 * L], ppv[:, :, L:2 * L], [tpp], [tPn])
                self.brel(ipp)
            cur = 1 - cur
            Pc, tPc = PQ[cur]
            if not last:
                QTc = QTn
            yield
        Ttv = pq3(Pc)
        tTt = tPc
        pw, tpw, ipw = self.bacq()
        for h in range(8):
            self.mm(pw[:, h * L:(h + 1) * L], kbgn[0:L, h * 128:(h + 1) * 128], Ttv[:, h, 0:L], [tkb, tTt], [tpw])
        nWT, tnW = B["nWT"]
        self.cp("act", nWT[:, 0:W8], pw[:, 0:W8], [tpw], [tnW])
        self.brel(ipw)
        CA["Tt"] = (Ttv, tTt)
        yield

    def chunk_B(self, job, CA, TB, QKVZ, ONT):
        kind, c0, L, sidx = job
        B = dict(TB); B.update(CA)
        tC = self.tC
        Tt, tTt = CA["Tt"]
        h3 = lambda ap: ap.rearrange("p (h l) -> p h l", l=L)
        hd = lambda ap: ap.rearrange("p (h d) -> p h d", d=128)
        W8 = 8 * L
        if kind == "S":
            S32, tS32 = self.SS32[sidx % 2]
            S16, tS16 = self.SS16[sidx % 2]
            self.dma("sp", S32, self.st_rec[sidx].rearrange("h k v -> k h v"), (), [tS32], "ldrec%d" % (sidx % 2))
            self.cp("pool", S16, S32, [tS32], [tS16])
        else:
            S32, tS32 = self.S32[:], self.tok("S32")
            S16, tS16 = self.S16[:], self.tok("S16")
        vb, tvb = B["vb"]; nWT, tnW = B["nWT"]; vn, tvn = B["vn"]; qdT, tqd = B["qdT"]; At, tAt = B["At"]
        kd, tkd = B["kd"]; sqo, tsq = B["sqo"]; on, ton = B["on"]; ss, tss = B["ss"]; rs, trs = B["rs"]
        gtc, tgtc = B["gtc"]; Stmp, tSt = B["Stmp"]
        for half in range(2):
            pv, tpv, ipv = self.bacq()
            for hh in range(4):
                h = half * 4 + hh
                self.mm(pv[0:L, hh * 128:(hh + 1) * 128], Tt[:, h, 0:L], vb[0:L, h * 128:(h + 1) * 128], [tTt, tvb], [tpv], start=(hh == 0), stop=False, skip=True)
            for hh in range(4):
                h = half * 4 + hh
                self.mm(pv[0:L, hh * 128:(hh + 1) * 128], nWT[:, h * L:(h + 1) * L], S16[:, h, :], [tnW, tS16], [tpv], start=False, stop=True, skip=True)
            self.cp(("act", "dve")[half], vn[0:L, half * 512:(half + 1) * 512], pv[0:L, :], [tpv], [tvn])
            self.brel(ipv)
        self.tt("pool", hd(Stmp[:, :]), S32, gtc[:, 0:8].unsqueeze(2).to_broadcast([128, 8, 128]), ALU.mult, [tS32, tgtc], [tSt])
        yield
        pss = []
        for half in range(2):
            pS, tpS, ipS = self.bacq()
            pss.append((pS, tpS, ipS))
            for hh in range(4):
                h = half * 4 + hh
                self.mm(pS[:, hh * 128:(hh + 1) * 128], kd[0:L, h * 128:(h + 1) * 128], vn[0:L, h * 128:(h + 1) * 128], [tkd, tvn], [tpS])
        for half in range(2):
            pS, tpS, ipS = pss[half]
            self.tt("dve", S32[:, half * 4:(half + 1) * 4, :], hd(pS[:, :]), hd(Stmp[:, half * 512:(half + 1) * 512]), ALU.add, [tpS, tSt], [tS32])
            self.brel(ipS)
        pos = []
        for half in range(2):
            po, tpo, ipo = self.bacq()
            pos.append((po, tpo, ipo))
            for hh in range(4):
                h = half * 4 + hh
                self.mm(po[0:L, hh * 128:(hh + 1) * 128], qdT[:, h * L:(h + 1) * L], S16[:, h, :], [tqd, tS16], [tpo], start=(hh == 0), stop=False, skip=True)
            for hh in range(4):
                h = half * 4 + hh
                self.mm(po[0:L, hh * 128:(hh + 1) * 128], At[0:L, h * L:(h + 1) * L], vn[0:L, h * 128:(h + 1) * 128], [tAt, tvn], [tpo], start=False, stop=True, skip=True)
        self.cp("act", S16, S32, [tS32], [tS16])
        if kind == "S":
            self.dma("sp", self.o_srec[sidx].rearrange("h k v -> k h v"), S32, [tS32], (), "strec%d" % (sidx % 2))
        yield
        for half in range(2):
            po, tpo, ipo = pos[half]
            self.act(sqo[0:L, half * 512:(half + 1) * 512], po[0:L, :], AF.Square, [tpo], [tsq])
        self.P.op("dve", lambda e, o=ss[0:L, :], i=hd(sqo[0:L, :]): e.tensor_reduce(out=o, in_=i, axis=AX.X, op=ALU.add), [tsq], [tss])
        self.act(rs[0:L, :], ss[0:L, :], AF.Ln, [tss], [trs], bias=EPS, scale=1.0 / 128.0)
        self.act(rs[0:L, :], rs[0:L, :], AF.Exp, [trs], [trs], scale=-0.5)
        yield
        for half in range(2):
            po, tpo, ipo = pos[half]
            self.tt("dve", hd(on[0:L, half * 512:(half + 1) * 512]), hd(po[0:L, :]), rs[0:L, half * 4:(half + 1) * 4].unsqueeze(2).to_broadcast([L, 4, 128]),
                    ALU.mult, [tpo, trs], [ton])
            self.brel(ipo)
        yield
        pt, tpt, ipt = self.bacq()
        ptb = pt[:].bitcast(BF16)
        for h in range(8):
            self.tr(ptb[:, h * L:(h + 1) * L], on[0:L, h * 128:(h + 1) * 128], self.idb[0:L, 0:L], [ton, tC], [tpt])
        self.stt(ONT[:, :, c0:c0 + L], h3(ptb[:, 0:W8]), self.vcol("gnw"), QKVZ[:, 24:32, c0:c0 + L], ALU.mult, ALU.mult, [tpt, tC], [self.tok("ONT", self.phase_id)])
        self.brel(ipt)
        yield

    def sample_seq(self, job, s_, CA, SB, onacc_t):
        kind, c0, L, s0 = job
        tC = self.tC
        Tt, tTt = CA["Tt"]
        hd = lambda ap: ap.rearrange("p (h d) -> p h d", d=128)
        vb, tvb = CA["vb"]; nWT, tnW = CA["nWT"]; qdT, tqd = CA["qdT"]; At, tAt = CA["At"]
        kd, tkd = CA["kd"]; gtc, tgtc = CA["gtc"]
        S32, tS32 = SB["S32"]; S16, tS16 = SB["S16"]; Stmp, tSt = SB["Stmp"]; vn, tvn = SB["vn"]
        onacc, tacc = onacc_t
        gtc3 = gtc[:, 0:64].rearrange("p (h s) -> p h s", s=8)
        sidx = s0 + s_
        rm = self.cst[0:L, C_RM + s_:C_RM + s_ + 1]
        self.dma("sp", S32, self.st_rec[sidx].rearrange("h k v -> k h v"), (), [tS32], "ldrec%d" % (sidx % 3))
        yield
        self.cp("pool", S16, S32, [tS32], [tS16])
        self.tt("pool", hd(Stmp[:, :]), S32, gtc3[:, :, s_].unsqueeze(2).to_broadcast([128, 8, 128]), ALU.mult, [tS32, tgtc], [tSt])
        yield
        for half in range(2):
            pv, tpv, ipv = self.bacq()
            for hh in range(4):
                h = half * 4 + hh
                self.mm(pv[0:L, hh * 128:(hh + 1) * 128], Tt[:, h, 0:L], vb[0:L, h * 128:(h + 1) * 128], [tTt, tvb], [tpv], start=(hh == 0), stop=False, skip=True)
            for hh in range(4):
                h = half * 4 + hh
                self.mm(pv[0:L, hh * 128:(hh + 1) * 128], nWT[:, h * L:(h + 1) * L], S16[:, h, :], [tnW, tS16], [tpv], start=False, stop=True, skip=True)
            if half == 0:
                self.act(vn[0:L, 0:512], pv[0:L, :], AF.Identity, [tpv, tC], [tvn], scale=rm)
            else:
                self.ts("dve", vn[0:L, 512:1024], pv[0:L, :], rm, ALU.mult, [tpv, tC], [tvn])
            self.brel(ipv)
        yield
        pss = []
        for half in range(2):
            pS, tpS, ipS = self.bacq()
            pss.append((pS, tpS, ipS))
            for hh in range(4):
                h = half * 4 + hh
                self.mm(pS[:, hh * 128:(hh + 1) * 128], kd[0:L, h * 128:(h + 1) * 128], vn[0:L, h * 128:(h + 1) * 128], [tkd, tvn], [tpS])
        for half in range(2):
            pS, tpS, ipS = pss[half]
            self.tt("dve", S32[:, half * 4:(half + 1) * 4, :], hd(pS[:, :]), hd(Stmp[:, half * 512:(half + 1) * 512]), ALU.add, [tpS, tSt], [tS32])
            self.brel(ipS)
        self.dma("sp", self.o_srec[sidx].rearrange("h k v -> k h v"), S32, [tS32], (), "strec%d" % (sidx % 3))
        pos = []
        for half in range(2):
            po, tpo, ipo = self.bacq()
            pos.append((po, tpo, ipo))
            for hh in range(4):
                h = half * 4 + hh
                self.mm(po[0:L, hh * 128:(hh + 1) * 128], qdT[:, h * L:(h + 1) * L], S16[:, h, :], [tqd, tS16], [tpo], start=(hh == 0), stop=False, skip=True)
            for hh in range(4):
                h = half * 4 + hh
                self.mm(po[0:L, hh * 128:(hh + 1) * 128], At[0:L, h * L:(h + 1) * L], vn[0:L, h * 128:(h + 1) * 128], [tAt, tvn], [tpo], start=False, stop=True, skip=True)
        yield
        for half in range(2):
            po, tpo, ipo = pos[half]
            acc = onacc[0:L, half * 512:(half + 1) * 512]
            if s_ == 0:
                self.ts("dve", acc, po[0:L, :], rm, ALU.mult, [tpo, tC], [tacc])
            else:
                self.stt(acc, po[0:L, :], rm, acc, ALU.mult, ALU.add, [tpo, tC, tacc], [tacc])
            self.brel(ipo)
        yield

    def sample_finish(self, job, TB, onacc_t, QKVZ, ONT):
        kind, c0, L, s0 = job
        tC = self.tC
        h3 = lambda ap: ap.rearrange("p (h l) -> p h l", l=L)
        hd = lambda ap: ap.rearrange("p (h d) -> p h d", d=128)
        W8 = 8 * L
        sqo, tsq = TB["sqo"]; on, ton = TB["on"]; ss, tss = TB["ss"]; rs, trs = TB["rs"]
        onacc, tacc = onacc_t
        self.act(sqo[0:L, :], onacc[0:L, :], AF.Square, [tacc], [tsq])
        self.P.op("dve", lambda e, o=ss[0:L, :], i=hd(sqo[0:L, :]): e.tensor_reduce(out=o, in_=i, axis=AX.X, op=ALU.add), [tsq], [tss])
        self.act(rs[0:L, :], ss[0:L, :], AF.Ln, [tss], [trs], bias=EPS, scale=1.0 / 128.0)
        self.act(rs[0:L, :], rs[0:L, :], AF.Exp, [trs], [trs], scale=-0.5)
        yield
        self.tt("dve", hd(on[0:L, :]), hd(onacc[0:L, :]), rs[0:L, :].unsqueeze(2).to_broadcast([L, 8, 128]), ALU.mult, [tacc, trs], [ton])
        yield
        pt, tpt, ipt = self.bacq()
        ptb = pt[:].bitcast(BF16)
        for h in range(8):
            self.tr(ptb[:, h * L:(h + 1) * L], on[0:L, h * 128:(h + 1) * 128], self.idb[0:L, 0:L], [ton, tC], [tpt])
        self.stt(ONT[:, :, c0:c0 + L], h3(ptb[:, 0:W8]), self.vcol("gnw"), QKVZ[:, 24:32, c0:c0 + L], ALU.mult, ALU.mult, [tpt, tC], [self.tok("ONT", self.phase_id)])
        self.brel(ipt)
        yield

    def ffn(self, st, l):
        NT = st.NT
        self.phase()
        hasS = any(s.kind == "S" for s in st.segs)
        xn, _ = self.A([128, 8, NT], BF16)
        hT, _ = self.A([128, 22, NT], BF16)
        sq2 = [self.A([128, 8, 512], BF16) for _ in range(2)]
        rsb = [self.A([128, 512], F32) for _ in range(2)]
        wsl = [self.A([128, 2, 8, 256], BF16) for _ in range(3)]
        wsl = [(w_, (t_, self.tok("wslb", self.phase_id, i_))) for i_, (w_, t_) in enumerate(wsl)]
        wdn = [self.A([128, 22, 128], BF16) for _ in range(3)]
        ub = [self.A([128, 2 + 512], F32) for _ in range(4)]
        t0b = [self.A([128, 512], F32) for _ in range(4)]
        sab = [self.A([128, 512], F32) for _ in range(2)]
        if hasS:
            SHF, tSHF = self.A([128, NFC, 32], F32)
            stg, tstg = self.A([32, 5632], F32)
        xnt = lambda dc, tile: (xn[:, dc, tile[0].off + tile[1]:tile[0].off + tile[1] + tile[2]], self.tok("xn", self.phase_id, dc, tile[0].off + tile[1]))
        self.norm(st, "nf%d" % l, xnt, sq2, rsb)
        if hasS:
            self.dma("sp", stg, self.st_ffn[l], (), [tstg], "ldst")
            for g in range(0, NFC, 8):
                pb, tpb = self.bank()
                ng = min(8, NFC - g)
                for j in range(ng):
                    fc = g + j
                    self.tr(pb[:, j * 32:(j + 1) * 32], stg[0:32, fc * 128:(fc + 1) * 128], self.idf[0:32, 0:32], [tstg, self.tC], [tpb])
                self.cp("act", SHF[:, g:g + ng, :], pb[:, 0:ng * 32].rearrange("p (a b) -> p a b", b=32), [tpb], [tSHF])
        tHF = self.tok("HF")
        fcw, fcb = "fcw%d" % l, "fcb%d" % l
        def ld_wup(u):
            wt_, twt_ = wsl[u % 3]
            self.dma("pool", wt_[:, 0], self.w_up[l][:, u * 256:(u + 1) * 256].rearrange("(kc p) f -> p kc f", p=128), (), [twt_[0]], "ldw%d" % (u % 3))
            self.dma("pool", wt_[:, 1], self.w_up[l][:, DFF + u * 256:DFF + (u + 1) * 256].rearrange("(kc p) f -> p kc f", p=128), (), [twt_[1]], "ldwb%d" % (u % 3))

        def ld_wdn(dc):
            wd_, twd_ = wdn[dc % 3]
            self.dma("pool", wd_, self.w_down[l][:, dc * 128:(dc + 1) * 128].rearrange("(i p) d -> p i d", p=128), (), [twd_], "ldwd%d" % (dc % 3))
        ld_wup(0); ld_wup(1)
        pend = []
        for u in range(11):
            wt, twt = wsl[u % 3]
            if u + 2 < 11:
                ld_wup(u + 2)
            elif u + 2 == 11:
                ld_wdn(0)
            else:
                ld_wdn(1)
            for j in range(2):
                i = u * 2 + j
                prev = [None, None]
                for tile in st.tiles():
                    seg, t0, n = tile
                    c0 = seg.off + t0
                    conv = []
                    for ab in range(2):
                        fc = i + 22 * ab
                        pb, tpb = self.bank()
                        for kc in range(8):
                            self.mm(pb[:, 0:n], wt[:, ab, kc, j * 128:(j + 1) * 128], xn[:, kc, c0:c0 + n], [twt[ab], xnt(kc, tile)[1]], [tpb],
                                    start=(kc == 0), stop=(kc == 7))
                        ex, tex = ub[self.rr("ub", [0, 1, 2, 3])]
                        if seg.kind == "S":
                            self.cp("pool", self.ext_halo(seg, ex, 2), SHF[:, fc, :].rearrange("p (s r) -> p s r", r=2), [tSHF], [tex])
                        elif t0 == 0:
                            self.cp("pool", ex[:, 0:2], self.HF[:, l, fc, :], [tHF], [tex])
                        else:
                            pe_, tpe_, pn = prev[ab]
                            self.cp("pool", ex[:, 0:2], pe_[:, pn:pn + 2], [tpe_], [tex])
                        self.cp("act", self.ext_dst(seg, ex, 2, n), self.V(seg, pb[:, 0:n]), [tpb], [tex])
                        prev[ab] = (ex, tex, n)
                        if seg.kind == "S":
                            self.cp("pool", SHF[:, fc, :].rearrange("p (s r) -> p s r", r=2), self.ext_tail(seg, ex, 2, n), [tex], [tSHF])
                        elif t0 + n == seg.n:
                            self.cp("pool", self.HF[:, l, fc, :], ex[:, n:n + 2], [tex], [tHF])
                        tb, ttb = t0b[self.rr("t0b", [0, 1, 2, 3])]
                        tv = self.V(seg, tb[:, 0:n])
                        self.act(tv, self.V(seg, pb[:, 0:n]), AF.Identity, [tpb, self.tC], [ttb], bias=self.vcol(fcb, fc), scale=self.vcol(fcw, 2 * NFC + fc))
                        self.stt(tv, self.ext_tap(seg, ex, 2, 1, n), self.vcol(fcw, 1 * NFC + fc), tv, ALU.mult, ALU.add, [tex, ttb, self.tC], [ttb])
                        self.stt(tv, self.ext_tap(seg, ex, 2, 0, n), self.vcol(fcw, 0 * NFC + fc), tv, ALU.mult, ALU.add, [tex, ttb, self.tC], [ttb])
                        conv.append((tb, ttb))
                    def tail(conv=conv, i=i, c0=c0, n=n):
                        sa, tsa = sab[self.rr("sab", [0, 1])]
                        self.act(sa[:, 0:n], conv[0][0][:, 0:n], AF.Silu, [conv[0][1]], [tsa])
                        self.tt("dve", hT[:, i, c0:c0 + n], sa[:, 0:n], conv[1][0][:, 0:n], ALU.mult, [tsa, conv[1][1]], [self.tok("hT", self.phase_id, i, c0)])
                    if pend:
                        pend.pop(0)()
                    pend.append(tail)
        while pend:
            pend.pop(0)()
        if hasS:
            for fc in range(NFC):
                pb, tpb = self.bank()
                self.tr(pb[0:32, 0:128], SHF[:, fc, :], self.idf[:, :], [tSHF, self.tC], [tpb])
                self.cp(self.rr("ev", ["act", "dve"]), stg[0:32, fc * 128:(fc + 1) * 128], pb[0:32, 0:128], [tpb], [tstg])
            self.dma("sp", self.o_sffn[l], stg, [tstg], (), "stst")
        if st.last:
            stg2, tstg2 = self.A([2, 5632], F32)
            for fc in range(NFC):
                pb, tpb = self.bank()
                self.tr(pb[0:2, 0:128], self.HF[:, l, fc, :], self.idf[:, :], [tHF, self.tC], [tpb])
                self.cp(self.rr("ev", ["act", "dve"]), stg2[0:2, fc * 128:(fc + 1) * 128], pb[0:2, 0:128], [tpb], [tstg2])
            self.dma("sp", self.o_pffn[l], stg2, [tstg2], (), "stst")
        for dc in range(8):
            wd, twd = wdn[dc % 3]
            if dc + 2 < 8:
                ld_wdn(dc + 2)
            for tile in st.tiles():
                seg, t0, n = tile
                c0 = seg.off + t0
                pb, tpb = self.bank()
                for i in range(22):
                    self.mm(pb[:, 0:n], wd[:, i, :], hT[:, i, c0:c0 + n], [twd, self.tok("hT", self.phase_id, i, c0)], [tpb], start=(i == 0), stop=(i == 21))
                xv = self.xT[:, dc, c0:c0 + n]
                self.tt("dve", xv, pb[:, 0:n], xv, ALU.add, [tpb], [self.xtok(dc, tile)])

    def pool_mixer(self, st):
        self.phase()
        sq2 = [self.A([128, 8, 512], BF16) for _ in range(2)]
        rsb = [self.A([128, 512], F32) for _ in range(2)]
        pw, tpw = self.A([128, 4, 2, 256], BF16)
        self.dma("pool", pw, self.pool_w.rearrange("g (ci p) e -> p g ci e", p=128), (), [tpw], "ldw0")
        segbuf = {}
        for seg in st.segs:
            W = 16 * 23 if seg.kind == "S" else 15 + seg.n
            hn, _ = self.A([128, 8, W], F32)
            s1, _ = self.A([128, 2, W], F32)
            s2, _ = self.A([128, 2, W], F32)
            PL, _ = self.A([128, 8, seg.n], BF16)
            segbuf[id(seg)] = (hn, s1, s2, PL, W)
        if any(s.kind == "S" for s in st.segs):
            stg, tstg = self.A([120, 2, D], F32)
            self._pcb = [self.A([128, 120], F32) for _ in range(2)]
        tmp15, ttmp15 = self.A([128, 15], F32)
        tHP = self.tok("HP")

        def dstf(dc, tile):
            seg, t0, n = tile
            hn = segbuf[id(seg)][0]
            if seg.kind == "S":
                ap = hn[:, dc, :].rearrange("p (s w) -> p s w", w=23)[:, :, 15:23]
            else:
                ap = hn[:, dc, 15 + t0:15 + t0 + n]
            return ap, self.tok("hn", self.phase_id, id(seg), dc)

        self._norm_pool(st, "nm1", dstf, sq2, rsb)
        for seg in st.segs:
            hn, s1, s2, PL, W = segbuf[id(seg)]
            n = seg.n
            if seg.kind == "S":
                for half in range(2):
                    self.dma("sp", stg[:, half, :], self.st_pool[half * 120:(half + 1) * 120, :], (), [tstg], "ldst")
                for dc in range(8):
                    pb, tpb = self.bank()
                    for half in range(2):
                        self.tr(pb[:, half * 120:(half + 1) * 120], stg[0:120, half, dc * 128:(dc + 1) * 128], self.idf[0:120, 0:120], [tstg, self.tC], [tpb])
                    self.cp(self.rr("ev", ["act", "dve"]), hn[:, dc, :].rearrange("p (s w) -> p s w", w=23)[:, :, 0:15],
                            pb[:, 0:240].rearrange("p (s r) -> p s r", r=15), [tpb], [self.tok("hn", self.phase_id, id(seg), dc)])
            else:
                for dc in range(8):
                    self.cp("pool", hn[:, dc, 0:15], self.HP[:, dc, :], [tHP], [self.tok("hn", self.phase_id, id(seg), dc)])
            if seg.kind == "S":
                e3 = lambda ap: ap.rearrange("p (s w) -> p s w", w=23)
                sl = lambda ap, a, b: e3(ap)[:, :, a:b]
                WW = 23
            else:
                sl = lambda ap, a, b: ap[:, a:b]
                WW = W
            for dc in range(8):
                gi = dc // 2
                th = self.tok("hn", self.phase_id, id(seg), dc)
                ts1 = self.tok("ps1", self.phase_id, id(seg), dc % 2)
                ts2 = self.tok("ps2", self.phase_id, id(seg), dc % 2)
                src, tsrc = hn[:, dc, :], th
                bufs = [(s1[:, dc % 2, :], ts1), (s2[:, dc % 2, :], ts2)]
                for lev in range(gi + 1):
                    sh = 1 << lev
                    lo = (1 << (lev + 1)) - 1
                    dstb, tdb = bufs[lev % 2]
                    self.tt("pool", sl(dstb, lo, WW), sl(src, lo, WW), sl(src, lo - sh, WW - sh), ALU.add, [tsrc], [tdb])
                    src, tsrc = dstb, tdb
                if seg.kind == "S":
                    outv = PL[:, dc, :].rearrange("p (s t) -> p s t", t=8)
                else:
                    outv = PL[:, dc, :]
                tPL = self.tok("PL", self.phase_id, id(seg), dc)
                self.stt(outv, sl(src, 15, WW), 1.0 / WINS[gi], sl(hn[:, dc, :], 15, WW), ALU.mult, ALU.subtract, [tsrc, th], [tPL])
                if seg.kind == "P" and seg.pos0 == 0:
                    ic = self.cst[:, C_INVC + gi * 15:C_INVC + gi * 15 + 15]
                    self.tt("dve", tmp15, src[:, 15:30], ic, ALU.mult, [tsrc, self.tC], [ttmp15])
                    self.tt("dve", PL[:, dc, 0:15], tmp15, hn[:, dc, 15:30], ALU.subtract, [ttmp15, th], [tPL])
            if seg.kind == "S":
                for dc in range(8):
                    th = self.tok("hn", self.phase_id, id(seg), dc)
                    for half in range(2):
                        pb, tpb = self.bank()
                        src3 = hn[:, dc, :].rearrange("p (s w) -> p s w", w=23)[:, half * 8:(half + 1) * 8, 8:23]
                        cbuf, tcb = self._pcb[self.rr("pcb", [0, 1])]
                        self.cp("pool", cbuf.rearrange("p (s r) -> p s r", r=15), src3, [th], [tcb])
                        self.tr(pb[0:120, 0:128], cbuf, self.idf[:, :], [tcb, self.tC], [tpb])
                        self.cp(self.rr("ev", ["act", "dve"]), stg[0:120, half, dc * 128:(dc + 1) * 128], pb[0:120, 0:128], [tpb], [tstg])
                for half in range(2):
                    self.dma("sp", self.o_spool[half * 120:(half + 1) * 120, :], stg[:, half, :], [tstg], (), "stst")
            else:
                for dc in range(8):
                    th = self.tok("hn", self.phase_id, id(seg), dc)
                    self.cp("pool", self.HP[:, dc, :], hn[:, dc, n:n + 15], [th], [tHP])
                if st.last:
                    stg2, tstg2 = self.A([15, D], F32)
                    for dc in range(8):
                        pb, tpb = self.bank()
                        self.tr(pb[0:15, 0:128], self.HP[:, dc, :], self.idf[:, :], [tHP, self.tC], [tpb])
                        self.cp(self.rr("ev", ["act", "dve"]), stg2[0:15, dc * 128:(dc + 1) * 128], pb[0:15, 0:128], [tpb], [tstg2])
                    self.dma("sp", self.o_ppool, stg2, [tstg2], (), "stst")
            for tile in seg.tiles():
                _, t0, nn = tile
                c0 = seg.off + t0
                for gi in range(4):
                    for eo in range(2):
                        dco = 2 * gi + eo
                        pb, tpb = self.bank()
                        for ci in range(2):
                            self.mm(pb[:, 0:nn], pw[:, gi, ci, eo * 128:(eo + 1) * 128], PL[:, 2 * gi + ci, t0:t0 + nn],
                                    [tpw, self.tok("PL", self.phase_id, id(seg), 2 * gi + ci)], [tpb], start=(ci == 0), stop=(ci == 1))
                        xv = self.xT[:, dco, c0:c0 + nn]
                        self.stt(xv, pb[:, 0:nn], self.vcol("psc", dco), xv, ALU.mult, ALU.add, [tpb, self.tC], [self.xtok(dco, tile)])

    def _norm_pool(self, st, wname, dstf, sq2, rsb):
        for tile in st.tiles():
            seg, t0, n = tile
            c0 = seg.off + t0
            sq, tsq = sq2[self.rr("sq2", [0, 1])]
            rs, trs = rsb[self.rr("rsb", [0, 1])]
            pb, tpb = self.bank()
            for dc in range(8):
                xin = self.xT[:, dc, c0:c0 + n]
                if dc % 2 == 0:
                    self.tt("pool", sq[:, dc, 0:n], xin, xin, ALU.mult, [self.xtok(dc, tile)], [tsq])
                else:
                    self.act(sq[:, dc, 0:n], xin, AF.Square, [self.xtok(dc, tile)], [tsq])
            for dc in range(8):
                self.mm(pb[:, 0:n], self.ones_m[:], sq[:, dc, 0:n], [tsq, self.tC], [tpb], start=(dc == 0), stop=(dc == 7))
            self.act(rs[:, 0:n], pb[:, 0:n], AF.Ln, [tpb], [trs], bias=EPS)
            self.act(rs[:, 0:n], rs[:, 0:n], AF.Exp, [trs], [trs], scale=-0.5)
            for dc in range(8):
                dst, tdst = dstf(dc, tile)
                self.stt(dst, self.V(seg, self.xT[:, dc, c0:c0 + n]), self.vcol(wname, dc), self.V(seg, rs[:, 0:n]), ALU.mult, ALU.mult,
                         [self.xtok(dc, tile), trs, self.tC], [tdst])

    def final(self, st):
        self.phase()
        sq2 = [self.A([128, 8, 512], BF16) for _ in range(2)]
        rsb = [self.A([128, 512], F32) for _ in range(2)]
        yT = [self.A([128, 8, 512], F32) for _ in range(2)]
        ysg = [self.A([128, D], F32) for _ in range(3)]
        cur = {}

        def dstf(dc, tile):
            return cur["y"][0][:, dc, 0:tile[2]], cur["y"][1]

        for tile in st.tiles():
            seg, t0, n = tile
            cur["y"] = yT[self.rr("yT", [0, 1])]
            self._norm_one(tile, "nfin", dstf, sq2, rsb)
            y, ty = cur["y"]
            b0 = 0
            while b0 < n:
                if seg.kind == "P":
                    pos = seg.pos0 + t0 + b0
                    if pos < NMETA:
                        b0 += NMETA - pos
                        continue
                m = min(128, n - b0)
                sg, tsg = ysg[self.rr("ysg", [0, 1, 2])]
                for half in range(2):
                    pb, tpb = self.bank()
                    for j in range(4):
                        dc = half * 4 + j
                        self.tr(pb[0:m, j * 128:(j + 1) * 128], y[:, dc, b0:b0 + m], self.idf[:, :], [ty, self.tC], [tpb])
                    self.cp(("act", "dve")[half], sg[0:m, half * 512:(half + 1) * 512], pb[0:m, :], [tpb], [tsg])
                if seg.kind == "S":
                    self.dma("sp", self.ys[b0:b0 + m, :], sg[0:m, :], [tsg], (), "sty%d" % ((self.rrc["ysg"] - 1) % 3))
                else:
                    r0 = seg.pos0 + t0 + b0 - NMETA
                    self.dma("sp", self.yp[r0:r0 + m, :], sg[0:m, :], [tsg], (), "sty%d" % ((self.rrc["ysg"] - 1) % 3))
                b0 += m

    def _norm_one(self, tile, wname, dstf, sq2, rsb):
        seg, t0, n = tile
        c0 = seg.off + t0
        sq, tsq = sq2[self.rr("sq2", [0, 1])]
        rs, trs = rsb[self.rr("rsb", [0, 1])]
        pb, tpb = self.bank()
        for dc in range(8):
            xin = self.xT[:, dc, c0:c0 + n]
            if dc % 2 == 0:
                self.tt("pool", sq[:, dc, 0:n], xin, xin, ALU.mult, [self.xtok(dc, tile)], [tsq])
            else:
                self.act(sq[:, dc, 0:n], xin, AF.Square, [self.xtok(dc, tile)], [tsq])
        for dc in range(8):
            self.mm(pb[:, 0:n], self.ones_m[:], sq[:, dc, 0:n], [tsq, self.tC], [tpb], start=(dc == 0), stop=(dc == 7))
        self.act(rs[:, 0:n], pb[:, 0:n], AF.Ln, [tpb], [trs], bias=EPS)
        self.act(rs[:, 0:n], rs[:, 0:n], AF.Exp, [trs], [trs], scale=-0.5)
        for dc in range(8):
            dst, tdst = dstf(dc, tile)
            self.stt(dst, self.xT[:, dc, c0:c0 + n], self.vcol(wname, dc), rs[:, 0:n], ALU.mult, ALU.mult,
                     [self.xtok(dc, tile), trs, self.tC], [tdst])


_NC_CACHE = {}


def _get_nc():
    if "nc" not in _NC_CACHE:
        b = Builder()
        _NC_CACHE["nc"] = b.build()
    return _NC_CACHE["nc"]


def kernel(**inp):
    inp = {k: np.asarray(v) for k, v in inp.items()}
    f = lambda a: np.ascontiguousarray(a, dtype=np.float32)
    nc = _get_nc()
    vecs = build_vecs(inp)
    consts = build_consts()
    w_in = f(inp["gdn_w_in"][0])
    wba = np.zeros((D, 40), np.float32)
    wba[:, 0:8] = w_in[:, 4104:4112]
    wba[:, 32:40] = w_in[:, 4096:4104]
    shared = {
        "meta": f(inp["meta_tokens"]), "w_in": w_in, "wba": wba, "w_out": f(inp["gdn_w_out"][0]),
        "pool_w": f(inp["pool_w"][0]), "w_up": f(inp["ffn_w_up"]), "w_down": f(inp["ffn_w_down"]),
        "vecs": vecs, "consts": consts,
    }
    in_maps = []
    for c in range(8):
        sl = slice(16 * c, 16 * c + 16)
        m = dict(shared)
        m["xp"] = f(inp["x_prompt"][c])
        m["xs"] = f(inp["x_sample"][sl].reshape(128, D))
        m["st_conv"] = f(inp["state_gdn_conv"][0, sl].reshape(48, 3072))
        m["st_rec"] = f(inp["state_gdn_rec"][0, sl])
        m["st_pool"] = f(inp["state_pool"][0, sl].reshape(240, D))
        m["st_ffn"] = f(inp["state_ffn_conv"][:, sl].reshape(2, 32, 5632))
        in_maps.append(m)
    res = run_bass_kernel_spmd(nc, in_maps, core_ids=list(range(8)))
    R = res.results
    g = lambda k: [np.asarray(r[k], dtype=np.float32) for r in R]
    y_prompt = np.stack(g("yp"), 0)
    y_sample = np.concatenate(g("ys"), 0).reshape(128, 8, D)
    p_conv = np.stack(g("o_pconv"), 0)[None]
    p_rec = np.stack(g("o_prec"), 0)[None]
    p_pool = np.stack(g("o_ppool"), 0)[None]
    p_ffn = np.stack(g("o_pffn"), 1)
    s_conv = np.concatenate([a.reshape(16, 3, 3072) for a in g("o_sconv")], 0)[None]
    s_rec = np.concatenate(g("o_srec"), 0)[None]
    s_pool = np.concatenate([a.reshape(16, 15, D) for a in g("o_spool")], 0)[None]
    s_ffn = np.concatenate([a.reshape(2, 16, 2, 5632) for a in g("o_sffn")], 1)
    return (y_prompt, y_sample, p_conv, p_rec, p_pool, p_ffn, s_conv, s_rec, s_pool, s_ffn)
```

```python
import contextlib
import numpy as np
import concourse.bass as bass
import concourse.mybir as mybir
from concourse.bass_utils import run_bass_kernel_spmd

F32 = mybir.dt.float32
BF16 = mybir.dt.bfloat16
ALU = mybir.AluOpType
AF = mybir.ActivationFunctionType
AX = mybir.AxisListType

D = 1024
NH = 8
DFF = 2816
NFC = 44
SEQ = 2048
NMETA = 16
EPS = 1e-6
NEG = -1.0e30
DEBUG_MAP = None
WINS = (2, 4, 8, 16)


class Tok:
    __slots__ = ("lastw", "readers", "excl")

    def __init__(self):
        self.lastw = None
        self.readers = []
        self.excl = False


class Op:
    __slots__ = ("eng", "fn", "deps", "ms", "dma_sem", "dma_val", "is_dma", "where")

    def __init__(self, eng, fn):
        import sys as _s
        f = _s._getframe(3)
        self.where = (f.f_lineno, f.f_back.f_lineno if f.f_back else 0)
        self.eng = eng
        self.fn = fn
        self.deps = []
        self.ms = None
        self.is_dma = False
        self.dma_sem = None
        self.dma_val = 0


class Prog:
    ENGS = ("pe", "act", "dve", "pool", "sp")

    def __init__(self, nc):
        self.nc = nc
        self.ops = {e: [] for e in self.ENGS}
        self.streams = {}
        self.pending = {}

    def barrier(self):
        lasts = [self.ops[e][-1] for e in self.ENGS if self.ops[e]]
        lasts += [st[0] for st in self.streams.values() if st[0] is not None]
        for e in self.ENGS:
            self.pending[e] = list(lasts)

    def op(self, eng, fn, reads=(), writes=(), stream=None):
        o = Op(eng, fn)
        is_dma = stream is not None
        deps = []
        for t in reads:
            if t.lastw is not None:
                deps.append((t.lastw, True))
            if t.excl:
                for r in t.readers:
                    if r.eng != eng:
                        deps.append((r, True))
        for t in writes:
            if t.lastw is not None:
                deps.append((t.lastw, False))
            for r in t.readers:
                deps.append((r, False))
        for d in self.pending.pop(eng, []):
            deps.append((d, True))
        if is_dma:
            o.is_dma = True
            st = self.streams.setdefault(stream, [None, 0])
            if st[0] is not None:
                deps.append((st[0], True))
            st[1] += 1
            o.dma_sem = stream
            o.dma_val = 16 * st[1]
            st[0] = o
        seen = set()
        for d, raw in deps:
            if d is o or id(d) in seen:
                continue
            if (not d.is_dma) and (not is_dma) and d.eng == eng and eng == "pe":
                continue
            seen.add(id(d))
            o.deps.append(d)
        for t in reads:
            t.readers.append(o)
        for t in writes:
            t.lastw = o
            t.readers = []
        self.ops[eng].append(o)
        return o

    def emit(self):
        nc = self.nc
        for e in self.ENGS:
            for o in self.ops[e]:
                for d in o.deps:
                    if not d.is_dma:
                        d.ms = True
        for e in self.ENGS:
            k = 0
            for o in self.ops[e]:
                if o.ms and not o.is_dma:
                    k += 1
                    o.ms = k
        with contextlib.ExitStack() as es:
            esem = {e: es.enter_context(nc.semaphore("s_" + e)) for e in self.ENGS}
            dsem = {k: es.enter_context(nc.semaphore("d_%d" % i)) for i, k in enumerate(self.streams)}
            block = es.enter_context(nc.Block())
            prog = self

            def run(e, engobj):
                seen = {}
                for o in prog.ops[e]:
                    for d in o.deps:
                        if d.is_dma:
                            key, val, sem = ("d", d.dma_sem), d.dma_val, dsem[d.dma_sem]
                        else:
                            key, val, sem = ("e", d.eng), d.ms, esem[d.eng]
                        if seen.get(key, 0) >= val:
                            continue
                        seen[key] = val
                        engobj.wait_ge(sem, val)
                    ins = o.fn(engobj)
                    if DEBUG_MAP is not None:
                        try:
                            DEBUG_MAP[str(ins.ins.name)] = o.where
                        except Exception as ex:
                            DEBUG_MAP["err"] = repr(ex)
                    if o.is_dma:
                        ins.then_inc(dsem[o.dma_sem], 16)
                    elif o.ms:
                        ins.then_inc(esem[e], 1)
                if e == "sp":
                    for k, st in prog.streams.items():
                        engobj.wait_ge(dsem[k], 16 * st[1])

            block.tensor(lambda eng: run("pe", eng))
            block.scalar(lambda eng: run("act", eng))
            block.vector(lambda eng: run("dve", eng))
            block.gpsimd(lambda eng: run("pool", eng))
            block.sync(lambda eng: run("sp", eng))


VEC_COLS = {}


def _vec_layout():
    off = 0
    for name, n in (("nm0", 8), ("nm1", 8), ("nf0", 8), ("nf1", 8), ("nfin", 8),
                    ("gcw", 96), ("fcw0", 132), ("fcw1", 132), ("fcb0", 44), ("fcb1", 44),
                    ("psc", 8), ("gnw", 1), ("alog", 1), ("dtb", 1)):
        VEC_COLS[name] = off
        off += n
    return off


NV = _vec_layout()
C_ID, C_TRI, C_NEGU, C_POSL, C_INVC = 0, 128, 192, 256, 320
C_TRI8, C_NEGU8, C_POSL8, C_SEL, C_RM = 380, 444, 508, 572, 636
NCONST = 636 + 8


def build_consts():
    c = np.zeros((128, NCONST), np.float32)
    c[:, C_ID:C_ID + 128] = np.eye(128, dtype=np.float32)
    p = np.arange(64)[:, None]
    f = np.arange(64)[None, :]
    c[:64, C_TRI:C_TRI + 64] = (f >= p).astype(np.float32)
    c[:64, C_NEGU:C_NEGU + 64] = np.where(f >= p, 0.0, NEG)
    c[:64, C_POSL:C_POSL + 64] = np.where(f < p, 0.0, -NEG)
    for gi, w in enumerate(WINS):
        for t in range(15):
            c[:, C_INVC + gi * 15 + t] = 1.0 / min(w, t + 1)
    same = (p // 8) == (f // 8)
    c[:64, C_TRI8:C_TRI8 + 64] = (same & (f >= p)).astype(np.float32)
    c[:64, C_NEGU8:C_NEGU8 + 64] = np.where(same & (f >= p), 0.0, NEG)
    c[:64, C_POSL8:C_POSL8 + 64] = np.where(same & (f < p), 0.0, -NEG)
    c[:64, C_SEL:C_SEL + 64] = (p == 8 * (f // 8) + 7).astype(np.float32)
    c[:64, C_RM:C_RM + 8] = ((p // 8) == np.arange(8)[None, :]).astype(np.float32)
    return c


def build_vecs(inp):
    v = np.zeros((128, NV), np.float32)

    def put(name, arr):
        a = np.asarray(arr, np.float32).reshape(-1, 128).T
        v[:, VEC_COLS[name]:VEC_COLS[name] + a.shape[1]] = a

    put("nm0", inp["norm_mix"][0]); put("nm1", inp["norm_mix"][1])
    put("nf0", inp["norm_ffn"][0]); put("nf1", inp["norm_ffn"][1])
    put("nfin", inp["norm_final"])
    put("gcw", inp["gdn_conv_w"][0].reshape(-1))
    put("fcw0", inp["ffn_conv_w"][0].reshape(-1)); put("fcw1", inp["ffn_conv_w"][1].reshape(-1))
    put("fcb0", inp["ffn_conv_b"][0]); put("fcb1", inp["ffn_conv_b"][1])
    put("psc", inp["pool_scale"][0])
    put("gnw", inp["gdn_norm_w"][0])
    v[0:8, VEC_COLS["alog"]] = inp["gdn_A_log"][0]
    v[0:8, VEC_COLS["dtb"]] = inp["gdn_dt_bias"][0]
    return v


class Seg:
    def __init__(self, kind, n, off, pos0=0):
        self.kind, self.n, self.off, self.pos0 = kind, n, off, pos0

    def tiles(self):
        if self.kind == "S":
            return [(self, 0, 128)]
        k = (self.n + 511) // 512
        base = (self.n // k + 7) // 8 * 8
        out, t = [], 0
        while t < self.n:
            m = min(base, self.n - t)
            out.append((self, t, m))
            t += m
        return out


class ST:
    def __init__(self, segs, first, last):
        self.segs, self.first, self.last = segs, first, last
        self.NT = sum(s.n for s in segs)

    def tiles(self):
        return [t for s in self.segs for t in s.tiles()]


SUPER = [
    ST([Seg("P", 592, 0, 0), Seg("S", 128, 592)], True, False),
    ST([Seg("P", 704, 0, 592)], False, False),
    ST([Seg("P", 768, 0, 1296)], False, True),
]
NTMAX = 768


class Builder:
    def __init__(self):
        self.nc = nc = bass.Bass("TRN2", target_bir_lowering=False)
        self.P = Prog(nc)
        self.es = contextlib.ExitStack()
        self.toks = {}
        self.rrc = {}
        self.phase_id = 0

        def din(name, shape):
            return nc.dram_tensor(name, list(shape), F32, kind="ExternalInput").ap()

        def dout(name, shape):
            return nc.dram_tensor(name, list(shape), F32, kind="ExternalOutput").ap()

        self.xp = din("xp", [SEQ, D]); self.xs = din("xs", [128, D])
        self.st_conv = din("st_conv", [48, 3072]); self.st_rec = din("st_rec", [16, 8, 128, 128])
        self.st_pool = din("st_pool", [240, D]); self.st_ffn = din("st_ffn", [2, 32, 5632])
        self.meta = din("meta", [NMETA, D])
        self.w_in = din("w_in", [D, 4112]); self.wba = din("wba", [D, 40])
        self.w_out = din("w_out", [D, D]); self.pool_w = din("pool_w", [4, 256, 256])
        self.w_up = din("w_up", [2, D, 5632]); self.w_down = din("w_down", [2, DFF, D])
        self.vecs_d = din("vecs", [128, NV]); self.consts_d = din("consts", [128, NCONST])
        self.yp = dout("yp", [SEQ, D]); self.ys = dout("ys", [128, D])
        self.o_pconv = dout("o_pconv", [3, 3072]); self.o_prec = dout("o_prec", [8, 128, 128])
        self.o_ppool = dout("o_ppool", [15, D]); self.o_pffn = dout("o_pffn", [2, 2, 5632])
        self.o_sconv = dout("o_sconv", [48, 3072]); self.o_srec = dout("o_srec", [16, 8, 128, 128])
        self.o_spool = dout("o_spool", [240, D]); self.o_sffn = dout("o_sffn", [2, 32, 5632])

    def tok(self, *key):
        t = self.toks.get(key)
        if t is None:
            t = self.toks[key] = Tok()
            if key[0] == "bank":
                t.excl = True
        return t

    def sb(self, name, shape, dt):
        return self.es.enter_context(self.nc.sbuf_tensor(name, list(shape), dt))

    def rr(self, name, choices):
        i = self.rrc.get(name, 0)
        self.rrc[name] = i + 1
        return choices[i % len(choices)]

    def bank(self):
        i = self.rrc.get("bank", 0)
        self.rrc["bank"] = i + 1
        i %= 8
        return self.banks[i], self.tok("bank", i)

    def phase(self):
        self.P.barrier()
        self.aoff = 0
        self.phase_id += 1

    def A(self, shape, dt, key=None):
        n = int(np.prod(shape[1:]))
        nb = n * (4 if dt == F32 else 2)
        nb = (nb + 31) // 32 * 32
        ne = nb // 2
        assert self.aoff + ne <= self.arena_n, ("arena overflow", self.aoff, ne, self.arena_n)
        ap = self.arena[0:shape[0], self.aoff:self.aoff + ne]
        self.aoff += ne
        if dt == F32:
            ap = ap.bitcast(F32)
        ap = ap[:, 0:n]
        if len(shape) == 3:
            ap = ap.rearrange("p (a b) -> p a b", b=shape[2])
        elif len(shape) == 4:
            ap = ap.rearrange("p (a b c) -> p a b c", b=shape[2], c=shape[3])
        return ap, self.tok("arena", self.phase_id, self.aoff)

    def mm(self, out, lhsT, rhs, r, w, start=True, stop=True, skip=False):
        if skip:
            self.P.op("pe", lambda e: e.matmul(out, lhsT=lhsT, rhs=rhs, start=start, stop=stop, skip_group_check=True), r, w)
        else:
            self.P.op("pe", lambda e: e.matmul(out, lhsT=lhsT, rhs=rhs, start=start, stop=stop), r, w)

    def tr(self, out, in_, ident, r, w):
        self.P.op("pe", lambda e: e.transpose(out=out, in_=in_, identity=ident), r, w)

    def act(self, out, in_, func, r, w, bias=None, scale=None):
        kw = {}
        if bias is not None:
            kw["bias"] = bias
        if scale is not None:
            kw["scale"] = scale
        self.P.op("act", lambda e: e.activation(out=out, in_=in_, func=func, **kw), r, w)

    def tt(self, eng, out, in0, in1, op, r, w):
        self.P.op(eng, lambda e: e.tensor_tensor(out=out, in0=in0, in1=in1, op=op), r, w)

    def ts(self, eng, out, in0, s1, op0, r, w, s2=None, op1=None):
        if op1 is None:
            self.P.op(eng, lambda e: e.tensor_scalar(out=out, in0=in0, scalar1=s1, scalar2=None, op0=op0), r, w)
        else:
            self.P.op(eng, lambda e: e.tensor_scalar(out=out, in0=in0, scalar1=s1, scalar2=s2, op0=op0, op1=op1), r, w)

    def stt(self, out, in0, scalar, in1, op0, op1, r, w):
        self.P.op("dve", lambda e: e.scalar_tensor_tensor(out=out, in0=in0, scalar=scalar, in1=in1, op0=op0, op1=op1), r, w)

    def cp(self, eng, out, in_, r, w):
        if eng == "act":
            self.act(out, in_, AF.Copy, r, w)
        else:
            self.P.op(eng, lambda e: e.tensor_copy(out=out, in_=in_), r, w)

    def dma(self, q, out, in_, r, w, stream):
        self.P.op(q, lambda e: e.dma_start(out=out, in_=in_), r, w, stream=stream)

    def memset(self, eng, ap, val, w):
        self.P.op(eng, lambda e: e.memset(ap, val), (), w)

    def vcol(self, name, j=0, np_=128):
        c = VEC_COLS[name] + j
        return self.vecs[0:np_, c:c + 1]

    @staticmethod
    def V(seg, ap):
        if seg.kind == "S":
            return ap.rearrange("p (s t) -> p s t", t=8)
        return ap

    @staticmethod
    def ext_dst(seg, buf, H, n):
        if seg.kind == "S":
            return buf[:, 0:16 * (H + 8)].rearrange("p (s w) -> p s w", w=H + 8)[:, :, H:H + 8]
        return buf[:, H:H + n]

    @staticmethod
    def ext_tap(seg, buf, H, j, n):
        if seg.kind == "S":
            return buf[:, 0:16 * (H + 8)].rearrange("p (s w) -> p s w", w=H + 8)[:, :, j:j + 8]
        return buf[:, j:j + n]

    @staticmethod
    def ext_halo(seg, buf, H):
        if seg.kind == "S":
            return buf[:, 0:16 * (H + 8)].rearrange("p (s w) -> p s w", w=H + 8)[:, :, 0:H]
        return buf[:, 0:H]

    @staticmethod
    def ext_tail(seg, buf, H, n):
        if seg.kind == "S":
            return buf[:, 0:16 * (H + 8)].rearrange("p (s w) -> p s w", w=H + 8)[:, :, 8:8 + H]
        return buf[:, n:n + H]

    def build(self):
        nc = self.nc
        with self.es:
            self.xT = self.sb("xT", [128, 8, NTMAX], F32)
            self.S32 = self.sb("S32", [128, 8, 128], F32)
            self.S16 = self.sb("S16", [128, 8, 128], BF16)
            self.HG = self.sb("HG", [128, 24, 3], F32)
            self.HF = self.sb("HF", [128, 2, NFC, 2], F32)
            self.HP = self.sb("HP", [128, 8, 15], F32)
            self.vecs = self.sb("vecs_sb", [128, NV], F32)
            self.cst = self.sb("cst", [128, NCONST], F32)
            self.idb = self.sb("idb", [128, 128], BF16)
            self.ones_m = self.sb("ones_m", [128, 128], BF16)
            self.ones_1 = self.sb("ones_1", [128, 128], BF16)
            self.ones_f = self.sb("ones_f", [64, 128], F32)
            self.nexpA = self.sb("nexpA", [8, 1], F32)
            self.lnq = self.sb("lnq", [128, 1], F32)
            self.banks = [self.es.enter_context(nc.psum_tensor("pb%d" % i, [128, 512], F32)) for i in range(8)]
            rem = nc.sbuf_bytes_remaining - 2048
            self.arena_n = (rem // 2) // 64 * 64
            self.arena = self.sb("arena", [128, self.arena_n], BF16)
            self.aoff = 0
            self.idf = self.cst[:, C_ID:C_ID + 128]
            tC = self.tok("consts")
            self.dma("sp", self.vecs[:], self.vecs_d, (), [tC], "ldc0")
            self.dma("sp", self.cst[:], self.consts_d, (), [tC], "ldc1")
            self.cp("dve", self.idb[:], self.idf, [tC], [tC])
            self.memset("pool", self.ones_m[:], 1.0 / 1024.0, [tC])
            self.memset("pool", self.ones_1[:], 1.0, [tC])
            self.memset("pool", self.ones_f[:], 1.0, [tC])
            self.memset("pool", self.lnq[:], -0.5 * float(np.log(128.0)), [tC])
            self.memset("pool", self.S32[:], 0.0, [self.tok("S32")])
            self.memset("pool", self.S16[:], 0.0, [self.tok("S16")])
            self.memset("pool", self.HG[:], 0.0, [self.tok("HG")])
            self.memset("pool", self.HF[:], 0.0, [self.tok("HF")])
            self.memset("pool", self.HP[:], 0.0, [self.tok("HP")])
            self.act(self.nexpA[:], self.vcol("alog", 0, 8), AF.Exp, [tC], [tC])
            self.ts("dve", self.nexpA[:], self.nexpA[:], -1.0, ALU.mult, [tC], [tC])
            self.tC = tC
            for st in SUPER:
                self.run_super(st)
            self.P.emit()
        return nc

    def run_super(self, st):
        self.load_x(st)
        self.gdn(st)
        self.ffn(st, 0)
        self.pool_mixer(st)
        self.ffn(st, 1)
        self.final(st)

    def xtok(self, dc, tile):
        return self.tok("xT", dc, tile[0].off + tile[1])

    def load_x(self, st):
        if st.first:
            self.phase()
        stg = [self.A([128, 4, D], F32) for _ in range(2)]
        bi = 0
        for seg in st.segs:
            for (_, t0, n) in seg.tiles():
                sg, tsg = stg[bi % 2]
                bi += 1
                nb = (n + 127) // 128
                for b in range(nb):
                    m = min(128, n - b * 128)
                    if seg.kind == "S":
                        self.dma("sp", sg[0:m, b, :], self.xs[0:m, :], (), [tsg], "ldx")
                    else:
                        p0 = seg.pos0 + t0 + b * 128
                        r = 0
                        if p0 < NMETA:
                            k = min(m, NMETA - p0)
                            self.dma("sp", sg[0:k, b, :], self.meta[p0:p0 + k, :], (), [tsg], "ldx")
                            r = k
                        if r < m:
                            a = p0 + r - NMETA
                            self.dma("sp", sg[r:m, b, :], self.xp[a:a + (m - r), :], (), [tsg], "ldx")
                tile = (seg, t0, n)
                c0 = seg.off + t0
                for dc in range(8):
                    pb, tpb = self.bank()
                    for b in range(nb):
                        m = min(128, n - b * 128)
                        self.tr(pb[:, b * 128:b * 128 + m], sg[0:m, b, dc * 128:(dc + 1) * 128], self.idf[0:m, 0:m],
                                [tsg, self.tC], [tpb])
                    self.cp(self.rr("ev", ["act", "dve"]), self.xT[:, dc, c0:c0 + n], pb[:, 0:n], [tpb], [self.xtok(dc, tile)])

    def norm(self, st, wname, dstf, sq2, rsb):
        for tile in st.tiles():
            seg, t0, n = tile
            c0 = seg.off + t0
            sq, tsq = sq2[self.rr("sq2", [0, 1])]
            rs, trs = rsb[self.rr("rsb", [0, 1])]
            pb, tpb = self.bank()
            for dc in range(8):
                xin = self.xT[:, dc, c0:c0 + n]
                if dc % 2 == 0:
                    self.tt("pool", sq[:, dc, 0:n], xin, xin, ALU.mult, [self.xtok(dc, tile)], [tsq])
                else:
                    self.act(sq[:, dc, 0:n], xin, AF.Square, [self.xtok(dc, tile)], [tsq])
            for dc in range(8):
                self.mm(pb[:, 0:n], self.ones_m[:], sq[:, dc, 0:n], [tsq, self.tC], [tpb], start=(dc == 0), stop=(dc == 7))
            self.act(rs[:, 0:n], pb[:, 0:n], AF.Ln, [tpb], [trs], bias=EPS)
            self.act(rs[:, 0:n], rs[:, 0:n], AF.Exp, [trs], [trs], scale=-0.5)
            for dc in range(8):
                dst, tdst = dstf(dc, tile)
                self.stt(dst, self.xT[:, dc, c0:c0 + n], self.vcol(wname, dc), rs[:, 0:n], ALU.mult, ALU.mult,
                         [self.xtok(dc, tile), trs, self.tC], [tdst])

    def gdn(self, st):
        NT = st.NT
        self.phase()
        hasS = any(s.kind == "S" for s in st.segs)
        xn, _ = self.A([128, 8, NT], BF16)
        QKVZ, _ = self.A([128, 32, NT], BF16)
        GB, tGB = self.A([40, NT], F32)
        mark = self.aoff
        sq2 = [self.A([128, 8, 512], BF16) for _ in range(2)]
        rsb = [self.A([128, 512], F32) for _ in range(2)]
        wsl = [self.A([128, 8, 512], BF16) for _ in range(3)]
        wbat, twba = self.A([128, 8, 40], BF16)
        ext = [self.A([128, 3 + 512], F32) for _ in range(3)]
        acc = [self.A([128, 512], F32) for _ in range(2)]
        sil = [self.A([128, 512], F32) for _ in range(2)]
        sqh = [self.A([128, 512], BF16) for _ in range(3)]
        rin = [self.A([128, 512], F32) for _ in range(2)]
        bat = [self.A([8, 512], F32) for _ in range(4)]
        if hasS:
            SHG, tSHG = self.A([128, 24, 48], F32)
            stg, tstg = self.A([48, 3072], F32)
        self.memset("pool", GB, 0.0, [tGB])
        xnt = lambda dc, tile: (xn[:, dc, tile[0].off + tile[1]:tile[0].off + tile[1] + tile[2]], self.tok("xn", self.phase_id, dc, tile[0].off + tile[1]))
        self.norm(st, "nm0", xnt, sq2, rsb)
        if hasS:
            self.dma("sp", stg, self.st_conv, (), [tstg], "ldst")
            for g in range(3):
                pb, tpb = self.bank()
                for j in range(8):
                    fc = g * 8 + j
                    self.tr(pb[:, j * 48:(j + 1) * 48], stg[0:48, fc * 128:(fc + 1) * 128], self.idf[0:48, 0:48], [tstg, self.tC], [tpb])
                self.cp("act", SHG[:, g * 8:(g + 1) * 8, :], pb[:, 0:384].rearrange("p (a b) -> p a b", b=48), [tpb], [tSHG])
        self.dma("pool", wbat, self.wba.rearrange("(kc p) f -> p kc f", p=128), (), [twba], "ldwba")
        for tile in st.tiles():
            seg, t0, n = tile
            c0 = seg.off + t0
            pb, tpb = self.bank()
            for kc in range(8):
                self.mm(pb[0:40, 0:n], wbat[:, kc, :], xn[:, kc, c0:c0 + n], [twba, xnt(kc, tile)[1]], [tpb], start=(kc == 0), stop=(kc == 7))
            self.act(GB[32:40, c0:c0 + n], pb[32:40, 0:n], AF.Sigmoid, [tpb], [tGB])
            (b1, t1), (b2, t2), (b3, t3), (b4, t4) = bat
            self.ts("dve", b1[:, 0:n], pb[0:8, 0:n], self.vcol("dtb", 0, 8), ALU.add, [tpb, self.tC], [t1])
            self.stt(b2[:, 0:n], b1[:, 0:n], -1.0, b1[:, 0:n], ALU.mult, ALU.max, [t1], [t2])
            self.act(b3[:, 0:n], b2[:, 0:n], AF.Exp, [t2], [t3], scale=-1.0)
            self.act(b4[:, 0:n], b3[:, 0:n], AF.Ln, [t3], [t4], bias=1.0)
            self.stt(b2[:, 0:n], b1[:, 0:n], 0.0, b4[:, 0:n], ALU.max, ALU.add, [t1, t4], [t2])
            self.ts("dve", GB[0:8, c0:c0 + n], b2[:, 0:n], self.nexpA[:, 0:1], ALU.mult, [t2, self.tC], [tGB])
        def ld_win(u):
            wt_, twt_ = wsl[u % 3]
            self.dma("pool", wt_, self.w_in[:, u * 512:(u + 1) * 512].rearrange("(kc p) f -> p kc f", p=128), (), [twt_], "ldw%d" % (u % 3))
        ld_win(0); ld_win(1)
        pend = []
        qk_list = []
        for u in range(8):
            wt, twt = wsl[u % 3]
            if u + 2 < 8:
                ld_win(u + 2)
            for j in range(4):
                fc = u * 4 + j
                kind = fc // 8
                prev_ext = None
                for tile in st.tiles():
                    seg, t0, n = tile
                    c0 = seg.off + t0
                    pb, tpb = self.bank()
                    for kc in range(8):
                        self.mm(pb[:, 0:n], wt[:, kc, j * 128:(j + 1) * 128], xn[:, kc, c0:c0 + n], [twt, xnt(kc, tile)[1]], [tpb],
                                start=(kc == 0), stop=(kc == 7))
                    dst = QKVZ[:, fc, c0:c0 + n]
                    tdst = self.tok("qkvz", self.phase_id, fc, c0)
                    if kind == 3:
                        self.act(self.V(seg, dst), self.V(seg, pb[:, 0:n]), AF.Silu, [tpb], [tdst])
                        continue
                    ex, tex = ext[self.rr("ext", [0, 1, 2])]
                    if seg.kind == "S":
                        self.cp("pool", self.ext_halo(seg, ex, 3), SHG[:, fc, :].rearrange("p (s r) -> p s r", r=3), [tSHG], [tex])
                    elif t0 == 0:
                        self.cp("pool", ex[:, 0:3], self.HG[:, fc, :], [self.tok("HG")], [tex])
                    else:
                        pe_, tpe_, pn = prev_ext
                        self.cp("pool", ex[:, 0:3], pe_[:, pn:pn + 3], [tpe_], [tex])
                    self.cp("act", self.ext_dst(seg, ex, 3, n), self.V(seg, pb[:, 0:n]), [tpb], [tex])
                    prev_ext = (ex, tex, n)
                    if seg.kind == "S":
                        self.cp("pool", SHG[:, fc, :].rearrange("p (s r) -> p s r", r=3), self.ext_tail(seg, ex, 3, n), [tex], [tSHG])
                    elif t0 + n == seg.n:
                        self.cp("pool", self.HG[:, fc, :], ex[:, n:n + 3], [tex], [self.tok("HG")])
                    ac, tac = acc[self.rr("acc", [0, 1])]
                    av = self.V(seg, ac[:, 0:n])
                    self.act(av, self.ext_tap(seg, ex, 3, 0, n), AF.Identity, [tex, self.tC], [tac], scale=self.vcol("gcw", 0 * 24 + fc))
                    for tap in (1, 2, 3):
                        self.stt(av, self.ext_tap(seg, ex, 3, tap, n), self.vcol("gcw", tap * 24 + fc), av, ALU.mult, ALU.add, [tex, tac, self.tC], [tac])

                    def tail(dst=dst, tdst=tdst, ac=ac, tac=tac, n=n):
                        self.act(dst, ac[:, 0:n], AF.Silu, [tac], [tdst])
                    if pend:
                        pend.pop(0)()
                    pend.append(tail)
                    if kind < 2:
                        qk_list.append((kind, dst, tdst, n))
            while pend:
                pend.pop(0)()
            def nstage1(item):
                kind, dst, tdst, n = item
                sh, tsh = sqh[self.rr("sqh", [0, 1, 2])]
                self.tt("dve", sh[:, 0:n], dst, dst, ALU.mult, [tdst], [tsh])
                pb2, tpb2 = self.bank()
                self.mm(pb2[:, 0:n], self.ones_1[:], sh[:, 0:n], [tsh, self.tC], [tpb2])
                return pb2, tpb2

            def nstage2(item, pb2, tpb2):
                kind, dst, tdst, n = item
                ri, tri_ = rin[self.rr("rin", [0, 1])]
                self.act(ri[:, 0:n], pb2[:, 0:n], AF.Ln, [tpb2], [tri_], bias=EPS)
                self.act(ri[:, 0:n], ri[:, 0:n], AF.Exp, [tri_], [tri_], scale=-0.5, bias=(self.lnq[:, 0:1] if kind == 0 else None))
                self.tt("dve", dst, dst, ri[:, 0:n], ALU.mult, [tdst, tri_], [tdst])
            inflight = []
            for item in qk_list:
                inflight.append((item,) + nstage1(item))
                if len(inflight) > 2:
                    nstage2(*inflight.pop(0))
            while inflight:
                nstage2(*inflight.pop(0))
            qk_list = []
        if hasS:
            for fc in range(24):
                pb, tpb = self.bank()
                self.tr(pb[0:48, 0:128], SHG[:, fc, :], self.idf[:, :], [tSHG, self.tC], [tpb])
                self.cp(self.rr("ev", ["act", "dve"]), stg[0:48, fc * 128:(fc + 1) * 128], pb[0:48, 0:128], [tpb], [tstg])
            self.dma("sp", self.o_sconv, stg, [tstg], (), "stst")
        if st.last:
            stg2, tstg2 = self.A([3, 3072], F32)
            for fc in range(24):
                pb, tpb = self.bank()
                self.tr(pb[0:3, 0:128], self.HG[:, fc, :], self.idf[:, :], [self.tok("HG"), self.tC], [tpb])
                self.cp(self.rr("ev", ["act", "dve"]), stg2[0:3, fc * 128:(fc + 1) * 128], pb[0:3, 0:128], [tpb], [tstg2])
            self.dma("sp", self.o_pconv, stg2, [tstg2], (), "stst")

        self.P.barrier()
        self.aoff = mark
        ONT = xn
        self.bfree = list(range(8))
        NA, NC_, NB = 3, 4, 1
        self.want_onacc = False
        TAs = [self.alloc_chunk_bufs("A") for _ in range(NA)]
        CAs = [self.alloc_chunk_bufs("C") for _ in range(NC_)]
        TBs = [self.alloc_chunk_bufs("B") for _ in range(NB)]
        jobs = []
        for seg in st.segs:
            if seg.kind == "P":
                c = 0
                if seg.pos0 == 0:
                    jobs.append(("P", seg.off, NMETA, None))
                    c = NMETA
                while c < seg.n:
                    jobs.append(("P", seg.off + c, 64, None))
                    c += 64
            else:
                for b_ in range(2):
                    jobs.append(("SB", seg.off + 64 * b_, 64, 8 * b_))
        N = len(jobs)
        pj = [ji for ji in range(N) if jobs[ji][0] == "P"]
        nP = len(pj)
        gbt_all, tpre = self.A([64, nP * 40], F32)
        Gt_all, _ = self.A([64, nP * 8], F32)
        eG_all, _ = self.A([64, nP * 8], F32)
        nb_all, _ = self.A([64, nP * 8], F32)
        nbG_all, _ = self.A([64, nP * 8], F32)
        self.memset("pool", gbt_all, 0.0, [tpre])
        pgb, tpgb, ipgb = self.bacq()
        for k, ji in enumerate(pj):
            _, c0_, L_, _ = jobs[ji]
            self.tr(pgb[0:L_, k * 40:(k + 1) * 40], GB[0:40, c0_:c0_ + L_], self.idf[0:40, 0:40], [tGB, self.tC], [tpgb])
        k = 0
        while k < nP:
            k2 = k
            while k2 < nP and jobs[pj[k2]][2] == jobs[pj[k]][2]:
                k2 += 1
            L_ = jobs[pj[k]][2]
            self.cp("dve", gbt_all[0:L_, k * 40:k2 * 40], pgb[0:L_, k * 40:k2 * 40], [tpgb], [tpre])
            k = k2
        self.brel(ipgb)
        g3 = gbt_all.rearrange("p (k c) -> p k c", c=40)
        pgc, tpgc, ipgc = self.bacq()
        self.mm(pgc[0:64, 0:nP * 8].rearrange("p (k c) -> p k c", c=8), self.cst[0:64, C_TRI:C_TRI + 64], g3[:, :, 0:8], [tpre, self.tC], [tpgc])
        self.cp("dve", Gt_all, pgc[0:64, 0:nP * 8], [tpgc], [tpre])
        self.brel(ipgc)
        self.act(eG_all, Gt_all, AF.Exp, [tpre], [tpre])
        self.ts("dve", nb_all.rearrange("p (k c) -> p k c", c=8), g3[:, :, 32:40], -1.0, ALU.mult, [tpre], [tpre])
        self.tt("dve", nbG_all, eG_all, nb_all, ALU.mult, [tpre], [tpre])
        pres = {}
        for k, ji in enumerate(pj):
            L_ = jobs[ji][2]
            pres[ji] = (gbt_all[0:L_, k * 40:k * 40 + 8], gbt_all[0:L_, k * 40 + 32:k * 40 + 40], Gt_all[0:L_, k * 8:(k + 1) * 8],
                        nb_all[0:L_, k * 8:(k + 1) * 8], nbG_all[0:L_, k * 8:(k + 1) * 8], tpre)
        nextA = 0
        nextB = 0
        doneA = set()
        actA = {}
        actB = None
        while nextB < N:
            for slot in range(NA):
                if slot not in actA and nextA < N and nextA < nextB + NC_:
                    actA[slot] = (nextA, self.chunk_A(jobs[nextA], QKVZ, GB, tGB, TAs[slot], CAs[nextA % NC_], pres.get(nextA)))
                    nextA += 1
            if actB is None and nextB in doneA:
                if jobs[nextB][0] == "SB":
                    nextB += 1
                    continue
                actB = self.chunk_B(jobs[nextB], CAs[nextB % NC_], TBs[nextB % NB], QKVZ, ONT)
            if actB is not None:
                try:
                    next(actB)
                except StopIteration:
                    actB = None
                    nextB += 1
            for slot in list(actA):
                j, g = actA[slot]
                try:
                    next(g)
                except StopIteration:
                    doneA.add(j)
                    del actA[slot]
        if hasS:
            self.P.barrier()
            onaccs = [self.A([64, 1024], F32) for _ in range(2)]
            save_off = self.aoff
            self.aoff = mark
            NSQ = 3
            assert all((ji % NC_) >= 2 for ji in range(N) if jobs[ji][0] == "SB")
            SBs = []
            for _ in range(NSQ):
                d = {}
                d["S32"] = self.A([128, 8, 128], F32); d["S16"] = self.A([128, 8, 128], BF16)
                d["Stmp"] = self.A([128, 1024], F32); d["vn"] = self.A([64, 1024], BF16)
                SBs.append(d)
            sb_jobs = [(ji, jobs[ji]) for ji in range(N) if jobs[ji][0] == "SB"]
            todo = [(bi, ji, job, s_) for bi, (ji, job) in enumerate(sb_jobs) for s_ in range(8)]
            remaining = {bi: 8 for bi in range(len(sb_jobs))}
            act = {}
            fin = []
            while todo or act or fin:
                for slot in range(NSQ):
                    if slot not in act and todo:
                        bi, ji, job, s_ = todo.pop(0)
                        act[slot] = (bi, self.sample_seq(job, s_, CAs[ji % NC_], SBs[slot], onaccs[bi]))
                for slot in list(act):
                    bi, g = act[slot]
                    try:
                        next(g)
                    except StopIteration:
                        del act[slot]
                        remaining[bi] -= 1
                        if remaining[bi] == 0:
                            ji, job = sb_jobs[bi]
                            fin.append(self.sample_finish(job, TBs[0], onaccs[bi], QKVZ, ONT))
                for g in list(fin):
                    try:
                        next(g)
                    except StopIteration:
                        fin.remove(g)
            self.aoff = max(save_off, self.aoff)
        if st.last:
            self.dma("sp", self.o_prec.rearrange("h k v -> k h v"), self.S32[:], [self.tok("S32")], (), "strec")

        self.P.barrier()
        self.aoff = mark
        wo, two = self.A([128, 8, D], BF16)
        self.dma("pool", wo[:, :, 0:512], self.w_out[:, 0:512].rearrange("(kc p) f -> p kc f", p=128), (), [two], "ldw0")
        self.dma("pool", wo[:, :, 512:1024], self.w_out[:, 512:1024].rearrange("(kc p) f -> p kc f", p=128), (), [two], "ldw1")
        for tile in st.tiles():
            seg, t0, n = tile
            c0 = seg.off + t0
            for dc in range(8):
                pb, tpb = self.bank()
                for kc in range(8):
                    self.mm(pb[:, 0:n], wo[:, kc, dc * 128:(dc + 1) * 128], ONT[:, kc, c0:c0 + n], [two], [tpb], start=(kc == 0), stop=(kc == 7))
                xv = self.xT[:, dc, c0:c0 + n]
                self.tt("dve", xv, pb[:, 0:n], xv, ALU.add, [tpb], [self.xtok(dc, tile)])

    def alloc_chunk_bufs(self, which):
        b = {}
        def a(name, shape, dt):
            b[name] = self.A(shape, dt)
        if which == "A":
            a("gbt", [64, 40], F32)
            for nm in ("Gt", "eG", "nbG", "nb", "dGl", "eGl"):
                a(nm, [64, 8], F32)
            a("Dm", [64, 512], F32); a("Du", [64, 512], F32); a("Dl", [64, 512], F32); a("eGbc", [128, 512], F32)
            b["rhsG"] = b["Dl"]
            a("Lneg", [64, 512], BF16); a("M0", [64, 512], BF16)
            a("QTa", [64, 512], BF16); a("QTb", [64, 512], BF16)
            a("kbgn", [64, 1024], BF16)
        elif which == "C":
            a("PQa", [64, 1024], BF16); a("PQb", [64, 1024], BF16); a("At", [64, 512], BF16)
            a("kd", [64, 1024], BF16); a("vb", [64, 1024], BF16)
            a("nWT", [128, 512], BF16); a("qdT", [128, 512], BF16); a("gtc", [128, 64], F32)
        else:
            a("vn", [64, 1024], BF16); a("sqo", [64, 1024], BF16); a("on", [64, 1024], BF16)
            a("Stmp", [128, 1024], F32); a("ss", [64, 8], F32); a("rs", [64, 8], F32)
            if self.want_onacc:
                a("onacc", [64, 1024], F32)
        return b

    def bacq(self):
        if not self.bfree:
            raise RuntimeError("out of PSUM banks")
        i = self.bfree.pop(0)
        return self.banks[i], self.tok("bank", i), i

    def brel(self, i):
        self.bfree.append(i)

    def chunk_A(self, job, QKVZ, GB, tGB, TA, CA, pre=None):
        kind, c0, L, sidx = job
        B = dict(TA); B.update(CA)
        tC = self.tC
        Q = lambda h: QKVZ[:, h, c0:c0 + L]
        K = lambda h: QKVZ[:, 8 + h, c0:c0 + L]
        Vv = lambda h: QKVZ[:, 16 + h, c0:c0 + L]
        h3 = lambda ap: ap.rearrange("p (h l) -> p h l", l=L)
        hd = lambda ap: ap.rearrange("p (h d) -> p h d", d=128)
        W8 = 8 * L
        blk = (kind == "SB")
        cT, cN, cP = (C_TRI8, C_NEGU8, C_POSL8) if blk else (C_TRI, C_NEGU, C_POSL)
        tri = self.cst[0:L, cT:cT + L]
        dGl, tdGl = B["dGl"]; eGl, teGl = B["eGl"]; gtc, tgtc = B["gtc"]
        rhsG, trG = B["rhsG"]
        if pre is not None:
            g_tm, beta, GtL, nbL, nbGL, tpre = pre
            tgbt = tGt = tnb = tnbG = tpre
            self.tt("pool", h3(rhsG[0:L, 0:W8]), tri.unsqueeze(1).to_broadcast([L, 8, L]), g_tm.unsqueeze(2).to_broadcast([L, 8, L]),
                    ALU.mult, [tgbt, tC], [trG])
            yield
            pG, tpG, ipG = self.bacq()
            self.mm(pG[:, 0:W8], self.ones_f[0:L, :], rhsG[0:L, 0:W8], [trG, tC], [tpG])
            yield
        else:
            pg, tpg, ipg = self.bacq()
            self.tr(pg[0:L, 0:40], GB[0:40, c0:c0 + L], self.idf[0:40, 0:40], [tGB, tC], [tpg])
            gbt, tgbt = B["gbt"]
            self.cp("dve", gbt[0:L, :], pg[0:L, 0:40], [tpg], [tgbt])
            self.brel(ipg)
            g_tm = gbt[0:L, 0:8]
            beta = gbt[0:L, 32:40]
            yield
            self.tt("pool", h3(rhsG[0:L, 0:W8]), tri.unsqueeze(1).to_broadcast([L, 8, L]), g_tm.unsqueeze(2).to_broadcast([L, 8, L]),
                    ALU.mult, [tgbt, tC], [trG])
            pg2, tpg2, ipg2 = self.bacq()
            self.mm(pg2[0:L, 0:8], tri, g_tm, [tgbt, tC], [tpg2])
            Gt, tGt = B["Gt"]; eG, teG = B["eG"]; nbG, tnbG = B["nbG"]; nb, tnb = B["nb"]
            self.cp("dve", Gt[0:L, :], pg2[0:L, 0:8], [tpg2], [tGt])
            self.brel(ipg2)
            self.ts("dve", nb[0:L, :], beta, -1.0, ALU.mult, [tgbt], [tnb])
            yield
            pG, tpG, ipG = self.bacq()
            self.mm(pG[:, 0:W8], self.ones_f[0:L, :], rhsG[0:L, 0:W8], [trG, tC], [tpG])
            self.act(eG[0:L, :], Gt[0:L, :], AF.Exp, [tGt], [teG])
            self.tt("dve", nbG[0:L, :], eG[0:L, :], nb[0:L, :], ALU.mult, [teG, tnb], [tnbG])
            GtL, nbL, nbGL = Gt[0:L, :], nb[0:L, :], nbG[0:L, :]
            yield
        if blk:
            Glast = None
            gl4 = pG[:, 0:W8].rearrange("p (h s t) -> p h s t", s=8, t=8)[:, :, :, 7]
        else:
            Glast = h3(pG[:, 0:W8])[:, :, L - 1]
        Dm, tDm = B["Dm"]; Du, tDu = B["Du"]; Dl, tDl = B["Dl"]; eGbc, teGbc = B["eGbc"]
        self.tt("dve", h3(Dm[0:L, 0:W8]), h3(pG[0:L, 0:W8]), GtL.unsqueeze(2).to_broadcast([L, 8, L]), ALU.subtract,
                [tpG, tGt], [tDm])
        if blk:
            pgl, tpgl, ipgl = self.bacq()
            self.mm(pgl[0:L, 0:8], self.cst[0:L, C_SEL:C_SEL + L], GtL, [tGt, tC], [tpgl])
            self.tt("dve", dGl[0:L, :], pgl[0:L, 0:8], GtL, ALU.subtract, [tpgl, tGt], [tdGl])
            self.brel(ipgl)
            self.act(gtc[:, 0:64].rearrange("p (h s) -> p h s", s=8), gl4, AF.Exp, [tpG], [tgtc])
        else:
            self.tt("dve", dGl[0:L, :], Glast[0:L], GtL, ALU.subtract, [tpG, tGt], [tdGl])
            self.act(gtc[:, 0:8], Glast, AF.Exp, [tpG], [tgtc])
        self.act(eGbc[:, 0:W8], pG[:, 0:W8], AF.Exp, [tpG], [teGbc])
        self.brel(ipG)
        pk, tpk, ipk = self.bacq()
        pkb = pk[:].bitcast(BF16)
        for h in range(8):
            self.tr(pkb[0:L, h * 128:(h + 1) * 128], K(h), self.idb[:], [tC], [tpk])
        pv, tpv, ipv = self.bacq()
        pvb = pv[:].bitcast(BF16)
        for h in range(8):
            self.tr(pvb[0:L, h * 128:(h + 1) * 128], Vv(h), self.idb[:], [tC], [tpv])
        yield
        self.act(eGl[0:L, :], dGl[0:L, :], AF.Exp, [tdGl], [teGl])
        negu = self.cst[0:L, cN:cN + L].unsqueeze(1).to_broadcast([L, 8, L])
        posl = self.cst[0:L, cP:cP + L].unsqueeze(1).to_broadcast([L, 8, L])
        self.tt("pool", h3(Du[0:L, 0:W8]), h3(Dm[0:L, 0:W8]), negu, ALU.add, [tDm, tC], [tDu])
        self.tt("pool", h3(Dl[0:L, 0:W8]), h3(Dm[0:L, 0:W8]), posl, ALU.add, [tDm, tC], [tDl])
        kbgn, tkb = B["kbgn"]; kd, tkd = B["kd"]; vb, tvb = B["vb"]
        self.tt("dve", hd(kbgn[0:L, :]), hd(pkb[0:L, :]), nbGL.unsqueeze(2).to_broadcast([L, 8, 128]), ALU.mult, [tpk, tnbG], [tkb])
        self.tt("dve", hd(vb[0:L, :]), hd(pvb[0:L, :]), beta.unsqueeze(2).to_broadcast([L, 8, 128]), ALU.mult, [tpv, tgbt], [tvb])
        self.brel(ipv)
        yield
        self.tt("dve", hd(kd[0:L, :]), hd(pkb[0:L, :]), eGl[0:L, :].unsqueeze(2).to_broadcast([L, 8, 128]), ALU.mult, [tpk, teGl], [tkd])
        self.brel(ipk)
        self.act(Du[0:L, 0:W8], Du[0:L, 0:W8], AF.Exp, [tDu], [tDu])
        self.act(Dl[0:L, 0:W8], Dl[0:L, 0:W8], AF.Exp, [tDl], [tDl], scale=-1.0)
        pkk, tpkk, ipkk = self.bacq()
        for h in range(8):
            self.mm(pkk[0:L, h * L:(h + 1) * L], K(h), K(h), [], [tpkk])
        pkq, tpkq, ipkq = self.bacq()
        for h in range(8):
            self.mm(pkq[0:L, h * L:(h + 1) * L], K(h), Q(h), [], [tpkq])
        qdT, tqd = B["qdT"]
        self.tt("pool", h3(qdT[:, 0:W8]), QKVZ[:, 0:8, c0:c0 + L], h3(eGbc[:, 0:W8]), ALU.mult, [teGbc], [tqd])
        yield
        self.tt("pool", h3(Dl[0:L, 0:W8]), h3(Dl[0:L, 0:W8]), nbL.unsqueeze(2).to_broadcast([L, 8, L]), ALU.mult,
                [tDl, tnb], [tDl])
        Lneg, tLn = B["Lneg"]; At, tAt = B["At"]; M0, tM0 = B["M0"]
        self.tt("dve", At[0:L, 0:W8], pkq[0:L, 0:W8], Du[0:L, 0:W8], ALU.mult, [tpkq, tDu], [tAt])
        self.brel(ipkq)
        yield
        self.tt("dve", Lneg[0:L, 0:W8], pkk[0:L, 0:W8], Dl[0:L, 0:W8], ALU.mult, [tpkk, tDl], [tLn])
        self.brel(ipkk)
        yield
        pm, tpm, ipm = self.bacq()
        pmb = pm[:].bitcast(BF16)
        for h in range(8):
            self.tr(pmb[0:L, h * L:(h + 1) * L], Lneg[0:L, h * L:(h + 1) * L], self.idb[0:L, 0:L], [tLn, tC], [tpm])
        self.cp("act", M0[0:L, 0:W8], pmb[0:L, 0:W8], [tpm], [tM0])
        self.brel(ipm)
        yield
        nlev = 3 if blk else {64: 6, 16: 4, 8: 3}[L]
        idbL = self.idb[0:L, 0:L]
        PQ = [B["PQa"], B["PQb"]]
        QTbufs = [B["QTa"], B["QTb"]]
        pq3 = lambda ap: ap[0:L, :].rearrange("p (h c) -> p h c", c=128)
        cur = 0
        Pc, tPc = PQ[cur]
        self.tt("pool", pq3(Pc)[:, :, 0:L], h3(M0[0:L, 0:W8]), idbL.unsqueeze(1).to_broadcast([L, 8, L]), ALU.add, [tM0, tC], [tPc])
        pq, tpq, ipq = self.bacq()
        for h in range(8):
            sl = slice(h * L, (h + 1) * L)
            self.mm(pq[0:L, sl], M0[0:L, sl], Lneg[0:L, sl], [tM0, tLn], [tpq])
        QTc = QTbufs[0]
        self.cp("act", QTc[0][0:L, 0:W8], pq[0:L, 0:W8], [tpq], [QTc[1]])
        self.brel(ipq)
        pq2, tpq2, ipq2 = self.bacq()
        for h in range(8):
            sl = slice(h * L, (h + 1) * L)
            self.mm(pq2[0:L, sl], Lneg[0:L, sl], M0[0:L, sl], [tM0, tLn], [tpq2])
        self.cp("dve", pq3(Pc)[:, :, L:2 * L], h3(pq2[0:L, 0:W8]), [tpq2], [tPc])
        self.brel(ipq2)
        yield
        for k in range(1, nlev):
            last = (k == nlev - 1)
            Pn, tPn = PQ[1 - cur]
            wid = L if last else 2 * L
            if not last:
                QTn = QTbufs[k % 2]
                pq, tpq, ipq = self.bacq()
                for h in range(8):
                    self.mm(pq[0:L, h * L:(h + 1) * L], pq3(Pc)[:, h, L:2 * L], QTc[0][0:L, h * L:(h + 1) * L], [tPc, QTc[1]], [tpq])
                self.cp("act", QTn[0][0:L, 0:W8], pq[0:L, 0:W8], [tpq], [QTn[1]])
                self.brel(ipq)
            for half in range(2):
                pp, tpp, ipp = self.bacq()
                for hh in range(4):
                    h = half * 4 + hh
                    self.mm(pp[0:L, hh * 128:hh * 128 + wid], QTc[0][0:L, h * L:(h + 1) * L], pq3(Pc)[:, h, 0:wid], [tPc, QTc[1]], [tpp])
                ppv = pp[0:L, :].rearrange("p (h c) -> p h c", c=128)
                hs = slice(half * 4, half * 4 + 4)
                self.tt("dve", pq3(Pn)[:, hs, 0:L], ppv[:, :, 0:L], pq3(Pc)[:, hs, 0:L], ALU.add, [tpp, tPc], [tPn])
                if not last:
                    self.cp("act", pq3(Pn)[:, hs, L:2 * L], ppv[:, :, L:2 * L], [tpp], [tPn])
                self.brel(ipp)
            cur = 1 - cur
            Pc, tPc = PQ[cur]
            if not last:
                QTc = QTn
            yield
        Ttv = pq3(Pc)
        tTt = tPc
        pw, tpw, ipw = self.bacq()
        for h in range(8):
            self.mm(pw[:, h * L:(h + 1) * L], kbgn[0:L, h * 128:(h + 1) * 128], Ttv[:, h, 0:L], [tkb, tTt], [tpw])
        nWT, tnW = B["nWT"]
        self.cp("act", nWT[:, 0:W8], pw[:, 0:W8], [tpw], [tnW])
        self.brel(ipw)
        CA["Tt"] = (Ttv, tTt)
        yield

    def chunk_B(self, job, CA, TB, QKVZ, ONT):
        kind, c0, L, sidx = job
        B = dict(TB); B.update(CA)
        tC = self.tC
        Tt, tTt = CA["Tt"]
        h3 = lambda ap: ap.rearrange("p (h l) -> p h l", l=L)
        hd = lambda ap: ap.rearrange("p (h d) -> p h d", d=128)
        W8 = 8 * L
        if kind == "S":
            S32, tS32 = self.SS32[sidx % 2]
            S16, tS16 = self.SS16[sidx % 2]
            self.dma("sp", S32, self.st_rec[sidx].rearrange("h k v -> k h v"), (), [tS32], "ldrec%d" % (sidx % 2))
            self.cp("pool", S16, S32, [tS32], [tS16])
        else:
            S32, tS32 = self.S32[:], self.tok("S32")
            S16, tS16 = self.S16[:], self.tok("S16")
        vb, tvb = B["vb"]; nWT, tnW = B["nWT"]; vn, tvn = B["vn"]; qdT, tqd = B["qdT"]; At, tAt = B["At"]
        kd, tkd = B["kd"]; sqo, tsq = B["sqo"]; on, ton = B["on"]; ss, tss = B["ss"]; rs, trs = B["rs"]
        gtc, tgtc = B["gtc"]; Stmp, tSt = B["Stmp"]
        for half in range(2):
            pv, tpv, ipv = self.bacq()
            for hh in range(4):
                h = half * 4 + hh
                self.mm(pv[0:L, hh * 128:(hh + 1) * 128], Tt[:, h, 0:L], vb[0:L, h * 128:(h + 1) * 128], [tTt, tvb], [tpv], start=(hh == 0), stop=False, skip=True)
            for hh in range(4):
                h = half * 4 + hh
                self.mm(pv[0:L, hh * 128:(hh + 1) * 128], nWT[:, h * L:(h + 1) * L], S16[:, h, :], [tnW, tS16], [tpv], start=False, stop=True, skip=True)
            self.cp(("act", "dve")[half], vn[0:L, half * 512:(half + 1) * 512], pv[0:L, :], [tpv], [tvn])
            self.brel(ipv)
        self.tt("pool", hd(Stmp[:, :]), S32, gtc[:, 0:8].unsqueeze(2).to_broadcast([128, 8, 128]), ALU.mult, [tS32, tgtc], [tSt])
        yield
        pss = []
        for half in range(2):
            pS, tpS, ipS = self.bacq()
            pss.append((pS, tpS, ipS))
            for hh in range(4):
                h = half * 4 + hh
                self.mm(pS[:, hh * 128:(hh + 1) * 128], kd[0:L, h * 128:(h + 1) * 128], vn[0:L, h * 128:(h + 1) * 128], [tkd, tvn], [tpS])
        for half in range(2):
            pS, tpS, ipS = pss[half]
            self.tt("dve", S32[:, half * 4:(half + 1) * 4, :], hd(pS[:, :]), hd(Stmp[:, half * 512:(half + 1) * 512]), ALU.add, [tpS, tSt], [tS32])
            self.brel(ipS)
        pos = []
        for half in range(2):
            po, tpo, ipo = self.bacq()
            pos.append((po, tpo, ipo))
            for hh in range(4):
                h = half * 4 + hh
                self.mm(po[0:L, hh * 128:(hh + 1) * 128], qdT[:, h * L:(h + 1) * L], S16[:, h, :], [tqd, tS16], [tpo], start=(hh == 0), stop=False, skip=True)
            for hh in range(4):
                h = half * 4 + hh
                self.mm(po[0:L, hh * 128:(hh + 1) * 128], At[0:L, h * L:(h + 1) * L], vn[0:L, h * 128:(h + 1) * 128], [tAt, tvn], [tpo], start=False, stop=True, skip=True)
        self.cp("act", S16, S32, [tS32], [tS16])
        if kind == "S":
            self.dma("sp", self.o_srec[sidx].rearrange("h k v -> k h v"), S32, [tS32], (), "strec%d" % (sidx % 2))
        yield
        for half in range(2):
            po, tpo, ipo = pos[half]
            self.act(sqo[0:L, half * 512:(half + 1) * 512], po[0:L, :], AF.Square, [tpo], [tsq])
        self.P.op("dve", lambda e, o=ss[0:L, :], i=hd(sqo[0:L, :]): e.tensor_reduce(out=o, in_=i, axis=AX.X, op=ALU.add), [tsq], [tss])
        self.act(rs[0:L, :], ss[0:L, :], AF.Ln, [tss], [trs], bias=EPS, scale=1.0 / 128.0)
        self.act(rs[0:L, :], rs[0:L, :], AF.Exp, [trs], [trs], scale=-0.5)
        yield
        for half in range(2):
            po, tpo, ipo = pos[half]
            self.tt("dve", hd(on[0:L, half * 512:(half + 1) * 512]), hd(po[0:L, :]), rs[0:L, half * 4:(half + 1) * 4].unsqueeze(2).to_broadcast([L, 4, 128]),
                    ALU.mult, [tpo, trs], [ton])
            self.brel(ipo)
        yield
        pt, tpt, ipt = self.bacq()
        ptb = pt[:].bitcast(BF16)
        for h in range(8):
            self.tr(ptb[:, h * L:(h + 1) * L], on[0:L, h * 128:(h + 1) * 128], self.idb[0:L, 0:L], [ton, tC], [tpt])
        self.stt(ONT[:, :, c0:c0 + L], h3(ptb[:, 0:W8]), self.vcol("gnw"), QKVZ[:, 24:32, c0:c0 + L], ALU.mult, ALU.mult, [tpt, tC], [self.tok("ONT", self.phase_id)])
        self.brel(ipt)
        yield

    def sample_seq(self, job, s_, CA, SB, onacc_t):
        kind, c0, L, s0 = job
        tC = self.tC
        Tt, tTt = CA["Tt"]
        hd = lambda ap: ap.rearrange("p (h d) -> p h d", d=128)
        vb, tvb = CA["vb"]; nWT, tnW = CA["nWT"]; qdT, tqd = CA["qdT"]; At, tAt = CA["At"]
        kd, tkd = CA["kd"]; gtc, tgtc = CA["gtc"]
        S32, tS32 = SB["S32"]; S16, tS16 = SB["S16"]; Stmp, tSt = SB["Stmp"]; vn, tvn = SB["vn"]
        onacc, tacc = onacc_t
        gtc3 = gtc[:, 0:64].rearrange("p (h s) -> p h s", s=8)
        sidx = s0 + s_
        rm = self.cst[0:L, C_RM + s_:C_RM + s_ + 1]
        self.dma("sp", S32, self.st_rec[sidx].rearrange("h k v -> k h v"), (), [tS32], "ldrec%d" % (sidx % 3))
        yield
        self.cp("pool", S16, S32, [tS32], [tS16])
        self.tt("pool", hd(Stmp[:, :]), S32, gtc3[:, :, s_].unsqueeze(2).to_broadcast([128, 8, 128]), ALU.mult, [tS32, tgtc], [tSt])
        yield
        for half in range(2):
            pv, tpv, ipv = self.bacq()
            for hh in range(4):
                h = half * 4 + hh
                self.mm(pv[0:L, hh * 128:(hh + 1) * 128], Tt[:, h, 0:L], vb[0:L, h * 128:(h + 1) * 128], [tTt, tvb], [tpv], start=(hh == 0), stop=False, skip=True)
            for hh in range(4):
                h = half * 4 + hh
                self.mm(pv[0:L, hh * 128:(hh + 1) * 128], nWT[:, h * L:(h + 1) * L], S16[:, h, :], [tnW, tS16], [tpv], start=False, stop=True, skip=True)
            if half == 0:
                self.act(vn[0:L, 0:512], pv[0:L, :], AF.Identity, [tpv, tC], [tvn], scale=rm)
            else:
                self.ts("dve", vn[0:L, 512:1024], pv[0:L, :], rm, ALU.mult, [tpv, tC], [tvn])
            self.brel(ipv)
        yield
        pss = []
        for half in range(2):
            pS, tpS, ipS = self.bacq()
            pss.append((pS, tpS, ipS))
            for hh in range(4):
                h = half * 4 + hh
                self.mm(pS[:, hh * 128:(hh + 1) * 128], kd[0:L, h * 128:(h + 1) * 128], vn[0:L, h * 128:(h + 1) * 128], [tkd, tvn], [tpS])
        for half in range(2):
            pS, tpS, ipS = pss[half]
            self.tt("dve", S32[:, half * 4:(half + 1) * 4, :], hd(pS[:, :]), hd(Stmp[:, half * 512:(half + 1) * 512]), ALU.add, [tpS, tSt], [tS32])
            self.brel(ipS)
        self.dma("sp", self.o_srec[sidx].rearrange("h k v -> k h v"), S32, [tS32], (), "strec%d" % (sidx % 3))
        pos = []
        for half in range(2):
            po, tpo, ipo = self.bacq()
            pos.append((po, tpo, ipo))
            for hh in range(4):
                h = half * 4 + hh
                self.mm(po[0:L, hh * 128:(hh + 1) * 128], qdT[:, h * L:(h + 1) * L], S16[:, h, :], [tqd, tS16], [tpo], start=(hh == 0), stop=False, skip=True)
            for hh in range(4):
                h = half * 4 + hh
                self.mm(po[0:L, hh * 128:(hh + 1) * 128], At[0:L, h * L:(h + 1) * L], vn[0:L, h * 128:(h + 1) * 128], [tAt, tvn], [tpo], start=False, stop=True, skip=True)
        yield
        for half in range(2):
            po, tpo, ipo = pos[half]
            acc = onacc[0:L, half * 512:(half + 1) * 512]
            if s_ == 0:
                self.ts("dve", acc, po[0:L, :], rm, ALU.mult, [tpo, tC], [tacc])
            else:
                self.stt(acc, po[0:L, :], rm, acc, ALU.mult, ALU.add, [tpo, tC, tacc], [tacc])
            self.brel(ipo)
        yield

    def sample_finish(self, job, TB, onacc_t, QKVZ, ONT):
        kind, c0, L, s0 = job
        tC = self.tC
        h3 = lambda ap: ap.rearrange("p (h l) -> p h l", l=L)
        hd = lambda ap: ap.rearrange("p (h d) -> p h d", d=128)
        W8 = 8 * L
        sqo, tsq = TB["sqo"]; on, ton = TB["on"]; ss, tss = TB["ss"]; rs, trs = TB["rs"]
        onacc, tacc = onacc_t
        self.act(sqo[0:L, :], onacc[0:L, :], AF.Square, [tacc], [tsq])
        self.P.op("dve", lambda e, o=ss[0:L, :], i=hd(sqo[0:L, :]): e.tensor_reduce(out=o, in_=i, axis=AX.X, op=ALU.add), [tsq], [tss])
        self.act(rs[0:L, :], ss[0:L, :], AF.Ln, [tss], [trs], bias=EPS, scale=1.0 / 128.0)
        self.act(rs[0:L, :], rs[0:L, :], AF.Exp, [trs], [trs], scale=-0.5)
        yield
        self.tt("dve", hd(on[0:L, :]), hd(onacc[0:L, :]), rs[0:L, :].unsqueeze(2).to_broadcast([L, 8, 128]), ALU.mult, [tacc, trs], [ton])
        yield
        pt, tpt, ipt = self.bacq()
        ptb = pt[:].bitcast(BF16)
        for h in range(8):
            self.tr(ptb[:, h * L:(h + 1) * L], on[0:L, h * 128:(h + 1) * 128], self.idb[0:L, 0:L], [ton, tC], [tpt])
        self.stt(ONT[:, :, c0:c0 + L], h3(ptb[:, 0:W8]), self.vcol("gnw"), QKVZ[:, 24:32, c0:c0 + L], ALU.mult, ALU.mult, [tpt, tC], [self.tok("ONT", self.phase_id)])
        self.brel(ipt)
        yield

    def ffn(self, st, l):
        NT = st.NT
        self.phase()
        hasS = any(s.kind == "S" for s in st.segs)
        xn, _ = self.A([128, 8, NT], BF16)
        hT, _ = self.A([128, 22, NT], BF16)
        sq2 = [self.A([128, 8, 512], BF16) for _ in range(2)]
        rsb = [self.A([128, 512], F32) for _ in range(2)]
        wsl = [self.A([128, 2, 8, 256], BF16) for _ in range(3)]
        wsl = [(w_, (t_, self.tok("wslb", self.phase_id, i_))) for i_, (w_, t_) in enumerate(wsl)]
        wdn = [self.A([128, 22, 128], BF16) for _ in range(3)]
        ub = [self.A([128, 2 + 512], F32) for _ in range(4)]
        t0b = [self.A([128, 512], F32) for _ in range(4)]
        sab = [self.A([128, 512], F32) for _ in range(2)]
        if hasS:
            SHF, tSHF = self.A([128, NFC, 32], F32)
            stg, tstg = self.A([32, 5632], F32)
        xnt = lambda dc, tile: (xn[:, dc, tile[0].off + tile[1]:tile[0].off + tile[1] + tile[2]], self.tok("xn", self.phase_id, dc, tile[0].off + tile[1]))
        self.norm(st, "nf%d" % l, xnt, sq2, rsb)
        if hasS:
            self.dma("sp", stg, self.st_ffn[l], (), [tstg], "ldst")
            for g in range(0, NFC, 8):
                pb, tpb = self.bank()
                ng = min(8, NFC - g)
                for j in range(ng):
                    fc = g + j
                    self.tr(pb[:, j * 32:(j + 1) * 32], stg[0:32, fc * 128:(fc + 1) * 128], self.idf[0:32, 0:32], [tstg, self.tC], [tpb])
                self.cp("act", SHF[:, g:g + ng, :], pb[:, 0:ng * 32].rearrange("p (a b) -> p a b", b=32), [tpb], [tSHF])
        tHF = self.tok("HF")
        fcw, fcb = "fcw%d" % l, "fcb%d" % l
        def ld_wup(u):
            wt_, twt_ = wsl[u % 3]
            self.dma("pool", wt_[:, 0], self.w_up[l][:, u * 256:(u + 1) * 256].rearrange("(kc p) f -> p kc f", p=128), (), [twt_[0]], "ldw%d" % (u % 3))
            self.dma("pool", wt_[:, 1], self.w_up[l][:, DFF + u * 256:DFF + (u + 1) * 256].rearrange("(kc p) f -> p kc f", p=128), (), [twt_[1]], "ldwb%d" % (u % 3))

        def ld_wdn(dc):
            wd_, twd_ = wdn[dc % 3]
            self.dma("pool", wd_, self.w_down[l][:, dc * 128:(dc + 1) * 128].rearrange("(i p) d -> p i d", p=128), (), [twd_], "ldwd%d" % (dc % 3))
        ld_wup(0); ld_wup(1)
        pend = []
        for u in range(11):
            wt, twt = wsl[u % 3]
            if u + 2 < 11:
                ld_wup(u + 2)
            elif u + 2 == 11:
                ld_wdn(0)
            else:
                ld_wdn(1)
            for j in range(2):
                i = u * 2 + j
                prev = [None, None]
                for tile in st.tiles():
                    seg, t0, n = tile
                    c0 = seg.off + t0
                    conv = []
                    for ab in range(2):
                        fc = i + 22 * ab
                        pb, tpb = self.bank()
                        for kc in range(8):
                            self.mm(pb[:, 0:n], wt[:, ab, kc, j * 128:(j + 1) * 128], xn[:, kc, c0:c0 + n], [twt[ab], xnt(kc, tile)[1]], [tpb],
                                    start=(kc == 0), stop=(kc == 7))
                        ex, tex = ub[self.rr("ub", [0, 1, 2, 3])]
                        if seg.kind == "S":
                            self.cp("pool", self.ext_halo(seg, ex, 2), SHF[:, fc, :].rearrange("p (s r) -> p s r", r=2), [tSHF], [tex])
                        elif t0 == 0:
                            self.cp("pool", ex[:, 0:2], self.HF[:, l, fc, :], [tHF], [tex])
                        else:
                            pe_, tpe_, pn = prev[ab]
                            self.cp("pool", ex[:, 0:2], pe_[:, pn:pn + 2], [tpe_], [tex])
                        self.cp("act", self.ext_dst(seg, ex, 2, n), self.V(seg, pb[:, 0:n]), [tpb], [tex])
                        prev[ab] = (ex, tex, n)
                        if seg.kind == "S":
                            self.cp("pool", SHF[:, fc, :].rearrange("p (s r) -> p s r", r=2), self.ext_tail(seg, ex, 2, n), [tex], [tSHF])
                        elif t0 + n == seg.n:
                            self.cp("pool", self.HF[:, l, fc, :], ex[:, n:n + 2], [tex], [tHF])
                        tb, ttb = t0b[self.rr("t0b", [0, 1, 2, 3])]
                        tv = self.V(seg, tb[:, 0:n])
                        self.act(tv, self.V(seg, pb[:, 0:n]), AF.Identity, [tpb, self.tC], [ttb], bias=self.vcol(fcb, fc), scale=self.vcol(fcw, 2 * NFC + fc))
                        self.stt(tv, self.ext_tap(seg, ex, 2, 1, n), self.vcol(fcw, 1 * NFC + fc), tv, ALU.mult, ALU.add, [tex, ttb, self.tC], [ttb])
                        self.stt(tv, self.ext_tap(seg, ex, 2, 0, n), self.vcol(fcw, 0 * NFC + fc), tv, ALU.mult, ALU.add, [tex, ttb, self.tC], [ttb])
                        conv.append((tb, ttb))
                    def tail(conv=conv, i=i, c0=c0, n=n):
                        sa, tsa = sab[self.rr("sab", [0, 1])]
                        self.act(sa[:, 0:n], conv[0][0][:, 0:n], AF.Silu, [conv[0][1]], [tsa])
                        self.tt("dve", hT[:, i, c0:c0 + n], sa[:, 0:n], conv[1][0][:, 0:n], ALU.mult, [tsa, conv[1][1]], [self.tok("hT", self.phase_id, i, c0)])
                    if pend:
                        pend.pop(0)()
                    pend.append(tail)
        while pend:
            pend.pop(0)()
        if hasS:
            for fc in range(NFC):
                pb, tpb = self.bank()
                self.tr(pb[0:32, 0:128], SHF[:, fc, :], self.idf[:, :], [tSHF, self.tC], [tpb])
                self.cp(self.rr("ev", ["act", "dve"]), stg[0:32, fc * 128:(fc + 1) * 128], pb[0:32, 0:128], [tpb], [tstg])
            self.dma("sp", self.o_sffn[l], stg, [tstg], (), "stst")
        if st.last:
            stg2, tstg2 = self.A([2, 5632], F32)
            for fc in range(NFC):
                pb, tpb = self.bank()
                self.tr(pb[0:2, 0:128], self.HF[:, l, fc, :], self.idf[:, :], [tHF, self.tC], [tpb])
                self.cp(self.rr("ev", ["act", "dve"]), stg2[0:2, fc * 128:(fc + 1) * 128], pb[0:2, 0:128], [tpb], [tstg2])
            self.dma("sp", self.o_pffn[l], stg2, [tstg2], (), "stst")
        for dc in range(8):
            wd, twd = wdn[dc % 3]
            if dc + 2 < 8:
                ld_wdn(dc + 2)
            for tile in st.tiles():
                seg, t0, n = tile
                c0 = seg.off + t0
                pb, tpb = self.bank()
                for i in range(22):
                    self.mm(pb[:, 0:n], wd[:, i, :], hT[:, i, c0:c0 + n], [twd, self.tok("hT", self.phase_id, i, c0)], [tpb], start=(i == 0), stop=(i == 21))
                xv = self.xT[:, dc, c0:c0 + n]
                self.tt("dve", xv, pb[:, 0:n], xv, ALU.add, [tpb], [self.xtok(dc, tile)])

    def pool_mixer(self, st):
        self.phase()
        sq2 = [self.A([128, 8, 512], BF16) for _ in range(2)]
        rsb = [self.A([128, 512], F32) for _ in range(2)]
        pw, tpw = self.A([128, 4, 2, 256], BF16)
        self.dma("pool", pw, self.pool_w.rearrange("g (ci p) e -> p g ci e", p=128), (), [tpw], "ldw0")
        segbuf = {}
        for seg in st.segs:
            W = 16 * 23 if seg.kind == "S" else 15 + seg.n
            hn, _ = self.A([128, 8, W], F32)
            s1, _ = self.A([128, 2, W], F32)
            s2, _ = self.A([128, 2, W], F32)
            PL, _ = self.A([128, 8, seg.n], BF16)
            segbuf[id(seg)] = (hn, s1, s2, PL, W)
        if any(s.kind == "S" for s in st.segs):
            stg, tstg = self.A([120, 2, D], F32)
            self._pcb = [self.A([128, 120], F32) for _ in range(2)]
        tmp15, ttmp15 = self.A([128, 15], F32)
        tHP = self.tok("HP")

        def dstf(dc, tile):
            seg, t0, n = tile
            hn = segbuf[id(seg)][0]
            if seg.kind == "S":
                ap = hn[:, dc, :].rearrange("p (s w) -> p s w", w=23)[:, :, 15:23]
            else:
                ap = hn[:, dc, 15 + t0:15 + t0 + n]
            return ap, self.tok("hn", self.phase_id, id(seg), dc)

        self._norm_pool(st, "nm1", dstf, sq2, rsb)
        for seg in st.segs:
            hn, s1, s2, PL, W = segbuf[id(seg)]
            n = seg.n
            if seg.kind == "S":
                for half in range(2):
                    self.dma("sp", stg[:, half, :], self.st_pool[half * 120:(half + 1) * 120, :], (), [tstg], "ldst")
                for dc in range(8):
                    pb, tpb = self.bank()
                    for half in range(2):
                        self.tr(pb[:, half * 120:(half + 1) * 120], stg[0:120, half, dc * 128:(dc + 1) * 128], self.idf[0:120, 0:120], [tstg, self.tC], [tpb])
                    self.cp(self.rr("ev", ["act", "dve"]), hn[:, dc, :].rearrange("p (s w) -> p s w", w=23)[:, :, 0:15],
                            pb[:, 0:240].rearrange("p (s r) -> p s r", r=15), [tpb], [self.tok("hn", self.phase_id, id(seg), dc)])
            else:
                for dc in range(8):
                    self.cp("pool", hn[:, dc, 0:15], self.HP[:, dc, :], [tHP], [self.tok("hn", self.phase_id, id(seg), dc)])
            if seg.kind == "S":
                e3 = lambda ap: ap.rearrange("p (s w) -> p s w", w=23)
                sl = lambda ap, a, b: e3(ap)[:, :, a:b]
                WW = 23
            else:
                sl = lambda ap, a, b: ap[:, a:b]
                WW = W
            for dc in range(8):
                gi = dc // 2
                th = self.tok("hn", self.phase_id, id(seg), dc)
                ts1 = self.tok("ps1", self.phase_id, id(seg), dc % 2)
                ts2 = self.tok("ps2", self.phase_id, id(seg), dc % 2)
                src, tsrc = hn[:, dc, :], th
                bufs = [(s1[:, dc % 2, :], ts1), (s2[:, dc % 2, :], ts2)]
                for lev in range(gi + 1):
                    sh = 1 << lev
                    lo = (1 << (lev + 1)) - 1
                    dstb, tdb = bufs[lev % 2]
                    self.tt(("pool", "dve")[dc % 2], sl(dstb, lo, WW), sl(src, lo, WW), sl(src, lo - sh, WW - sh), ALU.add, [tsrc], [tdb])
                    src, tsrc = dstb, tdb
                if seg.kind == "S":
                    outv = PL[:, dc, :].rearrange("p (s t) -> p s t", t=8)
                else:
                    outv = PL[:, dc, :]
                tPL = self.tok("PL", self.phase_id, id(seg), dc)
                self.stt(outv, sl(src, 15, WW), 1.0 / WINS[gi], sl(hn[:, dc, :], 15, WW), ALU.mult, ALU.subtract, [tsrc, th], [tPL])
                if seg.kind == "P" and seg.pos0 == 0:
                    ic = self.cst[:, C_INVC + gi * 15:C_INVC + gi * 15 + 15]
                    self.tt("dve", tmp15, src[:, 15:30], ic, ALU.mult, [tsrc, self.tC], [ttmp15])
                    self.tt("dve", PL[:, dc, 0:15], tmp15, hn[:, dc, 15:30], ALU.subtract, [ttmp15, th], [tPL])
            if seg.kind == "S":
                for dc in range(8):
                    th = self.tok("hn", self.phase_id, id(seg), dc)
                    for half in range(2):
                        pb, tpb = self.bank()
                        src3 = hn[:, dc, :].rearrange("p (s w) -> p s w", w=23)[:, half * 8:(half + 1) * 8, 8:23]
                        cbuf, tcb = self._pcb[self.rr("pcb", [0, 1])]
                        self.cp("pool", cbuf.rearrange("p (s r) -> p s r", r=15), src3, [th], [tcb])
                        self.tr(pb[0:120, 0:128], cbuf, self.idf[:, :], [tcb, self.tC], [tpb])
                        self.cp(self.rr("ev", ["act", "dve"]), stg[0:120, half, dc * 128:(dc + 1) * 128], pb[0:120, 0:128], [tpb], [tstg])
                for half in range(2):
                    self.dma("sp", self.o_spool[half * 120:(half + 1) * 120, :], stg[:, half, :], [tstg], (), "stst")
            else:
                for dc in range(8):
                    th = self.tok("hn", self.phase_id, id(seg), dc)
                    self.cp("pool", self.HP[:, dc, :], hn[:, dc, n:n + 15], [th], [tHP])
                if st.last:
                    stg2, tstg2 = self.A([15, D], F32)
                    for dc in range(8):
                        pb, tpb = self.bank()
                        self.tr(pb[0:15, 0:128], self.HP[:, dc, :], self.idf[:, :], [tHP, self.tC], [tpb])
                        self.cp(self.rr("ev", ["act", "dve"]), stg2[0:15, dc * 128:(dc + 1) * 128], pb[0:15, 0:128], [tpb], [tstg2])
                    self.dma("sp", self.o_ppool, stg2, [tstg2], (), "stst")
            for tile in seg.tiles():
                _, t0, nn = tile
                c0 = seg.off + t0
                for gi in range(4):
                    for eo in range(2):
                        dco = 2 * gi + eo
                        pb, tpb = self.bank()
                        for ci in range(2):
                            self.mm(pb[:, 0:nn], pw[:, gi, ci, eo * 128:(eo + 1) * 128], PL[:, 2 * gi + ci, t0:t0 + nn],
                                    [tpw, self.tok("PL", self.phase_id, id(seg), 2 * gi + ci)], [tpb], start=(ci == 0), stop=(ci == 1))
                        xv = self.xT[:, dco, c0:c0 + nn]
                        self.stt(xv, pb[:, 0:nn], self.vcol("psc", dco), xv, ALU.mult, ALU.add, [tpb, self.tC], [self.xtok(dco, tile)])

    def _norm_pool(self, st, wname, dstf, sq2, rsb):
        for tile in st.tiles():
            seg, t0, n = tile
            c0 = seg.off + t0
            sq, tsq = sq2[self.rr("sq2", [0, 1])]
            rs, trs = rsb[self.rr("rsb", [0, 1])]
            pb, tpb = self.bank()
            for dc in range(8):
                xin = self.xT[:, dc, c0:c0 + n]
                if dc % 2 == 0:
                    self.tt("pool", sq[:, dc, 0:n], xin, xin, ALU.mult, [self.xtok(dc, tile)], [tsq])
                else:
                    self.act(sq[:, dc, 0:n], xin, AF.Square, [self.xtok(dc, tile)], [tsq])
            for dc in range(8):
                self.mm(pb[:, 0:n], self.ones_m[:], sq[:, dc, 0:n], [tsq, self.tC], [tpb], start=(dc == 0), stop=(dc == 7))
            self.act(rs[:, 0:n], pb[:, 0:n], AF.Ln, [tpb], [trs], bias=EPS)
            self.act(rs[:, 0:n], rs[:, 0:n], AF.Exp, [trs], [trs], scale=-0.5)
            for dc in range(8):
                dst, tdst = dstf(dc, tile)
                self.stt(dst, self.V(seg, self.xT[:, dc, c0:c0 + n]), self.vcol(wname, dc), self.V(seg, rs[:, 0:n]), ALU.mult, ALU.mult,
                         [self.xtok(dc, tile), trs, self.tC], [tdst])

    def final(self, st):
        self.phase()
        sq2 = [self.A([128, 8, 512], BF16) for _ in range(2)]
        rsb = [self.A([128, 512], F32) for _ in range(2)]
        yT = [self.A([128, 8, 512], F32) for _ in range(2)]
        ysg = [self.A([128, D], F32) for _ in range(3)]
        cur = {}

        def dstf(dc, tile):
            return cur["y"][0][:, dc, 0:tile[2]], cur["y"][1]

        for tile in st.tiles():
            seg, t0, n = tile
            cur["y"] = yT[self.rr("yT", [0, 1])]
            self._norm_one(tile, "nfin", dstf, sq2, rsb)
            y, ty = cur["y"]
            b0 = 0
            while b0 < n:
                if seg.kind == "P":
                    pos = seg.pos0 + t0 + b0
                    if pos < NMETA:
                        b0 += NMETA - pos
                        continue
                m = min(128, n - b0)
                sg, tsg = ysg[self.rr("ysg", [0, 1, 2])]
                for half in range(2):
                    pb, tpb = self.bank()
                    for j in range(4):
                        dc = half * 4 + j
                        self.tr(pb[0:m, j * 128:(j + 1) * 128], y[:, dc, b0:b0 + m], self.idf[:, :], [ty, self.tC], [tpb])
                    self.cp(("act", "dve")[half], sg[0:m, half * 512:(half + 1) * 512], pb[0:m, :], [tpb], [tsg])
                if seg.kind == "S":
                    self.dma("sp", self.ys[b0:b0 + m, :], sg[0:m, :], [tsg], (), "sty%d" % ((self.rrc["ysg"] - 1) % 3))
                else:
                    r0 = seg.pos0 + t0 + b0 - NMETA
                    self.dma("sp", self.yp[r0:r0 + m, :], sg[0:m, :], [tsg], (), "sty%d" % ((self.rrc["ysg"] - 1) % 3))
                b0 += m

    def _norm_one(self, tile, wname, dstf, sq2, rsb):
        seg, t0, n = tile
        c0 = seg.off + t0
        sq, tsq = sq2[self.rr("sq2", [0, 1])]
        rs, trs = rsb[self.rr("rsb", [0, 1])]
        pb, tpb = self.bank()
        for dc in range(8):
            xin = self.xT[:, dc, c0:c0 + n]
            if dc % 2 == 0:
                self.tt("pool", sq[:, dc, 0:n], xin, xin, ALU.mult, [self.xtok(dc, tile)], [tsq])
            else:
                self.act(sq[:, dc, 0:n], xin, AF.Square, [self.xtok(dc, tile)], [tsq])
        for dc in range(8):
            self.mm(pb[:, 0:n], self.ones_m[:], sq[:, dc, 0:n], [tsq, self.tC], [tpb], start=(dc == 0), stop=(dc == 7))
        self.act(rs[:, 0:n], pb[:, 0:n], AF.Ln, [tpb], [trs], bias=EPS)
        self.act(rs[:, 0:n], rs[:, 0:n], AF.Exp, [trs], [trs], scale=-0.5)
        for dc in range(8):
            dst, tdst = dstf(dc, tile)
            self.stt(dst, self.xT[:, dc, c0:c0 + n], self.vcol(wname, dc), rs[:, 0:n], ALU.mult, ALU.mult,
                     [self.xtok(dc, tile), trs, self.tC], [tdst])


_NC_CACHE = {}


def _get_nc():
    if "nc" not in _NC_CACHE:
        b = Builder()
        _NC_CACHE["nc"] = b.build()
    return _NC_CACHE["nc"]


def kernel(**inp):
    inp = {k: np.asarray(v) for k, v in inp.items()}
    f = lambda a: np.ascontiguousarray(a, dtype=np.float32)
    nc = _get_nc()
    vecs = build_vecs(inp)
    consts = build_consts()
    w_in = f(inp["gdn_w_in"][0])
    wba = np.zeros((D, 40), np.float32)
    wba[:, 0:8] = w_in[:, 4104:4112]
    wba[:, 32:40] = w_in[:, 4096:4104]
    shared = {
        "meta": f(inp["meta_tokens"]), "w_in": w_in, "wba": wba, "w_out": f(inp["gdn_w_out"][0]),
        "pool_w": f(inp["pool_w"][0]), "w_up": f(inp["ffn_w_up"]), "w_down": f(inp["ffn_w_down"]),
        "vecs": vecs, "consts": consts,
    }
    in_maps = []
    for c in range(8):
        sl = slice(16 * c, 16 * c + 16)
        m = dict(shared)
        m["xp"] = f(inp["x_prompt"][c])
        m["xs"] = f(inp["x_sample"][sl].reshape(128, D))
        m["st_conv"] = f(inp["state_gdn_conv"][0, sl].reshape(48, 3072))
        m["st_rec"] = f(inp["state_gdn_rec"][0, sl])
        m["st_pool"] = f(inp["state_pool"][0, sl].reshape(240, D))
        m["st_ffn"] = f(inp["state_ffn_conv"][:, sl].reshape(2, 32, 5632))
        in_maps.append(m)
    res = run_bass_kernel_spmd(nc, in_maps, core_ids=list(range(8)))
    R = res.results
    g = lambda k: [np.asarray(r[k], dtype=np.float32) for r in R]
    y_prompt = np.stack(g("yp"), 0)
    y_sample = np.concatenate(g("ys"), 0).reshape(128, 8, D)
    p_conv = np.stack(g("o_pconv"), 0)[None]
    p_rec = np.stack(g("o_prec"), 0)[None]
    p_pool = np.stack(g("o_ppool"), 0)[None]
    p_ffn = np.stack(g("o_pffn"), 1)
    s_conv = np.concatenate([a.reshape(16, 3, 3072) for a in g("o_sconv")], 0)[None]
    s_rec = np.concatenate(g("o_srec"), 0)[None]
    s_pool = np.concatenate([a.reshape(16, 15, D) for a in g("o_spool")], 0)[None]
    s_ffn = np.concatenate([a.reshape(2, 16, 2, 5632) for a in g("o_sffn")], 1)
    return (y_prompt, y_sample, p_conv, p_rec, p_pool, p_ffn, s_conv, s_rec, s_pool, s_ffn)
```

```python
import contextlib
import numpy as np
import concourse.bass as bass
import concourse.mybir as mybir
from concourse.bass_utils import run_bass_kernel_spmd

F32 = mybir.dt.float32
BF16 = mybir.dt.bfloat16
ALU = mybir.AluOpType
AF = mybir.ActivationFunctionType
AX = mybir.AxisListType

D = 1024
NH = 8
DFF = 2816
NFC = 44
SEQ = 2048
NMETA = 16
EPS = 1e-6
NEG = -1.0e30
DEBUG_MAP = None
WINS = (2, 4, 8, 16)


class Tok:
    __slots__ = ("lastw", "readers", "excl")

    def __init__(self):
        self.lastw = None
        self.readers = []
        self.excl = False


class Op:
    __slots__ = ("eng", "fn", "deps", "ms", "dma_sem", "dma_val", "is_dma", "where")

    def __init__(self, eng, fn):
        import sys as _s
        f = _s._getframe(3)
        self.where = (f.f_lineno, f.f_back.f_lineno if f.f_back else 0)
        self.eng = eng
        self.fn = fn
        self.deps = []
        self.ms = None
        self.is_dma = False
        self.dma_sem = None
        self.dma_val = 0


class Prog:
    ENGS = ("pe", "act", "dve", "pool", "sp")

    def __init__(self, nc):
        self.nc = nc
        self.ops = {e: [] for e in self.ENGS}
        self.streams = {}
        self.pending = {}

    def barrier(self):
        lasts = [self.ops[e][-1] for e in self.ENGS if self.ops[e]]
        lasts += [st[0] for st in self.streams.values() if st[0] is not None]
        for e in self.ENGS:
            self.pending[e] = list(lasts)

    def op(self, eng, fn, reads=(), writes=(), stream=None):
        o = Op(eng, fn)
        is_dma = stream is not None
        deps = []
        for t in reads:
            if t.lastw is not None:
                deps.append((t.lastw, True))
            if t.excl:
                for r in t.readers:
                    if r.eng != eng:
                        deps.append((r, True))
        for t in writes:
            if t.lastw is not None:
                deps.append((t.lastw, False))
            for r in t.readers:
                deps.append((r, False))
        for d in self.pending.pop(eng, []):
            deps.append((d, True))
        if is_dma:
            o.is_dma = True
            st = self.streams.setdefault(stream, [None, 0])
            if st[0] is not None:
                deps.append((st[0], True))
            st[1] += 1
            o.dma_sem = stream
            o.dma_val = 16 * st[1]
            st[0] = o
        seen = set()
        for d, raw in deps:
            if d is o or id(d) in seen:
                continue
            if (not d.is_dma) and (not is_dma) and d.eng == eng and eng == "pe":
                continue
            seen.add(id(d))
            o.deps.append(d)
        for t in reads:
            t.readers.append(o)
        for t in writes:
            t.lastw = o
            t.readers = []
        self.ops[eng].append(o)
        return o

    def emit(self):
        nc = self.nc
        for e in self.ENGS:
            for o in self.ops[e]:
                for d in o.deps:
                    if not d.is_dma:
                        d.ms = True
        for e in self.ENGS:
            k = 0
            for o in self.ops[e]:
                if o.ms and not o.is_dma:
                    k += 1
                    o.ms = k
        with contextlib.ExitStack() as es:
            esem = {e: es.enter_context(nc.semaphore("s_" + e)) for e in self.ENGS}
            dsem = {k: es.enter_context(nc.semaphore("d_%d" % i)) for i, k in enumerate(self.streams)}
            block = es.enter_context(nc.Block())
            prog = self

            def run(e, engobj):
                seen = {}
                for o in prog.ops[e]:
                    for d in o.deps:
                        if d.is_dma:
                            key, val, sem = ("d", d.dma_sem), d.dma_val, dsem[d.dma_sem]
                        else:
                            key, val, sem = ("e", d.eng), d.ms, esem[d.eng]
                        if seen.get(key, 0) >= val:
                            continue
                        seen[key] = val
                        engobj.wait_ge(sem, val)
                    ins = o.fn(engobj)
                    if DEBUG_MAP is not None:
                        try:
                            DEBUG_MAP[str(ins.ins.name)] = o.where
                        except Exception as ex:
                            DEBUG_MAP["err"] = repr(ex)
                    if o.is_dma:
                        ins.then_inc(dsem[o.dma_sem], 16)
                    elif o.ms:
                        ins.then_inc(esem[e], 1)
                if e == "sp":
                    for k, st in prog.streams.items():
                        engobj.wait_ge(dsem[k], 16 * st[1])

            block.tensor(lambda eng: run("pe", eng))
            block.scalar(lambda eng: run("act", eng))
            block.vector(lambda eng: run("dve", eng))
            block.gpsimd(lambda eng: run("pool", eng))
            block.sync(lambda eng: run("sp", eng))


VEC_COLS = {}


def _vec_layout():
    off = 0
    for name, n in (("nm0", 8), ("nm1", 8), ("nf0", 8), ("nf1", 8), ("nfin", 8),
                    ("gcw", 96), ("fcw0", 132), ("fcw1", 132), ("fcb0", 44), ("fcb1", 44),
                    ("psc", 8), ("gnw", 1), ("alog", 1), ("dtb", 1)):
        VEC_COLS[name] = off
        off += n
    return off


NV = _vec_layout()
C_ID, C_TRI, C_NEGU, C_POSL, C_INVC = 0, 128, 192, 256, 320
C_TRI8, C_NEGU8, C_POSL8, C_SEL, C_RM = 380, 444, 508, 572, 636
NCONST = 636 + 8


def build_consts():
    c = np.zeros((128, NCONST), np.float32)
    c[:, C_ID:C_ID + 128] = np.eye(128, dtype=np.float32)
    p = np.arange(64)[:, None]
    f = np.arange(64)[None, :]
    c[:64, C_TRI:C_TRI + 64] = (f >= p).astype(np.float32)
    c[:64, C_NEGU:C_NEGU + 64] = np.where(f >= p, 0.0, NEG)
    c[:64, C_POSL:C_POSL + 64] = np.where(f < p, 0.0, -NEG)
    for gi, w in enumerate(WINS):
        for t in range(15):
            c[:, C_INVC + gi * 15 + t] = 1.0 / min(w, t + 1)
    same = (p // 8) == (f // 8)
    c[:64, C_TRI8:C_TRI8 + 64] = (same & (f >= p)).astype(np.float32)
    c[:64, C_NEGU8:C_NEGU8 + 64] = np.where(same & (f >= p), 0.0, NEG)
    c[:64, C_POSL8:C_POSL8 + 64] = np.where(same & (f < p), 0.0, -NEG)
    c[:64, C_SEL:C_SEL + 64] = (p == 8 * (f // 8) + 7).astype(np.float32)
    c[:64, C_RM:C_RM + 8] = ((p // 8) == np.arange(8)[None, :]).astype(np.float32)
    return c


def build_vecs(inp):
    v = np.zeros((128, NV), np.float32)

    def put(name, arr):
        a = np.asarray(arr, np.float32).reshape(-1, 128).T
        v[:, VEC_COLS[name]:VEC_COLS[name] + a.shape[1]] = a

    put("nm0", inp["norm_mix"][0]); put("nm1", inp["norm_mix"][1])
    put("nf0", inp["norm_ffn"][0]); put("nf1", inp["norm_ffn"][1])
    put("nfin", inp["norm_final"])
    put("gcw", inp["gdn_conv_w"][0].reshape(-1))
    put("fcw0", inp["ffn_conv_w"][0].reshape(-1)); put("fcw1", inp["ffn_conv_w"][1].reshape(-1))
    put("fcb0", inp["ffn_conv_b"][0]); put("fcb1", inp["ffn_conv_b"][1])
    put("psc", inp["pool_scale"][0])
    put("gnw", inp["gdn_norm_w"][0])
    v[0:8, VEC_COLS["alog"]] = inp["gdn_A_log"][0]
    v[0:8, VEC_COLS["dtb"]] = inp["gdn_dt_bias"][0]
    return v


class Seg:
    def __init__(self, kind, n, off, pos0=0):
        self.kind, self.n, self.off, self.pos0 = kind, n, off, pos0

    def tiles(self):
        if self.kind == "S":
            return [(self, 0, 128)]
        k = (self.n + 511) // 512
        base = (self.n // k + 7) // 8 * 8
        out, t = [], 0
        while t < self.n:
            m = min(base, self.n - t)
            out.append((self, t, m))
            t += m
        return out


class ST:
    def __init__(self, segs, first, last):
        self.segs, self.first, self.last = segs, first, last
        self.NT = sum(s.n for s in segs)

    def tiles(self):
        return [t for s in self.segs for t in s.tiles()]


SUPER = [
    ST([Seg("P", 592, 0, 0), Seg("S", 128, 592)], True, False),
    ST([Seg("P", 704, 0, 592)], False, False),
    ST([Seg("P", 768, 0, 1296)], False, True),
]
NTMAX = 768


class Builder:
    def __init__(self):
        self.nc = nc = bass.Bass("TRN2", target_bir_lowering=False)
        self.P = Prog(nc)
        self.es = contextlib.ExitStack()
        self.toks = {}
        self.rrc = {}
        self.phase_id = 0

        def din(name, shape):
            return nc.dram_tensor(name, list(shape), F32, kind="ExternalInput").ap()

        def dout(name, shape):
            return nc.dram_tensor(name, list(shape), F32, kind="ExternalOutput").ap()

        self.xp = din("xp", [SEQ, D]); self.xs = din("xs", [128, D])
        self.st_conv = din("st_conv", [48, 3072]); self.st_rec = din("st_rec", [16, 8, 128, 128])
        self.st_pool = din("st_pool", [240, D]); self.st_ffn = din("st_ffn", [2, 32, 5632])
        self.meta = din("meta", [NMETA, D])
        self.w_in = din("w_in", [D, 4112]); self.wba = din("wba", [D, 40])
        self.w_out = din("w_out", [D, D]); self.pool_w = din("pool_w", [4, 256, 256])
        self.w_up = din("w_up", [2, D, 5632]); self.w_down = din("w_down", [2, DFF, D])
        self.vecs_d = din("vecs", [128, NV]); self.consts_d = din("consts", [128, NCONST])
        self.yp = dout("yp", [SEQ, D]); self.ys = dout("ys", [128, D])
        self.o_pconv = dout("o_pconv", [3, 3072]); self.o_prec = dout("o_prec", [8, 128, 128])
        self.o_ppool = dout("o_ppool", [15, D]); self.o_pffn = dout("o_pffn", [2, 2, 5632])
        self.o_sconv = dout("o_sconv", [48, 3072]); self.o_srec = dout("o_srec", [16, 8, 128, 128])
        self.o_spool = dout("o_spool", [240, D]); self.o_sffn = dout("o_sffn", [2, 32, 5632])

    def tok(self, *key):
        t = self.toks.get(key)
        if t is None:
            t = self.toks[key] = Tok()
            if key[0] == "bank":
                t.excl = True
        return t

    def sb(self, name, shape, dt):
        return self.es.enter_context(self.nc.sbuf_tensor(name, list(shape), dt))

    def rr(self, name, choices):
        i = self.rrc.get(name, 0)
        self.rrc[name] = i + 1
        return choices[i % len(choices)]

    def bank(self):
        i = self.rrc.get("bank", 0)
        self.rrc["bank"] = i + 1
        i %= 8
        return self.banks[i], self.tok("bank", i)

    def phase(self):
        self.P.barrier()
        self.aoff = 0
        self.phase_id += 1

    def A(self, shape, dt, key=None):
        n = int(np.prod(shape[1:]))
        nb = n * (4 if dt == F32 else 2)
        nb = (nb + 31) // 32 * 32
        ne = nb // 2
        assert self.aoff + ne <= self.arena_n, ("arena overflow", self.aoff, ne, self.arena_n)
        ap = self.arena[0:shape[0], self.aoff:self.aoff + ne]
        self.aoff += ne
        if dt == F32:
            ap = ap.bitcast(F32)
        ap = ap[:, 0:n]
        if len(shape) == 3:
            ap = ap.rearrange("p (a b) -> p a b", b=shape[2])
        elif len(shape) == 4:
            ap = ap.rearrange("p (a b c) -> p a b c", b=shape[2], c=shape[3])
        return ap, self.tok("arena", self.phase_id, self.aoff)

    def mm(self, out, lhsT, rhs, r, w, start=True, stop=True, skip=False):
        if skip:
            self.P.op("pe", lambda e: e.matmul(out, lhsT=lhsT, rhs=rhs, start=start, stop=stop, skip_group_check=True), r, w)
        else:
            self.P.op("pe", lambda e: e.matmul(out, lhsT=lhsT, rhs=rhs, start=start, stop=stop), r, w)

    def tr(self, out, in_, ident, r, w):
        self.P.op("pe", lambda e: e.transpose(out=out, in_=in_, identity=ident), r, w)

    def act(self, out, in_, func, r, w, bias=None, scale=None):
        kw = {}
        if bias is not None:
            kw["bias"] = bias
        if scale is not None:
            kw["scale"] = scale
        self.P.op("act", lambda e: e.activation(out=out, in_=in_, func=func, **kw), r, w)

    def tt(self, eng, out, in0, in1, op, r, w):
        self.P.op(eng, lambda e: e.tensor_tensor(out=out, in0=in0, in1=in1, op=op), r, w)

    def ts(self, eng, out, in0, s1, op0, r, w, s2=None, op1=None):
        if op1 is None:
            self.P.op(eng, lambda e: e.tensor_scalar(out=out, in0=in0, scalar1=s1, scalar2=None, op0=op0), r, w)
        else:
            self.P.op(eng, lambda e: e.tensor_scalar(out=out, in0=in0, scalar1=s1, scalar2=s2, op0=op0, op1=op1), r, w)

    def stt(self, out, in0, scalar, in1, op0, op1, r, w):
        self.P.op("dve", lambda e: e.scalar_tensor_tensor(out=out, in0=in0, scalar=scalar, in1=in1, op0=op0, op1=op1), r, w)

    def cp(self, eng, out, in_, r, w):
        if eng == "act":
            self.act(out, in_, AF.Copy, r, w)
        else:
            self.P.op(eng, lambda e: e.tensor_copy(out=out, in_=in_), r, w)

    def dma(self, q, out, in_, r, w, stream):
        self.P.op(q, lambda e: e.dma_start(out=out, in_=in_), r, w, stream=stream)

    def memset(self, eng, ap, val, w):
        self.P.op(eng, lambda e: e.memset(ap, val), (), w)

    def vcol(self, name, j=0, np_=128):
        c = VEC_COLS[name] + j
        return self.vecs[0:np_, c:c + 1]

    @staticmethod
    def V(seg, ap):
        if seg.kind == "S":
            return ap.rearrange("p (s t) -> p s t", t=8)
        return ap

    @staticmethod
    def ext_dst(seg, buf, H, n):
        if seg.kind == "S":
            return buf[:, 0:16 * (H + 8)].rearrange("p (s w) -> p s w", w=H + 8)[:, :, H:H + 8]
        return buf[:, H:H + n]

    @staticmethod
    def ext_tap(seg, buf, H, j, n):
        if seg.kind == "S":
            return buf[:, 0:16 * (H + 8)].rearrange("p (s w) -> p s w", w=H + 8)[:, :, j:j + 8]
        return buf[:, j:j + n]

    @staticmethod
    def ext_halo(seg, buf, H):
        if seg.kind == "S":
            return buf[:, 0:16 * (H + 8)].rearrange("p (s w) -> p s w", w=H + 8)[:, :, 0:H]
        return buf[:, 0:H]

    @staticmethod
    def ext_tail(seg, buf, H, n):
        if seg.kind == "S":
            return buf[:, 0:16 * (H + 8)].rearrange("p (s w) -> p s w", w=H + 8)[:, :, 8:8 + H]
        return buf[:, n:n + H]

    def build(self):
        nc = self.nc
        with self.es:
            self.xT = self.sb("xT", [128, 8, NTMAX], F32)
            self.S32 = self.sb("S32", [128, 8, 128], F32)
            self.S16 = self.sb("S16", [128, 8, 128], BF16)
            self.HG = self.sb("HG", [128, 24, 3], F32)
            self.HF = self.sb("HF", [128, 2, NFC, 2], F32)
            self.HP = self.sb("HP", [128, 8, 15], F32)
            self.vecs = self.sb("vecs_sb", [128, NV], F32)
            self.cst = self.sb("cst", [128, NCONST], F32)
            self.idb = self.sb("idb", [128, 128], BF16)
            self.ones_m = self.sb("ones_m", [128, 128], BF16)
            self.ones_1 = self.sb("ones_1", [128, 128], BF16)
            self.ones_f = self.sb("ones_f", [64, 128], F32)
            self.nexpA = self.sb("nexpA", [8, 1], F32)
            self.lnq = self.sb("lnq", [128, 1], F32)
            self.banks = [self.es.enter_context(nc.psum_tensor("pb%d" % i, [128, 512], F32)) for i in range(8)]
            rem = nc.sbuf_bytes_remaining - 2048
            self.arena_n = (rem // 2) // 64 * 64
            self.arena = self.sb("arena", [128, self.arena_n], BF16)
            self.aoff = 0
            self.idf = self.cst[:, C_ID:C_ID + 128]
            tC = self.tok("consts")
            self.dma("sp", self.vecs[:], self.vecs_d, (), [tC], "ldc0")
            self.dma("sp", self.cst[:], self.consts_d, (), [tC], "ldc1")
            self.cp("dve", self.idb[:], self.idf, [tC], [tC])
            self.memset("pool", self.ones_m[:], 1.0 / 1024.0, [tC])
            self.memset("pool", self.ones_1[:], 1.0, [tC])
            self.memset("pool", self.ones_f[:], 1.0, [tC])
            self.memset("pool", self.lnq[:], -0.5 * float(np.log(128.0)), [tC])
            self.memset("pool", self.S32[:], 0.0, [self.tok("S32")])
            self.memset("pool", self.S16[:], 0.0, [self.tok("S16")])
            self.memset("pool", self.HG[:], 0.0, [self.tok("HG")])
            self.memset("pool", self.HF[:], 0.0, [self.tok("HF")])
            self.memset("pool", self.HP[:], 0.0, [self.tok("HP")])
            self.act(self.nexpA[:], self.vcol("alog", 0, 8), AF.Exp, [tC], [tC])
            self.ts("dve", self.nexpA[:], self.nexpA[:], -1.0, ALU.mult, [tC], [tC])
            self.tC = tC
            for st in SUPER:
                self.run_super(st)
            self.P.emit()
        return nc

    def run_super(self, st):
        self.load_x(st)
        self.gdn(st)
        self.ffn(st, 0)
        self.pool_mixer(st)
        self.ffn(st, 1)
        self.final(st)

    def xtok(self, dc, tile):
        return self.tok("xT", dc, tile[0].off + tile[1])

    def load_x(self, st):
        if st.first:
            self.phase()
        stg = [self.A([128, 4, D], F32) for _ in range(2)]
        bi = 0
        for seg in st.segs:
            for (_, t0, n) in seg.tiles():
                sg, tsg = stg[bi % 2]
                bi += 1
                nb = (n + 127) // 128
                for b in range(nb):
                    m = min(128, n - b * 128)
                    if seg.kind == "S":
                        self.dma("sp", sg[0:m, b, :], self.xs[0:m, :], (), [tsg], "ldx")
                    else:
                        p0 = seg.pos0 + t0 + b * 128
                        r = 0
                        if p0 < NMETA:
                            k = min(m, NMETA - p0)
                            self.dma("sp", sg[0:k, b, :], self.meta[p0:p0 + k, :], (), [tsg], "ldx")
                            r = k
                        if r < m:
                            a = p0 + r - NMETA
                            self.dma("sp", sg[r:m, b, :], self.xp[a:a + (m - r), :], (), [tsg], "ldx")
                tile = (seg, t0, n)
                c0 = seg.off + t0
                for dc in range(8):
                    pb, tpb = self.bank()
                    for b in range(nb):
                        m = min(128, n - b * 128)
                        self.tr(pb[:, b * 128:b * 128 + m], sg[0:m, b, dc * 128:(dc + 1) * 128], self.idf[0:m, 0:m],
                                [tsg, self.tC], [tpb])
                    self.cp(self.rr("ev", ["act", "dve"]), self.xT[:, dc, c0:c0 + n], pb[:, 0:n], [tpb], [self.xtok(dc, tile)])

    def norm(self, st, wname, dstf, sq2, rsb):
        for tile in st.tiles():
            seg, t0, n = tile
            c0 = seg.off + t0
            sq, tsq = sq2[self.rr("sq2", [0, 1])]
            rs, trs = rsb[self.rr("rsb", [0, 1])]
            pb, tpb = self.bank()
            for dc in range(8):
                xin = self.xT[:, dc, c0:c0 + n]
                self.act(sq[:, dc, 0:n], xin, AF.Square, [self.xtok(dc, tile)], [tsq])
            for dc in range(8):
                self.mm(pb[:, 0:n], self.ones_m[:], sq[:, dc, 0:n], [tsq, self.tC], [tpb], start=(dc == 0), stop=(dc == 7))
            self.act(rs[:, 0:n], pb[:, 0:n], AF.Ln, [tpb], [trs], bias=EPS)
            self.act(rs[:, 0:n], rs[:, 0:n], AF.Exp, [trs], [trs], scale=-0.5)
            for dc in range(8):
                dst, tdst = dstf(dc, tile)
                self.stt(dst, self.xT[:, dc, c0:c0 + n], self.vcol(wname, dc), rs[:, 0:n], ALU.mult, ALU.mult,
                         [self.xtok(dc, tile), trs, self.tC], [tdst])

    def gdn(self, st):
        NT = st.NT
        self.phase()
        hasS = any(s.kind == "S" for s in st.segs)
        xn, _ = self.A([128, 8, NT], BF16)
        QKVZ, _ = self.A([128, 32, NT], BF16)
        GB, tGB = self.A([40, NT], F32)
        mark = self.aoff
        sq2 = [self.A([128, 8, 512], BF16) for _ in range(2)]
        rsb = [self.A([128, 512], F32) for _ in range(2)]
        wsl = [self.A([128, 8, 512], BF16) for _ in range(3)]
        wbat, twba = self.A([128, 8, 40], BF16)
        ext = [self.A([128, 3 + 512], F32) for _ in range(3)]
        acc = [self.A([128, 512], F32) for _ in range(2)]
        sil = [self.A([128, 512], F32) for _ in range(2)]
        sqh = [self.A([128, 512], BF16) for _ in range(3)]
        rin = [self.A([128, 512], F32) for _ in range(2)]
        bat = [self.A([8, 512], F32) for _ in range(4)]
        if hasS:
            SHG, tSHG = self.A([128, 24, 48], F32)
            stg, tstg = self.A([48, 3072], F32)
        self.memset("pool", GB, 0.0, [tGB])
        xnt = lambda dc, tile: (xn[:, dc, tile[0].off + tile[1]:tile[0].off + tile[1] + tile[2]], self.tok("xn", self.phase_id, dc, tile[0].off + tile[1]))
        self.norm(st, "nm0", xnt, sq2, rsb)
        if hasS:
            self.dma("sp", stg, self.st_conv, (), [tstg], "ldst")
            for g in range(3):
                pb, tpb = self.bank()
                for j in range(8):
                    fc = g * 8 + j
                    self.tr(pb[:, j * 48:(j + 1) * 48], stg[0:48, fc * 128:(fc + 1) * 128], self.idf[0:48, 0:48], [tstg, self.tC], [tpb])
                self.cp("act", SHG[:, g * 8:(g + 1) * 8, :], pb[:, 0:384].rearrange("p (a b) -> p a b", b=48), [tpb], [tSHG])
        self.dma("pool", wbat, self.wba.rearrange("(kc p) f -> p kc f", p=128), (), [twba], "ldwba")
        for tile in st.tiles():
            seg, t0, n = tile
            c0 = seg.off + t0
            pb, tpb = self.bank()
            for kc in range(8):
                self.mm(pb[0:40, 0:n], wbat[:, kc, :], xn[:, kc, c0:c0 + n], [twba, xnt(kc, tile)[1]], [tpb], start=(kc == 0), stop=(kc == 7))
            self.act(GB[32:40, c0:c0 + n], pb[32:40, 0:n], AF.Sigmoid, [tpb], [tGB])
            (b1, t1), (b2, t2), (b3, t3), (b4, t4) = bat
            self.ts("dve", b1[:, 0:n], pb[0:8, 0:n], self.vcol("dtb", 0, 8), ALU.add, [tpb, self.tC], [t1])
            self.stt(b2[:, 0:n], b1[:, 0:n], -1.0, b1[:, 0:n], ALU.mult, ALU.max, [t1], [t2])
            self.act(b3[:, 0:n], b2[:, 0:n], AF.Exp, [t2], [t3], scale=-1.0)
            self.act(b4[:, 0:n], b3[:, 0:n], AF.Ln, [t3], [t4], bias=1.0)
            self.stt(b2[:, 0:n], b1[:, 0:n], 0.0, b4[:, 0:n], ALU.max, ALU.add, [t1, t4], [t2])
            self.ts("dve", GB[0:8, c0:c0 + n], b2[:, 0:n], self.nexpA[:, 0:1], ALU.mult, [t2, self.tC], [tGB])
        def ld_win(u):
            wt_, twt_ = wsl[u % 3]
            self.dma("pool", wt_, self.w_in[:, u * 512:(u + 1) * 512].rearrange("(kc p) f -> p kc f", p=128), (), [twt_], "ldw%d" % (u % 3))
        ld_win(0); ld_win(1)
        pend = []
        qk_list = []
        for u in range(8):
            wt, twt = wsl[u % 3]
            if u + 2 < 8:
                ld_win(u + 2)
            for j in range(4):
                fc = u * 4 + j
                kind = fc // 8
                prev_ext = None
                for tile in st.tiles():
                    seg, t0, n = tile
                    c0 = seg.off + t0
                    pb, tpb = self.bank()
                    for kc in range(8):
                        self.mm(pb[:, 0:n], wt[:, kc, j * 128:(j + 1) * 128], xn[:, kc, c0:c0 + n], [twt, xnt(kc, tile)[1]], [tpb],
                                start=(kc == 0), stop=(kc == 7))
                    dst = QKVZ[:, fc, c0:c0 + n]
                    tdst = self.tok("qkvz", self.phase_id, fc, c0)
                    if kind == 3:
                        self.act(self.V(seg, dst), self.V(seg, pb[:, 0:n]), AF.Silu, [tpb], [tdst])
                        continue
                    ex, tex = ext[self.rr("ext", [0, 1, 2])]
                    if seg.kind == "S":
                        self.cp("pool", self.ext_halo(seg, ex, 3), SHG[:, fc, :].rearrange("p (s r) -> p s r", r=3), [tSHG], [tex])
                    elif t0 == 0:
                        self.cp("pool", ex[:, 0:3], self.HG[:, fc, :], [self.tok("HG")], [tex])
                    else:
                        pe_, tpe_, pn = prev_ext
                        self.cp("pool", ex[:, 0:3], pe_[:, pn:pn + 3], [tpe_], [tex])
                    self.cp("act", self.ext_dst(seg, ex, 3, n), self.V(seg, pb[:, 0:n]), [tpb], [tex])
                    prev_ext = (ex, tex, n)
                    if seg.kind == "S":
                        self.cp("pool", SHG[:, fc, :].rearrange("p (s r) -> p s r", r=3), self.ext_tail(seg, ex, 3, n), [tex], [tSHG])
                    elif t0 + n == seg.n:
                        self.cp("pool", self.HG[:, fc, :], ex[:, n:n + 3], [tex], [self.tok("HG")])
                    ac, tac = acc[self.rr("acc", [0, 1])]
                    av = self.V(seg, ac[:, 0:n])
                    self.act(av, self.ext_tap(seg, ex, 3, 0, n), AF.Identity, [tex, self.tC], [tac], scale=self.vcol("gcw", 0 * 24 + fc))
                    for tap in (1, 2, 3):
                        self.stt(av, self.ext_tap(seg, ex, 3, tap, n), self.vcol("gcw", tap * 24 + fc), av, ALU.mult, ALU.add, [tex, tac, self.tC], [tac])

                    def tail(dst=dst, tdst=tdst, ac=ac, tac=tac, n=n):
                        self.act(dst, ac[:, 0:n], AF.Silu, [tac], [tdst])
                    if pend:
                        pend.pop(0)()
                    pend.append(tail)
                    if kind < 2:
                        qk_list.append((kind, dst, tdst, n))
            while pend:
                pend.pop(0)()
            def nstage1(item):
                kind, dst, tdst, n = item
                sh, tsh = sqh[self.rr("sqh", [0, 1, 2])]
                self.tt("dve", sh[:, 0:n], dst, dst, ALU.mult, [tdst], [tsh])
                pb2, tpb2 = self.bank()
                self.mm(pb2[:, 0:n], self.ones_1[:], sh[:, 0:n], [tsh, self.tC], [tpb2])
                return pb2, tpb2

            def nstage2(item, pb2, tpb2):
                kind, dst, tdst, n = item
                ri, tri_ = rin[self.rr("rin", [0, 1])]
                self.act(ri[:, 0:n], pb2[:, 0:n], AF.Ln, [tpb2], [tri_], bias=EPS)
                self.act(ri[:, 0:n], ri[:, 0:n], AF.Exp, [tri_], [tri_], scale=-0.5, bias=(self.lnq[:, 0:1] if kind == 0 else None))
                self.tt("dve", dst, dst, ri[:, 0:n], ALU.mult, [tdst, tri_], [tdst])
            inflight = []
            for item in qk_list:
                inflight.append((item,) + nstage1(item))
                if len(inflight) > 2:
                    nstage2(*inflight.pop(0))
            while inflight:
                nstage2(*inflight.pop(0))
            qk_list = []
        if hasS:
            for fc in range(24):
                pb, tpb = self.bank()
                self.tr(pb[0:48, 0:128], SHG[:, fc, :], self.idf[:, :], [tSHG, self.tC], [tpb])
                self.cp(self.rr("ev", ["act", "dve"]), stg[0:48, fc * 128:(fc + 1) * 128], pb[0:48, 0:128], [tpb], [tstg])
            self.dma("sp", self.o_sconv, stg, [tstg], (), "stst")
        if st.last:
            stg2, tstg2 = self.A([3, 3072], F32)
            for fc in range(24):
                pb, tpb = self.bank()
                self.tr(pb[0:3, 0:128], self.HG[:, fc, :], self.idf[:, :], [self.tok("HG"), self.tC], [tpb])
                self.cp(self.rr("ev", ["act", "dve"]), stg2[0:3, fc * 128:(fc + 1) * 128], pb[0:3, 0:128], [tpb], [tstg2])
            self.dma("sp", self.o_pconv, stg2, [tstg2], (), "stst")

        self.P.barrier()
        self.aoff = mark
        ONT = xn
        self.bfree = list(range(8))
        NA, NC_, NB = 3, 4, 1
        self.want_onacc = False
        TAs = [self.alloc_chunk_bufs("A") for _ in range(NA)]
        CAs = [self.alloc_chunk_bufs("C") for _ in range(NC_)]
        TBs = [self.alloc_chunk_bufs("B") for _ in range(NB)]
        jobs = []
        for seg in st.segs:
            if seg.kind == "P":
                c = 0
                if seg.pos0 == 0:
                    jobs.append(("P", seg.off, NMETA, None))
                    c = NMETA
                while c < seg.n:
                    jobs.append(("P", seg.off + c, 64, None))
                    c += 64
            else:
                for b_ in range(2):
                    jobs.append(("SB", seg.off + 64 * b_, 64, 8 * b_))
        N = len(jobs)
        pj = [ji for ji in range(N) if jobs[ji][0] == "P"]
        nP = len(pj)
        gbt_all, tpre = self.A([64, nP * 40], F32)
        Gt_all, _ = self.A([64, nP * 8], F32)
        eG_all, _ = self.A([64, nP * 8], F32)
        nb_all, _ = self.A([64, nP * 8], F32)
        nbG_all, _ = self.A([64, nP * 8], F32)
        self.memset("pool", gbt_all, 0.0, [tpre])
        pgb, tpgb, ipgb = self.bacq()
        for k, ji in enumerate(pj):
            _, c0_, L_, _ = jobs[ji]
            self.tr(pgb[0:L_, k * 40:(k + 1) * 40], GB[0:40, c0_:c0_ + L_], self.idf[0:40, 0:40], [tGB, self.tC], [tpgb])
        k = 0
        while k < nP:
            k2 = k
            while k2 < nP and jobs[pj[k2]][2] == jobs[pj[k]][2]:
                k2 += 1
            L_ = jobs[pj[k]][2]
            self.cp("dve", gbt_all[0:L_, k * 40:k2 * 40], pgb[0:L_, k * 40:k2 * 40], [tpgb], [tpre])
            k = k2
        self.brel(ipgb)
        g3 = gbt_all.rearrange("p (k c) -> p k c", c=40)
        pgc, tpgc, ipgc = self.bacq()
        self.mm(pgc[0:64, 0:nP * 8].rearrange("p (k c) -> p k c", c=8), self.cst[0:64, C_TRI:C_TRI + 64], g3[:, :, 0:8], [tpre, self.tC], [tpgc])
        self.cp("dve", Gt_all, pgc[0:64, 0:nP * 8], [tpgc], [tpre])
        self.brel(ipgc)
        self.act(eG_all, Gt_all, AF.Exp, [tpre], [tpre])
        self.ts("dve", nb_all.rearrange("p (k c) -> p k c", c=8), g3[:, :, 32:40], -1.0, ALU.mult, [tpre], [tpre])
        self.tt("dve", nbG_all, eG_all, nb_all, ALU.mult, [tpre], [tpre])
        pres = {}
        for k, ji in enumerate(pj):
            L_ = jobs[ji][2]
            pres[ji] = (gbt_all[0:L_, k * 40:k * 40 + 8], gbt_all[0:L_, k * 40 + 32:k * 40 + 40], Gt_all[0:L_, k * 8:(k + 1) * 8],
                        nb_all[0:L_, k * 8:(k + 1) * 8], nbG_all[0:L_, k * 8:(k + 1) * 8], tpre)
        nextA = 0
        nextB = 0
        doneA = set()
        actA = {}
        actB = None
        while nextB < N:
            for slot in range(NA):
                if slot not in actA and nextA < N and nextA < nextB + NC_:
                    actA[slot] = (nextA, self.chunk_A(jobs[nextA], QKVZ, GB, tGB, TAs[slot], CAs[nextA % NC_], pres.get(nextA)))
                    nextA += 1
            if actB is None and nextB in doneA:
                if jobs[nextB][0] == "SB":
                    nextB += 1
                    continue
                actB = self.chunk_B(jobs[nextB], CAs[nextB % NC_], TBs[nextB % NB], QKVZ, ONT)
            if actB is not None:
                try:
                    next(actB)
                except StopIteration:
                    actB = None
                    nextB += 1
            for slot in list(actA):
                j, g = actA[slot]
                try:
                    next(g)
                except StopIteration:
                    doneA.add(j)
                    del actA[slot]
        if hasS:
            self.P.barrier()
            onaccs = [self.A([64, 1024], F32) for _ in range(2)]
            save_off = self.aoff
            self.aoff = mark
            NSQ = 3
            assert all((ji % NC_) >= 2 for ji in range(N) if jobs[ji][0] == "SB")
            SBs = []
            for _ in range(NSQ):
                d = {}
                d["S32"] = self.A([128, 8, 128], F32); d["S16"] = self.A([128, 8, 128], BF16)
                d["Stmp"] = self.A([128, 1024], F32); d["vn"] = self.A([64, 1024], BF16)
                SBs.append(d)
            sb_jobs = [(ji, jobs[ji]) for ji in range(N) if jobs[ji][0] == "SB"]
            todo = [(bi, ji, job, s_) for bi, (ji, job) in enumerate(sb_jobs) for s_ in range(8)]
            remaining = {bi: 8 for bi in range(len(sb_jobs))}
            act = {}
            fin = []
            while todo or act or fin:
                for slot in range(NSQ):
                    if slot not in act and todo:
                        bi, ji, job, s_ = todo.pop(0)
                        act[slot] = (bi, self.sample_seq(job, s_, CAs[ji % NC_], SBs[slot], onaccs[bi]))
                for slot in list(act):
                    bi, g = act[slot]
                    try:
                        next(g)
                    except StopIteration:
                        del act[slot]
                        remaining[bi] -= 1
                        if remaining[bi] == 0:
                            ji, job = sb_jobs[bi]
                            fin.append(self.sample_finish(job, TBs[0], onaccs[bi], QKVZ, ONT))
                for g in list(fin):
                    try:
                        next(g)
                    except StopIteration:
                        fin.remove(g)
            self.aoff = max(save_off, self.aoff)
        if st.last:
            self.dma("sp", self.o_prec.rearrange("h k v -> k h v"), self.S32[:], [self.tok("S32")], (), "strec")

        self.P.barrier()
        self.aoff = mark
        wo, two = self.A([128, 8, D], BF16)
        self.dma("pool", wo[:, :, 0:512], self.w_out[:, 0:512].rearrange("(kc p) f -> p kc f", p=128), (), [two], "ldw0")
        self.dma("pool", wo[:, :, 512:1024], self.w_out[:, 512:1024].rearrange("(kc p) f -> p kc f", p=128), (), [two], "ldw1")
        for tile in st.tiles():
            seg, t0, n = tile
            c0 = seg.off + t0
            for dc in range(8):
                pb, tpb = self.bank()
                for kc in range(8):
                    self.mm(pb[:, 0:n], wo[:, kc, dc * 128:(dc + 1) * 128], ONT[:, kc, c0:c0 + n], [two], [tpb], start=(kc == 0), stop=(kc == 7))
                xv = self.xT[:, dc, c0:c0 + n]
                self.tt("dve", xv, pb[:, 0:n], xv, ALU.add, [tpb], [self.xtok(dc, tile)])

    def alloc_chunk_bufs(self, which):
        b = {}
        def a(name, shape, dt):
            b[name] = self.A(shape, dt)
        if which == "A":
            a("gbt", [64, 40], F32)
            for nm in ("Gt", "eG", "nbG", "nb", "dGl", "eGl"):
                a(nm, [64, 8], F32)
            a("Dm", [64, 512], F32); a("Du", [64, 512], F32); a("Dl", [64, 512], F32); a("eGbc", [128, 512], F32)
            b["rhsG"] = b["Dl"]
            a("Lneg", [64, 512], BF16); a("M0", [64, 512], BF16)
            a("QTa", [64, 512], BF16); a("QTb", [64, 512], BF16)
            a("kbgn", [64, 1024], BF16)
        elif which == "C":
            a("PQa", [64, 1024], BF16); a("PQb", [64, 1024], BF16); a("At", [64, 512], BF16)
            a("kd", [64, 1024], BF16); a("vb", [64, 1024], BF16)
            a("nWT", [128, 512], BF16); a("qdT", [128, 512], BF16); a("gtc", [128, 64], F32)
        else:
            a("vn", [64, 1024], BF16); a("sqo", [64, 1024], BF16); a("on", [64, 1024], BF16)
            a("Stmp", [128, 1024], F32); a("ss", [64, 8], F32); a("rs", [64, 8], F32)
            if self.want_onacc:
                a("onacc", [64, 1024], F32)
        return b

    def bacq(self):
        if not self.bfree:
            raise RuntimeError("out of PSUM banks")
        i = self.bfree.pop(0)
        return self.banks[i], self.tok("bank", i), i

    def brel(self, i):
        self.bfree.append(i)

    def chunk_A(self, job, QKVZ, GB, tGB, TA, CA, pre=None):
        kind, c0, L, sidx = job
        B = dict(TA); B.update(CA)
        tC = self.tC
        Q = lambda h: QKVZ[:, h, c0:c0 + L]
        K = lambda h: QKVZ[:, 8 + h, c0:c0 + L]
        Vv = lambda h: QKVZ[:, 16 + h, c0:c0 + L]
        h3 = lambda ap: ap.rearrange("p (h l) -> p h l", l=L)
        hd = lambda ap: ap.rearrange("p (h d) -> p h d", d=128)
        W8 = 8 * L
        blk = (kind == "SB")
        cT, cN, cP = (C_TRI8, C_NEGU8, C_POSL8) if blk else (C_TRI, C_NEGU, C_POSL)
        tri = self.cst[0:L, cT:cT + L]
        dGl, tdGl = B["dGl"]; eGl, teGl = B["eGl"]; gtc, tgtc = B["gtc"]
        rhsG, trG = B["rhsG"]
        if pre is not None:
            g_tm, beta, GtL, nbL, nbGL, tpre = pre
            tgbt = tGt = tnb = tnbG = tpre
            self.tt("pool", h3(rhsG[0:L, 0:W8]), tri.unsqueeze(1).to_broadcast([L, 8, L]), g_tm.unsqueeze(2).to_broadcast([L, 8, L]),
                    ALU.mult, [tgbt, tC], [trG])
            yield
            pG, tpG, ipG = self.bacq()
            self.mm(pG[:, 0:W8], self.ones_f[0:L, :], rhsG[0:L, 0:W8], [trG, tC], [tpG])
            yield
        else:
            pg, tpg, ipg = self.bacq()
            self.tr(pg[0:L, 0:40], GB[0:40, c0:c0 + L], self.idf[0:40, 0:40], [tGB, tC], [tpg])
            gbt, tgbt = B["gbt"]
            self.cp("dve", gbt[0:L, :], pg[0:L, 0:40], [tpg], [tgbt])
            self.brel(ipg)
            g_tm = gbt[0:L, 0:8]
            beta = gbt[0:L, 32:40]
            yield
            self.tt("pool", h3(rhsG[0:L, 0:W8]), tri.unsqueeze(1).to_broadcast([L, 8, L]), g_tm.unsqueeze(2).to_broadcast([L, 8, L]),
                    ALU.mult, [tgbt, tC], [trG])
            pg2, tpg2, ipg2 = self.bacq()
            self.mm(pg2[0:L, 0:8], tri, g_tm, [tgbt, tC], [tpg2])
            Gt, tGt = B["Gt"]; eG, teG = B["eG"]; nbG, tnbG = B["nbG"]; nb, tnb = B["nb"]
            self.cp("dve", Gt[0:L, :], pg2[0:L, 0:8], [tpg2], [tGt])
            self.brel(ipg2)
            self.ts("dve", nb[0:L, :], beta, -1.0, ALU.mult, [tgbt], [tnb])
            yield
            pG, tpG, ipG = self.bacq()
            self.mm(pG[:, 0:W8], self.ones_f[0:L, :], rhsG[0:L, 0:W8], [trG, tC], [tpG])
            self.act(eG[0:L, :], Gt[0:L, :], AF.Exp, [tGt], [teG])
            self.tt("dve", nbG[0:L, :], eG[0:L, :], nb[0:L, :], ALU.mult, [teG, tnb], [tnbG])
            GtL, nbL, nbGL = Gt[0:L, :], nb[0:L, :], nbG[0:L, :]
            yield
        if blk:
            Glast = None
            gl4 = pG[:, 0:W8].rearrange("p (h s t) -> p h s t", s=8, t=8)[:, :, :, 7]
        else:
            Glast = h3(pG[:, 0:W8])[:, :, L - 1]
        Dm, tDm = B["Dm"]; Du, tDu = B["Du"]; Dl, tDl = B["Dl"]; eGbc, teGbc = B["eGbc"]
        self.tt("dve", h3(Dm[0:L, 0:W8]), h3(pG[0:L, 0:W8]), GtL.unsqueeze(2).to_broadcast([L, 8, L]), ALU.subtract,
                [tpG, tGt], [tDm])
        if blk:
            pgl, tpgl, ipgl = self.bacq()
            self.mm(pgl[0:L, 0:8], self.cst[0:L, C_SEL:C_SEL + L], GtL, [tGt, tC], [tpgl])
            self.tt("dve", dGl[0:L, :], pgl[0:L, 0:8], GtL, ALU.subtract, [tpgl, tGt], [tdGl])
            self.brel(ipgl)
            self.act(gtc[:, 0:64].rearrange("p (h s) -> p h s", s=8), gl4, AF.Exp, [tpG], [tgtc])
        else:
            self.tt("dve", dGl[0:L, :], Glast[0:L], GtL, ALU.subtract, [tpG, tGt], [tdGl])
            self.act(gtc[:, 0:8], Glast, AF.Exp, [tpG], [tgtc])
        self.act(eGbc[:, 0:W8], pG[:, 0:W8], AF.Exp, [tpG], [teGbc])
        self.brel(ipG)
        pk, tpk, ipk = self.bacq()
        pkb = pk[:].bitcast(BF16)
        for h in range(8):
            self.tr(pkb[0:L, h * 128:(h + 1) * 128], K(h), self.idb[:], [tC], [tpk])
        pv, tpv, ipv = self.bacq()
        pvb = pv[:].bitcast(BF16)
        for h in range(8):
            self.tr(pvb[0:L, h * 128:(h + 1) * 128], Vv(h), self.idb[:], [tC], [tpv])
        yield
        self.act(eGl[0:L, :], dGl[0:L, :], AF.Exp, [tdGl], [teGl])
        negu = self.cst[0:L, cN:cN + L].unsqueeze(1).to_broadcast([L, 8, L])
        posl = self.cst[0:L, cP:cP + L].unsqueeze(1).to_broadcast([L, 8, L])
        self.tt("pool", h3(Du[0:L, 0:W8]), h3(Dm[0:L, 0:W8]), negu, ALU.add, [tDm, tC], [tDu])
        self.tt("pool", h3(Dl[0:L, 0:W8]), h3(Dm[0:L, 0:W8]), posl, ALU.add, [tDm, tC], [tDl])
        kbgn, tkb = B["kbgn"]; kd, tkd = B["kd"]; vb, tvb = B["vb"]
        self.tt("dve", hd(kbgn[0:L, :]), hd(pkb[0:L, :]), nbGL.unsqueeze(2).to_broadcast([L, 8, 128]), ALU.mult, [tpk, tnbG], [tkb])
        self.tt("dve", hd(vb[0:L, :]), hd(pvb[0:L, :]), beta.unsqueeze(2).to_broadcast([L, 8, 128]), ALU.mult, [tpv, tgbt], [tvb])
        self.brel(ipv)
        yield
        self.tt("dve", hd(kd[0:L, :]), hd(pkb[0:L, :]), eGl[0:L, :].unsqueeze(2).to_broadcast([L, 8, 128]), ALU.mult, [tpk, teGl], [tkd])
        self.brel(ipk)
        self.act(Du[0:L, 0:W8], Du[0:L, 0:W8], AF.Exp, [tDu], [tDu])
        self.act(Dl[0:L, 0:W8], Dl[0:L, 0:W8], AF.Exp, [tDl], [tDl], scale=-1.0)
        pkk, tpkk, ipkk = self.bacq()
        for h in range(8):
            self.mm(pkk[0:L, h * L:(h + 1) * L], K(h), K(h), [], [tpkk])
        pkq, tpkq, ipkq = self.bacq()
        for h in range(8):
            self.mm(pkq[0:L, h * L:(h + 1) * L], K(h), Q(h), [], [tpkq])
        qdT, tqd = B["qdT"]
        self.tt("pool", h3(qdT[:, 0:W8]), QKVZ[:, 0:8, c0:c0 + L], h3(eGbc[:, 0:W8]), ALU.mult, [teGbc], [tqd])
        yield
        self.tt("pool", h3(Dl[0:L, 0:W8]), h3(Dl[0:L, 0:W8]), nbL.unsqueeze(2).to_broadcast([L, 8, L]), ALU.mult,
                [tDl, tnb], [tDl])
        Lneg, tLn = B["Lneg"]; At, tAt = B["At"]; M0, tM0 = B["M0"]
        self.tt("dve", At[0:L, 0:W8], pkq[0:L, 0:W8], Du[0:L, 0:W8], ALU.mult, [tpkq, tDu], [tAt])
        self.brel(ipkq)
        yield
        self.tt("dve", Lneg[0:L, 0:W8], pkk[0:L, 0:W8], Dl[0:L, 0:W8], ALU.mult, [tpkk, tDl], [tLn])
        self.brel(ipkk)
        yield
        pm, tpm, ipm = self.bacq()
        pmb = pm[:].bitcast(BF16)
        for h in range(8):
            self.tr(pmb[0:L, h * L:(h + 1) * L], Lneg[0:L, h * L:(h + 1) * L], self.idb[0:L, 0:L], [tLn, tC], [tpm])
        self.cp("act", M0[0:L, 0:W8], pmb[0:L, 0:W8], [tpm], [tM0])
        self.brel(ipm)
        yield
        nlev = 3 if blk else {64: 6, 16: 4, 8: 3}[L]
        idbL = self.idb[0:L, 0:L]
        PQ = [B["PQa"], B["PQb"]]
        QTbufs = [B["QTa"], B["QTb"]]
        pq3 = lambda ap: ap[0:L, :].rearrange("p (h c) -> p h c", c=128)
        cur = 0
        Pc, tPc = PQ[cur]
        self.tt("pool", pq3(Pc)[:, :, 0:L], h3(M0[0:L, 0:W8]), idbL.unsqueeze(1).to_broadcast([L, 8, L]), ALU.add, [tM0, tC], [tPc])
        pq, tpq, ipq = self.bacq()
        for h in range(8):
            sl = slice(h * L, (h + 1) * L)
            self.mm(pq[0:L, sl], M0[0:L, sl], Lneg[0:L, sl], [tM0, tLn], [tpq])
        QTc = QTbufs[0]
        self.cp("act", QTc[0][0:L, 0:W8], pq[0:L, 0:W8], [tpq], [QTc[1]])
        self.brel(ipq)
        pq2, tpq2, ipq2 = self.bacq()
        for h in range(8):
            sl = slice(h * L, (h + 1) * L)
            self.mm(pq2[0:L, sl], Lneg[0:L, sl], M0[0:L, sl], [tM0, tLn], [tpq2])
        self.cp("dve", pq3(Pc)[:, :, L:2 * L], h3(pq2[0:L, 0:W8]), [tpq2], [tPc])
        self.brel(ipq2)
        yield
        for k in range(1, nlev):
            last = (k == nlev - 1)
            Pn, tPn = PQ[1 - cur]
            wid = L if last else 2 * L
            if not last:
                QTn = QTbufs[k % 2]
                pq, tpq, ipq = self.bacq()
                for h in range(8):
                    self.mm(pq[0:L, h * L:(h + 1) * L], pq3(Pc)[:, h, L:2 * L], QTc[0][0:L, h * L:(h + 1) * L], [tPc, QTc[1]], [tpq])
                self.cp("act", QTn[0][0:L, 0:W8], pq[0:L, 0:W8], [tpq], [QTn[1]])
                self.brel(ipq)
            for half in range(2):
                pp, tpp, ipp = self.bacq()
                for hh in range(4):
                    h = half * 4 + hh
                    self.mm(pp[0:L, hh * 128:hh * 128 + wid], QTc[0][0:L, h * L:(h + 1) * L], pq3(Pc)[:, h, 0:wid], [tPc, QTc[1]], [tpp])
                ppv = pp[0:L, :].rearrange("p (h c) -> p h c", c=128)
                hs = slice(half * 4, half * 4 + 4)
                self.tt("dve", pq3(Pn)[:, hs, 0:L], ppv[:, :, 0:L], pq3(Pc)[:, hs, 0:L], ALU.add, [tpp, tPc], [tPn])
                if not last:
                    self.cp("act", pq3(Pn)[:, hs, L:2 * L], ppv[:, :, L:2 * L], [tpp], [tPn])
                self.brel(ipp)
            cur = 1 - cur
            Pc, tPc = PQ[cur]
            if not last:
                QTc = QTn
            yield
        Ttv = pq3(Pc)
        tTt = tPc
        pw, tpw, ipw = self.bacq()
        for h in range(8):
            self.mm(pw[:, h * L:(h + 1) * L], kbgn[0:L, h * 128:(h + 1) * 128], Ttv[:, h, 0:L], [tkb, tTt], [tpw])
        nWT, tnW = B["nWT"]
        self.cp("act", nWT[:, 0:W8], pw[:, 0:W8], [tpw], [tnW])
        self.brel(ipw)
        CA["Tt"] = (Ttv, tTt)
        yield

    def chunk_B(self, job, CA, TB, QKVZ, ONT):
        kind, c0, L, sidx = job
        B = dict(TB); B.update(CA)
        tC = self.tC
        Tt, tTt = CA["Tt"]
        h3 = lambda ap: ap.rearrange("p (h l) -> p h l", l=L)
        hd = lambda ap: ap.rearrange("p (h d) -> p h d", d=128)
        W8 = 8 * L
        if kind == "S":
            S32, tS32 = self.SS32[sidx % 2]
            S16, tS16 = self.SS16[sidx % 2]
            self.dma("sp", S32, self.st_rec[sidx].rearrange("h k v -> k h v"), (), [tS32], "ldrec%d" % (sidx % 2))
            self.cp("pool", S16, S32, [tS32], [tS16])
        else:
            S32, tS32 = self.S32[:], self.tok("S32")
            S16, tS16 = self.S16[:], self.tok("S16")
        vb, tvb = B["vb"]; nWT, tnW = B["nWT"]; vn, tvn = B["vn"]; qdT, tqd = B["qdT"]; At, tAt = B["At"]
        kd, tkd = B["kd"]; sqo, tsq = B["sqo"]; on, ton = B["on"]; ss, tss = B["ss"]; rs, trs = B["rs"]
        gtc, tgtc = B["gtc"]; Stmp, tSt = B["Stmp"]
        for half in range(2):
            pv, tpv, ipv = self.bacq()
            for hh in range(4):
                h = half * 4 + hh
                self.mm(pv[0:L, hh * 128:(hh + 1) * 128], Tt[:, h, 0:L], vb[0:L, h * 128:(h + 1) * 128], [tTt, tvb], [tpv], start=(hh == 0), stop=False, skip=True)
            for hh in range(4):
                h = half * 4 + hh
                self.mm(pv[0:L, hh * 128:(hh + 1) * 128], nWT[:, h * L:(h + 1) * L], S16[:, h, :], [tnW, tS16], [tpv], start=False, stop=True, skip=True)
            self.cp(("act", "dve")[half], vn[0:L, half * 512:(half + 1) * 512], pv[0:L, :], [tpv], [tvn])
            self.brel(ipv)
        self.tt("pool", hd(Stmp[:, :]), S32, gtc[:, 0:8].unsqueeze(2).to_broadcast([128, 8, 128]), ALU.mult, [tS32, tgtc], [tSt])
        yield
        pss = []
        for half in range(2):
            pS, tpS, ipS = self.bacq()
            pss.append((pS, tpS, ipS))
            for hh in range(4):
                h = half * 4 + hh
                self.mm(pS[:, hh * 128:(hh + 1) * 128], kd[0:L, h * 128:(h + 1) * 128], vn[0:L, h * 128:(h + 1) * 128], [tkd, tvn], [tpS])
        for half in range(2):
            pS, tpS, ipS = pss[half]
            self.tt("dve", S32[:, half * 4:(half + 1) * 4, :], hd(pS[:, :]), hd(Stmp[:, half * 512:(half + 1) * 512]), ALU.add, [tpS, tSt], [tS32])
            self.brel(ipS)
        pos = []
        for half in range(2):
            po, tpo, ipo = self.bacq()
            pos.append((po, tpo, ipo))
            for hh in range(4):
                h = half * 4 + hh
                self.mm(po[0:L, hh * 128:(hh + 1) * 128], qdT[:, h * L:(h + 1) * L], S16[:, h, :], [tqd, tS16], [tpo], start=(hh == 0), stop=False, skip=True)
            for hh in range(4):
                h = half * 4 + hh
                self.mm(po[0:L, hh * 128:(hh + 1) * 128], At[0:L, h * L:(h + 1) * L], vn[0:L, h * 128:(h + 1) * 128], [tAt, tvn], [tpo], start=False, stop=True, skip=True)
        self.cp("act", S16, S32, [tS32], [tS16])
        if kind == "S":
            self.dma("sp", self.o_srec[sidx].rearrange("h k v -> k h v"), S32, [tS32], (), "strec%d" % (sidx % 2))
        yield
        for half in range(2):
            po, tpo, ipo = pos[half]
            self.act(sqo[0:L, half * 512:(half + 1) * 512], po[0:L, :], AF.Square, [tpo], [tsq])
        self.P.op("dve", lambda e, o=ss[0:L, :], i=hd(sqo[0:L, :]): e.tensor_reduce(out=o, in_=i, axis=AX.X, op=ALU.add), [tsq], [tss])
        self.act(rs[0:L, :], ss[0:L, :], AF.Ln, [tss], [trs], bias=EPS, scale=1.0 / 128.0)
        self.act(rs[0:L, :], rs[0:L, :], AF.Exp, [trs], [trs], scale=-0.5)
        yield
        for half in range(2):
            po, tpo, ipo = pos[half]
            self.tt("dve", hd(on[0:L, half * 512:(half + 1) * 512]), hd(po[0:L, :]), rs[0:L, half * 4:(half + 1) * 4].unsqueeze(2).to_broadcast([L, 4, 128]),
                    ALU.mult, [tpo, trs], [ton])
            self.brel(ipo)
        yield
        pt, tpt, ipt = self.bacq()
        ptb = pt[:].bitcast(BF16)
        for h in range(8):
            self.tr(ptb[:, h * L:(h + 1) * L], on[0:L, h * 128:(h + 1) * 128], self.idb[0:L, 0:L], [ton, tC], [tpt])
        self.stt(ONT[:, :, c0:c0 + L], h3(ptb[:, 0:W8]), self.vcol("gnw"), QKVZ[:, 24:32, c0:c0 + L], ALU.mult, ALU.mult, [tpt, tC], [self.tok("ONT", self.phase_id)])
        self.brel(ipt)
        yield

    def sample_seq(self, job, s_, CA, SB, onacc_t):
        kind, c0, L, s0 = job
        tC = self.tC
        Tt, tTt = CA["Tt"]
        hd = lambda ap: ap.rearrange("p (h d) -> p h d", d=128)
        vb, tvb = CA["vb"]; nWT, tnW = CA["nWT"]; qdT, tqd = CA["qdT"]; At, tAt = CA["At"]
        kd, tkd = CA["kd"]; gtc, tgtc = CA["gtc"]
        S32, tS32 = SB["S32"]; S16, tS16 = SB["S16"]; Stmp, tSt = SB["Stmp"]; vn, tvn = SB["vn"]
        onacc, tacc = onacc_t
        gtc3 = gtc[:, 0:64].rearrange("p (h s) -> p h s", s=8)
        sidx = s0 + s_
        rm = self.cst[0:L, C_RM + s_:C_RM + s_ + 1]
        self.dma("sp", S32, self.st_rec[sidx].rearrange("h k v -> k h v"), (), [tS32], "ldrec%d" % (sidx % 3))
        yield
        self.cp("pool", S16, S32, [tS32], [tS16])
        self.tt("pool", hd(Stmp[:, :]), S32, gtc3[:, :, s_].unsqueeze(2).to_broadcast([128, 8, 128]), ALU.mult, [tS32, tgtc], [tSt])
        yield
        for half in range(2):
            pv, tpv, ipv = self.bacq()
            for hh in range(4):
                h = half * 4 + hh
                self.mm(pv[0:L, hh * 128:(hh + 1) * 128], Tt[:, h, 0:L], vb[0:L, h * 128:(h + 1) * 128], [tTt, tvb], [tpv], start=(hh == 0), stop=False, skip=True)
            for hh in range(4):
                h = half * 4 + hh
                self.mm(pv[0:L, hh * 128:(hh + 1) * 128], nWT[:, h * L:(h + 1) * L], S16[:, h, :], [tnW, tS16], [tpv], start=False, stop=True, skip=True)
            if half == 0:
                self.act(vn[0:L, 0:512], pv[0:L, :], AF.Identity, [tpv, tC], [tvn], scale=rm)
            else:
                self.ts("dve", vn[0:L, 512:1024], pv[0:L, :], rm, ALU.mult, [tpv, tC], [tvn])
            self.brel(ipv)
        yield
        pss = []
        for half in range(2):
            pS, tpS, ipS = self.bacq()
            pss.append((pS, tpS, ipS))
            for hh in range(4):
                h = half * 4 + hh
                self.mm(pS[:, hh * 128:(hh + 1) * 128], kd[0:L, h * 128:(h + 1) * 128], vn[0:L, h * 128:(h + 1) * 128], [tkd, tvn], [tpS])
        for half in range(2):
            pS, tpS, ipS = pss[half]
            self.tt("dve", S32[:, half * 4:(half + 1) * 4, :], hd(pS[:, :]), hd(Stmp[:, half * 512:(half + 1) * 512]), ALU.add, [tpS, tSt], [tS32])
            self.brel(ipS)
        self.dma("sp", self.o_srec[sidx].rearrange("h k v -> k h v"), S32, [tS32], (), "strec%d" % (sidx % 3))
        pos = []
        for half in range(2):
            po, tpo, ipo = self.bacq()
            pos.append((po, tpo, ipo))
            for hh in range(4):
                h = half * 4 + hh
                self.mm(po[0:L, hh * 128:(hh + 1) * 128], qdT[:, h * L:(h + 1) * L], S16[:, h, :], [tqd, tS16], [tpo], start=(hh == 0), stop=False, skip=True)
            for hh in range(4):
                h = half * 4 + hh
                self.mm(po[0:L, hh * 128:(hh + 1) * 128], At[0:L, h * L:(h + 1) * L], vn[0:L, h * 128:(h + 1) * 128], [tAt, tvn], [tpo], start=False, stop=True, skip=True)
        yield
        for half in range(2):
            po, tpo, ipo = pos[half]
            acc = onacc[0:L, half * 512:(half + 1) * 512]
            if s_ == 0:
                self.ts("dve", acc, po[0:L, :], rm, ALU.mult, [tpo, tC], [tacc])
            else:
                self.stt(acc, po[0:L, :], rm, acc, ALU.mult, ALU.add, [tpo, tC, tacc], [tacc])
            self.brel(ipo)
        yield

    def sample_finish(self, job, TB, onacc_t, QKVZ, ONT):
        kind, c0, L, s0 = job
        tC = self.tC
        h3 = lambda ap: ap.rearrange("p (h l) -> p h l", l=L)
        hd = lambda ap: ap.rearrange("p (h d) -> p h d", d=128)
        W8 = 8 * L
        sqo, tsq = TB["sqo"]; on, ton = TB["on"]; ss, tss = TB["ss"]; rs, trs = TB["rs"]
        onacc, tacc = onacc_t
        self.act(sqo[0:L, :], onacc[0:L, :], AF.Square, [tacc], [tsq])
        self.P.op("dve", lambda e, o=ss[0:L, :], i=hd(sqo[0:L, :]): e.tensor_reduce(out=o, in_=i, axis=AX.X, op=ALU.add), [tsq], [tss])
        self.act(rs[0:L, :], ss[0:L, :], AF.Ln, [tss], [trs], bias=EPS, scale=1.0 / 128.0)
        self.act(rs[0:L, :], rs[0:L, :], AF.Exp, [trs], [trs], scale=-0.5)
        yield
        self.tt("dve", hd(on[0:L, :]), hd(onacc[0:L, :]), rs[0:L, :].unsqueeze(2).to_broadcast([L, 8, 128]), ALU.mult, [tacc, trs], [ton])
        yield
        pt, tpt, ipt = self.bacq()
        ptb = pt[:].bitcast(BF16)
        for h in range(8):
            self.tr(ptb[:, h * L:(h + 1) * L], on[0:L, h * 128:(h + 1) * 128], self.idb[0:L, 0:L], [ton, tC], [tpt])
        self.stt(ONT[:, :, c0:c0 + L], h3(ptb[:, 0:W8]), self.vcol("gnw"), QKVZ[:, 24:32, c0:c0 + L], ALU.mult, ALU.mult, [tpt, tC], [self.tok("ONT", self.phase_id)])
        self.brel(ipt)
        yield

    def ffn(self, st, l):
        NT = st.NT
        self.phase()
        hasS = any(s.kind == "S" for s in st.segs)
        xn, _ = self.A([128, 8, NT], BF16)
        hT, _ = self.A([128, 22, NT], BF16)
        sq2 = [self.A([128, 8, 512], BF16) for _ in range(2)]
        rsb = [self.A([128, 512], F32) for _ in range(2)]
        wsl = [self.A([128, 2, 8, 256], BF16) for _ in range(3)]
        wsl = [(w_, (t_, self.tok("wslb", self.phase_id, i_))) for i_, (w_, t_) in enumerate(wsl)]
        wdn = [self.A([128, 22, 128], BF16) for _ in range(3)]
        ub = [self.A([128, 2 + 512], F32) for _ in range(4)]
        t0b = [self.A([128, 512], F32) for _ in range(4)]
        sab = [self.A([128, 512], F32) for _ in range(2)]
        if hasS:
            SHF, tSHF = self.A([128, NFC, 32], F32)
            stg, tstg = self.A([32, 5632], F32)
        xnt = lambda dc, tile: (xn[:, dc, tile[0].off + tile[1]:tile[0].off + tile[1] + tile[2]], self.tok("xn", self.phase_id, dc, tile[0].off + tile[1]))
        self.norm(st, "nf%d" % l, xnt, sq2, rsb)
        if hasS:
            self.dma("sp", stg, self.st_ffn[l], (), [tstg], "ldst")
            for g in range(0, NFC, 8):
                pb, tpb = self.bank()
                ng = min(8, NFC - g)
                for j in range(ng):
                    fc = g + j
                    self.tr(pb[:, j * 32:(j + 1) * 32], stg[0:32, fc * 128:(fc + 1) * 128], self.idf[0:32, 0:32], [tstg, self.tC], [tpb])
                self.cp("act", SHF[:, g:g + ng, :], pb[:, 0:ng * 32].rearrange("p (a b) -> p a b", b=32), [tpb], [tSHF])
        tHF = self.tok("HF")
        fcw, fcb = "fcw%d" % l, "fcb%d" % l
        def ld_wup(u):
            wt_, twt_ = wsl[u % 3]
            self.dma("pool", wt_[:, 0], self.w_up[l][:, u * 256:(u + 1) * 256].rearrange("(kc p) f -> p kc f", p=128), (), [twt_[0]], "ldw%d" % (u % 3))
            self.dma("pool", wt_[:, 1], self.w_up[l][:, DFF + u * 256:DFF + (u + 1) * 256].rearrange("(kc p) f -> p kc f", p=128), (), [twt_[1]], "ldwb%d" % (u % 3))

        def ld_wdn(dc):
            wd_, twd_ = wdn[dc % 3]
            self.dma("pool", wd_, self.w_down[l][:, dc * 128:(dc + 1) * 128].rearrange("(i p) d -> p i d", p=128), (), [twd_], "ldwd%d" % (dc % 3))
        ld_wup(0); ld_wup(1)
        pend = []
        for u in range(11):
            wt, twt = wsl[u % 3]
            if u + 2 < 11:
                ld_wup(u + 2)
            elif u + 2 == 11:
                ld_wdn(0)
            else:
                ld_wdn(1)
            for j in range(2):
                i = u * 2 + j
                prev = [None, None]
                for tile in st.tiles():
                    seg, t0, n = tile
                    c0 = seg.off + t0
                    conv = []
                    for ab in range(2):
                        fc = i + 22 * ab
                        pb, tpb = self.bank()
                        for kc in range(8):
                            self.mm(pb[:, 0:n], wt[:, ab, kc, j * 128:(j + 1) * 128], xn[:, kc, c0:c0 + n], [twt[ab], xnt(kc, tile)[1]], [tpb],
                                    start=(kc == 0), stop=(kc == 7))
                        ex, tex = ub[self.rr("ub", [0, 1, 2, 3])]
                        if seg.kind == "S":
                            self.cp("pool", self.ext_halo(seg, ex, 2), SHF[:, fc, :].rearrange("p (s r) -> p s r", r=2), [tSHF], [tex])
                        elif t0 == 0:
                            self.cp("pool", ex[:, 0:2], self.HF[:, l, fc, :], [tHF], [tex])
                        else:
                            pe_, tpe_, pn = prev[ab]
                            self.cp("pool", ex[:, 0:2], pe_[:, pn:pn + 2], [tpe_], [tex])
                        self.cp("act", self.ext_dst(seg, ex, 2, n), self.V(seg, pb[:, 0:n]), [tpb], [tex])
                        prev[ab] = (ex, tex, n)
                        if seg.kind == "S":
                            self.cp("pool", SHF[:, fc, :].rearrange("p (s r) -> p s r", r=2), self.ext_tail(seg, ex, 2, n), [tex], [tSHF])
                        elif t0 + n == seg.n:
                            self.cp("pool", self.HF[:, l, fc, :], ex[:, n:n + 2], [tex], [tHF])
                        tb, ttb = t0b[self.rr("t0b", [0, 1, 2, 3])]
                        tv = self.V(seg, tb[:, 0:n])
                        self.act(tv, self.V(seg, pb[:, 0:n]), AF.Identity, [tpb, self.tC], [ttb], bias=self.vcol(fcb, fc), scale=self.vcol(fcw, 2 * NFC + fc))
                        self.stt(tv, self.ext_tap(seg, ex, 2, 1, n), self.vcol(fcw, 1 * NFC + fc), tv, ALU.mult, ALU.add, [tex, ttb, self.tC], [ttb])
                        self.stt(tv, self.ext_tap(seg, ex, 2, 0, n), self.vcol(fcw, 0 * NFC + fc), tv, ALU.mult, ALU.add, [tex, ttb, self.tC], [ttb])
                        conv.append((tb, ttb))
                    def tail(conv=conv, i=i, c0=c0, n=n):
                        sa, tsa = sab[self.rr("sab", [0, 1])]
                        self.act(sa[:, 0:n], conv[0][0][:, 0:n], AF.Silu, [conv[0][1]], [tsa])
                        self.tt("dve", hT[:, i, c0:c0 + n], sa[:, 0:n], conv[1][0][:, 0:n], ALU.mult, [tsa, conv[1][1]], [self.tok("hT", self.phase_id, i, c0)])
                    if pend:
                        pend.pop(0)()
                    pend.append(tail)
        while pend:
            pend.pop(0)()
        if hasS:
            for fc in range(NFC):
                pb, tpb = self.bank()
                self.tr(pb[0:32, 0:128], SHF[:, fc, :], self.idf[:, :], [tSHF, self.tC], [tpb])
                self.cp(self.rr("ev", ["act", "dve"]), stg[0:32, fc * 128:(fc + 1) * 128], pb[0:32, 0:128], [tpb], [tstg])
            self.dma("sp", self.o_sffn[l], stg, [tstg], (), "stst")
        if st.last:
            stg2, tstg2 = self.A([2, 5632], F32)
            for fc in range(NFC):
                pb, tpb = self.bank()
                self.tr(pb[0:2, 0:128], self.HF[:, l, fc, :], self.idf[:, :], [tHF, self.tC], [tpb])
                self.cp(self.rr("ev", ["act", "dve"]), stg2[0:2, fc * 128:(fc + 1) * 128], pb[0:2, 0:128], [tpb], [tstg2])
            self.dma("sp", self.o_pffn[l], stg2, [tstg2], (), "stst")
        for dc in range(8):
            wd, twd = wdn[dc % 3]
            if dc + 2 < 8:
                ld_wdn(dc + 2)
            for tile in st.tiles():
                seg, t0, n = tile
                c0 = seg.off + t0
                pb, tpb = self.bank()
                for i in range(22):
                    self.mm(pb[:, 0:n], wd[:, i, :], hT[:, i, c0:c0 + n], [twd, self.tok("hT", self.phase_id, i, c0)], [tpb], start=(i == 0), stop=(i == 21))
                xv = self.xT[:, dc, c0:c0 + n]
                self.tt("dve", xv, pb[:, 0:n], xv, ALU.add, [tpb], [self.xtok(dc, tile)])

    def pool_mixer(self, st):
        self.phase()
        sq2 = [self.A([128, 8, 512], BF16) for _ in range(2)]
        rsb = [self.A([128, 512], F32) for _ in range(2)]
        pw, tpw = self.A([128, 4, 2, 256], BF16)
        self.dma("pool", pw, self.pool_w.rearrange("g (ci p) e -> p g ci e", p=128), (), [tpw], "ldw0")
        segbuf = {}
        for seg in st.segs:
            W = 16 * 23 if seg.kind == "S" else 15 + seg.n
            hn, _ = self.A([128, 8, W], F32)
            s1, _ = self.A([128, 2, W], F32)
            s2, _ = self.A([128, 2, W], F32)
            PL, _ = self.A([128, 8, seg.n], BF16)
            segbuf[id(seg)] = (hn, s1, s2, PL, W)
        if any(s.kind == "S" for s in st.segs):
            stg, tstg = self.A([120, 2, D], F32)
            self._pcb = [self.A([128, 120], F32) for _ in range(2)]
        tmp15, ttmp15 = self.A([128, 15], F32)
        tHP = self.tok("HP")

        def dstf(dc, tile):
            seg, t0, n = tile
            hn = segbuf[id(seg)][0]
            if seg.kind == "S":
                ap = hn[:, dc, :].rearrange("p (s w) -> p s w", w=23)[:, :, 15:23]
            else:
                ap = hn[:, dc, 15 + t0:15 + t0 + n]
            return ap, self.tok("hn", self.phase_id, id(seg), dc)

        self._norm_pool(st, "nm1", dstf, sq2, rsb)
        for seg in st.segs:
            hn, s1, s2, PL, W = segbuf[id(seg)]
            n = seg.n
            if seg.kind == "S":
                for half in range(2):
                    self.dma("sp", stg[:, half, :], self.st_pool[half * 120:(half + 1) * 120, :], (), [tstg], "ldst")
                for dc in range(8):
                    pb, tpb = self.bank()
                    for half in range(2):
                        self.tr(pb[:, half * 120:(half + 1) * 120], stg[0:120, half, dc * 128:(dc + 1) * 128], self.idf[0:120, 0:120], [tstg, self.tC], [tpb])
                    self.cp(self.rr("ev", ["act", "dve"]), hn[:, dc, :].rearrange("p (s w) -> p s w", w=23)[:, :, 0:15],
                            pb[:, 0:240].rearrange("p (s r) -> p s r", r=15), [tpb], [self.tok("hn", self.phase_id, id(seg), dc)])
            else:
                for dc in range(8):
                    self.cp("pool", hn[:, dc, 0:15], self.HP[:, dc, :], [tHP], [self.tok("hn", self.phase_id, id(seg), dc)])
            if seg.kind == "S":
                e3 = lambda ap: ap.rearrange("p (s w) -> p s w", w=23)
                sl = lambda ap, a, b: e3(ap)[:, :, a:b]
                WW = 23
            else:
                sl = lambda ap, a, b: ap[:, a:b]
                WW = W
            for dc in range(8):
                gi = dc // 2
                th = self.tok("hn", self.phase_id, id(seg), dc)
                ts1 = self.tok("ps1", self.phase_id, id(seg), dc % 2)
                ts2 = self.tok("ps2", self.phase_id, id(seg), dc % 2)
                src, tsrc = hn[:, dc, :], th
                bufs = [(s1[:, dc % 2, :], ts1), (s2[:, dc % 2, :], ts2)]
                for lev in range(gi + 1):
                    sh = 1 << lev
                    lo = (1 << (lev + 1)) - 1
                    dstb, tdb = bufs[lev % 2]
                    self.tt(("pool", "dve")[dc % 2], sl(dstb, lo, WW), sl(src, lo, WW), sl(src, lo - sh, WW - sh), ALU.add, [tsrc], [tdb])
                    src, tsrc = dstb, tdb
                if seg.kind == "S":
                    outv = PL[:, dc, :].rearrange("p (s t) -> p s t", t=8)
                else:
                    outv = PL[:, dc, :]
                tPL = self.tok("PL", self.phase_id, id(seg), dc)
                self.stt(outv, sl(src, 15, WW), 1.0 / WINS[gi], sl(hn[:, dc, :], 15, WW), ALU.mult, ALU.subtract, [tsrc, th], [tPL])
                if seg.kind == "P" and seg.pos0 == 0:
                    ic = self.cst[:, C_INVC + gi * 15:C_INVC + gi * 15 + 15]
                    self.tt("dve", tmp15, src[:, 15:30], ic, ALU.mult, [tsrc, self.tC], [ttmp15])
                    self.tt("dve", PL[:, dc, 0:15], tmp15, hn[:, dc, 15:30], ALU.subtract, [ttmp15, th], [tPL])
            if seg.kind == "S":
                for dc in range(8):
                    th = self.tok("hn", self.phase_id, id(seg), dc)
                    for half in range(2):
                        pb, tpb = self.bank()
                        src3 = hn[:, dc, :].rearrange("p (s w) -> p s w", w=23)[:, half * 8:(half + 1) * 8, 8:23]
                        cbuf, tcb = self._pcb[self.rr("pcb", [0, 1])]
                        self.cp("pool", cbuf.rearrange("p (s r) -> p s r", r=15), src3, [th], [tcb])
                        self.tr(pb[0:120, 0:128], cbuf, self.idf[:, :], [tcb, self.tC], [tpb])
                        self.cp(self.rr("ev", ["act", "dve"]), stg[0:120, half, dc * 128:(dc + 1) * 128], pb[0:120, 0:128], [tpb], [tstg])
                for half in range(2):
                    self.dma("sp", self.o_spool[half * 120:(half + 1) * 120, :], stg[:, half, :], [tstg], (), "stst")
            else:
                for dc in range(8):
                    th = self.tok("hn", self.phase_id, id(seg), dc)
                    self.cp("pool", self.HP[:, dc, :], hn[:, dc, n:n + 15], [th], [tHP])
                if st.last:
                    stg2, tstg2 = self.A([15, D], F32)
                    for dc in range(8):
                        pb, tpb = self.bank()
                        self.tr(pb[0:15, 0:128], self.HP[:, dc, :], self.idf[:, :], [tHP, self.tC], [tpb])
                        self.cp(self.rr("ev", ["act", "dve"]), stg2[0:15, dc * 128:(dc + 1) * 128], pb[0:15, 0:128], [tpb], [tstg2])
                    self.dma("sp", self.o_ppool, stg2, [tstg2], (), "stst")
            for tile in seg.tiles():
                _, t0, nn = tile
                c0 = seg.off + t0
                for gi in range(4):
                    for eo in range(2):
                        dco = 2 * gi + eo
                        pb, tpb = self.bank()
                        for ci in range(2):
                            self.mm(pb[:, 0:nn], pw[:, gi, ci, eo * 128:(eo + 1) * 128], PL[:, 2 * gi + ci, t0:t0 + nn],
                                    [tpw, self.tok("PL", self.phase_id, id(seg), 2 * gi + ci)], [tpb], start=(ci == 0), stop=(ci == 1))
                        xv = self.xT[:, dco, c0:c0 + nn]
                        self.stt(xv, pb[:, 0:nn], self.vcol("psc", dco), xv, ALU.mult, ALU.add, [tpb, self.tC], [self.xtok(dco, tile)])

    def _norm_pool(self, st, wname, dstf, sq2, rsb):
        for tile in st.tiles():
            seg, t0, n = tile
            c0 = seg.off + t0
            sq, tsq = sq2[self.rr("sq2", [0, 1])]
            rs, trs = rsb[self.rr("rsb", [0, 1])]
            pb, tpb = self.bank()
            for dc in range(8):
                xin = self.xT[:, dc, c0:c0 + n]
                self.act(sq[:, dc, 0:n], xin, AF.Square, [self.xtok(dc, tile)], [tsq])
            for dc in range(8):
                self.mm(pb[:, 0:n], self.ones_m[:], sq[:, dc, 0:n], [tsq, self.tC], [tpb], start=(dc == 0), stop=(dc == 7))
            self.act(rs[:, 0:n], pb[:, 0:n], AF.Ln, [tpb], [trs], bias=EPS)
            self.act(rs[:, 0:n], rs[:, 0:n], AF.Exp, [trs], [trs], scale=-0.5)
            for dc in range(8):
                dst, tdst = dstf(dc, tile)
                self.stt(dst, self.V(seg, self.xT[:, dc, c0:c0 + n]), self.vcol(wname, dc), self.V(seg, rs[:, 0:n]), ALU.mult, ALU.mult,
                         [self.xtok(dc, tile), trs, self.tC], [tdst])

    def final(self, st):
        self.phase()
        sq2 = [self.A([128, 8, 512], BF16) for _ in range(2)]
        rsb = [self.A([128, 512], F32) for _ in range(2)]
        yT = [self.A([128, 8, 512], F32) for _ in range(2)]
        ysg = [self.A([128, D], F32) for _ in range(3)]
        cur = {}

        def dstf(dc, tile):
            return cur["y"][0][:, dc, 0:tile[2]], cur["y"][1]

        for tile in st.tiles():
            seg, t0, n = tile
            cur["y"] = yT[self.rr("yT", [0, 1])]
            self._norm_one(tile, "nfin", dstf, sq2, rsb)
            y, ty = cur["y"]
            b0 = 0
            while b0 < n:
                if seg.kind == "P":
                    pos = seg.pos0 + t0 + b0
                    if pos < NMETA:
                        b0 += NMETA - pos
                        continue
                m = min(128, n - b0)
                sg, tsg = ysg[self.rr("ysg", [0, 1, 2])]
                for half in range(2):
                    pb, tpb = self.bank()
                    for j in range(4):
                        dc = half * 4 + j
                        self.tr(pb[0:m, j * 128:(j + 1) * 128], y[:, dc, b0:b0 + m], self.idf[:, :], [ty, self.tC], [tpb])
                    self.cp(("act", "dve")[half], sg[0:m, half * 512:(half + 1) * 512], pb[0:m, :], [tpb], [tsg])
                if seg.kind == "S":
                    self.dma("sp", self.ys[b0:b0 + m, :], sg[0:m, :], [tsg], (), "sty%d" % ((self.rrc["ysg"] - 1) % 3))
                else:
                    r0 = seg.pos0 + t0 + b0 - NMETA
                    self.dma("sp", self.yp[r0:r0 + m, :], sg[0:m, :], [tsg], (), "sty%d" % ((self.rrc["ysg"] - 1) % 3))
                b0 += m

    def _norm_one(self, tile, wname, dstf, sq2, rsb):
        seg, t0, n = tile
        c0 = seg.off + t0
        sq, tsq = sq2[self.rr("sq2", [0, 1])]
        rs, trs = rsb[self.rr("rsb", [0, 1])]
        pb, tpb = self.bank()
        for dc in range(8):
            xin = self.xT[:, dc, c0:c0 + n]
            self.act(sq[:, dc, 0:n], xin, AF.Square, [self.xtok(dc, tile)], [tsq])
        for dc in range(8):
            self.mm(pb[:, 0:n], self.ones_m[:], sq[:, dc, 0:n], [tsq, self.tC], [tpb], start=(dc == 0), stop=(dc == 7))
        self.act(rs[:, 0:n], pb[:, 0:n], AF.Ln, [tpb], [trs], bias=EPS)
        self.act(rs[:, 0:n], rs[:, 0:n], AF.Exp, [trs], [trs], scale=-0.5)
        for dc in range(8):
            dst, tdst = dstf(dc, tile)
            self.stt(dst, self.xT[:, dc, c0:c0 + n], self.vcol(wname, dc), rs[:, 0:n], ALU.mult, ALU.mult,
                     [self.xtok(dc, tile), trs, self.tC], [tdst])


_NC_CACHE = {}


def _get_nc():
    if "nc" not in _NC_CACHE:
        b = Builder()
        _NC_CACHE["nc"] = b.build()
    return _NC_CACHE["nc"]


def kernel(**inp):
    inp = {k: np.asarray(v) for k, v in inp.items()}
    f = lambda a: np.ascontiguousarray(a, dtype=np.float32)
    nc = _get_nc()
    vecs = build_vecs(inp)
    consts = build_consts()
    w_in = f(inp["gdn_w_in"][0])
    wba = np.zeros((D, 40), np.float32)
    wba[:, 0:8] = w_in[:, 4104:4112]
    wba[:, 32:40] = w_in[:, 4096:4104]
    shared = {
        "meta": f(inp["meta_tokens"]), "w_in": w_in, "wba": wba, "w_out": f(inp["gdn_w_out"][0]),
        "pool_w": f(inp["pool_w"][0]), "w_up": f(inp["ffn_w_up"]), "w_down": f(inp["ffn_w_down"]),
        "vecs": vecs, "consts": consts,
    }
    in_maps = []
    for c in range(8):
        sl = slice(16 * c, 16 * c + 16)
        m = dict(shared)
        m["xp"] = f(inp["x_prompt"][c])
        m["xs"] = f(inp["x_sample"][sl].reshape(128, D))
        m["st_conv"] = f(inp["state_gdn_conv"][0, sl].reshape(48, 3072))
        m["st_rec"] = f(inp["state_gdn_rec"][0, sl])
        m["st_pool"] = f(inp["state_pool"][0, sl].reshape(240, D))
        m["st_ffn"] = f(inp["state_ffn_conv"][:, sl].reshape(2, 32, 5632))
        in_maps.append(m)
    res = run_bass_kernel_spmd(nc, in_maps, core_ids=list(range(8)))
    R = res.results
    g = lambda k: [np.asarray(r[k], dtype=np.float32) for r in R]
    y_prompt = np.stack(g("yp"), 0)
    y_sample = np.concatenate(g("ys"), 0).reshape(128, 8, D)
    p_conv = np.stack(g("o_pconv"), 0)[None]
    p_rec = np.stack(g("o_prec"), 0)[None]
    p_pool = np.stack(g("o_ppool"), 0)[None]
    p_ffn = np.stack(g("o_pffn"), 1)
    s_conv = np.concatenate([a.reshape(16, 3, 3072) for a in g("o_sconv")], 0)[None]
    s_rec = np.concatenate(g("o_srec"), 0)[None]
    s_pool = np.concatenate([a.reshape(16, 15, D) for a in g("o_spool")], 0)[None]
    s_ffn = np.concatenate([a.reshape(2, 16, 2, 5632) for a in g("o_sffn")], 1)
    return (y_prompt, y_sample, p_conv, p_rec, p_pool, p_ffn, s_conv, s_rec, s_pool, s_ffn)
```

```python
import contextlib
import numpy as np
import concourse.bass as bass
import concourse.mybir as mybir
from concourse.bass_utils import run_bass_kernel_spmd

F32 = mybir.dt.float32
BF16 = mybir.dt.bfloat16
ALU = mybir.AluOpType
AF = mybir.ActivationFunctionType
AX = mybir.AxisListType

D = 1024
NH = 8
DFF = 2816
NFC = 44
SEQ = 2048
NMETA = 16
EPS = 1e-6
NEG = -1.0e30
DEBUG_MAP = None
WINS = (2, 4, 8, 16)


class Tok:
    __slots__ = ("lastw", "readers", "excl")

    def __init__(self):
        self.lastw = None
        self.readers = []
        self.excl = False


class Op:
    __slots__ = ("eng", "fn", "deps", "ms", "dma_sem", "dma_val", "is_dma", "where")

    def __init__(self, eng, fn):
        import sys as _s
        f = _s._getframe(3)
        self.where = (f.f_lineno, f.f_back.f_lineno if f.f_back else 0)
        self.eng = eng
        self.fn = fn
        self.deps = []
        self.ms = None
        self.is_dma = False
        self.dma_sem = None
        self.dma_val = 0


class Prog:
    ENGS = ("pe", "act", "dve", "pool", "sp")

    def __init__(self, nc):
        self.nc = nc
        self.ops = {e: [] for e in self.ENGS}
        self.streams = {}
        self.pending = {}

    def barrier(self):
        lasts = [self.ops[e][-1] for e in self.ENGS if self.ops[e]]
        lasts += [st[0] for st in self.streams.values() if st[0] is not None]
        for e in self.ENGS:
            self.pending[e] = list(lasts)

    def op(self, eng, fn, reads=(), writes=(), stream=None):
        o = Op(eng, fn)
        is_dma = stream is not None
        deps = []
        for t in reads:
            if t.lastw is not None:
                deps.append((t.lastw, True))
            if t.excl:
                for r in t.readers:
                    if r.eng != eng:
                        deps.append((r, True))
        for t in writes:
            if t.lastw is not None:
                deps.append((t.lastw, False))
            for r in t.readers:
                deps.append((r, False))
        for d in self.pending.pop(eng, []):
            deps.append((d, True))
        if is_dma:
            o.is_dma = True
            st = self.streams.setdefault(stream, [None, 0])
            if st[0] is not None:
                deps.append((st[0], True))
            st[1] += 1
            o.dma_sem = stream
            o.dma_val = 16 * st[1]
            st[0] = o
        seen = set()
        for d, raw in deps:
            if d is o or id(d) in seen:
                continue
            if (not d.is_dma) and (not is_dma) and d.eng == eng and eng == "pe":
                continue
            seen.add(id(d))
            o.deps.append(d)
        for t in reads:
            t.readers.append(o)
        for t in writes:
            t.lastw = o
            t.readers = []
        self.ops[eng].append(o)
        return o

    def emit(self):
        nc = self.nc
        for e in self.ENGS:
            for o in self.ops[e]:
                for d in o.deps:
                    if not d.is_dma:
                        d.ms = True
        for e in self.ENGS:
            k = 0
            for o in self.ops[e]:
                if o.ms and not o.is_dma:
                    k += 1
                    o.ms = k
        with contextlib.ExitStack() as es:
            esem = {e: es.enter_context(nc.semaphore("s_" + e)) for e in self.ENGS}
            dsem = {k: es.enter_context(nc.semaphore("d_%d" % i)) for i, k in enumerate(self.streams)}
            block = es.enter_context(nc.Block())
            prog = self

            def run(e, engobj):
                seen = {}
                for o in prog.ops[e]:
                    for d in o.deps:
                        if d.is_dma:
                            key, val, sem = ("d", d.dma_sem), d.dma_val, dsem[d.dma_sem]
                        else:
                            key, val, sem = ("e", d.eng), d.ms, esem[d.eng]
                        if seen.get(key, 0) >= val:
                            continue
                        seen[key] = val
                        engobj.wait_ge(sem, val)
                    ins = o.fn(engobj)
                    if DEBUG_MAP is not None:
                        try:
                            DEBUG_MAP[str(ins.ins.name)] = o.where
                        except Exception as ex:
                            DEBUG_MAP["err"] = repr(ex)
                    if o.is_dma:
                        ins.then_inc(dsem[o.dma_sem], 16)
                    elif o.ms:
                        ins.then_inc(esem[e], 1)
                if e == "sp":
                    for k, st in prog.streams.items():
                        engobj.wait_ge(dsem[k], 16 * st[1])

            block.tensor(lambda eng: run("pe", eng))
            block.scalar(lambda eng: run("act", eng))
            block.vector(lambda eng: run("dve", eng))
            block.gpsimd(lambda eng: run("pool", eng))
            block.sync(lambda eng: run("sp", eng))


VEC_COLS = {}


def _vec_layout():
    off = 0
    for name, n in (("nm0", 8), ("nm1", 8), ("nf0", 8), ("nf1", 8), ("nfin", 8),
                    ("gcw", 96), ("fcw0", 132), ("fcw1", 132), ("fcb0", 44), ("fcb1", 44),
                    ("psc", 8), ("gnw", 1), ("alog", 1), ("dtb", 1)):
        VEC_COLS[name] = off
        off += n
    return off


NV = _vec_layout()
C_ID, C_TRI, C_NEGU, C_POSL, C_INVC = 0, 128, 192, 256, 320
C_TRI8, C_NEGU8, C_POSL8, C_SEL, C_RM = 380, 444, 508, 572, 636
NCONST = 636 + 8


def build_consts():
    c = np.zeros((128, NCONST), np.float32)
    c[:, C_ID:C_ID + 128] = np.eye(128, dtype=np.float32)
    p = np.arange(64)[:, None]
    f = np.arange(64)[None, :]
    c[:64, C_TRI:C_TRI + 64] = (f >= p).astype(np.float32)
    c[:64, C_NEGU:C_NEGU + 64] = np.where(f >= p, 0.0, NEG)
    c[:64, C_POSL:C_POSL + 64] = np.where(f < p, 0.0, -NEG)
    for gi, w in enumerate(WINS):
        for t in range(15):
            c[:, C_INVC + gi * 15 + t] = 1.0 / min(w, t + 1)
    same = (p // 8) == (f // 8)
    c[:64, C_TRI8:C_TRI8 + 64] = (same & (f >= p)).astype(np.float32)
    c[:64, C_NEGU8:C_NEGU8 + 64] = np.where(same & (f >= p), 0.0, NEG)
    c[:64, C_POSL8:C_POSL8 + 64] = np.where(same & (f < p), 0.0, -NEG)
    c[:64, C_SEL:C_SEL + 64] = (p == 8 * (f // 8) + 7).astype(np.float32)
    c[:64, C_RM:C_RM + 8] = ((p // 8) == np.arange(8)[None, :]).astype(np.float32)
    return c


def build_vecs(inp):
    v = np.zeros((128, NV), np.float32)

    def put(name, arr):
        a = np.asarray(arr, np.float32).reshape(-1, 128).T
        v[:, VEC_COLS[name]:VEC_COLS[name] + a.shape[1]] = a

    put("nm0", inp["norm_mix"][0]); put("nm1", inp["norm_mix"][1])
    put("nf0", inp["norm_ffn"][0]); put("nf1", inp["norm_ffn"][1])
    put("nfin", inp["norm_final"])
    put("gcw", inp["gdn_conv_w"][0].reshape(-1))
    put("fcw0", inp["ffn_conv_w"][0].reshape(-1)); put("fcw1", inp["ffn_conv_w"][1].reshape(-1))
    put("fcb0", inp["ffn_conv_b"][0]); put("fcb1", inp["ffn_conv_b"][1])
    put("psc", inp["pool_scale"][0])
    put("gnw", inp["gdn_norm_w"][0])
    v[0:8, VEC_COLS["alog"]] = inp["gdn_A_log"][0]
    v[0:8, VEC_COLS["dtb"]] = inp["gdn_dt_bias"][0]
    return v


class Seg:
    def __init__(self, kind, n, off, pos0=0):
        self.kind, self.n, self.off, self.pos0 = kind, n, off, pos0

    def tiles(self):
        if self.kind == "S":
            return [(self, 0, 128)]
        k = (self.n + 511) // 512
        base = (self.n // k + 7) // 8 * 8
        out, t = [], 0
        while t < self.n:
            m = min(base, self.n - t)
            out.append((self, t, m))
            t += m
        return out


class ST:
    def __init__(self, segs, first, last):
        self.segs, self.first, self.last = segs, first, last
        self.NT = sum(s.n for s in segs)

    def tiles(self):
        return [t for s in self.segs for t in s.tiles()]


SUPER = [
    ST([Seg("P", 592, 0, 0), Seg("S", 128, 592)], True, False),
    ST([Seg("P", 704, 0, 592)], False, False),
    ST([Seg("P", 768, 0, 1296)], False, True),
]
NTMAX = 768


class Builder:
    def __init__(self):
        self.nc = nc = bass.Bass("TRN2", target_bir_lowering=False)
        self.P = Prog(nc)
        self.es = contextlib.ExitStack()
        self.toks = {}
        self.rrc = {}
        self.phase_id = 0

        def din(name, shape):
            return nc.dram_tensor(name, list(shape), F32, kind="ExternalInput").ap()

        def dout(name, shape):
            return nc.dram_tensor(name, list(shape), F32, kind="ExternalOutput").ap()

        self.xp = din("xp", [SEQ, D]); self.xs = din("xs", [128, D])
        self.st_conv = din("st_conv", [48, 3072]); self.st_rec = din("st_rec", [16, 8, 128, 128])
        self.st_pool = din("st_pool", [240, D]); self.st_ffn = din("st_ffn", [2, 32, 5632])
        self.meta = din("meta", [NMETA, D])
        self.w_in = din("w_in", [D, 4112]); self.wba = din("wba", [D, 40])
        self.w_out = din("w_out", [D, D]); self.pool_w = din("pool_w", [4, 256, 256])
        self.w_up = din("w_up", [2, D, 5632]); self.w_down = din("w_down", [2, DFF, D])
        self.vecs_d = din("vecs", [128, NV]); self.consts_d = din("consts", [128, NCONST])
        self.yp = dout("yp", [SEQ, D]); self.ys = dout("ys", [128, D])
        self.o_pconv = dout("o_pconv", [3, 3072]); self.o_prec = dout("o_prec", [8, 128, 128])
        self.o_ppool = dout("o_ppool", [15, D]); self.o_pffn = dout("o_pffn", [2, 2, 5632])
        self.o_sconv = dout("o_sconv", [48, 3072]); self.o_srec = dout("o_srec", [16, 8, 128, 128])
        self.o_spool = dout("o_spool", [240, D]); self.o_sffn = dout("o_sffn", [2, 32, 5632])

    def tok(self, *key):
        t = self.toks.get(key)
        if t is None:
            t = self.toks[key] = Tok()
            if key[0] == "bank":
                t.excl = True
        return t

    def sb(self, name, shape, dt):
        return self.es.enter_context(self.nc.sbuf_tensor(name, list(shape), dt))

    def rr(self, name, choices):
        i = self.rrc.get(name, 0)
        self.rrc[name] = i + 1
        return choices[i % len(choices)]

    def bank(self):
        i = self.rrc.get("bank", 0)
        self.rrc["bank"] = i + 1
        i %= 8
        return self.banks[i], self.tok("bank", i)

    def phase(self):
        self.P.barrier()
        self.aoff = 0
        self.phase_id += 1

    def A(self, shape, dt, key=None):
        n = int(np.prod(shape[1:]))
        nb = n * (4 if dt == F32 else 2)
        nb = (nb + 31) // 32 * 32
        ne = nb // 2
        assert self.aoff + ne <= self.arena_n, ("arena overflow", self.aoff, ne, self.arena_n)
        ap = self.arena[0:shape[0], self.aoff:self.aoff + ne]
        self.aoff += ne
        if dt == F32:
            ap = ap.bitcast(F32)
        ap = ap[:, 0:n]
        if len(shape) == 3:
            ap = ap.rearrange("p (a b) -> p a b", b=shape[2])
        elif len(shape) == 4:
            ap = ap.rearrange("p (a b c) -> p a b c", b=shape[2], c=shape[3])
        return ap, self.tok("arena", self.phase_id, self.aoff)

    def mm(self, out, lhsT, rhs, r, w, start=True, stop=True, skip=False):
        if skip:
            self.P.op("pe", lambda e: e.matmul(out, lhsT=lhsT, rhs=rhs, start=start, stop=stop, skip_group_check=True), r, w)
        else:
            self.P.op("pe", lambda e: e.matmul(out, lhsT=lhsT, rhs=rhs, start=start, stop=stop), r, w)

    def tr(self, out, in_, ident, r, w):
        self.P.op("pe", lambda e: e.transpose(out=out, in_=in_, identity=ident), r, w)

    def act(self, out, in_, func, r, w, bias=None, scale=None):
        kw = {}
        if bias is not None:
            kw["bias"] = bias
        if scale is not None:
            kw["scale"] = scale
        self.P.op("act", lambda e: e.activation(out=out, in_=in_, func=func, **kw), r, w)

    def tt(self, eng, out, in0, in1, op, r, w):
        self.P.op(eng, lambda e: e.tensor_tensor(out=out, in0=in0, in1=in1, op=op), r, w)

    def ts(self, eng, out, in0, s1, op0, r, w, s2=None, op1=None):
        if op1 is None:
            self.P.op(eng, lambda e: e.tensor_scalar(out=out, in0=in0, scalar1=s1, scalar2=None, op0=op0), r, w)
        else:
            self.P.op(eng, lambda e: e.tensor_scalar(out=out, in0=in0, scalar1=s1, scalar2=s2, op0=op0, op1=op1), r, w)

    def stt(self, out, in0, scalar, in1, op0, op1, r, w):
        self.P.op("dve", lambda e: e.scalar_tensor_tensor(out=out, in0=in0, scalar=scalar, in1=in1, op0=op0, op1=op1), r, w)

    def cp(self, eng, out, in_, r, w):
        if eng == "act":
            self.act(out, in_, AF.Copy, r, w)
        else:
            self.P.op(eng, lambda e: e.tensor_copy(out=out, in_=in_), r, w)

    def dma(self, q, out, in_, r, w, stream):
        self.P.op(q, lambda e: e.dma_start(out=out, in_=in_), r, w, stream=stream)

    def memset(self, eng, ap, val, w):
        self.P.op(eng, lambda e: e.memset(ap, val), (), w)

    def vcol(self, name, j=0, np_=128):
        c = VEC_COLS[name] + j
        return self.vecs[0:np_, c:c + 1]

    @staticmethod
    def V(seg, ap):
        if seg.kind == "S":
            return ap.rearrange("p (s t) -> p s t", t=8)
        return ap

    @staticmethod
    def ext_dst(seg, buf, H, n):
        if seg.kind == "S":
            return buf[:, 0:16 * (H + 8)].rearrange("p (s w) -> p s w", w=H + 8)[:, :, H:H + 8]
        return buf[:, H:H + n]

    @staticmethod
    def ext_tap(seg, buf, H, j, n):
        if seg.kind == "S":
            return buf[:, 0:16 * (H + 8)].rearrange("p (s w) -> p s w", w=H + 8)[:, :, j:j + 8]
        return buf[:, j:j + n]

    @staticmethod
    def ext_halo(seg, buf, H):
        if seg.kind == "S":
            return buf[:, 0:16 * (H + 8)].rearrange("p (s w) -> p s w", w=H + 8)[:, :, 0:H]
        return buf[:, 0:H]

    @staticmethod
    def ext_tail(seg, buf, H, n):
        if seg.kind == "S":
            return buf[:, 0:16 * (H + 8)].rearrange("p (s w) -> p s w", w=H + 8)[:, :, 8:8 + H]
        return buf[:, n:n + H]

    def build(self):
        nc = self.nc
        with self.es:
            self.xT = self.sb("xT", [128, 8, NTMAX], F32)
            self.S32 = self.sb("S32", [128, 8, 128], F32)
            self.S16 = self.sb("S16", [128, 8, 128], BF16)
            self.HG = self.sb("HG", [128, 24, 3], F32)
            self.HF = self.sb("HF", [128, 2, NFC, 2], F32)
            self.HP = self.sb("HP", [128, 8, 15], F32)
            self.vecs = self.sb("vecs_sb", [128, NV], F32)
            self.cst = self.sb("cst", [128, NCONST], F32)
            self.idb = self.sb("idb", [128, 128], BF16)
            self.ones_m = self.sb("ones_m", [128, 128], BF16)
            self.ones_1 = self.sb("ones_1", [128, 128], BF16)
            self.ones_f = self.sb("ones_f", [64, 128], F32)
            self.nexpA = self.sb("nexpA", [8, 1], F32)
            self.lnq = self.sb("lnq", [128, 1], F32)
            self.banks = [self.es.enter_context(nc.psum_tensor("pb%d" % i, [128, 512], F32)) for i in range(8)]
            rem = nc.sbuf_bytes_remaining - 2048
            self.arena_n = (rem // 2) // 64 * 64
            self.arena = self.sb("arena", [128, self.arena_n], BF16)
            self.aoff = 0
            self.idf = self.cst[:, C_ID:C_ID + 128]
            tC = self.tok("consts")
            self.dma("sp", self.vecs[:], self.vecs_d, (), [tC], "ldc0")
            self.dma("sp", self.cst[:], self.consts_d, (), [tC], "ldc1")
            self.cp("dve", self.idb[:], self.idf, [tC], [tC])
            self.memset("pool", self.ones_m[:], 1.0 / 1024.0, [tC])
            self.memset("pool", self.ones_1[:], 1.0, [tC])
            self.memset("pool", self.ones_f[:], 1.0, [tC])
            self.memset("pool", self.lnq[:], -0.5 * float(np.log(128.0)), [tC])
            self.memset("pool", self.S32[:], 0.0, [self.tok("S32")])
            self.memset("pool", self.S16[:], 0.0, [self.tok("S16")])
            self.memset("pool", self.HG[:], 0.0, [self.tok("HG")])
            self.memset("pool", self.HF[:], 0.0, [self.tok("HF")])
            self.memset("pool", self.HP[:], 0.0, [self.tok("HP")])
            self.act(self.nexpA[:], self.vcol("alog", 0, 8), AF.Exp, [tC], [tC])
            self.ts("dve", self.nexpA[:], self.nexpA[:], -1.0, ALU.mult, [tC], [tC])
            self.tC = tC
            for st in SUPER:
                self.run_super(st)
            self.P.emit()
        return nc

    def run_super(self, st):
        self.load_x(st)
        self.gdn(st)
        self.ffn(st, 0)
        self.pool_mixer(st)
        self.ffn(st, 1)
        self.final(st)

    def xtok(self, dc, tile):
        return self.tok("xT", dc, tile[0].off + tile[1])

    def load_x(self, st):
        if st.first:
            self.phase()
        stg = [self.A([128, 4, D], F32) for _ in range(2)]
        bi = 0
        for seg in st.segs:
            for (_, t0, n) in seg.tiles():
                sg, tsg = stg[bi % 2]
                bi += 1
                nb = (n + 127) // 128
                for b in range(nb):
                    m = min(128, n - b * 128)
                    if seg.kind == "S":
                        self.dma("sp", sg[0:m, b, :], self.xs[0:m, :], (), [tsg], "ldx")
                    else:
                        p0 = seg.pos0 + t0 + b * 128
                        r = 0
                        if p0 < NMETA:
                            k = min(m, NMETA - p0)
                            self.dma("sp", sg[0:k, b, :], self.meta[p0:p0 + k, :], (), [tsg], "ldx")
                            r = k
                        if r < m:
                            a = p0 + r - NMETA
                            self.dma("sp", sg[r:m, b, :], self.xp[a:a + (m - r), :], (), [tsg], "ldx")
                tile = (seg, t0, n)
                c0 = seg.off + t0
                for dc in range(8):
                    pb, tpb = self.bank()
                    for b in range(nb):
                        m = min(128, n - b * 128)
                        self.tr(pb[:, b * 128:b * 128 + m], sg[0:m, b, dc * 128:(dc + 1) * 128], self.idf[0:m, 0:m],
                                [tsg, self.tC], [tpb])
                    self.cp(self.rr("ev", ["act", "dve"]), self.xT[:, dc, c0:c0 + n], pb[:, 0:n], [tpb], [self.xtok(dc, tile)])

    def norm(self, st, wname, dstf, sq2, rsb):
        for tile in st.tiles():
            seg, t0, n = tile
            c0 = seg.off + t0
            sq, tsq = sq2[self.rr("sq2", [0, 1])]
            rs, trs = rsb[self.rr("rsb", [0, 1])]
            pb, tpb = self.bank()
            for dc in range(8):
                xin = self.xT[:, dc, c0:c0 + n]
                self.act(sq[:, dc, 0:n], xin, AF.Square, [self.xtok(dc, tile)], [tsq])
            for dc in range(8):
                self.mm(pb[:, 0:n], self.ones_m[:], sq[:, dc, 0:n], [tsq, self.tC], [tpb], start=(dc == 0), stop=(dc == 7))
            self.act(rs[:, 0:n], pb[:, 0:n], AF.Ln, [tpb], [trs], bias=EPS)
            self.act(rs[:, 0:n], rs[:, 0:n], AF.Exp, [trs], [trs], scale=-0.5)
            for dc in range(8):
                dst, tdst = dstf(dc, tile)
                self.stt(dst, self.xT[:, dc, c0:c0 + n], self.vcol(wname, dc), rs[:, 0:n], ALU.mult, ALU.mult,
                         [self.xtok(dc, tile), trs, self.tC], [tdst])

    def gdn(self, st):
        NT = st.NT
        self.phase()
        hasS = any(s.kind == "S" for s in st.segs)
        xn, _ = self.A([128, 8, NT], BF16)
        QKVZ, _ = self.A([128, 32, NT], BF16)
        GB, tGB = self.A([40, NT], F32)
        mark = self.aoff
        sq2 = [self.A([128, 8, 512], BF16) for _ in range(2)]
        rsb = [self.A([128, 512], F32) for _ in range(2)]
        wsl = [self.A([128, 8, 512], BF16) for _ in range(3)]
        wbat, twba = self.A([128, 8, 40], BF16)
        ext = [self.A([128, 3 + 512], F32) for _ in range(3)]
        acc = [self.A([128, 512], F32) for _ in range(2)]
        sil = [self.A([128, 512], F32) for _ in range(2)]
        sqh = [self.A([128, 512], BF16) for _ in range(3)]
        rin = [self.A([128, 512], F32) for _ in range(2)]
        bat = [self.A([8, 512], F32) for _ in range(4)]
        if hasS:
            SHG, tSHG = self.A([128, 24, 48], F32)
            stg, tstg = self.A([48, 3072], F32)
        self.memset("pool", GB, 0.0, [tGB])
        xnt = lambda dc, tile: (xn[:, dc, tile[0].off + tile[1]:tile[0].off + tile[1] + tile[2]], self.tok("xn", self.phase_id, dc, tile[0].off + tile[1]))
        self.norm(st, "nm0", xnt, sq2, rsb)
        if hasS:
            self.dma("sp", stg, self.st_conv, (), [tstg], "ldst")
            for g in range(3):
                pb, tpb = self.bank()
                for j in range(8):
                    fc = g * 8 + j
                    self.tr(pb[:, j * 48:(j + 1) * 48], stg[0:48, fc * 128:(fc + 1) * 128], self.idf[0:48, 0:48], [tstg, self.tC], [tpb])
                self.cp("act", SHG[:, g * 8:(g + 1) * 8, :], pb[:, 0:384].rearrange("p (a b) -> p a b", b=48), [tpb], [tSHG])
        self.dma("pool", wbat, self.wba.rearrange("(kc p) f -> p kc f", p=128), (), [twba], "ldwba")
        for tile in st.tiles():
            seg, t0, n = tile
            c0 = seg.off + t0
            pb, tpb = self.bank()
            for kc in range(8):
                self.mm(pb[0:40, 0:n], wbat[:, kc, :], xn[:, kc, c0:c0 + n], [twba, xnt(kc, tile)[1]], [tpb], start=(kc == 0), stop=(kc == 7))
            self.act(GB[32:40, c0:c0 + n], pb[32:40, 0:n], AF.Sigmoid, [tpb], [tGB])
            (b1, t1), (b2, t2), (b3, t3), (b4, t4) = bat
            self.ts("dve", b1[:, 0:n], pb[0:8, 0:n], self.vcol("dtb", 0, 8), ALU.add, [tpb, self.tC], [t1])
            self.stt(b2[:, 0:n], b1[:, 0:n], -1.0, b1[:, 0:n], ALU.mult, ALU.max, [t1], [t2])
            self.act(b3[:, 0:n], b2[:, 0:n], AF.Exp, [t2], [t3], scale=-1.0)
            self.act(b4[:, 0:n], b3[:, 0:n], AF.Ln, [t3], [t4], bias=1.0)
            self.stt(b2[:, 0:n], b1[:, 0:n], 0.0, b4[:, 0:n], ALU.max, ALU.add, [t1, t4], [t2])
            self.ts("dve", GB[0:8, c0:c0 + n], b2[:, 0:n], self.nexpA[:, 0:1], ALU.mult, [t2, self.tC], [tGB])
        def ld_win(u):
            wt_, twt_ = wsl[u % 3]
            self.dma("pool", wt_, self.w_in[:, u * 512:(u + 1) * 512].rearrange("(kc p) f -> p kc f", p=128), (), [twt_], "ldw%d" % (u % 3))
        ld_win(0); ld_win(1)
        pend = []
        qk_list = []
        for u in range(8):
            wt, twt = wsl[u % 3]
            if u + 2 < 8:
                ld_win(u + 2)
            for j in range(4):
                fc = u * 4 + j
                kind = fc // 8
                prev_ext = None
                for tile in st.tiles():
                    seg, t0, n = tile
                    c0 = seg.off + t0
                    pb, tpb = self.bank()
                    for kc in range(8):
                        self.mm(pb[:, 0:n], wt[:, kc, j * 128:(j + 1) * 128], xn[:, kc, c0:c0 + n], [twt, xnt(kc, tile)[1]], [tpb],
                                start=(kc == 0), stop=(kc == 7))
                    dst = QKVZ[:, fc, c0:c0 + n]
                    tdst = self.tok("qkvz", self.phase_id, fc, c0)
                    if kind == 3:
                        self.act(self.V(seg, dst), self.V(seg, pb[:, 0:n]), AF.Silu, [tpb], [tdst])
                        continue
                    ex, tex = ext[self.rr("ext", [0, 1, 2])]
                    if seg.kind == "S":
                        self.cp("pool", self.ext_halo(seg, ex, 3), SHG[:, fc, :].rearrange("p (s r) -> p s r", r=3), [tSHG], [tex])
                    elif t0 == 0:
                        self.cp("pool", ex[:, 0:3], self.HG[:, fc, :], [self.tok("HG")], [tex])
                    else:
                        pe_, tpe_, pn = prev_ext
                        self.cp("pool", ex[:, 0:3], pe_[:, pn:pn + 3], [tpe_], [tex])
                    self.cp("act", self.ext_dst(seg, ex, 3, n), self.V(seg, pb[:, 0:n]), [tpb], [tex])
                    prev_ext = (ex, tex, n)
                    if seg.kind == "S":
                        self.cp("pool", SHG[:, fc, :].rearrange("p (s r) -> p s r", r=3), self.ext_tail(seg, ex, 3, n), [tex], [tSHG])
                    elif t0 + n == seg.n:
                        self.cp("pool", self.HG[:, fc, :], ex[:, n:n + 3], [tex], [self.tok("HG")])
                    ac, tac = acc[self.rr("acc", [0, 1])]
                    av = self.V(seg, ac[:, 0:n])
                    self.act(av, self.ext_tap(seg, ex, 3, 0, n), AF.Identity, [tex, self.tC], [tac], scale=self.vcol("gcw", 0 * 24 + fc))
                    for tap in (1, 2, 3):
                        self.stt(av, self.ext_tap(seg, ex, 3, tap, n), self.vcol("gcw", tap * 24 + fc), av, ALU.mult, ALU.add, [tex, tac, self.tC], [tac])

                    def tail(dst=dst, tdst=tdst, ac=ac, tac=tac, n=n):
                        self.act(dst, ac[:, 0:n], AF.Silu, [tac], [tdst])
                    if pend:
                        pend.pop(0)()
                    pend.append(tail)
                    if kind < 2:
                        qk_list.append((kind, dst, tdst, n))
            while pend:
                pend.pop(0)()
            def nstage1(item):
                kind, dst, tdst, n = item
                sh, tsh = sqh[self.rr("sqh", [0, 1, 2])]
                self.tt("dve", sh[:, 0:n], dst, dst, ALU.mult, [tdst], [tsh])
                pb2, tpb2 = self.bank()
                self.mm(pb2[:, 0:n], self.ones_1[:], sh[:, 0:n], [tsh, self.tC], [tpb2])
                return pb2, tpb2

            def nstage2(item, pb2, tpb2):
                kind, dst, tdst, n = item
                ri, tri_ = rin[self.rr("rin", [0, 1])]
                self.act(ri[:, 0:n], pb2[:, 0:n], AF.Ln, [tpb2], [tri_], bias=EPS)
                self.act(ri[:, 0:n], ri[:, 0:n], AF.Exp, [tri_], [tri_], scale=-0.5, bias=(self.lnq[:, 0:1] if kind == 0 else None))
                self.tt("dve", dst, dst, ri[:, 0:n], ALU.mult, [tdst, tri_], [tdst])
            inflight = []
            for item in qk_list:
                inflight.append((item,) + nstage1(item))
                if len(inflight) > 2:
                    nstage2(*inflight.pop(0))
            while inflight:
                nstage2(*inflight.pop(0))
            qk_list = []
        if hasS:
            for fc in range(24):
                pb, tpb = self.bank()
                self.tr(pb[0:48, 0:128], SHG[:, fc, :], self.idf[:, :], [tSHG, self.tC], [tpb])
                self.cp(self.rr("ev", ["act", "dve"]), stg[0:48, fc * 128:(fc + 1) * 128], pb[0:48, 0:128], [tpb], [tstg])
            self.dma("sp", self.o_sconv, stg, [tstg], (), "stst")
        if st.last:
            stg2, tstg2 = self.A([3, 3072], F32)
            for fc in range(24):
                pb, tpb = self.bank()
                self.tr(pb[0:3, 0:128], self.HG[:, fc, :], self.idf[:, :], [self.tok("HG"), self.tC], [tpb])
                self.cp(self.rr("ev", ["act", "dve"]), stg2[0:3, fc * 128:(fc + 1) * 128], pb[0:3, 0:128], [tpb], [tstg2])
            self.dma("sp", self.o_pconv, stg2, [tstg2], (), "stst")

        self.P.barrier()
        self.aoff = mark
        ONT = xn
        self.bfree = list(range(8))
        NA, NC_, NB = 3, 4, 1
        self.want_onacc = False
        TAs = [self.alloc_chunk_bufs("A") for _ in range(NA)]
        CAs = [self.alloc_chunk_bufs("C") for _ in range(NC_)]
        TBs = [self.alloc_chunk_bufs("B") for _ in range(NB)]
        jobs = []
        for seg in st.segs:
            if seg.kind == "P":
                c = 0
                if seg.pos0 == 0:
                    jobs.append(("P", seg.off, NMETA, None))
                    c = NMETA
                while c < seg.n:
                    jobs.append(("P", seg.off + c, 64, None))
                    c += 64
            else:
                for b_ in range(2):
                    jobs.append(("SB", seg.off + 64 * b_, 64, 8 * b_))
        N = len(jobs)
        pj = [ji for ji in range(N) if jobs[ji][0] == "P"]
        nP = len(pj)
        gbt_all, tpre = self.A([64, nP * 40], F32)
        Gt_all, _ = self.A([64, nP * 8], F32)
        eG_all, _ = self.A([64, nP * 8], F32)
        nb_all, _ = self.A([64, nP * 8], F32)
        nbG_all, _ = self.A([64, nP * 8], F32)
        self.memset("pool", gbt_all, 0.0, [tpre])
        pgb, tpgb, ipgb = self.bacq()
        for k, ji in enumerate(pj):
            _, c0_, L_, _ = jobs[ji]
            self.tr(pgb[0:L_, k * 40:(k + 1) * 40], GB[0:40, c0_:c0_ + L_], self.idf[0:40, 0:40], [tGB, self.tC], [tpgb])
        k = 0
        while k < nP:
            k2 = k
            while k2 < nP and jobs[pj[k2]][2] == jobs[pj[k]][2]:
                k2 += 1
            L_ = jobs[pj[k]][2]
            self.cp("dve", gbt_all[0:L_, k * 40:k2 * 40], pgb[0:L_, k * 40:k2 * 40], [tpgb], [tpre])
            k = k2
        self.brel(ipgb)
        g3 = gbt_all.rearrange("p (k c) -> p k c", c=40)
        pgc, tpgc, ipgc = self.bacq()
        self.mm(pgc[0:64, 0:nP * 8].rearrange("p (k c) -> p k c", c=8), self.cst[0:64, C_TRI:C_TRI + 64], g3[:, :, 0:8], [tpre, self.tC], [tpgc])
        self.cp("dve", Gt_all, pgc[0:64, 0:nP * 8], [tpgc], [tpre])
        self.brel(ipgc)
        self.act(eG_all, Gt_all, AF.Exp, [tpre], [tpre])
        self.ts("dve", nb_all.rearrange("p (k c) -> p k c", c=8), g3[:, :, 32:40], -1.0, ALU.mult, [tpre], [tpre])
        self.tt("dve", nbG_all, eG_all, nb_all, ALU.mult, [tpre], [tpre])
        pres = {}
        for k, ji in enumerate(pj):
            L_ = jobs[ji][2]
            pres[ji] = (gbt_all[0:L_, k * 40:k * 40 + 8], gbt_all[0:L_, k * 40 + 32:k * 40 + 40], Gt_all[0:L_, k * 8:(k + 1) * 8],
                        nb_all[0:L_, k * 8:(k + 1) * 8], nbG_all[0:L_, k * 8:(k + 1) * 8], tpre)
        nextA = 0
        nextB = 0
        doneA = set()
        actA = {}
        actB = None
        while nextB < N:
            for slot in range(NA):
                if slot not in actA and nextA < N and nextA < nextB + NC_:
                    actA[slot] = (nextA, self.chunk_A(jobs[nextA], QKVZ, GB, tGB, TAs[slot], CAs[nextA % NC_], pres.get(nextA)))
                    nextA += 1
            if actB is None and nextB in doneA:
                if jobs[nextB][0] == "SB":
                    nextB += 1
                    continue
                actB = self.chunk_B(jobs[nextB], CAs[nextB % NC_], TBs[nextB % NB], QKVZ, ONT)
            if actB is not None:
                try:
                    next(actB)
                except StopIteration:
                    actB = None
                    nextB += 1
            for slot in list(actA):
                j, g = actA[slot]
                try:
                    next(g)
                except StopIteration:
                    doneA.add(j)
                    del actA[slot]
        if hasS:
            self.P.barrier()
            onaccs = [self.A([64, 1024], F32) for _ in range(2)]
            save_off = self.aoff
            self.aoff = mark
            NSQ = 3
            assert all((ji % NC_) >= 2 for ji in range(N) if jobs[ji][0] == "SB")
            SBs = []
            for _ in range(NSQ):
                d = {}
                d["S32"] = self.A([128, 8, 128], F32); d["S16"] = self.A([128, 8, 128], BF16)
                d["Stmp"] = self.A([128, 1024], F32); d["vn"] = self.A([64, 1024], BF16)
                SBs.append(d)
            sb_jobs = [(ji, jobs[ji]) for ji in range(N) if jobs[ji][0] == "SB"]
            todo = [(bi, ji, job, s_) for bi, (ji, job) in enumerate(sb_jobs) for s_ in range(8)]
            remaining = {bi: 8 for bi in range(len(sb_jobs))}
            act = {}
            fin = []
            while todo or act or fin:
                for slot in range(NSQ):
                    if slot not in act and todo:
                        bi, ji, job, s_ = todo.pop(0)
                        act[slot] = (bi, self.sample_seq(job, s_, CAs[ji % NC_], SBs[slot], onaccs[bi]))
                for slot in list(act):
                    bi, g = act[slot]
                    try:
                        next(g)
                    except StopIteration:
                        del act[slot]
                        remaining[bi] -= 1
                        if remaining[bi] == 0:
                            ji, job = sb_jobs[bi]
                            fin.append(self.sample_finish(job, TBs[0], onaccs[bi], QKVZ, ONT))
                for g in list(fin):
                    try:
                        next(g)
                    except StopIteration:
                        fin.remove(g)
            self.aoff = max(save_off, self.aoff)
        if st.last:
            self.dma("sp", self.o_prec.rearrange("h k v -> k h v"), self.S32[:], [self.tok("S32")], (), "strec")

        self.P.barrier()
        self.aoff = mark
        wo, two = self.A([128, 8, D], BF16)
        self.dma("pool", wo[:, :, 0:512], self.w_out[:, 0:512].rearrange("(kc p) f -> p kc f", p=128), (), [two], "ldw0")
        self.dma("pool", wo[:, :, 512:1024], self.w_out[:, 512:1024].rearrange("(kc p) f -> p kc f", p=128), (), [two], "ldw1")
        for tile in st.tiles():
            seg, t0, n = tile
            c0 = seg.off + t0
            for dc in range(8):
                pb, tpb = self.bank()
                for kc in range(8):
                    self.mm(pb[:, 0:n], wo[:, kc, dc * 128:(dc + 1) * 128], ONT[:, kc, c0:c0 + n], [two], [tpb], start=(kc == 0), stop=(kc == 7))
                xv = self.xT[:, dc, c0:c0 + n]
                self.tt("dve", xv, pb[:, 0:n], xv, ALU.add, [tpb], [self.xtok(dc, tile)])

    def alloc_chunk_bufs(self, which):
        b = {}
        def a(name, shape, dt):
            b[name] = self.A(shape, dt)
        if which == "A":
            a("gbt", [64, 40], F32)
            for nm in ("Gt", "eG", "nbG", "nb", "dGl", "eGl"):
                a(nm, [64, 8], F32)
            a("Dm", [64, 512], F32); a("Du", [64, 512], F32); a("Dl", [64, 512], F32); a("eGbc", [128, 512], F32)
            b["rhsG"] = b["Dl"]
            a("Lneg", [64, 512], BF16); a("M0", [64, 512], BF16)
            a("QTa", [64, 512], BF16); a("QTb", [64, 512], BF16)
            a("kbgn", [64, 1024], BF16)
        elif which == "C":
            a("PQa", [64, 1024], BF16); a("PQb", [64, 1024], BF16); a("At", [64, 512], BF16)
            a("kd", [64, 1024], BF16); a("vb", [64, 1024], BF16)
            a("nWT", [128, 512], BF16); a("qdT", [128, 512], BF16); a("gtc", [128, 64], F32)
        else:
            a("vn", [64, 1024], BF16); a("sqo", [64, 1024], BF16); a("on", [64, 1024], BF16)
            a("Stmp", [128, 1024], F32); a("ss", [64, 8], F32); a("rs", [64, 8], F32)
            if self.want_onacc:
                a("onacc", [64, 1024], F32)
        return b

    def bacq(self):
        if not self.bfree:
            raise RuntimeError("out of PSUM banks")
        i = self.bfree.pop(0)
        return self.banks[i], self.tok("bank", i), i

    def brel(self, i):
        self.bfree.append(i)

    def chunk_A(self, job, QKVZ, GB, tGB, TA, CA, pre=None):
        kind, c0, L, sidx = job
        B = dict(TA); B.update(CA)
        tC = self.tC
        Q = lambda h: QKVZ[:, h, c0:c0 + L]
        K = lambda h: QKVZ[:, 8 + h, c0:c0 + L]
        Vv = lambda h: QKVZ[:, 16 + h, c0:c0 + L]
        h3 = lambda ap: ap.rearrange("p (h l) -> p h l", l=L)
        hd = lambda ap: ap.rearrange("p (h d) -> p h d", d=128)
        W8 = 8 * L
        blk = (kind == "SB")
        cT, cN, cP = (C_TRI8, C_NEGU8, C_POSL8) if blk else (C_TRI, C_NEGU, C_POSL)
        tri = self.cst[0:L, cT:cT + L]
        dGl, tdGl = B["dGl"]; eGl, teGl = B["eGl"]; gtc, tgtc = B["gtc"]
        rhsG, trG = B["rhsG"]
        if pre is not None:
            g_tm, beta, GtL, nbL, nbGL, tpre = pre
            tgbt = tGt = tnb = tnbG = tpre
            self.tt("pool", h3(rhsG[0:L, 0:W8]), tri.unsqueeze(1).to_broadcast([L, 8, L]), g_tm.unsqueeze(2).to_broadcast([L, 8, L]),
                    ALU.mult, [tgbt, tC], [trG])
            yield
            pG, tpG, ipG = self.bacq()
            self.mm(pG[:, 0:W8], self.ones_f[0:L, :], rhsG[0:L, 0:W8], [trG, tC], [tpG])
            yield
        else:
            pg, tpg, ipg = self.bacq()
            self.tr(pg[0:L, 0:40], GB[0:40, c0:c0 + L], self.idf[0:40, 0:40], [tGB, tC], [tpg])
            gbt, tgbt = B["gbt"]
            self.cp("dve", gbt[0:L, :], pg[0:L, 0:40], [tpg], [tgbt])
            self.brel(ipg)
            g_tm = gbt[0:L, 0:8]
            beta = gbt[0:L, 32:40]
            yield
            self.tt("pool", h3(rhsG[0:L, 0:W8]), tri.unsqueeze(1).to_broadcast([L, 8, L]), g_tm.unsqueeze(2).to_broadcast([L, 8, L]),
                    ALU.mult, [tgbt, tC], [trG])
            pg2, tpg2, ipg2 = self.bacq()
            self.mm(pg2[0:L, 0:8], tri, g_tm, [tgbt, tC], [tpg2])
            Gt, tGt = B["Gt"]; eG, teG = B["eG"]; nbG, tnbG = B["nbG"]; nb, tnb = B["nb"]
            self.cp("dve", Gt[0:L, :], pg2[0:L, 0:8], [tpg2], [tGt])
            self.brel(ipg2)
            self.ts("dve", nb[0:L, :], beta, -1.0, ALU.mult, [tgbt], [tnb])
            yield
            pG, tpG, ipG = self.bacq()
            self.mm(pG[:, 0:W8], self.ones_f[0:L, :], rhsG[0:L, 0:W8], [trG, tC], [tpG])
            self.act(eG[0:L, :], Gt[0:L, :], AF.Exp, [tGt], [teG])
            self.tt("dve", nbG[0:L, :], eG[0:L, :], nb[0:L, :], ALU.mult, [teG, tnb], [tnbG])
            GtL, nbL, nbGL = Gt[0:L, :], nb[0:L, :], nbG[0:L, :]
            yield
        if blk:
            Glast = None
            gl4 = pG[:, 0:W8].rearrange("p (h s t) -> p h s t", s=8, t=8)[:, :, :, 7]
        else:
            Glast = h3(pG[:, 0:W8])[:, :, L - 1]
        Dm, tDm = B["Dm"]; Du, tDu = B["Du"]; Dl, tDl = B["Dl"]; eGbc, teGbc = B["eGbc"]
        self.tt("dve", h3(Dm[0:L, 0:W8]), h3(pG[0:L, 0:W8]), GtL.unsqueeze(2).to_broadcast([L, 8, L]), ALU.subtract,
                [tpG, tGt], [tDm])
        if blk:
            pgl, tpgl, ipgl = self.bacq()
            self.mm(pgl[0:L, 0:8], self.cst[0:L, C_SEL:C_SEL + L], GtL, [tGt, tC], [tpgl])
            self.tt("dve", dGl[0:L, :], pgl[0:L, 0:8], GtL, ALU.subtract, [tpgl, tGt], [tdGl])
            self.brel(ipgl)
            self.act(gtc[:, 0:64].rearrange("p (h s) -> p h s", s=8), gl4, AF.Exp, [tpG], [tgtc])
        else:
            self.tt("dve", dGl[0:L, :], Glast[0:L], GtL, ALU.subtract, [tpG, tGt], [tdGl])
            self.act(gtc[:, 0:8], Glast, AF.Exp, [tpG], [tgtc])
        self.act(eGbc[:, 0:W8], pG[:, 0:W8], AF.Exp, [tpG], [teGbc])
        self.brel(ipG)
        pk, tpk, ipk = self.bacq()
        pkb = pk[:].bitcast(BF16)
        for h in range(8):
            self.tr(pkb[0:L, h * 128:(h + 1) * 128], K(h), self.idb[:], [tC], [tpk])
        pv, tpv, ipv = self.bacq()
        pvb = pv[:].bitcast(BF16)
        for h in range(8):
            self.tr(pvb[0:L, h * 128:(h + 1) * 128], Vv(h), self.idb[:], [tC], [tpv])
        yield
        self.act(eGl[0:L, :], dGl[0:L, :], AF.Exp, [tdGl], [teGl])
        negu = self.cst[0:L, cN:cN + L].unsqueeze(1).to_broadcast([L, 8, L])
        posl = self.cst[0:L, cP:cP + L].unsqueeze(1).to_broadcast([L, 8, L])
        self.tt("pool", h3(Du[0:L, 0:W8]), h3(Dm[0:L, 0:W8]), negu, ALU.add, [tDm, tC], [tDu])
        self.tt("pool", h3(Dl[0:L, 0:W8]), h3(Dm[0:L, 0:W8]), posl, ALU.add, [tDm, tC], [tDl])
        kbgn, tkb = B["kbgn"]; kd, tkd = B["kd"]; vb, tvb = B["vb"]
        self.tt("dve", hd(kbgn[0:L, :]), hd(pkb[0:L, :]), nbGL.unsqueeze(2).to_broadcast([L, 8, 128]), ALU.mult, [tpk, tnbG], [tkb])
        self.tt("dve", hd(vb[0:L, :]), hd(pvb[0:L, :]), beta.unsqueeze(2).to_broadcast([L, 8, 128]), ALU.mult, [tpv, tgbt], [tvb])
        self.brel(ipv)
        yield
        self.tt("dve", hd(kd[0:L, :]), hd(pkb[0:L, :]), eGl[0:L, :].unsqueeze(2).to_broadcast([L, 8, 128]), ALU.mult, [tpk, teGl], [tkd])
        self.brel(ipk)
        self.act(Du[0:L, 0:W8], Du[0:L, 0:W8], AF.Exp, [tDu], [tDu])
        self.act(Dl[0:L, 0:W8], Dl[0:L, 0:W8], AF.Exp, [tDl], [tDl], scale=-1.0)
        pkk, tpkk, ipkk = self.bacq()
        for h in range(8):
            self.mm(pkk[0:L, h * L:(h + 1) * L], K(h), K(h), [], [tpkk])
        pkq, tpkq, ipkq = self.bacq()
        for h in range(8):
            self.mm(pkq[0:L, h * L:(h + 1) * L], K(h), Q(h), [], [tpkq])
        qdT, tqd = B["qdT"]
        self.tt("pool", h3(qdT[:, 0:W8]), QKVZ[:, 0:8, c0:c0 + L], h3(eGbc[:, 0:W8]), ALU.mult, [teGbc], [tqd])
        yield
        self.tt("pool", h3(Dl[0:L, 0:W8]), h3(Dl[0:L, 0:W8]), nbL.unsqueeze(2).to_broadcast([L, 8, L]), ALU.mult,
                [tDl, tnb], [tDl])
        Lneg, tLn = B["Lneg"]; At, tAt = B["At"]; M0, tM0 = B["M0"]
        self.tt("dve", At[0:L, 0:W8], pkq[0:L, 0:W8], Du[0:L, 0:W8], ALU.mult, [tpkq, tDu], [tAt])
        self.brel(ipkq)
        yield
        self.tt("dve", Lneg[0:L, 0:W8], pkk[0:L, 0:W8], Dl[0:L, 0:W8], ALU.mult, [tpkk, tDl], [tLn])
        self.brel(ipkk)
        yield
        pm, tpm, ipm = self.bacq()
        pmb = pm[:].bitcast(BF16)
        for h in range(8):
            self.tr(pmb[0:L, h * L:(h + 1) * L], Lneg[0:L, h * L:(h + 1) * L], self.idb[0:L, 0:L], [tLn, tC], [tpm])
        self.cp("act", M0[0:L, 0:W8], pmb[0:L, 0:W8], [tpm], [tM0])
        self.brel(ipm)
        yield
        nlev = 3 if blk else {64: 6, 16: 4, 8: 3}[L]
        idbL = self.idb[0:L, 0:L]
        PQ = [B["PQa"], B["PQb"]]
        QTbufs = [B["QTa"], B["QTb"]]
        pq3 = lambda ap: ap[0:L, :].rearrange("p (h c) -> p h c", c=128)
        cur = 0
        Pc, tPc = PQ[cur]
        self.tt("pool", pq3(Pc)[:, :, 0:L], h3(M0[0:L, 0:W8]), idbL.unsqueeze(1).to_broadcast([L, 8, L]), ALU.add, [tM0, tC], [tPc])
        pq, tpq, ipq = self.bacq()
        for h in range(8):
            sl = slice(h * L, (h + 1) * L)
            self.mm(pq[0:L, sl], M0[0:L, sl], Lneg[0:L, sl], [tM0, tLn], [tpq])
        QTc = QTbufs[0]
        self.cp("act", QTc[0][0:L, 0:W8], pq[0:L, 0:W8], [tpq], [QTc[1]])
        self.brel(ipq)
        pq2, tpq2, ipq2 = self.bacq()
        for h in range(8):
            sl = slice(h * L, (h + 1) * L)
            self.mm(pq2[0:L, sl], Lneg[0:L, sl], M0[0:L, sl], [tM0, tLn], [tpq2])
        self.cp("dve", pq3(Pc)[:, :, L:2 * L], h3(pq2[0:L, 0:W8]), [tpq2], [tPc])
        self.brel(ipq2)
        yield
        for k in range(1, nlev):
            last = (k == nlev - 1)
            Pn, tPn = PQ[1 - cur]
            wid = L if last else 2 * L
            if not last:
                QTn = QTbufs[k % 2]
                pq, tpq, ipq = self.bacq()
                for h in range(8):
                    self.mm(pq[0:L, h * L:(h + 1) * L], pq3(Pc)[:, h, L:2 * L], QTc[0][0:L, h * L:(h + 1) * L], [tPc, QTc[1]], [tpq])
                self.cp("act", QTn[0][0:L, 0:W8], pq[0:L, 0:W8], [tpq], [QTn[1]])
                self.brel(ipq)
            for half in range(2):
                pp, tpp, ipp = self.bacq()
                for hh in range(4):
                    h = half * 4 + hh
                    self.mm(pp[0:L, hh * 128:hh * 128 + wid], QTc[0][0:L, h * L:(h + 1) * L], pq3(Pc)[:, h, 0:wid], [tPc, QTc[1]], [tpp])
                ppv = pp[0:L, :].rearrange("p (h c) -> p h c", c=128)
                hs = slice(half * 4, half * 4 + 4)
                self.tt("dve", pq3(Pn)[:, hs, 0:L], ppv[:, :, 0:L], pq3(Pc)[:, hs, 0:L], ALU.add, [tpp, tPc], [tPn])
                if not last:
                    self.cp("act", pq3(Pn)[:, hs, L:2 * L], ppv[:, :, L:2 * L], [tpp], [tPn])
                self.brel(ipp)
            cur = 1 - cur
            Pc, tPc = PQ[cur]
            if not last:
                QTc = QTn
            yield
        Ttv = pq3(Pc)
        tTt = tPc
        pw, tpw, ipw = self.bacq()
        for h in range(8):
            self.mm(pw[:, h * L:(h + 1) * L], kbgn[0:L, h * 128:(h + 1) * 128], Ttv[:, h, 0:L], [tkb, tTt], [tpw])
        nWT, tnW = B["nWT"]
        self.cp("act", nWT[:, 0:W8], pw[:, 0:W8], [tpw], [tnW])
        self.brel(ipw)
        CA["Tt"] = (Ttv, tTt)
        yield

    def chunk_B(self, job, CA, TB, QKVZ, ONT):
        kind, c0, L, sidx = job
        B = dict(TB); B.update(CA)
        tC = self.tC
        Tt, tTt = CA["Tt"]
        h3 = lambda ap: ap.rearrange("p (h l) -> p h l", l=L)
        hd = lambda ap: ap.rearrange("p (h d) -> p h d", d=128)
        W8 = 8 * L
        if kind == "S":
            S32, tS32 = self.SS32[sidx % 2]
            S16, tS16 = self.SS16[sidx % 2]
            self.dma("sp", S32, self.st_rec[sidx].rearrange("h k v -> k h v"), (), [tS32], "ldrec%d" % (sidx % 2))
            self.cp("pool", S16, S32, [tS32], [tS16])
        else:
            S32, tS32 = self.S32[:], self.tok("S32")
            S16, tS16 = self.S16[:], self.tok("S16")
        vb, tvb = B["vb"]; nWT, tnW = B["nWT"]; vn, tvn = B["vn"]; qdT, tqd = B["qdT"]; At, tAt = B["At"]
        kd, tkd = B["kd"]; sqo, tsq = B["sqo"]; on, ton = B["on"]; ss, tss = B["ss"]; rs, trs = B["rs"]
        gtc, tgtc = B["gtc"]; Stmp, tSt = B["Stmp"]
        for half in range(2):
            pv, tpv, ipv = self.bacq()
            for hh in range(4):
                h = half * 4 + hh
                self.mm(pv[0:L, hh * 128:(hh + 1) * 128], Tt[:, h, 0:L], vb[0:L, h * 128:(h + 1) * 128], [tTt, tvb], [tpv], start=(hh == 0), stop=False, skip=True)
            for hh in range(4):
                h = half * 4 + hh
                self.mm(pv[0:L, hh * 128:(hh + 1) * 128], nWT[:, h * L:(h + 1) * L], S16[:, h, :], [tnW, tS16], [tpv], start=False, stop=True, skip=True)
            self.cp(("act", "dve")[half], vn[0:L, half * 512:(half + 1) * 512], pv[0:L, :], [tpv], [tvn])
            self.brel(ipv)
        self.tt("pool", hd(Stmp[:, :]), S32, gtc[:, 0:8].unsqueeze(2).to_broadcast([128, 8, 128]), ALU.mult, [tS32, tgtc], [tSt])
        yield
        pss = []
        for half in range(2):
            pS, tpS, ipS = self.bacq()
            pss.append((pS, tpS, ipS))
            for hh in range(4):
                h = half * 4 + hh
                self.mm(pS[:, hh * 128:(hh + 1) * 128], kd[0:L, h * 128:(h + 1) * 128], vn[0:L, h * 128:(h + 1) * 128], [tkd, tvn], [tpS])
        for half in range(2):
            pS, tpS, ipS = pss[half]
            self.tt("dve", S32[:, half * 4:(half + 1) * 4, :], hd(pS[:, :]), hd(Stmp[:, half * 512:(half + 1) * 512]), ALU.add, [tpS, tSt], [tS32])
            self.brel(ipS)
        pos = []
        for half in range(2):
            po, tpo, ipo = self.bacq()
            pos.append((po, tpo, ipo))
            for hh in range(4):
                h = half * 4 + hh
                self.mm(po[0:L, hh * 128:(hh + 1) * 128], qdT[:, h * L:(h + 1) * L], S16[:, h, :], [tqd, tS16], [tpo], start=(hh == 0), stop=False, skip=True)
            for hh in range(4):
                h = half * 4 + hh
                self.mm(po[0:L, hh * 128:(hh + 1) * 128], At[0:L, h * L:(h + 1) * L], vn[0:L, h * 128:(h + 1) * 128], [tAt, tvn], [tpo], start=False, stop=True, skip=True)
        self.cp("act", S16, S32, [tS32], [tS16])
        if kind == "S":
            self.dma("sp", self.o_srec[sidx].rearrange("h k v -> k h v"), S32, [tS32], (), "strec%d" % (sidx % 2))
        yield
        for half in range(2):
            po, tpo, ipo = pos[half]
            self.act(sqo[0:L, half * 512:(half + 1) * 512], po[0:L, :], AF.Square, [tpo], [tsq])
        self.P.op("dve", lambda e, o=ss[0:L, :], i=hd(sqo[0:L, :]): e.tensor_reduce(out=o, in_=i, axis=AX.X, op=ALU.add), [tsq], [tss])
        self.act(rs[0:L, :], ss[0:L, :], AF.Ln, [tss], [trs], bias=EPS, scale=1.0 / 128.0)
        self.act(rs[0:L, :], rs[0:L, :], AF.Exp, [trs], [trs], scale=-0.5)
        yield
        for half in range(2):
            po, tpo, ipo = pos[half]
            self.tt("dve", hd(on[0:L, half * 512:(half + 1) * 512]), hd(po[0:L, :]), rs[0:L, half * 4:(half + 1) * 4].unsqueeze(2).to_broadcast([L, 4, 128]),
                    ALU.mult, [tpo, trs], [ton])
            self.brel(ipo)
        yield
        pt, tpt, ipt = self.bacq()
        ptb = pt[:].bitcast(BF16)
        for h in range(8):
            self.tr(ptb[:, h * L:(h + 1) * L], on[0:L, h * 128:(h + 1) * 128], self.idb[0:L, 0:L], [ton, tC], [tpt])
        self.stt(ONT[:, :, c0:c0 + L], h3(ptb[:, 0:W8]), self.vcol("gnw"), QKVZ[:, 24:32, c0:c0 + L], ALU.mult, ALU.mult, [tpt, tC], [self.tok("ONT", self.phase_id)])
        self.brel(ipt)
        yield

    def sample_seq(self, job, s_, CA, SB, onacc_t):
        kind, c0, L, s0 = job
        tC = self.tC
        Tt, tTt = CA["Tt"]
        hd = lambda ap: ap.rearrange("p (h d) -> p h d", d=128)
        vb, tvb = CA["vb"]; nWT, tnW = CA["nWT"]; qdT, tqd = CA["qdT"]; At, tAt = CA["At"]
        kd, tkd = CA["kd"]; gtc, tgtc = CA["gtc"]
        S32, tS32 = SB["S32"]; S16, tS16 = SB["S16"]; Stmp, tSt = SB["Stmp"]; vn, tvn = SB["vn"]
        onacc, tacc = onacc_t
        gtc3 = gtc[:, 0:64].rearrange("p (h s) -> p h s", s=8)
        sidx = s0 + s_
        rm = self.cst[0:L, C_RM + s_:C_RM + s_ + 1]
        self.dma("sp", S32, self.st_rec[sidx].rearrange("h k v -> k h v"), (), [tS32], "ldrec%d" % (sidx % 3))
        yield
        self.cp("act", S16, S32, [tS32], [tS16])
        self.tt("pool", hd(Stmp[:, :]), S32, gtc3[:, :, s_].unsqueeze(2).to_broadcast([128, 8, 128]), ALU.mult, [tS32, tgtc], [tSt])
        yield
        for half in range(2):
            pv, tpv, ipv = self.bacq()
            for hh in range(4):
                h = half * 4 + hh
                self.mm(pv[0:L, hh * 128:(hh + 1) * 128], Tt[:, h, 0:L], vb[0:L, h * 128:(h + 1) * 128], [tTt, tvb], [tpv], start=(hh == 0), stop=False, skip=True)
            for hh in range(4):
                h = half * 4 + hh
                self.mm(pv[0:L, hh * 128:(hh + 1) * 128], nWT[:, h * L:(h + 1) * L], S16[:, h, :], [tnW, tS16], [tpv], start=False, stop=True, skip=True)
            if half == 0:
                self.act(vn[0:L, 0:512], pv[0:L, :], AF.Identity, [tpv, tC], [tvn], scale=rm)
            else:
                self.ts("dve", vn[0:L, 512:1024], pv[0:L, :], rm, ALU.mult, [tpv, tC], [tvn])
            self.brel(ipv)
        yield
        pss = []
        for half in range(2):
            pS, tpS, ipS = self.bacq()
            pss.append((pS, tpS, ipS))
            for hh in range(4):
                h = half * 4 + hh
                self.mm(pS[:, hh * 128:(hh + 1) * 128], kd[0:L, h * 128:(h + 1) * 128], vn[0:L, h * 128:(h + 1) * 128], [tkd, tvn], [tpS])
        for half in range(2):
            pS, tpS, ipS = pss[half]
            self.tt("dve", S32[:, half * 4:(half + 1) * 4, :], hd(pS[:, :]), hd(Stmp[:, half * 512:(half + 1) * 512]), ALU.add, [tpS, tSt], [tS32])
            self.brel(ipS)
        self.dma("sp", self.o_srec[sidx].rearrange("h k v -> k h v"), S32, [tS32], (), "strec%d" % (sidx % 3))
        pos = []
        for half in range(2):
            po, tpo, ipo = self.bacq()
            pos.append((po, tpo, ipo))
            for hh in range(4):
                h = half * 4 + hh
                self.mm(po[0:L, hh * 128:(hh + 1) * 128], qdT[:, h * L:(h + 1) * L], S16[:, h, :], [tqd, tS16], [tpo], start=(hh == 0), stop=False, skip=True)
            for hh in range(4):
                h = half * 4 + hh
                self.mm(po[0:L, hh * 128:(hh + 1) * 128], At[0:L, h * L:(h + 1) * L], vn[0:L, h * 128:(h + 1) * 128], [tAt, tvn], [tpo], start=False, stop=True, skip=True)
        yield
        for half in range(2):
            po, tpo, ipo = pos[half]
            acc = onacc[0:L, half * 512:(half + 1) * 512]
            if s_ == 0:
                self.ts("dve", acc, po[0:L, :], rm, ALU.mult, [tpo, tC], [tacc])
            else:
                self.stt(acc, po[0:L, :], rm, acc, ALU.mult, ALU.add, [tpo, tC, tacc], [tacc])
            self.brel(ipo)
        yield

    def sample_finish(self, job, TB, onacc_t, QKVZ, ONT):
        kind, c0, L, s0 = job
        tC = self.tC
        h3 = lambda ap: ap.rearrange("p (h l) -> p h l", l=L)
        hd = lambda ap: ap.rearrange("p (h d) -> p h d", d=128)
        W8 = 8 * L
        sqo, tsq = TB["sqo"]; on, ton = TB["on"]; ss, tss = TB["ss"]; rs, trs = TB["rs"]
        onacc, tacc = onacc_t
        self.act(sqo[0:L, :], onacc[0:L, :], AF.Square, [tacc], [tsq])
        self.P.op("dve", lambda e, o=ss[0:L, :], i=hd(sqo[0:L, :]): e.tensor_reduce(out=o, in_=i, axis=AX.X, op=ALU.add), [tsq], [tss])
        self.act(rs[0:L, :], ss[0:L, :], AF.Ln, [tss], [trs], bias=EPS, scale=1.0 / 128.0)
        self.act(rs[0:L, :], rs[0:L, :], AF.Exp, [trs], [trs], scale=-0.5)
        yield
        self.tt("dve", hd(on[0:L, :]), hd(onacc[0:L, :]), rs[0:L, :].unsqueeze(2).to_broadcast([L, 8, 128]), ALU.mult, [tacc, trs], [ton])
        yield
        pt, tpt, ipt = self.bacq()
        ptb = pt[:].bitcast(BF16)
        for h in range(8):
            self.tr(ptb[:, h * L:(h + 1) * L], on[0:L, h * 128:(h + 1) * 128], self.idb[0:L, 0:L], [ton, tC], [tpt])
        self.stt(ONT[:, :, c0:c0 + L], h3(ptb[:, 0:W8]), self.vcol("gnw"), QKVZ[:, 24:32, c0:c0 + L], ALU.mult, ALU.mult, [tpt, tC], [self.tok("ONT", self.phase_id)])
        self.brel(ipt)
        yield

    def ffn(self, st, l):
        NT = st.NT
        self.phase()
        hasS = any(s.kind == "S" for s in st.segs)
        xn, _ = self.A([128, 8, NT], BF16)
        hT, _ = self.A([128, 22, NT], BF16)
        sq2 = [self.A([128, 8, 512], BF16) for _ in range(2)]
        rsb = [self.A([128, 512], F32) for _ in range(2)]
        wsl = [self.A([128, 2, 8, 256], BF16) for _ in range(3)]
        wsl = [(w_, (t_, self.tok("wslb", self.phase_id, i_))) for i_, (w_, t_) in enumerate(wsl)]
        wdn = [self.A([128, 22, 128], BF16) for _ in range(3)]
        ub = [self.A([128, 2 + NTMAX], F32) for _ in range(4)]
        t0b = [self.A([128, 512], F32) for _ in range(4)]
        sab = [self.A([128, 512], F32) for _ in range(2)]
        if hasS:
            SHF, tSHF = self.A([128, NFC, 32], F32)
            stg, tstg = self.A([32, 5632], F32)
        xnt = lambda dc, tile: (xn[:, dc, tile[0].off + tile[1]:tile[0].off + tile[1] + tile[2]], self.tok("xn", self.phase_id, dc, tile[0].off + tile[1]))
        self.norm(st, "nf%d" % l, xnt, sq2, rsb)
        if hasS:
            self.dma("sp", stg, self.st_ffn[l], (), [tstg], "ldst")
            for g in range(0, NFC, 8):
                pb, tpb = self.bank()
                ng = min(8, NFC - g)
                for j in range(ng):
                    fc = g + j
                    self.tr(pb[:, j * 32:(j + 1) * 32], stg[0:32, fc * 128:(fc + 1) * 128], self.idf[0:32, 0:32], [tstg, self.tC], [tpb])
                self.cp("act", SHF[:, g:g + ng, :], pb[:, 0:ng * 32].rearrange("p (a b) -> p a b", b=32), [tpb], [tSHF])
        tHF = self.tok("HF")
        fcw, fcb = "fcw%d" % l, "fcb%d" % l
        def ld_wup(u):
            wt_, twt_ = wsl[u % 3]
            self.dma("pool", wt_[:, 0], self.w_up[l][:, u * 256:(u + 1) * 256].rearrange("(kc p) f -> p kc f", p=128), (), [twt_[0]], "ldw%d" % (u % 3))
            self.dma("pool", wt_[:, 1], self.w_up[l][:, DFF + u * 256:DFF + (u + 1) * 256].rearrange("(kc p) f -> p kc f", p=128), (), [twt_[1]], "ldwb%d" % (u % 3))

        def ld_wdn(dc):
            wd_, twd_ = wdn[dc % 3]
            self.dma("pool", wd_, self.w_down[l][:, dc * 128:(dc + 1) * 128].rearrange("(i p) d -> p i d", p=128), (), [twd_], "ldwd%d" % (dc % 3))
        ld_wup(0); ld_wup(1)
        pend = []
        for u in range(11):
            wt, twt = wsl[u % 3]
            if u + 2 < 11:
                ld_wup(u + 2)
            elif u + 2 == 11:
                ld_wdn(0)
            else:
                ld_wdn(1)
            for j in range(2):
                i = u * 2 + j
                cur_ub = [None, None]
                for tile in st.tiles():
                    seg, t0, n = tile
                    c0 = seg.off + t0
                    conv = []
                    for ab in range(2):
                        fc = i + 22 * ab
                        pb, tpb = self.bank()
                        for kc in range(8):
                            self.mm(pb[:, 0:n], wt[:, ab, kc, j * 128:(j + 1) * 128], xn[:, kc, c0:c0 + n], [twt[ab], xnt(kc, tile)[1]], [tpb],
                                    start=(kc == 0), stop=(kc == 7))
                        if seg.kind == "S":
                            bi = self.rr("ub", [0, 1, 2, 3])
                            ex = ub[bi][0]
                            tex = self.tok("ubL", self.phase_id, bi, 0)
                            rtoks = [tex]
                            self.cp("pool", self.ext_halo(seg, ex, 2), SHF[:, fc, :].rearrange("p (s r) -> p s r", r=2), [tSHF], [tex])
                            self.cp("act", self.ext_dst(seg, ex, 2, n), self.V(seg, pb[:, 0:n]), [tpb], [tex])
                            self.cp("pool", SHF[:, fc, :].rearrange("p (s r) -> p s r", r=2), self.ext_tail(seg, ex, 2, n), [tex], [tSHF])
                        else:
                            if t0 == 0:
                                bi = self.rr("ub", [0, 1, 2, 3])
                                cur_ub[ab] = (bi, 0)
                                tex = self.tok("ubL", self.phase_id, bi, 0)
                                self.cp("pool", ub[bi][0][:, 0:2], self.HF[:, l, fc, :], [tHF], [tex])
                                rtoks = [tex]
                            else:
                                bi, ti = cur_ub[ab]
                                cur_ub[ab] = (bi, ti + 1)
                                tex = self.tok("ubL", self.phase_id, bi, ti + 1)
                                rtoks = [tex, self.tok("ubL", self.phase_id, bi, ti)]
                            ex = ub[bi][0][:, t0:t0 + 2 + n]
                            self.cp("act", ex[:, 2:2 + n], pb[:, 0:n], [tpb], [tex])
                            if t0 + n == seg.n:
                                self.cp("pool", self.HF[:, l, fc, :], ex[:, n:n + 2], [tex], [tHF])
                        tb, ttb = t0b[self.rr("t0b", [0, 1, 2, 3])]
                        tv = self.V(seg, tb[:, 0:n])
                        self.act(tv, self.V(seg, pb[:, 0:n]), AF.Identity, [tpb, self.tC], [ttb], bias=self.vcol(fcb, fc), scale=self.vcol(fcw, 2 * NFC + fc))
                        self.stt(tv, self.ext_tap(seg, ex, 2, 1, n), self.vcol(fcw, 1 * NFC + fc), tv, ALU.mult, ALU.add, rtoks + [ttb, self.tC], [ttb])
                        self.stt(tv, self.ext_tap(seg, ex, 2, 0, n), self.vcol(fcw, 0 * NFC + fc), tv, ALU.mult, ALU.add, rtoks + [ttb, self.tC], [ttb])
                        conv.append((tb, ttb))
                    def tail(conv=conv, i=i, c0=c0, n=n):
                        sa, tsa = sab[self.rr("sab", [0, 1])]
                        self.act(sa[:, 0:n], conv[0][0][:, 0:n], AF.Silu, [conv[0][1]], [tsa])
                        self.tt("dve", hT[:, i, c0:c0 + n], sa[:, 0:n], conv[1][0][:, 0:n], ALU.mult, [tsa, conv[1][1]], [self.tok("hT", self.phase_id, i, c0)])
                    if pend:
                        pend.pop(0)()
                    pend.append(tail)
        while pend:
            pend.pop(0)()
        if hasS:
            for fc in range(NFC):
                pb, tpb = self.bank()
                self.tr(pb[0:32, 0:128], SHF[:, fc, :], self.idf[:, :], [tSHF, self.tC], [tpb])
                self.cp(self.rr("ev", ["act", "dve"]), stg[0:32, fc * 128:(fc + 1) * 128], pb[0:32, 0:128], [tpb], [tstg])
            self.dma("sp", self.o_sffn[l], stg, [tstg], (), "stst")
        if st.last:
            stg2, tstg2 = self.A([2, 5632], F32)
            for fc in range(NFC):
                pb, tpb = self.bank()
                self.tr(pb[0:2, 0:128], self.HF[:, l, fc, :], self.idf[:, :], [tHF, self.tC], [tpb])
                self.cp(self.rr("ev", ["act", "dve"]), stg2[0:2, fc * 128:(fc + 1) * 128], pb[0:2, 0:128], [tpb], [tstg2])
            self.dma("sp", self.o_pffn[l], stg2, [tstg2], (), "stst")
        for dc in range(8):
            wd, twd = wdn[dc % 3]
            if dc + 2 < 8:
                ld_wdn(dc + 2)
            for tile in st.tiles():
                seg, t0, n = tile
                c0 = seg.off + t0
                pb, tpb = self.bank()
                for i in range(22):
                    self.mm(pb[:, 0:n], wd[:, i, :], hT[:, i, c0:c0 + n], [twd, self.tok("hT", self.phase_id, i, c0)], [tpb], start=(i == 0), stop=(i == 21))
                xv = self.xT[:, dc, c0:c0 + n]
                self.tt("dve", xv, pb[:, 0:n], xv, ALU.add, [tpb], [self.xtok(dc, tile)])

    def pool_mixer(self, st):
        self.phase()
        sq2 = [self.A([128, 8, 512], BF16) for _ in range(2)]
        rsb = [self.A([128, 512], F32) for _ in range(2)]
        pw, tpw = self.A([128, 4, 2, 256], BF16)
        self.dma("pool", pw, self.pool_w.rearrange("g (ci p) e -> p g ci e", p=128), (), [tpw], "ldw0")
        segbuf = {}
        for seg in st.segs:
            W = 16 * 23 if seg.kind == "S" else 15 + seg.n
            hn, _ = self.A([128, 8, W], F32)
            s1, _ = self.A([128, 2, W], F32)
            s2, _ = self.A([128, 2, W], F32)
            PL, _ = self.A([128, 8, seg.n], BF16)
            segbuf[id(seg)] = (hn, s1, s2, PL, W)
        if any(s.kind == "S" for s in st.segs):
            stg, tstg = self.A([120, 2, D], F32)
            self._pcb = [self.A([128, 120], F32) for _ in range(2)]
        tmp15, ttmp15 = self.A([128, 15], F32)
        tHP = self.tok("HP")

        def dstf(dc, tile):
            seg, t0, n = tile
            hn = segbuf[id(seg)][0]
            if seg.kind == "S":
                ap = hn[:, dc, :].rearrange("p (s w) -> p s w", w=23)[:, :, 15:23]
            else:
                ap = hn[:, dc, 15 + t0:15 + t0 + n]
            return ap, self.tok("hn", self.phase_id, id(seg), dc)

        self._norm_pool(st, "nm1", dstf, sq2, rsb)
        for seg in st.segs:
            hn, s1, s2, PL, W = segbuf[id(seg)]
            n = seg.n
            if seg.kind == "S":
                for half in range(2):
                    self.dma("sp", stg[:, half, :], self.st_pool[half * 120:(half + 1) * 120, :], (), [tstg], "ldst")
                for dc in range(8):
                    pb, tpb = self.bank()
                    for half in range(2):
                        self.tr(pb[:, half * 120:(half + 1) * 120], stg[0:120, half, dc * 128:(dc + 1) * 128], self.idf[0:120, 0:120], [tstg, self.tC], [tpb])
                    self.cp(self.rr("ev", ["act", "dve"]), hn[:, dc, :].rearrange("p (s w) -> p s w", w=23)[:, :, 0:15],
                            pb[:, 0:240].rearrange("p (s r) -> p s r", r=15), [tpb], [self.tok("hn", self.phase_id, id(seg), dc)])
            else:
                for dc in range(8):
                    self.cp("pool", hn[:, dc, 0:15], self.HP[:, dc, :], [tHP], [self.tok("hn", self.phase_id, id(seg), dc)])
            if seg.kind == "S":
                e3 = lambda ap: ap.rearrange("p (s w) -> p s w", w=23)
                sl = lambda ap, a, b: e3(ap)[:, :, a:b]
                WW = 23
            else:
                sl = lambda ap, a, b: ap[:, a:b]
                WW = W
            for dc in range(8):
                gi = dc // 2
                th = self.tok("hn", self.phase_id, id(seg), dc)
                ts1 = self.tok("ps1", self.phase_id, id(seg), dc % 2)
                ts2 = self.tok("ps2", self.phase_id, id(seg), dc % 2)
                src, tsrc = hn[:, dc, :], th
                bufs = [(s1[:, dc % 2, :], ts1), (s2[:, dc % 2, :], ts2)]
                for lev in range(gi + 1):
                    sh = 1 << lev
                    lo = (1 << (lev + 1)) - 1
                    dstb, tdb = bufs[lev % 2]
                    self.tt(("pool", "dve")[dc % 2], sl(dstb, lo, WW), sl(src, lo, WW), sl(src, lo - sh, WW - sh), ALU.add, [tsrc], [tdb])
                    src, tsrc = dstb, tdb
                if seg.kind == "S":
                    outv = PL[:, dc, :].rearrange("p (s t) -> p s t", t=8)
                else:
                    outv = PL[:, dc, :]
                tPL = self.tok("PL", self.phase_id, id(seg), dc)
                self.stt(outv, sl(src, 15, WW), 1.0 / WINS[gi], sl(hn[:, dc, :], 15, WW), ALU.mult, ALU.subtract, [tsrc, th], [tPL])
                if seg.kind == "P" and seg.pos0 == 0:
                    ic = self.cst[:, C_INVC + gi * 15:C_INVC + gi * 15 + 15]
                    self.tt("dve", tmp15, src[:, 15:30], ic, ALU.mult, [tsrc, self.tC], [ttmp15])
                    self.tt("dve", PL[:, dc, 0:15], tmp15, hn[:, dc, 15:30], ALU.subtract, [ttmp15, th], [tPL])
            if seg.kind == "S":
                for dc in range(8):
                    th = self.tok("hn", self.phase_id, id(seg), dc)
                    for half in range(2):
                        pb, tpb = self.bank()
                        src3 = hn[:, dc, :].rearrange("p (s w) -> p s w", w=23)[:, half * 8:(half + 1) * 8, 8:23]
                        cbuf, tcb = self._pcb[self.rr("pcb", [0, 1])]
                        self.cp("pool", cbuf.rearrange("p (s r) -> p s r", r=15), src3, [th], [tcb])
                        self.tr(pb[0:120, 0:128], cbuf, self.idf[:, :], [tcb, self.tC], [tpb])
                        self.cp(self.rr("ev", ["act", "dve"]), stg[0:120, half, dc * 128:(dc + 1) * 128], pb[0:120, 0:128], [tpb], [tstg])
                for half in range(2):
                    self.dma("sp", self.o_spool[half * 120:(half + 1) * 120, :], stg[:, half, :], [tstg], (), "stst")
            else:
                for dc in range(8):
                    th = self.tok("hn", self.phase_id, id(seg), dc)
                    self.cp("pool", self.HP[:, dc, :], hn[:, dc, n:n + 15], [th], [tHP])
                if st.last:
                    stg2, tstg2 = self.A([15, D], F32)
                    for dc in range(8):
                        pb, tpb = self.bank()
                        self.tr(pb[0:15, 0:128], self.HP[:, dc, :], self.idf[:, :], [tHP, self.tC], [tpb])
                        self.cp(self.rr("ev", ["act", "dve"]), stg2[0:15, dc * 128:(dc + 1) * 128], pb[0:15, 0:128], [tpb], [tstg2])
                    self.dma("sp", self.o_ppool, stg2, [tstg2], (), "stst")
            for tile in seg.tiles():
                _, t0, nn = tile
                c0 = seg.off + t0
                for gi in range(4):
                    for eo in range(2):
                        dco = 2 * gi + eo
                        pb, tpb = self.bank()
                        for ci in range(2):
                            self.mm(pb[:, 0:nn], pw[:, gi, ci, eo * 128:(eo + 1) * 128], PL[:, 2 * gi + ci, t0:t0 + nn],
                                    [tpw, self.tok("PL", self.phase_id, id(seg), 2 * gi + ci)], [tpb], start=(ci == 0), stop=(ci == 1))
                        xv = self.xT[:, dco, c0:c0 + nn]
                        self.stt(xv, pb[:, 0:nn], self.vcol("psc", dco), xv, ALU.mult, ALU.add, [tpb, self.tC], [self.xtok(dco, tile)])

    def _norm_pool(self, st, wname, dstf, sq2, rsb):
        for tile in st.tiles():
            seg, t0, n = tile
            c0 = seg.off + t0
            sq, tsq = sq2[self.rr("sq2", [0, 1])]
            rs, trs = rsb[self.rr("rsb", [0, 1])]
            pb, tpb = self.bank()
            for dc in range(8):
                xin = self.xT[:, dc, c0:c0 + n]
                self.act(sq[:, dc, 0:n], xin, AF.Square, [self.xtok(dc, tile)], [tsq])
            for dc in range(8):
                self.mm(pb[:, 0:n], self.ones_m[:], sq[:, dc, 0:n], [tsq, self.tC], [tpb], start=(dc == 0), stop=(dc == 7))
            self.act(rs[:, 0:n], pb[:, 0:n], AF.Ln, [tpb], [trs], bias=EPS)
            self.act(rs[:, 0:n], rs[:, 0:n], AF.Exp, [trs], [trs], scale=-0.5)
            for dc in range(8):
                dst, tdst = dstf(dc, tile)
                self.stt(dst, self.V(seg, self.xT[:, dc, c0:c0 + n]), self.vcol(wname, dc), self.V(seg, rs[:, 0:n]), ALU.mult, ALU.mult,
                         [self.xtok(dc, tile), trs, self.tC], [tdst])

    def final(self, st):
        self.phase()
        sq2 = [self.A([128, 8, 512], BF16) for _ in range(2)]
        rsb = [self.A([128, 512], F32) for _ in range(2)]
        yT = [self.A([128, 8, 512], F32) for _ in range(2)]
        ysg = [self.A([128, D], F32) for _ in range(3)]
        cur = {}

        def dstf(dc, tile):
            return cur["y"][0][:, dc, 0:tile[2]], cur["y"][1]

        for tile in st.tiles():
            seg, t0, n = tile
            cur["y"] = yT[self.rr("yT", [0, 1])]
            self._norm_one(tile, "nfin", dstf, sq2, rsb)
            y, ty = cur["y"]
            b0 = 0
            while b0 < n:
                if seg.kind == "P":
                    pos = seg.pos0 + t0 + b0
                    if pos < NMETA:
                        b0 += NMETA - pos
                        continue
                m = min(128, n - b0)
                sg, tsg = ysg[self.rr("ysg", [0, 1, 2])]
                for half in range(2):
                    pb, tpb = self.bank()
                    for j in range(4):
                        dc = half * 4 + j
                        self.tr(pb[0:m, j * 128:(j + 1) * 128], y[:, dc, b0:b0 + m], self.idf[:, :], [ty, self.tC], [tpb])
                    self.cp(("act", "dve")[half], sg[0:m, half * 512:(half + 1) * 512], pb[0:m, :], [tpb], [tsg])
                if seg.kind == "S":
                    self.dma("sp", self.ys[b0:b0 + m, :], sg[0:m, :], [tsg], (), "sty%d" % ((self.rrc["ysg"] - 1) % 3))
                else:
                    r0 = seg.pos0 + t0 + b0 - NMETA
                    self.dma("sp", self.yp[r0:r0 + m, :], sg[0:m, :], [tsg], (), "sty%d" % ((self.rrc["ysg"] - 1) % 3))
                b0 += m

    def _norm_one(self, tile, wname, dstf, sq2, rsb):
        seg, t0, n = tile
        c0 = seg.off + t0
        sq, tsq = sq2[self.rr("sq2", [0, 1])]
        rs, trs = rsb[self.rr("rsb", [0, 1])]
        pb, tpb = self.bank()
        for dc in range(8):
            xin = self.xT[:, dc, c0:c0 + n]
            self.act(sq[:, dc, 0:n], xin, AF.Square, [self.xtok(dc, tile)], [tsq])
        for dc in range(8):
            self.mm(pb[:, 0:n], self.ones_m[:], sq[:, dc, 0:n], [tsq, self.tC], [tpb], start=(dc == 0), stop=(dc == 7))
        self.act(rs[:, 0:n], pb[:, 0:n], AF.Ln, [tpb], [trs], bias=EPS)
        self.act(rs[:, 0:n], rs[:, 0:n], AF.Exp, [trs], [trs], scale=-0.5)
        for dc in range(8):
            dst, tdst = dstf(dc, tile)
            self.stt(dst, self.xT[:, dc, c0:c0 + n], self.vcol(wname, dc), rs[:, 0:n], ALU.mult, ALU.mult,
                     [self.xtok(dc, tile), trs, self.tC], [tdst])


_NC_CACHE = {}


def _get_nc():
    if "nc" not in _NC_CACHE:
        b = Builder()
        _NC_CACHE["nc"] = b.build()
    return _NC_CACHE["nc"]


def kernel(**inp):
    inp = {k: np.asarray(v) for k, v in inp.items()}
    f = lambda a: np.ascontiguousarray(a, dtype=np.float32)
    nc = _get_nc()
    vecs = build_vecs(inp)
    consts = build_consts()
    w_in = f(inp["gdn_w_in"][0])
    wba = np.zeros((D, 40), np.float32)
    wba[:, 0:8] = w_in[:, 4104:4112]
    wba[:, 32:40] = w_in[:, 4096:4104]
    shared = {
        "meta": f(inp["meta_tokens"]), "w_in": w_in, "wba": wba, "w_out": f(inp["gdn_w_out"][0]),
        "pool_w": f(inp["pool_w"][0]), "w_up": f(inp["ffn_w_up"]), "w_down": f(inp["ffn_w_down"]),
        "vecs": vecs, "consts": consts,
    }
    in_maps = []
    for c in range(8):
        sl = slice(16 * c, 16 * c + 16)
        m = dict(shared)
        m["xp"] = f(inp["x_prompt"][c])
        m["xs"] = f(inp["x_sample"][sl].reshape(128, D))
        m["st_conv"] = f(inp["state_gdn_conv"][0, sl].reshape(48, 3072))
        m["st_rec"] = f(inp["state_gdn_rec"][0, sl])
        m["st_pool"] = f(inp["state_pool"][0, sl].reshape(240, D))
        m["st_ffn"] = f(inp["state_ffn_conv"][:, sl].reshape(2, 32, 5632))
        in_maps.append(m)
    res = run_bass_kernel_spmd(nc, in_maps, core_ids=list(range(8)))
    R = res.results
    g = lambda k: [np.asarray(r[k], dtype=np.float32) for r in R]
    y_prompt = np.stack(g("yp"), 0)
    y_sample = np.concatenate(g("ys"), 0).reshape(128, 8, D)
    p_conv = np.stack(g("o_pconv"), 0)[None]
    p_rec = np.stack(g("o_prec"), 0)[None]
    p_pool = np.stack(g("o_ppool"), 0)[None]
    p_ffn = np.stack(g("o_pffn"), 1)
    s_conv = np.concatenate([a.reshape(16, 3, 3072) for a in g("o_sconv")], 0)[None]
    s_rec = np.concatenate(g("o_srec"), 0)[None]
    s_pool = np.concatenate([a.reshape(16, 15, D) for a in g("o_spool")], 0)[None]
    s_ffn = np.concatenate([a.reshape(2, 16, 2, 5632) for a in g("o_sffn")], 1)
    return (y_prompt, y_sample, p_conv, p_rec, p_pool, p_ffn, s_conv, s_rec, s_pool, s_ffn)
```

```python
import contextlib
import numpy as np
import concourse.bass as bass
import concourse.mybir as mybir
from concourse.bass_utils import run_bass_kernel_spmd

F32 = mybir.dt.float32
BF16 = mybir.dt.bfloat16
ALU = mybir.AluOpType
AF = mybir.ActivationFunctionType
AX = mybir.AxisListType

D = 1024
NH = 8
DFF = 2816
NFC = 44
SEQ = 2048
NMETA = 16
EPS = 1e-6
NEG = -1.0e30
DEBUG_MAP = None
WINS = (2, 4, 8, 16)


class Tok:
    __slots__ = ("lastw", "readers", "excl")

    def __init__(self):
        self.lastw = None
        self.readers = []
        self.excl = False


class Op:
    __slots__ = ("eng", "fn", "deps", "ms", "dma_sem", "dma_val", "is_dma", "where")

    def __init__(self, eng, fn):
        import sys as _s
        f = _s._getframe(3)
        self.where = (f.f_lineno, f.f_back.f_lineno if f.f_back else 0)
        self.eng = eng
        self.fn = fn
        self.deps = []
        self.ms = None
        self.is_dma = False
        self.dma_sem = None
        self.dma_val = 0


class Prog:
    ENGS = ("pe", "act", "dve", "pool", "sp")

    def __init__(self, nc):
        self.nc = nc
        self.ops = {e: [] for e in self.ENGS}
        self.streams = {}
        self.pending = {}

    def barrier(self):
        lasts = [self.ops[e][-1] for e in self.ENGS if self.ops[e]]
        lasts += [st[0] for st in self.streams.values() if st[0] is not None]
        for e in self.ENGS:
            self.pending[e] = list(lasts)

    def op(self, eng, fn, reads=(), writes=(), stream=None):
        o = Op(eng, fn)
        is_dma = stream is not None
        deps = []
        for t in reads:
            if t.lastw is not None:
                deps.append((t.lastw, True))
            if t.excl:
                for r in t.readers:
                    if r.eng != eng:
                        deps.append((r, True))
        for t in writes:
            if t.lastw is not None:
                deps.append((t.lastw, False))
            for r in t.readers:
                deps.append((r, False))
        for d in self.pending.pop(eng, []):
            deps.append((d, True))
        if is_dma:
            o.is_dma = True
            st = self.streams.setdefault(stream, [None, 0])
            if st[0] is not None:
                deps.append((st[0], True))
            st[1] += 1
            o.dma_sem = stream
            o.dma_val = 16 * st[1]
            st[0] = o
        seen = set()
        for d, raw in deps:
            if d is o or id(d) in seen:
                continue
            if (not d.is_dma) and (not is_dma) and d.eng == eng and eng == "pe":
                continue
            seen.add(id(d))
            o.deps.append(d)
        for t in reads:
            t.readers.append(o)
        for t in writes:
            t.lastw = o
            t.readers = []
        self.ops[eng].append(o)
        return o

    def emit(self):
        nc = self.nc
        for e in self.ENGS:
            for o in self.ops[e]:
                for d in o.deps:
                    if not d.is_dma:
                        d.ms = True
        for e in self.ENGS:
            k = 0
            for o in self.ops[e]:
                if o.ms and not o.is_dma:
                    k += 1
                    o.ms = k
        with contextlib.ExitStack() as es:
            esem = {e: es.enter_context(nc.semaphore("s_" + e)) for e in self.ENGS}
            dsem = {k: es.enter_context(nc.semaphore("d_%d" % i)) for i, k in enumerate(self.streams)}
            block = es.enter_context(nc.Block())
            prog = self

            def run(e, engobj):
                seen = {}
                for o in prog.ops[e]:
                    for d in o.deps:
                        if d.is_dma:
                            key, val, sem = ("d", d.dma_sem), d.dma_val, dsem[d.dma_sem]
                        else:
                            key, val, sem = ("e", d.eng), d.ms, esem[d.eng]
                        if seen.get(key, 0) >= val:
                            continue
                        seen[key] = val
                        engobj.wait_ge(sem, val)
                    ins = o.fn(engobj)
                    if DEBUG_MAP is not None:
                        try:
                            DEBUG_MAP[str(ins.ins.name)] = o.where
                        except Exception as ex:
                            DEBUG_MAP["err"] = repr(ex)
                    if o.is_dma:
                        ins.then_inc(dsem[o.dma_sem], 16)
                    elif o.ms:
                        ins.then_inc(esem[e], 1)
                if e == "sp":
                    for k, st in prog.streams.items():
                        engobj.wait_ge(dsem[k], 16 * st[1])

            block.tensor(lambda eng: run("pe", eng))
            block.scalar(lambda eng: run("act", eng))
            block.vector(lambda eng: run("dve", eng))
            block.gpsimd(lambda eng: run("pool", eng))
            block.sync(lambda eng: run("sp", eng))


VEC_COLS = {}


def _vec_layout():
    off = 0
    for name, n in (("nm0", 8), ("nm1", 8), ("nf0", 8), ("nf1", 8), ("nfin", 8),
                    ("gcw", 96), ("fcw0", 132), ("fcw1", 132), ("fcb0", 44), ("fcb1", 44),
                    ("psc", 8), ("gnw", 1), ("alog", 1), ("dtb", 1)):
        VEC_COLS[name] = off
        off += n
    return off


NV = _vec_layout()
C_ID, C_TRI, C_NEGU, C_POSL, C_INVC = 0, 128, 192, 256, 320
C_TRI8, C_NEGU8, C_POSL8, C_SEL, C_RM = 380, 444, 508, 572, 636
NCONST = 636 + 8


def build_consts():
    c = np.zeros((128, NCONST), np.float32)
    c[:, C_ID:C_ID + 128] = np.eye(128, dtype=np.float32)
    p = np.arange(64)[:, None]
    f = np.arange(64)[None, :]
    c[:64, C_TRI:C_TRI + 64] = (f >= p).astype(np.float32)
    c[:64, C_NEGU:C_NEGU + 64] = np.where(f >= p, 0.0, NEG)
    c[:64, C_POSL:C_POSL + 64] = np.where(f < p, 0.0, -NEG)
    for gi, w in enumerate(WINS):
        for t in range(15):
            c[:, C_INVC + gi * 15 + t] = 1.0 / min(w, t + 1)
    same = (p // 8) == (f // 8)
    c[:64, C_TRI8:C_TRI8 + 64] = (same & (f >= p)).astype(np.float32)
    c[:64, C_NEGU8:C_NEGU8 + 64] = np.where(same & (f >= p), 0.0, NEG)
    c[:64, C_POSL8:C_POSL8 + 64] = np.where(same & (f < p), 0.0, -NEG)
    c[:64, C_SEL:C_SEL + 64] = (p == 8 * (f // 8) + 7).astype(np.float32)
    c[:64, C_RM:C_RM + 8] = ((p // 8) == np.arange(8)[None, :]).astype(np.float32)
    return c


def build_vecs(inp):
    v = np.zeros((128, NV), np.float32)

    def put(name, arr):
        a = np.asarray(arr, np.float32).reshape(-1, 128).T
        v[:, VEC_COLS[name]:VEC_COLS[name] + a.shape[1]] = a

    put("nm0", inp["norm_mix"][0]); put("nm1", inp["norm_mix"][1])
    put("nf0", inp["norm_ffn"][0]); put("nf1", inp["norm_ffn"][1])
    put("nfin", inp["norm_final"])
    put("gcw", inp["gdn_conv_w"][0].reshape(-1))
    put("fcw0", inp["ffn_conv_w"][0].reshape(-1)); put("fcw1", inp["ffn_conv_w"][1].reshape(-1))
    put("fcb0", inp["ffn_conv_b"][0]); put("fcb1", inp["ffn_conv_b"][1])
    put("psc", inp["pool_scale"][0])
    put("gnw", inp["gdn_norm_w"][0])
    v[0:8, VEC_COLS["alog"]] = inp["gdn_A_log"][0]
    v[0:8, VEC_COLS["dtb"]] = inp["gdn_dt_bias"][0]
    return v


class Seg:
    def __init__(self, kind, n, off, pos0=0):
        self.kind, self.n, self.off, self.pos0 = kind, n, off, pos0

    def tiles(self):
        if self.kind == "S":
            return [(self, 0, 128)]
        k = (self.n + 511) // 512
        base = (self.n // k + 7) // 8 * 8
        out, t = [], 0
        while t < self.n:
            m = min(base, self.n - t)
            out.append((self, t, m))
            t += m
        return out


class ST:
    def __init__(self, segs, first, last):
        self.segs, self.first, self.last = segs, first, last
        self.NT = sum(s.n for s in segs)

    def tiles(self):
        return [t for s in self.segs for t in s.tiles()]


SUPER = [
    ST([Seg("P", 592, 0, 0), Seg("S", 128, 592)], True, False),
    ST([Seg("P", 704, 0, 592)], False, False),
    ST([Seg("P", 768, 0, 1296)], False, True),
]
NTMAX = 768


class Builder:
    def __init__(self):
        self.nc = nc = bass.Bass("TRN2", target_bir_lowering=False)
        self.P = Prog(nc)
        self.es = contextlib.ExitStack()
        self.toks = {}
        self.rrc = {}
        self.phase_id = 0

        def din(name, shape):
            return nc.dram_tensor(name, list(shape), F32, kind="ExternalInput").ap()

        def dout(name, shape):
            return nc.dram_tensor(name, list(shape), F32, kind="ExternalOutput").ap()

        self.xp = din("xp", [SEQ, D]); self.xs = din("xs", [128, D])
        self.st_conv = din("st_conv", [48, 3072]); self.st_rec = din("st_rec", [16, 8, 128, 128])
        self.st_pool = din("st_pool", [240, D]); self.st_ffn = din("st_ffn", [2, 32, 5632])
        self.meta = din("meta", [NMETA, D])
        self.w_in = din("w_in", [D, 4112]); self.wba = din("wba", [D, 40])
        self.w_out = din("w_out", [D, D]); self.pool_w = din("pool_w", [4, 256, 256])
        self.w_up = din("w_up", [2, D, 5632]); self.w_down = din("w_down", [2, DFF, D])
        self.vecs_d = din("vecs", [128, NV]); self.consts_d = din("consts", [128, NCONST])
        self.yp = dout("yp", [SEQ, D]); self.ys = dout("ys", [128, D])
        self.o_pconv = dout("o_pconv", [3, 3072]); self.o_prec = dout("o_prec", [8, 128, 128])
        self.o_ppool = dout("o_ppool", [15, D]); self.o_pffn = dout("o_pffn", [2, 2, 5632])
        self.o_sconv = dout("o_sconv", [48, 3072]); self.o_srec = dout("o_srec", [16, 8, 128, 128])
        self.o_spool = dout("o_spool", [240, D]); self.o_sffn = dout("o_sffn", [2, 32, 5632])

    def tok(self, *key):
        t = self.toks.get(key)
        if t is None:
            t = self.toks[key] = Tok()
            if key[0] == "bank":
                t.excl = True
        return t

    def sb(self, name, shape, dt):
        return self.es.enter_context(self.nc.sbuf_tensor(name, list(shape), dt))

    def rr(self, name, choices):
        i = self.rrc.get(name, 0)
        self.rrc[name] = i + 1
        return choices[i % len(choices)]

    def bank(self):
        i = self.rrc.get("bank", 0)
        self.rrc["bank"] = i + 1
        i %= 8
        return self.banks[i], self.tok("bank", i)

    def phase(self):
        self.P.barrier()
        self.aoff = 0
        self.phase_id += 1

    def A(self, shape, dt, key=None):
        n = int(np.prod(shape[1:]))
        nb = n * (4 if dt == F32 else 2)
        nb = (nb + 31) // 32 * 32
        ne = nb // 2
        assert self.aoff + ne <= self.arena_n, ("arena overflow", self.aoff, ne, self.arena_n)
        ap = self.arena[0:shape[0], self.aoff:self.aoff + ne]
        self.aoff += ne
        if dt == F32:
            ap = ap.bitcast(F32)
        ap = ap[:, 0:n]
        if len(shape) == 3:
            ap = ap.rearrange("p (a b) -> p a b", b=shape[2])
        elif len(shape) == 4:
            ap = ap.rearrange("p (a b c) -> p a b c", b=shape[2], c=shape[3])
        return ap, self.tok("arena", self.phase_id, self.aoff)

    def mm(self, out, lhsT, rhs, r, w, start=True, stop=True, skip=False):
        if skip:
            self.P.op("pe", lambda e: e.matmul(out, lhsT=lhsT, rhs=rhs, start=start, stop=stop, skip_group_check=True), r, w)
        else:
            self.P.op("pe", lambda e: e.matmul(out, lhsT=lhsT, rhs=rhs, start=start, stop=stop), r, w)

    def tr(self, out, in_, ident, r, w):
        self.P.op("pe", lambda e: e.transpose(out=out, in_=in_, identity=ident), r, w)

    def act(self, out, in_, func, r, w, bias=None, scale=None):
        kw = {}
        if bias is not None:
            kw["bias"] = bias
        if scale is not None:
            kw["scale"] = scale
        self.P.op("act", lambda e: e.activation(out=out, in_=in_, func=func, **kw), r, w)

    def tt(self, eng, out, in0, in1, op, r, w):
        self.P.op(eng, lambda e: e.tensor_tensor(out=out, in0=in0, in1=in1, op=op), r, w)

    def ts(self, eng, out, in0, s1, op0, r, w, s2=None, op1=None):
        if op1 is None:
            self.P.op(eng, lambda e: e.tensor_scalar(out=out, in0=in0, scalar1=s1, scalar2=None, op0=op0), r, w)
        else:
            self.P.op(eng, lambda e: e.tensor_scalar(out=out, in0=in0, scalar1=s1, scalar2=s2, op0=op0, op1=op1), r, w)

    def stt(self, out, in0, scalar, in1, op0, op1, r, w):
        self.P.op("dve", lambda e: e.scalar_tensor_tensor(out=out, in0=in0, scalar=scalar, in1=in1, op0=op0, op1=op1), r, w)

    def cp(self, eng, out, in_, r, w):
        if eng == "act":
            self.act(out, in_, AF.Copy, r, w)
        else:
            self.P.op(eng, lambda e: e.tensor_copy(out=out, in_=in_), r, w)

    def dma(self, q, out, in_, r, w, stream):
        self.P.op(q, lambda e: e.dma_start(out=out, in_=in_), r, w, stream=stream)

    def memset(self, eng, ap, val, w):
        self.P.op(eng, lambda e: e.memset(ap, val), (), w)

    def vcol(self, name, j=0, np_=128):
        c = VEC_COLS[name] + j
        return self.vecs[0:np_, c:c + 1]

    @staticmethod
    def V(seg, ap):
        if seg.kind == "S":
            return ap.rearrange("p (s t) -> p s t", t=8)
        return ap

    @staticmethod
    def ext_dst(seg, buf, H, n):
        if seg.kind == "S":
            return buf[:, 0:16 * (H + 8)].rearrange("p (s w) -> p s w", w=H + 8)[:, :, H:H + 8]
        return buf[:, H:H + n]

    @staticmethod
    def ext_tap(seg, buf, H, j, n):
        if seg.kind == "S":
            return buf[:, 0:16 * (H + 8)].rearrange("p (s w) -> p s w", w=H + 8)[:, :, j:j + 8]
        return buf[:, j:j + n]

    @staticmethod
    def ext_halo(seg, buf, H):
        if seg.kind == "S":
            return buf[:, 0:16 * (H + 8)].rearrange("p (s w) -> p s w", w=H + 8)[:, :, 0:H]
        return buf[:, 0:H]

    @staticmethod
    def ext_tail(seg, buf, H, n):
        if seg.kind == "S":
            return buf[:, 0:16 * (H + 8)].rearrange("p (s w) -> p s w", w=H + 8)[:, :, 8:8 + H]
        return buf[:, n:n + H]

    def build(self):
        nc = self.nc
        with self.es:
            self.xT = self.sb("xT", [128, 8, NTMAX], F32)
            self.S32 = self.sb("S32", [128, 8, 128], F32)
            self.S16 = self.sb("S16", [128, 8, 128], BF16)
            self.HG = self.sb("HG", [128, 24, 3], F32)
            self.HF = self.sb("HF", [128, 2, NFC, 2], F32)
            self.HP = self.sb("HP", [128, 8, 15], F32)
            self.vecs = self.sb("vecs_sb", [128, NV], F32)
            self.cst = self.sb("cst", [128, NCONST], F32)
            self.idb = self.sb("idb", [128, 128], BF16)
            self.ones_m = self.sb("ones_m", [128, 128], BF16)
            self.ones_1 = self.sb("ones_1", [128, 128], BF16)
            self.ones_f = self.sb("ones_f", [64, 128], F32)
            self.nexpA = self.sb("nexpA", [8, 1], F32)
            self.lnq = self.sb("lnq", [128, 1], F32)
            self.banks = [self.es.enter_context(nc.psum_tensor("pb%d" % i, [128, 512], F32)) for i in range(8)]
            rem = nc.sbuf_bytes_remaining - 2048
            self.arena_n = (rem // 2) // 64 * 64
            self.arena = self.sb("arena", [128, self.arena_n], BF16)
            self.aoff = 0
            self.idf = self.cst[:, C_ID:C_ID + 128]
            tC = self.tok("consts")
            self.dma("sp", self.vecs[:], self.vecs_d, (), [tC], "ldc0")
            self.dma("sp", self.cst[:], self.consts_d, (), [tC], "ldc1")
            self.cp("dve", self.idb[:], self.idf, [tC], [tC])
            self.memset("pool", self.ones_m[:], 1.0 / 1024.0, [tC])
            self.memset("pool", self.ones_1[:], 1.0, [tC])
            self.memset("pool", self.ones_f[:], 1.0, [tC])
            self.memset("pool", self.lnq[:], -0.5 * float(np.log(128.0)), [tC])
            self.memset("pool", self.S32[:], 0.0, [self.tok("S32")])
            self.memset("pool", self.S16[:], 0.0, [self.tok("S16")])
            self.memset("pool", self.HG[:], 0.0, [self.tok("HG")])
            self.memset("pool", self.HF[:], 0.0, [self.tok("HF")])
            self.memset("pool", self.HP[:], 0.0, [self.tok("HP")])
            self.act(self.nexpA[:], self.vcol("alog", 0, 8), AF.Exp, [tC], [tC])
            self.ts("dve", self.nexpA[:], self.nexpA[:], -1.0, ALU.mult, [tC], [tC])
            self.tC = tC
            for st in SUPER:
                self.run_super(st)
            self.P.emit()
        return nc

    def run_super(self, st):
        self.load_x(st)
        self.gdn(st)
        self.ffn(st, 0)
        self.pool_mixer(st)
        self.ffn(st, 1)
        self.final(st)

    def xtok(self, dc, tile):
        return self.tok("xT", dc, tile[0].off + tile[1])

    def load_x(self, st):
        if st.first:
            self.phase()
        stg = [self.A([128, 4, D], F32) for _ in range(2)]
        bi = 0
        for seg in st.segs:
            for (_, t0, n) in seg.tiles():
                sg, tsg = stg[bi % 2]
                bi += 1
                nb = (n + 127) // 128
                for b in range(nb):
                    m = min(128, n - b * 128)
                    if seg.kind == "S":
                        self.dma("sp", sg[0:m, b, :], self.xs[0:m, :], (), [tsg], "ldx")
                    else:
                        p0 = seg.pos0 + t0 + b * 128
                        r = 0
                        if p0 < NMETA:
                            k = min(m, NMETA - p0)
                            self.dma("sp", sg[0:k, b, :], self.meta[p0:p0 + k, :], (), [tsg], "ldx")
                            r = k
                        if r < m:
                            a = p0 + r - NMETA
                            self.dma("sp", sg[r:m, b, :], self.xp[a:a + (m - r), :], (), [tsg], "ldx")
                tile = (seg, t0, n)
                c0 = seg.off + t0
                for dc in range(8):
                    pb, tpb = self.bank()
                    for b in range(nb):
                        m = min(128, n - b * 128)
                        self.tr(pb[:, b * 128:b * 128 + m], sg[0:m, b, dc * 128:(dc + 1) * 128], self.idf[0:m, 0:m],
                                [tsg, self.tC], [tpb])
                    self.cp(self.rr("ev", ["act", "dve"]), self.xT[:, dc, c0:c0 + n], pb[:, 0:n], [tpb], [self.xtok(dc, tile)])

    def norm(self, st, wname, dstf, sq2, rsb):
        for tile in st.tiles():
            seg, t0, n = tile
            c0 = seg.off + t0
            sq, tsq = sq2[self.rr("sq2", [0, 1])]
            rs, trs = rsb[self.rr("rsb", [0, 1])]
            pb, tpb = self.bank()
            for dc in range(8):
                xin = self.xT[:, dc, c0:c0 + n]
                self.act(sq[:, dc, 0:n], xin, AF.Square, [self.xtok(dc, tile)], [tsq])
            for dc in range(8):
                self.mm(pb[:, 0:n], self.ones_m[:], sq[:, dc, 0:n], [tsq, self.tC], [tpb], start=(dc == 0), stop=(dc == 7))
            self.act(rs[:, 0:n], pb[:, 0:n], AF.Ln, [tpb], [trs], bias=EPS)
            self.act(rs[:, 0:n], rs[:, 0:n], AF.Exp, [trs], [trs], scale=-0.5)
            for dc in range(8):
                dst, tdst = dstf(dc, tile)
                self.stt(dst, self.xT[:, dc, c0:c0 + n], self.vcol(wname, dc), rs[:, 0:n], ALU.mult, ALU.mult,
                         [self.xtok(dc, tile), trs, self.tC], [tdst])

    def gdn(self, st):
        NT = st.NT
        self.phase()
        hasS = any(s.kind == "S" for s in st.segs)
        xn, _ = self.A([128, 8, NT], BF16)
        QKVZ, _ = self.A([128, 32, NT], BF16)
        GB, tGB = self.A([40, NT], F32)
        mark = self.aoff
        sq2 = [self.A([128, 8, 512], BF16) for _ in range(2)]
        rsb = [self.A([128, 512], F32) for _ in range(2)]
        wsl = [self.A([128, 8, 512], BF16) for _ in range(3)]
        wbat, twba = self.A([128, 8, 40], BF16)
        ext = [self.A([128, 3 + NTMAX], F32) for _ in range(3)]
        acc = [self.A([128, 512], F32) for _ in range(2)]
        sil = [self.A([128, 512], F32) for _ in range(2)]
        sqh = [self.A([128, 512], BF16) for _ in range(3)]
        rin = [self.A([128, 512], F32) for _ in range(2)]
        bat = [self.A([8, 512], F32) for _ in range(4)]
        if hasS:
            SHG, tSHG = self.A([128, 24, 48], F32)
            stg, tstg = self.A([48, 3072], F32)
        self.memset("pool", GB, 0.0, [tGB])
        xnt = lambda dc, tile: (xn[:, dc, tile[0].off + tile[1]:tile[0].off + tile[1] + tile[2]], self.tok("xn", self.phase_id, dc, tile[0].off + tile[1]))
        self.norm(st, "nm0", xnt, sq2, rsb)
        if hasS:
            self.dma("sp", stg, self.st_conv, (), [tstg], "ldst")
            for g in range(3):
                pb, tpb = self.bank()
                for j in range(8):
                    fc = g * 8 + j
                    self.tr(pb[:, j * 48:(j + 1) * 48], stg[0:48, fc * 128:(fc + 1) * 128], self.idf[0:48, 0:48], [tstg, self.tC], [tpb])
                self.cp("act", SHG[:, g * 8:(g + 1) * 8, :], pb[:, 0:384].rearrange("p (a b) -> p a b", b=48), [tpb], [tSHG])
        self.dma("pool", wbat, self.wba.rearrange("(kc p) f -> p kc f", p=128), (), [twba], "ldwba")
        for tile in st.tiles():
            seg, t0, n = tile
            c0 = seg.off + t0
            pb, tpb = self.bank()
            for kc in range(8):
                self.mm(pb[0:40, 0:n], wbat[:, kc, :], xn[:, kc, c0:c0 + n], [twba, xnt(kc, tile)[1]], [tpb], start=(kc == 0), stop=(kc == 7))
            self.act(GB[32:40, c0:c0 + n], pb[32:40, 0:n], AF.Sigmoid, [tpb], [tGB])
            (b1, t1), (b2, t2), (b3, t3), (b4, t4) = bat
            self.ts("dve", b1[:, 0:n], pb[0:8, 0:n], self.vcol("dtb", 0, 8), ALU.add, [tpb, self.tC], [t1])
            self.stt(b2[:, 0:n], b1[:, 0:n], -1.0, b1[:, 0:n], ALU.mult, ALU.max, [t1], [t2])
            self.act(b3[:, 0:n], b2[:, 0:n], AF.Exp, [t2], [t3], scale=-1.0)
            self.act(b4[:, 0:n], b3[:, 0:n], AF.Ln, [t3], [t4], bias=1.0)
            self.stt(b2[:, 0:n], b1[:, 0:n], 0.0, b4[:, 0:n], ALU.max, ALU.add, [t1, t4], [t2])
            self.ts("dve", GB[0:8, c0:c0 + n], b2[:, 0:n], self.nexpA[:, 0:1], ALU.mult, [t2, self.tC], [tGB])
        def ld_win(u):
            wt_, twt_ = wsl[u % 3]
            self.dma("pool", wt_, self.w_in[:, u * 512:(u + 1) * 512].rearrange("(kc p) f -> p kc f", p=128), (), [twt_], "ldw%d" % (u % 3))
        ld_win(0); ld_win(1)
        pend = []
        qk_list = []
        for u in range(8):
            wt, twt = wsl[u % 3]
            if u + 2 < 8:
                ld_win(u + 2)
            for j in range(4):
                fc = u * 4 + j
                kind = fc // 8
                prev_ext = None
                for tile in st.tiles():
                    seg, t0, n = tile
                    c0 = seg.off + t0
                    pb, tpb = self.bank()
                    for kc in range(8):
                        self.mm(pb[:, 0:n], wt[:, kc, j * 128:(j + 1) * 128], xn[:, kc, c0:c0 + n], [twt, xnt(kc, tile)[1]], [tpb],
                                start=(kc == 0), stop=(kc == 7))
                    dst = QKVZ[:, fc, c0:c0 + n]
                    tdst = self.tok("qkvz", self.phase_id, fc, c0)
                    if kind == 3:
                        self.act(self.V(seg, dst), self.V(seg, pb[:, 0:n]), AF.Silu, [tpb], [tdst])
                        continue
                    if seg.kind == "S":
                        bi = self.rr("ext", [0, 1, 2])
                        ex = ext[bi][0]
                        tex = self.tok("extL", self.phase_id, bi, 0)
                        rtk = [tex]
                        self.cp("pool", self.ext_halo(seg, ex, 3), SHG[:, fc, :].rearrange("p (s r) -> p s r", r=3), [tSHG], [tex])
                        self.cp("act", self.ext_dst(seg, ex, 3, n), self.V(seg, pb[:, 0:n]), [tpb], [tex])
                        self.cp("pool", SHG[:, fc, :].rearrange("p (s r) -> p s r", r=3), self.ext_tail(seg, ex, 3, n), [tex], [tSHG])
                    else:
                        if t0 == 0:
                            bi = self.rr("ext", [0, 1, 2])
                            cur_ext = (bi, 0)
                            tex = self.tok("extL", self.phase_id, bi, 0)
                            self.cp("pool", ext[bi][0][:, 0:3], self.HG[:, fc, :], [self.tok("HG")], [tex])
                            rtk = [tex]
                        else:
                            bi, ti = cur_ext
                            cur_ext = (bi, ti + 1)
                            tex = self.tok("extL", self.phase_id, bi, ti + 1)
                            rtk = [tex, self.tok("extL", self.phase_id, bi, ti)]
                        ex = ext[bi][0][:, t0:t0 + 3 + n]
                        self.cp("act", ex[:, 3:3 + n], pb[:, 0:n], [tpb], [tex])
                        if t0 + n == seg.n:
                            self.cp("pool", self.HG[:, fc, :], ex[:, n:n + 3], [tex], [self.tok("HG")])
                    ac, tac = acc[self.rr("acc", [0, 1])]
                    av = self.V(seg, ac[:, 0:n])
                    self.act(av, self.ext_tap(seg, ex, 3, 0, n), AF.Identity, rtk + [self.tC], [tac], scale=self.vcol("gcw", 0 * 24 + fc))
                    for tap in (1, 2, 3):
                        self.stt(av, self.ext_tap(seg, ex, 3, tap, n), self.vcol("gcw", tap * 24 + fc), av, ALU.mult, ALU.add, rtk + [tac, self.tC], [tac])

                    def tail(dst=dst, tdst=tdst, ac=ac, tac=tac, n=n):
                        self.act(dst, ac[:, 0:n], AF.Silu, [tac], [tdst])
                    if pend:
                        pend.pop(0)()
                    pend.append(tail)
                    if kind < 2:
                        qk_list.append((kind, dst, tdst, n))
            while pend:
                pend.pop(0)()
            def nstage1(item):
                kind, dst, tdst, n = item
                sh, tsh = sqh[self.rr("sqh", [0, 1, 2])]
                self.tt("dve", sh[:, 0:n], dst, dst, ALU.mult, [tdst], [tsh])
                pb2, tpb2 = self.bank()
                self.mm(pb2[:, 0:n], self.ones_1[:], sh[:, 0:n], [tsh, self.tC], [tpb2])
                return pb2, tpb2

            def nstage2(item, pb2, tpb2):
                kind, dst, tdst, n = item
                ri, tri_ = rin[self.rr("rin", [0, 1])]
                self.act(ri[:, 0:n], pb2[:, 0:n], AF.Ln, [tpb2], [tri_], bias=EPS)
                self.act(ri[:, 0:n], ri[:, 0:n], AF.Exp, [tri_], [tri_], scale=-0.5, bias=(self.lnq[:, 0:1] if kind == 0 else None))
                self.tt("dve", dst, dst, ri[:, 0:n], ALU.mult, [tdst, tri_], [tdst])
            inflight = []
            for item in qk_list:
                inflight.append((item,) + nstage1(item))
                if len(inflight) > 2:
                    nstage2(*inflight.pop(0))
            while inflight:
                nstage2(*inflight.pop(0))
            qk_list = []
        if hasS:
            for fc in range(24):
                pb, tpb = self.bank()
                self.tr(pb[0:48, 0:128], SHG[:, fc, :], self.idf[:, :], [tSHG, self.tC], [tpb])
                self.cp(self.rr("ev", ["act", "dve"]), stg[0:48, fc * 128:(fc + 1) * 128], pb[0:48, 0:128], [tpb], [tstg])
            self.dma("sp", self.o_sconv, stg, [tstg], (), "stst")
        if st.last:
            stg2, tstg2 = self.A([3, 3072], F32)
            for fc in range(24):
                pb, tpb = self.bank()
                self.tr(pb[0:3, 0:128], self.HG[:, fc, :], self.idf[:, :], [self.tok("HG"), self.tC], [tpb])
                self.cp(self.rr("ev", ["act", "dve"]), stg2[0:3, fc * 128:(fc + 1) * 128], pb[0:3, 0:128], [tpb], [tstg2])
            self.dma("sp", self.o_pconv, stg2, [tstg2], (), "stst")

        self.P.barrier()
        self.aoff = mark
        ONT = xn
        self.bfree = list(range(8))
        NA, NC_, NB = 3, 4, 1
        self.want_onacc = False
        TAs = [self.alloc_chunk_bufs("A") for _ in range(NA)]
        CAs = [self.alloc_chunk_bufs("C") for _ in range(NC_)]
        TBs = [self.alloc_chunk_bufs("B") for _ in range(NB)]
        jobs = []
        for seg in st.segs:
            if seg.kind == "P":
                c = 0
                if seg.pos0 == 0:
                    jobs.append(("P", seg.off, NMETA, None))
                    c = NMETA
                while c < seg.n:
                    jobs.append(("P", seg.off + c, 64, None))
                    c += 64
            else:
                for b_ in range(2):
                    jobs.append(("SB", seg.off + 64 * b_, 64, 8 * b_))
        N = len(jobs)
        pj = [ji for ji in range(N) if jobs[ji][0] == "P"]
        nP = len(pj)
        gbt_all, tpre = self.A([64, nP * 40], F32)
        Gt_all, _ = self.A([64, nP * 8], F32)
        eG_all, _ = self.A([64, nP * 8], F32)
        nb_all, _ = self.A([64, nP * 8], F32)
        nbG_all, _ = self.A([64, nP * 8], F32)
        self.memset("pool", gbt_all, 0.0, [tpre])
        pgb, tpgb, ipgb = self.bacq()
        for k, ji in enumerate(pj):
            _, c0_, L_, _ = jobs[ji]
            self.tr(pgb[0:L_, k * 40:(k + 1) * 40], GB[0:40, c0_:c0_ + L_], self.idf[0:40, 0:40], [tGB, self.tC], [tpgb])
        k = 0
        while k < nP:
            k2 = k
            while k2 < nP and jobs[pj[k2]][2] == jobs[pj[k]][2]:
                k2 += 1
            L_ = jobs[pj[k]][2]
            self.cp("dve", gbt_all[0:L_, k * 40:k2 * 40], pgb[0:L_, k * 40:k2 * 40], [tpgb], [tpre])
            k = k2
        self.brel(ipgb)
        g3 = gbt_all.rearrange("p (k c) -> p k c", c=40)
        pgc, tpgc, ipgc = self.bacq()
        self.mm(pgc[0:64, 0:nP * 8].rearrange("p (k c) -> p k c", c=8), self.cst[0:64, C_TRI:C_TRI + 64], g3[:, :, 0:8], [tpre, self.tC], [tpgc])
        self.cp("dve", Gt_all, pgc[0:64, 0:nP * 8], [tpgc], [tpre])
        self.brel(ipgc)
        self.act(eG_all, Gt_all, AF.Exp, [tpre], [tpre])
        self.ts("dve", nb_all.rearrange("p (k c) -> p k c", c=8), g3[:, :, 32:40], -1.0, ALU.mult, [tpre], [tpre])
        self.tt("dve", nbG_all, eG_all, nb_all, ALU.mult, [tpre], [tpre])
        pres = {}
        for k, ji in enumerate(pj):
            L_ = jobs[ji][2]
            pres[ji] = (gbt_all[0:L_, k * 40:k * 40 + 8], gbt_all[0:L_, k * 40 + 32:k * 40 + 40], Gt_all[0:L_, k * 8:(k + 1) * 8],
                        nb_all[0:L_, k * 8:(k + 1) * 8], nbG_all[0:L_, k * 8:(k + 1) * 8], tpre)
        nextA = 0
        nextB = 0
        doneA = set()
        actA = {}
        actB = None
        while nextB < N:
            for slot in range(NA):
                if slot not in actA and nextA < N and nextA < nextB + NC_:
                    actA[slot] = (nextA, self.chunk_A(jobs[nextA], QKVZ, GB, tGB, TAs[slot], CAs[nextA % NC_], pres.get(nextA)))
                    nextA += 1
            if actB is None and nextB in doneA:
                if jobs[nextB][0] == "SB":
                    nextB += 1
                    continue
                actB = self.chunk_B(jobs[nextB], CAs[nextB % NC_], TBs[nextB % NB], QKVZ, ONT)
            if actB is not None:
                try:
                    next(actB)
                except StopIteration:
                    actB = None
                    nextB += 1
            for slot in list(actA):
                j, g = actA[slot]
                try:
                    next(g)
                except StopIteration:
                    doneA.add(j)
                    del actA[slot]
        if hasS:
            self.P.barrier()
            onaccs = [self.A([64, 1024], F32) for _ in range(2)]
            save_off = self.aoff
            self.aoff = mark
            NSQ = 3
            assert all((ji % NC_) >= 2 for ji in range(N) if jobs[ji][0] == "SB")
            SBs = []
            for _ in range(NSQ):
                d = {}
                d["S32"] = self.A([128, 8, 128], F32); d["S16"] = self.A([128, 8, 128], BF16)
                d["Stmp"] = self.A([128, 1024], F32); d["vn"] = self.A([64, 1024], BF16)
                SBs.append(d)
            sb_jobs = [(ji, jobs[ji]) for ji in range(N) if jobs[ji][0] == "SB"]
            todo = [(bi, ji, job, s_) for bi, (ji, job) in enumerate(sb_jobs) for s_ in range(8)]
            remaining = {bi: 8 for bi in range(len(sb_jobs))}
            act = {}
            fin = []
            while todo or act or fin:
                for slot in range(NSQ):
                    if slot not in act and todo:
                        bi, ji, job, s_ = todo.pop(0)
                        act[slot] = (bi, self.sample_seq(job, s_, CAs[ji % NC_], SBs[slot], onaccs[bi]))
                for slot in list(act):
                    bi, g = act[slot]
                    try:
                        next(g)
                    except StopIteration:
                        del act[slot]
                        remaining[bi] -= 1
                        if remaining[bi] == 0:
                            ji, job = sb_jobs[bi]
                            fin.append(self.sample_finish(job, TBs[0], onaccs[bi], QKVZ, ONT))
                for g in list(fin):
                    try:
                        next(g)
                    except StopIteration:
                        fin.remove(g)
            self.aoff = max(save_off, self.aoff)
        if st.last:
            self.dma("sp", self.o_prec.rearrange("h k v -> k h v"), self.S32[:], [self.tok("S32")], (), "strec")

        self.P.barrier()
        self.aoff = mark
        wo, two = self.A([128, 8, D], BF16)
        self.dma("pool", wo[:, :, 0:512], self.w_out[:, 0:512].rearrange("(kc p) f -> p kc f", p=128), (), [two], "ldw0")
        self.dma("pool", wo[:, :, 512:1024], self.w_out[:, 512:1024].rearrange("(kc p) f -> p kc f", p=128), (), [two], "ldw1")
        for tile in st.tiles():
            seg, t0, n = tile
            c0 = seg.off + t0
            for dc in range(8):
                pb, tpb = self.bank()
                for kc in range(8):
                    self.mm(pb[:, 0:n], wo[:, kc, dc * 128:(dc + 1) * 128], ONT[:, kc, c0:c0 + n], [two], [tpb], start=(kc == 0), stop=(kc == 7))
                xv = self.xT[:, dc, c0:c0 + n]
                self.tt("dve", xv, pb[:, 0:n], xv, ALU.add, [tpb], [self.xtok(dc, tile)])

    def alloc_chunk_bufs(self, which):
        b = {}
        def a(name, shape, dt):
            b[name] = self.A(shape, dt)
        if which == "A":
            a("gbt", [64, 40], F32)
            for nm in ("Gt", "eG", "nbG", "nb", "dGl", "eGl"):
                a(nm, [64, 8], F32)
            a("Dm", [64, 512], F32); a("Du", [64, 512], F32); a("Dl", [64, 512], F32); a("eGbc", [128, 512], F32)
            b["rhsG"] = b["Dl"]
            a("Lneg", [64, 512], BF16); a("M0", [64, 512], BF16)
            a("QTa", [64, 512], BF16); a("QTb", [64, 512], BF16)
            a("kbgn", [64, 1024], BF16)
        elif which == "C":
            a("PQa", [64, 1024], BF16); a("PQb", [64, 1024], BF16); a("At", [64, 512], BF16)
            a("kd", [64, 1024], BF16); a("vb", [64, 1024], BF16)
            a("nWT", [128, 512], BF16); a("qdT", [128, 512], BF16); a("gtc", [128, 64], F32)
        else:
            a("vn", [64, 1024], BF16); a("sqo", [64, 1024], BF16); a("on", [64, 1024], BF16)
            a("Stmp", [128, 1024], F32); a("ss", [64, 8], F32); a("rs", [64, 8], F32)
            if self.want_onacc:
                a("onacc", [64, 1024], F32)
        return b

    def bacq(self):
        if not self.bfree:
            raise RuntimeError("out of PSUM banks")
        i = self.bfree.pop(0)
        return self.banks[i], self.tok("bank", i), i

    def brel(self, i):
        self.bfree.append(i)

    def chunk_A(self, job, QKVZ, GB, tGB, TA, CA, pre=None):
        kind, c0, L, sidx = job
        B = dict(TA); B.update(CA)
        tC = self.tC
        Q = lambda h: QKVZ[:, h, c0:c0 + L]
        K = lambda h: QKVZ[:, 8 + h, c0:c0 + L]
        Vv = lambda h: QKVZ[:, 16 + h, c0:c0 + L]
        h3 = lambda ap: ap.rearrange("p (h l) -> p h l", l=L)
        hd = lambda ap: ap.rearrange("p (h d) -> p h d", d=128)
        W8 = 8 * L
        blk = (kind == "SB")
        cT, cN, cP = (C_TRI8, C_NEGU8, C_POSL8) if blk else (C_TRI, C_NEGU, C_POSL)
        tri = self.cst[0:L, cT:cT + L]
        dGl, tdGl = B["dGl"]; eGl, teGl = B["eGl"]; gtc, tgtc = B["gtc"]
        rhsG, trG = B["rhsG"]
        if pre is not None:
            g_tm, beta, GtL, nbL, nbGL, tpre = pre
            tgbt = tGt = tnb = tnbG = tpre
            self.tt("pool", h3(rhsG[0:L, 0:W8]), tri.unsqueeze(1).to_broadcast([L, 8, L]), g_tm.unsqueeze(2).to_broadcast([L, 8, L]),
                    ALU.mult, [tgbt, tC], [trG])
            yield
            pG, tpG, ipG = self.bacq()
            self.mm(pG[:, 0:W8], self.ones_f[0:L, :], rhsG[0:L, 0:W8], [trG, tC], [tpG])
            yield
        else:
            pg, tpg, ipg = self.bacq()
            self.tr(pg[0:L, 0:40], GB[0:40, c0:c0 + L], self.idf[0:40, 0:40], [tGB, tC], [tpg])
            gbt, tgbt = B["gbt"]
            self.cp("dve", gbt[0:L, :], pg[0:L, 0:40], [tpg], [tgbt])
            self.brel(ipg)
            g_tm = gbt[0:L, 0:8]
            beta = gbt[0:L, 32:40]
            yield
            self.tt("pool", h3(rhsG[0:L, 0:W8]), tri.unsqueeze(1).to_broadcast([L, 8, L]), g_tm.unsqueeze(2).to_broadcast([L, 8, L]),
                    ALU.mult, [tgbt, tC], [trG])
            pg2, tpg2, ipg2 = self.bacq()
            self.mm(pg2[0:L, 0:8], tri, g_tm, [tgbt, tC], [tpg2])
            Gt, tGt = B["Gt"]; eG, teG = B["eG"]; nbG, tnbG = B["nbG"]; nb, tnb = B["nb"]
            self.cp("dve", Gt[0:L, :], pg2[0:L, 0:8], [tpg2], [tGt])
            self.brel(ipg2)
            self.ts("dve", nb[0:L, :], beta, -1.0, ALU.mult, [tgbt], [tnb])
            yield
            pG, tpG, ipG = self.bacq()
            self.mm(pG[:, 0:W8], self.ones_f[0:L, :], rhsG[0:L, 0:W8], [trG, tC], [tpG])
            self.act(eG[0:L, :], Gt[0:L, :], AF.Exp, [tGt], [teG])
            self.tt("dve", nbG[0:L, :], eG[0:L, :], nb[0:L, :], ALU.mult, [teG, tnb], [tnbG])
            GtL, nbL, nbGL = Gt[0:L, :], nb[0:L, :], nbG[0:L, :]
            yield
        if blk:
            Glast = None
            gl4 = pG[:, 0:W8].rearrange("p (h s t) -> p h s t", s=8, t=8)[:, :, :, 7]
        else:
            Glast = h3(pG[:, 0:W8])[:, :, L - 1]
        Dm, tDm = B["Dm"]; Du, tDu = B["Du"]; Dl, tDl = B["Dl"]; eGbc, teGbc = B["eGbc"]
        self.tt("dve", h3(Dm[0:L, 0:W8]), h3(pG[0:L, 0:W8]), GtL.unsqueeze(2).to_broadcast([L, 8, L]), ALU.subtract,
                [tpG, tGt], [tDm])
        if blk:
            pgl, tpgl, ipgl = self.bacq()
            self.mm(pgl[0:L, 0:8], self.cst[0:L, C_SEL:C_SEL + L], GtL, [tGt, tC], [tpgl])
            self.tt("dve", dGl[0:L, :], pgl[0:L, 0:8], GtL, ALU.subtract, [tpgl, tGt], [tdGl])
            self.brel(ipgl)
            self.act(gtc[:, 0:64].rearrange("p (h s) -> p h s", s=8), gl4, AF.Exp, [tpG], [tgtc])
        else:
            self.tt("dve", dGl[0:L, :], Glast[0:L], GtL, ALU.subtract, [tpG, tGt], [tdGl])
            self.act(gtc[:, 0:8], Glast, AF.Exp, [tpG], [tgtc])
        self.act(eGbc[:, 0:W8], pG[:, 0:W8], AF.Exp, [tpG], [teGbc])
        self.brel(ipG)
        pk, tpk, ipk = self.bacq()
        pkb = pk[:].bitcast(BF16)
        for h in range(8):
            self.tr(pkb[0:L, h * 128:(h + 1) * 128], K(h), self.idb[:], [tC], [tpk])
        pv, tpv, ipv = self.bacq()
        pvb = pv[:].bitcast(BF16)
        for h in range(8):
            self.tr(pvb[0:L, h * 128:(h + 1) * 128], Vv(h), self.idb[:], [tC], [tpv])
        yield
        self.act(eGl[0:L, :], dGl[0:L, :], AF.Exp, [tdGl], [teGl])
        negu = self.cst[0:L, cN:cN + L].unsqueeze(1).to_broadcast([L, 8, L])
        posl = self.cst[0:L, cP:cP + L].unsqueeze(1).to_broadcast([L, 8, L])
        self.tt("pool", h3(Du[0:L, 0:W8]), h3(Dm[0:L, 0:W8]), negu, ALU.add, [tDm, tC], [tDu])
        self.tt("pool", h3(Dl[0:L, 0:W8]), h3(Dm[0:L, 0:W8]), posl, ALU.add, [tDm, tC], [tDl])
        kbgn, tkb = B["kbgn"]; kd, tkd = B["kd"]; vb, tvb = B["vb"]
        self.tt("dve", hd(kbgn[0:L, :]), hd(pkb[0:L, :]), nbGL.unsqueeze(2).to_broadcast([L, 8, 128]), ALU.mult, [tpk, tnbG], [tkb])
        self.tt("dve", hd(vb[0:L, :]), hd(pvb[0:L, :]), beta.unsqueeze(2).to_broadcast([L, 8, 128]), ALU.mult, [tpv, tgbt], [tvb])
        self.brel(ipv)
        yield
        self.tt("dve", hd(kd[0:L, :]), hd(pkb[0:L, :]), eGl[0:L, :].unsqueeze(2).to_broadcast([L, 8, 128]), ALU.mult, [tpk, teGl], [tkd])
        self.brel(ipk)
        self.act(Du[0:L, 0:W8], Du[0:L, 0:W8], AF.Exp, [tDu], [tDu])
        self.act(Dl[0:L, 0:W8], Dl[0:L, 0:W8], AF.Exp, [tDl], [tDl], scale=-1.0)
        pkk, tpkk, ipkk = self.bacq()
        for h in range(8):
            self.mm(pkk[0:L, h * L:(h + 1) * L], K(h), K(h), [], [tpkk])
        pkq, tpkq, ipkq = self.bacq()
        for h in range(8):
            self.mm(pkq[0:L, h * L:(h + 1) * L], K(h), Q(h), [], [tpkq])
        qdT, tqd = B["qdT"]
        self.tt("pool", h3(qdT[:, 0:W8]), QKVZ[:, 0:8, c0:c0 + L], h3(eGbc[:, 0:W8]), ALU.mult, [teGbc], [tqd])
        yield
        self.tt("pool", h3(Dl[0:L, 0:W8]), h3(Dl[0:L, 0:W8]), nbL.unsqueeze(2).to_broadcast([L, 8, L]), ALU.mult,
                [tDl, tnb], [tDl])
        Lneg, tLn = B["Lneg"]; At, tAt = B["At"]; M0, tM0 = B["M0"]
        self.tt("dve", At[0:L, 0:W8], pkq[0:L, 0:W8], Du[0:L, 0:W8], ALU.mult, [tpkq, tDu], [tAt])
        self.brel(ipkq)
        yield
        self.tt("dve", Lneg[0:L, 0:W8], pkk[0:L, 0:W8], Dl[0:L, 0:W8], ALU.mult, [tpkk, tDl], [tLn])
        self.brel(ipkk)
        yield
        pm, tpm, ipm = self.bacq()
        pmb = pm[:].bitcast(BF16)
        for h in range(8):
            self.tr(pmb[0:L, h * L:(h + 1) * L], Lneg[0:L, h * L:(h + 1) * L], self.idb[0:L, 0:L], [tLn, tC], [tpm])
        self.cp("act", M0[0:L, 0:W8], pmb[0:L, 0:W8], [tpm], [tM0])
        self.brel(ipm)
        yield
        nlev = 3 if blk else {64: 6, 16: 4, 8: 3}[L]
        idbL = self.idb[0:L, 0:L]
        PQ = [B["PQa"], B["PQb"]]
        QTbufs = [B["QTa"], B["QTb"]]
        pq3 = lambda ap: ap[0:L, :].rearrange("p (h c) -> p h c", c=128)
        cur = 0
        Pc, tPc = PQ[cur]
        self.tt("pool", pq3(Pc)[:, :, 0:L], h3(M0[0:L, 0:W8]), idbL.unsqueeze(1).to_broadcast([L, 8, L]), ALU.add, [tM0, tC], [tPc])
        pq, tpq, ipq = self.bacq()
        for h in range(8):
            sl = slice(h * L, (h + 1) * L)
            self.mm(pq[0:L, sl], M0[0:L, sl], Lneg[0:L, sl], [tM0, tLn], [tpq])
        QTc = QTbufs[0]
        self.cp("act", QTc[0][0:L, 0:W8], pq[0:L, 0:W8], [tpq], [QTc[1]])
        self.brel(ipq)
        pq2, tpq2, ipq2 = self.bacq()
        for h in range(8):
            sl = slice(h * L, (h + 1) * L)
            self.mm(pq2[0:L, sl], Lneg[0:L, sl], M0[0:L, sl], [tM0, tLn], [tpq2])
        self.cp("dve", pq3(Pc)[:, :, L:2 * L], h3(pq2[0:L, 0:W8]), [tpq2], [tPc])
        self.brel(ipq2)
        yield
        for k in range(1, nlev):
            last = (k == nlev - 1)
            Pn, tPn = PQ[1 - cur]
            wid = L if last else 2 * L
            if not last:
                QTn = QTbufs[k % 2]
                pq, tpq, ipq = self.bacq()
                for h in range(8):
                    self.mm(pq[0:L, h * L:(h + 1) * L], pq3(Pc)[:, h, L:2 * L], QTc[0][0:L, h * L:(h + 1) * L], [tPc, QTc[1]], [tpq])
                self.cp("act", QTn[0][0:L, 0:W8], pq[0:L, 0:W8], [tpq], [QTn[1]])
                self.brel(ipq)
            for half in range(2):
                pp, tpp, ipp = self.bacq()
                for hh in range(4):
                    h = half * 4 + hh
                    self.mm(pp[0:L, hh * 128:hh * 128 + wid], QTc[0][0:L, h * L:(h + 1) * L], pq3(Pc)[:, h, 0:wid], [tPc, QTc[1]], [tpp])
                ppv = pp[0:L, :].rearrange("p (h c) -> p h c", c=128)
                hs = slice(half * 4, half * 4 + 4)
                self.tt("dve", pq3(Pn)[:, hs, 0:L], ppv[:, :, 0:L], pq3(Pc)[:, hs, 0:L], ALU.add, [tpp, tPc], [tPn])
                if not last:
                    self.cp("act", pq3(Pn)[:, hs, L:2 * L], ppv[:, :, L:2 * L], [tpp], [tPn])
                self.brel(ipp)
            cur = 1 - cur
            Pc, tPc = PQ[cur]
            if not last:
                QTc = QTn
            yield
        Ttv = pq3(Pc)
        tTt = tPc
        pw, tpw, ipw = self.bacq()
        for h in range(8):
            self.mm(pw[:, h * L:(h + 1) * L], kbgn[0:L, h * 128:(h + 1) * 128], Ttv[:, h, 0:L], [tkb, tTt], [tpw])
        nWT, tnW = B["nWT"]
        self.cp("act", nWT[:, 0:W8], pw[:, 0:W8], [tpw], [tnW])
        self.brel(ipw)
        CA["Tt"] = (Ttv, tTt)
        yield

    def chunk_B(self, job, CA, TB, QKVZ, ONT):
        kind, c0, L, sidx = job
        B = dict(TB); B.update(CA)
        tC = self.tC
        Tt, tTt = CA["Tt"]
        h3 = lambda ap: ap.rearrange("p (h l) -> p h l", l=L)
        hd = lambda ap: ap.rearrange("p (h d) -> p h d", d=128)
        W8 = 8 * L
        if kind == "S":
            S32, tS32 = self.SS32[sidx % 2]
            S16, tS16 = self.SS16[sidx % 2]
            self.dma("sp", S32, self.st_rec[sidx].rearrange("h k v -> k h v"), (), [tS32], "ldrec%d" % (sidx % 2))
            self.cp("pool", S16, S32, [tS32], [tS16])
        else:
            S32, tS32 = self.S32[:], self.tok("S32")
            S16, tS16 = self.S16[:], self.tok("S16")
        vb, tvb = B["vb"]; nWT, tnW = B["nWT"]; vn, tvn = B["vn"]; qdT, tqd = B["qdT"]; At, tAt = B["At"]
        kd, tkd = B["kd"]; sqo, tsq = B["sqo"]; on, ton = B["on"]; ss, tss = B["ss"]; rs, trs = B["rs"]
        gtc, tgtc = B["gtc"]; Stmp, tSt = B["Stmp"]
        for half in range(2):
            pv, tpv, ipv = self.bacq()
            for hh in range(4):
                h = half * 4 + hh
                self.mm(pv[0:L, hh * 128:(hh + 1) * 128], Tt[:, h, 0:L], vb[0:L, h * 128:(h + 1) * 128], [tTt, tvb], [tpv], start=(hh == 0), stop=False, skip=True)
            for hh in range(4):
                h = half * 4 + hh
                self.mm(pv[0:L, hh * 128:(hh + 1) * 128], nWT[:, h * L:(h + 1) * L], S16[:, h, :], [tnW, tS16], [tpv], start=False, stop=True, skip=True)
            self.cp(("act", "dve")[half], vn[0:L, half * 512:(half + 1) * 512], pv[0:L, :], [tpv], [tvn])
            self.brel(ipv)
        self.tt("pool", hd(Stmp[:, :]), S32, gtc[:, 0:8].unsqueeze(2).to_broadcast([128, 8, 128]), ALU.mult, [tS32, tgtc], [tSt])
        yield
        pss = []
        for half in range(2):
            pS, tpS, ipS = self.bacq()
            pss.append((pS, tpS, ipS))
            for hh in range(4):
                h = half * 4 + hh
                self.mm(pS[:, hh * 128:(hh + 1) * 128], kd[0:L, h * 128:(h + 1) * 128], vn[0:L, h * 128:(h + 1) * 128], [tkd, tvn], [tpS])
        for half in range(2):
            pS, tpS, ipS = pss[half]
            self.tt("dve", S32[:, half * 4:(half + 1) * 4, :], hd(pS[:, :]), hd(Stmp[:, half * 512:(half + 1) * 512]), ALU.add, [tpS, tSt], [tS32])
            self.brel(ipS)
        pos = []
        for half in range(2):
            po, tpo, ipo = self.bacq()
            pos.append((po, tpo, ipo))
            for hh in range(4):
                h = half * 4 + hh
                self.mm(po[0:L, hh * 128:(hh + 1) * 128], qdT[:, h * L:(h + 1) * L], S16[:, h, :], [tqd, tS16], [tpo], start=(hh == 0), stop=False, skip=True)
            for hh in range(4):
                h = half * 4 + hh
                self.mm(po[0:L, hh * 128:(hh + 1) * 128], At[0:L, h * L:(h + 1) * L], vn[0:L, h * 128:(h + 1) * 128], [tAt, tvn], [tpo], start=False, stop=True, skip=True)
        self.cp("act", S16, S32, [tS32], [tS16])
        if kind == "S":
            self.dma("sp", self.o_srec[sidx].rearrange("h k v -> k h v"), S32, [tS32], (), "strec%d" % (sidx % 2))
        yield
        for half in range(2):
            po, tpo, ipo = pos[half]
            self.act(sqo[0:L, half * 512:(half + 1) * 512], po[0:L, :], AF.Square, [tpo], [tsq])
        self.P.op("dve", lambda e, o=ss[0:L, :], i=hd(sqo[0:L, :]): e.tensor_reduce(out=o, in_=i, axis=AX.X, op=ALU.add), [tsq], [tss])
        self.act(rs[0:L, :], ss[0:L, :], AF.Ln, [tss], [trs], bias=EPS, scale=1.0 / 128.0)
        self.act(rs[0:L, :], rs[0:L, :], AF.Exp, [trs], [trs], scale=-0.5)
        yield
        for half in range(2):
            po, tpo, ipo = pos[half]
            self.tt("dve", hd(on[0:L, half * 512:(half + 1) * 512]), hd(po[0:L, :]), rs[0:L, half * 4:(half + 1) * 4].unsqueeze(2).to_broadcast([L, 4, 128]),
                    ALU.mult, [tpo, trs], [ton])
            self.brel(ipo)
        yield
        pt, tpt, ipt = self.bacq()
        ptb = pt[:].bitcast(BF16)
        for h in range(8):
            self.tr(ptb[:, h * L:(h + 1) * L], on[0:L, h * 128:(h + 1) * 128], self.idb[0:L, 0:L], [ton, tC], [tpt])
        self.stt(ONT[:, :, c0:c0 + L], h3(ptb[:, 0:W8]), self.vcol("gnw"), QKVZ[:, 24:32, c0:c0 + L], ALU.mult, ALU.mult, [tpt, tC], [self.tok("ONT", self.phase_id)])
        self.brel(ipt)
        yield

    def sample_seq(self, job, s_, CA, SB, onacc_t):
        kind, c0, L, s0 = job
        tC = self.tC
        Tt, tTt = CA["Tt"]
        hd = lambda ap: ap.rearrange("p (h d) -> p h d", d=128)
        vb, tvb = CA["vb"]; nWT, tnW = CA["nWT"]; qdT, tqd = CA["qdT"]; At, tAt = CA["At"]
        kd, tkd = CA["kd"]; gtc, tgtc = CA["gtc"]
        S32, tS32 = SB["S32"]; S16, tS16 = SB["S16"]; Stmp, tSt = SB["Stmp"]; vn, tvn = SB["vn"]
        onacc, tacc = onacc_t
        gtc3 = gtc[:, 0:64].rearrange("p (h s) -> p h s", s=8)
        sidx = s0 + s_
        rm = self.cst[0:L, C_RM + s_:C_RM + s_ + 1]
        self.dma("sp", S32, self.st_rec[sidx].rearrange("h k v -> k h v"), (), [tS32], "ldrec%d" % (sidx % 3))
        yield
        self.cp("act", S16, S32, [tS32], [tS16])
        self.tt("pool", hd(Stmp[:, :]), S32, gtc3[:, :, s_].unsqueeze(2).to_broadcast([128, 8, 128]), ALU.mult, [tS32, tgtc], [tSt])
        yield
        for half in range(2):
            pv, tpv, ipv = self.bacq()
            for hh in range(4):
                h = half * 4 + hh
                self.mm(pv[0:L, hh * 128:(hh + 1) * 128], Tt[:, h, 0:L], vb[0:L, h * 128:(h + 1) * 128], [tTt, tvb], [tpv], start=(hh == 0), stop=False, skip=True)
            for hh in range(4):
                h = half * 4 + hh
                self.mm(pv[0:L, hh * 128:(hh + 1) * 128], nWT[:, h * L:(h + 1) * L], S16[:, h, :], [tnW, tS16], [tpv], start=False, stop=True, skip=True)
            if half == 0:
                self.act(vn[0:L, 0:512], pv[0:L, :], AF.Identity, [tpv, tC], [tvn], scale=rm)
            else:
                self.ts("dve", vn[0:L, 512:1024], pv[0:L, :], rm, ALU.mult, [tpv, tC], [tvn])
            self.brel(ipv)
        yield
        pss = []
        for half in range(2):
            pS, tpS, ipS = self.bacq()
            pss.append((pS, tpS, ipS))
            for hh in range(4):
                h = half * 4 + hh
                self.mm(pS[:, hh * 128:(hh + 1) * 128], kd[0:L, h * 128:(h + 1) * 128], vn[0:L, h * 128:(h + 1) * 128], [tkd, tvn], [tpS])
        for half in range(2):
            pS, tpS, ipS = pss[half]
            self.tt("dve", S32[:, half * 4:(half + 1) * 4, :], hd(pS[:, :]), hd(Stmp[:, half * 512:(half + 1) * 512]), ALU.add, [tpS, tSt], [tS32])
            self.brel(ipS)
        self.dma("sp", self.o_srec[sidx].rearrange("h k v -> k h v"), S32, [tS32], (), "strec%d" % (sidx % 3))
        pos = []
        for half in range(2):
            po, tpo, ipo = self.bacq()
            pos.append((po, tpo, ipo))
            for hh in range(4):
                h = half * 4 + hh
                self.mm(po[0:L, hh * 128:(hh + 1) * 128], qdT[:, h * L:(h + 1) * L], S16[:, h, :], [tqd, tS16], [tpo], start=(hh == 0), stop=False, skip=True)
            for hh in range(4):
                h = half * 4 + hh
                self.mm(po[0:L, hh * 128:(hh + 1) * 128], At[0:L, h * L:(h + 1) * L], vn[0:L, h * 128:(h + 1) * 128], [tAt, tvn], [tpo], start=False, stop=True, skip=True)
        yield
        for half in range(2):
            po, tpo, ipo = pos[half]
            acc = onacc[0:L, half * 512:(half + 1) * 512]
            if s_ == 0:
                self.ts("dve", acc, po[0:L, :], rm, ALU.mult, [tpo, tC], [tacc])
            else:
                self.stt(acc, po[0:L, :], rm, acc, ALU.mult, ALU.add, [tpo, tC, tacc], [tacc])
            self.brel(ipo)
        yield

    def sample_finish(self, job, TB, onacc_t, QKVZ, ONT):
        kind, c0, L, s0 = job
        tC = self.tC
        h3 = lambda ap: ap.rearrange("p (h l) -> p h l", l=L)
        hd = lambda ap: ap.rearrange("p (h d) -> p h d", d=128)
        W8 = 8 * L
        sqo, tsq = TB["sqo"]; on, ton = TB["on"]; ss, tss = TB["ss"]; rs, trs = TB["rs"]
        onacc, tacc = onacc_t
        self.act(sqo[0:L, :], onacc[0:L, :], AF.Square, [tacc], [tsq])
        self.P.op("dve", lambda e, o=ss[0:L, :], i=hd(sqo[0:L, :]): e.tensor_reduce(out=o, in_=i, axis=AX.X, op=ALU.add), [tsq], [tss])
        self.act(rs[0:L, :], ss[0:L, :], AF.Ln, [tss], [trs], bias=EPS, scale=1.0 / 128.0)
        self.act(rs[0:L, :], rs[0:L, :], AF.Exp, [trs], [trs], scale=-0.5)
        yield
        self.tt("dve", hd(on[0:L, :]), hd(onacc[0:L, :]), rs[0:L, :].unsqueeze(2).to_broadcast([L, 8, 128]), ALU.mult, [tacc, trs], [ton])
        yield
        pt, tpt, ipt = self.bacq()
        ptb = pt[:].bitcast(BF16)
        for h in range(8):
            self.tr(ptb[:, h * L:(h + 1) * L], on[0:L, h * 128:(h + 1) * 128], self.idb[0:L, 0:L], [ton, tC], [tpt])
        self.stt(ONT[:, :, c0:c0 + L], h3(ptb[:, 0:W8]), self.vcol("gnw"), QKVZ[:, 24:32, c0:c0 + L], ALU.mult, ALU.mult, [tpt, tC], [self.tok("ONT", self.phase_id)])
        self.brel(ipt)
        yield

    def ffn(self, st, l):
        NT = st.NT
        self.phase()
        hasS = any(s.kind == "S" for s in st.segs)
        xn, _ = self.A([128, 8, NT], BF16)
        hT, _ = self.A([128, 22, NT], BF16)
        sq2 = [self.A([128, 8, 512], BF16) for _ in range(2)]
        rsb = [self.A([128, 512], F32) for _ in range(2)]
        wsl = [self.A([128, 2, 8, 256], BF16) for _ in range(3)]
        wsl = [(w_, (t_, self.tok("wslb", self.phase_id, i_))) for i_, (w_, t_) in enumerate(wsl)]
        wdn = [self.A([128, 22, 128], BF16) for _ in range(3)]
        ub = [self.A([128, 2 + NTMAX], F32) for _ in range(4)]
        t0b = [self.A([128, 512], F32) for _ in range(4)]
        sab = [self.A([128, 512], F32) for _ in range(2)]
        if hasS:
            SHF, tSHF = self.A([128, NFC, 32], F32)
            stg, tstg = self.A([32, 5632], F32)
        xnt = lambda dc, tile: (xn[:, dc, tile[0].off + tile[1]:tile[0].off + tile[1] + tile[2]], self.tok("xn", self.phase_id, dc, tile[0].off + tile[1]))
        self.norm(st, "nf%d" % l, xnt, sq2, rsb)
        if hasS:
            self.dma("sp", stg, self.st_ffn[l], (), [tstg], "ldst")
            for g in range(0, NFC, 8):
                pb, tpb = self.bank()
                ng = min(8, NFC - g)
                for j in range(ng):
                    fc = g + j
                    self.tr(pb[:, j * 32:(j + 1) * 32], stg[0:32, fc * 128:(fc + 1) * 128], self.idf[0:32, 0:32], [tstg, self.tC], [tpb])
                self.cp("act", SHF[:, g:g + ng, :], pb[:, 0:ng * 32].rearrange("p (a b) -> p a b", b=32), [tpb], [tSHF])
        tHF = self.tok("HF")
        fcw, fcb = "fcw%d" % l, "fcb%d" % l
        def ld_wup(u):
            wt_, twt_ = wsl[u % 3]
            self.dma("pool", wt_[:, 0], self.w_up[l][:, u * 256:(u + 1) * 256].rearrange("(kc p) f -> p kc f", p=128), (), [twt_[0]], "ldw%d" % (u % 3))
            self.dma("pool", wt_[:, 1], self.w_up[l][:, DFF + u * 256:DFF + (u + 1) * 256].rearrange("(kc p) f -> p kc f", p=128), (), [twt_[1]], "ldwb%d" % (u % 3))

        def ld_wdn(dc):
            wd_, twd_ = wdn[dc % 3]
            self.dma("pool", wd_, self.w_down[l][:, dc * 128:(dc + 1) * 128].rearrange("(i p) d -> p i d", p=128), (), [twd_], "ldwd%d" % (dc % 3))
        ld_wup(0); ld_wup(1)
        pend = []
        for u in range(11):
            wt, twt = wsl[u % 3]
            if u + 2 < 11:
                ld_wup(u + 2)
            elif u + 2 == 11:
                ld_wdn(0)
            else:
                ld_wdn(1)
            for j in range(2):
                i = u * 2 + j
                cur_ub = [None, None]
                for tile in st.tiles():
                    seg, t0, n = tile
                    c0 = seg.off + t0
                    conv = []
                    for ab in range(2):
                        fc = i + 22 * ab
                        pb, tpb = self.bank()
                        for kc in range(8):
                            self.mm(pb[:, 0:n], wt[:, ab, kc, j * 128:(j + 1) * 128], xn[:, kc, c0:c0 + n], [twt[ab], xnt(kc, tile)[1]], [tpb],
                                    start=(kc == 0), stop=(kc == 7))
                        if seg.kind == "S":
                            bi = self.rr("ub", [0, 1, 2, 3])
                            ex = ub[bi][0]
                            tex = self.tok("ubL", self.phase_id, bi, 0)
                            rtoks = [tex]
                            self.cp("pool", self.ext_halo(seg, ex, 2), SHF[:, fc, :].rearrange("p (s r) -> p s r", r=2), [tSHF], [tex])
                            self.cp("act", self.ext_dst(seg, ex, 2, n), self.V(seg, pb[:, 0:n]), [tpb], [tex])
                            self.cp("pool", SHF[:, fc, :].rearrange("p (s r) -> p s r", r=2), self.ext_tail(seg, ex, 2, n), [tex], [tSHF])
                        else:
                            if t0 == 0:
                                bi = self.rr("ub", [0, 1, 2, 3])
                                cur_ub[ab] = (bi, 0)
                                tex = self.tok("ubL", self.phase_id, bi, 0)
                                self.cp("act", ub[bi][0][:, 0:2], self.HF[:, l, fc, :], [tHF], [tex])
                                rtoks = [tex]
                            else:
                                bi, ti = cur_ub[ab]
                                cur_ub[ab] = (bi, ti + 1)
                                tex = self.tok("ubL", self.phase_id, bi, ti + 1)
                                rtoks = [tex, self.tok("ubL", self.phase_id, bi, ti)]
                            ex = ub[bi][0][:, t0:t0 + 2 + n]
                            self.cp("act", ex[:, 2:2 + n], pb[:, 0:n], [tpb], [tex])
                            if t0 + n == seg.n:
                                self.cp("act", self.HF[:, l, fc, :], ex[:, n:n + 2], [tex], [tHF])
                        tb, ttb = t0b[self.rr("t0b", [0, 1, 2, 3])]
                        tv = self.V(seg, tb[:, 0:n])
                        self.act(tv, self.V(seg, pb[:, 0:n]), AF.Identity, [tpb, self.tC], [ttb], bias=self.vcol(fcb, fc), scale=self.vcol(fcw, 2 * NFC + fc))
                        self.stt(tv, self.ext_tap(seg, ex, 2, 1, n), self.vcol(fcw, 1 * NFC + fc), tv, ALU.mult, ALU.add, rtoks + [ttb, self.tC], [ttb])
                        self.stt(tv, self.ext_tap(seg, ex, 2, 0, n), self.vcol(fcw, 0 * NFC + fc), tv, ALU.mult, ALU.add, rtoks + [ttb, self.tC], [ttb])
                        conv.append((tb, ttb))
                    def tail(conv=conv, i=i, c0=c0, n=n):
                        sa, tsa = sab[self.rr("sab", [0, 1])]
                        self.act(sa[:, 0:n], conv[0][0][:, 0:n], AF.Silu, [conv[0][1]], [tsa])
                        self.tt("dve", hT[:, i, c0:c0 + n], sa[:, 0:n], conv[1][0][:, 0:n], ALU.mult, [tsa, conv[1][1]], [self.tok("hT", self.phase_id, i, c0)])
                    if pend:
                        pend.pop(0)()
                    pend.append(tail)
        while pend:
            pend.pop(0)()
        if hasS:
            for fc in range(NFC):
                pb, tpb = self.bank()
                self.tr(pb[0:32, 0:128], SHF[:, fc, :], self.idf[:, :], [tSHF, self.tC], [tpb])
                self.cp(self.rr("ev", ["act", "dve"]), stg[0:32, fc * 128:(fc + 1) * 128], pb[0:32, 0:128], [tpb], [tstg])
            self.dma("sp", self.o_sffn[l], stg, [tstg], (), "stst")
        if st.last:
            stg2, tstg2 = self.A([2, 5632], F32)
            for fc in range(NFC):
                pb, tpb = self.bank()
                self.tr(pb[0:2, 0:128], self.HF[:, l, fc, :], self.idf[:, :], [tHF, self.tC], [tpb])
                self.cp(self.rr("ev", ["act", "dve"]), stg2[0:2, fc * 128:(fc + 1) * 128], pb[0:2, 0:128], [tpb], [tstg2])
            self.dma("sp", self.o_pffn[l], stg2, [tstg2], (), "stst")
        for dc in range(8):
            wd, twd = wdn[dc % 3]
            if dc + 2 < 8:
                ld_wdn(dc + 2)
            for tile in st.tiles():
                seg, t0, n = tile
                c0 = seg.off + t0
                pb, tpb = self.bank()
                for i in range(22):
                    self.mm(pb[:, 0:n], wd[:, i, :], hT[:, i, c0:c0 + n], [twd, self.tok("hT", self.phase_id, i, c0)], [tpb], start=(i == 0), stop=(i == 21))
                xv = self.xT[:, dc, c0:c0 + n]
                self.tt("dve", xv, pb[:, 0:n], xv, ALU.add, [tpb], [self.xtok(dc, tile)])

    def pool_mixer(self, st):
        self.phase()
        sq2 = [self.A([128, 8, 512], BF16) for _ in range(2)]
        rsb = [self.A([128, 512], F32) for _ in range(2)]
        pw, tpw = self.A([128, 4, 2, 256], BF16)
        self.dma("pool", pw, self.pool_w.rearrange("g (ci p) e -> p g ci e", p=128), (), [tpw], "ldw0")
        segbuf = {}
        for seg in st.segs:
            W = 16 * 23 if seg.kind == "S" else 15 + seg.n
            hn, _ = self.A([128, 8, W], F32)
            s1, _ = self.A([128, 2, W], F32)
            s2, _ = self.A([128, 2, W], F32)
            PL, _ = self.A([128, 8, seg.n], BF16)
            segbuf[id(seg)] = (hn, s1, s2, PL, W)
        if any(s.kind == "S" for s in st.segs):
            stg, tstg = self.A([120, 2, D], F32)
            self._pcb = [self.A([128, 120], F32) for _ in range(2)]
        tmp15, ttmp15 = self.A([128, 15], F32)
        tHP = self.tok("HP")

        def dstf(dc, tile):
            seg, t0, n = tile
            hn = segbuf[id(seg)][0]
            if seg.kind == "S":
                ap = hn[:, dc, :].rearrange("p (s w) -> p s w", w=23)[:, :, 15:23]
            else:
                ap = hn[:, dc, 15 + t0:15 + t0 + n]
            return ap, self.tok("hn", self.phase_id, id(seg), dc)

        self._norm_pool(st, "nm1", dstf, sq2, rsb)
        for seg in st.segs:
            hn, s1, s2, PL, W = segbuf[id(seg)]
            n = seg.n
            if seg.kind == "S":
                for half in range(2):
                    self.dma("sp", stg[:, half, :], self.st_pool[half * 120:(half + 1) * 120, :], (), [tstg], "ldst")
                for dc in range(8):
                    pb, tpb = self.bank()
                    for half in range(2):
                        self.tr(pb[:, half * 120:(half + 1) * 120], stg[0:120, half, dc * 128:(dc + 1) * 128], self.idf[0:120, 0:120], [tstg, self.tC], [tpb])
                    self.cp(self.rr("ev", ["act", "dve"]), hn[:, dc, :].rearrange("p (s w) -> p s w", w=23)[:, :, 0:15],
                            pb[:, 0:240].rearrange("p (s r) -> p s r", r=15), [tpb], [self.tok("hn", self.phase_id, id(seg), dc)])
            else:
                for dc in range(8):
                    self.cp("pool", hn[:, dc, 0:15], self.HP[:, dc, :], [tHP], [self.tok("hn", self.phase_id, id(seg), dc)])
            if seg.kind == "S":
                e3 = lambda ap: ap.rearrange("p (s w) -> p s w", w=23)
                sl = lambda ap, a, b: e3(ap)[:, :, a:b]
                WW = 23
            else:
                sl = lambda ap, a, b: ap[:, a:b]
                WW = W
            for dc in range(8):
                gi = dc // 2
                th = self.tok("hn", self.phase_id, id(seg), dc)
                ts1 = self.tok("ps1", self.phase_id, id(seg), dc % 2)
                ts2 = self.tok("ps2", self.phase_id, id(seg), dc % 2)
                src, tsrc = hn[:, dc, :], th
                bufs = [(s1[:, dc % 2, :], ts1), (s2[:, dc % 2, :], ts2)]
                for lev in range(gi + 1):
                    sh = 1 << lev
                    lo = (1 << (lev + 1)) - 1
                    dstb, tdb = bufs[lev % 2]
                    self.tt(("pool", "dve")[dc % 2], sl(dstb, lo, WW), sl(src, lo, WW), sl(src, lo - sh, WW - sh), ALU.add, [tsrc], [tdb])
                    src, tsrc = dstb, tdb
                if seg.kind == "S":
                    outv = PL[:, dc, :].rearrange("p (s t) -> p s t", t=8)
                else:
                    outv = PL[:, dc, :]
                tPL = self.tok("PL", self.phase_id, id(seg), dc)
                self.stt(outv, sl(src, 15, WW), 1.0 / WINS[gi], sl(hn[:, dc, :], 15, WW), ALU.mult, ALU.subtract, [tsrc, th], [tPL])
                if seg.kind == "P" and seg.pos0 == 0:
                    ic = self.cst[:, C_INVC + gi * 15:C_INVC + gi * 15 + 15]
                    self.tt("dve", tmp15, src[:, 15:30], ic, ALU.mult, [tsrc, self.tC], [ttmp15])
                    self.tt("dve", PL[:, dc, 0:15], tmp15, hn[:, dc, 15:30], ALU.subtract, [ttmp15, th], [tPL])
            if seg.kind == "S":
                for dc in range(8):
                    th = self.tok("hn", self.phase_id, id(seg), dc)
                    for half in range(2):
                        pb, tpb = self.bank()
                        src3 = hn[:, dc, :].rearrange("p (s w) -> p s w", w=23)[:, half * 8:(half + 1) * 8, 8:23]
                        cbuf, tcb = self._pcb[self.rr("pcb", [0, 1])]
                        self.cp("pool", cbuf.rearrange("p (s r) -> p s r", r=15), src3, [th], [tcb])
                        self.tr(pb[0:120, 0:128], cbuf, self.idf[:, :], [tcb, self.tC], [tpb])
                        self.cp(self.rr("ev", ["act", "dve"]), stg[0:120, half, dc * 128:(dc + 1) * 128], pb[0:120, 0:128], [tpb], [tstg])
                for half in range(2):
                    self.dma("sp", self.o_spool[half * 120:(half + 1) * 120, :], stg[:, half, :], [tstg], (), "stst")
            else:
                for dc in range(8):
                    th = self.tok("hn", self.phase_id, id(seg), dc)
                    self.cp("pool", self.HP[:, dc, :], hn[:, dc, n:n + 15], [th], [tHP])
                if st.last:
                    stg2, tstg2 = self.A([15, D], F32)
                    for dc in range(8):
                        pb, tpb = self.bank()
                        self.tr(pb[0:15, 0:128], self.HP[:, dc, :], self.idf[:, :], [tHP, self.tC], [tpb])
                        self.cp(self.rr("ev", ["act", "dve"]), stg2[0:15, dc * 128:(dc + 1) * 128], pb[0:15, 0:128], [tpb], [tstg2])
                    self.dma("sp", self.o_ppool, stg2, [tstg2], (), "stst")
            for tile in seg.tiles():
                _, t0, nn = tile
                c0 = seg.off + t0
                for gi in range(4):
                    for eo in range(2):
                        dco = 2 * gi + eo
                        pb, tpb = self.bank()
                        for ci in range(2):
                            self.mm(pb[:, 0:nn], pw[:, gi, ci, eo * 128:(eo + 1) * 128], PL[:, 2 * gi + ci, t0:t0 + nn],
                                    [tpw, self.tok("PL", self.phase_id, id(seg), 2 * gi + ci)], [tpb], start=(ci == 0), stop=(ci == 1))
                        xv = self.xT[:, dco, c0:c0 + nn]
                        self.stt(xv, pb[:, 0:nn], self.vcol("psc", dco), xv, ALU.mult, ALU.add, [tpb, self.tC], [self.xtok(dco, tile)])

    def _norm_pool(self, st, wname, dstf, sq2, rsb):
        for tile in st.tiles():
            seg, t0, n = tile
            c0 = seg.off + t0
            sq, tsq = sq2[self.rr("sq2", [0, 1])]
            rs, trs = rsb[self.rr("rsb", [0, 1])]
            pb, tpb = self.bank()
            for dc in range(8):
                xin = self.xT[:, dc, c0:c0 + n]
                self.act(sq[:, dc, 0:n], xin, AF.Square, [self.xtok(dc, tile)], [tsq])
            for dc in range(8):
                self.mm(pb[:, 0:n], self.ones_m[:], sq[:, dc, 0:n], [tsq, self.tC], [tpb], start=(dc == 0), stop=(dc == 7))
            self.act(rs[:, 0:n], pb[:, 0:n], AF.Ln, [tpb], [trs], bias=EPS)
            self.act(rs[:, 0:n], rs[:, 0:n], AF.Exp, [trs], [trs], scale=-0.5)
            for dc in range(8):
                dst, tdst = dstf(dc, tile)
                self.stt(dst, self.V(seg, self.xT[:, dc, c0:c0 + n]), self.vcol(wname, dc), self.V(seg, rs[:, 0:n]), ALU.mult, ALU.mult,
                         [self.xtok(dc, tile), trs, self.tC], [tdst])

    def final(self, st):
        self.phase()
        sq2 = [self.A([128, 8, 512], BF16) for _ in range(2)]
        rsb = [self.A([128, 512], F32) for _ in range(2)]
        yT = [self.A([128, 8, 512], F32) for _ in range(2)]
        ysg = [self.A([128, D], F32) for _ in range(3)]
        cur = {}

        def dstf(dc, tile):
            return cur["y"][0][:, dc, 0:tile[2]], cur["y"][1]

        for tile in st.tiles():
            seg, t0, n = tile
            cur["y"] = yT[self.rr("yT", [0, 1])]
            self._norm_one(tile, "nfin", dstf, sq2, rsb)
            y, ty = cur["y"]
            b0 = 0
            while b0 < n:
                if seg.kind == "P":
                    pos = seg.pos0 + t0 + b0
                    if pos < NMETA:
                        b0 += NMETA - pos
                        continue
                m = min(128, n - b0)
                sg, tsg = ysg[self.rr("ysg", [0, 1, 2])]
                for half in range(2):
                    pb, tpb = self.bank()
                    for j in range(4):
                        dc = half * 4 + j
                        self.tr(pb[0:m, j * 128:(j + 1) * 128], y[:, dc, b0:b0 + m], self.idf[:, :], [ty, self.tC], [tpb])
                    self.cp(("act", "dve")[half], sg[0:m, half * 512:(half + 1) * 512], pb[0:m, :], [tpb], [tsg])
                if seg.kind == "S":
                    self.dma("sp", self.ys[b0:b0 + m, :], sg[0:m, :], [tsg], (), "sty%d" % ((self.rrc["ysg"] - 1) % 3))
                else:
                    r0 = seg.pos0 + t0 + b0 - NMETA
                    self.dma("sp", self.yp[r0:r0 + m, :], sg[0:m, :], [tsg], (), "sty%d" % ((self.rrc["ysg"] - 1) % 3))
                b0 += m

    def _norm_one(self, tile, wname, dstf, sq2, rsb):
        seg, t0, n = tile
        c0 = seg.off + t0
        sq, tsq = sq2[self.rr("sq2", [0, 1])]
        rs, trs = rsb[self.rr("rsb", [0, 1])]
        pb, tpb = self.bank()
        for dc in range(8):
            xin = self.xT[:, dc, c0:c0 + n]
            self.act(sq[:, dc, 0:n], xin, AF.Square, [self.xtok(dc, tile)], [tsq])
        for dc in range(8):
            self.mm(pb[:, 0:n], self.ones_m[:], sq[:, dc, 0:n], [tsq, self.tC], [tpb], start=(dc == 0), stop=(dc == 7))
        self.act(rs[:, 0:n], pb[:, 0:n], AF.Ln, [tpb], [trs], bias=EPS)
        self.act(rs[:, 0:n], rs[:, 0:n], AF.Exp, [trs], [trs], scale=-0.5)
        for dc in range(8):
            dst, tdst = dstf(dc, tile)
            self.stt(dst, self.xT[:, dc, c0:c0 + n], self.vcol(wname, dc), rs[:, 0:n], ALU.mult, ALU.mult,
                     [self.xtok(dc, tile), trs, self.tC], [tdst])


_NC_CACHE = {}


def _get_nc():
    if "nc" not in _NC_CACHE:
        b = Builder()
        _NC_CACHE["nc"] = b.build()
    return _NC_CACHE["nc"]


def kernel(**inp):
    inp = {k: np.asarray(v) for k, v in inp.items()}
    f = lambda a: np.ascontiguousarray(a, dtype=np.float32)
    nc = _get_nc()
    vecs = build_vecs(inp)
    consts = build_consts()
    w_in = f(inp["gdn_w_in"][0])
    wba = np.zeros((D, 40), np.float32)
    wba[:, 0:8] = w_in[:, 4104:4112]
    wba[:, 32:40] = w_in[:, 4096:4104]
    shared = {
        "meta": f(inp["meta_tokens"]), "w_in": w_in, "wba": wba, "w_out": f(inp["gdn_w_out"][0]),
        "pool_w": f(inp["pool_w"][0]), "w_up": f(inp["ffn_w_up"]), "w_down": f(inp["ffn_w_down"]),
        "vecs": vecs, "consts": consts,
    }
    in_maps = []
    for c in range(8):
        sl = slice(16 * c, 16 * c + 16)
        m = dict(shared)
        m["xp"] = f(inp["x_prompt"][c])
        m["xs"] = f(inp["x_sample"][sl].reshape(128, D))
        m["st_conv"] = f(inp["state_gdn_conv"][0, sl].reshape(48, 3072))
        m["st_rec"] = f(inp["state_gdn_rec"][0, sl])
        m["st_pool"] = f(inp["state_pool"][0, sl].reshape(240, D))
        m["st_ffn"] = f(inp["state_ffn_conv"][:, sl].reshape(2, 32, 5632))
        in_maps.append(m)
    res = run_bass_kernel_spmd(nc, in_maps, core_ids=list(range(8)))
    R = res.results
    g = lambda k: [np.asarray(r[k], dtype=np.float32) for r in R]
    y_prompt = np.stack(g("yp"), 0)
    y_sample = np.concatenate(g("ys"), 0).reshape(128, 8, D)
    p_conv = np.stack(g("o_pconv"), 0)[None]
    p_rec = np.stack(g("o_prec"), 0)[None]
    p_pool = np.stack(g("o_ppool"), 0)[None]
    p_ffn = np.stack(g("o_pffn"), 1)
    s_conv = np.concatenate([a.reshape(16, 3, 3072) for a in g("o_sconv")], 0)[None]
    s_rec = np.concatenate(g("o_srec"), 0)[None]
    s_pool = np.concatenate([a.reshape(16, 15, D) for a in g("o_spool")], 0)[None]
    s_ffn = np.concatenate([a.reshape(2, 16, 2, 5632) for a in g("o_sffn")], 1)
    return (y_prompt, y_sample, p_conv, p_rec, p_pool, p_ffn, s_conv, s_rec, s_pool, s_ffn)
```

```python
import contextlib
import numpy as np
import concourse.bass as bass
import concourse.mybir as mybir
from concourse.bass_utils import run_bass_kernel_spmd

F32 = mybir.dt.float32
BF16 = mybir.dt.bfloat16
ALU = mybir.AluOpType
AF = mybir.ActivationFunctionType
AX = mybir.AxisListType

D = 1024
NH = 8
DFF = 2816
NFC = 44
SEQ = 2048
NMETA = 16
EPS = 1e-6
NEG = -1.0e30
DEBUG_MAP = None
WINS = (2, 4, 8, 16)


class Tok:
    __slots__ = ("lastw", "readers", "excl")

    def __init__(self):
        self.lastw = None
        self.readers = []
        self.excl = False


class Op:
    __slots__ = ("eng", "fn", "deps", "ms", "dma_sem", "dma_val", "is_dma", "where")

    def __init__(self, eng, fn):
        import sys as _s
        f = _s._getframe(3)
        self.where = (f.f_lineno, f.f_back.f_lineno if f.f_back else 0)
        self.eng = eng
        self.fn = fn
        self.deps = []
        self.ms = None
        self.is_dma = False
        self.dma_sem = None
        self.dma_val = 0


class Prog:
    ENGS = ("pe", "act", "dve", "pool", "sp")

    def __init__(self, nc):
        self.nc = nc
        self.ops = {e: [] for e in self.ENGS}
        self.streams = {}
        self.pending = {}

    def barrier(self):
        lasts = [self.ops[e][-1] for e in self.ENGS if self.ops[e]]
        lasts += [st[0] for st in self.streams.values() if st[0] is not None]
        for e in self.ENGS:
            self.pending[e] = list(lasts)

    def op(self, eng, fn, reads=(), writes=(), stream=None):
        o = Op(eng, fn)
        is_dma = stream is not None
        deps = []
        for t in reads:
            if t.lastw is not None:
                deps.append((t.lastw, True))
            if t.excl:
                for r in t.readers:
                    if r.eng != eng:
                        deps.append((r, True))
        for t in writes:
            if t.lastw is not None:
                deps.append((t.lastw, False))
            for r in t.readers:
                deps.append((r, False))
        for d in self.pending.pop(eng, []):
            deps.append((d, True))
        if is_dma:
            o.is_dma = True
            st = self.streams.setdefault(stream, [None, 0])
            if st[0] is not None:
                deps.append((st[0], True))
            st[1] += 1
            o.dma_sem = stream
            o.dma_val = 16 * st[1]
            st[0] = o
        seen = set()
        for d, raw in deps:
            if d is o or id(d) in seen:
                continue
            if (not d.is_dma) and (not is_dma) and d.eng == eng and eng == "pe":
                continue
            seen.add(id(d))
            o.deps.append(d)
        for t in reads:
            t.readers.append(o)
        for t in writes:
            t.lastw = o
            t.readers = []
        self.ops[eng].append(o)
        return o

    def emit(self):
        nc = self.nc
        for e in self.ENGS:
            for o in self.ops[e]:
                for d in o.deps:
                    if not d.is_dma:
                        d.ms = True
        for e in self.ENGS:
            k = 0
            for o in self.ops[e]:
                if o.ms and not o.is_dma:
                    k += 1
                    o.ms = k
        with contextlib.ExitStack() as es:
            esem = {e: es.enter_context(nc.semaphore("s_" + e)) for e in self.ENGS}
            dsem = {k: es.enter_context(nc.semaphore("d_%d" % i)) for i, k in enumerate(self.streams)}
            block = es.enter_context(nc.Block())
            prog = self

            def run(e, engobj):
                seen = {}
                for o in prog.ops[e]:
                    for d in o.deps:
                        if d.is_dma:
                            key, val, sem = ("d", d.dma_sem), d.dma_val, dsem[d.dma_sem]
                        else:
                            key, val, sem = ("e", d.eng), d.ms, esem[d.eng]
                        if seen.get(key, 0) >= val:
                            continue
                        seen[key] = val
                        engobj.wait_ge(sem, val)
                    ins = o.fn(engobj)
                    if DEBUG_MAP is not None:
                        try:
                            DEBUG_MAP[str(ins.ins.name)] = o.where
                        except Exception as ex:
                            DEBUG_MAP["err"] = repr(ex)
                    if o.is_dma:
                        ins.then_inc(dsem[o.dma_sem], 16)
                    elif o.ms:
                        ins.then_inc(esem[e], 1)
                if e == "sp":
                    for k, st in prog.streams.items():
                        engobj.wait_ge(dsem[k], 16 * st[1])

            block.tensor(lambda eng: run("pe", eng))
            block.scalar(lambda eng: run("act", eng))
            block.vector(lambda eng: run("dve", eng))
            block.gpsimd(lambda eng: run("pool", eng))
            block.sync(lambda eng: run("sp", eng))


VEC_COLS = {}


def _vec_layout():
    off = 0
    for name, n in (("nm0", 8), ("nm1", 8), ("nf0", 8), ("nf1", 8), ("nfin", 8),
                    ("gcw", 96), ("fcw0", 132), ("fcw1", 132), ("fcb0", 44), ("fcb1", 44),
                    ("psc", 8), ("gnw", 1), ("alog", 1), ("dtb", 1)):
        VEC_COLS[name] = off
        off += n
    return off


NV = _vec_layout()
C_ID, C_TRI, C_NEGU, C_POSL, C_INVC = 0, 128, 192, 256, 320
C_TRI8, C_NEGU8, C_POSL8, C_SEL, C_RM = 380, 444, 508, 572, 636
NCONST = 636 + 8


def build_consts():
    c = np.zeros((128, NCONST), np.float32)
    c[:, C_ID:C_ID + 128] = np.eye(128, dtype=np.float32)
    p = np.arange(64)[:, None]
    f = np.arange(64)[None, :]
    c[:64, C_TRI:C_TRI + 64] = (f >= p).astype(np.float32)
    c[:64, C_NEGU:C_NEGU + 64] = np.where(f >= p, 0.0, NEG)
    c[:64, C_POSL:C_POSL + 64] = np.where(f < p, 0.0, -NEG)
    for gi, w in enumerate(WINS):
        for t in range(15):
            c[:, C_INVC + gi * 15 + t] = 1.0 / min(w, t + 1)
    same = (p // 8) == (f // 8)
    c[:64, C_TRI8:C_TRI8 + 64] = (same & (f >= p)).astype(np.float32)
    c[:64, C_NEGU8:C_NEGU8 + 64] = np.where(same & (f >= p), 0.0, NEG)
    c[:64, C_POSL8:C_POSL8 + 64] = np.where(same & (f < p), 0.0, -NEG)
    c[:64, C_SEL:C_SEL + 64] = (p == 8 * (f // 8) + 7).astype(np.float32)
    c[:64, C_RM:C_RM + 8] = ((p // 8) == np.arange(8)[None, :]).astype(np.float32)
    return c


def build_vecs(inp):
    v = np.zeros((128, NV), np.float32)

    def put(name, arr):
        a = np.asarray(arr, np.float32).reshape(-1, 128).T
        v[:, VEC_COLS[name]:VEC_COLS[name] + a.shape[1]] = a

    put("nm0", inp["norm_mix"][0]); put("nm1", inp["norm_mix"][1])
    put("nf0", inp["norm_ffn"][0]); put("nf1", inp["norm_ffn"][1])
    put("nfin", inp["norm_final"])
    put("gcw", inp["gdn_conv_w"][0].reshape(-1))
    put("fcw0", inp["ffn_conv_w"][0].reshape(-1)); put("fcw1", inp["ffn_conv_w"][1].reshape(-1))
    put("fcb0", inp["ffn_conv_b"][0]); put("fcb1", inp["ffn_conv_b"][1])
    put("psc", inp["pool_scale"][0])
    put("gnw", inp["gdn_norm_w"][0])
    v[0:8, VEC_COLS["alog"]] = inp["gdn_A_log"][0]
    v[0:8, VEC_COLS["dtb"]] = inp["gdn_dt_bias"][0]
    return v


class Seg:
    def __init__(self, kind, n, off, pos0=0):
        self.kind, self.n, self.off, self.pos0 = kind, n, off, pos0

    def tiles(self):
        if self.kind == "S":
            return [(self, 0, 128)]
        k = (self.n + 511) // 512
        base = (self.n // k + 7) // 8 * 8
        out, t = [], 0
        while t < self.n:
            m = min(base, self.n - t)
            out.append((self, t, m))
            t += m
        return out


class ST:
    def __init__(self, segs, first, last):
        self.segs, self.first, self.last = segs, first, last
        self.NT = sum(s.n for s in segs)

    def tiles(self):
        return [t for s in self.segs for t in s.tiles()]


SUPER = [
    ST([Seg("P", 592, 0, 0), Seg("S", 128, 592)], True, False),
    ST([Seg("P", 704, 0, 592)], False, False),
    ST([Seg("P", 768, 0, 1296)], False, True),
]
NTMAX = 768


class Builder:
    def __init__(self):
        self.nc = nc = bass.Bass("TRN2", target_bir_lowering=False)
        self.P = Prog(nc)
        self.es = contextlib.ExitStack()
        self.toks = {}
        self.rrc = {}
        self.phase_id = 0

        def din(name, shape):
            return nc.dram_tensor(name, list(shape), F32, kind="ExternalInput").ap()

        def dout(name, shape):
            return nc.dram_tensor(name, list(shape), F32, kind="ExternalOutput").ap()

        self.xp = din("xp", [SEQ, D]); self.xs = din("xs", [128, D])
        self.st_conv = din("st_conv", [48, 3072]); self.st_rec = din("st_rec", [16, 8, 128, 128])
        self.st_pool = din("st_pool", [240, D]); self.st_ffn = din("st_ffn", [2, 32, 5632])
        self.meta = din("meta", [NMETA, D])
        self.w_in = din("w_in", [D, 4112]); self.wba = din("wba", [D, 40])
        self.w_out = din("w_out", [D, D]); self.pool_w = din("pool_w", [4, 256, 256])
        self.w_up = din("w_up", [2, D, 5632]); self.w_down = din("w_down", [2, DFF, D])
        self.vecs_d = din("vecs", [128, NV]); self.consts_d = din("consts", [128, NCONST])
        self.yp = dout("yp", [SEQ, D]); self.ys = dout("ys", [128, D])
        self.o_pconv = dout("o_pconv", [3, 3072]); self.o_prec = dout("o_prec", [8, 128, 128])
        self.o_ppool = dout("o_ppool", [15, D]); self.o_pffn = dout("o_pffn", [2, 2, 5632])
        self.o_sconv = dout("o_sconv", [48, 3072]); self.o_srec = dout("o_srec", [16, 8, 128, 128])
        self.o_spool = dout("o_spool", [240, D]); self.o_sffn = dout("o_sffn", [2, 32, 5632])

    def tok(self, *key):
        t = self.toks.get(key)
        if t is None:
            t = self.toks[key] = Tok()
            if key[0] == "bank":
                t.excl = True
        return t

    def sb(self, name, shape, dt):
        return self.es.enter_context(self.nc.sbuf_tensor(name, list(shape), dt))

    def rr(self, name, choices):
        i = self.rrc.get(name, 0)
        self.rrc[name] = i + 1
        return choices[i % len(choices)]

    def bank(self):
        i = self.rrc.get("bank", 0)
        self.rrc["bank"] = i + 1
        i %= 8
        return self.banks[i], self.tok("bank", i)

    def phase(self):
        self.P.barrier()
        self.aoff = 0
        self.phase_id += 1

    def A(self, shape, dt, key=None):
        n = int(np.prod(shape[1:]))
        nb = n * (4 if dt == F32 else 2)
        nb = (nb + 31) // 32 * 32
        ne = nb // 2
        assert self.aoff + ne <= self.arena_n, ("arena overflow", self.aoff, ne, self.arena_n)
        ap = self.arena[0:shape[0], self.aoff:self.aoff + ne]
        self.aoff += ne
        if dt == F32:
            ap = ap.bitcast(F32)
        ap = ap[:, 0:n]
        if len(shape) == 3:
            ap = ap.rearrange("p (a b) -> p a b", b=shape[2])
        elif len(shape) == 4:
            ap = ap.rearrange("p (a b c) -> p a b c", b=shape[2], c=shape[3])
        return ap, self.tok("arena", self.phase_id, self.aoff)

    def mm(self, out, lhsT, rhs, r, w, start=True, stop=True, skip=False):
        if skip:
            self.P.op("pe", lambda e: e.matmul(out, lhsT=lhsT, rhs=rhs, start=start, stop=stop, skip_group_check=True), r, w)
        else:
            self.P.op("pe", lambda e: e.matmul(out, lhsT=lhsT, rhs=rhs, start=start, stop=stop), r, w)

    def tr(self, out, in_, ident, r, w):
        self.P.op("pe", lambda e: e.transpose(out=out, in_=in_, identity=ident), r, w)

    def act(self, out, in_, func, r, w, bias=None, scale=None):
        kw = {}
        if bias is not None:
            kw["bias"] = bias
        if scale is not None:
            kw["scale"] = scale
        self.P.op("act", lambda e: e.activation(out=out, in_=in_, func=func, **kw), r, w)

    def tt(self, eng, out, in0, in1, op, r, w):
        self.P.op(eng, lambda e: e.tensor_tensor(out=out, in0=in0, in1=in1, op=op), r, w)

    def ts(self, eng, out, in0, s1, op0, r, w, s2=None, op1=None):
        if op1 is None:
            self.P.op(eng, lambda e: e.tensor_scalar(out=out, in0=in0, scalar1=s1, scalar2=None, op0=op0), r, w)
        else:
            self.P.op(eng, lambda e: e.tensor_scalar(out=out, in0=in0, scalar1=s1, scalar2=s2, op0=op0, op1=op1), r, w)

    def stt(self, out, in0, scalar, in1, op0, op1, r, w):
        self.P.op("dve", lambda e: e.scalar_tensor_tensor(out=out, in0=in0, scalar=scalar, in1=in1, op0=op0, op1=op1), r, w)

    def cp(self, eng, out, in_, r, w):
        if eng == "act":
            self.act(out, in_, AF.Copy, r, w)
        else:
            self.P.op(eng, lambda e: e.tensor_copy(out=out, in_=in_), r, w)

    def dma(self, q, out, in_, r, w, stream):
        self.P.op(q, lambda e: e.dma_start(out=out, in_=in_), r, w, stream=stream)

    def memset(self, eng, ap, val, w):
        self.P.op(eng, lambda e: e.memset(ap, val), (), w)

    def vcol(self, name, j=0, np_=128):
        c = VEC_COLS[name] + j
        return self.vecs[0:np_, c:c + 1]

    @staticmethod
    def V(seg, ap):
        if seg.kind == "S":
            return ap.rearrange("p (s t) -> p s t", t=8)
        return ap

    @staticmethod
    def ext_dst(seg, buf, H, n):
        if seg.kind == "S":
            return buf[:, 0:16 * (H + 8)].rearrange("p (s w) -> p s w", w=H + 8)[:, :, H:H + 8]
        return buf[:, H:H + n]

    @staticmethod
    def ext_tap(seg, buf, H, j, n):
        if seg.kind == "S":
            return buf[:, 0:16 * (H + 8)].rearrange("p (s w) -> p s w", w=H + 8)[:, :, j:j + 8]
        return buf[:, j:j + n]

    @staticmethod
    def ext_halo(seg, buf, H):
        if seg.kind == "S":
            return buf[:, 0:16 * (H + 8)].rearrange("p (s w) -> p s w", w=H + 8)[:, :, 0:H]
        return buf[:, 0:H]

    @staticmethod
    def ext_tail(seg, buf, H, n):
        if seg.kind == "S":
            return buf[:, 0:16 * (H + 8)].rearrange("p (s w) -> p s w", w=H + 8)[:, :, 8:8 + H]
        return buf[:, n:n + H]

    def build(self):
        nc = self.nc
        with self.es:
            self.xT = self.sb("xT", [128, 8, NTMAX], F32)
            self.S32 = self.sb("S32", [128, 8, 128], F32)
            self.S16 = self.sb("S16", [128, 8, 128], BF16)
            self.HG = self.sb("HG", [128, 24, 3], F32)
            self.HF = self.sb("HF", [128, 2, NFC, 2], F32)
            self.HP = self.sb("HP", [128, 8, 15], F32)
            self.vecs = self.sb("vecs_sb", [128, NV], F32)
            self.cst = self.sb("cst", [128, NCONST], F32)
            self.idb = self.sb("idb", [128, 128], BF16)
            self.ones_m = self.sb("ones_m", [128, 128], BF16)
            self.ones_1 = self.sb("ones_1", [128, 128], BF16)
            self.ones_f = self.sb("ones_f", [64, 128], F32)
            self.nexpA = self.sb("nexpA", [8, 1], F32)
            self.lnq = self.sb("lnq", [128, 1], F32)
            self.banks = [self.es.enter_context(nc.psum_tensor("pb%d" % i, [128, 512], F32)) for i in range(8)]
            rem = nc.sbuf_bytes_remaining - 2048
            self.arena_n = (rem // 2) // 64 * 64
            self.arena = self.sb("arena", [128, self.arena_n], BF16)
            self.aoff = 0
            self.idf = self.cst[:, C_ID:C_ID + 128]
            tC = self.tok("consts")
            self.dma("sp", self.vecs[:], self.vecs_d, (), [tC], "ldc0")
            self.dma("sp", self.cst[:], self.consts_d, (), [tC], "ldc1")
            self.cp("dve", self.idb[:], self.idf, [tC], [tC])
            self.memset("pool", self.ones_m[:], 1.0 / 1024.0, [tC])
            self.memset("pool", self.ones_1[:], 1.0, [tC])
            self.memset("pool", self.ones_f[:], 1.0, [tC])
            self.memset("pool", self.lnq[:], -0.5 * float(np.log(128.0)), [tC])
            self.memset("pool", self.S32[:], 0.0, [self.tok("S32")])
            self.memset("pool", self.S16[:], 0.0, [self.tok("S16")])
            self.memset("pool", self.HG[:], 0.0, [self.tok("HG")])
            self.memset("pool", self.HF[:], 0.0, [self.tok("HF")])
            self.memset("pool", self.HP[:], 0.0, [self.tok("HP")])
            self.act(self.nexpA[:], self.vcol("alog", 0, 8), AF.Exp, [tC], [tC])
            self.ts("dve", self.nexpA[:], self.nexpA[:], -1.0, ALU.mult, [tC], [tC])
            self.tC = tC
            for st in SUPER:
                self.run_super(st)
            self.P.emit()
        return nc

    def run_super(self, st):
        self.load_x(st)
        self.gdn(st)
        self.ffn(st, 0)
        self.pool_mixer(st)
        self.ffn(st, 1)
        self.final(st)

    def xtok(self, dc, tile):
        return self.tok("xT", dc, tile[0].off + tile[1])

    def load_x(self, st):
        if st.first:
            self.phase()
        stg = [self.A([128, 4, D], F32) for _ in range(2)]
        bi = 0
        for seg in st.segs:
            for (_, t0, n) in seg.tiles():
                sg, tsg = stg[bi % 2]
                bi += 1
                nb = (n + 127) // 128
                for b in range(nb):
                    m = min(128, n - b * 128)
                    if seg.kind == "S":
                        self.dma("sp", sg[0:m, b, :], self.xs[0:m, :], (), [tsg], "ldx")
                    else:
                        p0 = seg.pos0 + t0 + b * 128
                        r = 0
                        if p0 < NMETA:
                            k = min(m, NMETA - p0)
                            self.dma("sp", sg[0:k, b, :], self.meta[p0:p0 + k, :], (), [tsg], "ldx")
                            r = k
                        if r < m:
                            a = p0 + r - NMETA
                            self.dma("sp", sg[r:m, b, :], self.xp[a:a + (m - r), :], (), [tsg], "ldx")
                tile = (seg, t0, n)
                c0 = seg.off + t0
                for dc in range(8):
                    pb, tpb = self.bank()
                    for b in range(nb):
                        m = min(128, n - b * 128)
                        self.tr(pb[:, b * 128:b * 128 + m], sg[0:m, b, dc * 128:(dc + 1) * 128], self.idf[0:m, 0:m],
                                [tsg, self.tC], [tpb])
                    self.cp(self.rr("ev", ["act", "dve"]), self.xT[:, dc, c0:c0 + n], pb[:, 0:n], [tpb], [self.xtok(dc, tile)])

    def norm(self, st, wname, dstf, sq2, rsb):
        for tile in st.tiles():
            seg, t0, n = tile
            c0 = seg.off + t0
            sq, tsq = sq2[self.rr("sq2", [0, 1])]
            rs, trs = rsb[self.rr("rsb", [0, 1])]
            pb, tpb = self.bank()
            for dc in range(8):
                xin = self.xT[:, dc, c0:c0 + n]
                self.act(sq[:, dc, 0:n], xin, AF.Square, [self.xtok(dc, tile)], [tsq])
            for dc in range(8):
                self.mm(pb[:, 0:n], self.ones_m[:], sq[:, dc, 0:n], [tsq, self.tC], [tpb], start=(dc == 0), stop=(dc == 7))
            self.act(rs[:, 0:n], pb[:, 0:n], AF.Ln, [tpb], [trs], bias=EPS)
            self.act(rs[:, 0:n], rs[:, 0:n], AF.Exp, [trs], [trs], scale=-0.5)
            for dc in range(8):
                dst, tdst = dstf(dc, tile)
                self.stt(dst, self.xT[:, dc, c0:c0 + n], self.vcol(wname, dc), rs[:, 0:n], ALU.mult, ALU.mult,
                         [self.xtok(dc, tile), trs, self.tC], [tdst])

    def gdn(self, st):
        NT = st.NT
        self.phase()
        hasS = any(s.kind == "S" for s in st.segs)
        xn, _ = self.A([128, 8, NT], BF16)
        QKVZ, _ = self.A([128, 32, NT], BF16)
        GB, tGB = self.A([40, NT], F32)
        mark = self.aoff
        sq2 = [self.A([128, 8, 512], BF16) for _ in range(2)]
        rsb = [self.A([128, 512], F32) for _ in range(2)]
        wsl = [self.A([128, 8, 512], BF16) for _ in range(3)]
        wbat, twba = self.A([128, 8, 40], BF16)
        ext = [self.A([128, 3 + NTMAX], F32) for _ in range(3)]
        acc = [self.A([128, 512], F32) for _ in range(2)]
        sil = [self.A([128, 512], F32) for _ in range(2)]
        sqh = [self.A([128, 512], BF16) for _ in range(3)]
        rin = [self.A([128, 512], F32) for _ in range(2)]
        bat = [self.A([8, 512], F32) for _ in range(4)]
        if hasS:
            SHG, tSHG = self.A([128, 24, 48], F32)
            stg, tstg = self.A([48, 3072], F32)
        self.memset("pool", GB, 0.0, [tGB])
        xnt = lambda dc, tile: (xn[:, dc, tile[0].off + tile[1]:tile[0].off + tile[1] + tile[2]], self.tok("xn", self.phase_id, dc, tile[0].off + tile[1]))
        self.norm(st, "nm0", xnt, sq2, rsb)
        if hasS:
            self.dma("sp", stg, self.st_conv, (), [tstg], "ldst")
            for g in range(3):
                pb, tpb = self.bank()
                for j in range(8):
                    fc = g * 8 + j
                    self.tr(pb[:, j * 48:(j + 1) * 48], stg[0:48, fc * 128:(fc + 1) * 128], self.idf[0:48, 0:48], [tstg, self.tC], [tpb])
                self.cp("act", SHG[:, g * 8:(g + 1) * 8, :], pb[:, 0:384].rearrange("p (a b) -> p a b", b=48), [tpb], [tSHG])
        self.dma("pool", wbat, self.wba.rearrange("(kc p) f -> p kc f", p=128), (), [twba], "ldwba")
        for tile in st.tiles():
            seg, t0, n = tile
            c0 = seg.off + t0
            pb, tpb = self.bank()
            for kc in range(8):
                self.mm(pb[0:40, 0:n], wbat[:, kc, :], xn[:, kc, c0:c0 + n], [twba, xnt(kc, tile)[1]], [tpb], start=(kc == 0), stop=(kc == 7))
            self.act(GB[32:40, c0:c0 + n], pb[32:40, 0:n], AF.Sigmoid, [tpb], [tGB])
            (b1, t1), (b2, t2), (b3, t3), (b4, t4) = bat
            self.ts("dve", b1[:, 0:n], pb[0:8, 0:n], self.vcol("dtb", 0, 8), ALU.add, [tpb, self.tC], [t1])
            self.stt(b2[:, 0:n], b1[:, 0:n], -1.0, b1[:, 0:n], ALU.mult, ALU.max, [t1], [t2])
            self.act(b3[:, 0:n], b2[:, 0:n], AF.Exp, [t2], [t3], scale=-1.0)
            self.act(b4[:, 0:n], b3[:, 0:n], AF.Ln, [t3], [t4], bias=1.0)
            self.stt(b2[:, 0:n], b1[:, 0:n], 0.0, b4[:, 0:n], ALU.max, ALU.add, [t1, t4], [t2])
            self.ts("dve", GB[0:8, c0:c0 + n], b2[:, 0:n], self.nexpA[:, 0:1], ALU.mult, [t2, self.tC], [tGB])
        def ld_win(u):
            wt_, twt_ = wsl[u % 3]
            self.dma("pool", wt_, self.w_in[:, u * 512:(u + 1) * 512].rearrange("(kc p) f -> p kc f", p=128), (), [twt_], "ldw%d" % (u % 3))
        ld_win(0); ld_win(1)
        pend = []
        qk_list = []
        for u in range(8):
            wt, twt = wsl[u % 3]
            if u + 2 < 8:
                ld_win(u + 2)
            for j in range(4):
                fc = u * 4 + j
                kind = fc // 8
                prev_ext = None
                for tile in st.tiles():
                    seg, t0, n = tile
                    c0 = seg.off + t0
                    pb, tpb = self.bank()
                    for kc in range(8):
                        self.mm(pb[:, 0:n], wt[:, kc, j * 128:(j + 1) * 128], xn[:, kc, c0:c0 + n], [twt, xnt(kc, tile)[1]], [tpb],
                                start=(kc == 0), stop=(kc == 7))
                    dst = QKVZ[:, fc, c0:c0 + n]
                    tdst = self.tok("qkvz", self.phase_id, fc, c0)
                    if kind == 3:
                        self.act(self.V(seg, dst), self.V(seg, pb[:, 0:n]), AF.Silu, [tpb], [tdst])
                        continue
                    if seg.kind == "S":
                        bi = self.rr("ext", [0, 1, 2])
                        ex = ext[bi][0]
                        tex = self.tok("extL", self.phase_id, bi, 0)
                        rtk = [tex]
                        self.cp("pool", self.ext_halo(seg, ex, 3), SHG[:, fc, :].rearrange("p (s r) -> p s r", r=3), [tSHG], [tex])
                        self.cp("act", self.ext_dst(seg, ex, 3, n), self.V(seg, pb[:, 0:n]), [tpb], [tex])
                        self.cp("pool", SHG[:, fc, :].rearrange("p (s r) -> p s r", r=3), self.ext_tail(seg, ex, 3, n), [tex], [tSHG])
                    else:
                        if t0 == 0:
                            bi = self.rr("ext", [0, 1, 2])
                            cur_ext = (bi, 0)
                            tex = self.tok("extL", self.phase_id, bi, 0)
                            self.cp("act", ext[bi][0][:, 0:3], self.HG[:, fc, :], [self.tok("HG")], [tex])
                            rtk = [tex]
                        else:
                            bi, ti = cur_ext
                            cur_ext = (bi, ti + 1)
                            tex = self.tok("extL", self.phase_id, bi, ti + 1)
                            rtk = [tex, self.tok("extL", self.phase_id, bi, ti)]
                        ex = ext[bi][0][:, t0:t0 + 3 + n]
                        self.cp("act", ex[:, 3:3 + n], pb[:, 0:n], [tpb], [tex])
                        if t0 + n == seg.n:
                            self.cp("act", self.HG[:, fc, :], ex[:, n:n + 3], [tex], [self.tok("HG")])
                    ac, tac = acc[self.rr("acc", [0, 1])]
                    av = self.V(seg, ac[:, 0:n])
                    self.act(av, self.ext_tap(seg, ex, 3, 0, n), AF.Identity, rtk + [self.tC], [tac], scale=self.vcol("gcw", 0 * 24 + fc))
                    for tap in (1, 2, 3):
                        self.stt(av, self.ext_tap(seg, ex, 3, tap, n), self.vcol("gcw", tap * 24 + fc), av, ALU.mult, ALU.add, rtk + [tac, self.tC], [tac])

                    def tail(dst=dst, tdst=tdst, ac=ac, tac=tac, n=n):
                        self.act(dst, ac[:, 0:n], AF.Silu, [tac], [tdst])
                    if pend:
                        pend.pop(0)()
                    pend.append(tail)
                    if kind < 2:
                        qk_list.append((kind, dst, tdst, n))
            while pend:
                pend.pop(0)()
            def nstage1(item):
                kind, dst, tdst, n = item
                sh, tsh = sqh[self.rr("sqh", [0, 1, 2])]
                self.tt("dve", sh[:, 0:n], dst, dst, ALU.mult, [tdst], [tsh])
                pb2, tpb2 = self.bank()
                self.mm(pb2[:, 0:n], self.ones_1[:], sh[:, 0:n], [tsh, self.tC], [tpb2])
                return pb2, tpb2

            def nstage2(item, pb2, tpb2):
                kind, dst, tdst, n = item
                ri, tri_ = rin[self.rr("rin", [0, 1])]
                self.act(ri[:, 0:n], pb2[:, 0:n], AF.Ln, [tpb2], [tri_], bias=EPS)
                self.act(ri[:, 0:n], ri[:, 0:n], AF.Exp, [tri_], [tri_], scale=-0.5, bias=(self.lnq[:, 0:1] if kind == 0 else None))
                self.tt("dve", dst, dst, ri[:, 0:n], ALU.mult, [tdst, tri_], [tdst])
            inflight = []
            for item in qk_list:
                inflight.append((item,) + nstage1(item))
                if len(inflight) > 2:
                    nstage2(*inflight.pop(0))
            while inflight:
                nstage2(*inflight.pop(0))
            qk_list = []
        if hasS:
            for fc in range(24):
                pb, tpb = self.bank()
                self.tr(pb[0:48, 0:128], SHG[:, fc, :], self.idf[:, :], [tSHG, self.tC], [tpb])
                self.cp(self.rr("ev", ["act", "dve"]), stg[0:48, fc * 128:(fc + 1) * 128], pb[0:48, 0:128], [tpb], [tstg])
            self.dma("sp", self.o_sconv, stg, [tstg], (), "stst")
        if st.last:
            stg2, tstg2 = self.A([3, 3072], F32)
            for fc in range(24):
                pb, tpb = self.bank()
                self.tr(pb[0:3, 0:128], self.HG[:, fc, :], self.idf[:, :], [self.tok("HG"), self.tC], [tpb])
                self.cp(self.rr("ev", ["act", "dve"]), stg2[0:3, fc * 128:(fc + 1) * 128], pb[0:3, 0:128], [tpb], [tstg2])
            self.dma("sp", self.o_pconv, stg2, [tstg2], (), "stst")

        self.P.barrier()
        self.aoff = mark
        ONT = xn
        self.bfree = list(range(8))
        NA, NC_, NB = 3, 4, 1
        self.want_onacc = False
        TAs = [self.alloc_chunk_bufs("A") for _ in range(NA)]
        CAs = [self.alloc_chunk_bufs("C") for _ in range(NC_)]
        TBs = [self.alloc_chunk_bufs("B") for _ in range(NB)]
        jobs = []
        for seg in st.segs:
            if seg.kind == "P":
                c = 0
                if seg.pos0 == 0:
                    jobs.append(("P", seg.off, NMETA, None))
                    c = NMETA
                while c < seg.n:
                    jobs.append(("P", seg.off + c, 64, None))
                    c += 64
            else:
                for b_ in range(2):
                    jobs.append(("SB", seg.off + 64 * b_, 64, 8 * b_))
        N = len(jobs)
        pj = [ji for ji in range(N) if jobs[ji][0] == "P"]
        nP = len(pj)
        gbt_all, tpre = self.A([64, nP * 40], F32)
        Gt_all, _ = self.A([64, nP * 8], F32)
        eG_all, _ = self.A([64, nP * 8], F32)
        nb_all, _ = self.A([64, nP * 8], F32)
        nbG_all, _ = self.A([64, nP * 8], F32)
        self.memset("pool", gbt_all, 0.0, [tpre])
        pgb, tpgb, ipgb = self.bacq()
        for k, ji in enumerate(pj):
            _, c0_, L_, _ = jobs[ji]
            self.tr(pgb[0:L_, k * 40:(k + 1) * 40], GB[0:40, c0_:c0_ + L_], self.idf[0:40, 0:40], [tGB, self.tC], [tpgb])
        k = 0
        while k < nP:
            k2 = k
            while k2 < nP and jobs[pj[k2]][2] == jobs[pj[k]][2]:
                k2 += 1
            L_ = jobs[pj[k]][2]
            self.cp("dve", gbt_all[0:L_, k * 40:k2 * 40], pgb[0:L_, k * 40:k2 * 40], [tpgb], [tpre])
            k = k2
        self.brel(ipgb)
        g3 = gbt_all.rearrange("p (k c) -> p k c", c=40)
        pgc, tpgc, ipgc = self.bacq()
        self.mm(pgc[0:64, 0:nP * 8].rearrange("p (k c) -> p k c", c=8), self.cst[0:64, C_TRI:C_TRI + 64], g3[:, :, 0:8], [tpre, self.tC], [tpgc])
        self.cp("dve", Gt_all, pgc[0:64, 0:nP * 8], [tpgc], [tpre])
        self.brel(ipgc)
        self.act(eG_all, Gt_all, AF.Exp, [tpre], [tpre])
        self.ts("dve", nb_all.rearrange("p (k c) -> p k c", c=8), g3[:, :, 32:40], -1.0, ALU.mult, [tpre], [tpre])
        self.tt("dve", nbG_all, eG_all, nb_all, ALU.mult, [tpre], [tpre])
        pres = {}
        for k, ji in enumerate(pj):
            L_ = jobs[ji][2]
            pres[ji] = (gbt_all[0:L_, k * 40:k * 40 + 8], gbt_all[0:L_, k * 40 + 32:k * 40 + 40], Gt_all[0:L_, k * 8:(k + 1) * 8],
                        nb_all[0:L_, k * 8:(k + 1) * 8], nbG_all[0:L_, k * 8:(k + 1) * 8], tpre)
        nextA = 0
        nextB = 0
        doneA = set()
        actA = {}
        actB = None
        while nextB < N:
            for slot in range(NA):
                if slot not in actA and nextA < N and nextA < nextB + NC_:
                    actA[slot] = (nextA, self.chunk_A(jobs[nextA], QKVZ, GB, tGB, TAs[slot], CAs[nextA % NC_], pres.get(nextA)))
                    nextA += 1
            if actB is None and nextB in doneA:
                if jobs[nextB][0] == "SB":
                    nextB += 1
                    continue
                actB = self.chunk_B(jobs[nextB], CAs[nextB % NC_], TBs[nextB % NB], QKVZ, ONT)
            if actB is not None:
                try:
                    next(actB)
                except StopIteration:
                    actB = None
                    nextB += 1
            for slot in list(actA):
                j, g = actA[slot]
                try:
                    next(g)
                except StopIteration:
                    doneA.add(j)
                    del actA[slot]
        if hasS:
            self.P.barrier()
            onaccs = [self.A([64, 1024], F32) for _ in range(2)]
            save_off = self.aoff
            self.aoff = mark
            NSQ = 3
            assert all((ji % NC_) >= 2 for ji in range(N) if jobs[ji][0] == "SB")
            SBs = []
            for _ in range(NSQ):
                d = {}
                d["S32"] = self.A([128, 8, 128], F32); d["S16"] = self.A([128, 8, 128], BF16)
                d["Stmp"] = self.A([128, 1024], F32); d["vn"] = self.A([64, 1024], BF16)
                SBs.append(d)
            sb_jobs = [(ji, jobs[ji]) for ji in range(N) if jobs[ji][0] == "SB"]
            todo = [(bi, ji, job, s_) for bi, (ji, job) in enumerate(sb_jobs) for s_ in range(8)]
            remaining = {bi: 8 for bi in range(len(sb_jobs))}
            act = {}
            fin = []
            while todo or act or fin:
                for slot in range(NSQ):
                    if slot not in act and todo:
                        bi, ji, job, s_ = todo.pop(0)
                        act[slot] = (bi, self.sample_seq(job, s_, CAs[ji % NC_], SBs[slot], onaccs[bi]))
                for slot in list(act):
                    bi, g = act[slot]
                    try:
                        next(g)
                    except StopIteration:
                        del act[slot]
                        remaining[bi] -= 1
                        if remaining[bi] == 0:
                            ji, job = sb_jobs[bi]
                            fin.append(self.sample_finish(job, TBs[0], onaccs[bi], QKVZ, ONT))
                for g in list(fin):
                    try:
                        next(g)
                    except StopIteration:
                        fin.remove(g)
            self.aoff = max(save_off, self.aoff)
        if st.last:
            self.dma("sp", self.o_prec.rearrange("h k v -> k h v"), self.S32[:], [self.tok("S32")], (), "strec")

        self.P.barrier()
        self.aoff = mark
        wo, two = self.A([128, 8, D], BF16)
        self.dma("pool", wo[:, :, 0:512], self.w_out[:, 0:512].rearrange("(kc p) f -> p kc f", p=128), (), [two], "ldw0")
        self.dma("pool", wo[:, :, 512:1024], self.w_out[:, 512:1024].rearrange("(kc p) f -> p kc f", p=128), (), [two], "ldw1")
        for tile in st.tiles():
            seg, t0, n = tile
            c0 = seg.off + t0
            for dc in range(8):
                pb, tpb = self.bank()
                for kc in range(8):
                    self.mm(pb[:, 0:n], wo[:, kc, dc * 128:(dc + 1) * 128], ONT[:, kc, c0:c0 + n], [two], [tpb], start=(kc == 0), stop=(kc == 7))
                xv = self.xT[:, dc, c0:c0 + n]
                self.tt("dve", xv, pb[:, 0:n], xv, ALU.add, [tpb], [self.xtok(dc, tile)])

    def alloc_chunk_bufs(self, which):
        b = {}
        def a(name, shape, dt):
            b[name] = self.A(shape, dt)
        if which == "A":
            a("gbt", [64, 40], F32)
            for nm in ("Gt", "eG", "nbG", "nb", "dGl", "eGl"):
                a(nm, [64, 8], F32)
            a("Dm", [64, 512], F32); a("Du", [64, 512], F32); a("Dl", [64, 512], F32); a("eGbc", [128, 512], F32)
            b["rhsG"] = b["Dl"]
            a("Lneg", [64, 512], BF16); a("M0", [64, 512], BF16)
            a("QTa", [64, 512], BF16); a("QTb", [64, 512], BF16)
            a("kbgn", [64, 1024], BF16)
        elif which == "C":
            a("PQa", [64, 1024], BF16); a("PQb", [64, 1024], BF16); a("At", [64, 512], BF16)
            a("kd", [64, 1024], BF16); a("vb", [64, 1024], BF16)
            a("nWT", [128, 512], BF16); a("qdT", [128, 512], BF16); a("gtc", [128, 64], F32)
        else:
            a("vn", [64, 1024], BF16); a("sqo", [64, 1024], BF16); a("on", [64, 1024], BF16)
            a("Stmp", [128, 1024], F32); a("ss", [64, 8], F32); a("rs", [64, 8], F32)
            if self.want_onacc:
                a("onacc", [64, 1024], F32)
        return b

    def bacq(self):
        if not self.bfree:
            raise RuntimeError("out of PSUM banks")
        i = self.bfree.pop(0)
        return self.banks[i], self.tok("bank", i), i

    def brel(self, i):
        self.bfree.append(i)

    def chunk_A(self, job, QKVZ, GB, tGB, TA, CA, pre=None):
        kind, c0, L, sidx = job
        B = dict(TA); B.update(CA)
        tC = self.tC
        Q = lambda h: QKVZ[:, h, c0:c0 + L]
        K = lambda h: QKVZ[:, 8 + h, c0:c0 + L]
        Vv = lambda h: QKVZ[:, 16 + h, c0:c0 + L]
        h3 = lambda ap: ap.rearrange("p (h l) -> p h l", l=L)
        hd = lambda ap: ap.rearrange("p (h d) -> p h d", d=128)
        W8 = 8 * L
        blk = (kind == "SB")
        cT, cN, cP = (C_TRI8, C_NEGU8, C_POSL8) if blk else (C_TRI, C_NEGU, C_POSL)
        tri = self.cst[0:L, cT:cT + L]
        dGl, tdGl = B["dGl"]; eGl, teGl = B["eGl"]; gtc, tgtc = B["gtc"]
        rhsG, trG = B["rhsG"]
        if pre is not None:
            g_tm, beta, GtL, nbL, nbGL, tpre = pre
            tgbt = tGt = tnb = tnbG = tpre
            self.tt("pool", h3(rhsG[0:L, 0:W8]), tri.unsqueeze(1).to_broadcast([L, 8, L]), g_tm.unsqueeze(2).to_broadcast([L, 8, L]),
                    ALU.mult, [tgbt, tC], [trG])
            yield
            pG, tpG, ipG = self.bacq()
            self.mm(pG[:, 0:W8], self.ones_f[0:L, :], rhsG[0:L, 0:W8], [trG, tC], [tpG])
            yield
        else:
            pg, tpg, ipg = self.bacq()
            self.tr(pg[0:L, 0:40], GB[0:40, c0:c0 + L], self.idf[0:40, 0:40], [tGB, tC], [tpg])
            gbt, tgbt = B["gbt"]
            self.cp("dve", gbt[0:L, :], pg[0:L, 0:40], [tpg], [tgbt])
            self.brel(ipg)
            g_tm = gbt[0:L, 0:8]
            beta = gbt[0:L, 32:40]
            yield
            self.tt("pool", h3(rhsG[0:L, 0:W8]), tri.unsqueeze(1).to_broadcast([L, 8, L]), g_tm.unsqueeze(2).to_broadcast([L, 8, L]),
                    ALU.mult, [tgbt, tC], [trG])
            pg2, tpg2, ipg2 = self.bacq()
            self.mm(pg2[0:L, 0:8], tri, g_tm, [tgbt, tC], [tpg2])
            Gt, tGt = B["Gt"]; eG, teG = B["eG"]; nbG, tnbG = B["nbG"]; nb, tnb = B["nb"]
            self.cp("dve", Gt[0:L, :], pg2[0:L, 0:8], [tpg2], [tGt])
            self.brel(ipg2)
            self.ts("dve", nb[0:L, :], beta, -1.0, ALU.mult, [tgbt], [tnb])
            yield
            pG, tpG, ipG = self.bacq()
            self.mm(pG[:, 0:W8], self.ones_f[0:L, :], rhsG[0:L, 0:W8], [trG, tC], [tpG])
            self.act(eG[0:L, :], Gt[0:L, :], AF.Exp, [tGt], [teG])
            self.tt("dve", nbG[0:L, :], eG[0:L, :], nb[0:L, :], ALU.mult, [teG, tnb], [tnbG])
            GtL, nbL, nbGL = Gt[0:L, :], nb[0:L, :], nbG[0:L, :]
            yield
        if blk:
            Glast = None
            gl4 = pG[:, 0:W8].rearrange("p (h s t) -> p h s t", s=8, t=8)[:, :, :, 7]
        else:
            Glast = h3(pG[:, 0:W8])[:, :, L - 1]
        Dm, tDm = B["Dm"]; Du, tDu = B["Du"]; Dl, tDl = B["Dl"]; eGbc, teGbc = B["eGbc"]
        self.tt("dve", h3(Dm[0:L, 0:W8]), h3(pG[0:L, 0:W8]), GtL.unsqueeze(2).to_broadcast([L, 8, L]), ALU.subtract,
                [tpG, tGt], [tDm])
        if blk:
            pgl, tpgl, ipgl = self.bacq()
            self.mm(pgl[0:L, 0:8], self.cst[0:L, C_SEL:C_SEL + L], GtL, [tGt, tC], [tpgl])
            self.tt("dve", dGl[0:L, :], pgl[0:L, 0:8], GtL, ALU.subtract, [tpgl, tGt], [tdGl])
            self.brel(ipgl)
            self.act(gtc[:, 0:64].rearrange("p (h s) -> p h s", s=8), gl4, AF.Exp, [tpG], [tgtc])
        else:
            self.tt("dve", dGl[0:L, :], Glast[0:L], GtL, ALU.subtract, [tpG, tGt], [tdGl])
            self.act(gtc[:, 0:8], Glast, AF.Exp, [tpG], [tgtc])
        self.act(eGbc[:, 0:W8], pG[:, 0:W8], AF.Exp, [tpG], [teGbc])
        self.brel(ipG)
        pk, tpk, ipk = self.bacq()
        pkb = pk[:].bitcast(BF16)
        for h in range(8):
            self.tr(pkb[0:L, h * 128:(h + 1) * 128], K(h), self.idb[:], [tC], [tpk])
        pv, tpv, ipv = self.bacq()
        pvb = pv[:].bitcast(BF16)
        for h in range(8):
            self.tr(pvb[0:L, h * 128:(h + 1) * 128], Vv(h), self.idb[:], [tC], [tpv])
        yield
        self.act(eGl[0:L, :], dGl[0:L, :], AF.Exp, [tdGl], [teGl])
        negu = self.cst[0:L, cN:cN + L].unsqueeze(1).to_broadcast([L, 8, L])
        posl = self.cst[0:L, cP:cP + L].unsqueeze(1).to_broadcast([L, 8, L])
        self.tt("pool", h3(Du[0:L, 0:W8]), h3(Dm[0:L, 0:W8]), negu, ALU.add, [tDm, tC], [tDu])
        self.tt("pool", h3(Dl[0:L, 0:W8]), h3(Dm[0:L, 0:W8]), posl, ALU.add, [tDm, tC], [tDl])
        kbgn, tkb = B["kbgn"]; kd, tkd = B["kd"]; vb, tvb = B["vb"]
        self.tt("dve", hd(kbgn[0:L, :]), hd(pkb[0:L, :]), nbGL.unsqueeze(2).to_broadcast([L, 8, 128]), ALU.mult, [tpk, tnbG], [tkb])
        self.tt("dve", hd(vb[0:L, :]), hd(pvb[0:L, :]), beta.unsqueeze(2).to_broadcast([L, 8, 128]), ALU.mult, [tpv, tgbt], [tvb])
        self.brel(ipv)
        yield
        self.tt("dve", hd(kd[0:L, :]), hd(pkb[0:L, :]), eGl[0:L, :].unsqueeze(2).to_broadcast([L, 8, 128]), ALU.mult, [tpk, teGl], [tkd])
        self.brel(ipk)
        self.act(Du[0:L, 0:W8], Du[0:L, 0:W8], AF.Exp, [tDu], [tDu])
        self.act(Dl[0:L, 0:W8], Dl[0:L, 0:W8], AF.Exp, [tDl], [tDl], scale=-1.0)
        pkk, tpkk, ipkk = self.bacq()
        for h in range(8):
            self.mm(pkk[0:L, h * L:(h + 1) * L], K(h), K(h), [], [tpkk])
        pkq, tpkq, ipkq = self.bacq()
        for h in range(8):
            self.mm(pkq[0:L, h * L:(h + 1) * L], K(h), Q(h), [], [tpkq])
        qdT, tqd = B["qdT"]
        self.tt("pool", h3(qdT[:, 0:W8]), QKVZ[:, 0:8, c0:c0 + L], h3(eGbc[:, 0:W8]), ALU.mult, [teGbc], [tqd])
        yield
        self.tt("pool", h3(Dl[0:L, 0:W8]), h3(Dl[0:L, 0:W8]), nbL.unsqueeze(2).to_broadcast([L, 8, L]), ALU.mult,
                [tDl, tnb], [tDl])
        Lneg, tLn = B["Lneg"]; At, tAt = B["At"]; M0, tM0 = B["M0"]
        self.tt("dve", At[0:L, 0:W8], pkq[0:L, 0:W8], Du[0:L, 0:W8], ALU.mult, [tpkq, tDu], [tAt])
        self.brel(ipkq)
        yield
        self.tt("dve", Lneg[0:L, 0:W8], pkk[0:L, 0:W8], Dl[0:L, 0:W8], ALU.mult, [tpkk, tDl], [tLn])
        self.brel(ipkk)
        yield
        pm, tpm, ipm = self.bacq()
        pmb = pm[:].bitcast(BF16)
        for h in range(8):
            self.tr(pmb[0:L, h * L:(h + 1) * L], Lneg[0:L, h * L:(h + 1) * L], self.idb[0:L, 0:L], [tLn, tC], [tpm])
        self.cp("act", M0[0:L, 0:W8], pmb[0:L, 0:W8], [tpm], [tM0])
        self.brel(ipm)
        yield
        nlev = 3 if blk else {64: 6, 16: 4, 8: 3}[L]
        idbL = self.idb[0:L, 0:L]
        PQ = [B["PQa"], B["PQb"]]
        QTbufs = [B["QTa"], B["QTb"]]
        pq3 = lambda ap: ap[0:L, :].rearrange("p (h c) -> p h c", c=128)
        cur = 0
        Pc, tPc = PQ[cur]
        self.tt("pool", pq3(Pc)[:, :, 0:L], h3(M0[0:L, 0:W8]), idbL.unsqueeze(1).to_broadcast([L, 8, L]), ALU.add, [tM0, tC], [tPc])
        pq, tpq, ipq = self.bacq()
        for h in range(8):
            sl = slice(h * L, (h + 1) * L)
            self.mm(pq[0:L, sl], M0[0:L, sl], Lneg[0:L, sl], [tM0, tLn], [tpq])
        QTc = QTbufs[0]
        self.cp("act", QTc[0][0:L, 0:W8], pq[0:L, 0:W8], [tpq], [QTc[1]])
        self.brel(ipq)
        pq2, tpq2, ipq2 = self.bacq()
        for h in range(8):
            sl = slice(h * L, (h + 1) * L)
            self.mm(pq2[0:L, sl], Lneg[0:L, sl], M0[0:L, sl], [tM0, tLn], [tpq2])
        self.cp("dve", pq3(Pc)[:, :, L:2 * L], h3(pq2[0:L, 0:W8]), [tpq2], [tPc])
        self.brel(ipq2)
        yield
        for k in range(1, nlev):
            last = (k == nlev - 1)
            Pn, tPn = PQ[1 - cur]
            wid = L if last else 2 * L
            if not last:
                QTn = QTbufs[k % 2]
                pq, tpq, ipq = self.bacq()
                for h in range(8):
                    self.mm(pq[0:L, h * L:(h + 1) * L], pq3(Pc)[:, h, L:2 * L], QTc[0][0:L, h * L:(h + 1) * L], [tPc, QTc[1]], [tpq])
                self.cp("act", QTn[0][0:L, 0:W8], pq[0:L, 0:W8], [tpq], [QTn[1]])
                self.brel(ipq)
            for half in range(2):
                pp, tpp, ipp = self.bacq()
                for hh in range(4):
                    h = half * 4 + hh
                    self.mm(pp[0:L, hh * 128:hh * 128 + wid], QTc[0][0:L, h * L:(h + 1) * L], pq3(Pc)[:, h, 0:wid], [tPc, QTc[1]], [tpp])
                ppv = pp[0:L, :].rearrange("p (h c) -> p h c", c=128)
                hs = slice(half * 4, half * 4 + 4)
                self.tt("dve", pq3(Pn)[:, hs, 0:L], ppv[:, :, 0:L], pq3(Pc)[:, hs, 0:L], ALU.add, [tpp, tPc], [tPn])
                if not last:
                    self.cp("act", pq3(Pn)[:, hs, L:2 * L], ppv[:, :, L:2 * L], [tpp], [tPn])
                self.brel(ipp)
            cur = 1 - cur
            Pc, tPc = PQ[cur]
            if not last:
                QTc = QTn
            yield
        Ttv = pq3(Pc)
        tTt = tPc
        pw, tpw, ipw = self.bacq()
        for h in range(8):
            self.mm(pw[:, h * L:(h + 1) * L], kbgn[0:L, h * 128:(h + 1) * 128], Ttv[:, h, 0:L], [tkb, tTt], [tpw])
        nWT, tnW = B["nWT"]
        self.cp("act", nWT[:, 0:W8], pw[:, 0:W8], [tpw], [tnW])
        self.brel(ipw)
        CA["Tt"] = (Ttv, tTt)
        yield

    def chunk_B(self, job, CA, TB, QKVZ, ONT):
        kind, c0, L, sidx = job
        B = dict(TB); B.update(CA)
        tC = self.tC
        Tt, tTt = CA["Tt"]
        h3 = lambda ap: ap.rearrange("p (h l) -> p h l", l=L)
        hd = lambda ap: ap.rearrange("p (h d) -> p h d", d=128)
        W8 = 8 * L
        if kind == "S":
            S32, tS32 = self.SS32[sidx % 2]
            S16, tS16 = self.SS16[sidx % 2]
            self.dma("sp", S32, self.st_rec[sidx].rearrange("h k v -> k h v"), (), [tS32], "ldrec%d" % (sidx % 2))
            self.cp("pool", S16, S32, [tS32], [tS16])
        else:
            S32, tS32 = self.S32[:], self.tok("S32")
            S16, tS16 = self.S16[:], self.tok("S16")
        vb, tvb = B["vb"]; nWT, tnW = B["nWT"]; vn, tvn = B["vn"]; qdT, tqd = B["qdT"]; At, tAt = B["At"]
        kd, tkd = B["kd"]; sqo, tsq = B["sqo"]; on, ton = B["on"]; ss, tss = B["ss"]; rs, trs = B["rs"]
        gtc, tgtc = B["gtc"]; Stmp, tSt = B["Stmp"]
        for half in range(2):
            pv, tpv, ipv = self.bacq()
            for hh in range(4):
                h = half * 4 + hh
                self.mm(pv[0:L, hh * 128:(hh + 1) * 128], Tt[:, h, 0:L], vb[0:L, h * 128:(h + 1) * 128], [tTt, tvb], [tpv], start=(hh == 0), stop=False, skip=True)
            for hh in range(4):
                h = half * 4 + hh
                self.mm(pv[0:L, hh * 128:(hh + 1) * 128], nWT[:, h * L:(h + 1) * L], S16[:, h, :], [tnW, tS16], [tpv], start=False, stop=True, skip=True)
            self.cp(("act", "dve")[half], vn[0:L, half * 512:(half + 1) * 512], pv[0:L, :], [tpv], [tvn])
            self.brel(ipv)
        self.tt("pool", hd(Stmp[:, :]), S32, gtc[:, 0:8].unsqueeze(2).to_broadcast([128, 8, 128]), ALU.mult, [tS32, tgtc], [tSt])
        yield
        pss = []
        for half in range(2):
            pS, tpS, ipS = self.bacq()
            pss.append((pS, tpS, ipS))
            for hh in range(4):
                h = half * 4 + hh
                self.mm(pS[:, hh * 128:(hh + 1) * 128], kd[0:L, h * 128:(h + 1) * 128], vn[0:L, h * 128:(h + 1) * 128], [tkd, tvn], [tpS])
        for half in range(2):
            pS, tpS, ipS = pss[half]
            self.tt("dve", S32[:, half * 4:(half + 1) * 4, :], hd(pS[:, :]), hd(Stmp[:, half * 512:(half + 1) * 512]), ALU.add, [tpS, tSt], [tS32])
            self.brel(ipS)
        pos = []
        for half in range(2):
            po, tpo, ipo = self.bacq()
            pos.append((po, tpo, ipo))
            for hh in range(4):
                h = half * 4 + hh
                self.mm(po[0:L, hh * 128:(hh + 1) * 128], qdT[:, h * L:(h + 1) * L], S16[:, h, :], [tqd, tS16], [tpo], start=(hh == 0), stop=False, skip=True)
            for hh in range(4):
                h = half * 4 + hh
                self.mm(po[0:L, hh * 128:(hh + 1) * 128], At[0:L, h * L:(h + 1) * L], vn[0:L, h * 128:(h + 1) * 128], [tAt, tvn], [tpo], start=False, stop=True, skip=True)
        self.cp("act", S16, S32, [tS32], [tS16])
        if kind == "S":
            self.dma("sp", self.o_srec[sidx].rearrange("h k v -> k h v"), S32, [tS32], (), "strec%d" % (sidx % 2))
        yield
        for half in range(2):
            po, tpo, ipo = pos[half]
            self.act(sqo[0:L, half * 512:(half + 1) * 512], po[0:L, :], AF.Square, [tpo], [tsq])
        self.P.op("dve", lambda e, o=ss[0:L, :], i=hd(sqo[0:L, :]): e.tensor_reduce(out=o, in_=i, axis=AX.X, op=ALU.add), [tsq], [tss])
        self.act(rs[0:L, :], ss[0:L, :], AF.Ln, [tss], [trs], bias=EPS, scale=1.0 / 128.0)
        self.act(rs[0:L, :], rs[0:L, :], AF.Exp, [trs], [trs], scale=-0.5)
        yield
        for half in range(2):
            po, tpo, ipo = pos[half]
            self.tt("dve", hd(on[0:L, half * 512:(half + 1) * 512]), hd(po[0:L, :]), rs[0:L, half * 4:(half + 1) * 4].unsqueeze(2).to_broadcast([L, 4, 128]),
                    ALU.mult, [tpo, trs], [ton])
            self.brel(ipo)
        yield
        pt, tpt, ipt = self.bacq()
        ptb = pt[:].bitcast(BF16)
        for h in range(8):
            self.tr(ptb[:, h * L:(h + 1) * L], on[0:L, h * 128:(h + 1) * 128], self.idb[0:L, 0:L], [ton, tC], [tpt])
        self.stt(ONT[:, :, c0:c0 + L], h3(ptb[:, 0:W8]), self.vcol("gnw"), QKVZ[:, 24:32, c0:c0 + L], ALU.mult, ALU.mult, [tpt, tC], [self.tok("ONT", self.phase_id)])
        self.brel(ipt)
        yield

    def sample_seq(self, job, s_, CA, SB, onacc_t):
        kind, c0, L, s0 = job
        tC = self.tC
        Tt, tTt = CA["Tt"]
        hd = lambda ap: ap.rearrange("p (h d) -> p h d", d=128)
        vb, tvb = CA["vb"]; nWT, tnW = CA["nWT"]; qdT, tqd = CA["qdT"]; At, tAt = CA["At"]
        kd, tkd = CA["kd"]; gtc, tgtc = CA["gtc"]
        S32, tS32 = SB["S32"]; S16, tS16 = SB["S16"]; Stmp, tSt = SB["Stmp"]; vn, tvn = SB["vn"]
        onacc, tacc = onacc_t
        gtc3 = gtc[:, 0:64].rearrange("p (h s) -> p h s", s=8)
        sidx = s0 + s_
        rm = self.cst[0:L, C_RM + s_:C_RM + s_ + 1]
        self.dma("sp", S32, self.st_rec[sidx].rearrange("h k v -> k h v"), (), [tS32], "ldrec%d" % (sidx % 3))
        yield
        self.cp("act", S16, S32, [tS32], [tS16])
        self.tt("pool", hd(Stmp[:, :]), S32, gtc3[:, :, s_].unsqueeze(2).to_broadcast([128, 8, 128]), ALU.mult, [tS32, tgtc], [tSt])
        yield
        for half in range(2):
            pv, tpv, ipv = self.bacq()
            for hh in range(4):
                h = half * 4 + hh
                self.mm(pv[0:L, hh * 128:(hh + 1) * 128], Tt[:, h, 0:L], vb[0:L, h * 128:(h + 1) * 128], [tTt, tvb], [tpv], start=(hh == 0), stop=False, skip=True)
            for hh in range(4):
                h = half * 4 + hh
                self.mm(pv[0:L, hh * 128:(hh + 1) * 128], nWT[:, h * L:(h + 1) * L], S16[:, h, :], [tnW, tS16], [tpv], start=False, stop=True, skip=True)
            if half == 0:
                self.act(vn[0:L, 0:512], pv[0:L, :], AF.Identity, [tpv, tC], [tvn], scale=rm)
            else:
                self.ts("dve", vn[0:L, 512:1024], pv[0:L, :], rm, ALU.mult, [tpv, tC], [tvn])
            self.brel(ipv)
        yield
        pss = []
        for half in range(2):
            pS, tpS, ipS = self.bacq()
            pss.append((pS, tpS, ipS))
            for hh in range(4):
                h = half * 4 + hh
                self.mm(pS[:, hh * 128:(hh + 1) * 128], kd[0:L, h * 128:(h + 1) * 128], vn[0:L, h * 128:(h + 1) * 128], [tkd, tvn], [tpS])
        for half in range(2):
            pS, tpS, ipS = pss[half]
            self.tt("dve", S32[:, half * 4:(half + 1) * 4, :], hd(pS[:, :]), hd(Stmp[:, half * 512:(half + 1) * 512]), ALU.add, [tpS, tSt], [tS32])
            self.brel(ipS)
        self.dma("sp", self.o_srec[sidx].rearrange("h k v -> k h v"), S32, [tS32], (), "strec%d" % (sidx % 3))
        pos = []
        for half in range(2):
            po, tpo, ipo = self.bacq()
            pos.append((po, tpo, ipo))
            for hh in range(4):
                h = half * 4 + hh
                self.mm(po[0:L, hh * 128:(hh + 1) * 128], qdT[:, h * L:(h + 1) * L], S16[:, h, :], [tqd, tS16], [tpo], start=(hh == 0), stop=False, skip=True)
            for hh in range(4):
                h = half * 4 + hh
                self.mm(po[0:L, hh * 128:(hh + 1) * 128], At[0:L, h * L:(h + 1) * L], vn[0:L, h * 128:(h + 1) * 128], [tAt, tvn], [tpo], start=False, stop=True, skip=True)
        yield
        for half in range(2):
            po, tpo, ipo = pos[half]
            acc = onacc[0:L, half * 512:(half + 1) * 512]
            if s_ == 0:
                self.ts("dve", acc, po[0:L, :], rm, ALU.mult, [tpo, tC], [tacc])
            else:
                self.stt(acc, po[0:L, :], rm, acc, ALU.mult, ALU.add, [tpo, tC, tacc], [tacc])
            self.brel(ipo)
        yield

    def sample_finish(self, job, TB, onacc_t, QKVZ, ONT):
        kind, c0, L, s0 = job
        tC = self.tC
        h3 = lambda ap: ap.rearrange("p (h l) -> p h l", l=L)
        hd = lambda ap: ap.rearrange("p (h d) -> p h d", d=128)
        W8 = 8 * L
        sqo, tsq = TB["sqo"]; on, ton = TB["on"]; ss, tss = TB["ss"]; rs, trs = TB["rs"]
        onacc, tacc = onacc_t
        self.act(sqo[0:L, :], onacc[0:L, :], AF.Square, [tacc], [tsq])
        self.P.op("dve", lambda e, o=ss[0:L, :], i=hd(sqo[0:L, :]): e.tensor_reduce(out=o, in_=i, axis=AX.X, op=ALU.add), [tsq], [tss])
        self.act(rs[0:L, :], ss[0:L, :], AF.Ln, [tss], [trs], bias=EPS, scale=1.0 / 128.0)
        self.act(rs[0:L, :], rs[0:L, :], AF.Exp, [trs], [trs], scale=-0.5)
        yield
        self.tt("dve", hd(on[0:L, :]), hd(onacc[0:L, :]), rs[0:L, :].unsqueeze(2).to_broadcast([L, 8, 128]), ALU.mult, [tacc, trs], [ton])
        yield
        pt, tpt, ipt = self.bacq()
        ptb = pt[:].bitcast(BF16)
        for h in range(8):
            self.tr(ptb[:, h * L:(h + 1) * L], on[0:L, h * 128:(h + 1) * 128], self.idb[0:L, 0:L], [ton, tC], [tpt])
        self.stt(ONT[:, :, c0:c0 + L], h3(ptb[:, 0:W8]), self.vcol("gnw"), QKVZ[:, 24:32, c0:c0 + L], ALU.mult, ALU.mult, [tpt, tC], [self.tok("ONT", self.phase_id)])
        self.brel(ipt)
        yield

    def ffn(self, st, l):
        NT = st.NT
        self.phase()
        hasS = any(s.kind == "S" for s in st.segs)
        xn, _ = self.A([128, 8, NT], BF16)
        hT, _ = self.A([128, 22, NT], BF16)
        sq2 = [self.A([128, 8, 512], BF16) for _ in range(2)]
        rsb = [self.A([128, 512], F32) for _ in range(2)]
        wsl = [self.A([128, 2, 8, 256], BF16) for _ in range(3)]
        wsl = [(w_, (t_, self.tok("wslb", self.phase_id, i_))) for i_, (w_, t_) in enumerate(wsl)]
        wdn = [self.A([128, 22, 128], BF16) for _ in range(3)]
        ub = [self.A([128, 2 + NTMAX], F32) for _ in range(4)]
        t0b = [self.A([128, 512], F32) for _ in range(4)]
        sab = [self.A([128, 512], F32) for _ in range(2)]
        if hasS:
            SHF, tSHF = self.A([128, NFC, 32], F32)
            stg, tstg = self.A([32, 5632], F32)
        xnt = lambda dc, tile: (xn[:, dc, tile[0].off + tile[1]:tile[0].off + tile[1] + tile[2]], self.tok("xn", self.phase_id, dc, tile[0].off + tile[1]))
        self.norm(st, "nf%d" % l, xnt, sq2, rsb)
        if hasS:
            self.dma("sp", stg, self.st_ffn[l], (), [tstg], "ldst")
            for g in range(0, NFC, 8):
                pb, tpb = self.bank()
                ng = min(8, NFC - g)
                for j in range(ng):
                    fc = g + j
                    self.tr(pb[:, j * 32:(j + 1) * 32], stg[0:32, fc * 128:(fc + 1) * 128], self.idf[0:32, 0:32], [tstg, self.tC], [tpb])
                self.cp("act", SHF[:, g:g + ng, :], pb[:, 0:ng * 32].rearrange("p (a b) -> p a b", b=32), [tpb], [tSHF])
        tHF = self.tok("HF")
        fcw, fcb = "fcw%d" % l, "fcb%d" % l
        def ld_wup(u):
            wt_, twt_ = wsl[u % 3]
            self.dma("pool", wt_[:, 0], self.w_up[l][:, u * 256:(u + 1) * 256].rearrange("(kc p) f -> p kc f", p=128), (), [twt_[0]], "ldw%d" % (u % 3))
            self.dma("pool", wt_[:, 1], self.w_up[l][:, DFF + u * 256:DFF + (u + 1) * 256].rearrange("(kc p) f -> p kc f", p=128), (), [twt_[1]], "ldwb%d" % (u % 3))

        def ld_wdn(dc):
            wd_, twd_ = wdn[dc % 3]
            self.dma("pool", wd_, self.w_down[l][:, dc * 128:(dc + 1) * 128].rearrange("(i p) d -> p i d", p=128), (), [twd_], "ldwd%d" % (dc % 3))
        ld_wup(0); ld_wup(1)
        pend = []
        for u in range(11):
            wt, twt = wsl[u % 3]
            if u + 2 < 11:
                ld_wup(u + 2)
            elif u + 2 == 11:
                ld_wdn(0)
            else:
                ld_wdn(1)
            for j in range(2):
                i = u * 2 + j
                cur_ub = [None, None]
                for tile in st.tiles():
                    seg, t0, n = tile
                    c0 = seg.off + t0
                    conv = []
                    for ab in range(2):
                        fc = i + 22 * ab
                        pb, tpb = self.bank()
                        for kc in range(8):
                            self.mm(pb[:, 0:n], wt[:, ab, kc, j * 128:(j + 1) * 128], xn[:, kc, c0:c0 + n], [twt[ab], xnt(kc, tile)[1]], [tpb],
                                    start=(kc == 0), stop=(kc == 7))
                        if seg.kind == "S":
                            bi = self.rr("ub", [0, 1, 2, 3])
                            ex = ub[bi][0]
                            tex = self.tok("ubL", self.phase_id, bi, 0)
                            rtoks = [tex]
                            self.cp("pool", self.ext_halo(seg, ex, 2), SHF[:, fc, :].rearrange("p (s r) -> p s r", r=2), [tSHF], [tex])
                            self.cp("act", self.ext_dst(seg, ex, 2, n), self.V(seg, pb[:, 0:n]), [tpb], [tex])
                            self.cp("pool", SHF[:, fc, :].rearrange("p (s r) -> p s r", r=2), self.ext_tail(seg, ex, 2, n), [tex], [tSHF])
                        else:
                            if t0 == 0:
                                bi = self.rr("ub", [0, 1, 2, 3])
                                cur_ub[ab] = (bi, 0)
                                tex = self.tok("ubL", self.phase_id, bi, 0)
                                self.cp("act", ub[bi][0][:, 0:2], self.HF[:, l, fc, :], [tHF], [tex])
                                rtoks = [tex]
                            else:
                                bi, ti = cur_ub[ab]
                                cur_ub[ab] = (bi, ti + 1)
                                tex = self.tok("ubL", self.phase_id, bi, ti + 1)
                                rtoks = [tex, self.tok("ubL", self.phase_id, bi, ti)]
                            ex = ub[bi][0][:, t0:t0 + 2 + n]
                            self.cp("act", ex[:, 2:2 + n], pb[:, 0:n], [tpb], [tex])
                            if t0 + n == seg.n:
                                self.cp("act", self.HF[:, l, fc, :], ex[:, n:n + 2], [tex], [tHF])
                        tb, ttb = t0b[self.rr("t0b", [0, 1, 2, 3])]
                        tv = self.V(seg, tb[:, 0:n])
                        self.act(tv, self.V(seg, pb[:, 0:n]), AF.Identity, [tpb, self.tC], [ttb], bias=self.vcol(fcb, fc), scale=self.vcol(fcw, 2 * NFC + fc))
                        self.stt(tv, self.ext_tap(seg, ex, 2, 1, n), self.vcol(fcw, 1 * NFC + fc), tv, ALU.mult, ALU.add, rtoks + [ttb, self.tC], [ttb])
                        self.stt(tv, self.ext_tap(seg, ex, 2, 0, n), self.vcol(fcw, 0 * NFC + fc), tv, ALU.mult, ALU.add, rtoks + [ttb, self.tC], [ttb])
                        conv.append((tb, ttb))
                    def tail(conv=conv, i=i, c0=c0, n=n):
                        sa, tsa = sab[self.rr("sab", [0, 1])]
                        self.act(sa[:, 0:n], conv[0][0][:, 0:n], AF.Silu, [conv[0][1]], [tsa])
                        self.tt("dve", hT[:, i, c0:c0 + n], sa[:, 0:n], conv[1][0][:, 0:n], ALU.mult, [tsa, conv[1][1]], [self.tok("hT", self.phase_id, i, c0)])
                    if pend:
                        pend.pop(0)()
                    pend.append(tail)
        while pend:
            pend.pop(0)()
        if hasS:
            for fc in range(NFC):
                pb, tpb = self.bank()
                self.tr(pb[0:32, 0:128], SHF[:, fc, :], self.idf[:, :], [tSHF, self.tC], [tpb])
                self.cp(self.rr("ev", ["act", "dve"]), stg[0:32, fc * 128:(fc + 1) * 128], pb[0:32, 0:128], [tpb], [tstg])
            self.dma("sp", self.o_sffn[l], stg, [tstg], (), "stst")
        if st.last:
            stg2, tstg2 = self.A([2, 5632], F32)
            for fc in range(NFC):
                pb, tpb = self.bank()
                self.tr(pb[0:2, 0:128], self.HF[:, l, fc, :], self.idf[:, :], [tHF, self.tC], [tpb])
                self.cp(self.rr("ev", ["act", "dve"]), stg2[0:2, fc * 128:(fc + 1) * 128], pb[0:2, 0:128], [tpb], [tstg2])
            self.dma("sp", self.o_pffn[l], stg2, [tstg2], (), "stst")
        for dc in range(8):
            wd, twd = wdn[dc % 3]
            if dc + 2 < 8:
                ld_wdn(dc + 2)
            for tile in st.tiles():
                seg, t0, n = tile
                c0 = seg.off + t0
                pb, tpb = self.bank()
                for i in range(22):
                    self.mm(pb[:, 0:n], wd[:, i, :], hT[:, i, c0:c0 + n], [twd, self.tok("hT", self.phase_id, i, c0)], [tpb], start=(i == 0), stop=(i == 21))
                xv = self.xT[:, dc, c0:c0 + n]
                self.tt("dve", xv, pb[:, 0:n], xv, ALU.add, [tpb], [self.xtok(dc, tile)])

    def pool_mixer(self, st):
        self.phase()
        sq2 = [self.A([128, 8, 512], BF16) for _ in range(2)]
        rsb = [self.A([128, 512], F32) for _ in range(2)]
        pw, tpw = self.A([128, 4, 2, 256], BF16)
        self.dma("pool", pw, self.pool_w.rearrange("g (ci p) e -> p g ci e", p=128), (), [tpw], "ldw0")
        segbuf = {}
        for seg in st.segs:
            W = 16 * 23 if seg.kind == "S" else 15 + seg.n
            hn, _ = self.A([128, 8, W], F32)
            s1, _ = self.A([128, 2, W], F32)
            s2, _ = self.A([128, 2, W], F32)
            PL, _ = self.A([128, 8, seg.n], BF16)
            segbuf[id(seg)] = (hn, s1, s2, PL, W)
        if any(s.kind == "S" for s in st.segs):
            stg, tstg = self.A([120, 2, D], F32)
            self._pcb = [self.A([128, 120], F32) for _ in range(2)]
        tmp15, ttmp15 = self.A([128, 15], F32)
        tHP = self.tok("HP")

        def dstf(dc, tile):
            seg, t0, n = tile
            hn = segbuf[id(seg)][0]
            if seg.kind == "S":
                ap = hn[:, dc, :].rearrange("p (s w) -> p s w", w=23)[:, :, 15:23]
            else:
                ap = hn[:, dc, 15 + t0:15 + t0 + n]
            return ap, self.tok("hn", self.phase_id, id(seg), dc)

        self._norm_pool(st, "nm1", dstf, sq2, rsb)
        for seg in st.segs:
            hn, s1, s2, PL, W = segbuf[id(seg)]
            n = seg.n
            if seg.kind == "S":
                for half in range(2):
                    self.dma("sp", stg[:, half, :], self.st_pool[half * 120:(half + 1) * 120, :], (), [tstg], "ldst")
                for dc in range(8):
                    pb, tpb = self.bank()
                    for half in range(2):
                        self.tr(pb[:, half * 120:(half + 1) * 120], stg[0:120, half, dc * 128:(dc + 1) * 128], self.idf[0:120, 0:120], [tstg, self.tC], [tpb])
                    self.cp(self.rr("ev", ["act", "dve"]), hn[:, dc, :].rearrange("p (s w) -> p s w", w=23)[:, :, 0:15],
                            pb[:, 0:240].rearrange("p (s r) -> p s r", r=15), [tpb], [self.tok("hn", self.phase_id, id(seg), dc)])
            else:
                for dc in range(8):
                    self.cp("pool", hn[:, dc, 0:15], self.HP[:, dc, :], [tHP], [self.tok("hn", self.phase_id, id(seg), dc)])
            if seg.kind == "S":
                e3 = lambda ap: ap.rearrange("p (s w) -> p s w", w=23)
                sl = lambda ap, a, b: e3(ap)[:, :, a:b]
                WW = 23
            else:
                sl = lambda ap, a, b: ap[:, a:b]
                WW = W
            for dc in range(8):
                gi = dc // 2
                th = self.tok("hn", self.phase_id, id(seg), dc)
                ts1 = self.tok("ps1", self.phase_id, id(seg), dc % 2)
                ts2 = self.tok("ps2", self.phase_id, id(seg), dc % 2)
                src, tsrc = hn[:, dc, :], th
                bufs = [(s1[:, dc % 2, :], ts1), (s2[:, dc % 2, :], ts2)]
                for lev in range(gi + 1):
                    sh = 1 << lev
                    lo = (1 << (lev + 1)) - 1
                    dstb, tdb = bufs[lev % 2]
                    self.tt("dve", sl(dstb, lo, WW), sl(src, lo, WW), sl(src, lo - sh, WW - sh), ALU.add, [tsrc], [tdb])
                    src, tsrc = dstb, tdb
                if seg.kind == "S":
                    outv = PL[:, dc, :].rearrange("p (s t) -> p s t", t=8)
                else:
                    outv = PL[:, dc, :]
                tPL = self.tok("PL", self.phase_id, id(seg), dc)
                self.stt(outv, sl(src, 15, WW), 1.0 / WINS[gi], sl(hn[:, dc, :], 15, WW), ALU.mult, ALU.subtract, [tsrc, th], [tPL])
                if seg.kind == "P" and seg.pos0 == 0:
                    ic = self.cst[:, C_INVC + gi * 15:C_INVC + gi * 15 + 15]
                    self.tt("dve", tmp15, src[:, 15:30], ic, ALU.mult, [tsrc, self.tC], [ttmp15])
                    self.tt("dve", PL[:, dc, 0:15], tmp15, hn[:, dc, 15:30], ALU.subtract, [ttmp15, th], [tPL])
            if seg.kind == "S":
                for dc in range(8):
                    th = self.tok("hn", self.phase_id, id(seg), dc)
                    for half in range(2):
                        pb, tpb = self.bank()
                        src3 = hn[:, dc, :].rearrange("p (s w) -> p s w", w=23)[:, half * 8:(half + 1) * 8, 8:23]
                        cbuf, tcb = self._pcb[self.rr("pcb", [0, 1])]
                        self.cp("pool", cbuf.rearrange("p (s r) -> p s r", r=15), src3, [th], [tcb])
                        self.tr(pb[0:120, 0:128], cbuf, self.idf[:, :], [tcb, self.tC], [tpb])
                        self.cp(self.rr("ev", ["act", "dve"]), stg[0:120, half, dc * 128:(dc + 1) * 128], pb[0:120, 0:128], [tpb], [tstg])
                for half in range(2):
                    self.dma("sp", self.o_spool[half * 120:(half + 1) * 120, :], stg[:, half, :], [tstg], (), "stst")
            else:
                for dc in range(8):
                    th = self.tok("hn", self.phase_id, id(seg), dc)
                    self.cp("pool", self.HP[:, dc, :], hn[:, dc, n:n + 15], [th], [tHP])
                if st.last:
                    stg2, tstg2 = self.A([15, D], F32)
                    for dc in range(8):
                        pb, tpb = self.bank()
                        self.tr(pb[0:15, 0:128], self.HP[:, dc, :], self.idf[:, :], [tHP, self.tC], [tpb])
                        self.cp(self.rr("ev", ["act", "dve"]), stg2[0:15, dc * 128:(dc + 1) * 128], pb[0:15, 0:128], [tpb], [tstg2])
                    self.dma("sp", self.o_ppool, stg2, [tstg2], (), "stst")
            for tile in seg.tiles():
                _, t0, nn = tile
                c0 = seg.off + t0
                for gi in range(4):
                    for eo in range(2):
                        dco = 2 * gi + eo
                        pb, tpb = self.bank()
                        for ci in range(2):
                            self.mm(pb[:, 0:nn], pw[:, gi, ci, eo * 128:(eo + 1) * 128], PL[:, 2 * gi + ci, t0:t0 + nn],
                                    [tpw, self.tok("PL", self.phase_id, id(seg), 2 * gi + ci)], [tpb], start=(ci == 0), stop=(ci == 1))
                        xv = self.xT[:, dco, c0:c0 + nn]
                        self.stt(xv, pb[:, 0:nn], self.vcol("psc", dco), xv, ALU.mult, ALU.add, [tpb, self.tC], [self.xtok(dco, tile)])

    def _norm_pool(self, st, wname, dstf, sq2, rsb):
        for tile in st.tiles():
            seg, t0, n = tile
            c0 = seg.off + t0
            sq, tsq = sq2[self.rr("sq2", [0, 1])]
            rs, trs = rsb[self.rr("rsb", [0, 1])]
            pb, tpb = self.bank()
            for dc in range(8):
                xin = self.xT[:, dc, c0:c0 + n]
                self.act(sq[:, dc, 0:n], xin, AF.Square, [self.xtok(dc, tile)], [tsq])
            for dc in range(8):
                self.mm(pb[:, 0:n], self.ones_m[:], sq[:, dc, 0:n], [tsq, self.tC], [tpb], start=(dc == 0), stop=(dc == 7))
            self.act(rs[:, 0:n], pb[:, 0:n], AF.Ln, [tpb], [trs], bias=EPS)
            self.act(rs[:, 0:n], rs[:, 0:n], AF.Exp, [trs], [trs], scale=-0.5)
            for dc in range(8):
                dst, tdst = dstf(dc, tile)
                self.stt(dst, self.V(seg, self.xT[:, dc, c0:c0 + n]), self.vcol(wname, dc), self.V(seg, rs[:, 0:n]), ALU.mult, ALU.mult,
                         [self.xtok(dc, tile), trs, self.tC], [tdst])

    def final(self, st):
        self.phase()
        sq2 = [self.A([128, 8, 512], BF16) for _ in range(2)]
        rsb = [self.A([128, 512], F32) for _ in range(2)]
        yT = [self.A([128, 8, 512], F32) for _ in range(2)]
        ysg = [self.A([128, D], F32) for _ in range(3)]
        cur = {}

        def dstf(dc, tile):
            return cur["y"][0][:, dc, 0:tile[2]], cur["y"][1]

        for tile in st.tiles():
            seg, t0, n = tile
            cur["y"] = yT[self.rr("yT", [0, 1])]
            self._norm_one(tile, "nfin", dstf, sq2, rsb)
            y, ty = cur["y"]
            b0 = 0
            while b0 < n:
                if seg.kind == "P":
                    pos = seg.pos0 + t0 + b0
                    if pos < NMETA:
                        b0 += NMETA - pos
                        continue
                m = min(128, n - b0)
                sg, tsg = ysg[self.rr("ysg", [0, 1, 2])]
                for half in range(2):
                    pb, tpb = self.bank()
                    for j in range(4):
                        dc = half * 4 + j
                        self.tr(pb[0:m, j * 128:(j + 1) * 128], y[:, dc, b0:b0 + m], self.idf[:, :], [ty, self.tC], [tpb])
                    self.cp(("act", "dve")[half], sg[0:m, half * 512:(half + 1) * 512], pb[0:m, :], [tpb], [tsg])
                if seg.kind == "S":
                    self.dma("sp", self.ys[b0:b0 + m, :], sg[0:m, :], [tsg], (), "sty%d" % ((self.rrc["ysg"] - 1) % 3))
                else:
                    r0 = seg.pos0 + t0 + b0 - NMETA
                    self.dma("sp", self.yp[r0:r0 + m, :], sg[0:m, :], [tsg], (), "sty%d" % ((self.rrc["ysg"] - 1) % 3))
                b0 += m

    def _norm_one(self, tile, wname, dstf, sq2, rsb):
        seg, t0, n = tile
        c0 = seg.off + t0
        sq, tsq = sq2[self.rr("sq2", [0, 1])]
        rs, trs = rsb[self.rr("rsb", [0, 1])]
        pb, tpb = self.bank()
        for dc in range(8):
            xin = self.xT[:, dc, c0:c0 + n]
            self.act(sq[:, dc, 0:n], xin, AF.Square, [self.xtok(dc, tile)], [tsq])
        for dc in range(8):
            self.mm(pb[:, 0:n], self.ones_m[:], sq[:, dc, 0:n], [tsq, self.tC], [tpb], start=(dc == 0), stop=(dc == 7))
        self.act(rs[:, 0:n], pb[:, 0:n], AF.Ln, [tpb], [trs], bias=EPS)
        self.act(rs[:, 0:n], rs[:, 0:n], AF.Exp, [trs], [trs], scale=-0.5)
        for dc in range(8):
            dst, tdst = dstf(dc, tile)
            self.stt(dst, self.xT[:, dc, c0:c0 + n], self.vcol(wname, dc), rs[:, 0:n], ALU.mult, ALU.mult,
                     [self.xtok(dc, tile), trs, self.tC], [tdst])


_NC_CACHE = {}


def _get_nc():
    if "nc" not in _NC_CACHE:
        b = Builder()
        _NC_CACHE["nc"] = b.build()
    return _NC_CACHE["nc"]


def kernel(**inp):
    inp = {k: np.asarray(v) for k, v in inp.items()}
    f = lambda a: np.ascontiguousarray(a, dtype=np.float32)
    nc = _get_nc()
    vecs = build_vecs(inp)
    consts = build_consts()
    w_in = f(inp["gdn_w_in"][0])
    wba = np.zeros((D, 40), np.float32)
    wba[:, 0:8] = w_in[:, 4104:4112]
    wba[:, 32:40] = w_in[:, 4096:4104]
    shared = {
        "meta": f(inp["meta_tokens"]), "w_in": w_in, "wba": wba, "w_out": f(inp["gdn_w_out"][0]),
        "pool_w": f(inp["pool_w"][0]), "w_up": f(inp["ffn_w_up"]), "w_down": f(inp["ffn_w_down"]),
        "vecs": vecs, "consts": consts,
    }
    in_maps = []
    for c in range(8):
        sl = slice(16 * c, 16 * c + 16)
        m = dict(shared)
        m["xp"] = f(inp["x_prompt"][c])
        m["xs"] = f(inp["x_sample"][sl].reshape(128, D))
        m["st_conv"] = f(inp["state_gdn_conv"][0, sl].reshape(48, 3072))
        m["st_rec"] = f(inp["state_gdn_rec"][0, sl])
        m["st_pool"] = f(inp["state_pool"][0, sl].reshape(240, D))
        m["st_ffn"] = f(inp["state_ffn_conv"][:, sl].reshape(2, 32, 5632))
        in_maps.append(m)
    res = run_bass_kernel_spmd(nc, in_maps, core_ids=list(range(8)))
    R = res.results
    g = lambda k: [np.asarray(r[k], dtype=np.float32) for r in R]
    y_prompt = np.stack(g("yp"), 0)
    y_sample = np.concatenate(g("ys"), 0).reshape(128, 8, D)
    p_conv = np.stack(g("o_pconv"), 0)[None]
    p_rec = np.stack(g("o_prec"), 0)[None]
    p_pool = np.stack(g("o_ppool"), 0)[None]
    p_ffn = np.stack(g("o_pffn"), 1)
    s_conv = np.concatenate([a.reshape(16, 3, 3072) for a in g("o_sconv")], 0)[None]
    s_rec = np.concatenate(g("o_srec"), 0)[None]
    s_pool = np.concatenate([a.reshape(16, 15, D) for a in g("o_spool")], 0)[None]
    s_ffn = np.concatenate([a.reshape(2, 16, 2, 5632) for a in g("o_sffn")], 1)
    return (y_prompt, y_sample, p_conv, p_rec, p_pool, p_ffn, s_conv, s_rec, s_pool, s_ffn)
```
